# Optimizing a Trainium2 kernel written in Bass

```python
import math
import jax, jax.numpy as jnp
from jax import lax
import numpy as np

D_MODEL = 1024
BATCH = 4
SEQ = 8192
DEPTH = 2

GRID_W = 64
CTX_LEN = 256
HEAD_DIM = 64
N_GROUPS = 4
GROUP_HEADS = D_MODEL // HEAD_DIM // N_GROUPS
GROUP_W = GROUP_HEADS * HEAD_DIM
N_KV = 2
KV_W = N_KV * HEAD_DIM
WINDOW = 128
BLOCK = 128
RET_CHUNK = 128
D_FF = ((8 * D_MODEL // 3 + 255) // 256) * 256
W_RANK = 64
A_RANK = 64
G_RANK = 128
ROPE_BASE = 10000.0
RMS_EPS = 1e-6
GN_EPS = 64e-5
DECAY_SCALE = 0.6065306597126334
N_MOD = 9
PROJ_SIZES = (GROUP_W, KV_W, KV_W,
              GROUP_W, KV_W, KV_W,
              GROUP_W, GROUP_W, GROUP_W, GROUP_W, GROUP_W,
              3 * GROUP_W, G_RANK, W_RANK, W_RANK, A_RANK, A_RANK)
PROJ_COLS = sum(PROJ_SIZES)
RWKV_MIX = 3 * GROUP_W + W_RANK + A_RANK

kernel_name = 'hybrid_parallel_group_flow_block'


def rms_norm(x, g):
    xf = x.astype(jnp.float32)
    y = xf * lax.rsqrt(jnp.mean(xf * xf, -1, keepdims=True) + RMS_EPS)
    return (y * g.astype(jnp.float32)).astype(x.dtype)


def head_norm(y, g, b=None):
    yf = y.astype(jnp.float32)
    mu = jnp.mean(yf, -1, keepdims=True)
    yn = (yf - mu) * lax.rsqrt(jnp.mean(jnp.square(yf - mu), -1, keepdims=True) + GN_EPS)
    yn = yn.reshape(y.shape[:-2] + (y.shape[-2] * y.shape[-1],)) * g
    return yn if b is None else yn + b


def modulate(h, shift, scale):
    return h * (1.0 + scale) + shift


def swiglu(h, w_in, w_out):
    u = h @ w_in
    return (jax.nn.silu(u[..., :D_FF]) * u[..., D_FF:]) @ w_out


def heads(t):
    return t.reshape(t.shape[:-1] + (t.shape[-1] // HEAD_DIM, HEAD_DIM))


def flip(t):
    return jnp.flip(t, axis=1)


def split_cols(p):
    out, o = [], 0
    for s in PROJ_SIZES:
        out.append(p[..., o:o + s])
        o += s
    return out


def rope_cos_sin(pos, dim):
    inv = 1.0 / (ROPE_BASE ** (jnp.arange(0, dim, 2, dtype=jnp.float32) / dim))
    ang = pos.astype(jnp.float32)[:, None] * inv[None, :]
    return jnp.cos(ang), jnp.sin(ang)


def apply_rope(x, cos, sin):
    half = x.shape[-1] // 2
    xf = x.astype(jnp.float32)
    x1, x2 = xf[..., :half], xf[..., half:]
    c, s = cos[None, :, None, :], sin[None, :, None, :]
    return jnp.concatenate([x1 * c - x2 * s, x2 * c + x1 * s], -1).astype(x.dtype)


def apply_axial_rope(x, rope_row, rope_col):
    half = x.shape[-1] // 2
    return jnp.concatenate([apply_rope(x[..., :half], *rope_row), apply_rope(x[..., half:], *rope_col)], -1)


def dense_attn(q, k, v, sink=None):
    b_, tq, h, hd = q.shape
    g = h // N_KV
    qg = q.reshape(b_, tq, N_KV, g, hd)
    s = jnp.einsum('bqkgd,bskd->bkgqs', qg, k).astype(jnp.float32) * hd ** -0.5
    if sink is not None:
        sk = jnp.broadcast_to(sink.reshape(N_KV, g)[None, :, :, None, None].astype(jnp.float32), s.shape[:-1] + (1,))
        s = jnp.concatenate([s, sk], -1)
    p = jax.nn.softmax(s, axis=-1)[..., :k.shape[1]]
    o = jnp.einsum('bkgqs,bskd->bqkgd', p.astype(v.dtype), v)
    return o.reshape(b_, tq, h * hd)


def window_attn_latent(q, k, v, kc, vc, sink):
    b_, n, h, hd = q.shape
    nb = n // BLOCK
    g = h // N_KV
    qb = q.reshape(b_, nb, BLOCK, N_KV, g, hd)

    def band(t):
        tp = jnp.pad(t, ((0, 0), (BLOCK, BLOCK), (0, 0), (0, 0))).reshape(b_, nb + 2, BLOCK, N_KV, hd)
        return jnp.concatenate([tp[:, :-2], tp[:, 1:-1], tp[:, 2:]], axis=2)

    kw, vw = band(k), band(v)
    scale = hd ** -0.5
    s_win = jnp.einsum('bnqkgd,bnskd->bnkgqs', qb, kw).astype(jnp.float32) * scale
    s_ctx = jnp.einsum('bnqkgd,bskd->bnkgqs', qb, kc).astype(jnp.float32) * scale
    qpos = jnp.arange(nb)[:, None, None] * BLOCK + jnp.arange(BLOCK)[None, :, None]
    kpos = jnp.arange(nb)[:, None, None] * BLOCK - BLOCK + jnp.arange(3 * BLOCK)[None, None, :]
    valid = (jnp.abs(kpos - qpos) <= WINDOW) & (kpos >= 0) & (kpos < n)
    s_win = jnp.where(valid[None, :, None, None], s_win, -jnp.inf)
    sk = jnp.broadcast_to(sink.reshape(N_KV, g)[None, None, :, :, None, None].astype(jnp.float32), s_ctx.shape[:-1] + (1,))
    p = jax.nn.softmax(jnp.concatenate([s_win, s_ctx, sk], -1), axis=-1)
    w3 = 3 * BLOCK
    o = (jnp.einsum('bnkgqs,bnskd->bnqkgd', p[..., :w3].astype(v.dtype), vw)
         + jnp.einsum('bnkgqs,bskd->bnqkgd', p[..., w3:w3 + kc.shape[1]].astype(vc.dtype), vc))
    return o.reshape(b_, n, h * hd)


def global_attn_latent(q, k_all, v_all):
    b_, n, h, hd = q.shape
    nb = n // BLOCK
    qb = jnp.moveaxis(q.reshape(b_, nb, BLOCK, h, hd), 1, 0)
    ob = lax.map(lambda qi: dense_attn(qi, k_all, v_all), qb)
    return jnp.moveaxis(ob, 0, 1).reshape(b_, n, h * hd)


def retention_chunked(q, k, v, log_gamma, s0):
    b_, t, h, dk = q.shape
    c = RET_CHUNK
    nc = t // c
    qc = q.astype(jnp.float32).reshape(b_, nc, c, h, dk)
    kc = k.astype(jnp.float32).reshape(b_, nc, c, h, dk)
    vc = v.astype(jnp.float32).reshape(b_, nc, c, h, -1)
    idx = jnp.arange(c, dtype=jnp.float32)
    rel = idx[:, None] - idx[None, :]
    d_intra = jnp.where(rel[None] >= 0, jnp.exp(jnp.maximum(rel, 0.0)[None] * log_gamma[:, None, None]), 0.0)
    att = jnp.einsum('bnihd,bnjhd->bnhij', qc, kc) * d_intra[None, None]
    o_intra = jnp.einsum('bnhij,bnjhe->bnihe', att, vc)
    k_w = jnp.exp((c - 1 - idx)[None, :] * log_gamma[:, None])
    u = jnp.einsum('bnjhd,hj,bnjhe->nbhde', kc, k_w, vc)
    chunk_decay = jnp.exp(c * log_gamma)[None, :, None, None]

    def step(s, u_n):
        return chunk_decay * s + u_n, s

    s_final, s_prev = lax.scan(step, s0, u)
    q_w = jnp.exp((idx + 1)[None, :] * log_gamma[:, None])
    o_cross = jnp.einsum('bnihd,hi,nbhde->bnihe', qc, q_w, s_prev)
    return (o_intra + o_cross).reshape(b_, t, h, -1), s_final


def token_shift(z, reverse):
    if reverse:
        return jnp.pad(z[:, 1:], ((0, 0), (0, 1), (0, 0)))
    return jnp.pad(z[:, :-1], ((0, 0), (1, 0), (0, 0)))


def rwkv7_scan(r, w, k, v, a, b, s0, reverse):
    def step(s, inp):
        r_t, w_t, k_t, v_t, a_t, b_t = inp
        sa = jnp.einsum('bhij,bhj->bhi', s, a_t)
        s = s * w_t[:, :, None, :] + sa[..., None] * b_t[:, :, None, :] + v_t[..., None] * k_t[:, :, None, :]
        return s, jnp.einsum('bhij,bhj->bhi', s, r_t)

    xs = tuple(jnp.moveaxis(t, 1, 0) for t in (r, w, k, v, a, b))
    s_final, y = lax.scan(step, s0, xs, reverse=reverse)
    return jnp.moveaxis(y, 0, 1), s_final


def rwkv7_direction(rkv, w_low, a_low, s0, reverse, mu, w0, w2, a0, a2, rho, k_k, k_a, ln_g, ln_b):
    z = jnp.concatenate([rkv, w_low, a_low], -1).astype(jnp.float32)
    z = z + mu * (token_shift(z, reverse) - z)
    r, k, v = z[..., :GROUP_W], z[..., GROUP_W:2 * GROUP_W], z[..., 2 * GROUP_W:3 * GROUP_W]
    w_in_ = z[..., 3 * GROUP_W:3 * GROUP_W + W_RANK]
    a_in = z[..., 3 * GROUP_W + W_RANK:]
    decay = jnp.exp(-DECAY_SCALE * jax.nn.sigmoid(w0 + jnp.tanh(w_in_) @ w2))
    a = jax.nn.sigmoid(a0 + a_in @ a2)
    kk = heads(k * k_k)
    kk = kk / jnp.maximum(jnp.sqrt(jnp.sum(kk * kk, -1, keepdims=True)), 1e-12)
    k_t = k * (1.0 + (a - 1.0) * k_a)
    r_h, k_h, v_h = heads(r), heads(k_t), heads(v)
    y, s_final = rwkv7_scan(r_h, heads(decay), k_h, v_h, -kk, kk * heads(a), s0, reverse)
    bonus = jnp.sum(r_h * k_h * rho, -1, keepdims=True) * v_h
    y = head_norm(y, ln_g, ln_b) + bonus.reshape(bonus.shape[:2] + (GROUP_W,))
    return y, s_final


def mixer(h, hc, w_in_l, w_out_l, sink, qk_g, ret_g, mu, w0, w2, a0, a2, rho, k_k, k_a, g2, ln_g, ln_b,
          rope_row, rope_col, rope_seq, log_gamma, need_ctx_out):
    (qa, ka, va, qb, kb, vb, qr, kr, vr, grf, grb, rkv, gd, wdf, wdb, adf, adb) = split_cols(h @ w_in_l)
    (cqa, cka, cva, cqb, ckb, cvb, cqr, ckr, cvr, cgrf, cgrb, crkv, cgd, cwdf, cwdb, cadf, cadb) = split_cols(hc @ w_in_l)
    b_ = h.shape[0]
    cka_h, cva_h = heads(cka), heads(cva)
    out_a = window_attn_latent(apply_axial_rope(heads(qa), rope_row, rope_col),
                               apply_axial_rope(heads(ka), rope_row, rope_col), heads(va), cka_h, cva_h, sink)
    ckb_h, cvb_h = rms_norm(heads(ckb), qk_g[1]), heads(cvb)
    qb_h = apply_axial_rope(rms_norm(heads(qb), qk_g[0]), rope_row, rope_col)
    kb_h = apply_axial_rope(rms_norm(heads(kb), qk_g[1]), rope_row, rope_col)
    out_b = global_attn_latent(qb_h, jnp.concatenate([kb_h, ckb_h], 1), jnp.concatenate([heads(vb), cvb_h], 1))
    zeros = jnp.zeros((b_, GROUP_HEADS, HEAD_DIM, HEAD_DIM), jnp.float32)
    k_scale = HEAD_DIM ** -0.5
    cq_r, ck_r, cv_r = heads(cqr), heads(ckr) * k_scale, heads(cvr)
    oc_f, sc_f = retention_chunked(cq_r, ck_r, cv_r, log_gamma, zeros)
    oc_b, sc_b = retention_chunked(flip(cq_r), flip(ck_r), flip(cv_r), log_gamma, zeros)
    q_r = apply_rope(heads(qr), *rope_seq)
    k_r = apply_rope(heads(kr), *rope_seq) * k_scale
    v_r = heads(vr)
    o_f, _ = retention_chunked(q_r, k_r, v_r, log_gamma, sc_f)
    o_b, _ = retention_chunked(flip(q_r), flip(k_r), flip(v_r), log_gamma, sc_b)
    out_c = head_norm(o_f, ret_g) * jax.nn.silu(grf) + head_norm(flip(o_b), ret_g) * jax.nn.silu(grb)
    yc_f, sdc_f = rwkv7_direction(crkv, cwdf, cadf, zeros, False, mu[0], w0[0], w2[0], a0[0], a2[0], rho[0], k_k, k_a, ln_g, ln_b)
    yc_b, sdc_b = rwkv7_direction(crkv, cwdb, cadb, zeros, True, mu[1], w0[1], w2[1], a0[1], a2[1], rho[1], k_k, k_a, ln_g, ln_b)
    y_f, _ = rwkv7_direction(rkv, wdf, adf, sdc_f, False, mu[0], w0[0], w2[0], a0[0], a2[0], rho[0], k_k, k_a, ln_g, ln_b)
    y_b, _ = rwkv7_direction(rkv, wdb, adb, sdc_b, True, mu[1], w0[1], w2[1], a0[1], a2[1], rho[1], k_k, k_a, ln_g, ln_b)
    out_d = (y_f + y_b) * (jax.nn.sigmoid(gd) @ g2)
    dt = h.dtype
    out = jnp.concatenate([out_a, out_b, out_c.astype(dt), out_d.astype(dt)], -1) @ w_out_l
    if not need_ctx_out:
        return out, None
    oac = dense_attn(heads(cqa), cka_h, cva_h, sink)
    obc = dense_attn(rms_norm(heads(cqb), qk_g[0]), ckb_h, cvb_h)
    occ = head_norm(oc_f, ret_g) * jax.nn.silu(cgrf) + head_norm(flip(oc_b), ret_g) * jax.nn.silu(cgrb)
    odc = (yc_f + yc_b) * (jax.nn.sigmoid(cgd) @ g2)
    out_ctx = jnp.concatenate([oac, obc, occ.astype(dt), odc.astype(dt)], -1) @ w_out_l
    return out, out_ctx


def setup_inputs(seed: int = 0) -> dict:
    key = jax.random.key(seed)
    ks = jax.random.split(key, 26)
    D = D_MODEL
    L = DEPTH

    def nrm(k, shape, s):
        return jax.random.normal(k, shape, jnp.float32) * s

    return {
        'x': nrm(ks[0], (BATCH, SEQ, D), 1.0),
        'c': nrm(ks[1], (BATCH, D), 1.0),
        'ctx': nrm(ks[2], (BATCH, CTX_LEN, D), 1.0),
        'c_ctx': nrm(ks[3], (D,), 1.0),
        'w_mod': nrm(ks[4], (L, D, N_MOD * D), 0.5 * D ** -0.5),
        'b_mod': nrm(ks[5], (L, N_MOD * D), 0.01),
        'norm_g': 1.0 + nrm(ks[6], (L, 3, D), 0.02),
        'ffn_w_in': nrm(ks[7], (L, 2, D, 2 * D_FF), D ** -0.5),
        'ffn_w_out': nrm(ks[8], (L, 2, D_FF, D), D_FF ** -0.5),
        'w_in': nrm(ks[9], (L, D, PROJ_COLS), D ** -0.5),
        'w_out': nrm(ks[10], (L, N_GROUPS * GROUP_W, D), (N_GROUPS * GROUP_W) ** -0.5),
        'attn_sink': nrm(ks[11], (L, GROUP_HEADS), 1.0),
        'qk_norm_g': 1.0 + nrm(ks[12], (L, 2, HEAD_DIM), 0.02),
        'ret_norm_g': 1.0 + nrm(ks[13], (L, GROUP_W), 0.02),
        'rwkv_mu': jax.random.uniform(ks[14], (L, 2, RWKV_MIX), jnp.float32),
        'rwkv_w0': jax.random.uniform(ks[15], (L, 2, GROUP_W), jnp.float32, -4.0, 2.0),
        'rwkv_w2': nrm(ks[16], (L, 2, W_RANK, GROUP_W), 0.1 * W_RANK ** -0.5),
        'rwkv_a0': nrm(ks[17], (L, 2, GROUP_W), 0.5),
        'rwkv_a2': nrm(ks[18], (L, 2, A_RANK, GROUP_W), 0.1 * A_RANK ** -0.5),
        'rwkv_rho': nrm(ks[19], (L, 2, GROUP_HEADS, HEAD_DIM), 0.1),
        'rwkv_k_k': 1.0 + nrm(ks[20], (L, GROUP_W), 0.1),
        'rwkv_k_a': 1.0 + nrm(ks[21], (L, GROUP_W), 0.1),
        'rwkv_g2': nrm(ks[22], (L, G_RANK, GROUP_W), G_RANK ** -0.5),
        'rwkv_ln_g': 1.0 + nrm(ks[23], (L, GROUP_W), 0.02),
        'rwkv_ln_b': nrm(ks[24], (L, GROUP_W), 0.01),
        'final_norm_g': 1.0 + nrm(ks[25], (D,), 0.02),
    }


def reference(x, c, ctx, c_ctx, w_mod, b_mod, norm_g, ffn_w_in, ffn_w_out, w_in, w_out, attn_sink, qk_norm_g,
              ret_norm_g, rwkv_mu, rwkv_w0, rwkv_w2, rwkv_a0, rwkv_a2, rwkv_rho, rwkv_k_k, rwkv_k_a, rwkv_g2,
              rwkv_ln_g, rwkv_ln_b, final_norm_g):
    n_tok = x.shape[1]
    rows = n_tok // GRID_W
    row = jnp.repeat(jnp.arange(rows), GRID_W)
    col = jnp.arange(rows * GRID_W) % GRID_W
    rope_row = rope_cos_sin(row, HEAD_DIM // 2)
    rope_col = rope_cos_sin(col, HEAD_DIM // 2)
    rope_seq = rope_cos_sin(jnp.arange(n_tok), HEAD_DIM)
    log_gamma = jnp.log1p(-jnp.exp2(-5.0 - jnp.arange(GROUP_HEADS, dtype=jnp.float32)))
    xc = ctx
    for l in range(DEPTH):
        need_ctx_out = l < DEPTH - 1
        mod = (jax.nn.silu(c) @ w_mod[l] + b_mod[l]).reshape(c.shape[0], N_MOD, 1, D_MODEL)
        mod_c = (jax.nn.silu(c_ctx) @ w_mod[l] + b_mod[l]).reshape(N_MOD, 1, D_MODEL)
        x = x + 0.5 * mod[:, 2] * swiglu(modulate(rms_norm(x, norm_g[l, 0]), mod[:, 0], mod[:, 1]), ffn_w_in[l, 0], ffn_w_out[l, 0])
        xc = xc + 0.5 * mod_c[2] * swiglu(modulate(rms_norm(xc, norm_g[l, 0]), mod_c[0], mod_c[1]), ffn_w_in[l, 0], ffn_w_out[l, 0])
        h = modulate(rms_norm(x, norm_g[l, 1]), mod[:, 3], mod[:, 4])
        hc = modulate(rms_norm(xc, norm_g[l, 1]), mod_c[3], mod_c[4])
        y, yc = mixer(h, hc, w_in[l], w_out[l], attn_sink[l], qk_norm_g[l], ret_norm_g[l], rwkv_mu[l], rwkv_w0[l],
                      rwkv_w2[l], rwkv_a0[l], rwkv_a2[l], rwkv_rho[l], rwkv_k_k[l], rwkv_k_a[l], rwkv_g2[l],
                      rwkv_ln_g[l], rwkv_ln_b[l], rope_row, rope_col, rope_seq, log_gamma, need_ctx_out)
        x = x + mod[:, 5] * y
        x = x + 0.5 * mod[:, 8] * swiglu(modulate(rms_norm(x, norm_g[l, 2]), mod[:, 6], mod[:, 7]), ffn_w_in[l, 1], ffn_w_out[l, 1])
        if need_ctx_out:
            xc = xc + mod_c[5] * yc
            xc = xc + 0.5 * mod_c[8] * swiglu(modulate(rms_norm(xc, norm_g[l, 2]), mod_c[6], mod_c[7]), ffn_w_in[l, 1], ffn_w_out[l, 1])
    return rms_norm(x, final_norm_g)
```

```python
import contextlib
import numpy as np
import concourse.bass as bass
import concourse.mybir as mybir
from concourse.bass_utils import run_bass_kernel_spmd

ACT = mybir.ActivationFunctionType
ALU = mybir.AluOpType
AX = mybir.AxisListType
F32 = mybir.dt.float32
BF16 = mybir.dt.bfloat16

D = 1024
DFF = 2816
NMOD = 9
RMS_EPS = 1e-6


class PSem:
    def __init__(self, h):
        self.h = h
        self.val = 0


class Eng:
    def __init__(self, name, h, sem):
        self.name, self.h, self.sem = name, h, sem
        self.n = 0
        self.seen = {}
        self.dseen = {}


class Buf:
    def __init__(self, k, t, name):
        self.k, self.t, self.name = k, t, name
        self.w = None
        self.r = {}
        self.psem = None
        self.dma_w = 0
        self.dma_rw = 0

    def __getitem__(self, key):
        return self.t[key]

    @property
    def v(self):
        return _VA(self)


class View:
    def __init__(self, buf, ap):
        self.buf, self.ap = buf, ap

    def rr(self, pat, **kw):
        return View(self.buf, self.ap.rearrange(pat, **kw))

    def bc(self, shape):
        return View(self.buf, self.ap.to_broadcast(list(shape)))

    def __getitem__(self, key):
        return View(self.buf, self.ap[key])


class _VA:
    def __init__(self, b):
        self.b = b

    def __getitem__(self, key):
        return View(self.b, self.b.t[key])


def _bufs(*xs):
    out = []
    for x in xs:
        if isinstance(x, View) and x.buf not in out:
            out.append(x.buf)
    return out


def _ap(x):
    return x.ap if isinstance(x, View) else x


class K:
    def __init__(self, nc):
        self.nc = nc
        self.stack = contextlib.ExitStack()
        self.E = {}
        for name, h in (("pe", nc.tensor), ("act", nc.scalar), ("dve", nc.vector),
                        ("pool", nc.gpsimd), ("sp", nc.sync)):
            sem = self.stack.enter_context(nc.semaphore("s_" + name))
            self.E[name] = Eng(name, h, sem)
        self.free_psems = []
        self.n_psems = 0
        self.all_psems = []
        self.bufs = []
        self.uid = 0
        self.ninstr = 0

    def _psem(self):
        if self.free_psems:
            return self.free_psems.pop()
        self.n_psems += 1
        h = self.stack.enter_context(self.nc.semaphore("d%d" % self.n_psems))
        p = PSem(h)
        self.all_psems.append(p)
        return p

    def sb(self, st, shape, dtype, name):
        self.uid += 1
        t = st.enter_context(self.nc.sbuf_tensor("%s_%d" % (name, self.uid), list(shape), dtype))
        b = Buf(self, t, name)
        st.callback(self._release, b)
        return b

    def ps(self, st, shape, dtype, name):
        self.uid += 1
        t = st.enter_context(self.nc.psum_tensor("%s_%d" % (name, self.uid), list(shape), dtype))
        return Buf(self, t, name)

    def _release(self, b):
        if b.psem is not None and not getattr(b.psem, "sw", False):
            self.free_psems.append(b.psem)
        b.psem = None

    def _wait_eng(self, E, X, n):
        if E.seen.get(X.name, 0) >= n:
            return
        E.h.wait_ge(X.sem, n)
        E.seen[X.name] = n
        self.ninstr += 1

    def _wait_d(self, E, p, v):
        if v <= 0 or E.dseen.get(id(p), 0) >= v:
            return
        E.h.wait_ge(p.h, v)
        E.dseen[id(p)] = v
        self.ninstr += 1

    def _deps(self, E, r, w):
        for b in r:
            if b.w is not None:
                self._wait_eng(E, b.w[0], b.w[1])
            if b.psem is not None:
                self._wait_d(E, b.psem, b.dma_w)
        for b in w:
            if b.w is not None and b.w[0] is not E:
                self._wait_eng(E, b.w[0], b.w[1])
            for X, n in b.r.items():
                if X is not E:
                    self._wait_eng(E, X, n)
            if b.psem is not None:
                self._wait_d(E, b.psem, b.dma_rw)

    def op(self, eng, fn, r=(), w=()):
        E = self.E[eng]
        self._deps(E, r, w)
        ins = fn(E.h)
        E.n += 1
        ins.then_inc(E.sem, 1)
        self.ninstr += 1
        for b in w:
            b.w = (E, E.n)
            b.r = {}
        for b in r:
            if b.w is None or b.w[0] is not E or b.w[1] != E.n:
                b.r[E] = E.n
        return ins

    def dma(self, q, out, in_, r=(), w=(), **kw):
        Q = self.E[q]
        self._deps(Q, r, w)
        anchor = w[0] if w else r[0]
        if anchor.psem is None:
            if q == "pool":
                self.n_psems += 1
                anchor.psem = PSem(self.stack.enter_context(self.nc.semaphore("w%d" % self.n_psems)))
                anchor.psem.sw = True
                self.all_psems.append(anchor.psem)
            else:
                anchor.psem = self._psem()
        p = anchor.psem
        assert getattr(p, "sw", False) == (q == "pool"), "buffer mixes SW and HW DGE DMAs: " + anchor.name
        Q.h.dma_start(out=out, in_=in_, **kw).then_inc(p.h, 16)
        p.val += 16
        self.ninstr += 1
        anchor.dma_rw = p.val
        if w:
            anchor.dma_w = p.val
            anchor.w = None
            anchor.r = {}

    def mm(self, out, lhsT, rhs, start=True, stop=True):
        return self.op("pe", lambda e: e.matmul(out.ap, lhsT.ap, rhs.ap, start=start, stop=stop),
                       r=_bufs(lhsT, rhs), w=[out.buf])

    def tr(self, out, in_, ident):
        return self.op("pe", lambda e: e.transpose(out.ap, in_.ap, ident.ap), r=_bufs(in_, ident), w=[out.buf])

    def tt(self, eng, out, in0, in1, op):
        return self.op(eng, lambda e: e.tensor_tensor(out=out.ap, in0=in0.ap, in1=in1.ap, op=op),
                       r=_bufs(in0, in1), w=[out.buf])

    def ts(self, eng, out, in0, s1, s2, op0, op1=None):
        kw = {} if op1 is None else dict(op1=op1)
        return self.op(eng, lambda e: e.tensor_scalar(out=out.ap, in0=in0.ap, scalar1=_ap(s1), scalar2=_ap(s2),
                                                      op0=op0, **kw), r=_bufs(in0, s1, s2), w=[out.buf])

    def stt(self, eng, out, in0, scalar, in1, op0, op1):
        return self.op(eng, lambda e: e.scalar_tensor_tensor(out=out.ap, in0=in0.ap, scalar=_ap(scalar), in1=in1.ap,
                                                             op0=op0, op1=op1), r=_bufs(in0, scalar, in1), w=[out.buf])

    def act(self, out, in_, func, bias=None, scale=None, accum=None):
        kw = {}
        if bias is not None:
            kw["bias"] = _ap(bias)
        if scale is not None:
            kw["scale"] = _ap(scale)
        w = [out.buf]
        if accum is not None:
            kw["accum_out"] = accum.ap
            w.append(accum.buf)
        return self.op("act", lambda e: e.activation(out=out.ap, in_=in_.ap, func=func, **kw),
                       r=_bufs(in_, bias, scale), w=w)

    def cp(self, eng, out, in_):
        if eng == "act":
            return self.op("act", lambda e: e.copy(out=out.ap, in_=in_.ap), r=[in_.buf], w=[out.buf])
        return self.op(eng, lambda e: e.tensor_copy(out=out.ap, in_=in_.ap), r=[in_.buf], w=[out.buf])

    def memset(self, eng, out, val):
        return self.op(eng, lambda e: e.memset(out.ap, val), w=[out.buf])

    def recip(self, out, in_):
        return self.op("dve", lambda e: e.reciprocal(out=out.ap, in_=in_.ap), r=[in_.buf], w=[out.buf])

    def ld(self, q, out, dram, **kw):
        return self.dma(q, out.ap, dram, w=[out.buf], **kw)

    def stv(self, q, dram, in_, **kw):
        return self.dma(q, dram, in_.ap, r=[in_.buf], **kw)

    def barrier(self):
        for E in self.E.values():
            for X in self.E.values():
                if X is not E and X.n > 0:
                    self._wait_eng(E, X, X.n)
            for p in self.all_psems:
                self._wait_d(E, p, p.val)


def mm(k, out_ap, lhsT_ap, rhs_ap, start, stop, r, w):
    return k.op("pe", lambda e: e.matmul(out_ap, lhsT_ap, rhs_ap, start=start, stop=stop), r=r, w=w)


def rowphase(k, st, cfg):
    nc = k.nc
    ffn = cfg["ffn"]
    pre = cfg.get("pre")
    post = cfg.get("post")
    w1 = k.sb(st, [128, 8, 2 * DFF], BF16, "w1")
    w2 = k.sb(st, [128, 22, D], BF16, "w2")
    w1v = ffn["w_in"].rearrange("(c p) f -> p c f", p=128)
    for c in range(8):
        for hlf in range(4):
            f0 = hlf * 1408
            k.dma("pool", w1[:, c, f0:f0 + 1408], w1v[:, c, f0:f0 + 1408], w=[w1])
    w2v = ffn["w_out"].rearrange("(c p) d -> p c d", p=128)
    for c in range(22):
        k.dma("pool", w2[:, c, :], w2v[:, c, :], w=[w2])
    if pre is not None:
        wo = k.sb(st, [128, 8, D], BF16, "wo")
        wov = pre["w_out"].rearrange("(c p) d -> p c d", p=128)
        for c in range(8):
            k.dma("pool", wo[:, c, :], wov[:, c, :], w=[wo])

    identf = k.sb(st, [128, 128], F32, "identf")
    k.dma("sp", identf[:, :], cfg["ident"], w=[identf])
    xt = [k.sb(st, [128, D], F32, "xt%d" % i) for i in range(4)]
    hT = k.sb(st, [128, 8, 512], BF16, "hT")
    gT = k.sb(st, [128, 22, 512], BF16, "gT")
    sg = k.sb(st, [128, 512], BF16, "sg")
    tmp = k.sb(st, [128, D], F32, "tmp")
    ssq = k.sb(st, [128, 1], F32, "ssq")
    rinv = k.sb(st, [128, 1], F32, "rinv")
    ps_u = [k.ps(st, [128, 512], F32, "psu%d" % i) for i in range(4)]
    ps_y = [k.ps(st, [128, 512], F32, "psy%d" % i) for i in range(2)]
    ps_t = k.ps(st, [128, D], F32, "pst")
    Ac = k.sb(st, [128, 8], F32, "Ac")
    Bc = k.sb(st, [128, 8], F32, "Bc")
    Gc = k.sb(st, [128, 8], F32, "Gc")
    A2c = k.sb(st, [128, 8], F32, "A2c")
    B2c = k.sb(st, [128, 8], F32, "B2c")
    G = k.sb(st, [128, D], F32, "G")
    PG = k.sb(st, [128, D], F32, "PG") if pre is not None else None
    gvec2 = None
    if post is not None and post["kind"] == "final":
        gvec2 = k.sb(st, [128, D], F32, "gvec2")
        k.dma("sp", gvec2[:, :], post["g"].partition_broadcast(128), w=[gvec2])

    def col(ap1d):
        return ap1d.rearrange("(c p) -> p c", p=128)

    def load_cols(dst, ap1d):
        k.dma("sp", dst[:, :], col(ap1d), w=[dst], allow_slow_non_contiguous=True)

    def setup_mod(mr):
        load_cols(Gc, ffn["g"])
        load_cols(Ac, cfg["mod"][mr, ffn["scale"] * D:(ffn["scale"] + 1) * D])
        load_cols(Bc, cfg["mod"][mr, ffn["shift"] * D:(ffn["shift"] + 1) * D])
        k.op("dve", lambda e: e.scalar_tensor_tensor(out=Ac[:, :], in0=Ac[:, :], scalar=1.0, in1=Gc[:, :],
                                                     op0=ALU.add, op1=ALU.mult), r=[Ac, Gc], w=[Ac])
        if post is not None and post["kind"] == "hT":
            load_cols(Gc, post["g"])
            load_cols(A2c, cfg["mod"][mr, post["scale"] * D:(post["scale"] + 1) * D])
            load_cols(B2c, cfg["mod"][mr, post["shift"] * D:(post["shift"] + 1) * D])
            k.op("dve", lambda e: e.scalar_tensor_tensor(out=A2c[:, :], in0=A2c[:, :], scalar=1.0, in1=Gc[:, :],
                                                         op0=ALU.add, op1=ALU.mult), r=[A2c, Gc], w=[A2c])
        k.dma("sp", G[:, :], cfg["mod"][mr, ffn["gate"] * D:(ffn["gate"] + 1) * D].partition_broadcast(128), w=[G])
        k.op("dve", lambda e: e.tensor_scalar(out=G[:, :], in0=G[:, :], scalar1=0.5, scalar2=None, op0=ALU.mult),
             r=[G], w=[G])
        if pre is not None:
            k.dma("sp", PG[:, :], cfg["mod"][mr, pre["gate"] * D:(pre["gate"] + 1) * D].partition_broadcast(128),
                  w=[PG])

    def rms_rinv(xb):
        k.op("act", lambda e: e.activation(out=tmp[:, :], in_=xb[:, :], func=ACT.Square, accum_out=ssq[:, :]),
             r=[xb], w=[tmp, ssq])
        k.op("dve", lambda e: e.tensor_scalar(out=rinv[:, :], in0=ssq[:, :], scalar1=1.0 / D, scalar2=RMS_EPS,
                                              op0=ALU.mult, op1=ALU.add), r=[ssq], w=[rinv])
        k.op("act", lambda e: e.activation(out=rinv[:, :], in_=rinv[:, :], func=ACT.Sqrt), r=[rinv], w=[rinv])
        k.op("dve", lambda e: e.reciprocal(out=rinv[:, :], in_=rinv[:, :]), r=[rinv], w=[rinv])

    def norm_mod_T(xb, A, B, dstT, s):
        rms_rinv(xb)
        k.op("dve", lambda e: e.tensor_scalar(out=tmp[:, :], in0=xb[:, :], scalar1=rinv[:, 0:1], scalar2=None,
                                              op0=ALU.mult), r=[xb, rinv], w=[tmp])
        for c in range(8):
            k.op("pe", lambda e: e.transpose(ps_t[:, c * 128:(c + 1) * 128], tmp[:, c * 128:(c + 1) * 128],
                                             identf[:, :]), r=[tmp, identf], w=[ps_t])
        for c in range(8):
            if c % 2 == 0:
                k.op("dve", lambda e: e.tensor_scalar(out=dstT[:, c, s * 128:(s + 1) * 128],
                                                      in0=ps_t[:, c * 128:(c + 1) * 128], scalar1=A[:, c:c + 1],
                                                      scalar2=B[:, c:c + 1], op0=ALU.mult, op1=ALU.add),
                     r=[ps_t, A, B], w=[dstT])
            else:
                k.op("act", lambda e: e.activation(out=dstT[:, c, s * 128:(s + 1) * 128],
                                                   in_=ps_t[:, c * 128:(c + 1) * 128], func=ACT.Identity,
                                                   scale=A[:, c:c + 1], bias=B[:, c:c + 1]),
                     r=[ps_t, A, B], w=[dstT])

    ui = 0
    yi = 0
    for (tok0, ntok, mr) in cfg["segs"]:
        setup_mod(mr)
        for t0 in range(tok0, tok0 + ntok, 512):
            n = min(512, tok0 + ntok - t0)
            ns = n // 128
            for s in range(ns):
                k.dma("sp", xt[s][:, :], cfg["xin"][t0 + s * 128:t0 + (s + 1) * 128, :], w=[xt[s]])
            if pre is not None:
                for s in range(ns):
                    k.dma("sp", tmp[:, :], pre["o"][t0 + s * 128:t0 + (s + 1) * 128, :], w=[tmp])
                    for c in range(8):
                        k.op("pe", lambda e: e.transpose(ps_t[:, c * 128:(c + 1) * 128], tmp[:, c * 128:(c + 1) * 128],
                                                         identf[:, :]), r=[tmp, identf], w=[ps_t])
                    k.op("act", lambda e: e.copy(out=hT[:, :, s * 128:(s + 1) * 128],
                                                 in_=ps_t[:, :].rearrange("p (c t) -> p c t", c=8)), r=[ps_t], w=[hT])
                for s in range(ns):
                    for hf in range(2):
                        py = ps_y[yi % 2]
                        yi += 1
                        for c in range(8):
                            mm(k, py[:, :], hT[:, c, s * 128:(s + 1) * 128], wo[:, c, hf * 512:(hf + 1) * 512],
                               c == 0, c == 7, r=[hT, wo], w=[py])
                        sl = slice(hf * 512, (hf + 1) * 512)
                        k.op("dve", lambda e, py=py, sl=sl: e.tensor_tensor(out=tmp[:, sl], in0=py[:, :],
                                                                             in1=PG[:, sl], op=ALU.mult),
                             r=[py, PG], w=[tmp])
                        k.op("dve", lambda e, s=s, sl=sl: e.tensor_tensor(out=xt[s][:, sl], in0=xt[s][:, sl],
                                                                            in1=tmp[:, sl], op=ALU.add),
                             r=[xt[s], tmp], w=[xt[s]])
            for s in range(ns):
                norm_mod_T(xt[s], Ac, Bc, hT, s)
            for fc in range(22 if cfg.get("stage", 9) >= 2 else 0):
                p1 = ps_u[ui % 4]
                p2 = ps_u[(ui + 1) % 4]
                ui += 2
                for c in range(8):
                    mm(k, p1[:, 0:n], w1[:, c, fc * 128:(fc + 1) * 128], hT[:, c, 0:n], c == 0, c == 7,
                       r=[w1, hT], w=[p1])
                for c in range(8):
                    mm(k, p2[:, 0:n], w1[:, c, DFF + fc * 128:DFF + (fc + 1) * 128], hT[:, c, 0:n], c == 0, c == 7,
                       r=[w1, hT], w=[p2])
                k.op("act", lambda e, p1=p1: e.activation(out=sg[:, 0:n], in_=p1[:, 0:n], func=ACT.Silu),
                     r=[p1], w=[sg])
                k.op("dve", lambda e, p2=p2, fc=fc: e.tensor_tensor(out=gT[:, fc, 0:n], in0=sg[:, 0:n],
                                                                     in1=p2[:, 0:n], op=ALU.mult),
                     r=[sg, p2], w=[gT])
            for s in range(ns if cfg.get("stage", 9) >= 3 else 0):
                for hf in range(2):
                    py = ps_y[yi % 2]
                    yi += 1
                    for fc in range(22):
                        mm(k, py[:, :], gT[:, fc, s * 128:(s + 1) * 128], w2[:, fc, hf * 512:(hf + 1) * 512],
                           fc == 0, fc == 21, r=[gT, w2], w=[py])
                    sl = slice(hf * 512, (hf + 1) * 512)
                    k.op("dve", lambda e, py=py, sl=sl: e.tensor_tensor(out=tmp[:, sl], in0=py[:, :], in1=G[:, sl],
                                                                         op=ALU.mult), r=[py, G], w=[tmp])
                    k.op("dve", lambda e, s=s, sl=sl: e.tensor_tensor(out=xt[s][:, sl], in0=xt[s][:, sl],
                                                                        in1=tmp[:, sl], op=ALU.add),
                         r=[xt[s], tmp], w=[xt[s]])
            for s in range(ns):
                r0 = t0 + s * 128
                if post is not None and post["kind"] == "final":
                    if mr == 0:
                        rms_rinv(xt[s])
                        k.op("dve", lambda e, s=s: e.scalar_tensor_tensor(out=tmp[:, :], in0=xt[s][:, :],
                                                                           scalar=rinv[:, 0:1], in1=gvec2[:, :],
                                                                           op0=ALU.mult, op1=ALU.mult),
                             r=[xt[s], rinv, gvec2], w=[tmp])
                        k.dma("sp", post["out"][r0:r0 + 128, :], tmp[:, :], r=[tmp])
                    continue
                k.dma("sp", cfg["xout"][r0:r0 + 128, :], xt[s][:, :], r=[xt[s]])
                if post is not None and post["kind"] == "hT":
                    norm_mod_T(xt[s], A2c, B2c, hT, s)
            if post is not None and post["kind"] == "hT":
                for c in range(8):
                    k.dma("sp", post["hT"][c * 128:(c + 1) * 128, t0:t0 + n], hT[:, c, 0:n], r=[hT])


CH = 64
DECAY_SCALE = 0.6065306597126334


def projphase(k, st, cfg):
    S = cfg["S"]
    C = cfg["C"]
    Wt = cfg["W"]
    T, CTX = cfg["T"], cfg["CTX"]
    hTd = S["hT"]
    wabc = k.sb(st, [128, 8, 2304], BF16, "wabc")
    wgd = k.sb(st, [128, 8, 128], BF16, "wgd")
    wsw = k.sb(st, [128, 8, 1280], BF16, "wsw")
    wv = Wt["w_in"].rearrange("(c p) f -> p c f", p=128)
    for c in range(8):
        k.ld("pool", wabc.v[:, c, 0:1152], wv[:, c, 0:1152])
        k.ld("pool", wabc.v[:, c, 1152:2304], wv[:, c, 1152:2304])
        k.ld("pool", wgd.v[:, c, :], wv[:, c, 3072:3200])
        k.ld("pool", wsw.v[:, c, :], Wt["w_sw"].rearrange("(c p) f -> p c f", p=128)[:, c, :])
    wda = [k.sb(st, [128, 8, 896], BF16, "wda%d" % d) for d in range(2)]
    wdb = [k.sb(st, [128, 8, 896], BF16, "wdb%d" % d) for d in range(2)]
    with contextlib.ExitStack() as st2:
        wst = k.sb(st2, [128, 8, 896], F32, "wst")
        mub = k.sb(st2, [128, 896], F32, "mub")
        wtm = k.sb(st2, [128, 8, 896], F32, "wtm")
        for d in range(2):
            for c in range(8):
                k.ld("sp", wst.v[:, c, :], Wt["w_d"][d].rearrange("(c p) f -> p c f", p=128)[:, c, :])
            k.ld("sp", mub.v[:, :], Wt["mu"][d].partition_broadcast(128))
            for c in range(8):
                k.tt("dve", wtm.v[:, c, :], wst.v[:, c, :], mub.v[:, :], ALU.mult)
            k.cp("act", wda[d].v[:, :, :], wtm.v[:, :, :])
            k.tt("dve", wdb[d].v[:, :, :], wst.v[:, :, :], wtm.v[:, :, :], ALU.subtract)
        k.barrier()
    pc = k.sb(st, [128, 24], F32, "pcols")
    k.ld("sp", pc.v[:, :], Wt["pcols"])
    oma = k.sb(st, [128, 2], F32, "oma")
    k.ts("dve", oma.v[:, :], pc.v[:, 14:16], -1.0, 1.0, ALU.mult, ALU.add)
    identf = k.sb(st, [128, 128], F32, "identf")
    k.ld("sp", identf.v[:, :], C["ident"])
    bd = k.sb(st, [128, 128], F32, "bd")
    k.ld("sp", bd.v[:, :], C["bd64"])
    e2 = k.sb(st, [128, 2], F32, "e2")
    k.ld("sp", e2.v[:, :], C["e2"])
    rtab = k.sb(st, [128, 2, 2, 3, CH], F32, "rtab")
    k.ld("sp", rtab.v[:, :, :, :, :], C["ret_tab"])
    w2s = k.sb(st, [64, 2, 256], BF16, "w2s")
    a2s = k.sb(st, [64, 2, 256], BF16, "a2s")
    g2s = k.sb(st, [128, 256], BF16, "g2s")
    for d in range(2):
        k.ld("pool", w2s.v[:, d, :], Wt["w2"][d])
        k.ld("pool", a2s.v[:, d, :], Wt["a2"][d])
    k.ld("pool", g2s.v[:, :], Wt["g2"])

    NT = 512
    hTt = k.sb(st, [128, 8, NT + 2], BF16, "hTt")
    ca = k.sb(st, [128, NT], F32, "ca")
    sa = k.sb(st, [128, NT], F32, "sa")
    cs_ = k.sb(st, [128, NT], F32, "cs")
    ss_ = k.sb(st, [128, NT], F32, "ss")
    gca = k.sb(st, [128, 2, NT], F32, "gca")
    gsa = k.sb(st, [128, 2, NT], F32, "gsa")
    F = [k.sb(st, [128, NT], F32, "f%d" % i) for i in range(12)]
    ob = k.sb(st, [128, NT], BF16, "ob")
    tb = k.sb(st, [128, 512], F32, "tb")
    tbb = k.sb(st, [128, 512], BF16, "tbb")
    thb = k.sb(st, [64, NT], BF16, "thb")
    alb = k.sb(st, [64, NT], BF16, "alb")
    sgb = k.sb(st, [128, NT], BF16, "sgb")
    bs4 = k.sb(st, [128, 4, 4], F32, "bs4")
    pcx = k.sb(st, [128, NT // CH], F32, "pcx")
    PS = [k.ps(st, [128, 512], F32, "pp%d" % i) for i in range(8)]
    pi = [0]

    def nps():
        p = PS[pi[0] % 8]
        pi[0] += 1
        return p

    def chain(p, wlist, M, n, col0=1):
        tot = len(wlist) * 8
        i = 0
        for (wb, c0, sh) in wlist:
            for c in range(8):
                k.mm(p.v[0:M, 0:n], wb.v[:, c, c0:c0 + M], hTt.v[:, c, col0 + sh:col0 + sh + n], i == 0, i == tot - 1)
                i += 1

    def chain_tok(p, wlist, ncols, s):
        tot = len(wlist) * 8
        i = 0
        for (wb, c0, sh) in wlist:
            for c in range(8):
                k.mm(p.v[:, 0:ncols], hTt.v[:, c, 1 + sh + s * 128:1 + sh + (s + 1) * 128], wb.v[:, c, c0:c0 + ncols],
                     i == 0, i == tot - 1)
                i += 1

    def rope_out(P, Psw, ctab, stab, dst, n):
        k.tt("dve", F[10].v[:, 0:n], P.v[:, 0:n], ctab, ALU.mult)
        k.tt("dve", F[11].v[:, 0:n], Psw.v[:, 0:n], stab, ALU.mult)
        k.tt("dve", dst, F[10].v[:, 0:n], F[11].v[:, 0:n], ALU.add)

    segs = [(0, T, True), (T, CTX, False)]
    for (seg0, seglen, latent) in segs:
        for t0 in range(seg0, seg0 + seglen, NT):
            n = min(NT, seg0 + seglen - t0)
            ns = n // 128
            nch = n // CH
            lo = t0 - 1 if t0 > seg0 else t0
            hi = t0 + n + 1 if t0 + n < seg0 + seglen else t0 + n
            if lo == t0:
                k.memset("dve", hTt.v[:, :, 0:1], 0.0)
            if hi == t0 + n:
                k.memset("dve", hTt.v[:, :, n + 1:n + 2], 0.0)
            for c in range(8):
                k.ld("sp", hTt.v[:, c, 1 - (t0 - lo):1 + n + (hi - t0 - n)], hTd[c * 128:(c + 1) * 128, lo:hi])
            if latent:
                k.ld("sp", ca.v[:, 0:n], C["ropeA_c"][:, t0:t0 + n])
                k.ld("sp", sa.v[:, 0:n], C["ropeA_s"][:, t0:t0 + n])
                k.ld("sp", cs_.v[:, 0:n], C["ropeS_c"][:, t0:t0 + n])
                k.ld("sp", ss_.v[:, 0:n], C["ropeS_s"][:, t0:t0 + n])
                for qk in range(2):
                    k.ts("dve", gca.v[:, qk, 0:n], ca.v[:, 0:n], pc.v[:, 2 * qk:2 * qk + 1], None, ALU.mult)
                    k.ts("dve", gsa.v[:, qk, 0:n], sa.v[:, 0:n], pc.v[:, 2 * qk + 1:2 * qk + 2], None, ALU.mult)
            for grp, base, swb, dq, dk_, dv_ in (("A", 0, 0, S["QA"], S["KA"], S["VA"]),
                                                ("B", 512, 384, S["QB"], S["KB"], S["VB"])):
                for j in range(3):
                    P, Psw = nps(), nps()
                    chain(P, [(wabc, base + j * 128, 0)], 128, n)
                    if latent:
                        chain(Psw, [(wsw, swb + j * 128, 0)], 128, n)
                    qk = 0 if j < 2 else 1
                    if grp == "B":
                        k.act(F[0].v[:, 0:n], P.v[:, 0:n], ACT.Square)
                        pss = nps()
                        k.mm(pss.v[:, 0:n], bd.v[:, :], F[0].v[:, 0:n])
                        k.ts("dve", F[1].v[:, 0:n], pss.v[:, 0:n], 1.0 / 64, RMS_EPS, ALU.mult, ALU.add)
                        k.act(F[1].v[:, 0:n], F[1].v[:, 0:n], ACT.Sqrt)
                        k.recip(F[1].v[:, 0:n], F[1].v[:, 0:n])
                        k.tt("dve", F[2].v[:, 0:n], P.v[:, 0:n], F[1].v[:, 0:n], ALU.mult)
                        if latent:
                            k.tt("dve", F[3].v[:, 0:n], Psw.v[:, 0:n], F[1].v[:, 0:n], ALU.mult)
                            rope_out(F[2], F[3], gca.v[:, qk, 0:n], gsa.v[:, qk, 0:n], ob.v[:, 0:n], n)
                        else:
                            k.ts("dve", ob.v[:, 0:n], F[2].v[:, 0:n], pc.v[:, 2 * qk:2 * qk + 1], None, ALU.mult)
                    else:
                        if latent:
                            rope_out(P, Psw, ca.v[:, 0:n], sa.v[:, 0:n], ob.v[:, 0:n], n)
                        else:
                            k.cp("act", ob.v[:, 0:n], P.v[:, 0:n])
                    dst = dq[j * 128:(j + 1) * 128, t0:t0 + n] if j < 2 else dk_[:, t0:t0 + n]
                    k.stv("sp", dst, ob.v[:, 0:n])
                for s in range(ns):
                    P = nps()
                    chain_tok(P, [(wabc, base + 384, 0)], 128, s)
                    k.cp("act", tbb.v[:, 0:128], P.v[:, 0:128])
                    k.stv("sp", dv_[t0 + s * 128:t0 + (s + 1) * 128, :], tbb.v[:, 0:128])
            for j in range(4):
                P, Psw = nps(), nps()
                chain(P, [(wabc, 1024 + j * 128, 0)], 128, n)
                g = j % 2
                if latent:
                    chain(Psw, [(wsw, 768 + j * 128, 0)], 128, n)
                    rope_out(P, Psw, cs_.v[:, 0:n], ss_.v[:, 0:n], F[0].v[:, 0:n], n)
                else:
                    k.cp("act", F[0].v[:, 0:n], P.v[:, 0:n])
                x3 = F[0].v[:, 0:n].rr("p (c i) -> p c i", i=CH)
                for d in range(2):
                    if j < 2:
                        k.tt("dve", F[1].v[:, 0:n].rr("p (c i) -> p c i", i=CH), x3,
                             rtab.v[:, g, d, 0:1, :].bc([128, nch, CH]), ALU.mult)
                        k.stv("sp", S["C_Rt"][d, g * 128:(g + 1) * 128, t0:t0 + n], F[1].v[:, 0:n])
                    else:
                        k.tt("dve", F[1].v[:, 0:n].rr("p (c i) -> p c i", i=CH), x3,
                             rtab.v[:, g, d, 1:2, :].bc([128, nch, CH]), ALU.mult)
                        k.stv("sp", S["C_Kt"][d, g * 128:(g + 1) * 128, t0:t0 + n], F[1].v[:, 0:n])
                        k.tt("dve", F[2].v[:, 0:n].rr("p (c i) -> p c i", i=CH), x3,
                             rtab.v[:, g, d, 2:3, :].bc([128, nch, CH]), ALU.mult)
                        for s in range(ns):
                            pt = nps()
                            k.tr(pt.v[:, 0:128], F[2].v[:, s * 128:(s + 1) * 128], identf.v[:, :])
                            k.cp("act", tb.v[:, 0:128], pt.v[:, 0:128])
                            k.stv("sp", S["C_Kh"][d, t0 + s * 128:t0 + (s + 1) * 128, g * 128:(g + 1) * 128],
                                  tb.v[:, 0:128])
            for s in range(ns):
                P = nps()
                chain_tok(P, [(wabc, 1536, 0)], 256, s)
                k.cp("act", tb.v[:, 0:256], P.v[:, 0:256])
                k.stv("sp", S["C_V"][t0 + s * 128:t0 + (s + 1) * 128, :], tb.v[:, 0:256])
                P = nps()
                chain_tok(P, [(wabc, 1792, 0)], 512, s)
                k.act(tb.v[:, 0:512], P.v[:, 0:512], ACT.Silu)
                k.stv("sp", S["C_G"][t0 + s * 128:t0 + (s + 1) * 128, :], tb.v[:, 0:512])
            P = nps()
            chain(P, [(wgd, 0, 0)], 128, n)
            k.act(sgb.v[:, 0:n], P.v[:, 0:n], ACT.Sigmoid)
            for s in range(ns):
                P = nps()
                k.mm(P.v[:, 0:256], sgb.v[:, s * 128:(s + 1) * 128], g2s.v[:, :])
                k.cp("act", tb.v[:, 0:256], P.v[:, 0:256])
                k.stv("sp", S["D_G"][t0 + s * 128:t0 + (s + 1) * 128, :], tb.v[:, 0:256])
            for d in range(2):
                sh = -1 if d == 0 else 1
                wl = lambda c0: [(wdb[d], c0, 0), (wda[d], c0, sh)]
                P = nps()
                chain(P, wl(768), 64, n)
                k.act(thb.v[:, 0:n], P.v[0:64, 0:n], ACT.Tanh)
                P = nps()
                chain(P, wl(832), 64, n)
                k.cp("act", alb.v[:, 0:n], P.v[0:64, 0:n])
                for s in range(ns):
                    P = nps()
                    chain_tok(P, wl(512), 256, s)
                    k.cp("act", tb.v[:, 0:256], P.v[:, 0:256])
                    k.stv("sp", S["D_V"][d, t0 + s * 128:t0 + (s + 1) * 128, :], tb.v[:, 0:256])
                for g in range(2):
                    r_, k_, lw, cs, a_, kk, e_, t1, t2 = F[0], F[1], F[2], F[3], F[4], F[5], F[6], F[7], F[8]
                    gc = slice(g, g + 1)
                    P = nps()
                    chain(P, wl(g * 128), 128, n)
                    k.cp("act", r_.v[:, 0:n], P.v[:, 0:n])
                    P = nps()
                    chain(P, wl(256 + g * 128), 128, n)
                    k.cp("act", k_.v[:, 0:n], P.v[:, 0:n])
                    P = nps()
                    k.mm(P.v[:, 0:n], w2s.v[:, d, g * 128:(g + 1) * 128], thb.v[:, 0:n])
                    k.act(lw.v[:, 0:n], P.v[:, 0:n], ACT.Sigmoid, bias=pc.v[:, 4 + 2 * d + g:5 + 2 * d + g])
                    k.ts("dve", lw.v[:, 0:n], lw.v[:, 0:n], -DECAY_SCALE, None, ALU.mult)
                    P = nps()
                    k.mm(P.v[:, 0:n], a2s.v[:, d, g * 128:(g + 1) * 128], alb.v[:, 0:n])
                    k.act(a_.v[:, 0:n], P.v[:, 0:n], ACT.Sigmoid, bias=pc.v[:, 8 + 2 * d + g:9 + 2 * d + g])
                    k.ts("dve", kk.v[:, 0:n], k_.v[:, 0:n], pc.v[:, 12 + g:13 + g], None, ALU.mult)
                    k.act(t1.v[:, 0:n], kk.v[:, 0:n], ACT.Square)
                    P = nps()
                    k.mm(P.v[:, 0:n], bd.v[:, :], t1.v[:, 0:n])
                    k.act(t1.v[:, 0:n], P.v[:, 0:n], ACT.Sqrt)
                    k.ts("dve", t1.v[:, 0:n], t1.v[:, 0:n], 1e-12, None, ALU.max)
                    k.recip(t1.v[:, 0:n], t1.v[:, 0:n])
                    k.tt("dve", kk.v[:, 0:n], kk.v[:, 0:n], t1.v[:, 0:n], ALU.mult)
                    k.ts("dve", t1.v[:, 0:n], a_.v[:, 0:n], pc.v[:, 14 + g:15 + g], oma.v[:, gc], ALU.mult, ALU.add)
                    k.tt("dve", k_.v[:, 0:n], k_.v[:, 0:n], t1.v[:, 0:n], ALU.mult)
                    k.stt("dve", t1.v[:, 0:n], r_.v[:, 0:n], pc.v[:, 16 + 2 * d + g:17 + 2 * d + g], k_.v[:, 0:n],
                          ALU.mult, ALU.mult)
                    for s in range(ns):
                        P = nps()
                        k.mm(P.v[:, 0:2], t1.v[:, s * 128:(s + 1) * 128], e2.v[:, :])
                        k.cp("act", bs4.v[:, s, 2 * g:2 * g + 2], P.v[:, 0:2])
                    k.tt("dve", a_.v[:, 0:n], a_.v[:, 0:n], kk.v[:, 0:n], ALU.mult)
                    src, dst = lw, cs
                    k.cp("act", t2.v[:, 0:n], lw.v[:, 0:n])
                    src = t2
                    for stp in (1, 2, 4, 8, 16, 32):
                        s3 = src.v[:, 0:n].rr("p (c i) -> p c i", i=CH)
                        d3 = dst.v[:, 0:n].rr("p (c i) -> p c i", i=CH)
                        if d == 0:
                            k.tt("dve", d3[:, :, stp:], s3[:, :, stp:], s3[:, :, :CH - stp], ALU.add)
                            k.cp("act", d3[:, :, :stp], s3[:, :, :stp])
                        else:
                            k.tt("dve", d3[:, :, :CH - stp], s3[:, :, :CH - stp], s3[:, :, stp:], ALU.add)
                            k.cp("act", d3[:, :, CH - stp:], s3[:, :, CH - stp:])
                        src, dst = dst, src
                    cs = src
                    spare = dst
                    cs3 = cs.v[:, 0:n].rr("p (c i) -> p c i", i=CH)
                    tot3 = cs3[:, :, CH - 1:CH] if d == 0 else cs3[:, :, 0:1]
                    k.act(e_.v[:, 0:n], cs.v[:, 0:n], ACT.Exp)
                    k.tt("dve", t1.v[:, 0:n], r_.v[:, 0:n], e_.v[:, 0:n], ALU.mult)
                    k.stv("sp", S["D_Rt"][d, g * 128:(g + 1) * 128, t0:t0 + n], t1.v[:, 0:n])
                    k.tt("dve", e_.v[:, 0:n], cs.v[:, 0:n], lw.v[:, 0:n], ALU.subtract)
                    k.act(e_.v[:, 0:n], e_.v[:, 0:n], ACT.Exp)
                    k.stt("dve", t1.v[:, 0:n], kk.v[:, 0:n], -1.0, e_.v[:, 0:n], ALU.mult, ALU.mult)
                    k.stv("sp", S["D_At"][d, g * 128:(g + 1) * 128, t0:t0 + n], t1.v[:, 0:n])
                    for s in range(ns):
                        pt = nps()
                        k.tr(pt.v[:, 0:128], t1.v[:, s * 128:(s + 1) * 128], identf.v[:, :])
                        k.cp("act", tb.v[:, 0:128], pt.v[:, 0:128])
                        k.stv("sp", S["D_Atok"][d, t0 + s * 128:t0 + (s + 1) * 128, g * 128:(g + 1) * 128],
                              tb.v[:, 0:128])
                    k.act(e_.v[:, 0:n], cs.v[:, 0:n], ACT.Exp, scale=-1.0)
                    k.tt("dve", t1.v[:, 0:n], a_.v[:, 0:n], e_.v[:, 0:n], ALU.mult)
                    k.stv("sp", S["D_Bt"][d, g * 128:(g + 1) * 128, t0:t0 + n], t1.v[:, 0:n])
                    k.tt("dve", t1.v[:, 0:n], k_.v[:, 0:n], e_.v[:, 0:n], ALU.mult)
                    k.stv("sp", S["D_Kt"][d, g * 128:(g + 1) * 128, t0:t0 + n], t1.v[:, 0:n])
                    k.act(pcx.v[:, 0:nch], tot3.rr("p c o -> p (c o)"), ACT.Exp)
                    k.stv("sp", S["D_PC"][d, g * 128:(g + 1) * 128, t0 // CH:t0 // CH + nch], pcx.v[:, 0:nch])
                    k.tt("dve", e_.v[:, 0:n].rr("p (c i) -> p c i", i=CH), tot3.bc([128, nch, CH]), cs3, ALU.subtract)
                    k.act(e_.v[:, 0:n], e_.v[:, 0:n], ACT.Exp)
                    for (srcb, dstd) in ((a_, S["D_Bh"]), (k_, S["D_Kh"])):
                        k.tt("dve", t1.v[:, 0:n], srcb.v[:, 0:n], e_.v[:, 0:n], ALU.mult)
                        for s in range(ns):
                            pt = nps()
                            k.tr(pt.v[:, 0:128], t1.v[:, s * 128:(s + 1) * 128], identf.v[:, :])
                            k.cp("act", tb.v[:, 0:128], pt.v[:, 0:128])
                            k.stv("sp", dstd[d, t0 + s * 128:t0 + (s + 1) * 128, g * 128:(g + 1) * 128], tb.v[:, 0:128])
                for s in range(ns):
                    k.stv("sp", S["D_BS"][d, t0 + s * 128:t0 + (s + 1) * 128, :], bs4.v[:, s, :])


def attnphase(k, st, cfg):
    S, C, Wt = cfg["S"], cfg["C"], cfg["W"]
    T, CTX = cfg["T"], cfg["CTX"]
    TT = T + CTX
    NKC = TT // 128
    need_ctx = cfg["need_ctx"]
    kT = k.sb(st, [64, TT], BF16, "kT")
    va = k.sb(st, [128, NKC, 65], BF16, "va")
    qT = [k.sb(st, [64, 512], BF16, "qT%d" % i) for i in range(2)]
    pT = [k.sb(st, [128, 512], BF16, "pT%d" % i) for i in range(3)]
    ot = [k.sb(st, [128, 64], F32, "ot%d" % i) for i in range(2)]
    rc = k.sb(st, [128, 1], F32, "rc")
    esk = k.sb(st, [128, 4], F32, "esk")
    mprev = k.sb(st, [128, 128], BF16, "mprev")
    mnext = k.sb(st, [128, 128], BF16, "mnext")
    k.ld("pool", mprev.v[:, :], C["mprev"])
    k.ld("pool", mnext.v[:, :], C["mnext"])
    k.ld("sp", esk.v[:, :], Wt["sink"].partition_broadcast(128))
    k.act(esk.v[:, :], esk.v[:, :], ACT.Exp)
    ps_s = [k.ps(st, [128, 512], F32, "pss%d" % i) for i in range(3)]
    ps_o = [k.ps(st, [128, 512], F32, "pso%d" % i) for i in range(4)]
    cnt = dict(s=0, q=0, o=0, p=0)
    scale = 64 ** -0.5

    def load_kv(kd, vd, g):
        k.ld("sp", kT.v[:, :], kd[g * 64:(g + 1) * 64, :])
        k.memset("dve", va.v[:, :, 64:65], 1.0)
        k.ld("sp", va.v[:, :, 0:64], vd[:, g * 64:(g + 1) * 64].rearrange("(c p) j -> p c j", p=128))

    def finish(acc, nq_sub, t0, col, sinkcol):
        for s in range(nq_sub):
            o = ot[cnt["o"] % 2]
            cnt["o"] += 1
            if sinkcol is None:
                k.recip(rc.v[:, :], acc[s].v[:, 64:65])
            else:
                k.tt("dve", rc.v[:, :], acc[s].v[:, 64:65], esk.v[:, sinkcol:sinkcol + 1], ALU.add)
                k.recip(rc.v[:, :], rc.v[:, :])
            k.ts("dve", o.v[:, :], acc[s].v[:, 0:64], rc.v[:, 0:1], None, ALU.mult)
            k.stv("sp", S["O"][t0 + s * 128:t0 + (s + 1) * 128, col:col + 64], o.v[:, :])

    def block(qd, h, t0, nq, kcs, col, sinkcol, masks=None):
        q = qT[cnt["q"] % 2]
        cnt["q"] += 1
        k.ld("sp", q.v[:, 0:nq], qd[h * 64:(h + 1) * 64, t0:t0 + nq])
        nsub = nq // 128
        for i, kc in enumerate(kcs):
            ps = ps_s[cnt["s"] % 3]
            cnt["s"] += 1
            k.mm(ps.v[:, 0:nq], kT.v[:, kc * 128:(kc + 1) * 128], q.v[:, 0:nq])
            p = pT[cnt["p"] % 3]
            cnt["p"] += 1
            k.act(p.v[:, 0:nq], ps.v[:, 0:nq], ACT.Exp, scale=scale)
            if masks is not None and masks[i] is not None:
                k.tt("dve", p.v[:, 0:nq], p.v[:, 0:nq], masks[i].v[:, :], ALU.mult)
            for s in range(nsub):
                k.mm(ps_o[s].v[:, 0:65], p.v[:, s * 128:(s + 1) * 128], va.v[:, kc, :], i == 0, i == len(kcs) - 1)
        finish(ps_o, nsub, t0, col, sinkcol)

    for g in range(2):
        load_kv(S["KB"], S["VB"], g)
        for h in (2 * g, 2 * g + 1):
            for t0 in range(0, T, 512):
                block(S["QB"], h, t0, min(512, T - t0), list(range(NKC)), 256 + h * 64, None)
            if need_ctx:
                block(S["QB"], h, T, CTX, list(range(T // 128, NKC)), 256 + h * 64, None)
    nb = T // 128
    cchunks = list(range(T // 128, NKC))
    for g in range(2):
        load_kv(S["KA"], S["VA"], g)
        for h in (2 * g, 2 * g + 1):
            for b in range(nb):
                kcs, ms = [], []
                if b > 0:
                    kcs.append(b - 1)
                    ms.append(mprev)
                kcs.append(b)
                ms.append(None)
                if b < nb - 1:
                    kcs.append(b + 1)
                    ms.append(mnext)
                kcs += cchunks
                ms += [None] * len(cchunks)
                block(S["QA"], h, b * 128, 128, kcs, h * 64, h, ms)
            if need_ctx:
                block(S["QA"], h, T, CTX, cchunks, h * 64, h)


def scanphase(k, st, cfg):
    S, C = cfg["S"], cfg["C"]
    T, CTX = cfg["T"], cfg["CTX"]
    TT = T + CTX
    dplr = cfg["dplr"]
    pre = "D_" if dplr else "C_"
    GT_ = 4 * CH
    NCHT = TT // CH
    f3 = lambda b: b.v[:, :, :]
    chm = lambda nm: k.sb(st, [64, 8, GT_], F32, nm)
    tkm = lambda nm: k.sb(st, [64, 4, 8, 64], F32, nm)
    RT = [chm("RT%d" % i) for i in range(2)]
    KT = [chm("KT%d" % i) for i in range(2)]
    KH = [tkm("KH%d" % i) for i in range(2)]
    VV = [tkm("VV%d" % i) for i in range(2)]
    if dplr:
        AT = [chm("AT%d" % i) for i in range(2)]
        BT = [chm("BT%d" % i) for i in range(2)]
        BH = [tkm("BH%d" % i) for i in range(2)]
        AK = [tkm("AK%d" % i) for i in range(2)]
        Mst = k.sb(st, [64, 8, 64], F32, "Mst")
        Mts = k.sb(st, [64, 8, 64], F32, "Mts")
        k.ld("sp", Mst.v[:, :, :], C["m_st"].rearrange("p (q t) -> p q t", q=8))
        k.ld("sp", Mts.v[:, :, :], C["m_ts"].rearrange("p (q t) -> p q t", q=8))
        Ls = [k.sb(st, [64, 8, 64], F32, "L%d" % i) for i in range(5)]
        Ns = [k.sb(st, [64, 8, 64], F32, "N%d" % i) for i in range(6)]
        AakT = k.sb(st, [64, 8, 64], F32, "AakT")
        ArbT = k.sb(st, [64, 8, 64], F32, "ArbT")
        X = [k.sb(st, [64, 8, 128], F32, "X%d" % i) for i in range(2)]
        Qe = k.sb(st, [64, 8, 64], F32, "Qe")
        PCt = k.sb(st, [64, 8, NCHT], F32, "PCt")
        for d in range(2):
            k.ld("sp", PCt.v[:, d * 4:(d + 1) * 4, :], S["D_PC"][d].rearrange("(h j) c -> j h c", j=64))
    else:
        PCr = k.sb(st, [64, 8], F32, "PCr")
        k.ld("sp", PCr.v[:, :], C["ret_pc"])
    Min = k.sb(st, [64, 8, 64], F32, "Min")
    k.ld("sp", Min.v[:, :, :], C["m_in"].rearrange("p (q t) -> p q t", q=8))
    id8 = k.sb(st, [64, 8, 64], F32, "id8")
    k.ld("sp", id8.v[:, :, :], C["id8"].rearrange("p (q t) -> p q t", q=8))
    ArkT = k.sb(st, [64, 8, 64], F32, "ArkT")
    GTt = k.sb(st, [64, 8, 64], F32, "GTt")
    ST = [k.sb(st, [64, 8, 64], F32, "ST%d" % i) for i in range(2)]
    Yt = [k.sb(st, [64, 8, 64], F32, "Yt%d" % i) for i in range(2)]
    psA = [k.ps(st, [64, 512], F32, "psA%d" % i) for i in range(4)]
    psX = k.ps(st, [64, 1024], F32, "psX")
    psY = [k.ps(st, [64, 512], F32, "psY%d" % i) for i in range(2)]
    cnt = dict(a=0, y=0, e=0)
    p3 = lambda p: p.v[:, :].rr("p (q t) -> p q t", q=8)

    def npa():
        p = psA[cnt["a"] % 4]
        cnt["a"] += 1
        return p

    def ev():
        cnt["e"] += 1
        return "act" if cnt["e"] % 2 else "dve"

    def prod(dst, lhs, rhs, mask):
        p = npa()
        for q in range(8):
            k.mm(p3(p)[:, q, :], lhs(q), rhs(q))
        if mask is None:
            k.cp(ev(), f3(dst), p3(p))
        else:
            k.tt("dve", f3(dst), p3(p), f3(mask), ALU.mult)

    k.memset("dve", f3(ST[0]), 0.0)
    ngl = T // GT_
    order = {0: [ngl] + list(range(ngl)), 1: [ngl] + list(range(ngl - 1, -1, -1))}
    step = 0
    for gi in range(ngl + 1):
        rb = gi % 2
        grp = {d: order[d][gi] for d in range(2)}
        for d in range(2):
            t0 = grp[d] * GT_
            qs = slice(d * 4, (d + 1) * 4)
            chv = lambda nm: S[pre + nm][d, :, t0:t0 + GT_].rearrange("(h j) t -> j h t", j=64)
            tkv = lambda ap: ap[t0:t0 + GT_, :].rearrange("(n t) (h j) -> t n h j", t=CH, j=64)
            k.ld("sp", RT[rb].v[:, qs, :], chv("Rt"))
            k.ld("sp", KT[rb].v[:, qs, :], chv("Kt"))
            k.ld("sp", KH[rb].v[:, :, qs, :], tkv(S[pre + "Kh"][d]))
            k.ld("sp", VV[rb].v[:, :, qs, :], tkv(S["D_V"][d] if dplr else S["C_V"]))
            if dplr:
                k.ld("sp", AT[rb].v[:, qs, :], chv("At"))
                k.ld("sp", BT[rb].v[:, qs, :], chv("Bt"))
                k.ld("sp", BH[rb].v[:, :, qs, :], tkv(S["D_Bh"][d]))
                k.ld("sp", AK[rb].v[:, :, qs, :], tkv(S["D_Atok"][d]))
        for ci in range(4):
            cl = {0: ci, 1: 3 - ci}
            sl = {d: slice(cl[d] * CH, (cl[d] + 1) * CH) for d in range(2)}
            dq = lambda q: q // 4
            chs = lambda buf: (lambda q: buf[rb].v[:, q, sl[dq(q)]])
            tks = lambda buf: (lambda q: buf[rb].v[:, cl[dq(q)], q, :])
            rt, kt, kh, vv = chs(RT), chs(KT), tks(KH), tks(VV)
            Sc, Sn = ST[step % 2], ST[(step + 1) % 2]
            prod(ArkT, kt, rt, Min)
            if dplr:
                at, bt, bh, ak = chs(AT), chs(BT), tks(BH), tks(AK)
                prod(Ns[0], bt, at, Mst)
                prod(Ls[0], at, bt, Mts)
                prod(AakT, kt, at, Mst)
                prod(ArbT, bt, rt, Min)
                for lv in range(5):
                    if lv < 4:
                        prod(Ls[lv + 1], lambda q: Ns[lv].v[:, q, :], lambda q: Ls[lv].v[:, q, :], None)
                    prod(Ns[lv + 1], lambda q: Ls[lv].v[:, q, :], lambda q: Ns[lv].v[:, q, :], None)
                xc, xn = X[0], X[1]
                for d in range(2):
                    k.cp("act", xc.v[:, d * 4:(d + 1) * 4, 0:64], AK[rb].v[:, cl[d], d * 4:(d + 1) * 4, :])
                p = npa()
                for q in range(8):
                    k.mm(p3(p)[:, q, :], AakT.v[:, q, :], vv(q))
                k.cp("dve", xc.v[:, :, 64:128], p3(p))
                for lv in range(5, -1, -1):
                    px = psX.v[:, :].rr("p (q c) -> p q c", q=8)
                    for q in range(8):
                        k.mm(px[:, q, :], Ns[lv].v[:, q, :], xc.v[:, q, :])
                    k.tt("dve", f3(xn), f3(xc), px, ALU.add)
                    xc, xn = xn, xc
                wq = lambda q: xc.v[:, q, 0:64]
                uv = lambda q: xc.v[:, q, 64:128]
                p = npa()
                for q in range(8):
                    k.mm(p3(p)[:, q, :], wq(q), ArbT.v[:, q, :])
                for d in range(2):
                    k.tt("dve", Qe.v[:, d * 4:(d + 1) * 4, :], p3(p)[:, d * 4:(d + 1) * 4, :],
                         RT[rb].v[:, d * 4:(d + 1) * 4, sl[d]], ALU.add)
                qe = lambda q: Qe.v[:, q, :]
                p = npa()
                for q in range(8):
                    k.mm(p3(p)[:, q, :], wq(q), bh(q))
                for d in range(2):
                    c_abs = grp[d] * 4 + cl[d]
                    k.tt("dve", GTt.v[:, d * 4:(d + 1) * 4, :], id8.v[:, d * 4:(d + 1) * 4, :],
                         PCt.v[:, d * 4:(d + 1) * 4, c_abs:c_abs + 1].bc([64, 4, 64]), ALU.mult)
                k.tt("dve", f3(GTt), f3(GTt), p3(p), ALU.add)
            else:
                qe = rt
                if step == 0:
                    k.tt("dve", f3(GTt), f3(id8), PCr.v[:, :].rr("p (q o) -> p q o", o=1).bc([64, 8, 64]), ALU.mult)
            py = psY[cnt["y"] % 2]
            cnt["y"] += 1
            for q in range(8):
                k.mm(p3(py)[:, q, :], ArkT.v[:, q, :], vv(q), True, False)
                if dplr:
                    k.mm(p3(py)[:, q, :], ArbT.v[:, q, :], uv(q), False, False)
                k.mm(p3(py)[:, q, :], qe(q), Sc.v[:, q, :], False, True)
            yt = Yt[step % 2]
            k.cp("act", f3(yt), p3(py))
            for d in range(2):
                tt0 = grp[d] * GT_ + cl[d] * CH
                k.stv("sp", S[pre + "Y"][d, tt0:tt0 + CH, :].rearrange("t (h i) -> t h i", i=64),
                      yt.v[:, d * 4:(d + 1) * 4, :])
            p = npa()
            for q in range(8):
                k.mm(p3(p)[:, q, :], kh(q), vv(q), True, False)
                if dplr:
                    k.mm(p3(p)[:, q, :], bh(q), uv(q), False, False)
                k.mm(p3(p)[:, q, :], GTt.v[:, q, :], Sc.v[:, q, :], False, True)
            k.cp("dve", f3(Sn), p3(p))
            step += 1


GN_EPS = 64e-5


def postphase(k, st, cfg):
    S, Wt = cfg["S"], cfg["W"]
    TT = cfg["T"] + cfg["CTX"]
    row = lambda nm, ap: (lambda t: (k.ld("sp", t.v[:, :], ap.partition_broadcast(128)), t)[1])(k.sb(st, [128, 256], F32, nm))
    retg = row("retg", Wt["ret_g"])
    lng = row("lng", Wt["ln_g"])
    lnb = row("lnb", Wt["ln_b"])
    y = [k.sb(st, [128, 4, 64], F32, "y%d" % i) for i in range(2)]
    yc = k.sb(st, [128, 4, 64], F32, "yc")
    sq = k.sb(st, [128, 4, 64], F32, "sq")
    m4 = k.sb(st, [128, 4], F32, "m4")
    v4 = k.sb(st, [128, 4], F32, "v4")
    gc = k.sb(st, [128, 512], F32, "gc")
    gd = k.sb(st, [128, 256], F32, "gd")
    vd = k.sb(st, [128, 4, 64], F32, "vd")
    bs = k.sb(st, [128, 4], F32, "bs")
    acc = k.sb(st, [128, 256], F32, "acc")
    yn = k.sb(st, [128, 256], F32, "yn")
    i = 0

    def head_norm(yb, g, b, dst):
        k.op("dve", lambda e: e.tensor_reduce(out=m4[:, :], in_=yb[:, :, :], axis=AX.X, op=ALU.add), r=[yb], w=[m4])
        k.ts("dve", m4.v[:, :], m4.v[:, :], 1.0 / 64, None, ALU.mult)
        k.tt("dve", yc.v[:, :, :], yb.v[:, :, :], m4.v[:, :].rr("p (h o) -> p h o", o=1).bc([128, 4, 64]), ALU.subtract)
        k.tt("dve", sq.v[:, :, :], yc.v[:, :, :], yc.v[:, :, :], ALU.mult)
        k.op("dve", lambda e: e.tensor_reduce(out=v4[:, :], in_=sq[:, :, :], axis=AX.X, op=ALU.add), r=[sq], w=[v4])
        k.ts("dve", v4.v[:, :], v4.v[:, :], 1.0 / 64, GN_EPS, ALU.mult, ALU.add)
        k.act(v4.v[:, :], v4.v[:, :], ACT.Sqrt)
        k.recip(v4.v[:, :], v4.v[:, :])
        k.tt("dve", yc.v[:, :, :], yc.v[:, :, :], v4.v[:, :].rr("p (h o) -> p h o", o=1).bc([128, 4, 64]), ALU.mult)
        k.tt("dve", dst, yc.v[:, :, :].rr("p h i -> p (h i)"), g.v[:, :], ALU.mult)
        if b is not None:
            k.tt("dve", dst, dst, b.v[:, :], ALU.add)

    for t0 in range(0, TT, 128):
        tsl = slice(t0, t0 + 128)
        k.ld("sp", gc.v[:, :], S["C_G"][tsl, :])
        for d in range(2):
            yb = y[i % 2]
            i += 1
            k.ld("sp", yb.v[:, :, :], S["C_Y"][d, tsl, :].rearrange("t (h i) -> t h i", i=64))
            head_norm(yb, retg, None, yn.v[:, :])
            if d == 0:
                k.tt("dve", acc.v[:, :], yn.v[:, :], gc.v[:, 0:256], ALU.mult)
            else:
                k.tt("dve", yn.v[:, :], yn.v[:, :], gc.v[:, 256:512], ALU.mult)
                k.tt("dve", acc.v[:, :], acc.v[:, :], yn.v[:, :], ALU.add)
        k.stv("sp", S["O"][tsl, 512:768], acc.v[:, :])
        k.ld("sp", gd.v[:, :], S["D_G"][tsl, :])
        for d in range(2):
            yb = y[i % 2]
            i += 1
            k.ld("sp", yb.v[:, :, :], S["D_Y"][d, tsl, :].rearrange("t (h i) -> t h i", i=64))
            k.ld("sp", vd.v[:, :, :], S["D_V"][d, tsl, :].rearrange("t (h i) -> t h i", i=64))
            k.ld("sp", bs.v[:, :], S["D_BS"][d, tsl, :])
            head_norm(yb, lng, lnb, yn.v[:, :])
            k.tt("dve", vd.v[:, :, :], vd.v[:, :, :], bs.v[:, :].rr("p (h o) -> p h o", o=1).bc([128, 4, 64]), ALU.mult)
            k.tt("dve", yn.v[:, :], yn.v[:, :], vd.v[:, :, :].rr("p h i -> p (h i)"), ALU.add)
            if d == 0:
                k.cp("act", acc.v[:, :], yn.v[:, :])
            else:
                k.tt("dve", acc.v[:, :], acc.v[:, :], yn.v[:, :], ALU.add)
        k.tt("dve", acc.v[:, :], acc.v[:, :], gd.v[:, :], ALU.mult)
        k.stv("sp", S["O"][tsl, 768:1024], acc.v[:, :])


def modphase(k, st, cfg):
    cv = k.sb(st, [128, 2, 8], F32, "cv")
    for r in range(2):
        k.ld("sp", cv.v[:, r, :], cfg["cvec"][r].rearrange("(c p) -> p c", p=128), allow_slow_non_contiguous=True)
    sc = k.sb(st, [128, 8, 2], F32, "sc")
    k.act(sc.v[:, :, :].rr("p c r -> p r c"), cv.v[:, :, :], ACT.Silu)
    wm = [k.sb(st, [128, 8, 512], F32, "wm%d" % i) for i in range(2)]
    bm = k.sb(st, [2, 9 * D], F32, "bm")
    k.ld("sp", bm.v[:, :], cfg["b_mod"].partition_broadcast(2))
    ob = k.sb(st, [2, 9 * D], F32, "ob")
    pm = [k.ps(st, [128, 512], F32, "pm%d" % i) for i in range(2)]
    wv = cfg["w_mod"].rearrange("(c p) f -> p c f", p=128)
    for j in range(18):
        w = wm[j % 2]
        for c in range(8):
            k.ld("sp", w.v[:, c, :], wv[:, c, j * 512:(j + 1) * 512])
        p = pm[j % 2]
        for c in range(8):
            k.mm(p.v[0:2, :], sc.v[:, c, :], w.v[:, c, :], c == 0, c == 7)
        k.tt("dve", ob.v[:, j * 512:(j + 1) * 512], p.v[0:2, :], bm.v[:, j * 512:(j + 1) * 512], ALU.add)
    k.stv("sp", cfg["mod"], ob.v[:, :])


def build_program(T, CTX, L, stop_after=None):
    TT = T + CTX
    nc = bass.Bass("TRN2", target_bir_lowering=False, dynamic_dma_scratch_size=8192)

    def din(name, shape, dt=F32):
        return nc.dram_tensor(name, list(shape), dt, kind="ExternalInput").ap()

    def scr(name, shape, dt=F32):
        return nc.dram_tensor(name, list(shape), dt, kind="Internal").ap()

    I = dict(
        xin=din("xin", [TT, D]), cvec=din("cvec", [2, D]),
        w_mod=din("w_mod", [L, D, 9 * D]), b_mod=din("b_mod", [L, 9 * D]), norm_g=din("norm_g", [L, 3, D]),
        ffn_w_in=din("ffn_w_in", [L, 2, D, 2 * DFF]), ffn_w_out=din("ffn_w_out", [L, 2, DFF, D]),
        w_in=din("w_in", [L, D, 3456]), w_out=din("w_out", [L, D, D]), final_g=din("final_g", [D]),
        w_sw=din("w_sw", [L, D, 1280]), w_d=din("w_d", [L, 2, D, 896]), pcols=din("pcols", [L, 128, 24]),
        mu=din("mu", [L, 2, 896]), w2=din("w2", [L, 2, 64, 256]), a2=din("a2", [L, 2, 64, 256]),
        g2=din("g2", [L, 128, 256]), ret_g=din("ret_g", [L, 256]), ln_g=din("ln_g", [L, 256]),
        ln_b=din("ln_b", [L, 256]), sink=din("sink", [L, 4]),
    )
    Cn = dict(
        ident=din("ident", [128, 128]), bd64=din("bd64", [128, 128]), e2=din("e2", [128, 2]),
        ropeA_c=din("ropeA_c", [128, T]), ropeA_s=din("ropeA_s", [128, T]),
        ropeS_c=din("ropeS_c", [128, T]), ropeS_s=din("ropeS_s", [128, T]),
        ret_tab=din("ret_tab", [128, 2, 2, 3, CH]), ret_pc=din("ret_pc", [64, 8]),
        m_st=din("m_st", [64, 512]), m_ts=din("m_ts", [64, 512]), m_in=din("m_in", [64, 512]),
        id8=din("id8", [64, 512]), mprev=din("mprev", [128, 128]), mnext=din("mnext", [128, 128]),
    )
    out = nc.dram_tensor("out", [T, D], F32, kind="ExternalOutput").ap()
    dbg = stop_after is not None
    mk = (lambda name, shape, dt=F32: nc.dram_tensor(name, list(shape), dt, kind="ExternalOutput").ap()) if dbg else scr
    S = dict(
        xres=mk("xres", [TT, D]), mod=mk("modr", [2, 9 * D]), hT=mk("hT", [D, TT], BF16),
        QA=mk("QA", [256, TT], BF16), KA=mk("KA", [128, TT], BF16), VA=mk("VA", [TT, 128], BF16),
        QB=mk("QB", [256, TT], BF16), KB=mk("KB", [128, TT], BF16), VB=mk("VB", [TT, 128], BF16),
        O=mk("O", [TT, 1024]),
        C_Rt=mk("C_Rt", [2, 256, TT]), C_Kt=mk("C_Kt", [2, 256, TT]), C_Kh=mk("C_Kh", [2, TT, 256]),
        C_V=mk("C_V", [TT, 256]), C_G=mk("C_G", [TT, 512]), C_Y=mk("C_Y", [2, TT, 256]),
        D_Rt=mk("D_Rt", [2, 256, TT]), D_Kt=mk("D_Kt", [2, 256, TT]), D_At=mk("D_At", [2, 256, TT]),
        D_Bt=mk("D_Bt", [2, 256, TT]), D_Kh=mk("D_Kh", [2, TT, 256]), D_Bh=mk("D_Bh", [2, TT, 256]),
        D_Atok=mk("D_Atok", [2, TT, 256]), D_V=mk("D_V", [2, TT, 256]), D_Y=mk("D_Y", [2, TT, 256]),
        D_PC=mk("D_PC", [2, 256, TT // CH]), D_G=mk("D_G", [TT, 256]), D_BS=mk("D_BS", [2, TT, 4]),
    )
    k = K(nc)
    stages = []

    def phase(name, fn, cfg):
        if stop_after is not None and stop_after in stages:
            return
        with contextlib.ExitStack() as st:
            fn(k, st, cfg)
            k.barrier()
        stages.append(name)

    with k.stack:
        segs_all = [(0, T, 0), (T, CTX, 1)]
        for l in range(L):
            last = l == L - 1
            W = dict(w_in=I["w_in"][l], w_sw=I["w_sw"][l], w_d=I["w_d"][l], mu=I["mu"][l], pcols=I["pcols"][l],
                     w2=I["w2"][l], a2=I["a2"][l], g2=I["g2"][l], ret_g=I["ret_g"][l], ln_g=I["ln_g"][l],
                     ln_b=I["ln_b"][l], sink=I["sink"][l])
            base = dict(S=S, C=Cn, W=W, T=T, CTX=CTX)
            phase("mod%d" % l, modphase, dict(cvec=I["cvec"], w_mod=I["w_mod"][l], b_mod=I["b_mod"][l], mod=S["mod"]))
            phase("f1_%d" % l, rowphase, dict(
                xin=I["xin"] if l == 0 else S["xres"], xout=S["xres"], segs=segs_all, mod=S["mod"], ident=Cn["ident"],
                ffn=dict(w_in=I["ffn_w_in"][l, 0], w_out=I["ffn_w_out"][l, 0], g=I["norm_g"][l, 0], shift=0, scale=1,
                         gate=2),
                post=dict(kind="hT", g=I["norm_g"][l, 1], shift=3, scale=4, hT=S["hT"])))
            phase("proj%d" % l, projphase, base)
            phase("attn%d" % l, attnphase, dict(base, need_ctx=not last))
            phase("scanC%d" % l, scanphase, dict(base, dplr=False))
            phase("scanD%d" % l, scanphase, dict(base, dplr=True))
            phase("post%d" % l, postphase, base)
            phase("f2_%d" % l, rowphase, dict(
                xin=S["xres"], xout=S["xres"], segs=segs_all if not last else [(0, T, 0)], mod=S["mod"],
                ident=Cn["ident"],
                pre=dict(o=S["O"], w_out=I["w_out"][l], gate=5),
                ffn=dict(w_in=I["ffn_w_in"][l, 1], w_out=I["ffn_w_out"][l, 1], g=I["norm_g"][l, 2], shift=6, scale=7,
                         gate=8),
                post=dict(kind="final", g=I["final_g"], out=out) if last else None))
    return nc, k


def _consts(T):
    c = {}
    c["ident"] = np.eye(128, dtype=np.float32)
    bd = np.zeros((128, 128), np.float32)
    bd[:64, :64] = 1
    bd[64:, 64:] = 1
    c["bd64"] = bd
    e2 = np.zeros((128, 2), np.float32)
    e2[:64, 0] = 1
    e2[64:, 1] = 1
    c["e2"] = e2
    t = np.arange(T)
    row = (t // 64).astype(np.float32)
    col = (t % 64).astype(np.float32)
    inv16 = (1.0 / (np.float32(10000.0) ** (np.arange(0, 32, 2, dtype=np.float32) / np.float32(32)))).astype(np.float32)
    inv32 = (1.0 / (np.float32(10000.0) ** (np.arange(0, 64, 2, dtype=np.float32) / np.float32(64)))).astype(np.float32)
    ca = np.zeros((64, T), np.float32)
    sa = np.zeros((64, T), np.float32)
    for blk, pos in ((0, row), (32, col)):
        ang = pos[None, :] * inv16[:, None]
        ca[blk:blk + 16] = np.cos(ang)
        ca[blk + 16:blk + 32] = np.cos(ang)
        sa[blk:blk + 16] = -np.sin(ang)
        sa[blk + 16:blk + 32] = np.sin(ang)
    ang = t.astype(np.float32)[None, :] * inv32[:, None]
    cs = np.concatenate([np.cos(ang), np.cos(ang)], 0).astype(np.float32)
    ss = np.concatenate([-np.sin(ang), np.sin(ang)], 0).astype(np.float32)
    c["ropeA_c"] = np.concatenate([ca, ca], 0)
    c["ropeA_s"] = np.concatenate([sa, sa], 0)
    c["ropeS_c"] = np.concatenate([cs, cs], 0)
    c["ropeS_s"] = np.concatenate([ss, ss], 0)
    lg = np.log1p(-np.exp2(-5.0 - np.arange(4, dtype=np.float64)))
    i = np.arange(CH, dtype=np.float64)
    rt = np.zeros((128, 2, 2, 3, CH), np.float64)
    for p in range(128):
        for g in range(2):
            l_ = lg[2 * g + p // 64]
            for d in range(2):
                csum = (i + 1) * l_ if d == 0 else (CH - i) * l_
                rt[p, g, d, 0] = np.exp(csum)
                rt[p, g, d, 1] = np.exp(-csum) / 8.0
                rt[p, g, d, 2] = np.exp(CH * l_ - csum) / 8.0
    c["ret_tab"] = rt.astype(np.float32)
    c["ret_pc"] = np.tile(np.exp(CH * lg)[None, :], (64, 2)).astype(np.float32)
    s_ = np.arange(64)[:, None]
    t_ = np.arange(64)[None, :]
    lt = (s_ < t_).astype(np.float32)
    gt = (s_ > t_).astype(np.float32)
    eye = np.eye(64, dtype=np.float32)
    c["m_st"] = np.concatenate([lt] * 4 + [gt] * 4, 1)
    c["m_ts"] = np.concatenate([gt] * 4 + [lt] * 4, 1)
    c["m_in"] = np.concatenate([lt + eye] * 4 + [gt + eye] * 4, 1)
    c["id8"] = np.concatenate([eye] * 8, 1)
    j = np.arange(128)[:, None]
    ii = np.arange(128)[None, :]
    c["mprev"] = (ii <= j).astype(np.float32)
    c["mnext"] = (j <= ii).astype(np.float32)
    return c


def _host_maps(inp, T, CTX, L):
    f = lambda a: np.ascontiguousarray(np.asarray(a, dtype=np.float32))
    w_in = f(inp["w_in"])
    pa = np.concatenate([np.arange(16, 32), np.arange(0, 16), np.arange(48, 64), np.arange(32, 48)])
    psq = np.concatenate([np.arange(32, 64), np.arange(0, 32)])
    cols = []
    for base, nh, perm in ((0, 4, pa), (256, 2, pa), (512, 4, pa), (768, 2, pa), (1024, 4, psq), (1280, 4, psq)):
        for h in range(nh):
            cols.append(base + h * 64 + perm)
    cols = np.concatenate(cols)
    w_sw = np.ascontiguousarray(w_in[:, :, cols])
    dcols = [np.concatenate([np.arange(2304, 3072), np.arange(3200, 3264), np.arange(3328, 3392)]),
             np.concatenate([np.arange(2304, 3072), np.arange(3264, 3328), np.arange(3392, 3456)])]
    w_d = np.ascontiguousarray(np.stack([w_in[:, :, dc] for dc in dcols], 1))
    qkg = f(inp["qk_norm_g"])
    pcols = np.zeros((L, 128, 24), np.float32)
    two = lambda v: np.ascontiguousarray(v.reshape(2, 128).T)
    for l in range(L):
        for qk in range(2):
            g = qkg[l, qk]
            pcols[l, :, 2 * qk] = np.tile(g, 2)
            pcols[l, :, 2 * qk + 1] = np.tile(g[pa], 2)
        for d in range(2):
            pcols[l, :, 4 + 2 * d:6 + 2 * d] = two(f(inp["rwkv_w0"])[l, d])
            pcols[l, :, 8 + 2 * d:10 + 2 * d] = two(f(inp["rwkv_a0"])[l, d])
            pcols[l, :, 16 + 2 * d:18 + 2 * d] = two(f(inp["rwkv_rho"])[l, d].reshape(256))
        pcols[l, :, 12:14] = two(f(inp["rwkv_k_k"])[l])
        pcols[l, :, 14:16] = two(f(inp["rwkv_k_a"])[l])
    shared = dict(
        w_mod=f(inp["w_mod"]), b_mod=f(inp["b_mod"]), norm_g=f(inp["norm_g"]), ffn_w_in=f(inp["ffn_w_in"]),
        ffn_w_out=f(inp["ffn_w_out"]), w_in=w_in, w_out=f(inp["w_out"]), final_g=f(inp["final_norm_g"]),
        w_sw=w_sw, w_d=w_d, pcols=pcols, mu=f(inp["rwkv_mu"]), w2=f(inp["rwkv_w2"]), a2=f(inp["rwkv_a2"]),
        g2=f(inp["rwkv_g2"]), ret_g=f(inp["ret_norm_g"]), ln_g=f(inp["rwkv_ln_g"]), ln_b=f(inp["rwkv_ln_b"]),
        sink=f(inp["attn_sink"]))
    shared.update(_consts(T))
    x, c, ctx, c_ctx = f(inp["x"]), f(inp["c"]), f(inp["ctx"]), f(inp["c_ctx"])
    maps = []
    for b in range(x.shape[0]):
        m = dict(shared)
        m["xin"] = np.ascontiguousarray(np.concatenate([x[b], ctx[b]], 0))
        m["cvec"] = np.ascontiguousarray(np.stack([c[b], c_ctx], 0))
        maps.append(m)
    return maps


_PROG = {}


def kernel(**inputs):
    x = np.asarray(inputs["x"])
    B, T, _ = x.shape
    CTX = np.asarray(inputs["ctx"]).shape[1]
    L = np.asarray(inputs["w_in"]).shape[0]
    key = (T, CTX, L)
    if key not in _PROG:
        _PROG[key] = build_program(T, CTX, L)[0]
    nc = _PROG[key]
    maps = _host_maps(inputs, T, CTX, L)
    res = run_bass_kernel_spmd(nc, maps, core_ids=list(range(B)))
    return np.stack([np.asarray(r["out"], dtype=np.float32) for r in res.results], 0)
```

```python
import contextlib
import numpy as np
import concourse.bass as bass
import concourse.mybir as mybir
from concourse.bass_utils import run_bass_kernel_spmd

ACT = mybir.ActivationFunctionType
ALU = mybir.AluOpType
AX = mybir.AxisListType
F32 = mybir.dt.float32
BF16 = mybir.dt.bfloat16

D = 1024
DFF = 2816
NMOD = 9
RMS_EPS = 1e-6
STRICT = True


class PSem:
    def __init__(self, h):
        self.h = h
        self.val = 0


class Eng:
    def __init__(self, name, h, sem):
        self.name, self.h, self.sem = name, h, sem
        self.n = 0
        self.seen = {}
        self.dseen = {}


class Buf:
    def __init__(self, k, t, name):
        self.k, self.t, self.name = k, t, name
        self.w = None
        self.r = {}
        self.psem = None
        self.dma_w = 0
        self.dma_rw = 0

    def __getitem__(self, key):
        return self.t[key]

    @property
    def v(self):
        return _VA(self)


class View:
    def __init__(self, buf, ap):
        self.buf, self.ap = buf, ap

    def rr(self, pat, **kw):
        return View(self.buf, self.ap.rearrange(pat, **kw))

    def bc(self, shape):
        return View(self.buf, self.ap.to_broadcast(list(shape)))

    def __getitem__(self, key):
        return View(self.buf, self.ap[key])


class _VA:
    def __init__(self, b):
        self.b = b

    def __getitem__(self, key):
        return View(self.b, self.b.t[key])


def _bufs(*xs):
    out = []
    for x in xs:
        if isinstance(x, View) and x.buf not in out:
            out.append(x.buf)
    return out


def _ap(x):
    return x.ap if isinstance(x, View) else x


class K:
    def __init__(self, nc):
        self.nc = nc
        self.stack = contextlib.ExitStack()
        self.E = {}
        for name, h in (("pe", nc.tensor), ("act", nc.scalar), ("dve", nc.vector),
                        ("pool", nc.gpsimd), ("sp", nc.sync)):
            sem = self.stack.enter_context(nc.semaphore("s_" + name))
            self.E[name] = Eng(name, h, sem)
        self.free_psems = []
        self.n_psems = 0
        self.all_psems = []
        self.bufs = []
        self.uid = 0
        self.ninstr = 0
        self.marks = []

    def _psem(self):
        if self.free_psems:
            return self.free_psems.pop()
        self.n_psems += 1
        h = self.stack.enter_context(self.nc.semaphore("d%d" % self.n_psems))
        p = PSem(h)
        self.all_psems.append(p)
        return p

    def sb(self, st, shape, dtype, name):
        self.uid += 1
        t = st.enter_context(self.nc.sbuf_tensor("%s_%d" % (name, self.uid), list(shape), dtype))
        b = Buf(self, t, name)
        st.callback(self._release, b)
        return b

    def ps(self, st, shape, dtype, name):
        self.uid += 1
        t = st.enter_context(self.nc.psum_tensor("%s_%d" % (name, self.uid), list(shape), dtype))
        return Buf(self, t, name)

    def _release(self, b):
        if b.psem is not None and not getattr(b.psem, "sw", False):
            self.free_psems.append(b.psem)
        b.psem = None

    def _wait_eng(self, E, X, n):
        if E.seen.get(X.name, 0) >= n:
            return
        E.h.wait_ge(X.sem, n)
        E.seen[X.name] = n
        self.ninstr += 1

    def _wait_d(self, E, p, v):
        if v <= 0 or E.dseen.get(id(p), 0) >= v:
            return
        E.h.wait_ge(p.h, v)
        E.dseen[id(p)] = v
        self.ninstr += 1

    def _deps(self, E, r, w):
        for b in r:
            if b.w is not None:
                self._wait_eng(E, b.w[0], b.w[1])
            if b.psem is not None:
                self._wait_d(E, b.psem, b.dma_w)
        for b in w:
            if b.w is not None and (STRICT or b.w[0] is not E) and not (E.name == "pe" and b.w[0] is E):
                self._wait_eng(E, b.w[0], b.w[1])
            for X, n in b.r.items():
                if STRICT or X is not E:
                    self._wait_eng(E, X, n)
            if b.psem is not None:
                self._wait_d(E, b.psem, b.dma_rw)

    def op(self, eng, fn, r=(), w=()):
        E = self.E[eng]
        self._deps(E, r, w)
        ins = fn(E.h)
        E.n += 1
        ins.then_inc(E.sem, 1)
        self.ninstr += 1
        for b in w:
            b.w = (E, E.n)
            b.r = {}
        for b in r:
            if b.w is None or b.w[0] is not E or b.w[1] != E.n:
                b.r[E] = E.n
        return ins

    def dma(self, q, out, in_, r=(), w=(), **kw):
        Q = self.E[q]
        self._deps(Q, r, w)
        anchor = w[0] if w else r[0]
        if anchor.psem is None:
            if q == "pool":
                self.n_psems += 1
                anchor.psem = PSem(self.stack.enter_context(self.nc.semaphore("w%d" % self.n_psems)))
                anchor.psem.sw = True
                self.all_psems.append(anchor.psem)
            else:
                anchor.psem = self._psem()
        p = anchor.psem
        assert getattr(p, "sw", False) == (q == "pool"), "buffer mixes SW and HW DGE DMAs: " + anchor.name
        Q.h.dma_start(out=out, in_=in_, **kw).then_inc(p.h, 16)
        p.val += 16
        self.ninstr += 1
        anchor.dma_rw = p.val
        if w:
            anchor.dma_w = p.val
            anchor.w = None
            anchor.r = {}

    def mm(self, out, lhsT, rhs, start=True, stop=True):
        return self.op("pe", lambda e: e.matmul(out.ap, lhsT.ap, rhs.ap, start=start, stop=stop),
                       r=_bufs(lhsT, rhs), w=[out.buf])

    def tr(self, out, in_, ident):
        return self.op("pe", lambda e: e.transpose(out.ap, in_.ap, ident.ap), r=_bufs(in_, ident), w=[out.buf])

    def tt(self, eng, out, in0, in1, op):
        return self.op(eng, lambda e: e.tensor_tensor(out=out.ap, in0=in0.ap, in1=in1.ap, op=op),
                       r=_bufs(in0, in1), w=[out.buf])

    def ts(self, eng, out, in0, s1, s2, op0, op1=None):
        kw = {} if op1 is None else dict(op1=op1)
        return self.op(eng, lambda e: e.tensor_scalar(out=out.ap, in0=in0.ap, scalar1=_ap(s1), scalar2=_ap(s2),
                                                      op0=op0, **kw), r=_bufs(in0, s1, s2), w=[out.buf])

    def stt(self, eng, out, in0, scalar, in1, op0, op1):
        return self.op(eng, lambda e: e.scalar_tensor_tensor(out=out.ap, in0=in0.ap, scalar=_ap(scalar), in1=in1.ap,
                                                             op0=op0, op1=op1), r=_bufs(in0, scalar, in1), w=[out.buf])

    def act(self, out, in_, func, bias=None, scale=None, accum=None):
        kw = {}
        if bias is not None:
            kw["bias"] = _ap(bias)
        if scale is not None:
            kw["scale"] = _ap(scale)
        w = [out.buf]
        if accum is not None:
            kw["accum_out"] = accum.ap
            w.append(accum.buf)
        return self.op("act", lambda e: e.activation(out=out.ap, in_=in_.ap, func=func, **kw),
                       r=_bufs(in_, bias, scale), w=w)

    def cp(self, eng, out, in_):
        if eng == "act":
            return self.op("act", lambda e: e.copy(out=out.ap, in_=in_.ap), r=[in_.buf], w=[out.buf])
        return self.op(eng, lambda e: e.tensor_copy(out=out.ap, in_=in_.ap), r=[in_.buf], w=[out.buf])

    def memset(self, eng, out, val):
        return self.op(eng, lambda e: e.memset(out.ap, val), w=[out.buf])

    def recip(self, out, in_):
        return self.op("dve", lambda e: e.reciprocal(out=out.ap, in_=in_.ap), r=[in_.buf], w=[out.buf])

    def ld(self, q, out, dram, **kw):
        return self.dma(q, out.ap, dram, w=[out.buf], **kw)

    def stv(self, q, dram, in_, **kw):
        return self.dma(q, dram, in_.ap, r=[in_.buf], **kw)

    def barrier(self):
        for E in self.E.values():
            for X in self.E.values():
                if X is not E and X.n > 0:
                    self._wait_eng(E, X, X.n)
            for p in self.all_psems:
                self._wait_d(E, p, p.val)


def mm(k, out_ap, lhsT_ap, rhs_ap, start, stop, r, w):
    return k.op("pe", lambda e: e.matmul(out_ap, lhsT_ap, rhs_ap, start=start, stop=stop), r=r, w=w)


def rowphase(k, st, cfg):
    nc = k.nc
    ffn = cfg["ffn"]
    pre = cfg.get("pre")
    post = cfg.get("post")
    w1 = k.sb(st, [128, 8, 2 * DFF], BF16, "w1")
    w2 = k.sb(st, [128, 22, D], BF16, "w2")
    w1v = ffn["w_in"].rearrange("(c p) f -> p c f", p=128)
    for c in range(8):
        for hlf in range(4):
            f0 = hlf * 1408
            k.dma("pool", w1[:, c, f0:f0 + 1408], w1v[:, c, f0:f0 + 1408], w=[w1])
    w2v = ffn["w_out"].rearrange("(c p) d -> p c d", p=128)
    for c in range(22):
        k.dma("pool", w2[:, c, :], w2v[:, c, :], w=[w2])
    if pre is not None:
        wo = k.sb(st, [128, 8, D], BF16, "wo")
        wov = pre["w_out"].rearrange("(c p) d -> p c d", p=128)
        for c in range(8):
            k.dma("pool", wo[:, c, :], wov[:, c, :], w=[wo])

    identf = k.sb(st, [128, 128], F32, "identf")
    k.dma("sp", identf[:, :], cfg["ident"], w=[identf])
    xt = [k.sb(st, [128, D], F32, "xt%d" % i) for i in range(4)]
    hT = k.sb(st, [128, 8, 512], BF16, "hT")
    gT = k.sb(st, [128, 22, 512], BF16, "gT")
    sg = k.sb(st, [128, 512], BF16, "sg")
    tmp = k.sb(st, [128, D], F32, "tmp")
    ssq = k.sb(st, [128, 1], F32, "ssq")
    rinv = k.sb(st, [128, 1], F32, "rinv")
    ps_u = [k.ps(st, [128, 512], F32, "psu%d" % i) for i in range(4)]
    ps_y = [k.ps(st, [128, 512], F32, "psy%d" % i) for i in range(2)]
    ps_t = k.ps(st, [128, D], F32, "pst")
    Ac = k.sb(st, [128, 8], F32, "Ac")
    Bc = k.sb(st, [128, 8], F32, "Bc")
    Gc = k.sb(st, [128, 8], F32, "Gc")
    A2c = k.sb(st, [128, 8], F32, "A2c")
    B2c = k.sb(st, [128, 8], F32, "B2c")
    G = k.sb(st, [128, D], F32, "G")
    PG = k.sb(st, [128, D], F32, "PG") if pre is not None else None
    gvec2 = None
    if post is not None and post["kind"] == "final":
        gvec2 = k.sb(st, [128, D], F32, "gvec2")
        k.dma("sp", gvec2[:, :], post["g"].partition_broadcast(128), w=[gvec2])

    def col(ap1d):
        return ap1d.rearrange("(c p) -> p c", p=128)

    def load_cols(dst, ap1d):
        k.dma("sp", dst[:, :], col(ap1d), w=[dst], allow_slow_non_contiguous=True)

    def setup_mod(mr):
        load_cols(Gc, ffn["g"])
        load_cols(Ac, cfg["mod"][mr, ffn["scale"] * D:(ffn["scale"] + 1) * D])
        load_cols(Bc, cfg["mod"][mr, ffn["shift"] * D:(ffn["shift"] + 1) * D])
        k.op("dve", lambda e: e.scalar_tensor_tensor(out=Ac[:, :], in0=Ac[:, :], scalar=1.0, in1=Gc[:, :],
                                                     op0=ALU.add, op1=ALU.mult), r=[Ac, Gc], w=[Ac])
        if post is not None and post["kind"] == "hT":
            load_cols(Gc, post["g"])
            load_cols(A2c, cfg["mod"][mr, post["scale"] * D:(post["scale"] + 1) * D])
            load_cols(B2c, cfg["mod"][mr, post["shift"] * D:(post["shift"] + 1) * D])
            k.op("dve", lambda e: e.scalar_tensor_tensor(out=A2c[:, :], in0=A2c[:, :], scalar=1.0, in1=Gc[:, :],
                                                         op0=ALU.add, op1=ALU.mult), r=[A2c, Gc], w=[A2c])
        k.dma("sp", G[:, :], cfg["mod"][mr, ffn["gate"] * D:(ffn["gate"] + 1) * D].partition_broadcast(128), w=[G])
        k.op("dve", lambda e: e.tensor_scalar(out=G[:, :], in0=G[:, :], scalar1=0.5, scalar2=None, op0=ALU.mult),
             r=[G], w=[G])
        if pre is not None:
            k.dma("sp", PG[:, :], cfg["mod"][mr, pre["gate"] * D:(pre["gate"] + 1) * D].partition_broadcast(128),
                  w=[PG])

    def rms_rinv(xb):
        k.op("act", lambda e: e.activation(out=tmp[:, :], in_=xb[:, :], func=ACT.Square, accum_out=ssq[:, :]),
             r=[xb], w=[tmp, ssq])
        k.op("dve", lambda e: e.tensor_scalar(out=rinv[:, :], in0=ssq[:, :], scalar1=1.0 / D, scalar2=RMS_EPS,
                                              op0=ALU.mult, op1=ALU.add), r=[ssq], w=[rinv])
        k.op("act", lambda e: e.activation(out=rinv[:, :], in_=rinv[:, :], func=ACT.Sqrt), r=[rinv], w=[rinv])
        k.op("dve", lambda e: e.reciprocal(out=rinv[:, :], in_=rinv[:, :]), r=[rinv], w=[rinv])

    def norm_mod_T(xb, A, B, dstT, s):
        rms_rinv(xb)
        k.op("dve", lambda e: e.tensor_scalar(out=tmp[:, :], in0=xb[:, :], scalar1=rinv[:, 0:1], scalar2=None,
                                              op0=ALU.mult), r=[xb, rinv], w=[tmp])
        for c in range(8):
            k.op("pe", lambda e: e.transpose(ps_t[:, c * 128:(c + 1) * 128], tmp[:, c * 128:(c + 1) * 128],
                                             identf[:, :]), r=[tmp, identf], w=[ps_t])
        for c in range(8):
            if c % 2 == 0:
                k.op("dve", lambda e: e.tensor_scalar(out=dstT[:, c, s * 128:(s + 1) * 128],
                                                      in0=ps_t[:, c * 128:(c + 1) * 128], scalar1=A[:, c:c + 1],
                                                      scalar2=B[:, c:c + 1], op0=ALU.mult, op1=ALU.add),
                     r=[ps_t, A, B], w=[dstT])
            else:
                k.op("act", lambda e: e.activation(out=dstT[:, c, s * 128:(s + 1) * 128],
                                                   in_=ps_t[:, c * 128:(c + 1) * 128], func=ACT.Identity,
                                                   scale=A[:, c:c + 1], bias=B[:, c:c + 1]),
                     r=[ps_t, A, B], w=[dstT])

    ui = 0
    yi = 0
    for (tok0, ntok, mr) in cfg["segs"]:
        setup_mod(mr)
        for t0 in range(tok0, tok0 + ntok, 512):
            n = min(512, tok0 + ntok - t0)
            ns = n // 128
            for s in range(ns):
                k.dma("sp", xt[s][:, :], cfg["xin"][t0 + s * 128:t0 + (s + 1) * 128, :], w=[xt[s]])
            if pre is not None:
                for s in range(ns):
                    k.dma("sp", tmp[:, :], pre["o"][t0 + s * 128:t0 + (s + 1) * 128, :], w=[tmp])
                    for c in range(8):
                        k.op("pe", lambda e: e.transpose(ps_t[:, c * 128:(c + 1) * 128], tmp[:, c * 128:(c + 1) * 128],
                                                         identf[:, :]), r=[tmp, identf], w=[ps_t])
                    k.op("act", lambda e: e.copy(out=hT[:, :, s * 128:(s + 1) * 128],
                                                 in_=ps_t[:, :].rearrange("p (c t) -> p c t", c=8)), r=[ps_t], w=[hT])
                for s in range(ns):
                    for hf in range(2):
                        py = ps_y[yi % 2]
                        yi += 1
                        for c in range(8):
                            mm(k, py[:, :], hT[:, c, s * 128:(s + 1) * 128], wo[:, c, hf * 512:(hf + 1) * 512],
                               c == 0, c == 7, r=[hT, wo], w=[py])
                        sl = slice(hf * 512, (hf + 1) * 512)
                        k.op("dve", lambda e, py=py, sl=sl: e.tensor_tensor(out=tmp[:, sl], in0=py[:, :],
                                                                             in1=PG[:, sl], op=ALU.mult),
                             r=[py, PG], w=[tmp])
                        k.op("dve", lambda e, s=s, sl=sl: e.tensor_tensor(out=xt[s][:, sl], in0=xt[s][:, sl],
                                                                            in1=tmp[:, sl], op=ALU.add),
                             r=[xt[s], tmp], w=[xt[s]])
            for s in range(ns):
                norm_mod_T(xt[s], Ac, Bc, hT, s)
            for fc in range(22 if cfg.get("stage", 9) >= 2 else 0):
                p1 = ps_u[ui % 4]
                p2 = ps_u[(ui + 1) % 4]
                ui += 2
                for c in range(8):
                    mm(k, p1[:, 0:n], w1[:, c, fc * 128:(fc + 1) * 128], hT[:, c, 0:n], c == 0, c == 7,
                       r=[w1, hT], w=[p1])
                for c in range(8):
                    mm(k, p2[:, 0:n], w1[:, c, DFF + fc * 128:DFF + (fc + 1) * 128], hT[:, c, 0:n], c == 0, c == 7,
                       r=[w1, hT], w=[p2])
                k.op("act", lambda e, p1=p1: e.activation(out=sg[:, 0:n], in_=p1[:, 0:n], func=ACT.Silu),
                     r=[p1], w=[sg])
                k.op("dve", lambda e, p2=p2, fc=fc: e.tensor_tensor(out=gT[:, fc, 0:n], in0=sg[:, 0:n],
                                                                     in1=p2[:, 0:n], op=ALU.mult),
                     r=[sg, p2], w=[gT])
            for s in range(ns if cfg.get("stage", 9) >= 3 else 0):
                for hf in range(2):
                    py = ps_y[yi % 2]
                    yi += 1
                    for fc in range(22):
                        mm(k, py[:, :], gT[:, fc, s * 128:(s + 1) * 128], w2[:, fc, hf * 512:(hf + 1) * 512],
                           fc == 0, fc == 21, r=[gT, w2], w=[py])
                    sl = slice(hf * 512, (hf + 1) * 512)
                    k.op("dve", lambda e, py=py, sl=sl: e.tensor_tensor(out=tmp[:, sl], in0=py[:, :], in1=G[:, sl],
                                                                         op=ALU.mult), r=[py, G], w=[tmp])
                    k.op("dve", lambda e, s=s, sl=sl: e.tensor_tensor(out=xt[s][:, sl], in0=xt[s][:, sl],
                                                                        in1=tmp[:, sl], op=ALU.add),
                         r=[xt[s], tmp], w=[xt[s]])
            for s in range(ns):
                r0 = t0 + s * 128
                if post is not None and post["kind"] == "final":
                    if mr == 0:
                        rms_rinv(xt[s])
                        k.op("dve", lambda e, s=s: e.scalar_tensor_tensor(out=tmp[:, :], in0=xt[s][:, :],
                                                                           scalar=rinv[:, 0:1], in1=gvec2[:, :],
                                                                           op0=ALU.mult, op1=ALU.mult),
                             r=[xt[s], rinv, gvec2], w=[tmp])
                        k.dma("sp", post["out"][r0:r0 + 128, :], tmp[:, :], r=[tmp])
                    continue
                k.dma("sp", cfg["xout"][r0:r0 + 128, :], xt[s][:, :], r=[xt[s]])
                if post is not None and post["kind"] == "hT":
                    norm_mod_T(xt[s], A2c, B2c, hT, s)
            if post is not None and post["kind"] == "hT":
                for c in range(8):
                    k.dma("sp", post["hT"][c * 128:(c + 1) * 128, t0:t0 + n], hT[:, c, 0:n], r=[hT])


CH = 64
DECAY_SCALE = 0.6065306597126334


def projphase(k, st, cfg):
    S = cfg["S"]
    C = cfg["C"]
    Wt = cfg["W"]
    T, CTX = cfg["T"], cfg["CTX"]
    hTd = S["hT"]
    wabc = k.sb(st, [128, 8, 2304], BF16, "wabc")
    wgd = k.sb(st, [128, 8, 128], BF16, "wgd")
    wsw = k.sb(st, [128, 8, 1280], BF16, "wsw")
    wv = Wt["w_in"].rearrange("(c p) f -> p c f", p=128)
    for c in range(8):
        k.ld("pool", wabc.v[:, c, 0:1152], wv[:, c, 0:1152])
        k.ld("pool", wabc.v[:, c, 1152:2304], wv[:, c, 1152:2304])
        k.ld("pool", wgd.v[:, c, :], wv[:, c, 3072:3200])
        k.ld("pool", wsw.v[:, c, :], Wt["w_sw"].rearrange("(c p) f -> p c f", p=128)[:, c, :])
    wda = [k.sb(st, [128, 8, 896], BF16, "wda%d" % d) for d in range(2)]
    wdb = [k.sb(st, [128, 8, 896], BF16, "wdb%d" % d) for d in range(2)]
    with contextlib.ExitStack() as st2:
        wst = k.sb(st2, [128, 8, 896], F32, "wst")
        mub = k.sb(st2, [128, 896], F32, "mub")
        wtm = k.sb(st2, [128, 8, 896], F32, "wtm")
        for d in range(2):
            for c in range(8):
                k.ld("sp", wst.v[:, c, :], Wt["w_d"][d].rearrange("(c p) f -> p c f", p=128)[:, c, :])
            k.ld("sp", mub.v[:, :], Wt["mu"][d].partition_broadcast(128))
            for c in range(8):
                k.tt("dve", wtm.v[:, c, :], wst.v[:, c, :], mub.v[:, :], ALU.mult)
            k.cp("act", wda[d].v[:, :, :], wtm.v[:, :, :])
            k.tt("dve", wdb[d].v[:, :, :], wst.v[:, :, :], wtm.v[:, :, :], ALU.subtract)
        k.barrier()
    pc = k.sb(st, [128, 24], F32, "pcols")
    k.ld("sp", pc.v[:, :], Wt["pcols"])
    oma = k.sb(st, [128, 2], F32, "oma")
    k.ts("dve", oma.v[:, :], pc.v[:, 14:16], -1.0, 1.0, ALU.mult, ALU.add)
    identf = k.sb(st, [128, 128], F32, "identf")
    k.ld("sp", identf.v[:, :], C["ident"])
    bd = k.sb(st, [128, 128], F32, "bd")
    k.ld("sp", bd.v[:, :], C["bd64"])
    e2 = k.sb(st, [128, 2], F32, "e2")
    k.ld("sp", e2.v[:, :], C["e2"])
    rtab = k.sb(st, [128, 2, 2, 3, CH], F32, "rtab")
    k.ld("sp", rtab.v[:, :, :, :, :], C["ret_tab"])
    w2s = k.sb(st, [64, 2, 256], BF16, "w2s")
    a2s = k.sb(st, [64, 2, 256], BF16, "a2s")
    g2s = k.sb(st, [128, 256], BF16, "g2s")
    for d in range(2):
        k.ld("pool", w2s.v[:, d, :], Wt["w2"][d])
        k.ld("pool", a2s.v[:, d, :], Wt["a2"][d])
    k.ld("pool", g2s.v[:, :], Wt["g2"])

    NT = 512
    hTt = k.sb(st, [128, 8, NT + 2], BF16, "hTt")
    ca = k.sb(st, [128, NT], F32, "ca")
    sa = k.sb(st, [128, NT], F32, "sa")
    cs_ = k.sb(st, [128, NT], F32, "cs")
    ss_ = k.sb(st, [128, NT], F32, "ss")
    gca = k.sb(st, [128, 2, NT], F32, "gca")
    gsa = k.sb(st, [128, 2, NT], F32, "gsa")
    F = [k.sb(st, [128, NT], F32, "f%d" % i) for i in range(12)]
    ob = k.sb(st, [128, NT], BF16, "ob")
    tb = k.sb(st, [128, 512], F32, "tb")
    tbb = k.sb(st, [128, 512], BF16, "tbb")
    thb = k.sb(st, [64, NT], BF16, "thb")
    alb = k.sb(st, [64, NT], BF16, "alb")
    sgb = k.sb(st, [128, NT], BF16, "sgb")
    bs4 = k.sb(st, [128, 4, 4], F32, "bs4")
    pcx = k.sb(st, [128, NT // CH], F32, "pcx")
    PS = [k.ps(st, [128, 512], F32, "pp%d" % i) for i in range(8)]
    pi = [0]

    def nps():
        p = PS[pi[0] % 8]
        pi[0] += 1
        return p

    def chain(p, wlist, M, n, col0=1):
        tot = len(wlist) * 8
        i = 0
        for (wb, c0, sh) in wlist:
            for c in range(8):
                k.mm(p.v[0:M, 0:n], wb.v[:, c, c0:c0 + M], hTt.v[:, c, col0 + sh:col0 + sh + n], i == 0, i == tot - 1)
                i += 1

    def chain_tok(p, wlist, ncols, s):
        tot = len(wlist) * 8
        i = 0
        for (wb, c0, sh) in wlist:
            for c in range(8):
                k.mm(p.v[:, 0:ncols], hTt.v[:, c, 1 + sh + s * 128:1 + sh + (s + 1) * 128], wb.v[:, c, c0:c0 + ncols],
                     i == 0, i == tot - 1)
                i += 1

    def rope_out(P, Psw, ctab, stab, dst, n):
        k.tt("dve", F[10].v[:, 0:n], P.v[:, 0:n], ctab, ALU.mult)
        k.tt("dve", F[11].v[:, 0:n], Psw.v[:, 0:n], stab, ALU.mult)
        k.tt("dve", dst, F[10].v[:, 0:n], F[11].v[:, 0:n], ALU.add)

    segs = [(0, T, True), (T, CTX, False)]
    for (seg0, seglen, latent) in segs:
        for t0 in range(seg0, seg0 + seglen, NT):
            n = min(NT, seg0 + seglen - t0)
            ns = n // 128
            nch = n // CH
            lo = t0 - 1 if t0 > seg0 else t0
            hi = t0 + n + 1 if t0 + n < seg0 + seglen else t0 + n
            if lo == t0:
                k.memset("dve", hTt.v[:, :, 0:1], 0.0)
            if hi == t0 + n:
                k.memset("dve", hTt.v[:, :, n + 1:n + 2], 0.0)
            for c in range(8):
                k.ld("sp", hTt.v[:, c, 1 - (t0 - lo):1 + n + (hi - t0 - n)], hTd[c * 128:(c + 1) * 128, lo:hi])
            if latent:
                k.ld("sp", ca.v[:, 0:n], C["ropeA_c"][:, t0:t0 + n])
                k.ld("sp", sa.v[:, 0:n], C["ropeA_s"][:, t0:t0 + n])
                k.ld("sp", cs_.v[:, 0:n], C["ropeS_c"][:, t0:t0 + n])
                k.ld("sp", ss_.v[:, 0:n], C["ropeS_s"][:, t0:t0 + n])
                for qk in range(2):
                    k.ts("dve", gca.v[:, qk, 0:n], ca.v[:, 0:n], pc.v[:, 2 * qk:2 * qk + 1], None, ALU.mult)
                    k.ts("dve", gsa.v[:, qk, 0:n], sa.v[:, 0:n], pc.v[:, 2 * qk + 1:2 * qk + 2], None, ALU.mult)
            for grp, base, swb, dq, dk_, dv_ in (("A", 0, 0, S["QA"], S["KA"], S["VA"]),
                                                ("B", 512, 384, S["QB"], S["KB"], S["VB"])):
                for j in range(3):
                    P, Psw = nps(), nps()
                    chain(P, [(wabc, base + j * 128, 0)], 128, n)
                    if latent:
                        chain(Psw, [(wsw, swb + j * 128, 0)], 128, n)
                    qk = 0 if j < 2 else 1
                    if grp == "B":
                        k.act(F[0].v[:, 0:n], P.v[:, 0:n], ACT.Square)
                        pss = nps()
                        k.mm(pss.v[:, 0:n], bd.v[:, :], F[0].v[:, 0:n])
                        k.ts("dve", F[1].v[:, 0:n], pss.v[:, 0:n], 1.0 / 64, RMS_EPS, ALU.mult, ALU.add)
                        k.act(F[1].v[:, 0:n], F[1].v[:, 0:n], ACT.Sqrt)
                        k.recip(F[1].v[:, 0:n], F[1].v[:, 0:n])
                        k.tt("dve", F[2].v[:, 0:n], P.v[:, 0:n], F[1].v[:, 0:n], ALU.mult)
                        if latent:
                            k.tt("dve", F[3].v[:, 0:n], Psw.v[:, 0:n], F[1].v[:, 0:n], ALU.mult)
                            rope_out(F[2], F[3], gca.v[:, qk, 0:n], gsa.v[:, qk, 0:n], ob.v[:, 0:n], n)
                        else:
                            k.ts("dve", ob.v[:, 0:n], F[2].v[:, 0:n], pc.v[:, 2 * qk:2 * qk + 1], None, ALU.mult)
                    else:
                        if latent:
                            rope_out(P, Psw, ca.v[:, 0:n], sa.v[:, 0:n], ob.v[:, 0:n], n)
                        else:
                            k.cp("act", ob.v[:, 0:n], P.v[:, 0:n])
                    dst = dq[j * 128:(j + 1) * 128, t0:t0 + n] if j < 2 else dk_[:, t0:t0 + n]
                    k.stv("sp", dst, ob.v[:, 0:n])
                for s in range(ns):
                    P = nps()
                    chain_tok(P, [(wabc, base + 384, 0)], 128, s)
                    k.cp("act", tbb.v[:, 0:128], P.v[:, 0:128])
                    k.stv("sp", dv_[t0 + s * 128:t0 + (s + 1) * 128, :], tbb.v[:, 0:128])
            for j in range(4):
                P, Psw = nps(), nps()
                chain(P, [(wabc, 1024 + j * 128, 0)], 128, n)
                g = j % 2
                if latent:
                    chain(Psw, [(wsw, 768 + j * 128, 0)], 128, n)
                    rope_out(P, Psw, cs_.v[:, 0:n], ss_.v[:, 0:n], F[0].v[:, 0:n], n)
                else:
                    k.cp("act", F[0].v[:, 0:n], P.v[:, 0:n])
                x3 = F[0].v[:, 0:n].rr("p (c i) -> p c i", i=CH)
                for d in range(2):
                    if j < 2:
                        k.tt("dve", F[1].v[:, 0:n].rr("p (c i) -> p c i", i=CH), x3,
                             rtab.v[:, g, d, 0:1, :].bc([128, nch, CH]), ALU.mult)
                        k.stv("sp", S["C_Rt"][d, g * 128:(g + 1) * 128, t0:t0 + n], F[1].v[:, 0:n])
                    else:
                        k.tt("dve", F[1].v[:, 0:n].rr("p (c i) -> p c i", i=CH), x3,
                             rtab.v[:, g, d, 1:2, :].bc([128, nch, CH]), ALU.mult)
                        k.stv("sp", S["C_Kt"][d, g * 128:(g + 1) * 128, t0:t0 + n], F[1].v[:, 0:n])
                        k.tt("dve", F[2].v[:, 0:n].rr("p (c i) -> p c i", i=CH), x3,
                             rtab.v[:, g, d, 2:3, :].bc([128, nch, CH]), ALU.mult)
                        for s in range(ns):
                            pt = nps()
                            k.tr(pt.v[:, 0:128], F[2].v[:, s * 128:(s + 1) * 128], identf.v[:, :])
                            k.cp("act", tb.v[:, 0:128], pt.v[:, 0:128])
                            k.stv("sp", S["C_Kh"][d, t0 + s * 128:t0 + (s + 1) * 128, g * 128:(g + 1) * 128],
                                  tb.v[:, 0:128])
            for s in range(ns):
                P = nps()
                chain_tok(P, [(wabc, 1536, 0)], 256, s)
                k.cp("act", tb.v[:, 0:256], P.v[:, 0:256])
                k.stv("sp", S["C_V"][t0 + s * 128:t0 + (s + 1) * 128, :], tb.v[:, 0:256])
                P = nps()
                chain_tok(P, [(wabc, 1792, 0)], 512, s)
                k.act(tb.v[:, 0:512], P.v[:, 0:512], ACT.Silu)
                k.stv("sp", S["C_G"][t0 + s * 128:t0 + (s + 1) * 128, :], tb.v[:, 0:512])
            P = nps()
            chain(P, [(wgd, 0, 0)], 128, n)
            k.act(sgb.v[:, 0:n], P.v[:, 0:n], ACT.Sigmoid)
            for s in range(ns):
                P = nps()
                k.mm(P.v[:, 0:256], sgb.v[:, s * 128:(s + 1) * 128], g2s.v[:, :])
                k.cp("act", tb.v[:, 0:256], P.v[:, 0:256])
                k.stv("sp", S["D_G"][t0 + s * 128:t0 + (s + 1) * 128, :], tb.v[:, 0:256])
            for d in range(2):
                sh = -1 if d == 0 else 1
                wl = lambda c0: [(wdb[d], c0, 0), (wda[d], c0, sh)]
                P = nps()
                chain(P, wl(768), 64, n)
                k.act(thb.v[:, 0:n], P.v[0:64, 0:n], ACT.Tanh)
                P = nps()
                chain(P, wl(832), 64, n)
                k.cp("act", alb.v[:, 0:n], P.v[0:64, 0:n])
                for s in range(ns):
                    P = nps()
                    chain_tok(P, wl(512), 256, s)
                    k.cp("act", tb.v[:, 0:256], P.v[:, 0:256])
                    k.stv("sp", S["D_V"][d, t0 + s * 128:t0 + (s + 1) * 128, :], tb.v[:, 0:256])
                for g in range(2):
                    r_, k_, lw, cs, a_, kk, e_, t1, t2 = F[0], F[1], F[2], F[3], F[4], F[5], F[6], F[7], F[8]
                    gc = slice(g, g + 1)
                    P = nps()
                    chain(P, wl(g * 128), 128, n)
                    k.cp("act", r_.v[:, 0:n], P.v[:, 0:n])
                    P = nps()
                    chain(P, wl(256 + g * 128), 128, n)
                    k.cp("act", k_.v[:, 0:n], P.v[:, 0:n])
                    P = nps()
                    k.mm(P.v[:, 0:n], w2s.v[:, d, g * 128:(g + 1) * 128], thb.v[:, 0:n])
                    k.act(lw.v[:, 0:n], P.v[:, 0:n], ACT.Sigmoid, bias=pc.v[:, 4 + 2 * d + g:5 + 2 * d + g])
                    k.ts("dve", lw.v[:, 0:n], lw.v[:, 0:n], -DECAY_SCALE, None, ALU.mult)
                    P = nps()
                    k.mm(P.v[:, 0:n], a2s.v[:, d, g * 128:(g + 1) * 128], alb.v[:, 0:n])
                    k.act(a_.v[:, 0:n], P.v[:, 0:n], ACT.Sigmoid, bias=pc.v[:, 8 + 2 * d + g:9 + 2 * d + g])
                    k.ts("dve", kk.v[:, 0:n], k_.v[:, 0:n], pc.v[:, 12 + g:13 + g], None, ALU.mult)
                    k.act(t1.v[:, 0:n], kk.v[:, 0:n], ACT.Square)
                    P = nps()
                    k.mm(P.v[:, 0:n], bd.v[:, :], t1.v[:, 0:n])
                    k.act(t1.v[:, 0:n], P.v[:, 0:n], ACT.Sqrt)
                    k.ts("dve", t1.v[:, 0:n], t1.v[:, 0:n], 1e-12, None, ALU.max)
                    k.recip(t1.v[:, 0:n], t1.v[:, 0:n])
                    k.tt("dve", kk.v[:, 0:n], kk.v[:, 0:n], t1.v[:, 0:n], ALU.mult)
                    k.ts("dve", t1.v[:, 0:n], a_.v[:, 0:n], pc.v[:, 14 + g:15 + g], oma.v[:, gc], ALU.mult, ALU.add)
                    k.tt("dve", k_.v[:, 0:n], k_.v[:, 0:n], t1.v[:, 0:n], ALU.mult)
                    k.stt("dve", t1.v[:, 0:n], r_.v[:, 0:n], pc.v[:, 16 + 2 * d + g:17 + 2 * d + g], k_.v[:, 0:n],
                          ALU.mult, ALU.mult)
                    for s in range(ns):
                        P = nps()
                        k.mm(P.v[:, 0:2], t1.v[:, s * 128:(s + 1) * 128], e2.v[:, :])
                        k.cp("act", bs4.v[:, s, 2 * g:2 * g + 2], P.v[:, 0:2])
                    k.tt("dve", a_.v[:, 0:n], a_.v[:, 0:n], kk.v[:, 0:n], ALU.mult)
                    src, dst = lw, cs
                    k.cp("act", t2.v[:, 0:n], lw.v[:, 0:n])
                    src = t2
                    for stp in (1, 2, 4, 8, 16, 32):
                        s3 = src.v[:, 0:n].rr("p (c i) -> p c i", i=CH)
                        d3 = dst.v[:, 0:n].rr("p (c i) -> p c i", i=CH)
                        if d == 0:
                            k.tt("dve", d3[:, :, stp:], s3[:, :, stp:], s3[:, :, :CH - stp], ALU.add)
                            k.cp("act", d3[:, :, :stp], s3[:, :, :stp])
                        else:
                            k.tt("dve", d3[:, :, :CH - stp], s3[:, :, :CH - stp], s3[:, :, stp:], ALU.add)
                            k.cp("act", d3[:, :, CH - stp:], s3[:, :, CH - stp:])
                        src, dst = dst, src
                    cs = src
                    spare = dst
                    cs3 = cs.v[:, 0:n].rr("p (c i) -> p c i", i=CH)
                    tot3 = cs3[:, :, CH - 1:CH] if d == 0 else cs3[:, :, 0:1]
                    k.act(e_.v[:, 0:n], cs.v[:, 0:n], ACT.Exp)
                    k.tt("dve", t1.v[:, 0:n], r_.v[:, 0:n], e_.v[:, 0:n], ALU.mult)
                    k.stv("sp", S["D_Rt"][d, g * 128:(g + 1) * 128, t0:t0 + n], t1.v[:, 0:n])
                    k.tt("dve", e_.v[:, 0:n], cs.v[:, 0:n], lw.v[:, 0:n], ALU.subtract)
                    k.act(e_.v[:, 0:n], e_.v[:, 0:n], ACT.Exp)
                    k.stt("dve", t1.v[:, 0:n], kk.v[:, 0:n], -1.0, e_.v[:, 0:n], ALU.mult, ALU.mult)
                    k.stv("sp", S["D_At"][d, g * 128:(g + 1) * 128, t0:t0 + n], t1.v[:, 0:n])
                    for s in range(ns):
                        pt = nps()
                        k.tr(pt.v[:, 0:128], t1.v[:, s * 128:(s + 1) * 128], identf.v[:, :])
                        k.cp("act", tb.v[:, 0:128], pt.v[:, 0:128])
                        k.stv("sp", S["D_Atok"][d, t0 + s * 128:t0 + (s + 1) * 128, g * 128:(g + 1) * 128],
                              tb.v[:, 0:128])
                    k.act(e_.v[:, 0:n], cs.v[:, 0:n], ACT.Exp, scale=-1.0)
                    k.tt("dve", t1.v[:, 0:n], a_.v[:, 0:n], e_.v[:, 0:n], ALU.mult)
                    k.stv("sp", S["D_Bt"][d, g * 128:(g + 1) * 128, t0:t0 + n], t1.v[:, 0:n])
                    k.tt("dve", t1.v[:, 0:n], k_.v[:, 0:n], e_.v[:, 0:n], ALU.mult)
                    k.stv("sp", S["D_Kt"][d, g * 128:(g + 1) * 128, t0:t0 + n], t1.v[:, 0:n])
                    k.act(pcx.v[:, 0:nch], tot3.rr("p c o -> p (c o)"), ACT.Exp)
                    k.stv("sp", S["D_PC"][d, g * 128:(g + 1) * 128, t0 // CH:t0 // CH + nch], pcx.v[:, 0:nch])
                    k.tt("dve", e_.v[:, 0:n].rr("p (c i) -> p c i", i=CH), tot3.bc([128, nch, CH]), cs3, ALU.subtract)
                    k.act(e_.v[:, 0:n], e_.v[:, 0:n], ACT.Exp)
                    for (srcb, dstd) in ((a_, S["D_Bh"]), (k_, S["D_Kh"])):
                        k.tt("dve", t1.v[:, 0:n], srcb.v[:, 0:n], e_.v[:, 0:n], ALU.mult)
                        for s in range(ns):
                            pt = nps()
                            k.tr(pt.v[:, 0:128], t1.v[:, s * 128:(s + 1) * 128], identf.v[:, :])
                            k.cp("act", tb.v[:, 0:128], pt.v[:, 0:128])
                            k.stv("sp", dstd[d, t0 + s * 128:t0 + (s + 1) * 128, g * 128:(g + 1) * 128], tb.v[:, 0:128])
                for s in range(ns):
                    k.stv("sp", S["D_BS"][d, t0 + s * 128:t0 + (s + 1) * 128, :], bs4.v[:, s, :])


def attnphase(k, st, cfg):
    S, C, Wt = cfg["S"], cfg["C"], cfg["W"]
    T, CTX = cfg["T"], cfg["CTX"]
    TT = T + CTX
    NKC = TT // 128
    need_ctx = cfg["need_ctx"]
    kT = k.sb(st, [128, TT], BF16, "kT")
    va = k.sb(st, [128, NKC, 65], BF16, "va")
    qT = [k.sb(st, [128, 512], BF16, "qT%d" % i) for i in range(2)]
    for c0 in range(0, TT, 2048):
        k.memset("dve", kT.v[64:128, c0:min(TT, c0 + 2048)], 0.0)
    for q_ in qT:
        k.memset("dve", q_.v[64:128, :], 0.0)
    pT = [k.sb(st, [128, 512], BF16, "pT%d" % i) for i in range(3)]
    ot = [k.sb(st, [128, 64], F32, "ot%d" % i) for i in range(2)]
    rc = k.sb(st, [128, 1], F32, "rc")
    esk = k.sb(st, [128, 4], F32, "esk")
    mprev = k.sb(st, [128, 128], BF16, "mprev")
    mnext = k.sb(st, [128, 128], BF16, "mnext")
    k.ld("pool", mprev.v[:, :], C["mprev"])
    k.ld("pool", mnext.v[:, :], C["mnext"])
    k.ld("sp", esk.v[:, :], Wt["sink"].partition_broadcast(128))
    k.act(esk.v[:, :], esk.v[:, :], ACT.Exp)
    ps_s = [k.ps(st, [128, 512], F32, "pss%d" % i) for i in range(3)]
    ps_o = [k.ps(st, [128, 512], F32, "pso%d" % i) for i in range(4)]
    cnt = dict(s=0, q=0, o=0, p=0)
    scale = 64 ** -0.5

    def load_kv(kd, vd, g):
        k.ld("sp", kT.v[0:64, :], kd[g * 64:(g + 1) * 64, :])
        k.memset("dve", va.v[:, :, 64:65], 1.0)
        k.ld("sp", va.v[:, :, 0:64], vd[:, g * 64:(g + 1) * 64].rearrange("(c p) j -> p c j", p=128))

    def finish(acc, nq_sub, t0, col, sinkcol):
        for s in range(nq_sub):
            o = ot[cnt["o"] % 2]
            cnt["o"] += 1
            if sinkcol is None:
                k.recip(rc.v[:, :], acc[s].v[:, 64:65])
            else:
                k.tt("dve", rc.v[:, :], acc[s].v[:, 64:65], esk.v[:, sinkcol:sinkcol + 1], ALU.add)
                k.recip(rc.v[:, :], rc.v[:, :])
            k.ts("dve", o.v[:, :], acc[s].v[:, 0:64], rc.v[:, 0:1], None, ALU.mult)
            k.stv("sp", S["O"][t0 + s * 128:t0 + (s + 1) * 128, col:col + 64], o.v[:, :])

    def block(qd, h, t0, nq, kcs, col, sinkcol, masks=None):
        q = qT[cnt["q"] % 2]
        cnt["q"] += 1
        k.ld("sp", q.v[0:64, 0:nq], qd[h * 64:(h + 1) * 64, t0:t0 + nq])
        nsub = nq // 128
        for i, kc in enumerate(kcs):
            ps = ps_s[cnt["s"] % 3]
            cnt["s"] += 1
            k.mm(ps.v[:, 0:nq], kT.v[:, kc * 128:(kc + 1) * 128], q.v[:, 0:nq])
            p = pT[cnt["p"] % 3]
            cnt["p"] += 1
            k.act(p.v[:, 0:nq], ps.v[:, 0:nq], ACT.Exp, scale=scale)
            if masks is not None and masks[i] is not None:
                k.tt("dve", p.v[:, 0:nq], p.v[:, 0:nq], masks[i].v[:, :], ALU.mult)
            for s in range(nsub):
                k.mm(ps_o[s].v[:, 0:65], p.v[:, s * 128:(s + 1) * 128], va.v[:, kc, :], i == 0, i == len(kcs) - 1)
        finish(ps_o, nsub, t0, col, sinkcol)

    for g in range(2):
        load_kv(S["KB"], S["VB"], g)
        for h in (2 * g, 2 * g + 1):
            for t0 in range(0, T, 512):
                block(S["QB"], h, t0, min(512, T - t0), list(range(NKC)), 256 + h * 64, None)
            if need_ctx:
                block(S["QB"], h, T, CTX, list(range(T // 128, NKC)), 256 + h * 64, None)
    nb = T // 128
    cchunks = list(range(T // 128, NKC))
    for g in range(2):
        load_kv(S["KA"], S["VA"], g)
        for h in (2 * g, 2 * g + 1):
            for b in range(nb):
                kcs, ms = [], []
                if b > 0:
                    kcs.append(b - 1)
                    ms.append(mprev)
                kcs.append(b)
                ms.append(None)
                if b < nb - 1:
                    kcs.append(b + 1)
                    ms.append(mnext)
                kcs += cchunks
                ms += [None] * len(cchunks)
                block(S["QA"], h, b * 128, 128, kcs, h * 64, h, ms)
            if need_ctx:
                block(S["QA"], h, T, CTX, cchunks, h * 64, h)


def scanphase(k, st, cfg):
    S, C = cfg["S"], cfg["C"]
    T, CTX = cfg["T"], cfg["CTX"]
    TT = T + CTX
    dplr = cfg["dplr"]
    pre = "D_" if dplr else "C_"
    GT_ = 4 * CH
    NCHT = TT // CH
    f3 = lambda b: b.v[:, :, :]
    chm = lambda nm: k.sb(st, [64, 8, GT_], F32, nm)
    tkm = lambda nm: k.sb(st, [64, 4, 8, 64], F32, nm)
    RT = [chm("RT%d" % i) for i in range(2)]
    KT = [chm("KT%d" % i) for i in range(2)]
    KH = [tkm("KH%d" % i) for i in range(2)]
    VV = [tkm("VV%d" % i) for i in range(2)]
    if dplr:
        AT = [chm("AT%d" % i) for i in range(2)]
        BT = [chm("BT%d" % i) for i in range(2)]
        BH = [tkm("BH%d" % i) for i in range(2)]
        AK = [tkm("AK%d" % i) for i in range(2)]
        Mst = k.sb(st, [64, 8, 64], F32, "Mst")
        Mts = k.sb(st, [64, 8, 64], F32, "Mts")
        k.ld("sp", Mst.v[:, :, :], C["m_st"].rearrange("p (q t) -> p q t", q=8))
        k.ld("sp", Mts.v[:, :, :], C["m_ts"].rearrange("p (q t) -> p q t", q=8))
        Ls = [k.sb(st, [64, 8, 64], F32, "L%d" % i) for i in range(5)]
        Ns = [k.sb(st, [64, 8, 64], F32, "N%d" % i) for i in range(6)]
        AakT = k.sb(st, [64, 8, 64], F32, "AakT")
        ArbT = k.sb(st, [64, 8, 64], F32, "ArbT")
        X = [k.sb(st, [64, 8, 128], F32, "X%d" % i) for i in range(2)]
        Qe = k.sb(st, [64, 8, 64], F32, "Qe")
        PCt = k.sb(st, [64, 8, NCHT], F32, "PCt")
        for d in range(2):
            k.ld("sp", PCt.v[:, d * 4:(d + 1) * 4, :], S["D_PC"][d].rearrange("(h j) c -> j h c", j=64))
    else:
        PCr = k.sb(st, [64, 8], F32, "PCr")
        k.ld("sp", PCr.v[:, :], C["ret_pc"])
    Min = k.sb(st, [64, 8, 64], F32, "Min")
    k.ld("sp", Min.v[:, :, :], C["m_in"].rearrange("p (q t) -> p q t", q=8))
    id8 = k.sb(st, [64, 8, 64], F32, "id8")
    k.ld("sp", id8.v[:, :, :], C["id8"].rearrange("p (q t) -> p q t", q=8))
    ArkT = k.sb(st, [64, 8, 64], F32, "ArkT")
    GTt = k.sb(st, [64, 8, 64], F32, "GTt")
    ST = [k.sb(st, [64, 8, 64], F32, "ST%d" % i) for i in range(2)]
    Yt = [k.sb(st, [64, 8, 64], F32, "Yt%d" % i) for i in range(2)]
    psA = [k.ps(st, [64, 512], F32, "psA%d" % i) for i in range(4)]
    psX = k.ps(st, [64, 1024], F32, "psX")
    psY = [k.ps(st, [64, 512], F32, "psY%d" % i) for i in range(2)]
    cnt = dict(a=0, y=0, e=0)
    p3 = lambda p: p.v[:, :].rr("p (q t) -> p q t", q=8)

    def npa():
        p = psA[cnt["a"] % 4]
        cnt["a"] += 1
        return p

    def ev():
        cnt["e"] += 1
        return "act" if cnt["e"] % 2 else "dve"

    def prod(dst, lhs, rhs, mask):
        p = npa()
        for q in range(8):
            k.mm(p3(p)[:, q, :], lhs(q), rhs(q))
        if mask is None:
            k.cp(ev(), f3(dst), p3(p))
        else:
            k.tt("dve", f3(dst), p3(p), f3(mask), ALU.mult)

    k.memset("dve", f3(ST[0]), 0.0)
    ngl = T // GT_
    order = {0: [ngl] + list(range(ngl)), 1: [ngl] + list(range(ngl - 1, -1, -1))}
    step = 0
    for gi in range(ngl + 1):
        rb = gi % 2
        grp = {d: order[d][gi] for d in range(2)}
        for d in range(2):
            t0 = grp[d] * GT_
            qs = slice(d * 4, (d + 1) * 4)
            chv = lambda nm: S[pre + nm][d, :, t0:t0 + GT_].rearrange("(h j) t -> j h t", j=64)
            tkv = lambda ap: ap[t0:t0 + GT_, :].rearrange("(n t) (h j) -> t n h j", t=CH, j=64)
            k.ld("sp", RT[rb].v[:, qs, :], chv("Rt"))
            k.ld("sp", KT[rb].v[:, qs, :], chv("Kt"))
            k.ld("sp", KH[rb].v[:, :, qs, :], tkv(S[pre + "Kh"][d]))
            k.ld("sp", VV[rb].v[:, :, qs, :], tkv(S["D_V"][d] if dplr else S["C_V"]))
            if dplr:
                k.ld("sp", AT[rb].v[:, qs, :], chv("At"))
                k.ld("sp", BT[rb].v[:, qs, :], chv("Bt"))
                k.ld("sp", BH[rb].v[:, :, qs, :], tkv(S["D_Bh"][d]))
                k.ld("sp", AK[rb].v[:, :, qs, :], tkv(S["D_Atok"][d]))
        for ci in range(4):
            cl = {0: ci, 1: 3 - ci}
            sl = {d: slice(cl[d] * CH, (cl[d] + 1) * CH) for d in range(2)}
            dq = lambda q: q // 4
            chs = lambda buf: (lambda q: buf[rb].v[:, q, sl[dq(q)]])
            tks = lambda buf: (lambda q: buf[rb].v[:, cl[dq(q)], q, :])
            rt, kt, kh, vv = chs(RT), chs(KT), tks(KH), tks(VV)
            Sc, Sn = ST[step % 2], ST[(step + 1) % 2]
            prod(ArkT, kt, rt, Min)
            if dplr:
                at, bt, bh, ak = chs(AT), chs(BT), tks(BH), tks(AK)
                prod(Ns[0], bt, at, Mst)
                prod(Ls[0], at, bt, Mts)
                prod(AakT, kt, at, Mst)
                prod(ArbT, bt, rt, Min)
                for lv in range(5):
                    if lv < 4:
                        prod(Ls[lv + 1], lambda q: Ns[lv].v[:, q, :], lambda q: Ls[lv].v[:, q, :], None)
                    prod(Ns[lv + 1], lambda q: Ls[lv].v[:, q, :], lambda q: Ns[lv].v[:, q, :], None)
                xc, xn = X[0], X[1]
                for d in range(2):
                    k.cp("act", xc.v[:, d * 4:(d + 1) * 4, 0:64], AK[rb].v[:, cl[d], d * 4:(d + 1) * 4, :])
                p = npa()
                for q in range(8):
                    k.mm(p3(p)[:, q, :], AakT.v[:, q, :], vv(q))
                k.cp("dve", xc.v[:, :, 64:128], p3(p))
                for lv in range(5, -1, -1):
                    px = psX.v[:, :].rr("p (q c) -> p q c", q=8)
                    for q in range(8):
                        k.mm(px[:, q, :], Ns[lv].v[:, q, :], xc.v[:, q, :])
                    k.tt("dve", f3(xn), f3(xc), px, ALU.add)
                    xc, xn = xn, xc
                wq = lambda q: xc.v[:, q, 0:64]
                uv = lambda q: xc.v[:, q, 64:128]
                p = npa()
                for q in range(8):
                    k.mm(p3(p)[:, q, :], wq(q), ArbT.v[:, q, :])
                for d in range(2):
                    k.tt("dve", Qe.v[:, d * 4:(d + 1) * 4, :], p3(p)[:, d * 4:(d + 1) * 4, :],
                         RT[rb].v[:, d * 4:(d + 1) * 4, sl[d]], ALU.add)
                qe = lambda q: Qe.v[:, q, :]
                p = npa()
                for q in range(8):
                    k.mm(p3(p)[:, q, :], wq(q), bh(q))
                for d in range(2):
                    c_abs = grp[d] * 4 + cl[d]
                    k.tt("dve", GTt.v[:, d * 4:(d + 1) * 4, :], id8.v[:, d * 4:(d + 1) * 4, :],
                         PCt.v[:, d * 4:(d + 1) * 4, c_abs:c_abs + 1].bc([64, 4, 64]), ALU.mult)
                k.tt("dve", f3(GTt), f3(GTt), p3(p), ALU.add)
            else:
                qe = rt
                if step == 0:
                    k.tt("dve", f3(GTt), f3(id8), PCr.v[:, :].rr("p (q o) -> p q o", o=1).bc([64, 8, 64]), ALU.mult)
            py = psY[cnt["y"] % 2]
            cnt["y"] += 1
            for q in range(8):
                k.mm(p3(py)[:, q, :], ArkT.v[:, q, :], vv(q), True, False)
                if dplr:
                    k.mm(p3(py)[:, q, :], ArbT.v[:, q, :], uv(q), False, False)
                k.mm(p3(py)[:, q, :], qe(q), Sc.v[:, q, :], False, True)
            yt = Yt[step % 2]
            k.cp("act", f3(yt), p3(py))
            for d in range(2):
                tt0 = grp[d] * GT_ + cl[d] * CH
                k.stv("sp", S[pre + "Y"][d, tt0:tt0 + CH, :].rearrange("t (h i) -> t h i", i=64),
                      yt.v[:, d * 4:(d + 1) * 4, :])
            p = npa()
            for q in range(8):
                k.mm(p3(p)[:, q, :], kh(q), vv(q), True, False)
                if dplr:
                    k.mm(p3(p)[:, q, :], bh(q), uv(q), False, False)
                k.mm(p3(p)[:, q, :], GTt.v[:, q, :], Sc.v[:, q, :], False, True)
            k.cp("dve", f3(Sn), p3(p))
            step += 1


GN_EPS = 64e-5


def postphase(k, st, cfg):
    S, Wt = cfg["S"], cfg["W"]
    TT = cfg["T"] + cfg["CTX"]
    row = lambda nm, ap: (lambda t: (k.ld("sp", t.v[:, :], ap.partition_broadcast(128)), t)[1])(k.sb(st, [128, 256], F32, nm))
    retg = row("retg", Wt["ret_g"])
    lng = row("lng", Wt["ln_g"])
    lnb = row("lnb", Wt["ln_b"])
    y = [k.sb(st, [128, 4, 64], F32, "y%d" % i) for i in range(2)]
    yc = k.sb(st, [128, 4, 64], F32, "yc")
    sq = k.sb(st, [128, 4, 64], F32, "sq")
    m4 = k.sb(st, [128, 4], F32, "m4")
    v4 = k.sb(st, [128, 4], F32, "v4")
    gc = k.sb(st, [128, 512], F32, "gc")
    gd = k.sb(st, [128, 256], F32, "gd")
    vd = k.sb(st, [128, 4, 64], F32, "vd")
    bs = k.sb(st, [128, 4], F32, "bs")
    acc = k.sb(st, [128, 256], F32, "acc")
    yn = k.sb(st, [128, 256], F32, "yn")
    i = 0

    def head_norm(yb, g, b, dst):
        k.op("dve", lambda e: e.tensor_reduce(out=m4[:, :], in_=yb[:, :, :], axis=AX.X, op=ALU.add), r=[yb], w=[m4])
        k.ts("dve", m4.v[:, :], m4.v[:, :], 1.0 / 64, None, ALU.mult)
        k.tt("dve", yc.v[:, :, :], yb.v[:, :, :], m4.v[:, :].rr("p (h o) -> p h o", o=1).bc([128, 4, 64]), ALU.subtract)
        k.tt("dve", sq.v[:, :, :], yc.v[:, :, :], yc.v[:, :, :], ALU.mult)
        k.op("dve", lambda e: e.tensor_reduce(out=v4[:, :], in_=sq[:, :, :], axis=AX.X, op=ALU.add), r=[sq], w=[v4])
        k.ts("dve", v4.v[:, :], v4.v[:, :], 1.0 / 64, GN_EPS, ALU.mult, ALU.add)
        k.act(v4.v[:, :], v4.v[:, :], ACT.Sqrt)
        k.recip(v4.v[:, :], v4.v[:, :])
        k.tt("dve", yc.v[:, :, :], yc.v[:, :, :], v4.v[:, :].rr("p (h o) -> p h o", o=1).bc([128, 4, 64]), ALU.mult)
        k.tt("dve", dst, yc.v[:, :, :].rr("p h i -> p (h i)"), g.v[:, :], ALU.mult)
        if b is not None:
            k.tt("dve", dst, dst, b.v[:, :], ALU.add)

    for t0 in range(0, TT, 128):
        tsl = slice(t0, t0 + 128)
        k.ld("sp", gc.v[:, :], S["C_G"][tsl, :])
        for d in range(2):
            yb = y[i % 2]
            i += 1
            k.ld("sp", yb.v[:, :, :], S["C_Y"][d, tsl, :].rearrange("t (h i) -> t h i", i=64))
            head_norm(yb, retg, None, yn.v[:, :])
            if d == 0:
                k.tt("dve", acc.v[:, :], yn.v[:, :], gc.v[:, 0:256], ALU.mult)
            else:
                k.tt("dve", yn.v[:, :], yn.v[:, :], gc.v[:, 256:512], ALU.mult)
                k.tt("dve", acc.v[:, :], acc.v[:, :], yn.v[:, :], ALU.add)
        k.stv("sp", S["O"][tsl, 512:768], acc.v[:, :])
        k.ld("sp", gd.v[:, :], S["D_G"][tsl, :])
        for d in range(2):
            yb = y[i % 2]
            i += 1
            k.ld("sp", yb.v[:, :, :], S["D_Y"][d, tsl, :].rearrange("t (h i) -> t h i", i=64))
            k.ld("sp", vd.v[:, :, :], S["D_V"][d, tsl, :].rearrange("t (h i) -> t h i", i=64))
            k.ld("sp", bs.v[:, :], S["D_BS"][d, tsl, :])
            head_norm(yb, lng, lnb, yn.v[:, :])
            k.tt("dve", vd.v[:, :, :], vd.v[:, :, :], bs.v[:, :].rr("p (h o) -> p h o", o=1).bc([128, 4, 64]), ALU.mult)
            k.tt("dve", yn.v[:, :], yn.v[:, :], vd.v[:, :, :].rr("p h i -> p (h i)"), ALU.add)
            if d == 0:
                k.cp("act", acc.v[:, :], yn.v[:, :])
            else:
                k.tt("dve", acc.v[:, :], acc.v[:, :], yn.v[:, :], ALU.add)
        k.tt("dve", acc.v[:, :], acc.v[:, :], gd.v[:, :], ALU.mult)
        k.stv("sp", S["O"][tsl, 768:1024], acc.v[:, :])


def modphase(k, st, cfg):
    cv = k.sb(st, [128, 2, 8], F32, "cv")
    for r in range(2):
        k.ld("sp", cv.v[:, r, :], cfg["cvec"][r].rearrange("(c p) -> p c", p=128), allow_slow_non_contiguous=True)
    sc = k.sb(st, [128, 8, 2], F32, "sc")
    k.act(sc.v[:, :, :].rr("p c r -> p r c"), cv.v[:, :, :], ACT.Silu)
    wm = [k.sb(st, [128, 8, 512], F32, "wm%d" % i) for i in range(2)]
    bm = k.sb(st, [2, 9 * D], F32, "bm")
    k.ld("sp", bm.v[:, :], cfg["b_mod"].partition_broadcast(2))
    ob = k.sb(st, [2, 9 * D], F32, "ob")
    pm = [k.ps(st, [128, 512], F32, "pm%d" % i) for i in range(2)]
    wv = cfg["w_mod"].rearrange("(c p) f -> p c f", p=128)
    for j in range(18):
        w = wm[j % 2]
        for c in range(8):
            k.ld("sp", w.v[:, c, :], wv[:, c, j * 512:(j + 1) * 512])
        p = pm[j % 2]
        for c in range(8):
            k.mm(p.v[0:2, :], sc.v[:, c, :], w.v[:, c, :], c == 0, c == 7)
        k.tt("dve", ob.v[:, j * 512:(j + 1) * 512], p.v[0:2, :], bm.v[:, j * 512:(j + 1) * 512], ALU.add)
    k.stv("sp", cfg["mod"], ob.v[:, :])


def build_program(T, CTX, L, stop_after=None):
    TT = T + CTX
    nc = bass.Bass("TRN2", target_bir_lowering=False, dynamic_dma_scratch_size=8192)

    def din(name, shape, dt=F32):
        return nc.dram_tensor(name, list(shape), dt, kind="ExternalInput").ap()

    def scr(name, shape, dt=F32):
        return nc.dram_tensor(name, list(shape), dt, kind="Internal").ap()

    I = dict(
        xin=din("xin", [TT, D]), cvec=din("cvec", [2, D]),
        w_mod=din("w_mod", [L, D, 9 * D]), b_mod=din("b_mod", [L, 9 * D]), norm_g=din("norm_g", [L, 3, D]),
        ffn_w_in=din("ffn_w_in", [L, 2, D, 2 * DFF]), ffn_w_out=din("ffn_w_out", [L, 2, DFF, D]),
        w_in=din("w_in", [L, D, 3456]), w_out=din("w_out", [L, D, D]), final_g=din("final_g", [D]),
        w_sw=din("w_sw", [L, D, 1280]), w_d=din("w_d", [L, 2, D, 896]), pcols=din("pcols", [L, 128, 24]),
        mu=din("mu", [L, 2, 896]), w2=din("w2", [L, 2, 64, 256]), a2=din("a2", [L, 2, 64, 256]),
        g2=din("g2", [L, 128, 256]), ret_g=din("ret_g", [L, 256]), ln_g=din("ln_g", [L, 256]),
        ln_b=din("ln_b", [L, 256]), sink=din("sink", [L, 4]),
    )
    Cn = dict(
        ident=din("ident", [128, 128]), bd64=din("bd64", [128, 128]), e2=din("e2", [128, 2]),
        ropeA_c=din("ropeA_c", [128, T]), ropeA_s=din("ropeA_s", [128, T]),
        ropeS_c=din("ropeS_c", [128, T]), ropeS_s=din("ropeS_s", [128, T]),
        ret_tab=din("ret_tab", [128, 2, 2, 3, CH]), ret_pc=din("ret_pc", [64, 8]),
        m_st=din("m_st", [64, 512]), m_ts=din("m_ts", [64, 512]), m_in=din("m_in", [64, 512]),
        id8=din("id8", [64, 512]), mprev=din("mprev", [128, 128]), mnext=din("mnext", [128, 128]),
    )
    out = nc.dram_tensor("out", [T, D], F32, kind="ExternalOutput").ap()
    dbg = stop_after is not None
    mk = (lambda name, shape, dt=F32: nc.dram_tensor(name, list(shape), dt, kind="ExternalOutput").ap()) if dbg else scr
    S = dict(
        xres=mk("xres", [TT, D]), mod=mk("modr", [2, 9 * D]), hT=mk("hT", [D, TT], BF16),
        QA=mk("QA", [256, TT], BF16), KA=mk("KA", [128, TT], BF16), VA=mk("VA", [TT, 128], BF16),
        QB=mk("QB", [256, TT], BF16), KB=mk("KB", [128, TT], BF16), VB=mk("VB", [TT, 128], BF16),
        O=mk("O", [TT, 1024]),
        C_Rt=mk("C_Rt", [2, 256, TT]), C_Kt=mk("C_Kt", [2, 256, TT]), C_Kh=mk("C_Kh", [2, TT, 256]),
        C_V=mk("C_V", [TT, 256]), C_G=mk("C_G", [TT, 512]), C_Y=mk("C_Y", [2, TT, 256]),
        D_Rt=mk("D_Rt", [2, 256, TT]), D_Kt=mk("D_Kt", [2, 256, TT]), D_At=mk("D_At", [2, 256, TT]),
        D_Bt=mk("D_Bt", [2, 256, TT]), D_Kh=mk("D_Kh", [2, TT, 256]), D_Bh=mk("D_Bh", [2, TT, 256]),
        D_Atok=mk("D_Atok", [2, TT, 256]), D_V=mk("D_V", [2, TT, 256]), D_Y=mk("D_Y", [2, TT, 256]),
        D_PC=mk("D_PC", [2, 256, TT // CH]), D_G=mk("D_G", [TT, 256]), D_BS=mk("D_BS", [2, TT, 4]),
    )
    k = K(nc)
    stages = []

    def phase(name, fn, cfg):
        if stop_after is not None and stop_after in stages:
            return
        with contextlib.ExitStack() as st:
            fn(k, st, cfg)
            k.barrier()
        stages.append(name)
        k.marks.append((name, {e.name: e.n for e in k.E.values()}))

    with k.stack:
        segs_all = [(0, T, 0), (T, CTX, 1)]
        for l in range(L):
            last = l == L - 1
            W = dict(w_in=I["w_in"][l], w_sw=I["w_sw"][l], w_d=I["w_d"][l], mu=I["mu"][l], pcols=I["pcols"][l],
                     w2=I["w2"][l], a2=I["a2"][l], g2=I["g2"][l], ret_g=I["ret_g"][l], ln_g=I["ln_g"][l],
                     ln_b=I["ln_b"][l], sink=I["sink"][l])
            base = dict(S=S, C=Cn, W=W, T=T, CTX=CTX)
            phase("mod%d" % l, modphase, dict(cvec=I["cvec"], w_mod=I["w_mod"][l], b_mod=I["b_mod"][l], mod=S["mod"]))
            phase("f1_%d" % l, rowphase, dict(
                xin=I["xin"] if l == 0 else S["xres"], xout=S["xres"], segs=segs_all, mod=S["mod"], ident=Cn["ident"],
                ffn=dict(w_in=I["ffn_w_in"][l, 0], w_out=I["ffn_w_out"][l, 0], g=I["norm_g"][l, 0], shift=0, scale=1,
                         gate=2),
                post=dict(kind="hT", g=I["norm_g"][l, 1], shift=3, scale=4, hT=S["hT"])))
            phase("proj%d" % l, projphase, base)
            phase("attn%d" % l, attnphase, dict(base, need_ctx=not last))
            phase("scanC%d" % l, scanphase, dict(base, dplr=False))
            phase("scanD%d" % l, scanphase, dict(base, dplr=True))
            phase("post%d" % l, postphase, base)
            phase("f2_%d" % l, rowphase, dict(
                xin=S["xres"], xout=S["xres"], segs=segs_all if not last else [(0, T, 0)], mod=S["mod"],
                ident=Cn["ident"],
                pre=dict(o=S["O"], w_out=I["w_out"][l], gate=5),
                ffn=dict(w_in=I["ffn_w_in"][l, 1], w_out=I["ffn_w_out"][l, 1], g=I["norm_g"][l, 2], shift=6, scale=7,
                         gate=8),
                post=dict(kind="final", g=I["final_g"], out=out) if last else None))
    return nc, k


def _consts(T):
    c = {}
    c["ident"] = np.eye(128, dtype=np.float32)
    bd = np.zeros((128, 128), np.float32)
    bd[:64, :64] = 1
    bd[64:, 64:] = 1
    c["bd64"] = bd
    e2 = np.zeros((128, 2), np.float32)
    e2[:64, 0] = 1
    e2[64:, 1] = 1
    c["e2"] = e2
    t = np.arange(T)
    row = (t // 64).astype(np.float32)
    col = (t % 64).astype(np.float32)
    inv16 = (1.0 / (np.float32(10000.0) ** (np.arange(0, 32, 2, dtype=np.float32) / np.float32(32)))).astype(np.float32)
    inv32 = (1.0 / (np.float32(10000.0) ** (np.arange(0, 64, 2, dtype=np.float32) / np.float32(64)))).astype(np.float32)
    ca = np.zeros((64, T), np.float32)
    sa = np.zeros((64, T), np.float32)
    for blk, pos in ((0, row), (32, col)):
        ang = pos[None, :] * inv16[:, None]
        ca[blk:blk + 16] = np.cos(ang)
        ca[blk + 16:blk + 32] = np.cos(ang)
        sa[blk:blk + 16] = -np.sin(ang)
        sa[blk + 16:blk + 32] = np.sin(ang)
    ang = t.astype(np.float32)[None, :] * inv32[:, None]
    cs = np.concatenate([np.cos(ang), np.cos(ang)], 0).astype(np.float32)
    ss = np.concatenate([-np.sin(ang), np.sin(ang)], 0).astype(np.float32)
    c["ropeA_c"] = np.concatenate([ca, ca], 0)
    c["ropeA_s"] = np.concatenate([sa, sa], 0)
    c["ropeS_c"] = np.concatenate([cs, cs], 0)
    c["ropeS_s"] = np.concatenate([ss, ss], 0)
    lg = np.log1p(-np.exp2(-5.0 - np.arange(4, dtype=np.float64)))
    i = np.arange(CH, dtype=np.float64)
    rt = np.zeros((128, 2, 2, 3, CH), np.float64)
    for p in range(128):
        for g in range(2):
            l_ = lg[2 * g + p // 64]
            for d in range(2):
                csum = (i + 1) * l_ if d == 0 else (CH - i) * l_
                rt[p, g, d, 0] = np.exp(csum)
                rt[p, g, d, 1] = np.exp(-csum) / 8.0
                rt[p, g, d, 2] = np.exp(CH * l_ - csum) / 8.0
    c["ret_tab"] = rt.astype(np.float32)
    c["ret_pc"] = np.tile(np.exp(CH * lg)[None, :], (64, 2)).astype(np.float32)
    s_ = np.arange(64)[:, None]
    t_ = np.arange(64)[None, :]
    lt = (s_ < t_).astype(np.float32)
    gt = (s_ > t_).astype(np.float32)
    eye = np.eye(64, dtype=np.float32)
    c["m_st"] = np.concatenate([lt] * 4 + [gt] * 4, 1)
    c["m_ts"] = np.concatenate([gt] * 4 + [lt] * 4, 1)
    c["m_in"] = np.concatenate([lt + eye] * 4 + [gt + eye] * 4, 1)
    c["id8"] = np.concatenate([eye] * 8, 1)
    j = np.arange(128)[:, None]
    ii = np.arange(128)[None, :]
    c["mprev"] = (ii <= j).astype(np.float32)
    c["mnext"] = (j <= ii).astype(np.float32)
    return c


def _host_maps(inp, T, CTX, L):
    f = lambda a: np.ascontiguousarray(np.asarray(a, dtype=np.float32))
    w_in = f(inp["w_in"])
    pa = np.concatenate([np.arange(16, 32), np.arange(0, 16), np.arange(48, 64), np.arange(32, 48)])
    psq = np.concatenate([np.arange(32, 64), np.arange(0, 32)])
    cols = []
    for base, nh, perm in ((0, 4, pa), (256, 2, pa), (512, 4, pa), (768, 2, pa), (1024, 4, psq), (1280, 4, psq)):
        for h in range(nh):
            cols.append(base + h * 64 + perm)
    cols = np.concatenate(cols)
    w_sw = np.ascontiguousarray(w_in[:, :, cols])
    dcols = [np.concatenate([np.arange(2304, 3072), np.arange(3200, 3264), np.arange(3328, 3392)]),
             np.concatenate([np.arange(2304, 3072), np.arange(3264, 3328), np.arange(3392, 3456)])]
    w_d = np.ascontiguousarray(np.stack([w_in[:, :, dc] for dc in dcols], 1))
    qkg = f(inp["qk_norm_g"])
    pcols = np.zeros((L, 128, 24), np.float32)
    two = lambda v: np.ascontiguousarray(v.reshape(2, 128).T)
    for l in range(L):
        for qk in range(2):
            g = qkg[l, qk]
            pcols[l, :, 2 * qk] = np.tile(g, 2)
            pcols[l, :, 2 * qk + 1] = np.tile(g[pa], 2)
        for d in range(2):
            pcols[l, :, 4 + 2 * d:6 + 2 * d] = two(f(inp["rwkv_w0"])[l, d])
            pcols[l, :, 8 + 2 * d:10 + 2 * d] = two(f(inp["rwkv_a0"])[l, d])
            pcols[l, :, 16 + 2 * d:18 + 2 * d] = two(f(inp["rwkv_rho"])[l, d].reshape(256))
        pcols[l, :, 12:14] = two(f(inp["rwkv_k_k"])[l])
        pcols[l, :, 14:16] = two(f(inp["rwkv_k_a"])[l])
    shared = dict(
        w_mod=f(inp["w_mod"]), b_mod=f(inp["b_mod"]), norm_g=f(inp["norm_g"]), ffn_w_in=f(inp["ffn_w_in"]),
        ffn_w_out=f(inp["ffn_w_out"]), w_in=w_in, w_out=f(inp["w_out"]), final_g=f(inp["final_norm_g"]),
        w_sw=w_sw, w_d=w_d, pcols=pcols, mu=f(inp["rwkv_mu"]), w2=f(inp["rwkv_w2"]), a2=f(inp["rwkv_a2"]),
        g2=f(inp["rwkv_g2"]), ret_g=f(inp["ret_norm_g"]), ln_g=f(inp["rwkv_ln_g"]), ln_b=f(inp["rwkv_ln_b"]),
        sink=f(inp["attn_sink"]))
    shared.update(_consts(T))
    x, c, ctx, c_ctx = f(inp["x"]), f(inp["c"]), f(inp["ctx"]), f(inp["c_ctx"])
    maps = []
    for b in range(x.shape[0]):
        m = dict(shared)
        m["xin"] = np.ascontiguousarray(np.concatenate([x[b], ctx[b]], 0))
        m["cvec"] = np.ascontiguousarray(np.stack([c[b], c_ctx], 0))
        maps.append(m)
    return maps


_PROG = {}


def kernel(**inputs):
    x = np.asarray(inputs["x"])
    B, T, _ = x.shape
    CTX = np.asarray(inputs["ctx"]).shape[1]
    L = np.asarray(inputs["w_in"]).shape[0]
    key = (T, CTX, L)
    if key not in _PROG:
        _PROG[key] = build_program(T, CTX, L)[0]
    nc = _PROG[key]
    maps = _host_maps(inputs, T, CTX, L)
    res = run_bass_kernel_spmd(nc, maps, core_ids=list(range(B)))
    return np.stack([np.asarray(r["out"], dtype=np.float32) for r in res.results], 0)
```

```python
import contextlib
import numpy as np
import concourse.bass as bass
import concourse.mybir as mybir
from concourse.bass_utils import run_bass_kernel_spmd

ACT = mybir.ActivationFunctionType
ALU = mybir.AluOpType
AX = mybir.AxisListType
F32 = mybir.dt.float32
BF16 = mybir.dt.bfloat16

D = 1024
DFF = 2816
NMOD = 9
RMS_EPS = 1e-6
STRICT = True


class PSem:
    def __init__(self, h):
        self.h = h
        self.val = 0


class Eng:
    def __init__(self, name, h, sem):
        self.name, self.h, self.sem = name, h, sem
        self.n = 0
        self.seen = {}
        self.dseen = {}


class Buf:
    def __init__(self, k, t, name):
        self.k, self.t, self.name = k, t, name
        self.w = None
        self.r = {}
        self.psem = None
        self.dma_w = 0
        self.dma_rw = 0

    def __getitem__(self, key):
        return self.t[key]

    @property
    def v(self):
        return _VA(self)


class View:
    def __init__(self, buf, ap):
        self.buf, self.ap = buf, ap

    def rr(self, pat, **kw):
        return View(self.buf, self.ap.rearrange(pat, **kw))

    def bc(self, shape):
        return View(self.buf, self.ap.to_broadcast(list(shape)))

    def __getitem__(self, key):
        return View(self.buf, self.ap[key])


class _VA:
    def __init__(self, b):
        self.b = b

    def __getitem__(self, key):
        return View(self.b, self.b.t[key])


def _bufs(*xs):
    out = []
    for x in xs:
        if isinstance(x, View) and x.buf not in out:
            out.append(x.buf)
    return out


def _ap(x):
    return x.ap if isinstance(x, View) else x


class K:
    def __init__(self, nc):
        self.nc = nc
        self.stack = contextlib.ExitStack()
        self.E = {}
        for name, h in (("pe", nc.tensor), ("act", nc.scalar), ("dve", nc.vector),
                        ("pool", nc.gpsimd), ("sp", nc.sync)):
            sem = self.stack.enter_context(nc.semaphore("s_" + name))
            self.E[name] = Eng(name, h, sem)
        self.free_psems = []
        self.n_psems = 0
        self.all_psems = []
        self.bufs = []
        self.uid = 0
        self.ninstr = 0
        self.marks = []

    def _psem(self):
        if self.free_psems:
            return self.free_psems.pop()
        self.n_psems += 1
        h = self.stack.enter_context(self.nc.semaphore("d%d" % self.n_psems))
        p = PSem(h)
        self.all_psems.append(p)
        return p

    def sb(self, st, shape, dtype, name):
        self.uid += 1
        t = st.enter_context(self.nc.sbuf_tensor("%s_%d" % (name, self.uid), list(shape), dtype))
        b = Buf(self, t, name)
        st.callback(self._release, b)
        return b

    def ps(self, st, shape, dtype, name):
        self.uid += 1
        t = st.enter_context(self.nc.psum_tensor("%s_%d" % (name, self.uid), list(shape), dtype))
        return Buf(self, t, name)

    def _release(self, b):
        if b.psem is not None and not getattr(b.psem, "sw", False):
            self.free_psems.append(b.psem)
        b.psem = None

    def _wait_eng(self, E, X, n):
        if E.seen.get(X.name, 0) >= n:
            return
        E.h.wait_ge(X.sem, n)
        E.seen[X.name] = n
        self.ninstr += 1

    def _wait_d(self, E, p, v):
        if v <= 0 or E.dseen.get(id(p), 0) >= v:
            return
        E.h.wait_ge(p.h, v)
        E.dseen[id(p)] = v
        self.ninstr += 1

    def _deps(self, E, r, w):
        for b in r:
            if b.w is not None:
                self._wait_eng(E, b.w[0], b.w[1])
            if b.psem is not None:
                self._wait_d(E, b.psem, b.dma_w)
        for b in w:
            if b.w is not None and (STRICT or b.w[0] is not E) and not (E.name == "pe" and b.w[0] is E):
                self._wait_eng(E, b.w[0], b.w[1])
            for X, n in b.r.items():
                if STRICT or X is not E:
                    self._wait_eng(E, X, n)
            if b.psem is not None:
                self._wait_d(E, b.psem, b.dma_rw)

    def op(self, eng, fn, r=(), w=()):
        E = self.E[eng]
        self._deps(E, r, w)
        ins = fn(E.h)
        E.n += 1
        ins.then_inc(E.sem, 1)
        self.ninstr += 1
        for b in w:
            b.w = (E, E.n)
            b.r = {}
        for b in r:
            if b.w is None or b.w[0] is not E or b.w[1] != E.n:
                b.r[E] = E.n
        return ins

    def dma(self, q, out, in_, r=(), w=(), **kw):
        Q = self.E[q]
        self._deps(Q, r, w)
        anchor = w[0] if w else r[0]
        if anchor.psem is None:
            if q == "pool":
                self.n_psems += 1
                anchor.psem = PSem(self.stack.enter_context(self.nc.semaphore("w%d" % self.n_psems)))
                anchor.psem.sw = True
                self.all_psems.append(anchor.psem)
            else:
                anchor.psem = self._psem()
        p = anchor.psem
        assert getattr(p, "sw", False) == (q == "pool"), "buffer mixes SW and HW DGE DMAs: " + anchor.name
        Q.h.dma_start(out=out, in_=in_, **kw).then_inc(p.h, 16)
        p.val += 16
        self.ninstr += 1
        anchor.dma_rw = p.val
        if w:
            anchor.dma_w = p.val
            anchor.w = None
            anchor.r = {}

    def mm(self, out, lhsT, rhs, start=True, stop=True):
        return self.op("pe", lambda e: e.matmul(out.ap, lhsT.ap, rhs.ap, start=start, stop=stop),
                       r=_bufs(lhsT, rhs), w=[out.buf])

    def tr(self, out, in_, ident):
        return self.op("pe", lambda e: e.transpose(out.ap, in_.ap, ident.ap), r=_bufs(in_, ident), w=[out.buf])

    def tt(self, eng, out, in0, in1, op):
        return self.op(eng, lambda e: e.tensor_tensor(out=out.ap, in0=in0.ap, in1=in1.ap, op=op),
                       r=_bufs(in0, in1), w=[out.buf])

    def ts(self, eng, out, in0, s1, s2, op0, op1=None):
        kw = {} if op1 is None else dict(op1=op1)
        return self.op(eng, lambda e: e.tensor_scalar(out=out.ap, in0=in0.ap, scalar1=_ap(s1), scalar2=_ap(s2),
                                                      op0=op0, **kw), r=_bufs(in0, s1, s2), w=[out.buf])

    def stt(self, eng, out, in0, scalar, in1, op0, op1):
        return self.op(eng, lambda e: e.scalar_tensor_tensor(out=out.ap, in0=in0.ap, scalar=_ap(scalar), in1=in1.ap,
                                                             op0=op0, op1=op1), r=_bufs(in0, scalar, in1), w=[out.buf])

    def act(self, out, in_, func, bias=None, scale=None, accum=None):
        kw = {}
        if bias is not None:
            kw["bias"] = _ap(bias)
        if scale is not None:
            kw["scale"] = _ap(scale)
        w = [out.buf]
        if accum is not None:
            kw["accum_out"] = accum.ap
            w.append(accum.buf)
        return self.op("act", lambda e: e.activation(out=out.ap, in_=in_.ap, func=func, **kw),
                       r=_bufs(in_, bias, scale), w=w)

    def cp(self, eng, out, in_):
        if eng == "act":
            return self.op("act", lambda e: e.copy(out=out.ap, in_=in_.ap), r=[in_.buf], w=[out.buf])
        return self.op(eng, lambda e: e.tensor_copy(out=out.ap, in_=in_.ap), r=[in_.buf], w=[out.buf])

    def memset(self, eng, out, val):
        return self.op(eng, lambda e: e.memset(out.ap, val), w=[out.buf])

    def recip(self, out, in_):
        return self.op("dve", lambda e: e.reciprocal(out=out.ap, in_=in_.ap), r=[in_.buf], w=[out.buf])

    def ld(self, q, out, dram, **kw):
        return self.dma(q, out.ap, dram, w=[out.buf], **kw)

    def stv(self, q, dram, in_, **kw):
        return self.dma(q, dram, in_.ap, r=[in_.buf], **kw)

    def barrier(self):
        for E in self.E.values():
            for X in self.E.values():
                if X is not E and X.n > 0:
                    self._wait_eng(E, X, X.n)
            for p in self.all_psems:
                self._wait_d(E, p, p.val)


def mm(k, out_ap, lhsT_ap, rhs_ap, start, stop, r, w):
    return k.op("pe", lambda e: e.matmul(out_ap, lhsT_ap, rhs_ap, start=start, stop=stop), r=r, w=w)


def rowphase(k, st, cfg):
    nc = k.nc
    ffn = cfg["ffn"]
    pre = cfg.get("pre")
    post = cfg.get("post")
    w1 = k.sb(st, [128, 8, 2 * DFF], BF16, "w1")
    w2 = k.sb(st, [128, 22, D], BF16, "w2")
    w1v = ffn["w_in"].rearrange("(c p) f -> p c f", p=128)
    for c in range(8):
        for hlf in range(4):
            f0 = hlf * 1408
            k.dma("pool", w1[:, c, f0:f0 + 1408], w1v[:, c, f0:f0 + 1408], w=[w1])
    w2v = ffn["w_out"].rearrange("(c p) d -> p c d", p=128)
    for c in range(22):
        k.dma("pool", w2[:, c, :], w2v[:, c, :], w=[w2])
    if pre is not None:
        wo = k.sb(st, [128, 8, D], BF16, "wo")
        wov = pre["w_out"].rearrange("(c p) d -> p c d", p=128)
        for c in range(8):
            k.dma("pool", wo[:, c, :], wov[:, c, :], w=[wo])

    identf = k.sb(st, [128, 128], F32, "identf")
    k.dma("sp", identf[:, :], cfg["ident"], w=[identf])
    xt = [k.sb(st, [128, D], F32, "xt%d" % i) for i in range(4)]
    hT = k.sb(st, [128, 8, 512], BF16, "hT")
    gT = k.sb(st, [128, 22, 512], BF16, "gT")
    sg = k.sb(st, [128, 512], BF16, "sg")
    tmp = k.sb(st, [128, D], F32, "tmp")
    ssq = k.sb(st, [128, 1], F32, "ssq")
    rinv = k.sb(st, [128, 1], F32, "rinv")
    ps_u = [k.ps(st, [128, 512], F32, "psu%d" % i) for i in range(4)]
    ps_y = [k.ps(st, [128, 512], F32, "psy%d" % i) for i in range(2)]
    ps_t = k.ps(st, [128, D], F32, "pst")
    Ac = k.sb(st, [128, 8], F32, "Ac")
    Bc = k.sb(st, [128, 8], F32, "Bc")
    Gc = k.sb(st, [128, 8], F32, "Gc")
    A2c = k.sb(st, [128, 8], F32, "A2c")
    B2c = k.sb(st, [128, 8], F32, "B2c")
    G = k.sb(st, [128, D], F32, "G")
    PG = k.sb(st, [128, D], F32, "PG") if pre is not None else None
    gvec2 = None
    if post is not None and post["kind"] == "final":
        gvec2 = k.sb(st, [128, D], F32, "gvec2")
        k.dma("sp", gvec2[:, :], post["g"].partition_broadcast(128), w=[gvec2])

    def col(ap1d):
        return ap1d.rearrange("(c p) -> p c", p=128)

    def load_cols(dst, ap1d):
        k.dma("sp", dst[:, :], col(ap1d), w=[dst], allow_slow_non_contiguous=True)

    def setup_mod(mr):
        load_cols(Gc, ffn["g"])
        load_cols(Ac, cfg["mod"][mr, ffn["scale"] * D:(ffn["scale"] + 1) * D])
        load_cols(Bc, cfg["mod"][mr, ffn["shift"] * D:(ffn["shift"] + 1) * D])
        k.op("dve", lambda e: e.scalar_tensor_tensor(out=Ac[:, :], in0=Ac[:, :], scalar=1.0, in1=Gc[:, :],
                                                     op0=ALU.add, op1=ALU.mult), r=[Ac, Gc], w=[Ac])
        if post is not None and post["kind"] == "hT":
            load_cols(Gc, post["g"])
            load_cols(A2c, cfg["mod"][mr, post["scale"] * D:(post["scale"] + 1) * D])
            load_cols(B2c, cfg["mod"][mr, post["shift"] * D:(post["shift"] + 1) * D])
            k.op("dve", lambda e: e.scalar_tensor_tensor(out=A2c[:, :], in0=A2c[:, :], scalar=1.0, in1=Gc[:, :],
                                                         op0=ALU.add, op1=ALU.mult), r=[A2c, Gc], w=[A2c])
        k.dma("sp", G[:, :], cfg["mod"][mr, ffn["gate"] * D:(ffn["gate"] + 1) * D].partition_broadcast(128), w=[G])
        k.op("dve", lambda e: e.tensor_scalar(out=G[:, :], in0=G[:, :], scalar1=0.5, scalar2=None, op0=ALU.mult),
             r=[G], w=[G])
        if pre is not None:
            k.dma("sp", PG[:, :], cfg["mod"][mr, pre["gate"] * D:(pre["gate"] + 1) * D].partition_broadcast(128),
                  w=[PG])

    def rms_rinv(xb):
        k.op("act", lambda e: e.activation(out=tmp[:, :], in_=xb[:, :], func=ACT.Square, accum_out=ssq[:, :]),
             r=[xb], w=[tmp, ssq])
        k.op("dve", lambda e: e.tensor_scalar(out=rinv[:, :], in0=ssq[:, :], scalar1=1.0 / D, scalar2=RMS_EPS,
                                              op0=ALU.mult, op1=ALU.add), r=[ssq], w=[rinv])
        k.op("act", lambda e: e.activation(out=rinv[:, :], in_=rinv[:, :], func=ACT.Sqrt), r=[rinv], w=[rinv])
        k.op("dve", lambda e: e.reciprocal(out=rinv[:, :], in_=rinv[:, :]), r=[rinv], w=[rinv])

    def norm_mod_T(xb, A, B, dstT, s):
        rms_rinv(xb)
        k.op("dve", lambda e: e.tensor_scalar(out=tmp[:, :], in0=xb[:, :], scalar1=rinv[:, 0:1], scalar2=None,
                                              op0=ALU.mult), r=[xb, rinv], w=[tmp])
        for c in range(8):
            k.op("pe", lambda e: e.transpose(ps_t[:, c * 128:(c + 1) * 128], tmp[:, c * 128:(c + 1) * 128],
                                             identf[:, :]), r=[tmp, identf], w=[ps_t])
        for c in range(8):
            if c % 2 == 0:
                k.op("dve", lambda e: e.tensor_scalar(out=dstT[:, c, s * 128:(s + 1) * 128],
                                                      in0=ps_t[:, c * 128:(c + 1) * 128], scalar1=A[:, c:c + 1],
                                                      scalar2=B[:, c:c + 1], op0=ALU.mult, op1=ALU.add),
                     r=[ps_t, A, B], w=[dstT])
            else:
                k.op("act", lambda e: e.activation(out=dstT[:, c, s * 128:(s + 1) * 128],
                                                   in_=ps_t[:, c * 128:(c + 1) * 128], func=ACT.Identity,
                                                   scale=A[:, c:c + 1], bias=B[:, c:c + 1]),
                     r=[ps_t, A, B], w=[dstT])

    ui = 0
    yi = 0
    for (tok0, ntok, mr) in cfg["segs"]:
        setup_mod(mr)
        for t0 in range(tok0, tok0 + ntok, 512):
            n = min(512, tok0 + ntok - t0)
            ns = n // 128
            for s in range(ns):
                k.dma("sp", xt[s][:, :], cfg["xin"][t0 + s * 128:t0 + (s + 1) * 128, :], w=[xt[s]])
            if pre is not None:
                for s in range(ns):
                    k.dma("sp", tmp[:, :], pre["o"][t0 + s * 128:t0 + (s + 1) * 128, :], w=[tmp])
                    for c in range(8):
                        k.op("pe", lambda e: e.transpose(ps_t[:, c * 128:(c + 1) * 128], tmp[:, c * 128:(c + 1) * 128],
                                                         identf[:, :]), r=[tmp, identf], w=[ps_t])
                    k.op("act", lambda e: e.copy(out=hT[:, :, s * 128:(s + 1) * 128],
                                                 in_=ps_t[:, :].rearrange("p (c t) -> p c t", c=8)), r=[ps_t], w=[hT])
                for s in range(ns):
                    for hf in range(2):
                        py = ps_y[yi % 2]
                        yi += 1
                        for c in range(8):
                            mm(k, py[:, :], hT[:, c, s * 128:(s + 1) * 128], wo[:, c, hf * 512:(hf + 1) * 512],
                               c == 0, c == 7, r=[hT, wo], w=[py])
                        sl = slice(hf * 512, (hf + 1) * 512)
                        k.op("dve", lambda e, py=py, sl=sl: e.tensor_tensor(out=tmp[:, sl], in0=py[:, :],
                                                                             in1=PG[:, sl], op=ALU.mult),
                             r=[py, PG], w=[tmp])
                        k.op("dve", lambda e, s=s, sl=sl: e.tensor_tensor(out=xt[s][:, sl], in0=xt[s][:, sl],
                                                                            in1=tmp[:, sl], op=ALU.add),
                             r=[xt[s], tmp], w=[xt[s]])
            for s in range(ns):
                norm_mod_T(xt[s], Ac, Bc, hT, s)
            for fc in range(22 if cfg.get("stage", 9) >= 2 else 0):
                p1 = ps_u[ui % 4]
                p2 = ps_u[(ui + 1) % 4]
                ui += 2
                for c in range(8):
                    mm(k, p1[:, 0:n], w1[:, c, fc * 128:(fc + 1) * 128], hT[:, c, 0:n], c == 0, c == 7,
                       r=[w1, hT], w=[p1])
                for c in range(8):
                    mm(k, p2[:, 0:n], w1[:, c, DFF + fc * 128:DFF + (fc + 1) * 128], hT[:, c, 0:n], c == 0, c == 7,
                       r=[w1, hT], w=[p2])
                k.op("act", lambda e, p1=p1: e.activation(out=sg[:, 0:n], in_=p1[:, 0:n], func=ACT.Silu),
                     r=[p1], w=[sg])
                k.op("dve", lambda e, p2=p2, fc=fc: e.tensor_tensor(out=gT[:, fc, 0:n], in0=sg[:, 0:n],
                                                                     in1=p2[:, 0:n], op=ALU.mult),
                     r=[sg, p2], w=[gT])
            for s in range(ns if cfg.get("stage", 9) >= 3 else 0):
                for hf in range(2):
                    py = ps_y[yi % 2]
                    yi += 1
                    for fc in range(22):
                        mm(k, py[:, :], gT[:, fc, s * 128:(s + 1) * 128], w2[:, fc, hf * 512:(hf + 1) * 512],
                           fc == 0, fc == 21, r=[gT, w2], w=[py])
                    sl = slice(hf * 512, (hf + 1) * 512)
                    k.op("dve", lambda e, py=py, sl=sl: e.tensor_tensor(out=tmp[:, sl], in0=py[:, :], in1=G[:, sl],
                                                                         op=ALU.mult), r=[py, G], w=[tmp])
                    k.op("dve", lambda e, s=s, sl=sl: e.tensor_tensor(out=xt[s][:, sl], in0=xt[s][:, sl],
                                                                        in1=tmp[:, sl], op=ALU.add),
                         r=[xt[s], tmp], w=[xt[s]])
            for s in range(ns):
                r0 = t0 + s * 128
                if post is not None and post["kind"] == "final":
                    if mr == 0:
                        rms_rinv(xt[s])
                        k.op("dve", lambda e, s=s: e.scalar_tensor_tensor(out=tmp[:, :], in0=xt[s][:, :],
                                                                           scalar=rinv[:, 0:1], in1=gvec2[:, :],
                                                                           op0=ALU.mult, op1=ALU.mult),
                             r=[xt[s], rinv, gvec2], w=[tmp])
                        k.dma("sp", post["out"][r0:r0 + 128, :], tmp[:, :], r=[tmp])
                    continue
                k.dma("sp", cfg["xout"][r0:r0 + 128, :], xt[s][:, :], r=[xt[s]])
                if post is not None and post["kind"] == "hT":
                    norm_mod_T(xt[s], A2c, B2c, hT, s)
            if post is not None and post["kind"] == "hT":
                for c in range(8):
                    k.dma("sp", post["hT"][c * 128:(c + 1) * 128, t0:t0 + n], hT[:, c, 0:n], r=[hT])


CH = 64
DECAY_SCALE = 0.6065306597126334


def projphase(k, st, cfg):
    S = cfg["S"]
    C = cfg["C"]
    Wt = cfg["W"]
    T, CTX = cfg["T"], cfg["CTX"]
    hTd = S["hT"]
    wabc = k.sb(st, [128, 8, 2304], BF16, "wabc")
    wgd = k.sb(st, [128, 8, 128], BF16, "wgd")
    wsw = k.sb(st, [128, 8, 1280], BF16, "wsw")
    wv = Wt["w_in"].rearrange("(c p) f -> p c f", p=128)
    for c in range(8):
        k.ld("pool", wabc.v[:, c, 0:1152], wv[:, c, 0:1152])
        k.ld("pool", wabc.v[:, c, 1152:2304], wv[:, c, 1152:2304])
        k.ld("pool", wgd.v[:, c, :], wv[:, c, 3072:3200])
        k.ld("pool", wsw.v[:, c, :], Wt["w_sw"].rearrange("(c p) f -> p c f", p=128)[:, c, :])
    wda = [k.sb(st, [128, 8, 896], BF16, "wda%d" % d) for d in range(2)]
    wdb = [k.sb(st, [128, 8, 896], BF16, "wdb%d" % d) for d in range(2)]
    with contextlib.ExitStack() as st2:
        wst = k.sb(st2, [128, 8, 896], F32, "wst")
        mub = k.sb(st2, [128, 896], F32, "mub")
        wtm = k.sb(st2, [128, 8, 896], F32, "wtm")
        for d in range(2):
            for c in range(8):
                k.ld("sp", wst.v[:, c, :], Wt["w_d"][d].rearrange("(c p) f -> p c f", p=128)[:, c, :])
            k.ld("sp", mub.v[:, :], Wt["mu"][d].partition_broadcast(128))
            for c in range(8):
                k.tt("dve", wtm.v[:, c, :], wst.v[:, c, :], mub.v[:, :], ALU.mult)
            k.cp("act", wda[d].v[:, :, :], wtm.v[:, :, :])
            k.tt("dve", wdb[d].v[:, :, :], wst.v[:, :, :], wtm.v[:, :, :], ALU.subtract)
        k.barrier()
    pc = k.sb(st, [128, 24], F32, "pcols")
    k.ld("sp", pc.v[:, :], Wt["pcols"])
    oma = k.sb(st, [128, 2], F32, "oma")
    k.ts("dve", oma.v[:, :], pc.v[:, 14:16], -1.0, 1.0, ALU.mult, ALU.add)
    identf = k.sb(st, [128, 128], F32, "identf")
    k.ld("sp", identf.v[:, :], C["ident"])
    bd = k.sb(st, [128, 128], F32, "bd")
    k.ld("sp", bd.v[:, :], C["bd64"])
    e2 = k.sb(st, [128, 2], F32, "e2")
    k.ld("sp", e2.v[:, :], C["e2"])
    rtab = k.sb(st, [128, 2, 2, 3, CH], F32, "rtab")
    k.ld("sp", rtab.v[:, :, :, :, :], C["ret_tab"])
    w2s = k.sb(st, [64, 2, 256], BF16, "w2s")
    a2s = k.sb(st, [64, 2, 256], BF16, "a2s")
    g2s = k.sb(st, [128, 256], BF16, "g2s")
    for d in range(2):
        k.ld("pool", w2s.v[:, d, :], Wt["w2"][d])
        k.ld("pool", a2s.v[:, d, :], Wt["a2"][d])
    k.ld("pool", g2s.v[:, :], Wt["g2"])

    NT = 512
    hTt = k.sb(st, [128, 8, NT + 2], BF16, "hTt")
    ca = k.sb(st, [128, NT], F32, "ca")
    sa = k.sb(st, [128, NT], F32, "sa")
    cs_ = k.sb(st, [128, NT], F32, "cs")
    ss_ = k.sb(st, [128, NT], F32, "ss")
    gca = k.sb(st, [128, 2, NT], F32, "gca")
    gsa = k.sb(st, [128, 2, NT], F32, "gsa")
    F = [k.sb(st, [128, NT], F32, "f%d" % i) for i in range(12)]
    ob = k.sb(st, [128, NT], BF16, "ob")
    tb = k.sb(st, [128, 512], F32, "tb")
    tbb = k.sb(st, [128, 512], BF16, "tbb")
    thb = k.sb(st, [64, NT], BF16, "thb")
    alb = k.sb(st, [64, NT], BF16, "alb")
    sgb = k.sb(st, [128, NT], BF16, "sgb")
    bs4 = k.sb(st, [128, 4, 4], F32, "bs4")
    pcx = k.sb(st, [128, NT // CH], F32, "pcx")
    PS = [k.ps(st, [128, 512], F32, "pp%d" % i) for i in range(8)]
    pi = [0]

    def nps():
        p = PS[pi[0] % 8]
        pi[0] += 1
        return p

    def chain(p, wlist, M, n, col0=1):
        tot = len(wlist) * 8
        i = 0
        for (wb, c0, sh) in wlist:
            for c in range(8):
                k.mm(p.v[0:M, 0:n], wb.v[:, c, c0:c0 + M], hTt.v[:, c, col0 + sh:col0 + sh + n], i == 0, i == tot - 1)
                i += 1

    def chain_tok(p, wlist, ncols, s):
        tot = len(wlist) * 8
        i = 0
        for (wb, c0, sh) in wlist:
            for c in range(8):
                k.mm(p.v[:, 0:ncols], hTt.v[:, c, 1 + sh + s * 128:1 + sh + (s + 1) * 128], wb.v[:, c, c0:c0 + ncols],
                     i == 0, i == tot - 1)
                i += 1

    def rope_out(P, Psw, ctab, stab, dst, n):
        k.tt("dve", F[10].v[:, 0:n], P.v[:, 0:n], ctab, ALU.mult)
        k.tt("dve", F[11].v[:, 0:n], Psw.v[:, 0:n], stab, ALU.mult)
        k.tt("dve", dst, F[10].v[:, 0:n], F[11].v[:, 0:n], ALU.add)

    segs = [(0, T, True), (T, CTX, False)]
    for (seg0, seglen, latent) in segs:
        for t0 in range(seg0, seg0 + seglen, NT):
            n = min(NT, seg0 + seglen - t0)
            ns = n // 128
            nch = n // CH
            lo = t0 - 1 if t0 > seg0 else t0
            hi = t0 + n + 1 if t0 + n < seg0 + seglen else t0 + n
            if lo == t0:
                k.memset("dve", hTt.v[:, :, 0:1], 0.0)
            if hi == t0 + n:
                k.memset("dve", hTt.v[:, :, n + 1:n + 2], 0.0)
            for c in range(8):
                k.ld("sp", hTt.v[:, c, 1 - (t0 - lo):1 + n + (hi - t0 - n)], hTd[c * 128:(c + 1) * 128, lo:hi])
            if latent:
                k.ld("sp", ca.v[:, 0:n], C["ropeA_c"][:, t0:t0 + n])
                k.ld("sp", sa.v[:, 0:n], C["ropeA_s"][:, t0:t0 + n])
                k.ld("sp", cs_.v[:, 0:n], C["ropeS_c"][:, t0:t0 + n])
                k.ld("sp", ss_.v[:, 0:n], C["ropeS_s"][:, t0:t0 + n])
                for qk in range(2):
                    k.ts("dve", gca.v[:, qk, 0:n], ca.v[:, 0:n], pc.v[:, 2 * qk:2 * qk + 1], None, ALU.mult)
                    k.ts("dve", gsa.v[:, qk, 0:n], sa.v[:, 0:n], pc.v[:, 2 * qk + 1:2 * qk + 2], None, ALU.mult)
            for grp, base, swb, dq, dk_, dv_ in (("A", 0, 0, S["QA"], S["KA"], S["VA"]),
                                                ("B", 512, 384, S["QB"], S["KB"], S["VB"])):
                for j in range(3):
                    P, Psw = nps(), nps()
                    chain(P, [(wabc, base + j * 128, 0)], 128, n)
                    if latent:
                        chain(Psw, [(wsw, swb + j * 128, 0)], 128, n)
                    qk = 0 if j < 2 else 1
                    if grp == "B":
                        k.act(F[0].v[:, 0:n], P.v[:, 0:n], ACT.Square)
                        pss = nps()
                        k.mm(pss.v[:, 0:n], bd.v[:, :], F[0].v[:, 0:n])
                        k.ts("dve", F[1].v[:, 0:n], pss.v[:, 0:n], 1.0 / 64, RMS_EPS, ALU.mult, ALU.add)
                        k.act(F[1].v[:, 0:n], F[1].v[:, 0:n], ACT.Sqrt)
                        k.recip(F[1].v[:, 0:n], F[1].v[:, 0:n])
                        k.tt("dve", F[2].v[:, 0:n], P.v[:, 0:n], F[1].v[:, 0:n], ALU.mult)
                        if latent:
                            k.tt("dve", F[3].v[:, 0:n], Psw.v[:, 0:n], F[1].v[:, 0:n], ALU.mult)
                            rope_out(F[2], F[3], gca.v[:, qk, 0:n], gsa.v[:, qk, 0:n], ob.v[:, 0:n], n)
                        else:
                            k.ts("dve", ob.v[:, 0:n], F[2].v[:, 0:n], pc.v[:, 2 * qk:2 * qk + 1], None, ALU.mult)
                    else:
                        if latent:
                            rope_out(P, Psw, ca.v[:, 0:n], sa.v[:, 0:n], ob.v[:, 0:n], n)
                        else:
                            k.cp("act", ob.v[:, 0:n], P.v[:, 0:n])
                    dst = dq[j * 128:(j + 1) * 128, t0:t0 + n] if j < 2 else dk_[:, t0:t0 + n]
                    k.stv("sp", dst, ob.v[:, 0:n])
                for s in range(ns):
                    P = nps()
                    chain_tok(P, [(wabc, base + 384, 0)], 128, s)
                    k.cp("act", tbb.v[:, 0:128], P.v[:, 0:128])
                    k.stv("sp", dv_[t0 + s * 128:t0 + (s + 1) * 128, :], tbb.v[:, 0:128])
            for j in range(4):
                P, Psw = nps(), nps()
                chain(P, [(wabc, 1024 + j * 128, 0)], 128, n)
                g = j % 2
                if latent:
                    chain(Psw, [(wsw, 768 + j * 128, 0)], 128, n)
                    rope_out(P, Psw, cs_.v[:, 0:n], ss_.v[:, 0:n], F[0].v[:, 0:n], n)
                else:
                    k.cp("act", F[0].v[:, 0:n], P.v[:, 0:n])
                x3 = F[0].v[:, 0:n].rr("p (c i) -> p c i", i=CH)
                for d in range(2):
                    if j < 2:
                        k.tt("dve", F[1].v[:, 0:n].rr("p (c i) -> p c i", i=CH), x3,
                             rtab.v[:, g, d, 0:1, :].bc([128, nch, CH]), ALU.mult)
                        k.stv("sp", S["C_Rt"][d, g * 128:(g + 1) * 128, t0:t0 + n], F[1].v[:, 0:n])
                    else:
                        k.tt("dve", F[1].v[:, 0:n].rr("p (c i) -> p c i", i=CH), x3,
                             rtab.v[:, g, d, 1:2, :].bc([128, nch, CH]), ALU.mult)
                        k.stv("sp", S["C_Kt"][d, g * 128:(g + 1) * 128, t0:t0 + n], F[1].v[:, 0:n])
                        k.tt("dve", F[2].v[:, 0:n].rr("p (c i) -> p c i", i=CH), x3,
                             rtab.v[:, g, d, 2:3, :].bc([128, nch, CH]), ALU.mult)
                        for s in range(ns):
                            pt = nps()
                            k.tr(pt.v[:, 0:128], F[2].v[:, s * 128:(s + 1) * 128], identf.v[:, :])
                            k.cp("act", tb.v[:, 0:128], pt.v[:, 0:128])
                            k.stv("sp", S["C_Kh"][d, t0 + s * 128:t0 + (s + 1) * 128, g * 128:(g + 1) * 128],
                                  tb.v[:, 0:128])
            for s in range(ns):
                P = nps()
                chain_tok(P, [(wabc, 1536, 0)], 256, s)
                k.cp("act", tb.v[:, 0:256], P.v[:, 0:256])
                k.stv("sp", S["C_V"][t0 + s * 128:t0 + (s + 1) * 128, :], tb.v[:, 0:256])
                P = nps()
                chain_tok(P, [(wabc, 1792, 0)], 512, s)
                k.act(tb.v[:, 0:512], P.v[:, 0:512], ACT.Silu)
                k.stv("sp", S["C_G"][t0 + s * 128:t0 + (s + 1) * 128, :], tb.v[:, 0:512])
            P = nps()
            chain(P, [(wgd, 0, 0)], 128, n)
            k.act(sgb.v[:, 0:n], P.v[:, 0:n], ACT.Sigmoid)
            for s in range(ns):
                P = nps()
                k.mm(P.v[:, 0:256], sgb.v[:, s * 128:(s + 1) * 128], g2s.v[:, :])
                k.cp("act", tb.v[:, 0:256], P.v[:, 0:256])
                k.stv("sp", S["D_G"][t0 + s * 128:t0 + (s + 1) * 128, :], tb.v[:, 0:256])
            for d in range(2):
                sh = -1 if d == 0 else 1
                wl = lambda c0: [(wdb[d], c0, 0), (wda[d], c0, sh)]
                P = nps()
                chain(P, wl(768), 64, n)
                k.act(thb.v[:, 0:n], P.v[0:64, 0:n], ACT.Tanh)
                P = nps()
                chain(P, wl(832), 64, n)
                k.cp("act", alb.v[:, 0:n], P.v[0:64, 0:n])
                for s in range(ns):
                    P = nps()
                    chain_tok(P, wl(512), 256, s)
                    k.cp("act", tb.v[:, 0:256], P.v[:, 0:256])
                    k.stv("sp", S["D_V"][d, t0 + s * 128:t0 + (s + 1) * 128, :], tb.v[:, 0:256])
                for g in range(2):
                    r_, k_, lw, cs, a_, kk, e_, t1, t2 = F[0], F[1], F[2], F[3], F[4], F[5], F[6], F[7], F[8]
                    gc = slice(g, g + 1)
                    P = nps()
                    chain(P, wl(g * 128), 128, n)
                    k.cp("act", r_.v[:, 0:n], P.v[:, 0:n])
                    P = nps()
                    chain(P, wl(256 + g * 128), 128, n)
                    k.cp("act", k_.v[:, 0:n], P.v[:, 0:n])
                    P = nps()
                    k.mm(P.v[:, 0:n], w2s.v[:, d, g * 128:(g + 1) * 128], thb.v[:, 0:n])
                    k.act(lw.v[:, 0:n], P.v[:, 0:n], ACT.Sigmoid, bias=pc.v[:, 4 + 2 * d + g:5 + 2 * d + g])
                    k.ts("dve", lw.v[:, 0:n], lw.v[:, 0:n], -DECAY_SCALE, None, ALU.mult)
                    P = nps()
                    k.mm(P.v[:, 0:n], a2s.v[:, d, g * 128:(g + 1) * 128], alb.v[:, 0:n])
                    k.act(a_.v[:, 0:n], P.v[:, 0:n], ACT.Sigmoid, bias=pc.v[:, 8 + 2 * d + g:9 + 2 * d + g])
                    k.ts("dve", kk.v[:, 0:n], k_.v[:, 0:n], pc.v[:, 12 + g:13 + g], None, ALU.mult)
                    k.act(t1.v[:, 0:n], kk.v[:, 0:n], ACT.Square)
                    P = nps()
                    k.mm(P.v[:, 0:n], bd.v[:, :], t1.v[:, 0:n])
                    k.act(t1.v[:, 0:n], P.v[:, 0:n], ACT.Sqrt)
                    k.ts("dve", t1.v[:, 0:n], t1.v[:, 0:n], 1e-12, None, ALU.max)
                    k.recip(t1.v[:, 0:n], t1.v[:, 0:n])
                    k.tt("dve", kk.v[:, 0:n], kk.v[:, 0:n], t1.v[:, 0:n], ALU.mult)
                    k.ts("dve", t1.v[:, 0:n], a_.v[:, 0:n], pc.v[:, 14 + g:15 + g], oma.v[:, gc], ALU.mult, ALU.add)
                    k.tt("dve", k_.v[:, 0:n], k_.v[:, 0:n], t1.v[:, 0:n], ALU.mult)
                    k.stt("dve", t1.v[:, 0:n], r_.v[:, 0:n], pc.v[:, 16 + 2 * d + g:17 + 2 * d + g], k_.v[:, 0:n],
                          ALU.mult, ALU.mult)
                    for s in range(ns):
                        P = nps()
                        k.mm(P.v[:, 0:2], t1.v[:, s * 128:(s + 1) * 128], e2.v[:, :])
                        k.cp("act", bs4.v[:, s, 2 * g:2 * g + 2], P.v[:, 0:2])
                    k.tt("dve", a_.v[:, 0:n], a_.v[:, 0:n], kk.v[:, 0:n], ALU.mult)
                    src, dst = lw, cs
                    k.cp("act", t2.v[:, 0:n], lw.v[:, 0:n])
                    src = t2
                    for stp in (1, 2, 4, 8, 16, 32):
                        s3 = src.v[:, 0:n].rr("p (c i) -> p c i", i=CH)
                        d3 = dst.v[:, 0:n].rr("p (c i) -> p c i", i=CH)
                        if d == 0:
                            k.tt("dve", d3[:, :, stp:], s3[:, :, stp:], s3[:, :, :CH - stp], ALU.add)
                            k.cp("act", d3[:, :, :stp], s3[:, :, :stp])
                        else:
                            k.tt("dve", d3[:, :, :CH - stp], s3[:, :, :CH - stp], s3[:, :, stp:], ALU.add)
                            k.cp("act", d3[:, :, CH - stp:], s3[:, :, CH - stp:])
                        src, dst = dst, src
                    cs = src
                    spare = dst
                    cs3 = cs.v[:, 0:n].rr("p (c i) -> p c i", i=CH)
                    tot3 = cs3[:, :, CH - 1:CH] if d == 0 else cs3[:, :, 0:1]
                    k.act(e_.v[:, 0:n], cs.v[:, 0:n], ACT.Exp)
                    k.tt("dve", t1.v[:, 0:n], r_.v[:, 0:n], e_.v[:, 0:n], ALU.mult)
                    k.stv("sp", S["D_Rt"][d, g * 128:(g + 1) * 128, t0:t0 + n], t1.v[:, 0:n])
                    k.tt("dve", e_.v[:, 0:n], cs.v[:, 0:n], lw.v[:, 0:n], ALU.subtract)
                    k.act(e_.v[:, 0:n], e_.v[:, 0:n], ACT.Exp)
                    k.stt("dve", t1.v[:, 0:n], kk.v[:, 0:n], -1.0, e_.v[:, 0:n], ALU.mult, ALU.mult)
                    k.stv("sp", S["D_At"][d, g * 128:(g + 1) * 128, t0:t0 + n], t1.v[:, 0:n])
                    for s in range(ns):
                        pt = nps()
                        k.tr(pt.v[:, 0:128], t1.v[:, s * 128:(s + 1) * 128], identf.v[:, :])
                        k.cp("act", tb.v[:, 0:128], pt.v[:, 0:128])
                        k.stv("sp", S["D_Atok"][d, t0 + s * 128:t0 + (s + 1) * 128, g * 128:(g + 1) * 128],
                              tb.v[:, 0:128])
                    k.act(e_.v[:, 0:n], cs.v[:, 0:n], ACT.Exp, scale=-1.0)
                    k.tt("dve", t1.v[:, 0:n], a_.v[:, 0:n], e_.v[:, 0:n], ALU.mult)
                    k.stv("sp", S["D_Bt"][d, g * 128:(g + 1) * 128, t0:t0 + n], t1.v[:, 0:n])
                    k.tt("dve", t1.v[:, 0:n], k_.v[:, 0:n], e_.v[:, 0:n], ALU.mult)
                    k.stv("sp", S["D_Kt"][d, g * 128:(g + 1) * 128, t0:t0 + n], t1.v[:, 0:n])
                    k.act(pcx.v[:, 0:nch], tot3.rr("p c o -> p (c o)"), ACT.Exp)
                    k.stv("sp", S["D_PC"][d, g * 128:(g + 1) * 128, t0 // CH:t0 // CH + nch], pcx.v[:, 0:nch])
                    k.tt("dve", e_.v[:, 0:n].rr("p (c i) -> p c i", i=CH), tot3.bc([128, nch, CH]), cs3, ALU.subtract)
                    k.act(e_.v[:, 0:n], e_.v[:, 0:n], ACT.Exp)
                    for (srcb, dstd) in ((a_, S["D_Bh"]), (k_, S["D_Kh"])):
                        k.tt("dve", t1.v[:, 0:n], srcb.v[:, 0:n], e_.v[:, 0:n], ALU.mult)
                        for s in range(ns):
                            pt = nps()
                            k.tr(pt.v[:, 0:128], t1.v[:, s * 128:(s + 1) * 128], identf.v[:, :])
                            k.cp("act", tb.v[:, 0:128], pt.v[:, 0:128])
                            k.stv("sp", dstd[d, t0 + s * 128:t0 + (s + 1) * 128, g * 128:(g + 1) * 128], tb.v[:, 0:128])
                for s in range(ns):
                    k.stv("sp", S["D_BS"][d, t0 + s * 128:t0 + (s + 1) * 128, :], bs4.v[:, s, :])


def attnphase(k, st, cfg):
    S, C, Wt = cfg["S"], cfg["C"], cfg["W"]
    T, CTX = cfg["T"], cfg["CTX"]
    TT = T + CTX
    NKC = TT // 128
    need_ctx = cfg["need_ctx"]
    kT = k.sb(st, [128, TT], BF16, "kT")
    va = k.sb(st, [128, NKC, 65], BF16, "va")
    qT = [k.sb(st, [128, 512], BF16, "qT%d" % i) for i in range(2)]
    for c0 in range(0, TT, 2048):
        k.memset("dve", kT.v[64:128, c0:min(TT, c0 + 2048)], 0.0)
    for q_ in qT:
        k.memset("dve", q_.v[64:128, :], 0.0)
    pT = [k.sb(st, [128, 512], BF16, "pT%d" % i) for i in range(3)]
    ot = [k.sb(st, [128, 64], F32, "ot%d" % i) for i in range(2)]
    rc = k.sb(st, [128, 1], F32, "rc")
    esk = k.sb(st, [128, 4], F32, "esk")
    mprev = k.sb(st, [128, 128], BF16, "mprev")
    mnext = k.sb(st, [128, 128], BF16, "mnext")
    k.ld("pool", mprev.v[:, :], C["mprev"])
    k.ld("pool", mnext.v[:, :], C["mnext"])
    k.ld("sp", esk.v[:, :], Wt["sink"].partition_broadcast(128))
    k.act(esk.v[:, :], esk.v[:, :], ACT.Exp)
    ps_s = [k.ps(st, [128, 512], F32, "pss%d" % i) for i in range(3)]
    ps_o = [k.ps(st, [128, 512], F32, "pso%d" % i) for i in range(4)]
    cnt = dict(s=0, q=0, o=0, p=0)
    scale = 64 ** -0.5

    def load_kv(kd, vd, g):
        k.ld("sp", kT.v[0:64, :], kd[g * 64:(g + 1) * 64, :])
        k.memset("dve", va.v[:, :, 64:65], 1.0)
        k.ld("sp", va.v[:, :, 0:64], vd[:, g * 64:(g + 1) * 64].rearrange("(c p) j -> p c j", p=128))

    def finish(acc, nq_sub, t0, col, sinkcol):
        for s in range(nq_sub):
            o = ot[cnt["o"] % 2]
            cnt["o"] += 1
            if sinkcol is None:
                k.recip(rc.v[:, :], acc[s].v[:, 64:65])
            else:
                k.tt("dve", rc.v[:, :], acc[s].v[:, 64:65], esk.v[:, sinkcol:sinkcol + 1], ALU.add)
                k.recip(rc.v[:, :], rc.v[:, :])
            k.ts("dve", o.v[:, :], acc[s].v[:, 0:64], rc.v[:, 0:1], None, ALU.mult)
            k.stv("sp", S["O"][t0 + s * 128:t0 + (s + 1) * 128, col:col + 64], o.v[:, :])

    def block(qd, h, t0, nq, chunks, col, sinkcol):
        q = qT[cnt["q"] % 2]
        cnt["q"] += 1
        k.ld("sp", q.v[0:64, 0:nq], qd[h * 64:(h + 1) * 64, t0:t0 + nq])
        nsub = nq // 128
        first, last = {}, {}
        for i, (kc, subs) in enumerate(chunks):
            for (s, _) in subs:
                first.setdefault(s, i)
                last[s] = i

        def qk(i):
            ps = ps_s[cnt["s"] % 3]
            cnt["s"] += 1
            kc = chunks[i][0]
            k.mm(ps.v[:, 0:nq], kT.v[:, kc * 128:(kc + 1) * 128], q.v[:, 0:nq])
            return ps

        nxt = qk(0)
        for i, (kc, subs) in enumerate(chunks):
            ps = nxt
            if i + 1 < len(chunks):
                nxt = qk(i + 1)
            p = pT[cnt["p"] % 3]
            cnt["p"] += 1
            k.act(p.v[:, 0:nq], ps.v[:, 0:nq], ACT.Exp, scale=scale)
            for (s, m_) in subs:
                if m_ is not None:
                    k.tt("dve", p.v[:, s * 128:(s + 1) * 128], p.v[:, s * 128:(s + 1) * 128], m_.v[:, :], ALU.mult)
            for (s, m_) in subs:
                k.mm(ps_o[s].v[:, 0:65], p.v[:, s * 128:(s + 1) * 128], va.v[:, kc, :], i == first[s], i == last[s])
        finish(ps_o, nsub, t0, col, sinkcol)

    for g in range(2):
        load_kv(S["KB"], S["VB"], g)
        for h in (2 * g, 2 * g + 1):
            for t0 in range(0, T, 512):
                nq = min(512, T - t0)
                block(S["QB"], h, t0, nq, [(kc, [(s, None) for s in range(nq // 128)]) for kc in range(NKC)],
                      256 + h * 64, None)
            if need_ctx:
                block(S["QB"], h, T, CTX, [(kc, [(s, None) for s in range(CTX // 128)]) for kc in range(T // 128, NKC)],
                      256 + h * 64, None)
    nb = T // 128
    cchunks = list(range(T // 128, NKC))
    for g in range(2):
        load_kv(S["KA"], S["VA"], g)
        for h in (2 * g, 2 * g + 1):
            for b0 in range(0, nb, 4):
                nsb = min(4, nb - b0)
                chunks = []
                for kc in range(max(0, b0 - 1), min(nb, b0 + nsb + 1)):
                    subs = []
                    for s in range(nsb):
                        dlt = kc - (b0 + s)
                        if dlt == -1:
                            subs.append((s, mprev))
                        elif dlt == 0:
                            subs.append((s, None))
                        elif dlt == 1:
                            subs.append((s, mnext))
                    chunks.append((kc, subs))
                for kc in cchunks:
                    chunks.append((kc, [(s, None) for s in range(nsb)]))
                block(S["QA"], h, b0 * 128, nsb * 128, chunks, h * 64, h)
            if need_ctx:
                block(S["QA"], h, T, CTX, [(kc, [(s, None) for s in range(CTX // 128)]) for kc in cchunks], h * 64, h)


def scanphase(k, st, cfg):
    S, C = cfg["S"], cfg["C"]
    T, CTX = cfg["T"], cfg["CTX"]
    TT = T + CTX
    dplr = cfg["dplr"]
    pre = "D_" if dplr else "C_"
    GT_ = 4 * CH
    NCHT = TT // CH
    f3 = lambda b: b.v[:, :, :]
    chm = lambda nm: k.sb(st, [64, 8, GT_], F32, nm)
    tkm = lambda nm: k.sb(st, [64, 4, 8, 64], F32, nm)
    RT = [chm("RT%d" % i) for i in range(2)]
    KT = [chm("KT%d" % i) for i in range(2)]
    KH = [tkm("KH%d" % i) for i in range(2)]
    VV = [tkm("VV%d" % i) for i in range(2)]
    if dplr:
        AT = [chm("AT%d" % i) for i in range(2)]
        BT = [chm("BT%d" % i) for i in range(2)]
        BH = [tkm("BH%d" % i) for i in range(2)]
        AK = [tkm("AK%d" % i) for i in range(2)]
        Mst = k.sb(st, [64, 8, 64], F32, "Mst")
        Mts = k.sb(st, [64, 8, 64], F32, "Mts")
        k.ld("sp", Mst.v[:, :, :], C["m_st"].rearrange("p (q t) -> p q t", q=8))
        k.ld("sp", Mts.v[:, :, :], C["m_ts"].rearrange("p (q t) -> p q t", q=8))
        Ls = [k.sb(st, [64, 8, 64], F32, "L%d" % i) for i in range(5)]
        Ns = [k.sb(st, [64, 8, 64], F32, "N%d" % i) for i in range(6)]
        AakT = k.sb(st, [64, 8, 64], F32, "AakT")
        ArbT = k.sb(st, [64, 8, 64], F32, "ArbT")
        X = [k.sb(st, [64, 8, 128], F32, "X%d" % i) for i in range(2)]
        Qe = k.sb(st, [64, 8, 64], F32, "Qe")
        PCt = k.sb(st, [64, 8, NCHT], F32, "PCt")
        for d in range(2):
            k.ld("sp", PCt.v[:, d * 4:(d + 1) * 4, :], S["D_PC"][d].rearrange("(h j) c -> j h c", j=64))
    else:
        PCr = k.sb(st, [64, 8], F32, "PCr")
        k.ld("sp", PCr.v[:, :], C["ret_pc"])
    Min = k.sb(st, [64, 8, 64], F32, "Min")
    k.ld("sp", Min.v[:, :, :], C["m_in"].rearrange("p (q t) -> p q t", q=8))
    id8 = k.sb(st, [64, 8, 64], F32, "id8")
    k.ld("sp", id8.v[:, :, :], C["id8"].rearrange("p (q t) -> p q t", q=8))
    ArkT = k.sb(st, [64, 8, 64], F32, "ArkT")
    GTt = k.sb(st, [64, 8, 64], F32, "GTt")
    ST = [k.sb(st, [64, 8, 64], F32, "ST%d" % i) for i in range(2)]
    Yt = [k.sb(st, [64, 8, 64], F32, "Yt%d" % i) for i in range(2)]
    psA = [k.ps(st, [64, 512], F32, "psA%d" % i) for i in range(4)]
    psX = k.ps(st, [64, 1024], F32, "psX")
    psY = [k.ps(st, [64, 512], F32, "psY%d" % i) for i in range(2)]
    cnt = dict(a=0, y=0, e=0)
    p3 = lambda p: p.v[:, :].rr("p (q t) -> p q t", q=8)

    def npa():
        p = psA[cnt["a"] % 4]
        cnt["a"] += 1
        return p

    def ev():
        cnt["e"] += 1
        return "act" if cnt["e"] % 2 else "dve"

    def prod(dst, lhs, rhs, mask):
        p = npa()
        for q in range(8):
            k.mm(p3(p)[:, q, :], lhs(q), rhs(q))
        if mask is None:
            k.cp(ev(), f3(dst), p3(p))
        else:
            k.tt("dve", f3(dst), p3(p), f3(mask), ALU.mult)

    k.memset("dve", f3(ST[0]), 0.0)
    ngl = T // GT_
    order = {0: [ngl] + list(range(ngl)), 1: [ngl] + list(range(ngl - 1, -1, -1))}
    step = 0
    for gi in range(ngl + 1):
        rb = gi % 2
        grp = {d: order[d][gi] for d in range(2)}
        for d in range(2):
            t0 = grp[d] * GT_
            qs = slice(d * 4, (d + 1) * 4)
            chv = lambda nm: S[pre + nm][d, :, t0:t0 + GT_].rearrange("(h j) t -> j h t", j=64)
            tkv = lambda ap: ap[t0:t0 + GT_, :].rearrange("(n t) (h j) -> t n h j", t=CH, j=64)
            k.ld("sp", RT[rb].v[:, qs, :], chv("Rt"))
            k.ld("sp", KT[rb].v[:, qs, :], chv("Kt"))
            k.ld("sp", KH[rb].v[:, :, qs, :], tkv(S[pre + "Kh"][d]))
            k.ld("sp", VV[rb].v[:, :, qs, :], tkv(S["D_V"][d] if dplr else S["C_V"]))
            if dplr:
                k.ld("sp", AT[rb].v[:, qs, :], chv("At"))
                k.ld("sp", BT[rb].v[:, qs, :], chv("Bt"))
                k.ld("sp", BH[rb].v[:, :, qs, :], tkv(S["D_Bh"][d]))
                k.ld("sp", AK[rb].v[:, :, qs, :], tkv(S["D_Atok"][d]))
        for ci in range(4):
            cl = {0: ci, 1: 3 - ci}
            sl = {d: slice(cl[d] * CH, (cl[d] + 1) * CH) for d in range(2)}
            dq = lambda q: q // 4
            chs = lambda buf: (lambda q: buf[rb].v[:, q, sl[dq(q)]])
            tks = lambda buf: (lambda q: buf[rb].v[:, cl[dq(q)], q, :])
            rt, kt, kh, vv = chs(RT), chs(KT), tks(KH), tks(VV)
            Sc, Sn = ST[step % 2], ST[(step + 1) % 2]
            prod(ArkT, kt, rt, Min)
            if dplr:
                at, bt, bh, ak = chs(AT), chs(BT), tks(BH), tks(AK)
                prod(Ns[0], bt, at, Mst)
                prod(Ls[0], at, bt, Mts)
                prod(AakT, kt, at, Mst)
                prod(ArbT, bt, rt, Min)
                for lv in range(5):
                    if lv < 4:
                        prod(Ls[lv + 1], lambda q: Ns[lv].v[:, q, :], lambda q: Ls[lv].v[:, q, :], None)
                    prod(Ns[lv + 1], lambda q: Ls[lv].v[:, q, :], lambda q: Ns[lv].v[:, q, :], None)
                xc, xn = X[0], X[1]
                for d in range(2):
                    k.cp("act", xc.v[:, d * 4:(d + 1) * 4, 0:64], AK[rb].v[:, cl[d], d * 4:(d + 1) * 4, :])
                p = npa()
                for q in range(8):
                    k.mm(p3(p)[:, q, :], AakT.v[:, q, :], vv(q))
                k.cp("dve", xc.v[:, :, 64:128], p3(p))
                for lv in range(5, -1, -1):
                    px = psX.v[:, :].rr("p (q c) -> p q c", q=8)
                    for q in range(8):
                        k.mm(px[:, q, :], Ns[lv].v[:, q, :], xc.v[:, q, :])
                    k.tt("dve", f3(xn), f3(xc), px, ALU.add)
                    xc, xn = xn, xc
                wq = lambda q: xc.v[:, q, 0:64]
                uv = lambda q: xc.v[:, q, 64:128]
                p = npa()
                for q in range(8):
                    k.mm(p3(p)[:, q, :], wq(q), ArbT.v[:, q, :])
                for d in range(2):
                    k.tt("dve", Qe.v[:, d * 4:(d + 1) * 4, :], p3(p)[:, d * 4:(d + 1) * 4, :],
                         RT[rb].v[:, d * 4:(d + 1) * 4, sl[d]], ALU.add)
                qe = lambda q: Qe.v[:, q, :]
                p = npa()
                for q in range(8):
                    k.mm(p3(p)[:, q, :], wq(q), bh(q))
                for d in range(2):
                    c_abs = grp[d] * 4 + cl[d]
                    k.tt("dve", GTt.v[:, d * 4:(d + 1) * 4, :], id8.v[:, d * 4:(d + 1) * 4, :],
                         PCt.v[:, d * 4:(d + 1) * 4, c_abs:c_abs + 1].bc([64, 4, 64]), ALU.mult)
                k.tt("dve", f3(GTt), f3(GTt), p3(p), ALU.add)
            else:
                qe = rt
                if step == 0:
                    k.tt("dve", f3(GTt), f3(id8), PCr.v[:, :].rr("p (q o) -> p q o", o=1).bc([64, 8, 64]), ALU.mult)
            py = psY[cnt["y"] % 2]
            cnt["y"] += 1
            for q in range(8):
                k.mm(p3(py)[:, q, :], ArkT.v[:, q, :], vv(q), True, False)
                if dplr:
                    k.mm(p3(py)[:, q, :], ArbT.v[:, q, :], uv(q), False, False)
                k.mm(p3(py)[:, q, :], qe(q), Sc.v[:, q, :], False, True)
            yt = Yt[step % 2]
            k.cp("act", f3(yt), p3(py))
            for d in range(2):
                tt0 = grp[d] * GT_ + cl[d] * CH
                k.stv("sp", S[pre + "Y"][d, tt0:tt0 + CH, :].rearrange("t (h i) -> t h i", i=64),
                      yt.v[:, d * 4:(d + 1) * 4, :])
            p = npa()
            for q in range(8):
                k.mm(p3(p)[:, q, :], kh(q), vv(q), True, False)
                if dplr:
                    k.mm(p3(p)[:, q, :], bh(q), uv(q), False, False)
                k.mm(p3(p)[:, q, :], GTt.v[:, q, :], Sc.v[:, q, :], False, True)
            k.cp("dve", f3(Sn), p3(p))
            step += 1


GN_EPS = 64e-5


def postphase(k, st, cfg):
    S, Wt = cfg["S"], cfg["W"]
    TT = cfg["T"] + cfg["CTX"]
    row = lambda nm, ap: (lambda t: (k.ld("sp", t.v[:, :], ap.partition_broadcast(128)), t)[1])(k.sb(st, [128, 256], F32, nm))
    retg = row("retg", Wt["ret_g"])
    lng = row("lng", Wt["ln_g"])
    lnb = row("lnb", Wt["ln_b"])
    y = [k.sb(st, [128, 4, 64], F32, "y%d" % i) for i in range(2)]
    yc = k.sb(st, [128, 4, 64], F32, "yc")
    sq = k.sb(st, [128, 4, 64], F32, "sq")
    m4 = k.sb(st, [128, 4], F32, "m4")
    v4 = k.sb(st, [128, 4], F32, "v4")
    gc = k.sb(st, [128, 512], F32, "gc")
    gd = k.sb(st, [128, 256], F32, "gd")
    vd = k.sb(st, [128, 4, 64], F32, "vd")
    bs = k.sb(st, [128, 4], F32, "bs")
    acc = k.sb(st, [128, 256], F32, "acc")
    yn = k.sb(st, [128, 256], F32, "yn")
    i = 0

    def head_norm(yb, g, b, dst):
        k.op("dve", lambda e: e.tensor_reduce(out=m4[:, :], in_=yb[:, :, :], axis=AX.X, op=ALU.add), r=[yb], w=[m4])
        k.ts("dve", m4.v[:, :], m4.v[:, :], 1.0 / 64, None, ALU.mult)
        k.tt("dve", yc.v[:, :, :], yb.v[:, :, :], m4.v[:, :].rr("p (h o) -> p h o", o=1).bc([128, 4, 64]), ALU.subtract)
        k.tt("dve", sq.v[:, :, :], yc.v[:, :, :], yc.v[:, :, :], ALU.mult)
        k.op("dve", lambda e: e.tensor_reduce(out=v4[:, :], in_=sq[:, :, :], axis=AX.X, op=ALU.add), r=[sq], w=[v4])
        k.ts("dve", v4.v[:, :], v4.v[:, :], 1.0 / 64, GN_EPS, ALU.mult, ALU.add)
        k.act(v4.v[:, :], v4.v[:, :], ACT.Sqrt)
        k.recip(v4.v[:, :], v4.v[:, :])
        k.tt("dve", yc.v[:, :, :], yc.v[:, :, :], v4.v[:, :].rr("p (h o) -> p h o", o=1).bc([128, 4, 64]), ALU.mult)
        k.tt("dve", dst, yc.v[:, :, :].rr("p h i -> p (h i)"), g.v[:, :], ALU.mult)
        if b is not None:
            k.tt("dve", dst, dst, b.v[:, :], ALU.add)

    for t0 in range(0, TT, 128):
        tsl = slice(t0, t0 + 128)
        k.ld("sp", gc.v[:, :], S["C_G"][tsl, :])
        for d in range(2):
            yb = y[i % 2]
            i += 1
            k.ld("sp", yb.v[:, :, :], S["C_Y"][d, tsl, :].rearrange("t (h i) -> t h i", i=64))
            head_norm(yb, retg, None, yn.v[:, :])
            if d == 0:
                k.tt("dve", acc.v[:, :], yn.v[:, :], gc.v[:, 0:256], ALU.mult)
            else:
                k.tt("dve", yn.v[:, :], yn.v[:, :], gc.v[:, 256:512], ALU.mult)
                k.tt("dve", acc.v[:, :], acc.v[:, :], yn.v[:, :], ALU.add)
        k.stv("sp", S["O"][tsl, 512:768], acc.v[:, :])
        k.ld("sp", gd.v[:, :], S["D_G"][tsl, :])
        for d in range(2):
            yb = y[i % 2]
            i += 1
            k.ld("sp", yb.v[:, :, :], S["D_Y"][d, tsl, :].rearrange("t (h i) -> t h i", i=64))
            k.ld("sp", vd.v[:, :, :], S["D_V"][d, tsl, :].rearrange("t (h i) -> t h i", i=64))
            k.ld("sp", bs.v[:, :], S["D_BS"][d, tsl, :])
            head_norm(yb, lng, lnb, yn.v[:, :])
            k.tt("dve", vd.v[:, :, :], vd.v[:, :, :], bs.v[:, :].rr("p (h o) -> p h o", o=1).bc([128, 4, 64]), ALU.mult)
            k.tt("dve", yn.v[:, :], yn.v[:, :], vd.v[:, :, :].rr("p h i -> p (h i)"), ALU.add)
            if d == 0:
                k.cp("act", acc.v[:, :], yn.v[:, :])
            else:
                k.tt("dve", acc.v[:, :], acc.v[:, :], yn.v[:, :], ALU.add)
        k.tt("dve", acc.v[:, :], acc.v[:, :], gd.v[:, :], ALU.mult)
        k.stv("sp", S["O"][tsl, 768:1024], acc.v[:, :])


def modphase(k, st, cfg):
    cv = k.sb(st, [128, 2, 8], F32, "cv")
    for r in range(2):
        k.ld("sp", cv.v[:, r, :], cfg["cvec"][r].rearrange("(c p) -> p c", p=128), allow_slow_non_contiguous=True)
    sc = k.sb(st, [128, 8, 2], F32, "sc")
    k.act(sc.v[:, :, :].rr("p c r -> p r c"), cv.v[:, :, :], ACT.Silu)
    wm = [k.sb(st, [128, 8, 512], F32, "wm%d" % i) for i in range(2)]
    bm = k.sb(st, [2, 9 * D], F32, "bm")
    k.ld("sp", bm.v[:, :], cfg["b_mod"].partition_broadcast(2))
    ob = k.sb(st, [2, 9 * D], F32, "ob")
    pm = [k.ps(st, [128, 512], F32, "pm%d" % i) for i in range(2)]
    wv = cfg["w_mod"].rearrange("(c p) f -> p c f", p=128)
    for j in range(18):
        w = wm[j % 2]
        for c in range(8):
            k.ld("sp", w.v[:, c, :], wv[:, c, j * 512:(j + 1) * 512])
        p = pm[j % 2]
        for c in range(8):
            k.mm(p.v[0:2, :], sc.v[:, c, :], w.v[:, c, :], c == 0, c == 7)
        k.tt("dve", ob.v[:, j * 512:(j + 1) * 512], p.v[0:2, :], bm.v[:, j * 512:(j + 1) * 512], ALU.add)
    k.stv("sp", cfg["mod"], ob.v[:, :])


def build_program(T, CTX, L, stop_after=None):
    TT = T + CTX
    nc = bass.Bass("TRN2", target_bir_lowering=False, dynamic_dma_scratch_size=8192)

    def din(name, shape, dt=F32):
        return nc.dram_tensor(name, list(shape), dt, kind="ExternalInput").ap()

    def scr(name, shape, dt=F32):
        return nc.dram_tensor(name, list(shape), dt, kind="Internal").ap()

    I = dict(
        xin=din("xin", [TT, D]), cvec=din("cvec", [2, D]),
        w_mod=din("w_mod", [L, D, 9 * D]), b_mod=din("b_mod", [L, 9 * D]), norm_g=din("norm_g", [L, 3, D]),
        ffn_w_in=din("ffn_w_in", [L, 2, D, 2 * DFF]), ffn_w_out=din("ffn_w_out", [L, 2, DFF, D]),
        w_in=din("w_in", [L, D, 3456]), w_out=din("w_out", [L, D, D]), final_g=din("final_g", [D]),
        w_sw=din("w_sw", [L, D, 1280]), w_d=din("w_d", [L, 2, D, 896]), pcols=din("pcols", [L, 128, 24]),
        mu=din("mu", [L, 2, 896]), w2=din("w2", [L, 2, 64, 256]), a2=din("a2", [L, 2, 64, 256]),
        g2=din("g2", [L, 128, 256]), ret_g=din("ret_g", [L, 256]), ln_g=din("ln_g", [L, 256]),
        ln_b=din("ln_b", [L, 256]), sink=din("sink", [L, 4]),
    )
    Cn = dict(
        ident=din("ident", [128, 128]), bd64=din("bd64", [128, 128]), e2=din("e2", [128, 2]),
        ropeA_c=din("ropeA_c", [128, T]), ropeA_s=din("ropeA_s", [128, T]),
        ropeS_c=din("ropeS_c", [128, T]), ropeS_s=din("ropeS_s", [128, T]),
        ret_tab=din("ret_tab", [128, 2, 2, 3, CH]), ret_pc=din("ret_pc", [64, 8]),
        m_st=din("m_st", [64, 512]), m_ts=din("m_ts", [64, 512]), m_in=din("m_in", [64, 512]),
        id8=din("id8", [64, 512]), mprev=din("mprev", [128, 128]), mnext=din("mnext", [128, 128]),
    )
    out = nc.dram_tensor("out", [T, D], F32, kind="ExternalOutput").ap()
    dbg = stop_after is not None
    mk = (lambda name, shape, dt=F32: nc.dram_tensor(name, list(shape), dt, kind="ExternalOutput").ap()) if dbg else scr
    S = dict(
        xres=mk("xres", [TT, D]), mod=mk("modr", [2, 9 * D]), hT=mk("hT", [D, TT], BF16),
        QA=mk("QA", [256, TT], BF16), KA=mk("KA", [128, TT], BF16), VA=mk("VA", [TT, 128], BF16),
        QB=mk("QB", [256, TT], BF16), KB=mk("KB", [128, TT], BF16), VB=mk("VB", [TT, 128], BF16),
        O=mk("O", [TT, 1024]),
        C_Rt=mk("C_Rt", [2, 256, TT]), C_Kt=mk("C_Kt", [2, 256, TT]), C_Kh=mk("C_Kh", [2, TT, 256]),
        C_V=mk("C_V", [TT, 256]), C_G=mk("C_G", [TT, 512]), C_Y=mk("C_Y", [2, TT, 256]),
        D_Rt=mk("D_Rt", [2, 256, TT]), D_Kt=mk("D_Kt", [2, 256, TT]), D_At=mk("D_At", [2, 256, TT]),
        D_Bt=mk("D_Bt", [2, 256, TT]), D_Kh=mk("D_Kh", [2, TT, 256]), D_Bh=mk("D_Bh", [2, TT, 256]),
        D_Atok=mk("D_Atok", [2, TT, 256]), D_V=mk("D_V", [2, TT, 256]), D_Y=mk("D_Y", [2, TT, 256]),
        D_PC=mk("D_PC", [2, 256, TT // CH]), D_G=mk("D_G", [TT, 256]), D_BS=mk("D_BS", [2, TT, 4]),
    )
    k = K(nc)
    stages = []

    def phase(name, fn, cfg):
        if stop_after is not None and stop_after in stages:
            return
        with contextlib.ExitStack() as st:
            fn(k, st, cfg)
            k.barrier()
        stages.append(name)
        k.marks.append((name, {e.name: e.n for e in k.E.values()}))

    with k.stack:
        segs_all = [(0, T, 0), (T, CTX, 1)]
        for l in range(L):
            last = l == L - 1
            W = dict(w_in=I["w_in"][l], w_sw=I["w_sw"][l], w_d=I["w_d"][l], mu=I["mu"][l], pcols=I["pcols"][l],
                     w2=I["w2"][l], a2=I["a2"][l], g2=I["g2"][l], ret_g=I["ret_g"][l], ln_g=I["ln_g"][l],
                     ln_b=I["ln_b"][l], sink=I["sink"][l])
            base = dict(S=S, C=Cn, W=W, T=T, CTX=CTX)
            phase("mod%d" % l, modphase, dict(cvec=I["cvec"], w_mod=I["w_mod"][l], b_mod=I["b_mod"][l], mod=S["mod"]))
            phase("f1_%d" % l, rowphase, dict(
                xin=I["xin"] if l == 0 else S["xres"], xout=S["xres"], segs=segs_all, mod=S["mod"], ident=Cn["ident"],
                ffn=dict(w_in=I["ffn_w_in"][l, 0], w_out=I["ffn_w_out"][l, 0], g=I["norm_g"][l, 0], shift=0, scale=1,
                         gate=2),
                post=dict(kind="hT", g=I["norm_g"][l, 1], shift=3, scale=4, hT=S["hT"])))
            phase("proj%d" % l, projphase, base)
            phase("attn%d" % l, attnphase, dict(base, need_ctx=not last))
            phase("scanC%d" % l, scanphase, dict(base, dplr=False))
            phase("scanD%d" % l, scanphase, dict(base, dplr=True))
            phase("post%d" % l, postphase, base)
            phase("f2_%d" % l, rowphase, dict(
                xin=S["xres"], xout=S["xres"], segs=segs_all if not last else [(0, T, 0)], mod=S["mod"],
                ident=Cn["ident"],
                pre=dict(o=S["O"], w_out=I["w_out"][l], gate=5),
                ffn=dict(w_in=I["ffn_w_in"][l, 1], w_out=I["ffn_w_out"][l, 1], g=I["norm_g"][l, 2], shift=6, scale=7,
                         gate=8),
                post=dict(kind="final", g=I["final_g"], out=out) if last else None))
    return nc, k


def _consts(T):
    c = {}
    c["ident"] = np.eye(128, dtype=np.float32)
    bd = np.zeros((128, 128), np.float32)
    bd[:64, :64] = 1
    bd[64:, 64:] = 1
    c["bd64"] = bd
    e2 = np.zeros((128, 2), np.float32)
    e2[:64, 0] = 1
    e2[64:, 1] = 1
    c["e2"] = e2
    t = np.arange(T)
    row = (t // 64).astype(np.float32)
    col = (t % 64).astype(np.float32)
    inv16 = (1.0 / (np.float32(10000.0) ** (np.arange(0, 32, 2, dtype=np.float32) / np.float32(32)))).astype(np.float32)
    inv32 = (1.0 / (np.float32(10000.0) ** (np.arange(0, 64, 2, dtype=np.float32) / np.float32(64)))).astype(np.float32)
    ca = np.zeros((64, T), np.float32)
    sa = np.zeros((64, T), np.float32)
    for blk, pos in ((0, row), (32, col)):
        ang = pos[None, :] * inv16[:, None]
        ca[blk:blk + 16] = np.cos(ang)
        ca[blk + 16:blk + 32] = np.cos(ang)
        sa[blk:blk + 16] = -np.sin(ang)
        sa[blk + 16:blk + 32] = np.sin(ang)
    ang = t.astype(np.float32)[None, :] * inv32[:, None]
    cs = np.concatenate([np.cos(ang), np.cos(ang)], 0).astype(np.float32)
    ss = np.concatenate([-np.sin(ang), np.sin(ang)], 0).astype(np.float32)
    c["ropeA_c"] = np.concatenate([ca, ca], 0)
    c["ropeA_s"] = np.concatenate([sa, sa], 0)
    c["ropeS_c"] = np.concatenate([cs, cs], 0)
    c["ropeS_s"] = np.concatenate([ss, ss], 0)
    lg = np.log1p(-np.exp2(-5.0 - np.arange(4, dtype=np.float64)))
    i = np.arange(CH, dtype=np.float64)
    rt = np.zeros((128, 2, 2, 3, CH), np.float64)
    for p in range(128):
        for g in range(2):
            l_ = lg[2 * g + p // 64]
            for d in range(2):
                csum = (i + 1) * l_ if d == 0 else (CH - i) * l_
                rt[p, g, d, 0] = np.exp(csum)
                rt[p, g, d, 1] = np.exp(-csum) / 8.0
                rt[p, g, d, 2] = np.exp(CH * l_ - csum) / 8.0
    c["ret_tab"] = rt.astype(np.float32)
    c["ret_pc"] = np.tile(np.exp(CH * lg)[None, :], (64, 2)).astype(np.float32)
    s_ = np.arange(64)[:, None]
    t_ = np.arange(64)[None, :]
    lt = (s_ < t_).astype(np.float32)
    gt = (s_ > t_).astype(np.float32)
    eye = np.eye(64, dtype=np.float32)
    c["m_st"] = np.concatenate([lt] * 4 + [gt] * 4, 1)
    c["m_ts"] = np.concatenate([gt] * 4 + [lt] * 4, 1)
    c["m_in"] = np.concatenate([lt + eye] * 4 + [gt + eye] * 4, 1)
    c["id8"] = np.concatenate([eye] * 8, 1)
    j = np.arange(128)[:, None]
    ii = np.arange(128)[None, :]
    c["mprev"] = (ii <= j).astype(np.float32)
    c["mnext"] = (j <= ii).astype(np.float32)
    return c


def _host_maps(inp, T, CTX, L):
    f = lambda a: np.ascontiguousarray(np.asarray(a, dtype=np.float32))
    w_in = f(inp["w_in"])
    pa = np.concatenate([np.arange(16, 32), np.arange(0, 16), np.arange(48, 64), np.arange(32, 48)])
    psq = np.concatenate([np.arange(32, 64), np.arange(0, 32)])
    cols = []
    for base, nh, perm in ((0, 4, pa), (256, 2, pa), (512, 4, pa), (768, 2, pa), (1024, 4, psq), (1280, 4, psq)):
        for h in range(nh):
            cols.append(base + h * 64 + perm)
    cols = np.concatenate(cols)
    w_sw = np.ascontiguousarray(w_in[:, :, cols])
    dcols = [np.concatenate([np.arange(2304, 3072), np.arange(3200, 3264), np.arange(3328, 3392)]),
             np.concatenate([np.arange(2304, 3072), np.arange(3264, 3328), np.arange(3392, 3456)])]
    w_d = np.ascontiguousarray(np.stack([w_in[:, :, dc] for dc in dcols], 1))
    qkg = f(inp["qk_norm_g"])
    pcols = np.zeros((L, 128, 24), np.float32)
    two = lambda v: np.ascontiguousarray(v.reshape(2, 128).T)
    for l in range(L):
        for qk in range(2):
            g = qkg[l, qk]
            pcols[l, :, 2 * qk] = np.tile(g, 2)
            pcols[l, :, 2 * qk + 1] = np.tile(g[pa], 2)
        for d in range(2):
            pcols[l, :, 4 + 2 * d:6 + 2 * d] = two(f(inp["rwkv_w0"])[l, d])
            pcols[l, :, 8 + 2 * d:10 + 2 * d] = two(f(inp["rwkv_a0"])[l, d])
            pcols[l, :, 16 + 2 * d:18 + 2 * d] = two(f(inp["rwkv_rho"])[l, d].reshape(256))
        pcols[l, :, 12:14] = two(f(inp["rwkv_k_k"])[l])
        pcols[l, :, 14:16] = two(f(inp["rwkv_k_a"])[l])
    shared = dict(
        w_mod=f(inp["w_mod"]), b_mod=f(inp["b_mod"]), norm_g=f(inp["norm_g"]), ffn_w_in=f(inp["ffn_w_in"]),
        ffn_w_out=f(inp["ffn_w_out"]), w_in=w_in, w_out=f(inp["w_out"]), final_g=f(inp["final_norm_g"]),
        w_sw=w_sw, w_d=w_d, pcols=pcols, mu=f(inp["rwkv_mu"]), w2=f(inp["rwkv_w2"]), a2=f(inp["rwkv_a2"]),
        g2=f(inp["rwkv_g2"]), ret_g=f(inp["ret_norm_g"]), ln_g=f(inp["rwkv_ln_g"]), ln_b=f(inp["rwkv_ln_b"]),
        sink=f(inp["attn_sink"]))
    shared.update(_consts(T))
    x, c, ctx, c_ctx = f(inp["x"]), f(inp["c"]), f(inp["ctx"]), f(inp["c_ctx"])
    maps = []
    for b in range(x.shape[0]):
        m = dict(shared)
        m["xin"] = np.ascontiguousarray(np.concatenate([x[b], ctx[b]], 0))
        m["cvec"] = np.ascontiguousarray(np.stack([c[b], c_ctx], 0))
        maps.append(m)
    return maps


_PROG = {}


def kernel(**inputs):
    x = np.asarray(inputs["x"])
    B, T, _ = x.shape
    CTX = np.asarray(inputs["ctx"]).shape[1]
    L = np.asarray(inputs["w_in"]).shape[0]
    key = (T, CTX, L)
    if key not in _PROG:
        _PROG[key] = build_program(T, CTX, L)[0]
    nc = _PROG[key]
    maps = _host_maps(inputs, T, CTX, L)
    res = run_bass_kernel_spmd(nc, maps, core_ids=list(range(B)))
    return np.stack([np.asarray(r["out"], dtype=np.float32) for r in res.results], 0)
```

```python
import contextlib
import numpy as np
import concourse.bass as bass
import concourse.mybir as mybir
from concourse.bass_utils import run_bass_kernel_spmd

ACT = mybir.ActivationFunctionType
ALU = mybir.AluOpType
AX = mybir.AxisListType
F32 = mybir.dt.float32
BF16 = mybir.dt.bfloat16

D = 1024
DFF = 2816
NMOD = 9
RMS_EPS = 1e-6
STRICT = True


class PSem:
    def __init__(self, h):
        self.h = h
        self.val = 0


class Eng:
    def __init__(self, name, h, sem):
        self.name, self.h, self.sem = name, h, sem
        self.n = 0
        self.seen = {}
        self.dseen = {}


class Buf:
    def __init__(self, k, t, name):
        self.k, self.t, self.name = k, t, name
        self.w = None
        self.r = {}
        self.psem = None
        self.dma_w = 0
        self.dma_rw = 0

    def __getitem__(self, key):
        return self.t[key]

    @property
    def v(self):
        return _VA(self)


class View:
    def __init__(self, buf, ap):
        self.buf, self.ap = buf, ap

    def rr(self, pat, **kw):
        return View(self.buf, self.ap.rearrange(pat, **kw))

    def bc(self, shape):
        return View(self.buf, self.ap.to_broadcast(list(shape)))

    def __getitem__(self, key):
        return View(self.buf, self.ap[key])


class _VA:
    def __init__(self, b):
        self.b = b

    def __getitem__(self, key):
        return View(self.b, self.b.t[key])


def _bufs(*xs):
    out = []
    for x in xs:
        if isinstance(x, View) and x.buf not in out:
            out.append(x.buf)
    return out


def _ap(x):
    return x.ap if isinstance(x, View) else x


class K:
    def __init__(self, nc):
        self.nc = nc
        self.stack = contextlib.ExitStack()
        self.E = {}
        for name, h in (("pe", nc.tensor), ("act", nc.scalar), ("dve", nc.vector),
                        ("pool", nc.gpsimd), ("sp", nc.sync)):
            sem = self.stack.enter_context(nc.semaphore("s_" + name))
            self.E[name] = Eng(name, h, sem)
        self.free_psems = []
        self.n_psems = 0
        self.all_psems = []
        self.bufs = []
        self.uid = 0
        self.ninstr = 0
        self.marks = []

    def _psem(self):
        if self.free_psems:
            return self.free_psems.pop()
        self.n_psems += 1
        h = self.stack.enter_context(self.nc.semaphore("d%d" % self.n_psems))
        p = PSem(h)
        self.all_psems.append(p)
        return p

    def sb(self, st, shape, dtype, name):
        self.uid += 1
        t = st.enter_context(self.nc.sbuf_tensor("%s_%d" % (name, self.uid), list(shape), dtype))
        b = Buf(self, t, name)
        st.callback(self._release, b)
        return b

    def ps(self, st, shape, dtype, name):
        self.uid += 1
        t = st.enter_context(self.nc.psum_tensor("%s_%d" % (name, self.uid), list(shape), dtype))
        return Buf(self, t, name)

    def _release(self, b):
        if b.psem is not None and not getattr(b.psem, "sw", False):
            self.free_psems.append(b.psem)
        b.psem = None

    def _wait_eng(self, E, X, n):
        if E.seen.get(X.name, 0) >= n:
            return
        E.h.wait_ge(X.sem, n)
        E.seen[X.name] = n
        self.ninstr += 1

    def _wait_d(self, E, p, v):
        if v <= 0 or E.dseen.get(id(p), 0) >= v:
            return
        E.h.wait_ge(p.h, v)
        E.dseen[id(p)] = v
        self.ninstr += 1

    def _deps(self, E, r, w):
        for b in r:
            if b.w is not None:
                self._wait_eng(E, b.w[0], b.w[1])
            if b.psem is not None:
                self._wait_d(E, b.psem, b.dma_w)
        for b in w:
            if b.w is not None and (STRICT or b.w[0] is not E) and not (E.name == "pe" and b.w[0] is E):
                self._wait_eng(E, b.w[0], b.w[1])
            for X, n in b.r.items():
                if STRICT or X is not E:
                    self._wait_eng(E, X, n)
            if b.psem is not None:
                self._wait_d(E, b.psem, b.dma_rw)

    def op(self, eng, fn, r=(), w=()):
        E = self.E[eng]
        self._deps(E, r, w)
        ins = fn(E.h)
        E.n += 1
        ins.then_inc(E.sem, 1)
        self.ninstr += 1
        for b in w:
            b.w = (E, E.n)
            b.r = {}
        for b in r:
            if b.w is None or b.w[0] is not E or b.w[1] != E.n:
                b.r[E] = E.n
        return ins

    def dma(self, q, out, in_, r=(), w=(), **kw):
        Q = self.E[q]
        self._deps(Q, r, w)
        anchor = w[0] if w else r[0]
        if anchor.psem is None:
            if q == "pool":
                self.n_psems += 1
                anchor.psem = PSem(self.stack.enter_context(self.nc.semaphore("w%d" % self.n_psems)))
                anchor.psem.sw = True
                self.all_psems.append(anchor.psem)
            else:
                anchor.psem = self._psem()
        p = anchor.psem
        assert getattr(p, "sw", False) == (q == "pool"), "buffer mixes SW and HW DGE DMAs: " + anchor.name
        Q.h.dma_start(out=out, in_=in_, **kw).then_inc(p.h, 16)
        p.val += 16
        self.ninstr += 1
        anchor.dma_rw = p.val
        if w:
            anchor.dma_w = p.val
            anchor.w = None
            anchor.r = {}

    def mm(self, out, lhsT, rhs, start=True, stop=True):
        return self.op("pe", lambda e: e.matmul(out.ap, lhsT.ap, rhs.ap, start=start, stop=stop),
                       r=_bufs(lhsT, rhs), w=[out.buf])

    def tr(self, out, in_, ident):
        return self.op("pe", lambda e: e.transpose(out.ap, in_.ap, ident.ap), r=_bufs(in_, ident), w=[out.buf])

    def tt(self, eng, out, in0, in1, op):
        return self.op(eng, lambda e: e.tensor_tensor(out=out.ap, in0=in0.ap, in1=in1.ap, op=op),
                       r=_bufs(in0, in1), w=[out.buf])

    def ts(self, eng, out, in0, s1, s2, op0, op1=None):
        kw = {} if op1 is None else dict(op1=op1)
        return self.op(eng, lambda e: e.tensor_scalar(out=out.ap, in0=in0.ap, scalar1=_ap(s1), scalar2=_ap(s2),
                                                      op0=op0, **kw), r=_bufs(in0, s1, s2), w=[out.buf])

    def stt(self, eng, out, in0, scalar, in1, op0, op1):
        return self.op(eng, lambda e: e.scalar_tensor_tensor(out=out.ap, in0=in0.ap, scalar=_ap(scalar), in1=in1.ap,
                                                             op0=op0, op1=op1), r=_bufs(in0, scalar, in1), w=[out.buf])

    def act(self, out, in_, func, bias=None, scale=None, accum=None):
        kw = {}
        if bias is not None:
            kw["bias"] = _ap(bias)
        if scale is not None:
            kw["scale"] = _ap(scale)
        w = [out.buf]
        if accum is not None:
            kw["accum_out"] = accum.ap
            w.append(accum.buf)
        return self.op("act", lambda e: e.activation(out=out.ap, in_=in_.ap, func=func, **kw),
                       r=_bufs(in_, bias, scale), w=w)

    def cp(self, eng, out, in_):
        if eng == "act":
            return self.op("act", lambda e: e.copy(out=out.ap, in_=in_.ap), r=[in_.buf], w=[out.buf])
        return self.op(eng, lambda e: e.tensor_copy(out=out.ap, in_=in_.ap), r=[in_.buf], w=[out.buf])

    def memset(self, eng, out, val):
        return self.op(eng, lambda e: e.memset(out.ap, val), w=[out.buf])

    def recip(self, out, in_):
        return self.op("dve", lambda e: e.reciprocal(out=out.ap, in_=in_.ap), r=[in_.buf], w=[out.buf])

    def ld(self, q, out, dram, **kw):
        return self.dma(q, out.ap, dram, w=[out.buf], **kw)

    def stv(self, q, dram, in_, **kw):
        return self.dma(q, dram, in_.ap, r=[in_.buf], **kw)

    def barrier(self):
        for E in self.E.values():
            for X in self.E.values():
                if X is not E and X.n > 0:
                    self._wait_eng(E, X, X.n)
            for p in self.all_psems:
                self._wait_d(E, p, p.val)


def mm(k, out_ap, lhsT_ap, rhs_ap, start, stop, r, w):
    return k.op("pe", lambda e: e.matmul(out_ap, lhsT_ap, rhs_ap, start=start, stop=stop), r=r, w=w)


def rowphase(k, st, cfg):
    nc = k.nc
    ffn = cfg["ffn"]
    pre = cfg.get("pre")
    post = cfg.get("post")
    w1 = k.sb(st, [128, 8, 2 * DFF], BF16, "w1")
    w2 = k.sb(st, [128, 22, D], BF16, "w2")
    w1v = ffn["w_in"].rearrange("(c p) f -> p c f", p=128)
    for c in range(8):
        for hlf in range(4):
            f0 = hlf * 1408
            k.dma("pool", w1[:, c, f0:f0 + 1408], w1v[:, c, f0:f0 + 1408], w=[w1])
    w2v = ffn["w_out"].rearrange("(c p) d -> p c d", p=128)
    for c in range(22):
        k.dma("pool", w2[:, c, :], w2v[:, c, :], w=[w2])
    if pre is not None:
        wo = k.sb(st, [128, 8, D], BF16, "wo")
        wov = pre["w_out"].rearrange("(c p) d -> p c d", p=128)
        for c in range(8):
            k.dma("pool", wo[:, c, :], wov[:, c, :], w=[wo])

    identf = k.sb(st, [128, 128], F32, "identf")
    k.dma("sp", identf[:, :], cfg["ident"], w=[identf])
    xt = [k.sb(st, [128, D], F32, "xt%d" % i) for i in range(4)]
    hT = k.sb(st, [128, 8, 512], BF16, "hT")
    gT = k.sb(st, [128, 22, 512], BF16, "gT")
    sg = k.sb(st, [128, 512], BF16, "sg")
    tmp = k.sb(st, [128, D], F32, "tmp")
    ssq = k.sb(st, [128, 1], F32, "ssq")
    rinv = k.sb(st, [128, 1], F32, "rinv")
    ps_u = [k.ps(st, [128, 512], F32, "psu%d" % i) for i in range(4)]
    ps_y = [k.ps(st, [128, 512], F32, "psy%d" % i) for i in range(2)]
    ps_t = k.ps(st, [128, D], F32, "pst")
    Ac = k.sb(st, [128, 8], F32, "Ac")
    Bc = k.sb(st, [128, 8], F32, "Bc")
    Gc = k.sb(st, [128, 8], F32, "Gc")
    A2c = k.sb(st, [128, 8], F32, "A2c")
    B2c = k.sb(st, [128, 8], F32, "B2c")
    G = k.sb(st, [128, D], F32, "G")
    PG = k.sb(st, [128, D], F32, "PG") if pre is not None else None
    gvec2 = None
    if post is not None and post["kind"] == "final":
        gvec2 = k.sb(st, [128, D], F32, "gvec2")
        k.dma("sp", gvec2[:, :], post["g"].partition_broadcast(128), w=[gvec2])

    def col(ap1d):
        return ap1d.rearrange("(c p) -> p c", p=128)

    def load_cols(dst, ap1d):
        k.dma("sp", dst[:, :], col(ap1d), w=[dst], allow_slow_non_contiguous=True)

    def setup_mod(mr):
        load_cols(Gc, ffn["g"])
        load_cols(Ac, cfg["mod"][mr, ffn["scale"] * D:(ffn["scale"] + 1) * D])
        load_cols(Bc, cfg["mod"][mr, ffn["shift"] * D:(ffn["shift"] + 1) * D])
        k.op("dve", lambda e: e.scalar_tensor_tensor(out=Ac[:, :], in0=Ac[:, :], scalar=1.0, in1=Gc[:, :],
                                                     op0=ALU.add, op1=ALU.mult), r=[Ac, Gc], w=[Ac])
        if post is not None and post["kind"] == "hT":
            load_cols(Gc, post["g"])
            load_cols(A2c, cfg["mod"][mr, post["scale"] * D:(post["scale"] + 1) * D])
            load_cols(B2c, cfg["mod"][mr, post["shift"] * D:(post["shift"] + 1) * D])
            k.op("dve", lambda e: e.scalar_tensor_tensor(out=A2c[:, :], in0=A2c[:, :], scalar=1.0, in1=Gc[:, :],
                                                         op0=ALU.add, op1=ALU.mult), r=[A2c, Gc], w=[A2c])
        k.dma("sp", G[:, :], cfg["mod"][mr, ffn["gate"] * D:(ffn["gate"] + 1) * D].partition_broadcast(128), w=[G])
        k.op("dve", lambda e: e.tensor_scalar(out=G[:, :], in0=G[:, :], scalar1=0.5, scalar2=None, op0=ALU.mult),
             r=[G], w=[G])
        if pre is not None:
            k.dma("sp", PG[:, :], cfg["mod"][mr, pre["gate"] * D:(pre["gate"] + 1) * D].partition_broadcast(128),
                  w=[PG])

    def rms_rinv(xb):
        k.op("act", lambda e: e.activation(out=tmp[:, :], in_=xb[:, :], func=ACT.Square, accum_out=ssq[:, :]),
             r=[xb], w=[tmp, ssq])
        k.op("dve", lambda e: e.tensor_scalar(out=rinv[:, :], in0=ssq[:, :], scalar1=1.0 / D, scalar2=RMS_EPS,
                                              op0=ALU.mult, op1=ALU.add), r=[ssq], w=[rinv])
        k.op("act", lambda e: e.activation(out=rinv[:, :], in_=rinv[:, :], func=ACT.Sqrt), r=[rinv], w=[rinv])
        k.op("dve", lambda e: e.reciprocal(out=rinv[:, :], in_=rinv[:, :]), r=[rinv], w=[rinv])

    def norm_mod_T(xb, A, B, dstT, s):
        rms_rinv(xb)
        k.op("dve", lambda e: e.tensor_scalar(out=tmp[:, :], in0=xb[:, :], scalar1=rinv[:, 0:1], scalar2=None,
                                              op0=ALU.mult), r=[xb, rinv], w=[tmp])
        for c in range(8):
            k.op("pe", lambda e: e.transpose(ps_t[:, c * 128:(c + 1) * 128], tmp[:, c * 128:(c + 1) * 128],
                                             identf[:, :]), r=[tmp, identf], w=[ps_t])
        for c in range(8):
            if c % 2 == 0:
                k.op("dve", lambda e: e.tensor_scalar(out=dstT[:, c, s * 128:(s + 1) * 128],
                                                      in0=ps_t[:, c * 128:(c + 1) * 128], scalar1=A[:, c:c + 1],
                                                      scalar2=B[:, c:c + 1], op0=ALU.mult, op1=ALU.add),
                     r=[ps_t, A, B], w=[dstT])
            else:
                k.op("act", lambda e: e.activation(out=dstT[:, c, s * 128:(s + 1) * 128],
                                                   in_=ps_t[:, c * 128:(c + 1) * 128], func=ACT.Identity,
                                                   scale=A[:, c:c + 1], bias=B[:, c:c + 1]),
                     r=[ps_t, A, B], w=[dstT])

    ui = 0
    yi = 0
    for (tok0, ntok, mr) in cfg["segs"]:
        setup_mod(mr)
        for t0 in range(tok0, tok0 + ntok, 512):
            n = min(512, tok0 + ntok - t0)
            ns = n // 128
            for s in range(ns):
                k.dma("sp", xt[s][:, :], cfg["xin"][t0 + s * 128:t0 + (s + 1) * 128, :], w=[xt[s]])
            if pre is not None:
                for s in range(ns):
                    k.dma("sp", tmp[:, :], pre["o"][t0 + s * 128:t0 + (s + 1) * 128, :], w=[tmp])
                    for c in range(8):
                        k.op("pe", lambda e: e.transpose(ps_t[:, c * 128:(c + 1) * 128], tmp[:, c * 128:(c + 1) * 128],
                                                         identf[:, :]), r=[tmp, identf], w=[ps_t])
                    k.op("act", lambda e: e.copy(out=hT[:, :, s * 128:(s + 1) * 128],
                                                 in_=ps_t[:, :].rearrange("p (c t) -> p c t", c=8)), r=[ps_t], w=[hT])
                for s in range(ns):
                    for hf in range(2):
                        py = ps_y[yi % 2]
                        yi += 1
                        for c in range(8):
                            mm(k, py[:, :], hT[:, c, s * 128:(s + 1) * 128], wo[:, c, hf * 512:(hf + 1) * 512],
                               c == 0, c == 7, r=[hT, wo], w=[py])
                        sl = slice(hf * 512, (hf + 1) * 512)
                        k.op("dve", lambda e, py=py, sl=sl: e.tensor_tensor(out=tmp[:, sl], in0=py[:, :],
                                                                             in1=PG[:, sl], op=ALU.mult),
                             r=[py, PG], w=[tmp])
                        k.op("dve", lambda e, s=s, sl=sl: e.tensor_tensor(out=xt[s][:, sl], in0=xt[s][:, sl],
                                                                            in1=tmp[:, sl], op=ALU.add),
                             r=[xt[s], tmp], w=[xt[s]])
            for s in range(ns):
                norm_mod_T(xt[s], Ac, Bc, hT, s)
            for fc in range(22 if cfg.get("stage", 9) >= 2 else 0):
                p1 = ps_u[ui % 4]
                p2 = ps_u[(ui + 1) % 4]
                ui += 2
                for c in range(8):
                    mm(k, p1[:, 0:n], w1[:, c, fc * 128:(fc + 1) * 128], hT[:, c, 0:n], c == 0, c == 7,
                       r=[w1, hT], w=[p1])
                for c in range(8):
                    mm(k, p2[:, 0:n], w1[:, c, DFF + fc * 128:DFF + (fc + 1) * 128], hT[:, c, 0:n], c == 0, c == 7,
                       r=[w1, hT], w=[p2])
                k.op("act", lambda e, p1=p1: e.activation(out=sg[:, 0:n], in_=p1[:, 0:n], func=ACT.Silu),
                     r=[p1], w=[sg])
                k.op("dve", lambda e, p2=p2, fc=fc: e.tensor_tensor(out=gT[:, fc, 0:n], in0=sg[:, 0:n],
                                                                     in1=p2[:, 0:n], op=ALU.mult),
                     r=[sg, p2], w=[gT])
            for s in range(ns if cfg.get("stage", 9) >= 3 else 0):
                for hf in range(2):
                    py = ps_y[yi % 2]
                    yi += 1
                    for fc in range(22):
                        mm(k, py[:, :], gT[:, fc, s * 128:(s + 1) * 128], w2[:, fc, hf * 512:(hf + 1) * 512],
                           fc == 0, fc == 21, r=[gT, w2], w=[py])
                    sl = slice(hf * 512, (hf + 1) * 512)
                    k.op("dve", lambda e, py=py, sl=sl: e.tensor_tensor(out=tmp[:, sl], in0=py[:, :], in1=G[:, sl],
                                                                         op=ALU.mult), r=[py, G], w=[tmp])
                    k.op("dve", lambda e, s=s, sl=sl: e.tensor_tensor(out=xt[s][:, sl], in0=xt[s][:, sl],
                                                                        in1=tmp[:, sl], op=ALU.add),
                         r=[xt[s], tmp], w=[xt[s]])
            for s in range(ns):
                r0 = t0 + s * 128
                if post is not None and post["kind"] == "final":
                    if mr == 0:
                        rms_rinv(xt[s])
                        k.op("dve", lambda e, s=s: e.scalar_tensor_tensor(out=tmp[:, :], in0=xt[s][:, :],
                                                                           scalar=rinv[:, 0:1], in1=gvec2[:, :],
                                                                           op0=ALU.mult, op1=ALU.mult),
                             r=[xt[s], rinv, gvec2], w=[tmp])
                        k.dma("sp", post["out"][r0:r0 + 128, :], tmp[:, :], r=[tmp])
                    continue
                k.dma("sp", cfg["xout"][r0:r0 + 128, :], xt[s][:, :], r=[xt[s]])
                if post is not None and post["kind"] == "hT":
                    norm_mod_T(xt[s], A2c, B2c, hT, s)
            if post is not None and post["kind"] == "hT":
                for c in range(8):
                    k.dma("sp", post["hT"][c * 128:(c + 1) * 128, t0:t0 + n], hT[:, c, 0:n], r=[hT])


CH = 64
DECAY_SCALE = 0.6065306597126334


def projphase(k, st, cfg):
    S = cfg["S"]
    C = cfg["C"]
    Wt = cfg["W"]
    T, CTX = cfg["T"], cfg["CTX"]
    hTd = S["hT"]
    wabc = k.sb(st, [128, 8, 2304], BF16, "wabc")
    wgd = k.sb(st, [128, 8, 128], BF16, "wgd")
    wsw = k.sb(st, [128, 8, 1280], BF16, "wsw")
    wv = Wt["w_in"].rearrange("(c p) f -> p c f", p=128)
    for c in range(8):
        k.ld("pool", wabc.v[:, c, 0:1152], wv[:, c, 0:1152])
        k.ld("pool", wabc.v[:, c, 1152:2304], wv[:, c, 1152:2304])
        k.ld("pool", wgd.v[:, c, :], wv[:, c, 3072:3200])
        k.ld("pool", wsw.v[:, c, :], Wt["w_sw"].rearrange("(c p) f -> p c f", p=128)[:, c, :])
    wda = [k.sb(st, [128, 8, 896], BF16, "wda%d" % d) for d in range(2)]
    wdb = [k.sb(st, [128, 8, 896], BF16, "wdb%d" % d) for d in range(2)]
    with contextlib.ExitStack() as st2:
        wst = k.sb(st2, [128, 8, 896], F32, "wst")
        mub = k.sb(st2, [128, 896], F32, "mub")
        wtm = k.sb(st2, [128, 8, 896], F32, "wtm")
        for d in range(2):
            for c in range(8):
                k.ld("sp", wst.v[:, c, :], Wt["w_d"][d].rearrange("(c p) f -> p c f", p=128)[:, c, :])
            k.ld("sp", mub.v[:, :], Wt["mu"][d].partition_broadcast(128))
            for c in range(8):
                k.tt("dve", wtm.v[:, c, :], wst.v[:, c, :], mub.v[:, :], ALU.mult)
            k.cp("act", wda[d].v[:, :, :], wtm.v[:, :, :])
            k.tt("dve", wdb[d].v[:, :, :], wst.v[:, :, :], wtm.v[:, :, :], ALU.subtract)
        k.barrier()
    pc = k.sb(st, [128, 24], F32, "pcols")
    k.ld("sp", pc.v[:, :], Wt["pcols"])
    oma = k.sb(st, [128, 2], F32, "oma")
    k.ts("dve", oma.v[:, :], pc.v[:, 14:16], -1.0, 1.0, ALU.mult, ALU.add)
    identf = k.sb(st, [128, 128], F32, "identf")
    k.ld("sp", identf.v[:, :], C["ident"])
    bd = k.sb(st, [128, 128], F32, "bd")
    k.ld("sp", bd.v[:, :], C["bd64"])
    e2 = k.sb(st, [128, 2], F32, "e2")
    k.ld("sp", e2.v[:, :], C["e2"])
    rtab = k.sb(st, [128, 2, 2, 3, CH], F32, "rtab")
    k.ld("sp", rtab.v[:, :, :, :, :], C["ret_tab"])
    w2s = k.sb(st, [64, 2, 256], BF16, "w2s")
    a2s = k.sb(st, [64, 2, 256], BF16, "a2s")
    g2s = k.sb(st, [128, 256], BF16, "g2s")
    for d in range(2):
        k.ld("pool", w2s.v[:, d, :], Wt["w2"][d])
        k.ld("pool", a2s.v[:, d, :], Wt["a2"][d])
    k.ld("pool", g2s.v[:, :], Wt["g2"])

    NT = 512
    hTt = k.sb(st, [128, 8, NT + 2], BF16, "hTt")
    ca = k.sb(st, [128, NT], F32, "ca")
    sa = k.sb(st, [128, NT], F32, "sa")
    cs_ = k.sb(st, [128, NT], F32, "cs")
    ss_ = k.sb(st, [128, NT], F32, "ss")
    gca = k.sb(st, [128, 2, NT], F32, "gca")
    gsa = k.sb(st, [128, 2, NT], F32, "gsa")
    F = [k.sb(st, [128, NT], F32, "f%d" % i) for i in range(12)]
    ob = k.sb(st, [128, NT], BF16, "ob")
    tb = k.sb(st, [128, 512], F32, "tb")
    tbb = k.sb(st, [128, 512], BF16, "tbb")
    thb = k.sb(st, [64, NT], BF16, "thb")
    alb = k.sb(st, [64, NT], BF16, "alb")
    sgb = k.sb(st, [128, NT], BF16, "sgb")
    bs4 = k.sb(st, [128, 4, 4], F32, "bs4")
    pcx = k.sb(st, [128, NT // CH], F32, "pcx")
    PS = [k.ps(st, [128, 512], F32, "pp%d" % i) for i in range(8)]
    pi = [0]

    def nps():
        p = PS[pi[0] % 8]
        pi[0] += 1
        return p

    def chain(p, wlist, M, n, col0=1):
        tot = len(wlist) * 8
        i = 0
        for (wb, c0, sh) in wlist:
            for c in range(8):
                k.mm(p.v[0:M, 0:n], wb.v[:, c, c0:c0 + M], hTt.v[:, c, col0 + sh:col0 + sh + n], i == 0, i == tot - 1)
                i += 1

    def chain_tok(p, wlist, ncols, s):
        tot = len(wlist) * 8
        i = 0
        for (wb, c0, sh) in wlist:
            for c in range(8):
                k.mm(p.v[:, 0:ncols], hTt.v[:, c, 1 + sh + s * 128:1 + sh + (s + 1) * 128], wb.v[:, c, c0:c0 + ncols],
                     i == 0, i == tot - 1)
                i += 1

    def rope_out(P, Psw, ctab, stab, dst, n):
        k.tt("dve", F[10].v[:, 0:n], P.v[:, 0:n], ctab, ALU.mult)
        k.tt("dve", F[11].v[:, 0:n], Psw.v[:, 0:n], stab, ALU.mult)
        k.tt("dve", dst, F[10].v[:, 0:n], F[11].v[:, 0:n], ALU.add)

    segs = [(0, T, True), (T, CTX, False)]
    for (seg0, seglen, latent) in segs:
        for t0 in range(seg0, seg0 + seglen, NT):
            n = min(NT, seg0 + seglen - t0)
            ns = n // 128
            nch = n // CH
            lo = t0 - 1 if t0 > seg0 else t0
            hi = t0 + n + 1 if t0 + n < seg0 + seglen else t0 + n
            if lo == t0:
                k.memset("dve", hTt.v[:, :, 0:1], 0.0)
            if hi == t0 + n:
                k.memset("dve", hTt.v[:, :, n + 1:n + 2], 0.0)
            for c in range(8):
                k.ld("sp", hTt.v[:, c, 1 - (t0 - lo):1 + n + (hi - t0 - n)], hTd[c * 128:(c + 1) * 128, lo:hi])
            if latent:
                k.ld("sp", ca.v[:, 0:n], C["ropeA_c"][:, t0:t0 + n])
                k.ld("sp", sa.v[:, 0:n], C["ropeA_s"][:, t0:t0 + n])
                k.ld("sp", cs_.v[:, 0:n], C["ropeS_c"][:, t0:t0 + n])
                k.ld("sp", ss_.v[:, 0:n], C["ropeS_s"][:, t0:t0 + n])
                for qk in range(2):
                    k.ts("dve", gca.v[:, qk, 0:n], ca.v[:, 0:n], pc.v[:, 2 * qk:2 * qk + 1], None, ALU.mult)
                    k.ts("dve", gsa.v[:, qk, 0:n], sa.v[:, 0:n], pc.v[:, 2 * qk + 1:2 * qk + 2], None, ALU.mult)
            for grp, base, swb, dq, dk_, dv_ in (("A", 0, 0, S["QA"], S["KA"], S["VA"]),
                                                ("B", 512, 384, S["QB"], S["KB"], S["VB"])):
                for j in range(3):
                    P, Psw = nps(), nps()
                    chain(P, [(wabc, base + j * 128, 0)], 128, n)
                    if latent:
                        chain(Psw, [(wsw, swb + j * 128, 0)], 128, n)
                    qk = 0 if j < 2 else 1
                    if grp == "B":
                        k.act(F[0].v[:, 0:n], P.v[:, 0:n], ACT.Square)
                        pss = nps()
                        k.mm(pss.v[:, 0:n], bd.v[:, :], F[0].v[:, 0:n])
                        k.ts("dve", F[1].v[:, 0:n], pss.v[:, 0:n], 1.0 / 64, RMS_EPS, ALU.mult, ALU.add)
                        k.act(F[1].v[:, 0:n], F[1].v[:, 0:n], ACT.Sqrt)
                        k.recip(F[1].v[:, 0:n], F[1].v[:, 0:n])
                        k.tt("dve", F[2].v[:, 0:n], P.v[:, 0:n], F[1].v[:, 0:n], ALU.mult)
                        if latent:
                            k.tt("dve", F[3].v[:, 0:n], Psw.v[:, 0:n], F[1].v[:, 0:n], ALU.mult)
                            rope_out(F[2], F[3], gca.v[:, qk, 0:n], gsa.v[:, qk, 0:n], ob.v[:, 0:n], n)
                        else:
                            k.ts("dve", ob.v[:, 0:n], F[2].v[:, 0:n], pc.v[:, 2 * qk:2 * qk + 1], None, ALU.mult)
                    else:
                        if latent:
                            rope_out(P, Psw, ca.v[:, 0:n], sa.v[:, 0:n], ob.v[:, 0:n], n)
                        else:
                            k.cp("act", ob.v[:, 0:n], P.v[:, 0:n])
                    dst = dq[j * 128:(j + 1) * 128, t0:t0 + n] if j < 2 else dk_[:, t0:t0 + n]
                    k.stv("sp", dst, ob.v[:, 0:n])
                for s in range(ns):
                    P = nps()
                    chain_tok(P, [(wabc, base + 384, 0)], 128, s)
                    k.cp("act", tbb.v[:, 0:128], P.v[:, 0:128])
                    k.stv("sp", dv_[t0 + s * 128:t0 + (s + 1) * 128, :], tbb.v[:, 0:128])
            for j in range(4):
                P, Psw = nps(), nps()
                chain(P, [(wabc, 1024 + j * 128, 0)], 128, n)
                g = j % 2
                if latent:
                    chain(Psw, [(wsw, 768 + j * 128, 0)], 128, n)
                    rope_out(P, Psw, cs_.v[:, 0:n], ss_.v[:, 0:n], F[0].v[:, 0:n], n)
                else:
                    k.cp("act", F[0].v[:, 0:n], P.v[:, 0:n])
                x3 = F[0].v[:, 0:n].rr("p (c i) -> p c i", i=CH)
                for d in range(2):
                    if j < 2:
                        k.tt("dve", F[1].v[:, 0:n].rr("p (c i) -> p c i", i=CH), x3,
                             rtab.v[:, g, d, 0:1, :].bc([128, nch, CH]), ALU.mult)
                        k.stv("sp", S["C_Rt"][d, g * 128:(g + 1) * 128, t0:t0 + n], F[1].v[:, 0:n])
                    else:
                        k.tt("dve", F[1].v[:, 0:n].rr("p (c i) -> p c i", i=CH), x3,
                             rtab.v[:, g, d, 1:2, :].bc([128, nch, CH]), ALU.mult)
                        k.stv("sp", S["C_Kt"][d, g * 128:(g + 1) * 128, t0:t0 + n], F[1].v[:, 0:n])
                        k.tt("dve", F[2].v[:, 0:n].rr("p (c i) -> p c i", i=CH), x3,
                             rtab.v[:, g, d, 2:3, :].bc([128, nch, CH]), ALU.mult)
                        for s in range(ns):
                            pt = nps()
                            k.tr(pt.v[:, 0:128], F[2].v[:, s * 128:(s + 1) * 128], identf.v[:, :])
                            k.cp("act", tb.v[:, 0:128], pt.v[:, 0:128])
                            k.stv("sp", S["C_Kh"][d, t0 + s * 128:t0 + (s + 1) * 128, g * 128:(g + 1) * 128],
                                  tb.v[:, 0:128])
            for s in range(ns):
                P = nps()
                chain_tok(P, [(wabc, 1536, 0)], 256, s)
                k.cp("act", tb.v[:, 0:256], P.v[:, 0:256])
                k.stv("sp", S["C_V"][t0 + s * 128:t0 + (s + 1) * 128, :], tb.v[:, 0:256])
                P = nps()
                chain_tok(P, [(wabc, 1792, 0)], 512, s)
                k.act(tb.v[:, 0:512], P.v[:, 0:512], ACT.Silu)
                k.stv("sp", S["C_G"][t0 + s * 128:t0 + (s + 1) * 128, :], tb.v[:, 0:512])
            P = nps()
            chain(P, [(wgd, 0, 0)], 128, n)
            k.act(sgb.v[:, 0:n], P.v[:, 0:n], ACT.Sigmoid)
            for s in range(ns):
                P = nps()
                k.mm(P.v[:, 0:256], sgb.v[:, s * 128:(s + 1) * 128], g2s.v[:, :])
                k.cp("act", tb.v[:, 0:256], P.v[:, 0:256])
                k.stv("sp", S["D_G"][t0 + s * 128:t0 + (s + 1) * 128, :], tb.v[:, 0:256])
            for d in range(2):
                sh = -1 if d == 0 else 1
                wl = lambda c0: [(wdb[d], c0, 0), (wda[d], c0, sh)]
                P = nps()
                chain(P, wl(768), 64, n)
                k.act(thb.v[:, 0:n], P.v[0:64, 0:n], ACT.Tanh)
                P = nps()
                chain(P, wl(832), 64, n)
                k.cp("act", alb.v[:, 0:n], P.v[0:64, 0:n])
                for s in range(ns):
                    P = nps()
                    chain_tok(P, wl(512), 256, s)
                    k.cp("act", tb.v[:, 0:256], P.v[:, 0:256])
                    k.stv("sp", S["D_V"][d, t0 + s * 128:t0 + (s + 1) * 128, :], tb.v[:, 0:256])
                for g in range(2):
                    r_, k_, lw, cs, a_, kk, e_, t1, t2 = F[0], F[1], F[2], F[3], F[4], F[5], F[6], F[7], F[8]
                    gc = slice(g, g + 1)
                    P = nps()
                    chain(P, wl(g * 128), 128, n)
                    k.cp("act", r_.v[:, 0:n], P.v[:, 0:n])
                    P = nps()
                    chain(P, wl(256 + g * 128), 128, n)
                    k.cp("act", k_.v[:, 0:n], P.v[:, 0:n])
                    P = nps()
                    k.mm(P.v[:, 0:n], w2s.v[:, d, g * 128:(g + 1) * 128], thb.v[:, 0:n])
                    k.act(lw.v[:, 0:n], P.v[:, 0:n], ACT.Sigmoid, bias=pc.v[:, 4 + 2 * d + g:5 + 2 * d + g])
                    k.ts("dve", lw.v[:, 0:n], lw.v[:, 0:n], -DECAY_SCALE, None, ALU.mult)
                    P = nps()
                    k.mm(P.v[:, 0:n], a2s.v[:, d, g * 128:(g + 1) * 128], alb.v[:, 0:n])
                    k.act(a_.v[:, 0:n], P.v[:, 0:n], ACT.Sigmoid, bias=pc.v[:, 8 + 2 * d + g:9 + 2 * d + g])
                    k.ts("dve", kk.v[:, 0:n], k_.v[:, 0:n], pc.v[:, 12 + g:13 + g], None, ALU.mult)
                    k.act(t1.v[:, 0:n], kk.v[:, 0:n], ACT.Square)
                    P = nps()
                    k.mm(P.v[:, 0:n], bd.v[:, :], t1.v[:, 0:n])
                    k.act(t1.v[:, 0:n], P.v[:, 0:n], ACT.Sqrt)
                    k.ts("dve", t1.v[:, 0:n], t1.v[:, 0:n], 1e-12, None, ALU.max)
                    k.recip(t1.v[:, 0:n], t1.v[:, 0:n])
                    k.tt("dve", kk.v[:, 0:n], kk.v[:, 0:n], t1.v[:, 0:n], ALU.mult)
                    k.ts("dve", t1.v[:, 0:n], a_.v[:, 0:n], pc.v[:, 14 + g:15 + g], oma.v[:, gc], ALU.mult, ALU.add)
                    k.tt("dve", k_.v[:, 0:n], k_.v[:, 0:n], t1.v[:, 0:n], ALU.mult)
                    k.stt("dve", t1.v[:, 0:n], r_.v[:, 0:n], pc.v[:, 16 + 2 * d + g:17 + 2 * d + g], k_.v[:, 0:n],
                          ALU.mult, ALU.mult)
                    for s in range(ns):
                        P = nps()
                        k.mm(P.v[:, 0:2], t1.v[:, s * 128:(s + 1) * 128], e2.v[:, :])
                        k.cp("act", bs4.v[:, s, 2 * g:2 * g + 2], P.v[:, 0:2])
                    k.tt("dve", a_.v[:, 0:n], a_.v[:, 0:n], kk.v[:, 0:n], ALU.mult)
                    src, dst = lw, cs
                    k.cp("act", t2.v[:, 0:n], lw.v[:, 0:n])
                    src = t2
                    for stp in (1, 2, 4, 8, 16, 32):
                        s3 = src.v[:, 0:n].rr("p (c i) -> p c i", i=CH)
                        d3 = dst.v[:, 0:n].rr("p (c i) -> p c i", i=CH)
                        if d == 0:
                            k.tt("dve", d3[:, :, stp:], s3[:, :, stp:], s3[:, :, :CH - stp], ALU.add)
                            k.cp("act", d3[:, :, :stp], s3[:, :, :stp])
                        else:
                            k.tt("dve", d3[:, :, :CH - stp], s3[:, :, :CH - stp], s3[:, :, stp:], ALU.add)
                            k.cp("act", d3[:, :, CH - stp:], s3[:, :, CH - stp:])
                        src, dst = dst, src
                    cs = src
                    spare = dst
                    cs3 = cs.v[:, 0:n].rr("p (c i) -> p c i", i=CH)
                    tot3 = cs3[:, :, CH - 1:CH] if d == 0 else cs3[:, :, 0:1]
                    k.act(e_.v[:, 0:n], cs.v[:, 0:n], ACT.Exp)
                    k.tt("dve", t1.v[:, 0:n], r_.v[:, 0:n], e_.v[:, 0:n], ALU.mult)
                    k.stv("sp", S["D_Rt"][d, g * 128:(g + 1) * 128, t0:t0 + n], t1.v[:, 0:n])
                    k.tt("dve", e_.v[:, 0:n], cs.v[:, 0:n], lw.v[:, 0:n], ALU.subtract)
                    k.act(e_.v[:, 0:n], e_.v[:, 0:n], ACT.Exp)
                    k.stt("dve", t1.v[:, 0:n], kk.v[:, 0:n], -1.0, e_.v[:, 0:n], ALU.mult, ALU.mult)
                    k.stv("sp", S["D_At"][d, g * 128:(g + 1) * 128, t0:t0 + n], t1.v[:, 0:n])
                    for s in range(ns):
                        pt = nps()
                        k.tr(pt.v[:, 0:128], t1.v[:, s * 128:(s + 1) * 128], identf.v[:, :])
                        k.cp("act", tb.v[:, 0:128], pt.v[:, 0:128])
                        k.stv("sp", S["D_Atok"][d, t0 + s * 128:t0 + (s + 1) * 128, g * 128:(g + 1) * 128],
                              tb.v[:, 0:128])
                    k.act(e_.v[:, 0:n], cs.v[:, 0:n], ACT.Exp, scale=-1.0)
                    k.tt("dve", t1.v[:, 0:n], a_.v[:, 0:n], e_.v[:, 0:n], ALU.mult)
                    k.stv("sp", S["D_Bt"][d, g * 128:(g + 1) * 128, t0:t0 + n], t1.v[:, 0:n])
                    k.tt("dve", t1.v[:, 0:n], k_.v[:, 0:n], e_.v[:, 0:n], ALU.mult)
                    k.stv("sp", S["D_Kt"][d, g * 128:(g + 1) * 128, t0:t0 + n], t1.v[:, 0:n])
                    k.act(pcx.v[:, 0:nch], tot3.rr("p c o -> p (c o)"), ACT.Exp)
                    k.stv("sp", S["D_PC"][d, g * 128:(g + 1) * 128, t0 // CH:t0 // CH + nch], pcx.v[:, 0:nch])
                    k.tt("dve", e_.v[:, 0:n].rr("p (c i) -> p c i", i=CH), tot3.bc([128, nch, CH]), cs3, ALU.subtract)
                    k.act(e_.v[:, 0:n], e_.v[:, 0:n], ACT.Exp)
                    for (srcb, dstd) in ((a_, S["D_Bh"]), (k_, S["D_Kh"])):
                        k.tt("dve", t1.v[:, 0:n], srcb.v[:, 0:n], e_.v[:, 0:n], ALU.mult)
                        for s in range(ns):
                            pt = nps()
                            k.tr(pt.v[:, 0:128], t1.v[:, s * 128:(s + 1) * 128], identf.v[:, :])
                            k.cp("act", tb.v[:, 0:128], pt.v[:, 0:128])
                            k.stv("sp", dstd[d, t0 + s * 128:t0 + (s + 1) * 128, g * 128:(g + 1) * 128], tb.v[:, 0:128])
                for s in range(ns):
                    k.stv("sp", S["D_BS"][d, t0 + s * 128:t0 + (s + 1) * 128, :], bs4.v[:, s, :])


def attnphase(k, st, cfg):
    S, C, Wt = cfg["S"], cfg["C"], cfg["W"]
    T, CTX = cfg["T"], cfg["CTX"]
    TT = T + CTX
    NKC = TT // 128
    need_ctx = cfg["need_ctx"]
    kT = k.sb(st, [128, TT], BF16, "kT")
    va = k.sb(st, [128, NKC, 65], BF16, "va")
    qT = [k.sb(st, [128, 512], BF16, "qT%d" % i) for i in range(2)]
    for c0 in range(0, TT, 2048):
        k.memset("dve", kT.v[64:128, c0:min(TT, c0 + 2048)], 0.0)
    for q_ in qT:
        k.memset("dve", q_.v[64:128, :], 0.0)
    pT = [k.sb(st, [128, 512], BF16, "pT%d" % i) for i in range(3)]
    ot = [k.sb(st, [128, 64], F32, "ot%d" % i) for i in range(2)]
    rc = k.sb(st, [128, 1], F32, "rc")
    esk = k.sb(st, [128, 4], F32, "esk")
    mprev = k.sb(st, [128, 128], BF16, "mprev")
    mnext = k.sb(st, [128, 128], BF16, "mnext")
    k.ld("pool", mprev.v[:, :], C["mprev"])
    k.ld("pool", mnext.v[:, :], C["mnext"])
    k.ld("sp", esk.v[:, :], Wt["sink"].partition_broadcast(128))
    k.act(esk.v[:, :], esk.v[:, :], ACT.Exp)
    ps_s = [k.ps(st, [128, 512], F32, "pss%d" % i) for i in range(3)]
    ps_o = [k.ps(st, [128, 512], F32, "pso%d" % i) for i in range(4)]
    cnt = dict(s=0, q=0, o=0, p=0)
    scale = 64 ** -0.5

    def load_kv(kd, vd, g):
        k.ld("sp", kT.v[0:64, :], kd[g * 64:(g + 1) * 64, :])
        k.memset("dve", va.v[:, :, 64:65], 1.0)
        k.ld("sp", va.v[:, :, 0:64], vd[:, g * 64:(g + 1) * 64].rearrange("(c p) j -> p c j", p=128))

    def finish(acc, nq_sub, t0, col, sinkcol):
        for s in range(nq_sub):
            o = ot[cnt["o"] % 2]
            cnt["o"] += 1
            if sinkcol is None:
                k.recip(rc.v[:, :], acc[s].v[:, 64:65])
            else:
                k.tt("dve", rc.v[:, :], acc[s].v[:, 64:65], esk.v[:, sinkcol:sinkcol + 1], ALU.add)
                k.recip(rc.v[:, :], rc.v[:, :])
            k.ts("dve", o.v[:, :], acc[s].v[:, 0:64], rc.v[:, 0:1], None, ALU.mult)
            k.stv("sp", S["O"][t0 + s * 128:t0 + (s + 1) * 128, col:col + 64], o.v[:, :])

    def block(qd, h, t0, nq, chunks, col, sinkcol):
        q = qT[cnt["q"] % 2]
        cnt["q"] += 1
        k.ld("sp", q.v[0:64, 0:nq], qd[h * 64:(h + 1) * 64, t0:t0 + nq])
        nsub = nq // 128
        first, last = {}, {}
        for i, (kc, subs) in enumerate(chunks):
            for (s, _) in subs:
                first.setdefault(s, i)
                last[s] = i

        def qk(i):
            ps = ps_s[cnt["s"] % 3]
            cnt["s"] += 1
            kc = chunks[i][0]
            k.mm(ps.v[:, 0:nq], kT.v[:, kc * 128:(kc + 1) * 128], q.v[:, 0:nq])
            return ps

        nxt = qk(0)
        for i, (kc, subs) in enumerate(chunks):
            ps = nxt
            if i + 1 < len(chunks):
                nxt = qk(i + 1)
            p = pT[cnt["p"] % 3]
            cnt["p"] += 1
            k.act(p.v[:, 0:nq], ps.v[:, 0:nq], ACT.Exp, scale=scale)
            for (s, m_) in subs:
                if m_ is not None:
                    k.tt("dve", p.v[:, s * 128:(s + 1) * 128], p.v[:, s * 128:(s + 1) * 128], m_.v[:, :], ALU.mult)
            for (s, m_) in subs:
                k.mm(ps_o[s].v[:, 0:65], p.v[:, s * 128:(s + 1) * 128], va.v[:, kc, :], i == first[s], i == last[s])
        finish(ps_o, nsub, t0, col, sinkcol)

    for g in range(2):
        load_kv(S["KB"], S["VB"], g)
        for h in (2 * g, 2 * g + 1):
            for t0 in range(0, T, 512):
                nq = min(512, T - t0)
                block(S["QB"], h, t0, nq, [(kc, [(s, None) for s in range(nq // 128)]) for kc in range(NKC)],
                      256 + h * 64, None)
            if need_ctx:
                block(S["QB"], h, T, CTX, [(kc, [(s, None) for s in range(CTX // 128)]) for kc in range(T // 128, NKC)],
                      256 + h * 64, None)
    nb = T // 128
    cchunks = list(range(T // 128, NKC))
    for g in range(2):
        load_kv(S["KA"], S["VA"], g)
        for h in (2 * g, 2 * g + 1):
            for b0 in range(0, nb, 4):
                nsb = min(4, nb - b0)
                chunks = []
                for kc in range(max(0, b0 - 1), min(nb, b0 + nsb + 1)):
                    subs = []
                    for s in range(nsb):
                        dlt = kc - (b0 + s)
                        if dlt == -1:
                            subs.append((s, mprev))
                        elif dlt == 0:
                            subs.append((s, None))
                        elif dlt == 1:
                            subs.append((s, mnext))
                    chunks.append((kc, subs))
                for kc in cchunks:
                    chunks.append((kc, [(s, None) for s in range(nsb)]))
                block(S["QA"], h, b0 * 128, nsb * 128, chunks, h * 64, h)
            if need_ctx:
                block(S["QA"], h, T, CTX, [(kc, [(s, None) for s in range(CTX // 128)]) for kc in cchunks], h * 64, h)


def scanphase(k, st, cfg):
    S, C = cfg["S"], cfg["C"]
    T, CTX = cfg["T"], cfg["CTX"]
    TT = T + CTX
    dplr = cfg["dplr"]
    pre = "D_" if dplr else "C_"
    NCG = 2
    GT_ = NCG * CH
    NCHT = TT // CH
    f3 = lambda b: b.v[:, :, :]
    t64 = lambda nm: k.sb(st, [64, 8, 64], F32, nm)
    chm = lambda nm: k.sb(st, [64, 8, GT_], F32, nm)
    tkm = lambda nm: k.sb(st, [64, NCG, 8, 64], F32, nm)
    RT = [chm("RT%d" % i) for i in range(2)]
    KT = [chm("KT%d" % i) for i in range(2)]
    KH = [tkm("KH%d" % i) for i in range(2)]
    VV = [tkm("VV%d" % i) for i in range(2)]
    Min = t64("Min")
    k.ld("sp", f3(Min), C["m_in"].rearrange("p (q t) -> p q t", q=8))
    id8 = t64("id8")
    k.ld("sp", f3(id8), C["id8"].rearrange("p (q t) -> p q t", q=8))
    if dplr:
        AT = [chm("AT%d" % i) for i in range(2)]
        BT = [chm("BT%d" % i) for i in range(2)]
        BH = [tkm("BH%d" % i) for i in range(2)]
        AK = [tkm("AK%d" % i) for i in range(2)]
        Mst, Mts = t64("Mst"), t64("Mts")
        k.ld("sp", f3(Mst), C["m_st"].rearrange("p (q t) -> p q t", q=8))
        k.ld("sp", f3(Mts), C["m_ts"].rearrange("p (q t) -> p q t", q=8))
        PCt = k.sb(st, [64, 8, NCHT], F32, "PCt")
        for d in range(2):
            k.ld("sp", PCt.v[:, d * 4:(d + 1) * 4, :], S["D_PC"][d].rearrange("(h j) c -> j h c", j=64))
    else:
        PCr = k.sb(st, [64, 8], F32, "PCr")
        k.ld("sp", PCr.v[:, :], C["ret_pc"])
        GTc = t64("GTc")
        k.tt("dve", f3(GTc), f3(id8), PCr.v[:, :].rr("p (q o) -> p q o", o=1).bc([64, 8, 64]), ALU.mult)
    sets = []
    for c in range(NCG):
        B = dict(ArkT=t64("ArkT%d" % c), Yv=t64("Yv%d" % c), H=t64("H%d" % c))
        if dplr:
            B.update(Ls=[t64("L%d_%d" % (i, c)) for i in range(5)], Ns=[t64("N%d_%d" % (i, c)) for i in range(6)],
                     AakT=t64("AakT%d" % c), ArbT=t64("ArbT%d" % c), Qe=t64("Qe%d" % c), GTt=t64("GTt%d" % c),
                     X=[k.sb(st, [64, 8, 128], F32, "X%d_%d" % (i, c)) for i in range(2)],
                     psX=k.ps(st, [64, 1024], F32, "psX%d" % c))
        sets.append(B)
    ST = [t64("ST%d" % i) for i in range(2)]
    Yt = [t64("Yt%d" % i) for i in range(2)]
    psA = [k.ps(st, [64, 512], F32, "psA%d" % i) for i in range(2)]
    psY = [k.ps(st, [64, 512], F32, "psY%d" % i) for i in range(2)]
    cnt = dict(a=0, y=0, e=0)
    p3 = lambda p: p.v[:, :].rr("p (q t) -> p q t", q=8)

    def npa():
        p = psA[cnt["a"] % 2]
        cnt["a"] += 1
        return p

    def ev():
        cnt["e"] += 1
        return "act" if cnt["e"] % 2 else "dve"

    def prod(dst, lhs, rhs, mask):
        p = npa()
        for q in range(8):
            k.mm(p3(p)[:, q, :], lhs(q), rhs(q))
        if mask is None:
            k.cp(ev(), f3(dst), p3(p))
        else:
            k.tt("dve", f3(dst), p3(p), f3(mask), ALU.mult)

    def views(rb, cl):
        sl = {d: slice(cl[d] * CH, (cl[d] + 1) * CH) for d in range(2)}
        dq = lambda q: q // 4
        chs = lambda buf: (lambda q: buf[rb].v[:, q, sl[dq(q)]])
        tks = lambda buf: (lambda q: buf[rb].v[:, cl[dq(q)], q, :])
        return sl, chs, tks

    def ppart(B, rb, cl, grp):
        sl, chs, tks = views(rb, cl)
        rt, kt, kh, vv = chs(RT), chs(KT), tks(KH), tks(VV)
        prod(B["ArkT"], kt, rt, Min)
        yield
        if dplr:
            at, bt, bh = chs(AT), chs(BT), tks(BH)
            Ls, Ns, AakT, ArbT = B["Ls"], B["Ns"], B["AakT"], B["ArbT"]
            prod(Ns[0], bt, at, Mst)
            prod(Ls[0], at, bt, Mts)
            yield
            prod(AakT, kt, at, Mst)
            prod(ArbT, bt, rt, Min)
            yield
            xc, xn = B["X"]
            for d in range(2):
                k.cp("act", xc.v[:, d * 4:(d + 1) * 4, 0:64], AK[rb].v[:, cl[d], d * 4:(d + 1) * 4, :])
            p = npa()
            for q in range(8):
                k.mm(p3(p)[:, q, :], AakT.v[:, q, :], vv(q))
            k.cp("dve", xc.v[:, :, 64:128], p3(p))
            yield
            px = B["psX"].v[:, :].rr("p (q c) -> p q c", q=8)
            for lv in range(6):
                for q in range(8):
                    k.mm(px[:, q, :], Ns[lv].v[:, q, :], xc.v[:, q, :])
                if lv < 5:
                    if lv < 4:
                        prod(Ls[lv + 1], lambda q: Ns[lv].v[:, q, :], lambda q: Ls[lv].v[:, q, :], None)
                    prod(Ns[lv + 1], lambda q: Ls[lv].v[:, q, :], lambda q: Ns[lv].v[:, q, :], None)
                k.tt("dve", f3(xn), f3(xc), px, ALU.add)
                xc, xn = xn, xc
                yield
            wq = lambda q: xc.v[:, q, 0:64]
            uv = lambda q: xc.v[:, q, 64:128]
            p = npa()
            for q in range(8):
                k.mm(p3(p)[:, q, :], wq(q), ArbT.v[:, q, :])
            for d in range(2):
                k.tt("dve", B["Qe"].v[:, d * 4:(d + 1) * 4, :], p3(p)[:, d * 4:(d + 1) * 4, :],
                     RT[rb].v[:, d * 4:(d + 1) * 4, sl[d]], ALU.add)
            p = npa()
            for q in range(8):
                k.mm(p3(p)[:, q, :], wq(q), bh(q))
            for d in range(2):
                c_abs = grp[d] * NCG + cl[d]
                k.tt("dve", B["GTt"].v[:, d * 4:(d + 1) * 4, :], id8.v[:, d * 4:(d + 1) * 4, :],
                     PCt.v[:, d * 4:(d + 1) * 4, c_abs:c_abs + 1].bc([64, 4, 64]), ALU.mult)
            k.tt("dve", f3(B["GTt"]), f3(B["GTt"]), p3(p), ALU.add)
            yield
        p = npa()
        for q in range(8):
            k.mm(p3(p)[:, q, :], B["ArkT"].v[:, q, :], vv(q), True, not dplr)
            if dplr:
                k.mm(p3(p)[:, q, :], ArbT.v[:, q, :], uv(q), False, True)
        k.cp(ev(), f3(B["Yv"]), p3(p))
        p = npa()
        for q in range(8):
            k.mm(p3(p)[:, q, :], kh(q), vv(q), True, not dplr)
            if dplr:
                k.mm(p3(p)[:, q, :], bh(q), uv(q), False, True)
        k.cp(ev(), f3(B["H"]), p3(p))
        yield

    k.memset("dve", f3(ST[0]), 0.0)
    ngl = T // GT_
    ncg_ctx = CTX // GT_
    fw = list(range(ngl, ngl + ncg_ctx)) + list(range(ngl))
    bw = list(range(ngl + ncg_ctx - 1, ngl - 1, -1)) + list(range(ngl - 1, -1, -1))
    order = {0: fw, 1: bw}
    step = 0
    for gi in range(ngl + ncg_ctx):
        rb = gi % 2
        grp = {d: order[d][gi] for d in range(2)}
        for d in range(2):
            t0 = grp[d] * GT_
            qs = slice(d * 4, (d + 1) * 4)
            chv = lambda nm: S[pre + nm][d, :, t0:t0 + GT_].rearrange("(h j) t -> j h t", j=64)
            tkv = lambda ap: ap[t0:t0 + GT_, :].rearrange("(n t) (h j) -> t n h j", t=CH, j=64)
            k.ld("sp", RT[rb].v[:, qs, :], chv("Rt"))
            k.ld("sp", KT[rb].v[:, qs, :], chv("Kt"))
            k.ld("sp", KH[rb].v[:, :, qs, :], tkv(S[pre + "Kh"][d]))
            k.ld("sp", VV[rb].v[:, :, qs, :], tkv(S["D_V"][d] if dplr else S["C_V"]))
            if dplr:
                k.ld("sp", AT[rb].v[:, qs, :], chv("At"))
                k.ld("sp", BT[rb].v[:, qs, :], chv("Bt"))
                k.ld("sp", BH[rb].v[:, :, qs, :], tkv(S["D_Bh"][d]))
                k.ld("sp", AK[rb].v[:, :, qs, :], tkv(S["D_Atok"][d]))
        cls = [{0: ci, 1: NCG - 1 - ci} for ci in range(NCG)]
        gens = [ppart(sets[ci], rb, cls[ci], grp) for ci in range(NCG)]
        live = list(gens)
        while live:
            for g_ in list(live):
                try:
                    next(g_)
                except StopIteration:
                    live.remove(g_)
        for ci in range(NCG):
            B, cl = sets[ci], cls[ci]
            sl, chs, tks = views(rb, cl)
            Sc, Sn = ST[step % 2], ST[(step + 1) % 2]
            qe = (lambda q: B["Qe"].v[:, q, :]) if dplr else chs(RT)
            gt = (lambda q: B["GTt"].v[:, q, :]) if dplr else (lambda q: GTc.v[:, q, :])
            py = psY[cnt["y"] % 2]
            cnt["y"] += 1
            for q in range(8):
                k.mm(p3(py)[:, q, :], qe(q), Sc.v[:, q, :])
            p = npa()
            for q in range(8):
                k.mm(p3(p)[:, q, :], gt(q), Sc.v[:, q, :])
            k.tt("dve", f3(Sn), p3(p), f3(B["H"]), ALU.add)
            yt = Yt[step % 2]
            k.tt("dve", f3(yt), p3(py), f3(B["Yv"]), ALU.add)
            for d in range(2):
                tt0 = grp[d] * GT_ + cl[d] * CH
                k.stv("sp", S[pre + "Y"][d, tt0:tt0 + CH, :].rearrange("t (h i) -> t h i", i=64),
                      yt.v[:, d * 4:(d + 1) * 4, :])
            step += 1


GN_EPS = 64e-5


def postphase(k, st, cfg):
    S, Wt = cfg["S"], cfg["W"]
    TT = cfg["T"] + cfg["CTX"]
    row = lambda nm, ap: (lambda t: (k.ld("sp", t.v[:, :], ap.partition_broadcast(128)), t)[1])(k.sb(st, [128, 256], F32, nm))
    retg = row("retg", Wt["ret_g"])
    lng = row("lng", Wt["ln_g"])
    lnb = row("lnb", Wt["ln_b"])
    y = [k.sb(st, [128, 4, 64], F32, "y%d" % i) for i in range(2)]
    yc = k.sb(st, [128, 4, 64], F32, "yc")
    sq = k.sb(st, [128, 4, 64], F32, "sq")
    m4 = k.sb(st, [128, 4], F32, "m4")
    v4 = k.sb(st, [128, 4], F32, "v4")
    gc = k.sb(st, [128, 512], F32, "gc")
    gd = k.sb(st, [128, 256], F32, "gd")
    vd = k.sb(st, [128, 4, 64], F32, "vd")
    bs = k.sb(st, [128, 4], F32, "bs")
    acc = k.sb(st, [128, 256], F32, "acc")
    yn = k.sb(st, [128, 256], F32, "yn")
    i = 0

    def head_norm(yb, g, b, dst):
        k.op("dve", lambda e: e.tensor_reduce(out=m4[:, :], in_=yb[:, :, :], axis=AX.X, op=ALU.add), r=[yb], w=[m4])
        k.ts("dve", m4.v[:, :], m4.v[:, :], 1.0 / 64, None, ALU.mult)
        k.tt("dve", yc.v[:, :, :], yb.v[:, :, :], m4.v[:, :].rr("p (h o) -> p h o", o=1).bc([128, 4, 64]), ALU.subtract)
        k.tt("dve", sq.v[:, :, :], yc.v[:, :, :], yc.v[:, :, :], ALU.mult)
        k.op("dve", lambda e: e.tensor_reduce(out=v4[:, :], in_=sq[:, :, :], axis=AX.X, op=ALU.add), r=[sq], w=[v4])
        k.ts("dve", v4.v[:, :], v4.v[:, :], 1.0 / 64, GN_EPS, ALU.mult, ALU.add)
        k.act(v4.v[:, :], v4.v[:, :], ACT.Sqrt)
        k.recip(v4.v[:, :], v4.v[:, :])
        k.tt("dve", yc.v[:, :, :], yc.v[:, :, :], v4.v[:, :].rr("p (h o) -> p h o", o=1).bc([128, 4, 64]), ALU.mult)
        k.tt("dve", dst, yc.v[:, :, :].rr("p h i -> p (h i)"), g.v[:, :], ALU.mult)
        if b is not None:
            k.tt("dve", dst, dst, b.v[:, :], ALU.add)

    for t0 in range(0, TT, 128):
        tsl = slice(t0, t0 + 128)
        k.ld("sp", gc.v[:, :], S["C_G"][tsl, :])
        for d in range(2):
            yb = y[i % 2]
            i += 1
            k.ld("sp", yb.v[:, :, :], S["C_Y"][d, tsl, :].rearrange("t (h i) -> t h i", i=64))
            head_norm(yb, retg, None, yn.v[:, :])
            if d == 0:
                k.tt("dve", acc.v[:, :], yn.v[:, :], gc.v[:, 0:256], ALU.mult)
            else:
                k.tt("dve", yn.v[:, :], yn.v[:, :], gc.v[:, 256:512], ALU.mult)
                k.tt("dve", acc.v[:, :], acc.v[:, :], yn.v[:, :], ALU.add)
        k.stv("sp", S["O"][tsl, 512:768], acc.v[:, :])
        k.ld("sp", gd.v[:, :], S["D_G"][tsl, :])
        for d in range(2):
            yb = y[i % 2]
            i += 1
            k.ld("sp", yb.v[:, :, :], S["D_Y"][d, tsl, :].rearrange("t (h i) -> t h i", i=64))
            k.ld("sp", vd.v[:, :, :], S["D_V"][d, tsl, :].rearrange("t (h i) -> t h i", i=64))
            k.ld("sp", bs.v[:, :], S["D_BS"][d, tsl, :])
            head_norm(yb, lng, lnb, yn.v[:, :])
            k.tt("dve", vd.v[:, :, :], vd.v[:, :, :], bs.v[:, :].rr("p (h o) -> p h o", o=1).bc([128, 4, 64]), ALU.mult)
            k.tt("dve", yn.v[:, :], yn.v[:, :], vd.v[:, :, :].rr("p h i -> p (h i)"), ALU.add)
            if d == 0:
                k.cp("act", acc.v[:, :], yn.v[:, :])
            else:
                k.tt("dve", acc.v[:, :], acc.v[:, :], yn.v[:, :], ALU.add)
        k.tt("dve", acc.v[:, :], acc.v[:, :], gd.v[:, :], ALU.mult)
        k.stv("sp", S["O"][tsl, 768:1024], acc.v[:, :])


def modphase(k, st, cfg):
    cv = k.sb(st, [128, 2, 8], F32, "cv")
    for r in range(2):
        k.ld("sp", cv.v[:, r, :], cfg["cvec"][r].rearrange("(c p) -> p c", p=128), allow_slow_non_contiguous=True)
    sc = k.sb(st, [128, 8, 2], F32, "sc")
    k.act(sc.v[:, :, :].rr("p c r -> p r c"), cv.v[:, :, :], ACT.Silu)
    wm = [k.sb(st, [128, 8, 512], F32, "wm%d" % i) for i in range(2)]
    bm = k.sb(st, [2, 9 * D], F32, "bm")
    k.ld("sp", bm.v[:, :], cfg["b_mod"].partition_broadcast(2))
    ob = k.sb(st, [2, 9 * D], F32, "ob")
    pm = [k.ps(st, [128, 512], F32, "pm%d" % i) for i in range(2)]
    wv = cfg["w_mod"].rearrange("(c p) f -> p c f", p=128)
    for j in range(18):
        w = wm[j % 2]
        for c in range(8):
            k.ld("sp", w.v[:, c, :], wv[:, c, j * 512:(j + 1) * 512])
        p = pm[j % 2]
        for c in range(8):
            k.mm(p.v[0:2, :], sc.v[:, c, :], w.v[:, c, :], c == 0, c == 7)
        k.tt("dve", ob.v[:, j * 512:(j + 1) * 512], p.v[0:2, :], bm.v[:, j * 512:(j + 1) * 512], ALU.add)
    k.stv("sp", cfg["mod"], ob.v[:, :])


def build_program(T, CTX, L, stop_after=None):
    TT = T + CTX
    nc = bass.Bass("TRN2", target_bir_lowering=False, dynamic_dma_scratch_size=8192)

    def din(name, shape, dt=F32):
        return nc.dram_tensor(name, list(shape), dt, kind="ExternalInput").ap()

    def scr(name, shape, dt=F32):
        return nc.dram_tensor(name, list(shape), dt, kind="Internal").ap()

    I = dict(
        xin=din("xin", [TT, D]), cvec=din("cvec", [2, D]),
        w_mod=din("w_mod", [L, D, 9 * D]), b_mod=din("b_mod", [L, 9 * D]), norm_g=din("norm_g", [L, 3, D]),
        ffn_w_in=din("ffn_w_in", [L, 2, D, 2 * DFF]), ffn_w_out=din("ffn_w_out", [L, 2, DFF, D]),
        w_in=din("w_in", [L, D, 3456]), w_out=din("w_out", [L, D, D]), final_g=din("final_g", [D]),
        w_sw=din("w_sw", [L, D, 1280]), w_d=din("w_d", [L, 2, D, 896]), pcols=din("pcols", [L, 128, 24]),
        mu=din("mu", [L, 2, 896]), w2=din("w2", [L, 2, 64, 256]), a2=din("a2", [L, 2, 64, 256]),
        g2=din("g2", [L, 128, 256]), ret_g=din("ret_g", [L, 256]), ln_g=din("ln_g", [L, 256]),
        ln_b=din("ln_b", [L, 256]), sink=din("sink", [L, 4]),
    )
    Cn = dict(
        ident=din("ident", [128, 128]), bd64=din("bd64", [128, 128]), e2=din("e2", [128, 2]),
        ropeA_c=din("ropeA_c", [128, T]), ropeA_s=din("ropeA_s", [128, T]),
        ropeS_c=din("ropeS_c", [128, T]), ropeS_s=din("ropeS_s", [128, T]),
        ret_tab=din("ret_tab", [128, 2, 2, 3, CH]), ret_pc=din("ret_pc", [64, 8]),
        m_st=din("m_st", [64, 512]), m_ts=din("m_ts", [64, 512]), m_in=din("m_in", [64, 512]),
        id8=din("id8", [64, 512]), mprev=din("mprev", [128, 128]), mnext=din("mnext", [128, 128]),
    )
    out = nc.dram_tensor("out", [T, D], F32, kind="ExternalOutput").ap()
    dbg = stop_after is not None
    mk = (lambda name, shape, dt=F32: nc.dram_tensor(name, list(shape), dt, kind="ExternalOutput").ap()) if dbg else scr
    S = dict(
        xres=mk("xres", [TT, D]), mod=mk("modr", [2, 9 * D]), hT=mk("hT", [D, TT], BF16),
        QA=mk("QA", [256, TT], BF16), KA=mk("KA", [128, TT], BF16), VA=mk("VA", [TT, 128], BF16),
        QB=mk("QB", [256, TT], BF16), KB=mk("KB", [128, TT], BF16), VB=mk("VB", [TT, 128], BF16),
        O=mk("O", [TT, 1024]),
        C_Rt=mk("C_Rt", [2, 256, TT]), C_Kt=mk("C_Kt", [2, 256, TT]), C_Kh=mk("C_Kh", [2, TT, 256]),
        C_V=mk("C_V", [TT, 256]), C_G=mk("C_G", [TT, 512]), C_Y=mk("C_Y", [2, TT, 256]),
        D_Rt=mk("D_Rt", [2, 256, TT]), D_Kt=mk("D_Kt", [2, 256, TT]), D_At=mk("D_At", [2, 256, TT]),
        D_Bt=mk("D_Bt", [2, 256, TT]), D_Kh=mk("D_Kh", [2, TT, 256]), D_Bh=mk("D_Bh", [2, TT, 256]),
        D_Atok=mk("D_Atok", [2, TT, 256]), D_V=mk("D_V", [2, TT, 256]), D_Y=mk("D_Y", [2, TT, 256]),
        D_PC=mk("D_PC", [2, 256, TT // CH]), D_G=mk("D_G", [TT, 256]), D_BS=mk("D_BS", [2, TT, 4]),
    )
    k = K(nc)
    stages = []

    def phase(name, fn, cfg):
        if stop_after is not None and stop_after in stages:
            return
        with contextlib.ExitStack() as st:
            fn(k, st, cfg)
            k.barrier()
        stages.append(name)
        k.marks.append((name, {e.name: e.n for e in k.E.values()}))

    with k.stack:
        segs_all = [(0, T, 0), (T, CTX, 1)]
        for l in range(L):
            last = l == L - 1
            W = dict(w_in=I["w_in"][l], w_sw=I["w_sw"][l], w_d=I["w_d"][l], mu=I["mu"][l], pcols=I["pcols"][l],
                     w2=I["w2"][l], a2=I["a2"][l], g2=I["g2"][l], ret_g=I["ret_g"][l], ln_g=I["ln_g"][l],
                     ln_b=I["ln_b"][l], sink=I["sink"][l])
            base = dict(S=S, C=Cn, W=W, T=T, CTX=CTX)
            phase("mod%d" % l, modphase, dict(cvec=I["cvec"], w_mod=I["w_mod"][l], b_mod=I["b_mod"][l], mod=S["mod"]))
            phase("f1_%d" % l, rowphase, dict(
                xin=I["xin"] if l == 0 else S["xres"], xout=S["xres"], segs=segs_all, mod=S["mod"], ident=Cn["ident"],
                ffn=dict(w_in=I["ffn_w_in"][l, 0], w_out=I["ffn_w_out"][l, 0], g=I["norm_g"][l, 0], shift=0, scale=1,
                         gate=2),
                post=dict(kind="hT", g=I["norm_g"][l, 1], shift=3, scale=4, hT=S["hT"])))
            phase("proj%d" % l, projphase, base)
            phase("attn%d" % l, attnphase, dict(base, need_ctx=not last))
            phase("scanC%d" % l, scanphase, dict(base, dplr=False))
            phase("scanD%d" % l, scanphase, dict(base, dplr=True))
            phase("post%d" % l, postphase, base)
            phase("f2_%d" % l, rowphase, dict(
                xin=S["xres"], xout=S["xres"], segs=segs_all if not last else [(0, T, 0)], mod=S["mod"],
                ident=Cn["ident"],
                pre=dict(o=S["O"], w_out=I["w_out"][l], gate=5),
                ffn=dict(w_in=I["ffn_w_in"][l, 1], w_out=I["ffn_w_out"][l, 1], g=I["norm_g"][l, 2], shift=6, scale=7,
                         gate=8),
                post=dict(kind="final", g=I["final_g"], out=out) if last else None))
    return nc, k


def _consts(T):
    c = {}
    c["ident"] = np.eye(128, dtype=np.float32)
    bd = np.zeros((128, 128), np.float32)
    bd[:64, :64] = 1
    bd[64:, 64:] = 1
    c["bd64"] = bd
    e2 = np.zeros((128, 2), np.float32)
    e2[:64, 0] = 1
    e2[64:, 1] = 1
    c["e2"] = e2
    t = np.arange(T)
    row = (t // 64).astype(np.float32)
    col = (t % 64).astype(np.float32)
    inv16 = (1.0 / (np.float32(10000.0) ** (np.arange(0, 32, 2, dtype=np.float32) / np.float32(32)))).astype(np.float32)
    inv32 = (1.0 / (np.float32(10000.0) ** (np.arange(0, 64, 2, dtype=np.float32) / np.float32(64)))).astype(np.float32)
    ca = np.zeros((64, T), np.float32)
    sa = np.zeros((64, T), np.float32)
    for blk, pos in ((0, row), (32, col)):
        ang = pos[None, :] * inv16[:, None]
        ca[blk:blk + 16] = np.cos(ang)
        ca[blk + 16:blk + 32] = np.cos(ang)
        sa[blk:blk + 16] = -np.sin(ang)
        sa[blk + 16:blk + 32] = np.sin(ang)
    ang = t.astype(np.float32)[None, :] * inv32[:, None]
    cs = np.concatenate([np.cos(ang), np.cos(ang)], 0).astype(np.float32)
    ss = np.concatenate([-np.sin(ang), np.sin(ang)], 0).astype(np.float32)
    c["ropeA_c"] = np.concatenate([ca, ca], 0)
    c["ropeA_s"] = np.concatenate([sa, sa], 0)
    c["ropeS_c"] = np.concatenate([cs, cs], 0)
    c["ropeS_s"] = np.concatenate([ss, ss], 0)
    lg = np.log1p(-np.exp2(-5.0 - np.arange(4, dtype=np.float64)))
    i = np.arange(CH, dtype=np.float64)
    rt = np.zeros((128, 2, 2, 3, CH), np.float64)
    for p in range(128):
        for g in range(2):
            l_ = lg[2 * g + p // 64]
            for d in range(2):
                csum = (i + 1) * l_ if d == 0 else (CH - i) * l_
                rt[p, g, d, 0] = np.exp(csum)
                rt[p, g, d, 1] = np.exp(-csum) / 8.0
                rt[p, g, d, 2] = np.exp(CH * l_ - csum) / 8.0
    c["ret_tab"] = rt.astype(np.float32)
    c["ret_pc"] = np.tile(np.exp(CH * lg)[None, :], (64, 2)).astype(np.float32)
    s_ = np.arange(64)[:, None]
    t_ = np.arange(64)[None, :]
    lt = (s_ < t_).astype(np.float32)
    gt = (s_ > t_).astype(np.float32)
    eye = np.eye(64, dtype=np.float32)
    c["m_st"] = np.concatenate([lt] * 4 + [gt] * 4, 1)
    c["m_ts"] = np.concatenate([gt] * 4 + [lt] * 4, 1)
    c["m_in"] = np.concatenate([lt + eye] * 4 + [gt + eye] * 4, 1)
    c["id8"] = np.concatenate([eye] * 8, 1)
    j = np.arange(128)[:, None]
    ii = np.arange(128)[None, :]
    c["mprev"] = (ii <= j).astype(np.float32)
    c["mnext"] = (j <= ii).astype(np.float32)
    return c


def _host_maps(inp, T, CTX, L):
    f = lambda a: np.ascontiguousarray(np.asarray(a, dtype=np.float32))
    w_in = f(inp["w_in"])
    pa = np.concatenate([np.arange(16, 32), np.arange(0, 16), np.arange(48, 64), np.arange(32, 48)])
    psq = np.concatenate([np.arange(32, 64), np.arange(0, 32)])
    cols = []
    for base, nh, perm in ((0, 4, pa), (256, 2, pa), (512, 4, pa), (768, 2, pa), (1024, 4, psq), (1280, 4, psq)):
        for h in range(nh):
            cols.append(base + h * 64 + perm)
    cols = np.concatenate(cols)
    w_sw = np.ascontiguousarray(w_in[:, :, cols])
    dcols = [np.concatenate([np.arange(2304, 3072), np.arange(3200, 3264), np.arange(3328, 3392)]),
             np.concatenate([np.arange(2304, 3072), np.arange(3264, 3328), np.arange(3392, 3456)])]
    w_d = np.ascontiguousarray(np.stack([w_in[:, :, dc] for dc in dcols], 1))
    qkg = f(inp["qk_norm_g"])
    pcols = np.zeros((L, 128, 24), np.float32)
    two = lambda v: np.ascontiguousarray(v.reshape(2, 128).T)
    for l in range(L):
        for qk in range(2):
            g = qkg[l, qk]
            pcols[l, :, 2 * qk] = np.tile(g, 2)
            pcols[l, :, 2 * qk + 1] = np.tile(g[pa], 2)
        for d in range(2):
            pcols[l, :, 4 + 2 * d:6 + 2 * d] = two(f(inp["rwkv_w0"])[l, d])
            pcols[l, :, 8 + 2 * d:10 + 2 * d] = two(f(inp["rwkv_a0"])[l, d])
            pcols[l, :, 16 + 2 * d:18 + 2 * d] = two(f(inp["rwkv_rho"])[l, d].reshape(256))
        pcols[l, :, 12:14] = two(f(inp["rwkv_k_k"])[l])
        pcols[l, :, 14:16] = two(f(inp["rwkv_k_a"])[l])
    shared = dict(
        w_mod=f(inp["w_mod"]), b_mod=f(inp["b_mod"]), norm_g=f(inp["norm_g"]), ffn_w_in=f(inp["ffn_w_in"]),
        ffn_w_out=f(inp["ffn_w_out"]), w_in=w_in, w_out=f(inp["w_out"]), final_g=f(inp["final_norm_g"]),
        w_sw=w_sw, w_d=w_d, pcols=pcols, mu=f(inp["rwkv_mu"]), w2=f(inp["rwkv_w2"]), a2=f(inp["rwkv_a2"]),
        g2=f(inp["rwkv_g2"]), ret_g=f(inp["ret_norm_g"]), ln_g=f(inp["rwkv_ln_g"]), ln_b=f(inp["rwkv_ln_b"]),
        sink=f(inp["attn_sink"]))
    shared.update(_consts(T))
    x, c, ctx, c_ctx = f(inp["x"]), f(inp["c"]), f(inp["ctx"]), f(inp["c_ctx"])
    maps = []
    for b in range(x.shape[0]):
        m = dict(shared)
        m["xin"] = np.ascontiguousarray(np.concatenate([x[b], ctx[b]], 0))
        m["cvec"] = np.ascontiguousarray(np.stack([c[b], c_ctx], 0))
        maps.append(m)
    return maps


_PROG = {}


def kernel(**inputs):
    x = np.asarray(inputs["x"])
    B, T, _ = x.shape
    CTX = np.asarray(inputs["ctx"]).shape[1]
    L = np.asarray(inputs["w_in"]).shape[0]
    key = (T, CTX, L)
    if key not in _PROG:
        _PROG[key] = build_program(T, CTX, L)[0]
    nc = _PROG[key]
    maps = _host_maps(inputs, T, CTX, L)
    res = run_bass_kernel_spmd(nc, maps, core_ids=list(range(B)))
    return np.stack([np.asarray(r["out"], dtype=np.float32) for r in res.results], 0)
```

```python
import contextlib
import numpy as np
import concourse.bass as bass
import concourse.mybir as mybir
from concourse.bass_utils import run_bass_kernel_spmd

ACT = mybir.ActivationFunctionType
ALU = mybir.AluOpType
AX = mybir.AxisListType
F32 = mybir.dt.float32
BF16 = mybir.dt.bfloat16

D = 1024
DFF = 2816
NMOD = 9
RMS_EPS = 1e-6
STRICT = True


class PSem:
    def __init__(self, h):
        self.h = h
        self.val = 0


class Eng:
    def __init__(self, name, h, sem):
        self.name, self.h, self.sem = name, h, sem
        self.n = 0
        self.seen = {}
        self.dseen = {}


class Buf:
    def __init__(self, k, t, name):
        self.k, self.t, self.name = k, t, name
        self.w = None
        self.r = {}
        self.psem = None
        self.dma_w = 0
        self.dma_rw = 0

    def __getitem__(self, key):
        return self.t[key]

    @property
    def v(self):
        return _VA(self)


class View:
    def __init__(self, buf, ap):
        self.buf, self.ap = buf, ap

    def rr(self, pat, **kw):
        return View(self.buf, self.ap.rearrange(pat, **kw))

    def bc(self, shape):
        return View(self.buf, self.ap.to_broadcast(list(shape)))

    def __getitem__(self, key):
        return View(self.buf, self.ap[key])


class _VA:
    def __init__(self, b):
        self.b = b

    def __getitem__(self, key):
        return View(self.b, self.b.t[key])


def _bufs(*xs):
    out = []
    for x in xs:
        if isinstance(x, View) and x.buf not in out:
            out.append(x.buf)
    return out


def _ap(x):
    return x.ap if isinstance(x, View) else x


class K:
    def __init__(self, nc):
        self.nc = nc
        self.stack = contextlib.ExitStack()
        self.E = {}
        for name, h in (("pe", nc.tensor), ("act", nc.scalar), ("dve", nc.vector),
                        ("pool", nc.gpsimd), ("sp", nc.sync)):
            sem = self.stack.enter_context(nc.semaphore("s_" + name))
            self.E[name] = Eng(name, h, sem)
        self.free_psems = []
        self.n_psems = 0
        self.all_psems = []
        self.bufs = []
        self.uid = 0
        self.ninstr = 0
        self.marks = []

    def _psem(self):
        if self.free_psems:
            return self.free_psems.pop()
        self.n_psems += 1
        h = self.stack.enter_context(self.nc.semaphore("d%d" % self.n_psems))
        p = PSem(h)
        self.all_psems.append(p)
        return p

    def sb(self, st, shape, dtype, name):
        self.uid += 1
        t = st.enter_context(self.nc.sbuf_tensor("%s_%d" % (name, self.uid), list(shape), dtype))
        b = Buf(self, t, name)
        st.callback(self._release, b)
        return b

    def ps(self, st, shape, dtype, name):
        self.uid += 1
        t = st.enter_context(self.nc.psum_tensor("%s_%d" % (name, self.uid), list(shape), dtype))
        return Buf(self, t, name)

    def _release(self, b):
        if b.psem is not None and not getattr(b.psem, "sw", False):
            self.free_psems.append(b.psem)
        b.psem = None

    def _wait_eng(self, E, X, n):
        if E.seen.get(X.name, 0) >= n:
            return
        E.h.wait_ge(X.sem, n)
        E.seen[X.name] = n
        self.ninstr += 1

    def _wait_d(self, E, p, v):
        if v <= 0 or E.dseen.get(id(p), 0) >= v:
            return
        E.h.wait_ge(p.h, v)
        E.dseen[id(p)] = v
        self.ninstr += 1

    def _deps(self, E, r, w):
        for b in r:
            if b.w is not None:
                self._wait_eng(E, b.w[0], b.w[1])
            if b.psem is not None:
                self._wait_d(E, b.psem, b.dma_w)
        for b in w:
            if b.w is not None and (STRICT or b.w[0] is not E) and not (E.name == "pe" and b.w[0] is E):
                self._wait_eng(E, b.w[0], b.w[1])
            for X, n in b.r.items():
                if STRICT or X is not E:
                    self._wait_eng(E, X, n)
            if b.psem is not None:
                self._wait_d(E, b.psem, b.dma_rw)

    def op(self, eng, fn, r=(), w=()):
        E = self.E[eng]
        self._deps(E, r, w)
        ins = fn(E.h)
        E.n += 1
        ins.then_inc(E.sem, 1)
        self.ninstr += 1
        for b in w:
            b.w = (E, E.n)
            b.r = {}
        for b in r:
            if b.w is None or b.w[0] is not E or b.w[1] != E.n:
                b.r[E] = E.n
        return ins

    def dma(self, q, out, in_, r=(), w=(), **kw):
        Q = self.E[q]
        self._deps(Q, r, w)
        anchor = w[0] if w else r[0]
        if anchor.psem is None:
            if q == "pool":
                self.n_psems += 1
                anchor.psem = PSem(self.stack.enter_context(self.nc.semaphore("w%d" % self.n_psems)))
                anchor.psem.sw = True
                self.all_psems.append(anchor.psem)
            else:
                anchor.psem = self._psem()
        p = anchor.psem
        assert getattr(p, "sw", False) == (q == "pool"), "buffer mixes SW and HW DGE DMAs: " + anchor.name
        Q.h.dma_start(out=out, in_=in_, **kw).then_inc(p.h, 16)
        p.val += 16
        self.ninstr += 1
        anchor.dma_rw = p.val
        if w:
            anchor.dma_w = p.val
            anchor.w = None
            anchor.r = {}

    def mm(self, out, lhsT, rhs, start=True, stop=True):
        return self.op("pe", lambda e: e.matmul(out.ap, lhsT.ap, rhs.ap, start=start, stop=stop),
                       r=_bufs(lhsT, rhs), w=[out.buf])

    def tr(self, out, in_, ident):
        return self.op("pe", lambda e: e.transpose(out.ap, in_.ap, ident.ap), r=_bufs(in_, ident), w=[out.buf])

    def tt(self, eng, out, in0, in1, op):
        return self.op(eng, lambda e: e.tensor_tensor(out=out.ap, in0=in0.ap, in1=in1.ap, op=op),
                       r=_bufs(in0, in1), w=[out.buf])

    def ts(self, eng, out, in0, s1, s2, op0, op1=None):
        kw = {} if op1 is None else dict(op1=op1)
        return self.op(eng, lambda e: e.tensor_scalar(out=out.ap, in0=in0.ap, scalar1=_ap(s1), scalar2=_ap(s2),
                                                      op0=op0, **kw), r=_bufs(in0, s1, s2), w=[out.buf])

    def stt(self, eng, out, in0, scalar, in1, op0, op1):
        return self.op(eng, lambda e: e.scalar_tensor_tensor(out=out.ap, in0=in0.ap, scalar=_ap(scalar), in1=in1.ap,
                                                             op0=op0, op1=op1), r=_bufs(in0, scalar, in1), w=[out.buf])

    def act(self, out, in_, func, bias=None, scale=None, accum=None):
        kw = {}
        if bias is not None:
            kw["bias"] = _ap(bias)
        if scale is not None:
            kw["scale"] = _ap(scale)
        w = [out.buf]
        if accum is not None:
            kw["accum_out"] = accum.ap
            w.append(accum.buf)
        return self.op("act", lambda e: e.activation(out=out.ap, in_=in_.ap, func=func, **kw),
                       r=_bufs(in_, bias, scale), w=w)

    def cp(self, eng, out, in_):
        if eng == "act":
            return self.op("act", lambda e: e.copy(out=out.ap, in_=in_.ap), r=[in_.buf], w=[out.buf])
        return self.op(eng, lambda e: e.tensor_copy(out=out.ap, in_=in_.ap), r=[in_.buf], w=[out.buf])

    def memset(self, eng, out, val):
        return self.op(eng, lambda e: e.memset(out.ap, val), w=[out.buf])

    def recip(self, out, in_):
        return self.op("dve", lambda e: e.reciprocal(out=out.ap, in_=in_.ap), r=[in_.buf], w=[out.buf])

    def ld(self, q, out, dram, **kw):
        return self.dma(q, out.ap, dram, w=[out.buf], **kw)

    def stv(self, q, dram, in_, **kw):
        return self.dma(q, dram, in_.ap, r=[in_.buf], **kw)

    def barrier(self):
        for E in self.E.values():
            for X in self.E.values():
                if X is not E and X.n > 0:
                    self._wait_eng(E, X, X.n)
            for p in self.all_psems:
                self._wait_d(E, p, p.val)


def mm(k, out_ap, lhsT_ap, rhs_ap, start, stop, r, w):
    return k.op("pe", lambda e: e.matmul(out_ap, lhsT_ap, rhs_ap, start=start, stop=stop), r=r, w=w)


def rowphase(k, st, cfg):
    nc = k.nc
    ffn = cfg["ffn"]
    pre = cfg.get("pre")
    post = cfg.get("post")
    w1 = k.sb(st, [128, 8, 2 * DFF], BF16, "w1")
    w2 = k.sb(st, [128, 22, D], BF16, "w2")
    w1v = ffn["w_in"].rearrange("(c p) f -> p c f", p=128)
    for c in range(8):
        for hlf in range(4):
            f0 = hlf * 1408
            k.dma("pool", w1[:, c, f0:f0 + 1408], w1v[:, c, f0:f0 + 1408], w=[w1])
    w2v = ffn["w_out"].rearrange("(c p) d -> p c d", p=128)
    for c in range(22):
        k.dma("pool", w2[:, c, :], w2v[:, c, :], w=[w2])
    if pre is not None:
        wo = k.sb(st, [128, 8, D], BF16, "wo")
        wov = pre["w_out"].rearrange("(c p) d -> p c d", p=128)
        for c in range(8):
            k.dma("pool", wo[:, c, :], wov[:, c, :], w=[wo])

    identf = k.sb(st, [128, 128], F32, "identf")
    k.dma("sp", identf[:, :], cfg["ident"], w=[identf])
    xt = [k.sb(st, [128, D], F32, "xt%d" % i) for i in range(4)]
    hT = k.sb(st, [128, 8, 512], BF16, "hT")
    gT = k.sb(st, [128, 22, 512], BF16, "gT")
    sg = k.sb(st, [128, 512], BF16, "sg")
    tmp = k.sb(st, [128, D], F32, "tmp")
    ssq = k.sb(st, [128, 1], F32, "ssq")
    rinv = k.sb(st, [128, 1], F32, "rinv")
    ps_u = [k.ps(st, [128, 512], F32, "psu%d" % i) for i in range(4)]
    ps_y = [k.ps(st, [128, 512], F32, "psy%d" % i) for i in range(2)]
    ps_t = k.ps(st, [128, D], F32, "pst")
    Ac = k.sb(st, [128, 8], F32, "Ac")
    Bc = k.sb(st, [128, 8], F32, "Bc")
    Gc = k.sb(st, [128, 8], F32, "Gc")
    A2c = k.sb(st, [128, 8], F32, "A2c")
    B2c = k.sb(st, [128, 8], F32, "B2c")
    G = k.sb(st, [128, D], F32, "G")
    PG = k.sb(st, [128, D], F32, "PG") if pre is not None else None
    gvec2 = None
    if post is not None and post["kind"] == "final":
        gvec2 = k.sb(st, [128, D], F32, "gvec2")
        k.dma("sp", gvec2[:, :], post["g"].partition_broadcast(128), w=[gvec2])

    def col(ap1d):
        return ap1d.rearrange("(c p) -> p c", p=128)

    def load_cols(dst, ap1d):
        k.dma("sp", dst[:, :], col(ap1d), w=[dst], allow_slow_non_contiguous=True)

    def setup_mod(mr):
        load_cols(Gc, ffn["g"])
        load_cols(Ac, cfg["mod"][mr, ffn["scale"] * D:(ffn["scale"] + 1) * D])
        load_cols(Bc, cfg["mod"][mr, ffn["shift"] * D:(ffn["shift"] + 1) * D])
        k.op("dve", lambda e: e.scalar_tensor_tensor(out=Ac[:, :], in0=Ac[:, :], scalar=1.0, in1=Gc[:, :],
                                                     op0=ALU.add, op1=ALU.mult), r=[Ac, Gc], w=[Ac])
        if post is not None and post["kind"] == "hT":
            load_cols(Gc, post["g"])
            load_cols(A2c, cfg["mod"][mr, post["scale"] * D:(post["scale"] + 1) * D])
            load_cols(B2c, cfg["mod"][mr, post["shift"] * D:(post["shift"] + 1) * D])
            k.op("dve", lambda e: e.scalar_tensor_tensor(out=A2c[:, :], in0=A2c[:, :], scalar=1.0, in1=Gc[:, :],
                                                         op0=ALU.add, op1=ALU.mult), r=[A2c, Gc], w=[A2c])
        k.dma("sp", G[:, :], cfg["mod"][mr, ffn["gate"] * D:(ffn["gate"] + 1) * D].partition_broadcast(128), w=[G])
        k.op("dve", lambda e: e.tensor_scalar(out=G[:, :], in0=G[:, :], scalar1=0.5, scalar2=None, op0=ALU.mult),
             r=[G], w=[G])
        if pre is not None:
            k.dma("sp", PG[:, :], cfg["mod"][mr, pre["gate"] * D:(pre["gate"] + 1) * D].partition_broadcast(128),
                  w=[PG])

    def rms_rinv(xb):
        k.op("act", lambda e: e.activation(out=tmp[:, :], in_=xb[:, :], func=ACT.Square, accum_out=ssq[:, :]),
             r=[xb], w=[tmp, ssq])
        k.op("dve", lambda e: e.tensor_scalar(out=rinv[:, :], in0=ssq[:, :], scalar1=1.0 / D, scalar2=RMS_EPS,
                                              op0=ALU.mult, op1=ALU.add), r=[ssq], w=[rinv])
        k.op("act", lambda e: e.activation(out=rinv[:, :], in_=rinv[:, :], func=ACT.Sqrt), r=[rinv], w=[rinv])
        k.op("dve", lambda e: e.reciprocal(out=rinv[:, :], in_=rinv[:, :]), r=[rinv], w=[rinv])

    def norm_mod_T(xb, A, B, dstT, s):
        rms_rinv(xb)
        k.op("dve", lambda e: e.tensor_scalar(out=tmp[:, :], in0=xb[:, :], scalar1=rinv[:, 0:1], scalar2=None,
                                              op0=ALU.mult), r=[xb, rinv], w=[tmp])
        for c in range(8):
            k.op("pe", lambda e: e.transpose(ps_t[:, c * 128:(c + 1) * 128], tmp[:, c * 128:(c + 1) * 128],
                                             identf[:, :]), r=[tmp, identf], w=[ps_t])
        for c in range(8):
            if c % 2 == 0:
                k.op("dve", lambda e: e.tensor_scalar(out=dstT[:, c, s * 128:(s + 1) * 128],
                                                      in0=ps_t[:, c * 128:(c + 1) * 128], scalar1=A[:, c:c + 1],
                                                      scalar2=B[:, c:c + 1], op0=ALU.mult, op1=ALU.add),
                     r=[ps_t, A, B], w=[dstT])
            else:
                k.op("act", lambda e: e.activation(out=dstT[:, c, s * 128:(s + 1) * 128],
                                                   in_=ps_t[:, c * 128:(c + 1) * 128], func=ACT.Identity,
                                                   scale=A[:, c:c + 1], bias=B[:, c:c + 1]),
                     r=[ps_t, A, B], w=[dstT])

    ui = 0
    yi = 0
    for (tok0, ntok, mr) in cfg["segs"]:
        setup_mod(mr)
        for t0 in range(tok0, tok0 + ntok, 512):
            n = min(512, tok0 + ntok - t0)
            ns = n // 128
            for s in range(ns):
                k.dma("sp", xt[s][:, :], cfg["xin"][t0 + s * 128:t0 + (s + 1) * 128, :], w=[xt[s]])
            if pre is not None:
                for s in range(ns):
                    k.dma("sp", tmp[:, :], pre["o"][t0 + s * 128:t0 + (s + 1) * 128, :], w=[tmp])
                    for c in range(8):
                        k.op("pe", lambda e: e.transpose(ps_t[:, c * 128:(c + 1) * 128], tmp[:, c * 128:(c + 1) * 128],
                                                         identf[:, :]), r=[tmp, identf], w=[ps_t])
                    k.op("act", lambda e: e.copy(out=hT[:, :, s * 128:(s + 1) * 128],
                                                 in_=ps_t[:, :].rearrange("p (c t) -> p c t", c=8)), r=[ps_t], w=[hT])
                for s in range(ns):
                    for hf in range(2):
                        py = ps_y[yi % 2]
                        yi += 1
                        for c in range(8):
                            mm(k, py[:, :], hT[:, c, s * 128:(s + 1) * 128], wo[:, c, hf * 512:(hf + 1) * 512],
                               c == 0, c == 7, r=[hT, wo], w=[py])
                        sl = slice(hf * 512, (hf + 1) * 512)
                        k.op("dve", lambda e, py=py, sl=sl: e.tensor_tensor(out=tmp[:, sl], in0=py[:, :],
                                                                             in1=PG[:, sl], op=ALU.mult),
                             r=[py, PG], w=[tmp])
                        k.op("dve", lambda e, s=s, sl=sl: e.tensor_tensor(out=xt[s][:, sl], in0=xt[s][:, sl],
                                                                            in1=tmp[:, sl], op=ALU.add),
                             r=[xt[s], tmp], w=[xt[s]])
            for s in range(ns):
                norm_mod_T(xt[s], Ac, Bc, hT, s)
            for fc in range(22 if cfg.get("stage", 9) >= 2 else 0):
                p1 = ps_u[ui % 4]
                p2 = ps_u[(ui + 1) % 4]
                ui += 2
                for c in range(8):
                    mm(k, p1[:, 0:n], w1[:, c, fc * 128:(fc + 1) * 128], hT[:, c, 0:n], c == 0, c == 7,
                       r=[w1, hT], w=[p1])
                for c in range(8):
                    mm(k, p2[:, 0:n], w1[:, c, DFF + fc * 128:DFF + (fc + 1) * 128], hT[:, c, 0:n], c == 0, c == 7,
                       r=[w1, hT], w=[p2])
                k.op("act", lambda e, p1=p1: e.activation(out=sg[:, 0:n], in_=p1[:, 0:n], func=ACT.Silu),
                     r=[p1], w=[sg])
                k.op("dve", lambda e, p2=p2, fc=fc: e.tensor_tensor(out=gT[:, fc, 0:n], in0=sg[:, 0:n],
                                                                     in1=p2[:, 0:n], op=ALU.mult),
                     r=[sg, p2], w=[gT])
            for s in range(ns if cfg.get("stage", 9) >= 3 else 0):
                for hf in range(2):
                    py = ps_y[yi % 2]
                    yi += 1
                    for fc in range(22):
                        mm(k, py[:, :], gT[:, fc, s * 128:(s + 1) * 128], w2[:, fc, hf * 512:(hf + 1) * 512],
                           fc == 0, fc == 21, r=[gT, w2], w=[py])
                    sl = slice(hf * 512, (hf + 1) * 512)
                    k.op("dve", lambda e, py=py, sl=sl: e.tensor_tensor(out=tmp[:, sl], in0=py[:, :], in1=G[:, sl],
                                                                         op=ALU.mult), r=[py, G], w=[tmp])
                    k.op("dve", lambda e, s=s, sl=sl: e.tensor_tensor(out=xt[s][:, sl], in0=xt[s][:, sl],
                                                                        in1=tmp[:, sl], op=ALU.add),
                         r=[xt[s], tmp], w=[xt[s]])
            for s in range(ns):
                r0 = t0 + s * 128
                if post is not None and post["kind"] == "final":
                    if mr == 0:
                        rms_rinv(xt[s])
                        k.op("dve", lambda e, s=s: e.scalar_tensor_tensor(out=tmp[:, :], in0=xt[s][:, :],
                                                                           scalar=rinv[:, 0:1], in1=gvec2[:, :],
                                                                           op0=ALU.mult, op1=ALU.mult),
                             r=[xt[s], rinv, gvec2], w=[tmp])
                        k.dma("sp", post["out"][r0:r0 + 128, :], tmp[:, :], r=[tmp])
                    continue
                k.dma("sp", cfg["xout"][r0:r0 + 128, :], xt[s][:, :], r=[xt[s]])
                if post is not None and post["kind"] == "hT":
                    norm_mod_T(xt[s], A2c, B2c, hT, s)
            if post is not None and post["kind"] == "hT":
                for c in range(8):
                    k.dma("sp", post["hT"][c * 128:(c + 1) * 128, t0:t0 + n], hT[:, c, 0:n], r=[hT])


CH = 64
DECAY_SCALE = 0.6065306597126334


def projphase(k, st, cfg):
    S = cfg["S"]
    C = cfg["C"]
    Wt = cfg["W"]
    T, CTX = cfg["T"], cfg["CTX"]
    hTd = S["hT"]
    wabc = k.sb(st, [128, 8, 2304], BF16, "wabc")
    wgd = k.sb(st, [128, 8, 128], BF16, "wgd")
    wsw = k.sb(st, [128, 8, 1280], BF16, "wsw")
    wv = Wt["w_in"].rearrange("(c p) f -> p c f", p=128)
    for c in range(8):
        k.ld("pool", wabc.v[:, c, 0:1152], wv[:, c, 0:1152])
        k.ld("pool", wabc.v[:, c, 1152:2304], wv[:, c, 1152:2304])
        k.ld("pool", wgd.v[:, c, :], wv[:, c, 3072:3200])
        k.ld("pool", wsw.v[:, c, :], Wt["w_sw"].rearrange("(c p) f -> p c f", p=128)[:, c, :])
    wda = [k.sb(st, [128, 8, 896], BF16, "wda%d" % d) for d in range(2)]
    wdb = [k.sb(st, [128, 8, 896], BF16, "wdb%d" % d) for d in range(2)]
    with contextlib.ExitStack() as st2:
        wst = k.sb(st2, [128, 8, 896], F32, "wst")
        mub = k.sb(st2, [128, 896], F32, "mub")
        wtm = k.sb(st2, [128, 8, 896], F32, "wtm")
        for d in range(2):
            for c in range(8):
                k.ld("sp", wst.v[:, c, :], Wt["w_d"][d].rearrange("(c p) f -> p c f", p=128)[:, c, :])
            k.ld("sp", mub.v[:, :], Wt["mu"][d].partition_broadcast(128))
            for c in range(8):
                k.tt("dve", wtm.v[:, c, :], wst.v[:, c, :], mub.v[:, :], ALU.mult)
            k.cp("act", wda[d].v[:, :, :], wtm.v[:, :, :])
            k.tt("dve", wdb[d].v[:, :, :], wst.v[:, :, :], wtm.v[:, :, :], ALU.subtract)
        k.barrier()
    pc = k.sb(st, [128, 24], F32, "pcols")
    k.ld("sp", pc.v[:, :], Wt["pcols"])
    oma = k.sb(st, [128, 2], F32, "oma")
    k.ts("dve", oma.v[:, :], pc.v[:, 14:16], -1.0, 1.0, ALU.mult, ALU.add)
    identf = k.sb(st, [128, 128], F32, "identf")
    k.ld("sp", identf.v[:, :], C["ident"])
    bd = k.sb(st, [128, 128], F32, "bd")
    k.ld("sp", bd.v[:, :], C["bd64"])
    e2 = k.sb(st, [128, 2], F32, "e2")
    k.ld("sp", e2.v[:, :], C["e2"])
    rtab = k.sb(st, [128, 2, 2, 3, CH], F32, "rtab")
    k.ld("sp", rtab.v[:, :, :, :, :], C["ret_tab"])
    w2s = k.sb(st, [64, 2, 256], BF16, "w2s")
    a2s = k.sb(st, [64, 2, 256], BF16, "a2s")
    g2s = k.sb(st, [128, 256], BF16, "g2s")
    for d in range(2):
        k.ld("pool", w2s.v[:, d, :], Wt["w2"][d])
        k.ld("pool", a2s.v[:, d, :], Wt["a2"][d])
    k.ld("pool", g2s.v[:, :], Wt["g2"])

    NT = 512
    hTt = k.sb(st, [128, 8, NT + 2], BF16, "hTt")
    ca = k.sb(st, [128, NT], F32, "ca")
    sa = k.sb(st, [128, NT], F32, "sa")
    cs_ = k.sb(st, [128, NT], F32, "cs")
    ss_ = k.sb(st, [128, NT], F32, "ss")
    gca = k.sb(st, [128, 2, NT], F32, "gca")
    gsa = k.sb(st, [128, 2, NT], F32, "gsa")
    F = [k.sb(st, [128, NT], F32, "f%d" % i) for i in range(12)]
    ob = k.sb(st, [128, NT], BF16, "ob")
    sbf = [k.sb(st, [128, NT], BF16, "sbf%d" % i) for i in range(2)]
    sbi = [0]

    def st_bf(dst, srcv, n):
        b = sbf[sbi[0] % 2]
        sbi[0] += 1
        k.cp("act", b.v[:, 0:n], srcv)
        k.stv("sp", dst, b.v[:, 0:n])
    tb = k.sb(st, [128, 512], F32, "tb")
    tbb = k.sb(st, [128, 512], BF16, "tbb")
    thb = k.sb(st, [64, NT], BF16, "thb")
    alb = k.sb(st, [64, NT], BF16, "alb")
    sgb = k.sb(st, [128, NT], BF16, "sgb")
    bs4 = k.sb(st, [128, 4, 4], F32, "bs4")
    pcx = k.sb(st, [128, NT // CH], F32, "pcx")
    PS = [k.ps(st, [128, 512], F32, "pp%d" % i) for i in range(8)]
    pi = [0]

    def nps():
        p = PS[pi[0] % 8]
        pi[0] += 1
        return p

    def chain(p, wlist, M, n, col0=1):
        tot = len(wlist) * 8
        i = 0
        for (wb, c0, sh) in wlist:
            for c in range(8):
                k.mm(p.v[0:M, 0:n], wb.v[:, c, c0:c0 + M], hTt.v[:, c, col0 + sh:col0 + sh + n], i == 0, i == tot - 1)
                i += 1

    def chain_tok(p, wlist, ncols, s):
        tot = len(wlist) * 8
        i = 0
        for (wb, c0, sh) in wlist:
            for c in range(8):
                k.mm(p.v[:, 0:ncols], hTt.v[:, c, 1 + sh + s * 128:1 + sh + (s + 1) * 128], wb.v[:, c, c0:c0 + ncols],
                     i == 0, i == tot - 1)
                i += 1

    def rope_out(P, Psw, ctab, stab, dst, n):
        k.tt("dve", F[10].v[:, 0:n], P.v[:, 0:n], ctab, ALU.mult)
        k.tt("dve", F[11].v[:, 0:n], Psw.v[:, 0:n], stab, ALU.mult)
        k.tt("dve", dst, F[10].v[:, 0:n], F[11].v[:, 0:n], ALU.add)

    segs = [(0, T, True), (T, CTX, False)]
    for (seg0, seglen, latent) in segs:
        for t0 in range(seg0, seg0 + seglen, NT):
            n = min(NT, seg0 + seglen - t0)
            ns = n // 128
            nch = n // CH
            lo = t0 - 1 if t0 > seg0 else t0
            hi = t0 + n + 1 if t0 + n < seg0 + seglen else t0 + n
            if lo == t0:
                k.memset("dve", hTt.v[:, :, 0:1], 0.0)
            if hi == t0 + n:
                k.memset("dve", hTt.v[:, :, n + 1:n + 2], 0.0)
            for c in range(8):
                k.ld("sp", hTt.v[:, c, 1 - (t0 - lo):1 + n + (hi - t0 - n)], hTd[c * 128:(c + 1) * 128, lo:hi])
            if latent:
                k.ld("sp", ca.v[:, 0:n], C["ropeA_c"][:, t0:t0 + n])
                k.ld("sp", sa.v[:, 0:n], C["ropeA_s"][:, t0:t0 + n])
                k.ld("sp", cs_.v[:, 0:n], C["ropeS_c"][:, t0:t0 + n])
                k.ld("sp", ss_.v[:, 0:n], C["ropeS_s"][:, t0:t0 + n])
                for qk in range(2):
                    k.ts("dve", gca.v[:, qk, 0:n], ca.v[:, 0:n], pc.v[:, 2 * qk:2 * qk + 1], None, ALU.mult)
                    k.ts("dve", gsa.v[:, qk, 0:n], sa.v[:, 0:n], pc.v[:, 2 * qk + 1:2 * qk + 2], None, ALU.mult)
            for grp, base, swb, dq, dk_, dv_ in (("A", 0, 0, S["QA"], S["KA"], S["VA"]),
                                                ("B", 512, 384, S["QB"], S["KB"], S["VB"])):
                for j in range(3):
                    P, Psw = nps(), nps()
                    chain(P, [(wabc, base + j * 128, 0)], 128, n)
                    if latent:
                        chain(Psw, [(wsw, swb + j * 128, 0)], 128, n)
                    qk = 0 if j < 2 else 1
                    if grp == "B":
                        k.act(F[0].v[:, 0:n], P.v[:, 0:n], ACT.Square)
                        pss = nps()
                        k.mm(pss.v[:, 0:n], bd.v[:, :], F[0].v[:, 0:n])
                        k.ts("dve", F[1].v[:, 0:n], pss.v[:, 0:n], 1.0 / 64, RMS_EPS, ALU.mult, ALU.add)
                        k.act(F[1].v[:, 0:n], F[1].v[:, 0:n], ACT.Sqrt)
                        k.recip(F[1].v[:, 0:n], F[1].v[:, 0:n])
                        k.tt("dve", F[2].v[:, 0:n], P.v[:, 0:n], F[1].v[:, 0:n], ALU.mult)
                        if latent:
                            k.tt("dve", F[3].v[:, 0:n], Psw.v[:, 0:n], F[1].v[:, 0:n], ALU.mult)
                            rope_out(F[2], F[3], gca.v[:, qk, 0:n], gsa.v[:, qk, 0:n], ob.v[:, 0:n], n)
                        else:
                            k.ts("dve", ob.v[:, 0:n], F[2].v[:, 0:n], pc.v[:, 2 * qk:2 * qk + 1], None, ALU.mult)
                    else:
                        if latent:
                            rope_out(P, Psw, ca.v[:, 0:n], sa.v[:, 0:n], ob.v[:, 0:n], n)
                        else:
                            k.cp("act", ob.v[:, 0:n], P.v[:, 0:n])
                    dst = dq[j * 128:(j + 1) * 128, t0:t0 + n] if j < 2 else dk_[:, t0:t0 + n]
                    k.stv("sp", dst, ob.v[:, 0:n])
                for s in range(ns):
                    P = nps()
                    chain_tok(P, [(wabc, base + 384, 0)], 128, s)
                    k.cp("act", tbb.v[:, 0:128], P.v[:, 0:128])
                    k.stv("sp", dv_[t0 + s * 128:t0 + (s + 1) * 128, :], tbb.v[:, 0:128])
            for j in range(4):
                P, Psw = nps(), nps()
                chain(P, [(wabc, 1024 + j * 128, 0)], 128, n)
                g = j % 2
                if latent:
                    chain(Psw, [(wsw, 768 + j * 128, 0)], 128, n)
                    rope_out(P, Psw, cs_.v[:, 0:n], ss_.v[:, 0:n], F[0].v[:, 0:n], n)
                else:
                    k.cp("act", F[0].v[:, 0:n], P.v[:, 0:n])
                x3 = F[0].v[:, 0:n].rr("p (c i) -> p c i", i=CH)
                for d in range(2):
                    if j < 2:
                        k.tt("dve", F[1].v[:, 0:n].rr("p (c i) -> p c i", i=CH), x3,
                             rtab.v[:, g, d, 0:1, :].bc([128, nch, CH]), ALU.mult)
                        st_bf(S["C_Rt"][d, g * 128:(g + 1) * 128, t0:t0 + n], F[1].v[:, 0:n], n)
                    else:
                        k.tt("dve", F[1].v[:, 0:n].rr("p (c i) -> p c i", i=CH), x3,
                             rtab.v[:, g, d, 1:2, :].bc([128, nch, CH]), ALU.mult)
                        st_bf(S["C_Kt"][d, g * 128:(g + 1) * 128, t0:t0 + n], F[1].v[:, 0:n], n)
                        k.tt("dve", F[2].v[:, 0:n].rr("p (c i) -> p c i", i=CH), x3,
                             rtab.v[:, g, d, 2:3, :].bc([128, nch, CH]), ALU.mult)
                        for s in range(ns):
                            pt = nps()
                            k.tr(pt.v[:, 0:128], F[2].v[:, s * 128:(s + 1) * 128], identf.v[:, :])
                            k.cp("act", tbb.v[:, 0:128], pt.v[:, 0:128])
                            k.stv("sp", S["C_Kh"][d, t0 + s * 128:t0 + (s + 1) * 128, g * 128:(g + 1) * 128],
                                  tbb.v[:, 0:128])
            for s in range(ns):
                P = nps()
                chain_tok(P, [(wabc, 1536, 0)], 256, s)
                k.cp("act", tbb.v[:, 0:256], P.v[:, 0:256])
                k.stv("sp", S["C_V"][t0 + s * 128:t0 + (s + 1) * 128, :], tbb.v[:, 0:256])
                P = nps()
                chain_tok(P, [(wabc, 1792, 0)], 512, s)
                k.act(tb.v[:, 0:512], P.v[:, 0:512], ACT.Silu)
                k.stv("sp", S["C_G"][t0 + s * 128:t0 + (s + 1) * 128, :], tb.v[:, 0:512])
            P = nps()
            chain(P, [(wgd, 0, 0)], 128, n)
            k.act(sgb.v[:, 0:n], P.v[:, 0:n], ACT.Sigmoid)
            for s in range(ns):
                P = nps()
                k.mm(P.v[:, 0:256], sgb.v[:, s * 128:(s + 1) * 128], g2s.v[:, :])
                k.cp("act", tb.v[:, 0:256], P.v[:, 0:256])
                k.stv("sp", S["D_G"][t0 + s * 128:t0 + (s + 1) * 128, :], tb.v[:, 0:256])
            for d in range(2):
                sh = -1 if d == 0 else 1
                wl = lambda c0: [(wdb[d], c0, 0), (wda[d], c0, sh)]
                P = nps()
                chain(P, wl(768), 64, n)
                k.act(thb.v[:, 0:n], P.v[0:64, 0:n], ACT.Tanh)
                P = nps()
                chain(P, wl(832), 64, n)
                k.cp("act", alb.v[:, 0:n], P.v[0:64, 0:n])
                for s in range(ns):
                    P = nps()
                    chain_tok(P, wl(512), 256, s)
                    k.cp("act", tbb.v[:, 0:256], P.v[:, 0:256])
                    k.stv("sp", S["D_V"][d, t0 + s * 128:t0 + (s + 1) * 128, :], tbb.v[:, 0:256])
                for g in range(2):
                    r_, k_, lw, cs, a_, kk, e_, t1, t2 = F[0], F[1], F[2], F[3], F[4], F[5], F[6], F[7], F[8]
                    gc = slice(g, g + 1)
                    P = nps()
                    chain(P, wl(g * 128), 128, n)
                    k.cp("act", r_.v[:, 0:n], P.v[:, 0:n])
                    P = nps()
                    chain(P, wl(256 + g * 128), 128, n)
                    k.cp("act", k_.v[:, 0:n], P.v[:, 0:n])
                    P = nps()
                    k.mm(P.v[:, 0:n], w2s.v[:, d, g * 128:(g + 1) * 128], thb.v[:, 0:n])
                    k.act(lw.v[:, 0:n], P.v[:, 0:n], ACT.Sigmoid, bias=pc.v[:, 4 + 2 * d + g:5 + 2 * d + g])
                    k.ts("dve", lw.v[:, 0:n], lw.v[:, 0:n], -DECAY_SCALE, None, ALU.mult)
                    P = nps()
                    k.mm(P.v[:, 0:n], a2s.v[:, d, g * 128:(g + 1) * 128], alb.v[:, 0:n])
                    k.act(a_.v[:, 0:n], P.v[:, 0:n], ACT.Sigmoid, bias=pc.v[:, 8 + 2 * d + g:9 + 2 * d + g])
                    k.ts("dve", kk.v[:, 0:n], k_.v[:, 0:n], pc.v[:, 12 + g:13 + g], None, ALU.mult)
                    k.act(t1.v[:, 0:n], kk.v[:, 0:n], ACT.Square)
                    P = nps()
                    k.mm(P.v[:, 0:n], bd.v[:, :], t1.v[:, 0:n])
                    k.act(t1.v[:, 0:n], P.v[:, 0:n], ACT.Sqrt)
                    k.ts("dve", t1.v[:, 0:n], t1.v[:, 0:n], 1e-12, None, ALU.max)
                    k.recip(t1.v[:, 0:n], t1.v[:, 0:n])
                    k.tt("dve", kk.v[:, 0:n], kk.v[:, 0:n], t1.v[:, 0:n], ALU.mult)
                    k.ts("dve", t1.v[:, 0:n], a_.v[:, 0:n], pc.v[:, 14 + g:15 + g], oma.v[:, gc], ALU.mult, ALU.add)
                    k.tt("dve", k_.v[:, 0:n], k_.v[:, 0:n], t1.v[:, 0:n], ALU.mult)
                    k.stt("dve", t1.v[:, 0:n], r_.v[:, 0:n], pc.v[:, 16 + 2 * d + g:17 + 2 * d + g], k_.v[:, 0:n],
                          ALU.mult, ALU.mult)
                    for s in range(ns):
                        P = nps()
                        k.mm(P.v[:, 0:2], t1.v[:, s * 128:(s + 1) * 128], e2.v[:, :])
                        k.cp("act", bs4.v[:, s, 2 * g:2 * g + 2], P.v[:, 0:2])
                    k.tt("dve", a_.v[:, 0:n], a_.v[:, 0:n], kk.v[:, 0:n], ALU.mult)
                    src, dst = lw, cs
                    k.cp("act", t2.v[:, 0:n], lw.v[:, 0:n])
                    src = t2
                    for stp in (1, 2, 4, 8, 16, 32):
                        s3 = src.v[:, 0:n].rr("p (c i) -> p c i", i=CH)
                        d3 = dst.v[:, 0:n].rr("p (c i) -> p c i", i=CH)
                        if d == 0:
                            k.tt("dve", d3[:, :, stp:], s3[:, :, stp:], s3[:, :, :CH - stp], ALU.add)
                            k.cp("act", d3[:, :, :stp], s3[:, :, :stp])
                        else:
                            k.tt("dve", d3[:, :, :CH - stp], s3[:, :, :CH - stp], s3[:, :, stp:], ALU.add)
                            k.cp("act", d3[:, :, CH - stp:], s3[:, :, CH - stp:])
                        src, dst = dst, src
                    cs = src
                    spare = dst
                    cs3 = cs.v[:, 0:n].rr("p (c i) -> p c i", i=CH)
                    tot3 = cs3[:, :, CH - 1:CH] if d == 0 else cs3[:, :, 0:1]
                    k.act(e_.v[:, 0:n], cs.v[:, 0:n], ACT.Exp)
                    k.tt("dve", t1.v[:, 0:n], r_.v[:, 0:n], e_.v[:, 0:n], ALU.mult)
                    st_bf(S["D_Rt"][d, g * 128:(g + 1) * 128, t0:t0 + n], t1.v[:, 0:n], n)
                    k.tt("dve", e_.v[:, 0:n], cs.v[:, 0:n], lw.v[:, 0:n], ALU.subtract)
                    k.act(e_.v[:, 0:n], e_.v[:, 0:n], ACT.Exp)
                    k.stt("dve", t1.v[:, 0:n], kk.v[:, 0:n], -1.0, e_.v[:, 0:n], ALU.mult, ALU.mult)
                    st_bf(S["D_At"][d, g * 128:(g + 1) * 128, t0:t0 + n], t1.v[:, 0:n], n)
                    for s in range(ns):
                        pt = nps()
                        k.tr(pt.v[:, 0:128], t1.v[:, s * 128:(s + 1) * 128], identf.v[:, :])
                        k.cp("act", tbb.v[:, 0:128], pt.v[:, 0:128])
                        k.stv("sp", S["D_Atok"][d, t0 + s * 128:t0 + (s + 1) * 128, g * 128:(g + 1) * 128],
                              tbb.v[:, 0:128])
                    k.act(e_.v[:, 0:n], cs.v[:, 0:n], ACT.Exp, scale=-1.0)
                    k.tt("dve", t1.v[:, 0:n], a_.v[:, 0:n], e_.v[:, 0:n], ALU.mult)
                    st_bf(S["D_Bt"][d, g * 128:(g + 1) * 128, t0:t0 + n], t1.v[:, 0:n], n)
                    k.tt("dve", t1.v[:, 0:n], k_.v[:, 0:n], e_.v[:, 0:n], ALU.mult)
                    st_bf(S["D_Kt"][d, g * 128:(g + 1) * 128, t0:t0 + n], t1.v[:, 0:n], n)
                    k.act(pcx.v[:, 0:nch], tot3.rr("p c o -> p (c o)"), ACT.Exp)
                    k.stv("sp", S["D_PC"][d, g * 128:(g + 1) * 128, t0 // CH:t0 // CH + nch], pcx.v[:, 0:nch])
                    k.tt("dve", e_.v[:, 0:n].rr("p (c i) -> p c i", i=CH), tot3.bc([128, nch, CH]), cs3, ALU.subtract)
                    k.act(e_.v[:, 0:n], e_.v[:, 0:n], ACT.Exp)
                    for (srcb, dstd) in ((a_, S["D_Bh"]), (k_, S["D_Kh"])):
                        k.tt("dve", t1.v[:, 0:n], srcb.v[:, 0:n], e_.v[:, 0:n], ALU.mult)
                        for s in range(ns):
                            pt = nps()
                            k.tr(pt.v[:, 0:128], t1.v[:, s * 128:(s + 1) * 128], identf.v[:, :])
                            k.cp("act", tbb.v[:, 0:128], pt.v[:, 0:128])
                            k.stv("sp", dstd[d, t0 + s * 128:t0 + (s + 1) * 128, g * 128:(g + 1) * 128], tbb.v[:, 0:128])
                for s in range(ns):
                    k.stv("sp", S["D_BS"][d, t0 + s * 128:t0 + (s + 1) * 128, :], bs4.v[:, s, :])


def attnphase(k, st, cfg):
    S, C, Wt = cfg["S"], cfg["C"], cfg["W"]
    T, CTX = cfg["T"], cfg["CTX"]
    TT = T + CTX
    NKC = TT // 128
    need_ctx = cfg["need_ctx"]
    kT = k.sb(st, [128, TT], BF16, "kT")
    va = k.sb(st, [128, NKC, 65], BF16, "va")
    qT = [k.sb(st, [128, 512], BF16, "qT%d" % i) for i in range(2)]
    for c0 in range(0, TT, 2048):
        k.memset("dve", kT.v[64:128, c0:min(TT, c0 + 2048)], 0.0)
    for q_ in qT:
        k.memset("dve", q_.v[64:128, :], 0.0)
    pT = [k.sb(st, [128, 512], BF16, "pT%d" % i) for i in range(3)]
    ot = [k.sb(st, [128, 64], F32, "ot%d" % i) for i in range(2)]
    rc = k.sb(st, [128, 1], F32, "rc")
    esk = k.sb(st, [128, 4], F32, "esk")
    mprev = k.sb(st, [128, 128], BF16, "mprev")
    mnext = k.sb(st, [128, 128], BF16, "mnext")
    k.ld("pool", mprev.v[:, :], C["mprev"])
    k.ld("pool", mnext.v[:, :], C["mnext"])
    k.ld("sp", esk.v[:, :], Wt["sink"].partition_broadcast(128))
    k.act(esk.v[:, :], esk.v[:, :], ACT.Exp)
    ps_s = [k.ps(st, [128, 512], F32, "pss%d" % i) for i in range(3)]
    ps_o = [k.ps(st, [128, 512], F32, "pso%d" % i) for i in range(4)]
    cnt = dict(s=0, q=0, o=0, p=0)
    scale = 64 ** -0.5

    def load_kv(kd, vd, g):
        k.ld("sp", kT.v[0:64, :], kd[g * 64:(g + 1) * 64, :])
        k.memset("dve", va.v[:, :, 64:65], 1.0)
        k.ld("sp", va.v[:, :, 0:64], vd[:, g * 64:(g + 1) * 64].rearrange("(c p) j -> p c j", p=128))

    def finish(acc, nq_sub, t0, col, sinkcol):
        for s in range(nq_sub):
            o = ot[cnt["o"] % 2]
            cnt["o"] += 1
            if sinkcol is None:
                k.recip(rc.v[:, :], acc[s].v[:, 64:65])
            else:
                k.tt("dve", rc.v[:, :], acc[s].v[:, 64:65], esk.v[:, sinkcol:sinkcol + 1], ALU.add)
                k.recip(rc.v[:, :], rc.v[:, :])
            k.ts("dve", o.v[:, :], acc[s].v[:, 0:64], rc.v[:, 0:1], None, ALU.mult)
            k.stv("sp", S["O"][t0 + s * 128:t0 + (s + 1) * 128, col:col + 64], o.v[:, :])

    def block(qd, h, t0, nq, chunks, col, sinkcol):
        q = qT[cnt["q"] % 2]
        cnt["q"] += 1
        k.ld("sp", q.v[0:64, 0:nq], qd[h * 64:(h + 1) * 64, t0:t0 + nq])
        nsub = nq // 128
        first, last = {}, {}
        for i, (kc, subs) in enumerate(chunks):
            for (s, _) in subs:
                first.setdefault(s, i)
                last[s] = i

        def qk(i):
            ps = ps_s[cnt["s"] % 3]
            cnt["s"] += 1
            kc = chunks[i][0]
            k.mm(ps.v[:, 0:nq], kT.v[:, kc * 128:(kc + 1) * 128], q.v[:, 0:nq])
            return ps

        nxt = qk(0)
        for i, (kc, subs) in enumerate(chunks):
            ps = nxt
            if i + 1 < len(chunks):
                nxt = qk(i + 1)
            p = pT[cnt["p"] % 3]
            cnt["p"] += 1
            k.act(p.v[:, 0:nq], ps.v[:, 0:nq], ACT.Exp, scale=scale)
            for (s, m_) in subs:
                if m_ is not None:
                    k.tt("dve", p.v[:, s * 128:(s + 1) * 128], p.v[:, s * 128:(s + 1) * 128], m_.v[:, :], ALU.mult)
            for (s, m_) in subs:
                k.mm(ps_o[s].v[:, 0:65], p.v[:, s * 128:(s + 1) * 128], va.v[:, kc, :], i == first[s], i == last[s])
        finish(ps_o, nsub, t0, col, sinkcol)

    for g in range(2):
        load_kv(S["KB"], S["VB"], g)
        for h in (2 * g, 2 * g + 1):
            for t0 in range(0, T, 512):
                nq = min(512, T - t0)
                block(S["QB"], h, t0, nq, [(kc, [(s, None) for s in range(nq // 128)]) for kc in range(NKC)],
                      256 + h * 64, None)
            if need_ctx:
                block(S["QB"], h, T, CTX, [(kc, [(s, None) for s in range(CTX // 128)]) for kc in range(T // 128, NKC)],
                      256 + h * 64, None)
    nb = T // 128
    cchunks = list(range(T // 128, NKC))
    for g in range(2):
        load_kv(S["KA"], S["VA"], g)
        for h in (2 * g, 2 * g + 1):
            for b0 in range(0, nb, 4):
                nsb = min(4, nb - b0)
                chunks = []
                for kc in range(max(0, b0 - 1), min(nb, b0 + nsb + 1)):
                    subs = []
                    for s in range(nsb):
                        dlt = kc - (b0 + s)
                        if dlt == -1:
                            subs.append((s, mprev))
                        elif dlt == 0:
                            subs.append((s, None))
                        elif dlt == 1:
                            subs.append((s, mnext))
                    chunks.append((kc, subs))
                for kc in cchunks:
                    chunks.append((kc, [(s, None) for s in range(nsb)]))
                block(S["QA"], h, b0 * 128, nsb * 128, chunks, h * 64, h)
            if need_ctx:
                block(S["QA"], h, T, CTX, [(kc, [(s, None) for s in range(CTX // 128)]) for kc in cchunks], h * 64, h)


def scanphase(k, st, cfg):
    S, C = cfg["S"], cfg["C"]
    T, CTX = cfg["T"], cfg["CTX"]
    TT = T + CTX
    dplr = cfg["dplr"]
    pre = "D_" if dplr else "C_"
    NCG = 2
    GT_ = NCG * CH
    NCHT = TT // CH
    f3 = lambda b: b.v[:, :, :]
    t64 = lambda nm: k.sb(st, [64, 8, 64], F32, nm)
    b64 = lambda nm: k.sb(st, [64, 8, 64], BF16, nm)
    chm = lambda nm: k.sb(st, [64, 8, GT_], BF16, nm)
    tkm = lambda nm: k.sb(st, [64, NCG, 8, 64], BF16, nm)
    RT = [chm("RT%d" % i) for i in range(2)]
    KT = [chm("KT%d" % i) for i in range(2)]
    KH = [tkm("KH%d" % i) for i in range(2)]
    VV = [tkm("VV%d" % i) for i in range(2)]
    Min = t64("Min")
    k.ld("sp", f3(Min), C["m_in"].rearrange("p (q t) -> p q t", q=8))
    id8 = t64("id8")
    k.ld("sp", f3(id8), C["id8"].rearrange("p (q t) -> p q t", q=8))
    if dplr:
        AT = [chm("AT%d" % i) for i in range(2)]
        BT = [chm("BT%d" % i) for i in range(2)]
        BH = [tkm("BH%d" % i) for i in range(2)]
        AK = [tkm("AK%d" % i) for i in range(2)]
        Mst, Mts = t64("Mst"), t64("Mts")
        k.ld("sp", f3(Mst), C["m_st"].rearrange("p (q t) -> p q t", q=8))
        k.ld("sp", f3(Mts), C["m_ts"].rearrange("p (q t) -> p q t", q=8))
        PCt = k.sb(st, [64, 8, NCHT], F32, "PCt")
        for d in range(2):
            k.ld("sp", PCt.v[:, d * 4:(d + 1) * 4, :], S["D_PC"][d].rearrange("(h j) c -> j h c", j=64))
    else:
        PCr = k.sb(st, [64, 8], F32, "PCr")
        k.ld("sp", PCr.v[:, :], C["ret_pc"])
        GTc = b64("GTc")
        k.tt("dve", f3(GTc), f3(id8), PCr.v[:, :].rr("p (q o) -> p q o", o=1).bc([64, 8, 64]), ALU.mult)
    sets = []
    for c in range(NCG):
        B = dict(ArkT=b64("ArkT%d" % c), Yv=t64("Yv%d" % c), H=t64("H%d" % c))
        if dplr:
            B.update(Ls=[b64("L%d_%d" % (i, c)) for i in range(5)], Ns=[b64("N%d_%d" % (i, c)) for i in range(6)],
                     AakT=b64("AakT%d" % c), ArbT=b64("ArbT%d" % c), Qe=b64("Qe%d" % c), GTt=b64("GTt%d" % c),
                     X=[k.sb(st, [64, 8, 128], F32, "X%d_%d" % (i, c)) for i in range(2)],
                     Xb=k.sb(st, [64, 8, 128], BF16, "Xb_%d" % c),
                     psX=k.ps(st, [64, 1024], F32, "psX%d" % c))
        sets.append(B)
    ST = [b64("ST%d" % i) for i in range(2)]
    Yt = [t64("Yt%d" % i) for i in range(2)]
    psA = [k.ps(st, [64, 512], F32, "psA%d" % i) for i in range(2)]
    psY = [k.ps(st, [64, 512], F32, "psY%d" % i) for i in range(2)]
    cnt = dict(a=0, y=0, e=0)
    p3 = lambda p: p.v[:, :].rr("p (q t) -> p q t", q=8)

    def npa():
        p = psA[cnt["a"] % 2]
        cnt["a"] += 1
        return p

    def ev():
        cnt["e"] += 1
        return "act" if cnt["e"] % 2 else "dve"

    def prod(dst, lhs, rhs, mask):
        p = npa()
        for q in range(8):
            k.mm(p3(p)[:, q, :], lhs(q), rhs(q))
        if mask is None:
            k.cp(ev(), f3(dst), p3(p))
        else:
            k.tt("dve", f3(dst), p3(p), f3(mask), ALU.mult)

    def views(rb, cl):
        sl = {d: slice(cl[d] * CH, (cl[d] + 1) * CH) for d in range(2)}
        dq = lambda q: q // 4
        chs = lambda buf: (lambda q: buf[rb].v[:, q, sl[dq(q)]])
        tks = lambda buf: (lambda q: buf[rb].v[:, cl[dq(q)], q, :])
        return sl, chs, tks

    def ppart(B, rb, cl, grp):
        sl, chs, tks = views(rb, cl)
        rt, kt, kh, vv = chs(RT), chs(KT), tks(KH), tks(VV)
        prod(B["ArkT"], kt, rt, Min)
        yield
        if dplr:
            at, bt, bh = chs(AT), chs(BT), tks(BH)
            Ls, Ns, AakT, ArbT = B["Ls"], B["Ns"], B["AakT"], B["ArbT"]
            prod(Ns[0], bt, at, Mst)
            prod(Ls[0], at, bt, Mts)
            yield
            prod(AakT, kt, at, Mst)
            prod(ArbT, bt, rt, Min)
            yield
            xc, xn = B["X"]
            xb = B["Xb"]
            for d in range(2):
                k.cp("act", xc.v[:, d * 4:(d + 1) * 4, 0:64], AK[rb].v[:, cl[d], d * 4:(d + 1) * 4, :])
            p = npa()
            for q in range(8):
                k.mm(p3(p)[:, q, :], AakT.v[:, q, :], vv(q))
            k.cp("dve", xc.v[:, :, 64:128], p3(p))
            k.cp("act", f3(xb), f3(xc))
            yield
            px = B["psX"].v[:, :].rr("p (q c) -> p q c", q=8)
            for lv in range(6):
                for q in range(8):
                    k.mm(px[:, q, :], Ns[lv].v[:, q, :], xb.v[:, q, :])
                if lv < 5:
                    if lv < 4:
                        prod(Ls[lv + 1], lambda q: Ns[lv].v[:, q, :], lambda q: Ls[lv].v[:, q, :], None)
                    prod(Ns[lv + 1], lambda q: Ls[lv].v[:, q, :], lambda q: Ns[lv].v[:, q, :], None)
                k.tt("dve", f3(xn), f3(xc), px, ALU.add)
                xc, xn = xn, xc
                k.cp("act", f3(xb), f3(xc))
                yield
            wq = lambda q: xb.v[:, q, 0:64]
            uv = lambda q: xb.v[:, q, 64:128]
            p = npa()
            for q in range(8):
                k.mm(p3(p)[:, q, :], wq(q), ArbT.v[:, q, :])
            for d in range(2):
                k.tt("dve", B["Qe"].v[:, d * 4:(d + 1) * 4, :], p3(p)[:, d * 4:(d + 1) * 4, :],
                     RT[rb].v[:, d * 4:(d + 1) * 4, sl[d]], ALU.add)
            p = npa()
            for q in range(8):
                k.mm(p3(p)[:, q, :], wq(q), bh(q))
            for d in range(2):
                c_abs = grp[d] * NCG + cl[d]
                k.tt("dve", B["GTt"].v[:, d * 4:(d + 1) * 4, :], id8.v[:, d * 4:(d + 1) * 4, :],
                     PCt.v[:, d * 4:(d + 1) * 4, c_abs:c_abs + 1].bc([64, 4, 64]), ALU.mult)
            k.tt("dve", f3(B["GTt"]), f3(B["GTt"]), p3(p), ALU.add)
            yield
        p = npa()
        for q in range(8):
            k.mm(p3(p)[:, q, :], B["ArkT"].v[:, q, :], vv(q), True, not dplr)
            if dplr:
                k.mm(p3(p)[:, q, :], ArbT.v[:, q, :], uv(q), False, True)
        k.cp(ev(), f3(B["Yv"]), p3(p))
        p = npa()
        for q in range(8):
            k.mm(p3(p)[:, q, :], kh(q), vv(q), True, not dplr)
            if dplr:
                k.mm(p3(p)[:, q, :], bh(q), uv(q), False, True)
        k.cp(ev(), f3(B["H"]), p3(p))
        yield

    k.memset("dve", f3(ST[0]), 0.0)
    ngl = T // GT_
    ncg_ctx = CTX // GT_
    fw = list(range(ngl, ngl + ncg_ctx)) + list(range(ngl))
    bw = list(range(ngl + ncg_ctx - 1, ngl - 1, -1)) + list(range(ngl - 1, -1, -1))
    order = {0: fw, 1: bw}
    step = 0
    for gi in range(ngl + ncg_ctx):
        rb = gi % 2
        grp = {d: order[d][gi] for d in range(2)}
        for d in range(2):
            t0 = grp[d] * GT_
            qs = slice(d * 4, (d + 1) * 4)
            chv = lambda nm: S[pre + nm][d, :, t0:t0 + GT_].rearrange("(h j) t -> j h t", j=64)
            tkv = lambda ap: ap[t0:t0 + GT_, :].rearrange("(n t) (h j) -> t n h j", t=CH, j=64)
            k.ld("sp", RT[rb].v[:, qs, :], chv("Rt"))
            k.ld("sp", KT[rb].v[:, qs, :], chv("Kt"))
            k.ld("sp", KH[rb].v[:, :, qs, :], tkv(S[pre + "Kh"][d]))
            k.ld("sp", VV[rb].v[:, :, qs, :], tkv(S["D_V"][d] if dplr else S["C_V"]))
            if dplr:
                k.ld("sp", AT[rb].v[:, qs, :], chv("At"))
                k.ld("sp", BT[rb].v[:, qs, :], chv("Bt"))
                k.ld("sp", BH[rb].v[:, :, qs, :], tkv(S["D_Bh"][d]))
                k.ld("sp", AK[rb].v[:, :, qs, :], tkv(S["D_Atok"][d]))
        cls = [{0: ci, 1: NCG - 1 - ci} for ci in range(NCG)]
        gens = [ppart(sets[ci], rb, cls[ci], grp) for ci in range(NCG)]
        live = list(gens)
        while live:
            for g_ in list(live):
                try:
                    next(g_)
                except StopIteration:
                    live.remove(g_)
        for ci in range(NCG):
            B, cl = sets[ci], cls[ci]
            sl, chs, tks = views(rb, cl)
            Sc, Sn = ST[step % 2], ST[(step + 1) % 2]
            qe = (lambda q: B["Qe"].v[:, q, :]) if dplr else chs(RT)
            gt = (lambda q: B["GTt"].v[:, q, :]) if dplr else (lambda q: GTc.v[:, q, :])
            py = psY[cnt["y"] % 2]
            cnt["y"] += 1
            for q in range(8):
                k.mm(p3(py)[:, q, :], qe(q), Sc.v[:, q, :])
            p = npa()
            for q in range(8):
                k.mm(p3(p)[:, q, :], gt(q), Sc.v[:, q, :])
            k.tt("dve", f3(Sn), p3(p), f3(B["H"]), ALU.add)
            yt = Yt[step % 2]
            k.tt("dve", f3(yt), p3(py), f3(B["Yv"]), ALU.add)
            for d in range(2):
                tt0 = grp[d] * GT_ + cl[d] * CH
                k.stv("sp", S[pre + "Y"][d, tt0:tt0 + CH, :].rearrange("t (h i) -> t h i", i=64),
                      yt.v[:, d * 4:(d + 1) * 4, :])
            step += 1


GN_EPS = 64e-5


def postphase(k, st, cfg):
    S, Wt = cfg["S"], cfg["W"]
    TT = cfg["T"] + cfg["CTX"]
    row = lambda nm, ap: (lambda t: (k.ld("sp", t.v[:, :], ap.partition_broadcast(128)), t)[1])(k.sb(st, [128, 256], F32, nm))
    retg = row("retg", Wt["ret_g"])
    lng = row("lng", Wt["ln_g"])
    lnb = row("lnb", Wt["ln_b"])
    y = [k.sb(st, [128, 4, 64], F32, "y%d" % i) for i in range(2)]
    yc = k.sb(st, [128, 4, 64], F32, "yc")
    sq = k.sb(st, [128, 4, 64], F32, "sq")
    m4 = k.sb(st, [128, 4], F32, "m4")
    v4 = k.sb(st, [128, 4], F32, "v4")
    gc = k.sb(st, [128, 512], F32, "gc")
    gd = k.sb(st, [128, 256], F32, "gd")
    vd = k.sb(st, [128, 4, 64], BF16, "vd")
    vdf = k.sb(st, [128, 4, 64], F32, "vdf")
    bs = k.sb(st, [128, 4], F32, "bs")
    acc = k.sb(st, [128, 256], F32, "acc")
    yn = k.sb(st, [128, 256], F32, "yn")
    i = 0

    def head_norm(yb, g, b, dst):
        k.op("dve", lambda e: e.tensor_reduce(out=m4[:, :], in_=yb[:, :, :], axis=AX.X, op=ALU.add), r=[yb], w=[m4])
        k.ts("dve", m4.v[:, :], m4.v[:, :], 1.0 / 64, None, ALU.mult)
        k.tt("dve", yc.v[:, :, :], yb.v[:, :, :], m4.v[:, :].rr("p (h o) -> p h o", o=1).bc([128, 4, 64]), ALU.subtract)
        k.tt("dve", sq.v[:, :, :], yc.v[:, :, :], yc.v[:, :, :], ALU.mult)
        k.op("dve", lambda e: e.tensor_reduce(out=v4[:, :], in_=sq[:, :, :], axis=AX.X, op=ALU.add), r=[sq], w=[v4])
        k.ts("dve", v4.v[:, :], v4.v[:, :], 1.0 / 64, GN_EPS, ALU.mult, ALU.add)
        k.act(v4.v[:, :], v4.v[:, :], ACT.Sqrt)
        k.recip(v4.v[:, :], v4.v[:, :])
        k.tt("dve", yc.v[:, :, :], yc.v[:, :, :], v4.v[:, :].rr("p (h o) -> p h o", o=1).bc([128, 4, 64]), ALU.mult)
        k.tt("dve", dst, yc.v[:, :, :].rr("p h i -> p (h i)"), g.v[:, :], ALU.mult)
        if b is not None:
            k.tt("dve", dst, dst, b.v[:, :], ALU.add)

    for t0 in range(0, TT, 128):
        tsl = slice(t0, t0 + 128)
        k.ld("sp", gc.v[:, :], S["C_G"][tsl, :])
        for d in range(2):
            yb = y[i % 2]
            i += 1
            k.ld("sp", yb.v[:, :, :], S["C_Y"][d, tsl, :].rearrange("t (h i) -> t h i", i=64))
            head_norm(yb, retg, None, yn.v[:, :])
            if d == 0:
                k.tt("dve", acc.v[:, :], yn.v[:, :], gc.v[:, 0:256], ALU.mult)
            else:
                k.tt("dve", yn.v[:, :], yn.v[:, :], gc.v[:, 256:512], ALU.mult)
                k.tt("dve", acc.v[:, :], acc.v[:, :], yn.v[:, :], ALU.add)
        k.stv("sp", S["O"][tsl, 512:768], acc.v[:, :])
        k.ld("sp", gd.v[:, :], S["D_G"][tsl, :])
        for d in range(2):
            yb = y[i % 2]
            i += 1
            k.ld("sp", yb.v[:, :, :], S["D_Y"][d, tsl, :].rearrange("t (h i) -> t h i", i=64))
            k.ld("sp", vd.v[:, :, :], S["D_V"][d, tsl, :].rearrange("t (h i) -> t h i", i=64))
            k.ld("sp", bs.v[:, :], S["D_BS"][d, tsl, :])
            head_norm(yb, lng, lnb, yn.v[:, :])
            k.tt("dve", vdf.v[:, :, :], vd.v[:, :, :], bs.v[:, :].rr("p (h o) -> p h o", o=1).bc([128, 4, 64]), ALU.mult)
            k.tt("dve", yn.v[:, :], yn.v[:, :], vdf.v[:, :, :].rr("p h i -> p (h i)"), ALU.add)
            if d == 0:
                k.cp("act", acc.v[:, :], yn.v[:, :])
            else:
                k.tt("dve", acc.v[:, :], acc.v[:, :], yn.v[:, :], ALU.add)
        k.tt("dve", acc.v[:, :], acc.v[:, :], gd.v[:, :], ALU.mult)
        k.stv("sp", S["O"][tsl, 768:1024], acc.v[:, :])


def modphase(k, st, cfg):
    cv = k.sb(st, [128, 2, 8], F32, "cv")
    for r in range(2):
        k.ld("sp", cv.v[:, r, :], cfg["cvec"][r].rearrange("(c p) -> p c", p=128), allow_slow_non_contiguous=True)
    sc = k.sb(st, [128, 8, 2], F32, "sc")
    k.act(sc.v[:, :, :].rr("p c r -> p r c"), cv.v[:, :, :], ACT.Silu)
    wm = [k.sb(st, [128, 8, 512], F32, "wm%d" % i) for i in range(2)]
    bm = k.sb(st, [2, 9 * D], F32, "bm")
    k.ld("sp", bm.v[:, :], cfg["b_mod"].partition_broadcast(2))
    ob = k.sb(st, [2, 9 * D], F32, "ob")
    pm = [k.ps(st, [128, 512], F32, "pm%d" % i) for i in range(2)]
    wv = cfg["w_mod"].rearrange("(c p) f -> p c f", p=128)
    for j in range(18):
        w = wm[j % 2]
        for c in range(8):
            k.ld("sp", w.v[:, c, :], wv[:, c, j * 512:(j + 1) * 512])
        p = pm[j % 2]
        for c in range(8):
            k.mm(p.v[0:2, :], sc.v[:, c, :], w.v[:, c, :], c == 0, c == 7)
        k.tt("dve", ob.v[:, j * 512:(j + 1) * 512], p.v[0:2, :], bm.v[:, j * 512:(j + 1) * 512], ALU.add)
    k.stv("sp", cfg["mod"], ob.v[:, :])


def build_program(T, CTX, L, stop_after=None):
    TT = T + CTX
    nc = bass.Bass("TRN2", target_bir_lowering=False, dynamic_dma_scratch_size=8192)

    def din(name, shape, dt=F32):
        return nc.dram_tensor(name, list(shape), dt, kind="ExternalInput").ap()

    def scr(name, shape, dt=F32):
        return nc.dram_tensor(name, list(shape), dt, kind="Internal").ap()

    I = dict(
        xin=din("xin", [TT, D]), cvec=din("cvec", [2, D]),
        w_mod=din("w_mod", [L, D, 9 * D]), b_mod=din("b_mod", [L, 9 * D]), norm_g=din("norm_g", [L, 3, D]),
        ffn_w_in=din("ffn_w_in", [L, 2, D, 2 * DFF]), ffn_w_out=din("ffn_w_out", [L, 2, DFF, D]),
        w_in=din("w_in", [L, D, 3456]), w_out=din("w_out", [L, D, D]), final_g=din("final_g", [D]),
        w_sw=din("w_sw", [L, D, 1280]), w_d=din("w_d", [L, 2, D, 896]), pcols=din("pcols", [L, 128, 24]),
        mu=din("mu", [L, 2, 896]), w2=din("w2", [L, 2, 64, 256]), a2=din("a2", [L, 2, 64, 256]),
        g2=din("g2", [L, 128, 256]), ret_g=din("ret_g", [L, 256]), ln_g=din("ln_g", [L, 256]),
        ln_b=din("ln_b", [L, 256]), sink=din("sink", [L, 4]),
    )
    Cn = dict(
        ident=din("ident", [128, 128]), bd64=din("bd64", [128, 128]), e2=din("e2", [128, 2]),
        ropeA_c=din("ropeA_c", [128, T]), ropeA_s=din("ropeA_s", [128, T]),
        ropeS_c=din("ropeS_c", [128, T]), ropeS_s=din("ropeS_s", [128, T]),
        ret_tab=din("ret_tab", [128, 2, 2, 3, CH]), ret_pc=din("ret_pc", [64, 8]),
        m_st=din("m_st", [64, 512]), m_ts=din("m_ts", [64, 512]), m_in=din("m_in", [64, 512]),
        id8=din("id8", [64, 512]), mprev=din("mprev", [128, 128]), mnext=din("mnext", [128, 128]),
    )
    out = nc.dram_tensor("out", [T, D], F32, kind="ExternalOutput").ap()
    dbg = stop_after is not None
    mk = (lambda name, shape, dt=F32: nc.dram_tensor(name, list(shape), dt, kind="ExternalOutput").ap()) if dbg else scr
    S = dict(
        xres=mk("xres", [TT, D]), mod=mk("modr", [2, 9 * D]), hT=mk("hT", [D, TT], BF16),
        QA=mk("QA", [256, TT], BF16), KA=mk("KA", [128, TT], BF16), VA=mk("VA", [TT, 128], BF16),
        QB=mk("QB", [256, TT], BF16), KB=mk("KB", [128, TT], BF16), VB=mk("VB", [TT, 128], BF16),
        O=mk("O", [TT, 1024]),
        C_Rt=mk("C_Rt", [2, 256, TT], BF16), C_Kt=mk("C_Kt", [2, 256, TT], BF16), C_Kh=mk("C_Kh", [2, TT, 256], BF16),
        C_V=mk("C_V", [TT, 256], BF16), C_G=mk("C_G", [TT, 512]), C_Y=mk("C_Y", [2, TT, 256]),
        D_Rt=mk("D_Rt", [2, 256, TT], BF16), D_Kt=mk("D_Kt", [2, 256, TT], BF16), D_At=mk("D_At", [2, 256, TT], BF16),
        D_Bt=mk("D_Bt", [2, 256, TT], BF16), D_Kh=mk("D_Kh", [2, TT, 256], BF16), D_Bh=mk("D_Bh", [2, TT, 256], BF16),
        D_Atok=mk("D_Atok", [2, TT, 256], BF16), D_V=mk("D_V", [2, TT, 256], BF16), D_Y=mk("D_Y", [2, TT, 256]),
        D_PC=mk("D_PC", [2, 256, TT // CH]), D_G=mk("D_G", [TT, 256]), D_BS=mk("D_BS", [2, TT, 4]),
    )
    k = K(nc)
    stages = []

    def phase(name, fn, cfg):
        if stop_after is not None and stop_after in stages:
            return
        with contextlib.ExitStack() as st:
            fn(k, st, cfg)
            k.barrier()
        stages.append(name)
        k.marks.append((name, {e.name: e.n for e in k.E.values()}))

    with k.stack:
        segs_all = [(0, T, 0), (T, CTX, 1)]
        for l in range(L):
            last = l == L - 1
            W = dict(w_in=I["w_in"][l], w_sw=I["w_sw"][l], w_d=I["w_d"][l], mu=I["mu"][l], pcols=I["pcols"][l],
                     w2=I["w2"][l], a2=I["a2"][l], g2=I["g2"][l], ret_g=I["ret_g"][l], ln_g=I["ln_g"][l],
                     ln_b=I["ln_b"][l], sink=I["sink"][l])
            base = dict(S=S, C=Cn, W=W, T=T, CTX=CTX)
            phase("mod%d" % l, modphase, dict(cvec=I["cvec"], w_mod=I["w_mod"][l], b_mod=I["b_mod"][l], mod=S["mod"]))
            phase("f1_%d" % l, rowphase, dict(
                xin=I["xin"] if l == 0 else S["xres"], xout=S["xres"], segs=segs_all, mod=S["mod"], ident=Cn["ident"],
                ffn=dict(w_in=I["ffn_w_in"][l, 0], w_out=I["ffn_w_out"][l, 0], g=I["norm_g"][l, 0], shift=0, scale=1,
                         gate=2),
                post=dict(kind="hT", g=I["norm_g"][l, 1], shift=3, scale=4, hT=S["hT"])))
            phase("proj%d" % l, projphase, base)
            phase("attn%d" % l, attnphase, dict(base, need_ctx=not last))
            phase("scanC%d" % l, scanphase, dict(base, dplr=False))
            phase("scanD%d" % l, scanphase, dict(base, dplr=True))
            phase("post%d" % l, postphase, base)
            phase("f2_%d" % l, rowphase, dict(
                xin=S["xres"], xout=S["xres"], segs=segs_all if not last else [(0, T, 0)], mod=S["mod"],
                ident=Cn["ident"],
                pre=dict(o=S["O"], w_out=I["w_out"][l], gate=5),
                ffn=dict(w_in=I["ffn_w_in"][l, 1], w_out=I["ffn_w_out"][l, 1], g=I["norm_g"][l, 2], shift=6, scale=7,
                         gate=8),
                post=dict(kind="final", g=I["final_g"], out=out) if last else None))
    return nc, k


def _consts(T):
    c = {}
    c["ident"] = np.eye(128, dtype=np.float32)
    bd = np.zeros((128, 128), np.float32)
    bd[:64, :64] = 1
    bd[64:, 64:] = 1
    c["bd64"] = bd
    e2 = np.zeros((128, 2), np.float32)
    e2[:64, 0] = 1
    e2[64:, 1] = 1
    c["e2"] = e2
    t = np.arange(T)
    row = (t // 64).astype(np.float32)
    col = (t % 64).astype(np.float32)
    inv16 = (1.0 / (np.float32(10000.0) ** (np.arange(0, 32, 2, dtype=np.float32) / np.float32(32)))).astype(np.float32)
    inv32 = (1.0 / (np.float32(10000.0) ** (np.arange(0, 64, 2, dtype=np.float32) / np.float32(64)))).astype(np.float32)
    ca = np.zeros((64, T), np.float32)
    sa = np.zeros((64, T), np.float32)
    for blk, pos in ((0, row), (32, col)):
        ang = pos[None, :] * inv16[:, None]
        ca[blk:blk + 16] = np.cos(ang)
        ca[blk + 16:blk + 32] = np.cos(ang)
        sa[blk:blk + 16] = -np.sin(ang)
        sa[blk + 16:blk + 32] = np.sin(ang)
    ang = t.astype(np.float32)[None, :] * inv32[:, None]
    cs = np.concatenate([np.cos(ang), np.cos(ang)], 0).astype(np.float32)
    ss = np.concatenate([-np.sin(ang), np.sin(ang)], 0).astype(np.float32)
    c["ropeA_c"] = np.concatenate([ca, ca], 0)
    c["ropeA_s"] = np.concatenate([sa, sa], 0)
    c["ropeS_c"] = np.concatenate([cs, cs], 0)
    c["ropeS_s"] = np.concatenate([ss, ss], 0)
    lg = np.log1p(-np.exp2(-5.0 - np.arange(4, dtype=np.float64)))
    i = np.arange(CH, dtype=np.float64)
    rt = np.zeros((128, 2, 2, 3, CH), np.float64)
    for p in range(128):
        for g in range(2):
            l_ = lg[2 * g + p // 64]
            for d in range(2):
                csum = (i + 1) * l_ if d == 0 else (CH - i) * l_
                rt[p, g, d, 0] = np.exp(csum)
                rt[p, g, d, 1] = np.exp(-csum) / 8.0
                rt[p, g, d, 2] = np.exp(CH * l_ - csum) / 8.0
    c["ret_tab"] = rt.astype(np.float32)
    c["ret_pc"] = np.tile(np.exp(CH * lg)[None, :], (64, 2)).astype(np.float32)
    s_ = np.arange(64)[:, None]
    t_ = np.arange(64)[None, :]
    lt = (s_ < t_).astype(np.float32)
    gt = (s_ > t_).astype(np.float32)
    eye = np.eye(64, dtype=np.float32)
    c["m_st"] = np.concatenate([lt] * 4 + [gt] * 4, 1)
    c["m_ts"] = np.concatenate([gt] * 4 + [lt] * 4, 1)
    c["m_in"] = np.concatenate([lt + eye] * 4 + [gt + eye] * 4, 1)
    c["id8"] = np.concatenate([eye] * 8, 1)
    j = np.arange(128)[:, None]
    ii = np.arange(128)[None, :]
    c["mprev"] = (ii <= j).astype(np.float32)
    c["mnext"] = (j <= ii).astype(np.float32)
    return c


def _host_maps(inp, T, CTX, L):
    f = lambda a: np.ascontiguousarray(np.asarray(a, dtype=np.float32))
    w_in = f(inp["w_in"])
    pa = np.concatenate([np.arange(16, 32), np.arange(0, 16), np.arange(48, 64), np.arange(32, 48)])
    psq = np.concatenate([np.arange(32, 64), np.arange(0, 32)])
    cols = []
    for base, nh, perm in ((0, 4, pa), (256, 2, pa), (512, 4, pa), (768, 2, pa), (1024, 4, psq), (1280, 4, psq)):
        for h in range(nh):
            cols.append(base + h * 64 + perm)
    cols = np.concatenate(cols)
    w_sw = np.ascontiguousarray(w_in[:, :, cols])
    dcols = [np.concatenate([np.arange(2304, 3072), np.arange(3200, 3264), np.arange(3328, 3392)]),
             np.concatenate([np.arange(2304, 3072), np.arange(3264, 3328), np.arange(3392, 3456)])]
    w_d = np.ascontiguousarray(np.stack([w_in[:, :, dc] for dc in dcols], 1))
    qkg = f(inp["qk_norm_g"])
    pcols = np.zeros((L, 128, 24), np.float32)
    two = lambda v: np.ascontiguousarray(v.reshape(2, 128).T)
    for l in range(L):
        for qk in range(2):
            g = qkg[l, qk]
            pcols[l, :, 2 * qk] = np.tile(g, 2)
            pcols[l, :, 2 * qk + 1] = np.tile(g[pa], 2)
        for d in range(2):
            pcols[l, :, 4 + 2 * d:6 + 2 * d] = two(f(inp["rwkv_w0"])[l, d])
            pcols[l, :, 8 + 2 * d:10 + 2 * d] = two(f(inp["rwkv_a0"])[l, d])
            pcols[l, :, 16 + 2 * d:18 + 2 * d] = two(f(inp["rwkv_rho"])[l, d].reshape(256))
        pcols[l, :, 12:14] = two(f(inp["rwkv_k_k"])[l])
        pcols[l, :, 14:16] = two(f(inp["rwkv_k_a"])[l])
    shared = dict(
        w_mod=f(inp["w_mod"]), b_mod=f(inp["b_mod"]), norm_g=f(inp["norm_g"]), ffn_w_in=f(inp["ffn_w_in"]),
        ffn_w_out=f(inp["ffn_w_out"]), w_in=w_in, w_out=f(inp["w_out"]), final_g=f(inp["final_norm_g"]),
        w_sw=w_sw, w_d=w_d, pcols=pcols, mu=f(inp["rwkv_mu"]), w2=f(inp["rwkv_w2"]), a2=f(inp["rwkv_a2"]),
        g2=f(inp["rwkv_g2"]), ret_g=f(inp["ret_norm_g"]), ln_g=f(inp["rwkv_ln_g"]), ln_b=f(inp["rwkv_ln_b"]),
        sink=f(inp["attn_sink"]))
    shared.update(_consts(T))
    x, c, ctx, c_ctx = f(inp["x"]), f(inp["c"]), f(inp["ctx"]), f(inp["c_ctx"])
    maps = []
    for b in range(x.shape[0]):
        m = dict(shared)
        m["xin"] = np.ascontiguousarray(np.concatenate([x[b], ctx[b]], 0))
        m["cvec"] = np.ascontiguousarray(np.stack([c[b], c_ctx], 0))
        maps.append(m)
    return maps


_PROG = {}


def kernel(**inputs):
    x = np.asarray(inputs["x"])
    B, T, _ = x.shape
    CTX = np.asarray(inputs["ctx"]).shape[1]
    L = np.asarray(inputs["w_in"]).shape[0]
    key = (T, CTX, L)
    if key not in _PROG:
        _PROG[key] = build_program(T, CTX, L)[0]
    nc = _PROG[key]
    maps = _host_maps(inputs, T, CTX, L)
    res = run_bass_kernel_spmd(nc, maps, core_ids=list(range(B)))
    return np.stack([np.asarray(r["out"], dtype=np.float32) for r in res.results], 0)
```

```python
import contextlib
import numpy as np
import concourse.bass as bass
import concourse.mybir as mybir
from concourse.bass_utils import run_bass_kernel_spmd

ACT = mybir.ActivationFunctionType
ALU = mybir.AluOpType
AX = mybir.AxisListType
F32 = mybir.dt.float32
BF16 = mybir.dt.bfloat16

D = 1024
DFF = 2816
NMOD = 9
RMS_EPS = 1e-6
STRICT = True


class PSem:
    def __init__(self, h):
        self.h = h
        self.val = 0


class Eng:
    def __init__(self, name, h, sem):
        self.name, self.h, self.sem = name, h, sem
        self.n = 0
        self.seen = {}
        self.dseen = {}


class Buf:
    def __init__(self, k, t, name):
        self.k, self.t, self.name = k, t, name
        self.w = None
        self.r = {}
        self.psem = None
        self.dma_w = 0
        self.dma_rw = 0

    def __getitem__(self, key):
        return self.t[key]

    @property
    def v(self):
        return _VA(self)


class View:
    def __init__(self, buf, ap):
        self.buf, self.ap = buf, ap

    def rr(self, pat, **kw):
        return View(self.buf, self.ap.rearrange(pat, **kw))

    def bc(self, shape):
        return View(self.buf, self.ap.to_broadcast(list(shape)))

    def __getitem__(self, key):
        return View(self.buf, self.ap[key])


class _VA:
    def __init__(self, b):
        self.b = b

    def __getitem__(self, key):
        return View(self.b, self.b.t[key])


def _bufs(*xs):
    out = []
    for x in xs:
        if isinstance(x, View) and x.buf not in out:
            out.append(x.buf)
    return out


def _ap(x):
    return x.ap if isinstance(x, View) else x


class K:
    def __init__(self, nc):
        self.nc = nc
        self.stack = contextlib.ExitStack()
        self.E = {}
        for name, h in (("pe", nc.tensor), ("act", nc.scalar), ("dve", nc.vector),
                        ("pool", nc.gpsimd), ("sp", nc.sync)):
            sem = self.stack.enter_context(nc.semaphore("s_" + name))
            self.E[name] = Eng(name, h, sem)
        self.free_psems = []
        self.n_psems = 0
        self.all_psems = []
        self.bufs = []
        self.uid = 0
        self.ninstr = 0
        self.marks = []

    def _psem(self):
        if self.free_psems:
            return self.free_psems.pop()
        self.n_psems += 1
        h = self.stack.enter_context(self.nc.semaphore("d%d" % self.n_psems))
        p = PSem(h)
        self.all_psems.append(p)
        return p

    def sb(self, st, shape, dtype, name):
        self.uid += 1
        t = st.enter_context(self.nc.sbuf_tensor("%s_%d" % (name, self.uid), list(shape), dtype))
        b = Buf(self, t, name)
        st.callback(self._release, b)
        return b

    def ps(self, st, shape, dtype, name):
        self.uid += 1
        t = st.enter_context(self.nc.psum_tensor("%s_%d" % (name, self.uid), list(shape), dtype))
        return Buf(self, t, name)

    def _release(self, b):
        if b.psem is not None and not getattr(b.psem, "sw", False):
            self.free_psems.append(b.psem)
        b.psem = None

    def _wait_eng(self, E, X, n):
        if E.seen.get(X.name, 0) >= n:
            return
        E.h.wait_ge(X.sem, n)
        E.seen[X.name] = n
        self.ninstr += 1

    def _wait_d(self, E, p, v):
        if v <= 0 or E.dseen.get(id(p), 0) >= v:
            return
        E.h.wait_ge(p.h, v)
        E.dseen[id(p)] = v
        self.ninstr += 1

    def _deps(self, E, r, w):
        for b in r:
            if b.w is not None:
                self._wait_eng(E, b.w[0], b.w[1])
            if b.psem is not None:
                self._wait_d(E, b.psem, b.dma_w)
        for b in w:
            if b.w is not None and (STRICT or b.w[0] is not E) and not (E.name == "pe" and b.w[0] is E):
                self._wait_eng(E, b.w[0], b.w[1])
            for X, n in b.r.items():
                if STRICT or X is not E:
                    self._wait_eng(E, X, n)
            if b.psem is not None:
                self._wait_d(E, b.psem, b.dma_rw)

    def op(self, eng, fn, r=(), w=()):
        E = self.E[eng]
        self._deps(E, r, w)
        ins = fn(E.h)
        E.n += 1
        ins.then_inc(E.sem, 1)
        self.ninstr += 1
        for b in w:
            b.w = (E, E.n)
            b.r = {}
        for b in r:
            if b.w is None or b.w[0] is not E or b.w[1] != E.n:
                b.r[E] = E.n
        return ins

    def dma(self, q, out, in_, r=(), w=(), **kw):
        Q = self.E[q]
        self._deps(Q, r, w)
        anchor = w[0] if w else r[0]
        if anchor.psem is None:
            if q == "pool":
                self.n_psems += 1
                anchor.psem = PSem(self.stack.enter_context(self.nc.semaphore("w%d" % self.n_psems)))
                anchor.psem.sw = True
                self.all_psems.append(anchor.psem)
            else:
                anchor.psem = self._psem()
        p = anchor.psem
        assert getattr(p, "sw", False) == (q == "pool"), "buffer mixes SW and HW DGE DMAs: " + anchor.name
        Q.h.dma_start(out=out, in_=in_, **kw).then_inc(p.h, 16)
        p.val += 16
        self.ninstr += 1
        anchor.dma_rw = p.val
        if w:
            anchor.dma_w = p.val
            anchor.w = None
            anchor.r = {}

    def mm(self, out, lhsT, rhs, start=True, stop=True):
        return self.op("pe", lambda e: e.matmul(out.ap, lhsT.ap, rhs.ap, start=start, stop=stop),
                       r=_bufs(lhsT, rhs), w=[out.buf])

    def tr(self, out, in_, ident):
        return self.op("pe", lambda e: e.transpose(out.ap, in_.ap, ident.ap), r=_bufs(in_, ident), w=[out.buf])

    def tt(self, eng, out, in0, in1, op):
        return self.op(eng, lambda e: e.tensor_tensor(out=out.ap, in0=in0.ap, in1=in1.ap, op=op),
                       r=_bufs(in0, in1), w=[out.buf])

    def ts(self, eng, out, in0, s1, s2, op0, op1=None):
        kw = {} if op1 is None else dict(op1=op1)
        return self.op(eng, lambda e: e.tensor_scalar(out=out.ap, in0=in0.ap, scalar1=_ap(s1), scalar2=_ap(s2),
                                                      op0=op0, **kw), r=_bufs(in0, s1, s2), w=[out.buf])

    def stt(self, eng, out, in0, scalar, in1, op0, op1):
        return self.op(eng, lambda e: e.scalar_tensor_tensor(out=out.ap, in0=in0.ap, scalar=_ap(scalar), in1=in1.ap,
                                                             op0=op0, op1=op1), r=_bufs(in0, scalar, in1), w=[out.buf])

    def act(self, out, in_, func, bias=None, scale=None, accum=None):
        kw = {}
        if bias is not None:
            kw["bias"] = _ap(bias)
        if scale is not None:
            kw["scale"] = _ap(scale)
        w = [out.buf]
        if accum is not None:
            kw["accum_out"] = accum.ap
            w.append(accum.buf)
        return self.op("act", lambda e: e.activation(out=out.ap, in_=in_.ap, func=func, **kw),
                       r=_bufs(in_, bias, scale), w=w)

    def cp(self, eng, out, in_):
        if eng == "act":
            return self.op("act", lambda e: e.copy(out=out.ap, in_=in_.ap), r=[in_.buf], w=[out.buf])
        return self.op(eng, lambda e: e.tensor_copy(out=out.ap, in_=in_.ap), r=[in_.buf], w=[out.buf])

    def memset(self, eng, out, val):
        return self.op(eng, lambda e: e.memset(out.ap, val), w=[out.buf])

    def recip(self, out, in_):
        return self.op("dve", lambda e: e.reciprocal(out=out.ap, in_=in_.ap), r=[in_.buf], w=[out.buf])

    def ld(self, q, out, dram, **kw):
        return self.dma(q, out.ap, dram, w=[out.buf], **kw)

    def stv(self, q, dram, in_, **kw):
        return self.dma(q, dram, in_.ap, r=[in_.buf], **kw)

    def barrier(self):
        for E in self.E.values():
            for X in self.E.values():
                if X is not E and X.n > 0:
                    self._wait_eng(E, X, X.n)
            for p in self.all_psems:
                self._wait_d(E, p, p.val)


def mm(k, out_ap, lhsT_ap, rhs_ap, start, stop, r, w):
    return k.op("pe", lambda e: e.matmul(out_ap, lhsT_ap, rhs_ap, start=start, stop=stop), r=r, w=w)


def rowphase(k, st, cfg):
    nc = k.nc
    ffn = cfg["ffn"]
    pre = cfg.get("pre")
    post = cfg.get("post")
    w1 = k.sb(st, [128, 8, 2 * DFF], BF16, "w1")
    w2 = k.sb(st, [128, 22, D], BF16, "w2")
    w1v = ffn["w_in"].rearrange("(c p) f -> p c f", p=128)
    for c in range(8):
        for hlf in range(4):
            f0 = hlf * 1408
            k.dma("pool", w1[:, c, f0:f0 + 1408], w1v[:, c, f0:f0 + 1408], w=[w1])
    w2v = ffn["w_out"].rearrange("(c p) d -> p c d", p=128)
    for c in range(22):
        k.dma("pool", w2[:, c, :], w2v[:, c, :], w=[w2])
    if pre is not None:
        wo = k.sb(st, [128, 8, D], BF16, "wo")
        wov = pre["w_out"].rearrange("(c p) d -> p c d", p=128)
        for c in range(8):
            k.dma("pool", wo[:, c, :], wov[:, c, :], w=[wo])

    identf = k.sb(st, [128, 128], F32, "identf")
    k.dma("sp", identf[:, :], cfg["ident"], w=[identf])
    xt = [k.sb(st, [128, D], F32, "xt%d" % i) for i in range(4)]
    hT = k.sb(st, [128, 8, 512], BF16, "hT")
    gT = k.sb(st, [128, 22, 512], BF16, "gT")
    sg = k.sb(st, [128, 512], BF16, "sg")
    tmp = k.sb(st, [128, D], F32, "tmp")
    ssq = k.sb(st, [128, 1], F32, "ssq")
    rinv = k.sb(st, [128, 1], F32, "rinv")
    ps_u = [k.ps(st, [128, 512], F32, "psu%d" % i) for i in range(4)]
    ps_y = [k.ps(st, [128, 512], F32, "psy%d" % i) for i in range(2)]
    ps_t = k.ps(st, [128, D], F32, "pst")
    Ac = k.sb(st, [128, 8], F32, "Ac")
    Bc = k.sb(st, [128, 8], F32, "Bc")
    Gc = k.sb(st, [128, 8], F32, "Gc")
    A2c = k.sb(st, [128, 8], F32, "A2c")
    B2c = k.sb(st, [128, 8], F32, "B2c")
    G = k.sb(st, [128, D], F32, "G")
    PG = k.sb(st, [128, D], F32, "PG") if pre is not None else None
    gvec2 = None
    if post is not None and post["kind"] == "final":
        gvec2 = k.sb(st, [128, D], F32, "gvec2")
        k.dma("sp", gvec2[:, :], post["g"].partition_broadcast(128), w=[gvec2])

    def col(ap1d):
        return ap1d.rearrange("(c p) -> p c", p=128)

    def load_cols(dst, ap1d):
        k.dma("sp", dst[:, :], col(ap1d), w=[dst], allow_slow_non_contiguous=True)

    def setup_mod(mr):
        load_cols(Gc, ffn["g"])
        load_cols(Ac, cfg["mod"][mr, ffn["scale"] * D:(ffn["scale"] + 1) * D])
        load_cols(Bc, cfg["mod"][mr, ffn["shift"] * D:(ffn["shift"] + 1) * D])
        k.op("dve", lambda e: e.scalar_tensor_tensor(out=Ac[:, :], in0=Ac[:, :], scalar=1.0, in1=Gc[:, :],
                                                     op0=ALU.add, op1=ALU.mult), r=[Ac, Gc], w=[Ac])
        if post is not None and post["kind"] == "hT":
            load_cols(Gc, post["g"])
            load_cols(A2c, cfg["mod"][mr, post["scale"] * D:(post["scale"] + 1) * D])
            load_cols(B2c, cfg["mod"][mr, post["shift"] * D:(post["shift"] + 1) * D])
            k.op("dve", lambda e: e.scalar_tensor_tensor(out=A2c[:, :], in0=A2c[:, :], scalar=1.0, in1=Gc[:, :],
                                                         op0=ALU.add, op1=ALU.mult), r=[A2c, Gc], w=[A2c])
        k.dma("sp", G[:, :], cfg["mod"][mr, ffn["gate"] * D:(ffn["gate"] + 1) * D].partition_broadcast(128), w=[G])
        k.op("dve", lambda e: e.tensor_scalar(out=G[:, :], in0=G[:, :], scalar1=0.5, scalar2=None, op0=ALU.mult),
             r=[G], w=[G])
        if pre is not None:
            k.dma("sp", PG[:, :], cfg["mod"][mr, pre["gate"] * D:(pre["gate"] + 1) * D].partition_broadcast(128),
                  w=[PG])

    def rms_rinv(xb):
        k.op("act", lambda e: e.activation(out=tmp[:, :], in_=xb[:, :], func=ACT.Square, accum_out=ssq[:, :]),
             r=[xb], w=[tmp, ssq])
        k.op("dve", lambda e: e.tensor_scalar(out=rinv[:, :], in0=ssq[:, :], scalar1=1.0 / D, scalar2=RMS_EPS,
                                              op0=ALU.mult, op1=ALU.add), r=[ssq], w=[rinv])
        k.op("act", lambda e: e.activation(out=rinv[:, :], in_=rinv[:, :], func=ACT.Sqrt), r=[rinv], w=[rinv])
        k.op("dve", lambda e: e.reciprocal(out=rinv[:, :], in_=rinv[:, :]), r=[rinv], w=[rinv])

    def norm_mod_T(xb, A, B, dstT, s):
        rms_rinv(xb)
        k.op("dve", lambda e: e.tensor_scalar(out=tmp[:, :], in0=xb[:, :], scalar1=rinv[:, 0:1], scalar2=None,
                                              op0=ALU.mult), r=[xb, rinv], w=[tmp])
        for c in range(8):
            k.op("pe", lambda e: e.transpose(ps_t[:, c * 128:(c + 1) * 128], tmp[:, c * 128:(c + 1) * 128],
                                             identf[:, :]), r=[tmp, identf], w=[ps_t])
        for c in range(8):
            if c % 2 == 0:
                k.op("dve", lambda e: e.tensor_scalar(out=dstT[:, c, s * 128:(s + 1) * 128],
                                                      in0=ps_t[:, c * 128:(c + 1) * 128], scalar1=A[:, c:c + 1],
                                                      scalar2=B[:, c:c + 1], op0=ALU.mult, op1=ALU.add),
                     r=[ps_t, A, B], w=[dstT])
            else:
                k.op("act", lambda e: e.activation(out=dstT[:, c, s * 128:(s + 1) * 128],
                                                   in_=ps_t[:, c * 128:(c + 1) * 128], func=ACT.Identity,
                                                   scale=A[:, c:c + 1], bias=B[:, c:c + 1]),
                     r=[ps_t, A, B], w=[dstT])

    ui = 0
    yi = 0
    for (tok0, ntok, mr) in cfg["segs"]:
        setup_mod(mr)
        for t0 in range(tok0, tok0 + ntok, 512):
            n = min(512, tok0 + ntok - t0)
            ns = n // 128
            for s in range(ns):
                k.dma("sp", xt[s][:, :], cfg["xin"][t0 + s * 128:t0 + (s + 1) * 128, :], w=[xt[s]])
            if pre is not None:
                for s in range(ns):
                    k.dma("sp", tmp[:, :], pre["o"][t0 + s * 128:t0 + (s + 1) * 128, :], w=[tmp])
                    for c in range(8):
                        k.op("pe", lambda e: e.transpose(ps_t[:, c * 128:(c + 1) * 128], tmp[:, c * 128:(c + 1) * 128],
                                                         identf[:, :]), r=[tmp, identf], w=[ps_t])
                    k.op("act", lambda e: e.copy(out=hT[:, :, s * 128:(s + 1) * 128],
                                                 in_=ps_t[:, :].rearrange("p (c t) -> p c t", c=8)), r=[ps_t], w=[hT])
                for s in range(ns):
                    for hf in range(2):
                        py = ps_y[yi % 2]
                        yi += 1
                        for c in range(8):
                            mm(k, py[:, :], hT[:, c, s * 128:(s + 1) * 128], wo[:, c, hf * 512:(hf + 1) * 512],
                               c == 0, c == 7, r=[hT, wo], w=[py])
                        sl = slice(hf * 512, (hf + 1) * 512)
                        k.op("dve", lambda e, py=py, sl=sl: e.tensor_tensor(out=tmp[:, sl], in0=py[:, :],
                                                                             in1=PG[:, sl], op=ALU.mult),
                             r=[py, PG], w=[tmp])
                        k.op("dve", lambda e, s=s, sl=sl: e.tensor_tensor(out=xt[s][:, sl], in0=xt[s][:, sl],
                                                                            in1=tmp[:, sl], op=ALU.add),
                             r=[xt[s], tmp], w=[xt[s]])
            for s in range(ns):
                norm_mod_T(xt[s], Ac, Bc, hT, s)
            for fc in range(22 if cfg.get("stage", 9) >= 2 else 0):
                p1 = ps_u[ui % 4]
                p2 = ps_u[(ui + 1) % 4]
                ui += 2
                for c in range(8):
                    mm(k, p1[:, 0:n], w1[:, c, fc * 128:(fc + 1) * 128], hT[:, c, 0:n], c == 0, c == 7,
                       r=[w1, hT], w=[p1])
                for c in range(8):
                    mm(k, p2[:, 0:n], w1[:, c, DFF + fc * 128:DFF + (fc + 1) * 128], hT[:, c, 0:n], c == 0, c == 7,
                       r=[w1, hT], w=[p2])
                k.op("act", lambda e, p1=p1: e.activation(out=sg[:, 0:n], in_=p1[:, 0:n], func=ACT.Silu),
                     r=[p1], w=[sg])
                k.op("dve", lambda e, p2=p2, fc=fc: e.tensor_tensor(out=gT[:, fc, 0:n], in0=sg[:, 0:n],
                                                                     in1=p2[:, 0:n], op=ALU.mult),
                     r=[sg, p2], w=[gT])
            for s in range(ns if cfg.get("stage", 9) >= 3 else 0):
                for hf in range(2):
                    py = ps_y[yi % 2]
                    yi += 1
                    for fc in range(22):
                        mm(k, py[:, :], gT[:, fc, s * 128:(s + 1) * 128], w2[:, fc, hf * 512:(hf + 1) * 512],
                           fc == 0, fc == 21, r=[gT, w2], w=[py])
                    sl = slice(hf * 512, (hf + 1) * 512)
                    k.op("dve", lambda e, py=py, sl=sl: e.tensor_tensor(out=tmp[:, sl], in0=py[:, :], in1=G[:, sl],
                                                                         op=ALU.mult), r=[py, G], w=[tmp])
                    k.op("dve", lambda e, s=s, sl=sl: e.tensor_tensor(out=xt[s][:, sl], in0=xt[s][:, sl],
                                                                        in1=tmp[:, sl], op=ALU.add),
                         r=[xt[s], tmp], w=[xt[s]])
            for s in range(ns):
                r0 = t0 + s * 128
                if post is not None and post["kind"] == "final":
                    if mr == 0:
                        rms_rinv(xt[s])
                        k.op("dve", lambda e, s=s: e.scalar_tensor_tensor(out=tmp[:, :], in0=xt[s][:, :],
                                                                           scalar=rinv[:, 0:1], in1=gvec2[:, :],
                                                                           op0=ALU.mult, op1=ALU.mult),
                             r=[xt[s], rinv, gvec2], w=[tmp])
                        k.dma("sp", post["out"][r0:r0 + 128, :], tmp[:, :], r=[tmp])
                    continue
                k.dma("sp", cfg["xout"][r0:r0 + 128, :], xt[s][:, :], r=[xt[s]])
                if post is not None and post["kind"] == "hT":
                    norm_mod_T(xt[s], A2c, B2c, hT, s)
            if post is not None and post["kind"] == "hT":
                for c in range(8):
                    k.dma("sp", post["hT"][c * 128:(c + 1) * 128, t0:t0 + n], hT[:, c, 0:n], r=[hT])


CH = 64
DECAY_SCALE = 0.6065306597126334


def projphase(k, st, cfg):
    S = cfg["S"]
    C = cfg["C"]
    Wt = cfg["W"]
    T, CTX = cfg["T"], cfg["CTX"]
    hTd = S["hT"]
    wabc = k.sb(st, [128, 8, 2304], BF16, "wabc")
    wgd = k.sb(st, [128, 8, 128], BF16, "wgd")
    wsw = k.sb(st, [128, 8, 1280], BF16, "wsw")
    wv = Wt["w_in"].rearrange("(c p) f -> p c f", p=128)
    for c in range(8):
        k.ld("pool", wabc.v[:, c, 0:1152], wv[:, c, 0:1152])
        k.ld("pool", wabc.v[:, c, 1152:2304], wv[:, c, 1152:2304])
        k.ld("pool", wgd.v[:, c, :], wv[:, c, 3072:3200])
        k.ld("pool", wsw.v[:, c, :], Wt["w_sw"].rearrange("(c p) f -> p c f", p=128)[:, c, :])
    wda = [k.sb(st, [128, 8, 896], BF16, "wda%d" % d) for d in range(2)]
    wdb = [k.sb(st, [128, 8, 896], BF16, "wdb%d" % d) for d in range(2)]
    with contextlib.ExitStack() as st2:
        wst = k.sb(st2, [128, 8, 896], F32, "wst")
        mub = k.sb(st2, [128, 896], F32, "mub")
        wtm = k.sb(st2, [128, 8, 896], F32, "wtm")
        for d in range(2):
            for c in range(8):
                k.ld("sp", wst.v[:, c, :], Wt["w_d"][d].rearrange("(c p) f -> p c f", p=128)[:, c, :])
            k.ld("sp", mub.v[:, :], Wt["mu"][d].partition_broadcast(128))
            for c in range(8):
                k.tt("dve", wtm.v[:, c, :], wst.v[:, c, :], mub.v[:, :], ALU.mult)
            k.cp("act", wda[d].v[:, :, :], wtm.v[:, :, :])
            k.tt("dve", wdb[d].v[:, :, :], wst.v[:, :, :], wtm.v[:, :, :], ALU.subtract)
        k.barrier()
    pc = k.sb(st, [128, 24], F32, "pcols")
    k.ld("sp", pc.v[:, :], Wt["pcols"])
    oma = k.sb(st, [128, 2], F32, "oma")
    k.ts("dve", oma.v[:, :], pc.v[:, 14:16], -1.0, 1.0, ALU.mult, ALU.add)
    identf = k.sb(st, [128, 128], F32, "identf")
    k.ld("sp", identf.v[:, :], C["ident"])
    bd = k.sb(st, [128, 128], F32, "bd")
    k.ld("sp", bd.v[:, :], C["bd64"])
    e2 = k.sb(st, [128, 2], F32, "e2")
    k.ld("sp", e2.v[:, :], C["e2"])
    rtab = k.sb(st, [128, 2, 2, 3, CH], F32, "rtab")
    k.ld("sp", rtab.v[:, :, :, :, :], C["ret_tab"])
    w2s = k.sb(st, [64, 2, 256], BF16, "w2s")
    a2s = k.sb(st, [64, 2, 256], BF16, "a2s")
    g2s = k.sb(st, [128, 256], BF16, "g2s")
    for d in range(2):
        k.ld("pool", w2s.v[:, d, :], Wt["w2"][d])
        k.ld("pool", a2s.v[:, d, :], Wt["a2"][d])
    k.ld("pool", g2s.v[:, :], Wt["g2"])

    NT = 512
    hTt = k.sb(st, [128, 8, NT + 2], BF16, "hTt")
    ca = k.sb(st, [128, NT], F32, "ca")
    sa = k.sb(st, [128, NT], F32, "sa")
    cs_ = k.sb(st, [128, NT], F32, "cs")
    ss_ = k.sb(st, [128, NT], F32, "ss")
    gca = k.sb(st, [128, 2, NT], F32, "gca")
    gsa = k.sb(st, [128, 2, NT], F32, "gsa")
    F = [k.sb(st, [128, NT], F32, "f%d" % i) for i in range(12)]
    ob = k.sb(st, [128, NT], BF16, "ob")
    sbf = [k.sb(st, [128, NT], BF16, "sbf%d" % i) for i in range(2)]
    sbi = [0]

    def st_bf(dst, srcv, n):
        b = sbf[sbi[0] % 2]
        sbi[0] += 1
        k.cp("act", b.v[:, 0:n], srcv)
        k.stv("sp", dst, b.v[:, 0:n])
    tb = k.sb(st, [128, 512], F32, "tb")
    tbb = k.sb(st, [128, 512], BF16, "tbb")
    thb = k.sb(st, [64, NT], BF16, "thb")
    alb = k.sb(st, [64, NT], BF16, "alb")
    sgb = k.sb(st, [128, NT], BF16, "sgb")
    bs4 = k.sb(st, [128, 4, 4], F32, "bs4")
    pcx = k.sb(st, [128, NT // CH], F32, "pcx")
    pcx2 = k.sb(st, [128, NT // CH], F32, "pcx2")
    F2 = [k.sb(st, [128, NT], F32, "g%d" % i) for i in range(9)]
    PS = [k.ps(st, [128, 512], F32, "pp%d" % i) for i in range(8)]
    pi = [0]

    def nps():
        p = PS[pi[0] % 8]
        pi[0] += 1
        return p

    def chain(p, wlist, M, n, col0=1):
        tot = len(wlist) * 8
        i = 0
        for (wb, c0, sh) in wlist:
            for c in range(8):
                k.mm(p.v[0:M, 0:n], wb.v[:, c, c0:c0 + M], hTt.v[:, c, col0 + sh:col0 + sh + n], i == 0, i == tot - 1)
                i += 1

    def chain_tok(p, wlist, ncols, s):
        tot = len(wlist) * 8
        i = 0
        for (wb, c0, sh) in wlist:
            for c in range(8):
                k.mm(p.v[:, 0:ncols], hTt.v[:, c, 1 + sh + s * 128:1 + sh + (s + 1) * 128], wb.v[:, c, c0:c0 + ncols],
                     i == 0, i == tot - 1)
                i += 1

    def rope_out(P, Psw, ctab, stab, dst, n):
        k.tt("dve", F[10].v[:, 0:n], P.v[:, 0:n], ctab, ALU.mult)
        k.tt("dve", F[11].v[:, 0:n], Psw.v[:, 0:n], stab, ALU.mult)
        k.tt("dve", dst, F[10].v[:, 0:n], F[11].v[:, 0:n], ALU.add)

    segs = [(0, T, True), (T, CTX, False)]
    for (seg0, seglen, latent) in segs:
        for t0 in range(seg0, seg0 + seglen, NT):
            n = min(NT, seg0 + seglen - t0)
            ns = n // 128
            nch = n // CH
            lo = t0 - 1 if t0 > seg0 else t0
            hi = t0 + n + 1 if t0 + n < seg0 + seglen else t0 + n
            if lo == t0:
                k.memset("dve", hTt.v[:, :, 0:1], 0.0)
            if hi == t0 + n:
                k.memset("dve", hTt.v[:, :, n + 1:n + 2], 0.0)
            for c in range(8):
                k.ld("sp", hTt.v[:, c, 1 - (t0 - lo):1 + n + (hi - t0 - n)], hTd[c * 128:(c + 1) * 128, lo:hi])
            if latent:
                k.ld("sp", ca.v[:, 0:n], C["ropeA_c"][:, t0:t0 + n])
                k.ld("sp", sa.v[:, 0:n], C["ropeA_s"][:, t0:t0 + n])
                k.ld("sp", cs_.v[:, 0:n], C["ropeS_c"][:, t0:t0 + n])
                k.ld("sp", ss_.v[:, 0:n], C["ropeS_s"][:, t0:t0 + n])
                for qk in range(2):
                    k.ts("dve", gca.v[:, qk, 0:n], ca.v[:, 0:n], pc.v[:, 2 * qk:2 * qk + 1], None, ALU.mult)
                    k.ts("dve", gsa.v[:, qk, 0:n], sa.v[:, 0:n], pc.v[:, 2 * qk + 1:2 * qk + 2], None, ALU.mult)
            for grp, base, swb, dq, dk_, dv_ in (("A", 0, 0, S["QA"], S["KA"], S["VA"]),
                                                ("B", 512, 384, S["QB"], S["KB"], S["VB"])):
                for j in range(3):
                    P, Psw = nps(), nps()
                    chain(P, [(wabc, base + j * 128, 0)], 128, n)
                    if latent:
                        chain(Psw, [(wsw, swb + j * 128, 0)], 128, n)
                    qk = 0 if j < 2 else 1
                    if grp == "B":
                        k.act(F[0].v[:, 0:n], P.v[:, 0:n], ACT.Square)
                        pss = nps()
                        k.mm(pss.v[:, 0:n], bd.v[:, :], F[0].v[:, 0:n])
                        k.ts("dve", F[1].v[:, 0:n], pss.v[:, 0:n], 1.0 / 64, RMS_EPS, ALU.mult, ALU.add)
                        k.act(F[1].v[:, 0:n], F[1].v[:, 0:n], ACT.Sqrt)
                        k.recip(F[1].v[:, 0:n], F[1].v[:, 0:n])
                        k.tt("dve", F[2].v[:, 0:n], P.v[:, 0:n], F[1].v[:, 0:n], ALU.mult)
                        if latent:
                            k.tt("dve", F[3].v[:, 0:n], Psw.v[:, 0:n], F[1].v[:, 0:n], ALU.mult)
                            rope_out(F[2], F[3], gca.v[:, qk, 0:n], gsa.v[:, qk, 0:n], ob.v[:, 0:n], n)
                        else:
                            k.ts("dve", ob.v[:, 0:n], F[2].v[:, 0:n], pc.v[:, 2 * qk:2 * qk + 1], None, ALU.mult)
                    else:
                        if latent:
                            rope_out(P, Psw, ca.v[:, 0:n], sa.v[:, 0:n], ob.v[:, 0:n], n)
                        else:
                            k.cp("act", ob.v[:, 0:n], P.v[:, 0:n])
                    dst = dq[j * 128:(j + 1) * 128, t0:t0 + n] if j < 2 else dk_[:, t0:t0 + n]
                    k.stv("sp", dst, ob.v[:, 0:n])
                for s in range(ns):
                    P = nps()
                    chain_tok(P, [(wabc, base + 384, 0)], 128, s)
                    k.cp("act", tbb.v[:, 0:128], P.v[:, 0:128])
                    k.stv("sp", dv_[t0 + s * 128:t0 + (s + 1) * 128, :], tbb.v[:, 0:128])
            for j in range(4):
                P, Psw = nps(), nps()
                chain(P, [(wabc, 1024 + j * 128, 0)], 128, n)
                g = j % 2
                if latent:
                    chain(Psw, [(wsw, 768 + j * 128, 0)], 128, n)
                    rope_out(P, Psw, cs_.v[:, 0:n], ss_.v[:, 0:n], F[0].v[:, 0:n], n)
                else:
                    k.cp("act", F[0].v[:, 0:n], P.v[:, 0:n])
                x3 = F[0].v[:, 0:n].rr("p (c i) -> p c i", i=CH)
                for d in range(2):
                    if j < 2:
                        k.tt("dve", F[1].v[:, 0:n].rr("p (c i) -> p c i", i=CH), x3,
                             rtab.v[:, g, d, 0:1, :].bc([128, nch, CH]), ALU.mult)
                        st_bf(S["C_Rt"][d, g * 128:(g + 1) * 128, t0:t0 + n], F[1].v[:, 0:n], n)
                    else:
                        k.tt("dve", F[1].v[:, 0:n].rr("p (c i) -> p c i", i=CH), x3,
                             rtab.v[:, g, d, 1:2, :].bc([128, nch, CH]), ALU.mult)
                        st_bf(S["C_Kt"][d, g * 128:(g + 1) * 128, t0:t0 + n], F[1].v[:, 0:n], n)
                        k.tt("dve", F[2].v[:, 0:n].rr("p (c i) -> p c i", i=CH), x3,
                             rtab.v[:, g, d, 2:3, :].bc([128, nch, CH]), ALU.mult)
                        for s in range(ns):
                            pt = nps()
                            k.tr(pt.v[:, 0:128], F[2].v[:, s * 128:(s + 1) * 128], identf.v[:, :])
                            k.cp("act", tbb.v[:, 0:128], pt.v[:, 0:128])
                            k.stv("sp", S["C_Kh"][d, t0 + s * 128:t0 + (s + 1) * 128, g * 128:(g + 1) * 128],
                                  tbb.v[:, 0:128])
            for s in range(ns):
                P = nps()
                chain_tok(P, [(wabc, 1536, 0)], 256, s)
                k.cp("act", tbb.v[:, 0:256], P.v[:, 0:256])
                k.stv("sp", S["C_V"][t0 + s * 128:t0 + (s + 1) * 128, :], tbb.v[:, 0:256])
                P = nps()
                chain_tok(P, [(wabc, 1792, 0)], 512, s)
                k.act(tb.v[:, 0:512], P.v[:, 0:512], ACT.Silu)
                k.stv("sp", S["C_G"][t0 + s * 128:t0 + (s + 1) * 128, :], tb.v[:, 0:512])
            P = nps()
            chain(P, [(wgd, 0, 0)], 128, n)
            k.act(sgb.v[:, 0:n], P.v[:, 0:n], ACT.Sigmoid)
            for s in range(ns):
                P = nps()
                k.mm(P.v[:, 0:256], sgb.v[:, s * 128:(s + 1) * 128], g2s.v[:, :])
                k.cp("act", tb.v[:, 0:256], P.v[:, 0:256])
                k.stv("sp", S["D_G"][t0 + s * 128:t0 + (s + 1) * 128, :], tb.v[:, 0:256])
            for d in range(2):
                sh = -1 if d == 0 else 1
                wl = lambda c0: [(wdb[d], c0, 0), (wda[d], c0, sh)]
                P = nps()
                chain(P, wl(768), 64, n)
                k.act(thb.v[:, 0:n], P.v[0:64, 0:n], ACT.Tanh)
                P = nps()
                chain(P, wl(832), 64, n)
                k.cp("act", alb.v[:, 0:n], P.v[0:64, 0:n])
                for s in range(ns):
                    P = nps()
                    chain_tok(P, wl(512), 256, s)
                    k.cp("act", tbb.v[:, 0:256], P.v[:, 0:256])
                    k.stv("sp", S["D_V"][d, t0 + s * 128:t0 + (s + 1) * 128, :], tbb.v[:, 0:256])
                def dgroup(g, FB, pcx):
                    r_, k_, lw, cs, a_, kk, e_, t1, t2 = FB
                    gc = slice(g, g + 1)
                    P = nps()
                    chain(P, wl(g * 128), 128, n)
                    k.cp("act", r_.v[:, 0:n], P.v[:, 0:n])
                    P = nps()
                    chain(P, wl(256 + g * 128), 128, n)
                    k.cp("act", k_.v[:, 0:n], P.v[:, 0:n])
                    yield
                    P = nps()
                    k.mm(P.v[:, 0:n], w2s.v[:, d, g * 128:(g + 1) * 128], thb.v[:, 0:n])
                    k.act(lw.v[:, 0:n], P.v[:, 0:n], ACT.Sigmoid, bias=pc.v[:, 4 + 2 * d + g:5 + 2 * d + g])
                    k.ts("dve", lw.v[:, 0:n], lw.v[:, 0:n], -DECAY_SCALE, None, ALU.mult)
                    yield
                    P = nps()
                    k.mm(P.v[:, 0:n], a2s.v[:, d, g * 128:(g + 1) * 128], alb.v[:, 0:n])
                    k.act(a_.v[:, 0:n], P.v[:, 0:n], ACT.Sigmoid, bias=pc.v[:, 8 + 2 * d + g:9 + 2 * d + g])
                    yield
                    k.ts("dve", kk.v[:, 0:n], k_.v[:, 0:n], pc.v[:, 12 + g:13 + g], None, ALU.mult)
                    k.act(t1.v[:, 0:n], kk.v[:, 0:n], ACT.Square)
                    yield
                    P = nps()
                    k.mm(P.v[:, 0:n], bd.v[:, :], t1.v[:, 0:n])
                    k.act(t1.v[:, 0:n], P.v[:, 0:n], ACT.Sqrt)
                    yield
                    k.ts("dve", t1.v[:, 0:n], t1.v[:, 0:n], 1e-12, None, ALU.max)
                    k.recip(t1.v[:, 0:n], t1.v[:, 0:n])
                    yield
                    k.tt("dve", kk.v[:, 0:n], kk.v[:, 0:n], t1.v[:, 0:n], ALU.mult)
                    yield
                    k.ts("dve", t1.v[:, 0:n], a_.v[:, 0:n], pc.v[:, 14 + g:15 + g], oma.v[:, gc], ALU.mult, ALU.add)
                    k.tt("dve", k_.v[:, 0:n], k_.v[:, 0:n], t1.v[:, 0:n], ALU.mult)
                    yield
                    k.stt("dve", t1.v[:, 0:n], r_.v[:, 0:n], pc.v[:, 16 + 2 * d + g:17 + 2 * d + g], k_.v[:, 0:n],
                          ALU.mult, ALU.mult)
                    for s in range(ns):
                        P = nps()
                        k.mm(P.v[:, 0:2], t1.v[:, s * 128:(s + 1) * 128], e2.v[:, :])
                        k.cp("act", bs4.v[:, s, 2 * g:2 * g + 2], P.v[:, 0:2])
                    k.tt("dve", a_.v[:, 0:n], a_.v[:, 0:n], kk.v[:, 0:n], ALU.mult)
                    yield
                    src, dst = lw, cs
                    k.cp("act", t2.v[:, 0:n], lw.v[:, 0:n])
                    src = t2
                    for stp in (1, 2, 4, 8, 16, 32):
                        s3 = src.v[:, 0:n].rr("p (c i) -> p c i", i=CH)
                        d3 = dst.v[:, 0:n].rr("p (c i) -> p c i", i=CH)
                        if d == 0:
                            k.tt("dve", d3[:, :, stp:], s3[:, :, stp:], s3[:, :, :CH - stp], ALU.add)
                            k.cp("act", d3[:, :, :stp], s3[:, :, :stp])
                        else:
                            k.tt("dve", d3[:, :, :CH - stp], s3[:, :, :CH - stp], s3[:, :, stp:], ALU.add)
                            k.cp("act", d3[:, :, CH - stp:], s3[:, :, CH - stp:])
                        src, dst = dst, src
                        yield
                    cs = src
                    spare = dst
                    cs3 = cs.v[:, 0:n].rr("p (c i) -> p c i", i=CH)
                    tot3 = cs3[:, :, CH - 1:CH] if d == 0 else cs3[:, :, 0:1]
                    k.act(e_.v[:, 0:n], cs.v[:, 0:n], ACT.Exp)
                    yield
                    k.tt("dve", t1.v[:, 0:n], r_.v[:, 0:n], e_.v[:, 0:n], ALU.mult)
                    st_bf(S["D_Rt"][d, g * 128:(g + 1) * 128, t0:t0 + n], t1.v[:, 0:n], n)
                    yield
                    k.tt("dve", e_.v[:, 0:n], cs.v[:, 0:n], lw.v[:, 0:n], ALU.subtract)
                    k.act(e_.v[:, 0:n], e_.v[:, 0:n], ACT.Exp)
                    yield
                    k.stt("dve", t1.v[:, 0:n], kk.v[:, 0:n], -1.0, e_.v[:, 0:n], ALU.mult, ALU.mult)
                    st_bf(S["D_At"][d, g * 128:(g + 1) * 128, t0:t0 + n], t1.v[:, 0:n], n)
                    yield
                    for s in range(ns):
                        pt = nps()
                        k.tr(pt.v[:, 0:128], t1.v[:, s * 128:(s + 1) * 128], identf.v[:, :])
                        k.cp("act", tbb.v[:, 0:128], pt.v[:, 0:128])
                        k.stv("sp", S["D_Atok"][d, t0 + s * 128:t0 + (s + 1) * 128, g * 128:(g + 1) * 128],
                              tbb.v[:, 0:128])
                    k.act(e_.v[:, 0:n], cs.v[:, 0:n], ACT.Exp, scale=-1.0)
                    k.tt("dve", t1.v[:, 0:n], a_.v[:, 0:n], e_.v[:, 0:n], ALU.mult)
                    st_bf(S["D_Bt"][d, g * 128:(g + 1) * 128, t0:t0 + n], t1.v[:, 0:n], n)
                    yield
                    k.tt("dve", t1.v[:, 0:n], k_.v[:, 0:n], e_.v[:, 0:n], ALU.mult)
                    st_bf(S["D_Kt"][d, g * 128:(g + 1) * 128, t0:t0 + n], t1.v[:, 0:n], n)
                    yield
                    k.act(pcx.v[:, 0:nch], tot3.rr("p c o -> p (c o)"), ACT.Exp)
                    k.stv("sp", S["D_PC"][d, g * 128:(g + 1) * 128, t0 // CH:t0 // CH + nch], pcx.v[:, 0:nch])
                    k.tt("dve", e_.v[:, 0:n].rr("p (c i) -> p c i", i=CH), tot3.bc([128, nch, CH]), cs3, ALU.subtract)
                    k.act(e_.v[:, 0:n], e_.v[:, 0:n], ACT.Exp)
                    yield
                    for (srcb, dstd) in ((a_, S["D_Bh"]), (k_, S["D_Kh"])):
                        k.tt("dve", t1.v[:, 0:n], srcb.v[:, 0:n], e_.v[:, 0:n], ALU.mult)
                        for s in range(ns):
                            pt = nps()
                            k.tr(pt.v[:, 0:128], t1.v[:, s * 128:(s + 1) * 128], identf.v[:, :])
                            k.cp("act", tbb.v[:, 0:128], pt.v[:, 0:128])
                            k.stv("sp", dstd[d, t0 + s * 128:t0 + (s + 1) * 128, g * 128:(g + 1) * 128], tbb.v[:, 0:128])
                gens = [dgroup(0, F[0:9], pcx), dgroup(1, F2, pcx2)]
                live = list(gens)
                while live:
                    for g_ in list(live):
                        try:
                            next(g_)
                        except StopIteration:
                            live.remove(g_)
                for s in range(ns):
                    k.stv("sp", S["D_BS"][d, t0 + s * 128:t0 + (s + 1) * 128, :], bs4.v[:, s, :])


def attnphase(k, st, cfg):
    S, C, Wt = cfg["S"], cfg["C"], cfg["W"]
    T, CTX = cfg["T"], cfg["CTX"]
    TT = T + CTX
    NKC = TT // 128
    need_ctx = cfg["need_ctx"]
    kT = k.sb(st, [128, TT], BF16, "kT")
    va = k.sb(st, [128, NKC, 65], BF16, "va")
    qT = [k.sb(st, [128, 512], BF16, "qT%d" % i) for i in range(2)]
    for c0 in range(0, TT, 2048):
        k.memset("dve", kT.v[64:128, c0:min(TT, c0 + 2048)], 0.0)
    for q_ in qT:
        k.memset("dve", q_.v[64:128, :], 0.0)
    pT = [k.sb(st, [128, 512], BF16, "pT%d" % i) for i in range(3)]
    ot = [k.sb(st, [128, 64], F32, "ot%d" % i) for i in range(2)]
    rc = k.sb(st, [128, 1], F32, "rc")
    esk = k.sb(st, [128, 4], F32, "esk")
    mprev = k.sb(st, [128, 128], BF16, "mprev")
    mnext = k.sb(st, [128, 128], BF16, "mnext")
    k.ld("pool", mprev.v[:, :], C["mprev"])
    k.ld("pool", mnext.v[:, :], C["mnext"])
    k.ld("sp", esk.v[:, :], Wt["sink"].partition_broadcast(128))
    k.act(esk.v[:, :], esk.v[:, :], ACT.Exp)
    ps_s = [k.ps(st, [128, 512], F32, "pss%d" % i) for i in range(3)]
    ps_o = [k.ps(st, [128, 512], F32, "pso%d" % i) for i in range(4)]
    cnt = dict(s=0, q=0, o=0, p=0)
    scale = 64 ** -0.5

    def load_kv(kd, vd, g):
        k.ld("sp", kT.v[0:64, :], kd[g * 64:(g + 1) * 64, :])
        k.memset("dve", va.v[:, :, 64:65], 1.0)
        k.ld("sp", va.v[:, :, 0:64], vd[:, g * 64:(g + 1) * 64].rearrange("(c p) j -> p c j", p=128))

    def finish(acc, nq_sub, t0, col, sinkcol):
        for s in range(nq_sub):
            o = ot[cnt["o"] % 2]
            cnt["o"] += 1
            if sinkcol is None:
                k.recip(rc.v[:, :], acc[s].v[:, 64:65])
            else:
                k.tt("dve", rc.v[:, :], acc[s].v[:, 64:65], esk.v[:, sinkcol:sinkcol + 1], ALU.add)
                k.recip(rc.v[:, :], rc.v[:, :])
            k.ts("dve", o.v[:, :], acc[s].v[:, 0:64], rc.v[:, 0:1], None, ALU.mult)
            k.stv("sp", S["O"][t0 + s * 128:t0 + (s + 1) * 128, col:col + 64], o.v[:, :])

    def block(qd, h, t0, nq, chunks, col, sinkcol):
        q = qT[cnt["q"] % 2]
        cnt["q"] += 1
        k.ld("sp", q.v[0:64, 0:nq], qd[h * 64:(h + 1) * 64, t0:t0 + nq])
        nsub = nq // 128
        first, last = {}, {}
        for i, (kc, subs) in enumerate(chunks):
            for (s, _) in subs:
                first.setdefault(s, i)
                last[s] = i

        def qk(i):
            ps = ps_s[cnt["s"] % 3]
            cnt["s"] += 1
            kc = chunks[i][0]
            k.mm(ps.v[:, 0:nq], kT.v[:, kc * 128:(kc + 1) * 128], q.v[:, 0:nq])
            return ps

        nxt = qk(0)
        for i, (kc, subs) in enumerate(chunks):
            ps = nxt
            if i + 1 < len(chunks):
                nxt = qk(i + 1)
            p = pT[cnt["p"] % 3]
            cnt["p"] += 1
            k.act(p.v[:, 0:nq], ps.v[:, 0:nq], ACT.Exp, scale=scale)
            for (s, m_) in subs:
                if m_ is not None:
                    k.tt("dve", p.v[:, s * 128:(s + 1) * 128], p.v[:, s * 128:(s + 1) * 128], m_.v[:, :], ALU.mult)
            for (s, m_) in subs:
                k.mm(ps_o[s].v[:, 0:65], p.v[:, s * 128:(s + 1) * 128], va.v[:, kc, :], i == first[s], i == last[s])
        finish(ps_o, nsub, t0, col, sinkcol)

    for g in range(2):
        load_kv(S["KB"], S["VB"], g)
        for h in (2 * g, 2 * g + 1):
            for t0 in range(0, T, 512):
                nq = min(512, T - t0)
                block(S["QB"], h, t0, nq, [(kc, [(s, None) for s in range(nq // 128)]) for kc in range(NKC)],
                      256 + h * 64, None)
            if need_ctx:
                block(S["QB"], h, T, CTX, [(kc, [(s, None) for s in range(CTX // 128)]) for kc in range(T // 128, NKC)],
                      256 + h * 64, None)
    nb = T // 128
    cchunks = list(range(T // 128, NKC))
    for g in range(2):
        load_kv(S["KA"], S["VA"], g)
        for h in (2 * g, 2 * g + 1):
            for b0 in range(0, nb, 4):
                nsb = min(4, nb - b0)
                chunks = []
                for kc in range(max(0, b0 - 1), min(nb, b0 + nsb + 1)):
                    subs = []
                    for s in range(nsb):
                        dlt = kc - (b0 + s)
                        if dlt == -1:
                            subs.append((s, mprev))
                        elif dlt == 0:
                            subs.append((s, None))
                        elif dlt == 1:
                            subs.append((s, mnext))
                    chunks.append((kc, subs))
                for kc in cchunks:
                    chunks.append((kc, [(s, None) for s in range(nsb)]))
                block(S["QA"], h, b0 * 128, nsb * 128, chunks, h * 64, h)
            if need_ctx:
                block(S["QA"], h, T, CTX, [(kc, [(s, None) for s in range(CTX // 128)]) for kc in cchunks], h * 64, h)


def scanphase(k, st, cfg):
    S, C = cfg["S"], cfg["C"]
    T, CTX = cfg["T"], cfg["CTX"]
    TT = T + CTX
    dplr = cfg["dplr"]
    pre = "D_" if dplr else "C_"
    NCG = 2
    GT_ = NCG * CH
    NCHT = TT // CH
    f3 = lambda b: b.v[:, :, :]
    t64 = lambda nm: k.sb(st, [64, 8, 64], F32, nm)
    b64 = lambda nm: k.sb(st, [64, 8, 64], BF16, nm)
    chm = lambda nm: k.sb(st, [64, 8, GT_], BF16, nm)
    tkm = lambda nm: k.sb(st, [64, NCG, 8, 64], BF16, nm)
    RT = [chm("RT%d" % i) for i in range(2)]
    KT = [chm("KT%d" % i) for i in range(2)]
    KH = [tkm("KH%d" % i) for i in range(2)]
    VV = [tkm("VV%d" % i) for i in range(2)]
    Min = t64("Min")
    k.ld("sp", f3(Min), C["m_in"].rearrange("p (q t) -> p q t", q=8))
    id8 = t64("id8")
    k.ld("sp", f3(id8), C["id8"].rearrange("p (q t) -> p q t", q=8))
    if dplr:
        AT = [chm("AT%d" % i) for i in range(2)]
        BT = [chm("BT%d" % i) for i in range(2)]
        BH = [tkm("BH%d" % i) for i in range(2)]
        AK = [tkm("AK%d" % i) for i in range(2)]
        Mst, Mts = t64("Mst"), t64("Mts")
        k.ld("sp", f3(Mst), C["m_st"].rearrange("p (q t) -> p q t", q=8))
        k.ld("sp", f3(Mts), C["m_ts"].rearrange("p (q t) -> p q t", q=8))
        PCt = k.sb(st, [64, 8, NCHT], F32, "PCt")
        for d in range(2):
            k.ld("sp", PCt.v[:, d * 4:(d + 1) * 4, :], S["D_PC"][d].rearrange("(h j) c -> j h c", j=64))
    else:
        PCr = k.sb(st, [64, 8], F32, "PCr")
        k.ld("sp", PCr.v[:, :], C["ret_pc"])
        GTc = b64("GTc")
        k.tt("dve", f3(GTc), f3(id8), PCr.v[:, :].rr("p (q o) -> p q o", o=1).bc([64, 8, 64]), ALU.mult)
    sets = []
    for c in range(NCG):
        B = dict(ArkT=b64("ArkT%d" % c), Yv=t64("Yv%d" % c), H=t64("H%d" % c))
        if dplr:
            B.update(Ls=[b64("L%d_%d" % (i, c)) for i in range(5)], Ns=[b64("N%d_%d" % (i, c)) for i in range(6)],
                     AakT=b64("AakT%d" % c), ArbT=b64("ArbT%d" % c), Qe=b64("Qe%d" % c), GTt=b64("GTt%d" % c),
                     X=[k.sb(st, [64, 8, 128], F32, "X%d_%d" % (i, c)) for i in range(2)],
                     Xb=k.sb(st, [64, 8, 128], BF16, "Xb_%d" % c),
                     psX=k.ps(st, [64, 1024], F32, "psX%d" % c))
        sets.append(B)
    ST = [b64("ST%d" % i) for i in range(2)]
    Yt = [t64("Yt%d" % i) for i in range(2)]
    psA = [k.ps(st, [64, 512], F32, "psA%d" % i) for i in range(2)]
    psY = [k.ps(st, [64, 512], F32, "psY%d" % i) for i in range(2)]
    cnt = dict(a=0, y=0, e=0)
    p3 = lambda p: p.v[:, :].rr("p (q t) -> p q t", q=8)

    def npa():
        p = psA[cnt["a"] % 2]
        cnt["a"] += 1
        return p

    def ev():
        cnt["e"] += 1
        return "act" if cnt["e"] % 2 else "dve"

    def prod(dst, lhs, rhs, mask):
        p = npa()
        for q in range(8):
            k.mm(p3(p)[:, q, :], lhs(q), rhs(q))
        if mask is None:
            k.cp(ev(), f3(dst), p3(p))
        else:
            k.tt("dve", f3(dst), p3(p), f3(mask), ALU.mult)

    def views(rb, cl):
        sl = {d: slice(cl[d] * CH, (cl[d] + 1) * CH) for d in range(2)}
        dq = lambda q: q // 4
        chs = lambda buf: (lambda q: buf[rb].v[:, q, sl[dq(q)]])
        tks = lambda buf: (lambda q: buf[rb].v[:, cl[dq(q)], q, :])
        return sl, chs, tks

    def ppart(B, rb, cl, grp):
        sl, chs, tks = views(rb, cl)
        rt, kt, kh, vv = chs(RT), chs(KT), tks(KH), tks(VV)
        prod(B["ArkT"], kt, rt, Min)
        yield
        if dplr:
            at, bt, bh = chs(AT), chs(BT), tks(BH)
            Ls, Ns, AakT, ArbT = B["Ls"], B["Ns"], B["AakT"], B["ArbT"]
            prod(Ns[0], bt, at, Mst)
            prod(Ls[0], at, bt, Mts)
            yield
            prod(AakT, kt, at, Mst)
            prod(ArbT, bt, rt, Min)
            yield
            xc, xn = B["X"]
            xb = B["Xb"]
            for d in range(2):
                k.cp("act", xc.v[:, d * 4:(d + 1) * 4, 0:64], AK[rb].v[:, cl[d], d * 4:(d + 1) * 4, :])
            p = npa()
            for q in range(8):
                k.mm(p3(p)[:, q, :], AakT.v[:, q, :], vv(q))
            k.cp("dve", xc.v[:, :, 64:128], p3(p))
            k.cp("act", f3(xb), f3(xc))
            yield
            px = B["psX"].v[:, :].rr("p (q c) -> p q c", q=8)
            for lv in range(6):
                for q in range(8):
                    k.mm(px[:, q, :], Ns[lv].v[:, q, :], xb.v[:, q, :])
                if lv < 5:
                    if lv < 4:
                        prod(Ls[lv + 1], lambda q: Ns[lv].v[:, q, :], lambda q: Ls[lv].v[:, q, :], None)
                    prod(Ns[lv + 1], lambda q: Ls[lv].v[:, q, :], lambda q: Ns[lv].v[:, q, :], None)
                k.tt("dve", f3(xn), f3(xc), px, ALU.add)
                xc, xn = xn, xc
                k.cp("act", f3(xb), f3(xc))
                yield
            wq = lambda q: xb.v[:, q, 0:64]
            uv = lambda q: xb.v[:, q, 64:128]
            p = npa()
            for q in range(8):
                k.mm(p3(p)[:, q, :], wq(q), ArbT.v[:, q, :])
            for d in range(2):
                k.tt("dve", B["Qe"].v[:, d * 4:(d + 1) * 4, :], p3(p)[:, d * 4:(d + 1) * 4, :],
                     RT[rb].v[:, d * 4:(d + 1) * 4, sl[d]], ALU.add)
            p = npa()
            for q in range(8):
                k.mm(p3(p)[:, q, :], wq(q), bh(q))
            for d in range(2):
                c_abs = grp[d] * NCG + cl[d]
                k.tt("dve", B["GTt"].v[:, d * 4:(d + 1) * 4, :], id8.v[:, d * 4:(d + 1) * 4, :],
                     PCt.v[:, d * 4:(d + 1) * 4, c_abs:c_abs + 1].bc([64, 4, 64]), ALU.mult)
            k.tt("dve", f3(B["GTt"]), f3(B["GTt"]), p3(p), ALU.add)
            yield
        p = npa()
        for q in range(8):
            k.mm(p3(p)[:, q, :], B["ArkT"].v[:, q, :], vv(q), True, not dplr)
            if dplr:
                k.mm(p3(p)[:, q, :], ArbT.v[:, q, :], uv(q), False, True)
        k.cp(ev(), f3(B["Yv"]), p3(p))
        p = npa()
        for q in range(8):
            k.mm(p3(p)[:, q, :], kh(q), vv(q), True, not dplr)
            if dplr:
                k.mm(p3(p)[:, q, :], bh(q), uv(q), False, True)
        k.cp(ev(), f3(B["H"]), p3(p))
        yield

    k.memset("dve", f3(ST[0]), 0.0)
    ngl = T // GT_
    ncg_ctx = CTX // GT_
    fw = list(range(ngl, ngl + ncg_ctx)) + list(range(ngl))
    bw = list(range(ngl + ncg_ctx - 1, ngl - 1, -1)) + list(range(ngl - 1, -1, -1))
    order = {0: fw, 1: bw}
    step = 0
    for gi in range(ngl + ncg_ctx):
        rb = gi % 2
        grp = {d: order[d][gi] for d in range(2)}
        for d in range(2):
            t0 = grp[d] * GT_
            qs = slice(d * 4, (d + 1) * 4)
            chv = lambda nm: S[pre + nm][d, :, t0:t0 + GT_].rearrange("(h j) t -> j h t", j=64)
            tkv = lambda ap: ap[t0:t0 + GT_, :].rearrange("(n t) (h j) -> t n h j", t=CH, j=64)
            k.ld("sp", RT[rb].v[:, qs, :], chv("Rt"))
            k.ld("sp", KT[rb].v[:, qs, :], chv("Kt"))
            k.ld("sp", KH[rb].v[:, :, qs, :], tkv(S[pre + "Kh"][d]))
            k.ld("sp", VV[rb].v[:, :, qs, :], tkv(S["D_V"][d] if dplr else S["C_V"]))
            if dplr:
                k.ld("sp", AT[rb].v[:, qs, :], chv("At"))
                k.ld("sp", BT[rb].v[:, qs, :], chv("Bt"))
                k.ld("sp", BH[rb].v[:, :, qs, :], tkv(S["D_Bh"][d]))
                k.ld("sp", AK[rb].v[:, :, qs, :], tkv(S["D_Atok"][d]))
        cls = [{0: ci, 1: NCG - 1 - ci} for ci in range(NCG)]
        gens = [ppart(sets[ci], rb, cls[ci], grp) for ci in range(NCG)]
        live = list(gens)
        while live:
            for g_ in list(live):
                try:
                    next(g_)
                except StopIteration:
                    live.remove(g_)
        for ci in range(NCG):
            B, cl = sets[ci], cls[ci]
            sl, chs, tks = views(rb, cl)
            Sc, Sn = ST[step % 2], ST[(step + 1) % 2]
            qe = (lambda q: B["Qe"].v[:, q, :]) if dplr else chs(RT)
            gt = (lambda q: B["GTt"].v[:, q, :]) if dplr else (lambda q: GTc.v[:, q, :])
            py = psY[cnt["y"] % 2]
            cnt["y"] += 1
            for q in range(8):
                k.mm(p3(py)[:, q, :], qe(q), Sc.v[:, q, :])
            p = npa()
            for q in range(8):
                k.mm(p3(p)[:, q, :], gt(q), Sc.v[:, q, :])
            k.tt("dve", f3(Sn), p3(p), f3(B["H"]), ALU.add)
            yt = Yt[step % 2]
            k.tt("dve", f3(yt), p3(py), f3(B["Yv"]), ALU.add)
            for d in range(2):
                tt0 = grp[d] * GT_ + cl[d] * CH
                k.stv("sp", S[pre + "Y"][d, tt0:tt0 + CH, :].rearrange("t (h i) -> t h i", i=64),
                      yt.v[:, d * 4:(d + 1) * 4, :])
            step += 1


GN_EPS = 64e-5


def postphase(k, st, cfg):
    S, Wt = cfg["S"], cfg["W"]
    TT = cfg["T"] + cfg["CTX"]
    row = lambda nm, ap: (lambda t: (k.ld("sp", t.v[:, :], ap.partition_broadcast(128)), t)[1])(k.sb(st, [128, 256], F32, nm))
    retg = row("retg", Wt["ret_g"])
    lng = row("lng", Wt["ln_g"])
    lnb = row("lnb", Wt["ln_b"])
    y = [k.sb(st, [128, 4, 64], F32, "y%d" % i) for i in range(2)]
    yc = k.sb(st, [128, 4, 64], F32, "yc")
    sq = k.sb(st, [128, 4, 64], F32, "sq")
    m4 = k.sb(st, [128, 4], F32, "m4")
    v4 = k.sb(st, [128, 4], F32, "v4")
    gc = k.sb(st, [128, 512], F32, "gc")
    gd = k.sb(st, [128, 256], F32, "gd")
    vd = k.sb(st, [128, 4, 64], BF16, "vd")
    vdf = k.sb(st, [128, 4, 64], F32, "vdf")
    bs = k.sb(st, [128, 4], F32, "bs")
    acc = k.sb(st, [128, 256], F32, "acc")
    yn = k.sb(st, [128, 256], F32, "yn")
    i = 0

    def head_norm(yb, g, b, dst):
        k.op("dve", lambda e: e.tensor_reduce(out=m4[:, :], in_=yb[:, :, :], axis=AX.X, op=ALU.add), r=[yb], w=[m4])
        k.ts("dve", m4.v[:, :], m4.v[:, :], 1.0 / 64, None, ALU.mult)
        k.tt("dve", yc.v[:, :, :], yb.v[:, :, :], m4.v[:, :].rr("p (h o) -> p h o", o=1).bc([128, 4, 64]), ALU.subtract)
        k.tt("dve", sq.v[:, :, :], yc.v[:, :, :], yc.v[:, :, :], ALU.mult)
        k.op("dve", lambda e: e.tensor_reduce(out=v4[:, :], in_=sq[:, :, :], axis=AX.X, op=ALU.add), r=[sq], w=[v4])
        k.ts("dve", v4.v[:, :], v4.v[:, :], 1.0 / 64, GN_EPS, ALU.mult, ALU.add)
        k.act(v4.v[:, :], v4.v[:, :], ACT.Sqrt)
        k.recip(v4.v[:, :], v4.v[:, :])
        k.tt("dve", yc.v[:, :, :], yc.v[:, :, :], v4.v[:, :].rr("p (h o) -> p h o", o=1).bc([128, 4, 64]), ALU.mult)
        k.tt("dve", dst, yc.v[:, :, :].rr("p h i -> p (h i)"), g.v[:, :], ALU.mult)
        if b is not None:
            k.tt("dve", dst, dst, b.v[:, :], ALU.add)

    for t0 in range(0, TT, 128):
        tsl = slice(t0, t0 + 128)
        k.ld("sp", gc.v[:, :], S["C_G"][tsl, :])
        for d in range(2):
            yb = y[i % 2]
            i += 1
            k.ld("sp", yb.v[:, :, :], S["C_Y"][d, tsl, :].rearrange("t (h i) -> t h i", i=64))
            head_norm(yb, retg, None, yn.v[:, :])
            if d == 0:
                k.tt("dve", acc.v[:, :], yn.v[:, :], gc.v[:, 0:256], ALU.mult)
            else:
                k.tt("dve", yn.v[:, :], yn.v[:, :], gc.v[:, 256:512], ALU.mult)
                k.tt("dve", acc.v[:, :], acc.v[:, :], yn.v[:, :], ALU.add)
        k.stv("sp", S["O"][tsl, 512:768], acc.v[:, :])
        k.ld("sp", gd.v[:, :], S["D_G"][tsl, :])
        for d in range(2):
            yb = y[i % 2]
            i += 1
            k.ld("sp", yb.v[:, :, :], S["D_Y"][d, tsl, :].rearrange("t (h i) -> t h i", i=64))
            k.ld("sp", vd.v[:, :, :], S["D_V"][d, tsl, :].rearrange("t (h i) -> t h i", i=64))
            k.ld("sp", bs.v[:, :], S["D_BS"][d, tsl, :])
            head_norm(yb, lng, lnb, yn.v[:, :])
            k.tt("dve", vdf.v[:, :, :], vd.v[:, :, :], bs.v[:, :].rr("p (h o) -> p h o", o=1).bc([128, 4, 64]), ALU.mult)
            k.tt("dve", yn.v[:, :], yn.v[:, :], vdf.v[:, :, :].rr("p h i -> p (h i)"), ALU.add)
            if d == 0:
                k.cp("act", acc.v[:, :], yn.v[:, :])
            else:
                k.tt("dve", acc.v[:, :], acc.v[:, :], yn.v[:, :], ALU.add)
        k.tt("dve", acc.v[:, :], acc.v[:, :], gd.v[:, :], ALU.mult)
        k.stv("sp", S["O"][tsl, 768:1024], acc.v[:, :])


def modphase(k, st, cfg):
    cv = k.sb(st, [128, 2, 8], F32, "cv")
    for r in range(2):
        k.ld("sp", cv.v[:, r, :], cfg["cvec"][r].rearrange("(c p) -> p c", p=128), allow_slow_non_contiguous=True)
    sc = k.sb(st, [128, 8, 2], F32, "sc")
    k.act(sc.v[:, :, :].rr("p c r -> p r c"), cv.v[:, :, :], ACT.Silu)
    wm = [k.sb(st, [128, 8, 512], F32, "wm%d" % i) for i in range(2)]
    bm = k.sb(st, [2, 9 * D], F32, "bm")
    k.ld("sp", bm.v[:, :], cfg["b_mod"].partition_broadcast(2))
    ob = k.sb(st, [2, 9 * D], F32, "ob")
    pm = [k.ps(st, [128, 512], F32, "pm%d" % i) for i in range(2)]
    wv = cfg["w_mod"].rearrange("(c p) f -> p c f", p=128)
    for j in range(18):
        w = wm[j % 2]
        for c in range(8):
            k.ld("sp", w.v[:, c, :], wv[:, c, j * 512:(j + 1) * 512])
        p = pm[j % 2]
        for c in range(8):
            k.mm(p.v[0:2, :], sc.v[:, c, :], w.v[:, c, :], c == 0, c == 7)
        k.tt("dve", ob.v[:, j * 512:(j + 1) * 512], p.v[0:2, :], bm.v[:, j * 512:(j + 1) * 512], ALU.add)
    k.stv("sp", cfg["mod"], ob.v[:, :])


def build_program(T, CTX, L, stop_after=None):
    TT = T + CTX
    nc = bass.Bass("TRN2", target_bir_lowering=False, dynamic_dma_scratch_size=8192)

    def din(name, shape, dt=F32):
        return nc.dram_tensor(name, list(shape), dt, kind="ExternalInput").ap()

    def scr(name, shape, dt=F32):
        return nc.dram_tensor(name, list(shape), dt, kind="Internal").ap()

    I = dict(
        xin=din("xin", [TT, D]), cvec=din("cvec", [2, D]),
        w_mod=din("w_mod", [L, D, 9 * D]), b_mod=din("b_mod", [L, 9 * D]), norm_g=din("norm_g", [L, 3, D]),
        ffn_w_in=din("ffn_w_in", [L, 2, D, 2 * DFF]), ffn_w_out=din("ffn_w_out", [L, 2, DFF, D]),
        w_in=din("w_in", [L, D, 3456]), w_out=din("w_out", [L, D, D]), final_g=din("final_g", [D]),
        w_sw=din("w_sw", [L, D, 1280]), w_d=din("w_d", [L, 2, D, 896]), pcols=din("pcols", [L, 128, 24]),
        mu=din("mu", [L, 2, 896]), w2=din("w2", [L, 2, 64, 256]), a2=din("a2", [L, 2, 64, 256]),
        g2=din("g2", [L, 128, 256]), ret_g=din("ret_g", [L, 256]), ln_g=din("ln_g", [L, 256]),
        ln_b=din("ln_b", [L, 256]), sink=din("sink", [L, 4]),
    )
    Cn = dict(
        ident=din("ident", [128, 128]), bd64=din("bd64", [128, 128]), e2=din("e2", [128, 2]),
        ropeA_c=din("ropeA_c", [128, T]), ropeA_s=din("ropeA_s", [128, T]),
        ropeS_c=din("ropeS_c", [128, T]), ropeS_s=din("ropeS_s", [128, T]),
        ret_tab=din("ret_tab", [128, 2, 2, 3, CH]), ret_pc=din("ret_pc", [64, 8]),
        m_st=din("m_st", [64, 512]), m_ts=din("m_ts", [64, 512]), m_in=din("m_in", [64, 512]),
        id8=din("id8", [64, 512]), mprev=din("mprev", [128, 128]), mnext=din("mnext", [128, 128]),
    )
    out = nc.dram_tensor("out", [T, D], F32, kind="ExternalOutput").ap()
    dbg = stop_after is not None
    mk = (lambda name, shape, dt=F32: nc.dram_tensor(name, list(shape), dt, kind="ExternalOutput").ap()) if dbg else scr
    S = dict(
        xres=mk("xres", [TT, D]), mod=mk("modr", [2, 9 * D]), hT=mk("hT", [D, TT], BF16),
        QA=mk("QA", [256, TT], BF16), KA=mk("KA", [128, TT], BF16), VA=mk("VA", [TT, 128], BF16),
        QB=mk("QB", [256, TT], BF16), KB=mk("KB", [128, TT], BF16), VB=mk("VB", [TT, 128], BF16),
        O=mk("O", [TT, 1024]),
        C_Rt=mk("C_Rt", [2, 256, TT], BF16), C_Kt=mk("C_Kt", [2, 256, TT], BF16), C_Kh=mk("C_Kh", [2, TT, 256], BF16),
        C_V=mk("C_V", [TT, 256], BF16), C_G=mk("C_G", [TT, 512]), C_Y=mk("C_Y", [2, TT, 256]),
        D_Rt=mk("D_Rt", [2, 256, TT], BF16), D_Kt=mk("D_Kt", [2, 256, TT], BF16), D_At=mk("D_At", [2, 256, TT], BF16),
        D_Bt=mk("D_Bt", [2, 256, TT], BF16), D_Kh=mk("D_Kh", [2, TT, 256], BF16), D_Bh=mk("D_Bh", [2, TT, 256], BF16),
        D_Atok=mk("D_Atok", [2, TT, 256], BF16), D_V=mk("D_V", [2, TT, 256], BF16), D_Y=mk("D_Y", [2, TT, 256]),
        D_PC=mk("D_PC", [2, 256, TT // CH]), D_G=mk("D_G", [TT, 256]), D_BS=mk("D_BS", [2, TT, 4]),
    )
    k = K(nc)
    stages = []

    def phase(name, fn, cfg):
        if stop_after is not None and stop_after in stages:
            return
        with contextlib.ExitStack() as st:
            fn(k, st, cfg)
            k.barrier()
        stages.append(name)
        k.marks.append((name, {e.name: e.n for e in k.E.values()}))

    with k.stack:
        segs_all = [(0, T, 0), (T, CTX, 1)]
        for l in range(L):
            last = l == L - 1
            W = dict(w_in=I["w_in"][l], w_sw=I["w_sw"][l], w_d=I["w_d"][l], mu=I["mu"][l], pcols=I["pcols"][l],
                     w2=I["w2"][l], a2=I["a2"][l], g2=I["g2"][l], ret_g=I["ret_g"][l], ln_g=I["ln_g"][l],
                     ln_b=I["ln_b"][l], sink=I["sink"][l])
            base = dict(S=S, C=Cn, W=W, T=T, CTX=CTX)
            phase("mod%d" % l, modphase, dict(cvec=I["cvec"], w_mod=I["w_mod"][l], b_mod=I["b_mod"][l], mod=S["mod"]))
            phase("f1_%d" % l, rowphase, dict(
                xin=I["xin"] if l == 0 else S["xres"], xout=S["xres"], segs=segs_all, mod=S["mod"], ident=Cn["ident"],
                ffn=dict(w_in=I["ffn_w_in"][l, 0], w_out=I["ffn_w_out"][l, 0], g=I["norm_g"][l, 0], shift=0, scale=1,
                         gate=2),
                post=dict(kind="hT", g=I["norm_g"][l, 1], shift=3, scale=4, hT=S["hT"])))
            phase("proj%d" % l, projphase, base)
            phase("attn%d" % l, attnphase, dict(base, need_ctx=not last))
            phase("scanC%d" % l, scanphase, dict(base, dplr=False))
            phase("scanD%d" % l, scanphase, dict(base, dplr=True))
            phase("post%d" % l, postphase, base)
            phase("f2_%d" % l, rowphase, dict(
                xin=S["xres"], xout=S["xres"], segs=segs_all if not last else [(0, T, 0)], mod=S["mod"],
                ident=Cn["ident"],
                pre=dict(o=S["O"], w_out=I["w_out"][l], gate=5),
                ffn=dict(w_in=I["ffn_w_in"][l, 1], w_out=I["ffn_w_out"][l, 1], g=I["norm_g"][l, 2], shift=6, scale=7,
                         gate=8),
                post=dict(kind="final", g=I["final_g"], out=out) if last else None))
    return nc, k


def _consts(T):
    c = {}
    c["ident"] = np.eye(128, dtype=np.float32)
    bd = np.zeros((128, 128), np.float32)
    bd[:64, :64] = 1
    bd[64:, 64:] = 1
    c["bd64"] = bd
    e2 = np.zeros((128, 2), np.float32)
    e2[:64, 0] = 1
    e2[64:, 1] = 1
    c["e2"] = e2
    t = np.arange(T)
    row = (t // 64).astype(np.float32)
    col = (t % 64).astype(np.float32)
    inv16 = (1.0 / (np.float32(10000.0) ** (np.arange(0, 32, 2, dtype=np.float32) / np.float32(32)))).astype(np.float32)
    inv32 = (1.0 / (np.float32(10000.0) ** (np.arange(0, 64, 2, dtype=np.float32) / np.float32(64)))).astype(np.float32)
    ca = np.zeros((64, T), np.float32)
    sa = np.zeros((64, T), np.float32)
    for blk, pos in ((0, row), (32, col)):
        ang = pos[None, :] * inv16[:, None]
        ca[blk:blk + 16] = np.cos(ang)
        ca[blk + 16:blk + 32] = np.cos(ang)
        sa[blk:blk + 16] = -np.sin(ang)
        sa[blk + 16:blk + 32] = np.sin(ang)
    ang = t.astype(np.float32)[None, :] * inv32[:, None]
    cs = np.concatenate([np.cos(ang), np.cos(ang)], 0).astype(np.float32)
    ss = np.concatenate([-np.sin(ang), np.sin(ang)], 0).astype(np.float32)
    c["ropeA_c"] = np.concatenate([ca, ca], 0)
    c["ropeA_s"] = np.concatenate([sa, sa], 0)
    c["ropeS_c"] = np.concatenate([cs, cs], 0)
    c["ropeS_s"] = np.concatenate([ss, ss], 0)
    lg = np.log1p(-np.exp2(-5.0 - np.arange(4, dtype=np.float64)))
    i = np.arange(CH, dtype=np.float64)
    rt = np.zeros((128, 2, 2, 3, CH), np.float64)
    for p in range(128):
        for g in range(2):
            l_ = lg[2 * g + p // 64]
            for d in range(2):
                csum = (i + 1) * l_ if d == 0 else (CH - i) * l_
                rt[p, g, d, 0] = np.exp(csum)
                rt[p, g, d, 1] = np.exp(-csum) / 8.0
                rt[p, g, d, 2] = np.exp(CH * l_ - csum) / 8.0
    c["ret_tab"] = rt.astype(np.float32)
    c["ret_pc"] = np.tile(np.exp(CH * lg)[None, :], (64, 2)).astype(np.float32)
    s_ = np.arange(64)[:, None]
    t_ = np.arange(64)[None, :]
    lt = (s_ < t_).astype(np.float32)
    gt = (s_ > t_).astype(np.float32)
    eye = np.eye(64, dtype=np.float32)
    c["m_st"] = np.concatenate([lt] * 4 + [gt] * 4, 1)
    c["m_ts"] = np.concatenate([gt] * 4 + [lt] * 4, 1)
    c["m_in"] = np.concatenate([lt + eye] * 4 + [gt + eye] * 4, 1)
    c["id8"] = np.concatenate([eye] * 8, 1)
    j = np.arange(128)[:, None]
    ii = np.arange(128)[None, :]
    c["mprev"] = (ii <= j).astype(np.float32)
    c["mnext"] = (j <= ii).astype(np.float32)
    return c


def _host_maps(inp, T, CTX, L):
    f = lambda a: np.ascontiguousarray(np.asarray(a, dtype=np.float32))
    w_in = f(inp["w_in"])
    pa = np.concatenate([np.arange(16, 32), np.arange(0, 16), np.arange(48, 64), np.arange(32, 48)])
    psq = np.concatenate([np.arange(32, 64), np.arange(0, 32)])
    cols = []
    for base, nh, perm in ((0, 4, pa), (256, 2, pa), (512, 4, pa), (768, 2, pa), (1024, 4, psq), (1280, 4, psq)):
        for h in range(nh):
            cols.append(base + h * 64 + perm)
    cols = np.concatenate(cols)
    w_sw = np.ascontiguousarray(w_in[:, :, cols])
    dcols = [np.concatenate([np.arange(2304, 3072), np.arange(3200, 3264), np.arange(3328, 3392)]),
             np.concatenate([np.arange(2304, 3072), np.arange(3264, 3328), np.arange(3392, 3456)])]
    w_d = np.ascontiguousarray(np.stack([w_in[:, :, dc] for dc in dcols], 1))
    qkg = f(inp["qk_norm_g"])
    pcols = np.zeros((L, 128, 24), np.float32)
    two = lambda v: np.ascontiguousarray(v.reshape(2, 128).T)
    for l in range(L):
        for qk in range(2):
            g = qkg[l, qk]
            pcols[l, :, 2 * qk] = np.tile(g, 2)
            pcols[l, :, 2 * qk + 1] = np.tile(g[pa], 2)
        for d in range(2):
            pcols[l, :, 4 + 2 * d:6 + 2 * d] = two(f(inp["rwkv_w0"])[l, d])
            pcols[l, :, 8 + 2 * d:10 + 2 * d] = two(f(inp["rwkv_a0"])[l, d])
            pcols[l, :, 16 + 2 * d:18 + 2 * d] = two(f(inp["rwkv_rho"])[l, d].reshape(256))
        pcols[l, :, 12:14] = two(f(inp["rwkv_k_k"])[l])
        pcols[l, :, 14:16] = two(f(inp["rwkv_k_a"])[l])
    shared = dict(
        w_mod=f(inp["w_mod"]), b_mod=f(inp["b_mod"]), norm_g=f(inp["norm_g"]), ffn_w_in=f(inp["ffn_w_in"]),
        ffn_w_out=f(inp["ffn_w_out"]), w_in=w_in, w_out=f(inp["w_out"]), final_g=f(inp["final_norm_g"]),
        w_sw=w_sw, w_d=w_d, pcols=pcols, mu=f(inp["rwkv_mu"]), w2=f(inp["rwkv_w2"]), a2=f(inp["rwkv_a2"]),
        g2=f(inp["rwkv_g2"]), ret_g=f(inp["ret_norm_g"]), ln_g=f(inp["rwkv_ln_g"]), ln_b=f(inp["rwkv_ln_b"]),
        sink=f(inp["attn_sink"]))
    shared.update(_consts(T))
    x, c, ctx, c_ctx = f(inp["x"]), f(inp["c"]), f(inp["ctx"]), f(inp["c_ctx"])
    maps = []
    for b in range(x.shape[0]):
        m = dict(shared)
        m["xin"] = np.ascontiguousarray(np.concatenate([x[b], ctx[b]], 0))
        m["cvec"] = np.ascontiguousarray(np.stack([c[b], c_ctx], 0))
        maps.append(m)
    return maps


_PROG = {}


def kernel(**inputs):
    x = np.asarray(inputs["x"])
    B, T, _ = x.shape
    CTX = np.asarray(inputs["ctx"]).shape[1]
    L = np.asarray(inputs["w_in"]).shape[0]
    key = (T, CTX, L)
    if key not in _PROG:
        _PROG[key] = build_program(T, CTX, L)[0]
    nc = _PROG[key]
    maps = _host_maps(inputs, T, CTX, L)
    res = run_bass_kernel_spmd(nc, maps, core_ids=list(range(B)))
    return np.stack([np.asarray(r["out"], dtype=np.float32) for r in res.results], 0)
```

```python
import contextlib
import numpy as np
import concourse.bass as bass
import concourse.mybir as mybir
from concourse.bass_utils import run_bass_kernel_spmd

ACT = mybir.ActivationFunctionType
ALU = mybir.AluOpType
AX = mybir.AxisListType
F32 = mybir.dt.float32
BF16 = mybir.dt.bfloat16

D = 1024
DFF = 2816
NMOD = 9
RMS_EPS = 1e-6
STRICT = True


class PSem:
    def __init__(self, h):
        self.h = h
        self.val = 0


class Eng:
    def __init__(self, name, h, sem):
        self.name, self.h, self.sem = name, h, sem
        self.n = 0
        self.seen = {}
        self.dseen = {}


class Buf:
    def __init__(self, k, t, name):
        self.k, self.t, self.name = k, t, name
        self.w = None
        self.r = {}
        self.psem = None
        self.dma_w = 0
        self.dma_rw = 0

    def __getitem__(self, key):
        return self.t[key]

    @property
    def v(self):
        return _VA(self)


class View:
    def __init__(self, buf, ap):
        self.buf, self.ap = buf, ap

    def rr(self, pat, **kw):
        return View(self.buf, self.ap.rearrange(pat, **kw))

    def bc(self, shape):
        return View(self.buf, self.ap.to_broadcast(list(shape)))

    def __getitem__(self, key):
        return View(self.buf, self.ap[key])


class _VA:
    def __init__(self, b):
        self.b = b

    def __getitem__(self, key):
        return View(self.b, self.b.t[key])


def _bufs(*xs):
    out = []
    for x in xs:
        if isinstance(x, View) and x.buf not in out:
            out.append(x.buf)
    return out


def _ap(x):
    return x.ap if isinstance(x, View) else x


class K:
    def __init__(self, nc):
        self.nc = nc
        self.stack = contextlib.ExitStack()
        self.E = {}
        for name, h in (("pe", nc.tensor), ("act", nc.scalar), ("dve", nc.vector),
                        ("pool", nc.gpsimd), ("sp", nc.sync)):
            sem = self.stack.enter_context(nc.semaphore("s_" + name))
            self.E[name] = Eng(name, h, sem)
        self.free_psems = []
        self.n_psems = 0
        self.all_psems = []
        self.bufs = []
        self.uid = 0
        self.ninstr = 0
        self.marks = []

    def _psem(self):
        if self.free_psems:
            return self.free_psems.pop()
        self.n_psems += 1
        h = self.stack.enter_context(self.nc.semaphore("d%d" % self.n_psems))
        p = PSem(h)
        self.all_psems.append(p)
        return p

    def sb(self, st, shape, dtype, name):
        self.uid += 1
        t = st.enter_context(self.nc.sbuf_tensor("%s_%d" % (name, self.uid), list(shape), dtype))
        b = Buf(self, t, name)
        st.callback(self._release, b)
        return b

    def ps(self, st, shape, dtype, name):
        self.uid += 1
        t = st.enter_context(self.nc.psum_tensor("%s_%d" % (name, self.uid), list(shape), dtype))
        return Buf(self, t, name)

    def _release(self, b):
        if b.psem is not None and not getattr(b.psem, "sw", False):
            self.free_psems.append(b.psem)
        b.psem = None

    def _wait_eng(self, E, X, n):
        if E.seen.get(X.name, 0) >= n:
            return
        E.h.wait_ge(X.sem, n)
        E.seen[X.name] = n
        self.ninstr += 1

    def _wait_d(self, E, p, v):
        if v <= 0 or E.dseen.get(id(p), 0) >= v:
            return
        E.h.wait_ge(p.h, v)
        E.dseen[id(p)] = v
        self.ninstr += 1

    def _deps(self, E, r, w):
        for b in r:
            if b.w is not None:
                self._wait_eng(E, b.w[0], b.w[1])
            if b.psem is not None:
                self._wait_d(E, b.psem, b.dma_w)
        for b in w:
            if b.w is not None and (STRICT or b.w[0] is not E) and not (E.name == "pe" and b.w[0] is E):
                self._wait_eng(E, b.w[0], b.w[1])
            for X, n in b.r.items():
                if STRICT or X is not E:
                    self._wait_eng(E, X, n)
            if b.psem is not None:
                self._wait_d(E, b.psem, b.dma_rw)

    def op(self, eng, fn, r=(), w=()):
        E = self.E[eng]
        self._deps(E, r, w)
        ins = fn(E.h)
        E.n += 1
        ins.then_inc(E.sem, 1)
        self.ninstr += 1
        for b in w:
            b.w = (E, E.n)
            b.r = {}
        for b in r:
            if b.w is None or b.w[0] is not E or b.w[1] != E.n:
                b.r[E] = E.n
        return ins

    def dma(self, q, out, in_, r=(), w=(), **kw):
        Q = self.E[q]
        self._deps(Q, r, w)
        anchor = w[0] if w else r[0]
        if anchor.psem is None:
            if q == "pool":
                self.n_psems += 1
                anchor.psem = PSem(self.stack.enter_context(self.nc.semaphore("w%d" % self.n_psems)))
                anchor.psem.sw = True
                self.all_psems.append(anchor.psem)
            else:
                anchor.psem = self._psem()
        p = anchor.psem
        assert getattr(p, "sw", False) == (q == "pool"), "buffer mixes SW and HW DGE DMAs: " + anchor.name
        Q.h.dma_start(out=out, in_=in_, **kw).then_inc(p.h, 16)
        p.val += 16
        self.ninstr += 1
        anchor.dma_rw = p.val
        if w:
            anchor.dma_w = p.val
            anchor.w = None
            anchor.r = {}

    def mm(self, out, lhsT, rhs, start=True, stop=True):
        return self.op("pe", lambda e: e.matmul(out.ap, lhsT.ap, rhs.ap, start=start, stop=stop),
                       r=_bufs(lhsT, rhs), w=[out.buf])

    def tr(self, out, in_, ident):
        return self.op("pe", lambda e: e.transpose(out.ap, in_.ap, ident.ap), r=_bufs(in_, ident), w=[out.buf])

    def tt(self, eng, out, in0, in1, op):
        return self.op(eng, lambda e: e.tensor_tensor(out=out.ap, in0=in0.ap, in1=in1.ap, op=op),
                       r=_bufs(in0, in1), w=[out.buf])

    def ts(self, eng, out, in0, s1, s2, op0, op1=None):
        kw = {} if op1 is None else dict(op1=op1)
        return self.op(eng, lambda e: e.tensor_scalar(out=out.ap, in0=in0.ap, scalar1=_ap(s1), scalar2=_ap(s2),
                                                      op0=op0, **kw), r=_bufs(in0, s1, s2), w=[out.buf])

    def stt(self, eng, out, in0, scalar, in1, op0, op1):
        return self.op(eng, lambda e: e.scalar_tensor_tensor(out=out.ap, in0=in0.ap, scalar=_ap(scalar), in1=in1.ap,
                                                             op0=op0, op1=op1), r=_bufs(in0, scalar, in1), w=[out.buf])

    def act(self, out, in_, func, bias=None, scale=None, accum=None):
        kw = {}
        if bias is not None:
            kw["bias"] = _ap(bias)
        if scale is not None:
            kw["scale"] = _ap(scale)
        w = [out.buf]
        if accum is not None:
            kw["accum_out"] = accum.ap
            w.append(accum.buf)
        return self.op("act", lambda e: e.activation(out=out.ap, in_=in_.ap, func=func, **kw),
                       r=_bufs(in_, bias, scale), w=w)

    def cp(self, eng, out, in_):
        if eng == "act":
            return self.op("act", lambda e: e.copy(out=out.ap, in_=in_.ap), r=[in_.buf], w=[out.buf])
        return self.op(eng, lambda e: e.tensor_copy(out=out.ap, in_=in_.ap), r=[in_.buf], w=[out.buf])

    def memset(self, eng, out, val):
        return self.op(eng, lambda e: e.memset(out.ap, val), w=[out.buf])

    def recip(self, out, in_):
        return self.op("dve", lambda e: e.reciprocal(out=out.ap, in_=in_.ap), r=[in_.buf], w=[out.buf])

    def ld(self, q, out, dram, **kw):
        return self.dma(q, out.ap, dram, w=[out.buf], **kw)

    def stv(self, q, dram, in_, **kw):
        return self.dma(q, dram, in_.ap, r=[in_.buf], **kw)

    def barrier(self):
        for E in self.E.values():
            for X in self.E.values():
                if X is not E and X.n > 0:
                    self._wait_eng(E, X, X.n)
            for p in self.all_psems:
                self._wait_d(E, p, p.val)


def mm(k, out_ap, lhsT_ap, rhs_ap, start, stop, r, w):
    return k.op("pe", lambda e: e.matmul(out_ap, lhsT_ap, rhs_ap, start=start, stop=stop), r=r, w=w)


def rowphase(k, st, cfg):
    nc = k.nc
    ffn = cfg["ffn"]
    pre = cfg.get("pre")
    post = cfg.get("post")
    w1 = k.sb(st, [128, 8, 2 * DFF], BF16, "w1")
    w2 = k.sb(st, [128, 22, D], BF16, "w2")
    w1v = ffn["w_in"].rearrange("(c p) f -> p c f", p=128)
    for c in range(8):
        for hlf in range(4):
            f0 = hlf * 1408
            k.dma("pool", w1[:, c, f0:f0 + 1408], w1v[:, c, f0:f0 + 1408], w=[w1])
    w2v = ffn["w_out"].rearrange("(c p) d -> p c d", p=128)
    for c in range(22):
        k.dma("pool", w2[:, c, :], w2v[:, c, :], w=[w2])
    if pre is not None:
        wo = k.sb(st, [128, 8, D], BF16, "wo")
        wov = pre["w_out"].rearrange("(c p) d -> p c d", p=128)
        for c in range(8):
            k.dma("pool", wo[:, c, :], wov[:, c, :], w=[wo])

    identf = k.sb(st, [128, 128], F32, "identf")
    k.dma("sp", identf[:, :], cfg["ident"], w=[identf])
    xt = [k.sb(st, [128, D], F32, "xt%d" % i) for i in range(4)]
    hT = k.sb(st, [128, 8, 512], BF16, "hT")
    gT = k.sb(st, [128, 22, 512], BF16, "gT")
    sg = k.sb(st, [128, 512], BF16, "sg")
    tmp = k.sb(st, [128, D], F32, "tmp")
    ssq = k.sb(st, [128, 1], F32, "ssq")
    rinv = k.sb(st, [128, 1], F32, "rinv")
    ps_u = [k.ps(st, [128, 512], F32, "psu%d" % i) for i in range(4)]
    ps_y = [k.ps(st, [128, 512], F32, "psy%d" % i) for i in range(2)]
    ps_t = k.ps(st, [128, D], F32, "pst")
    Ac = k.sb(st, [128, 8], F32, "Ac")
    Bc = k.sb(st, [128, 8], F32, "Bc")
    Gc = k.sb(st, [128, 8], F32, "Gc")
    A2c = k.sb(st, [128, 8], F32, "A2c")
    B2c = k.sb(st, [128, 8], F32, "B2c")
    G = k.sb(st, [128, D], F32, "G")
    PG = k.sb(st, [128, D], F32, "PG") if pre is not None else None
    gvec2 = None
    if post is not None and post["kind"] == "final":
        gvec2 = k.sb(st, [128, D], F32, "gvec2")
        k.dma("sp", gvec2[:, :], post["g"].partition_broadcast(128), w=[gvec2])

    def col(ap1d):
        return ap1d.rearrange("(c p) -> p c", p=128)

    def load_cols(dst, ap1d):
        k.dma("sp", dst[:, :], col(ap1d), w=[dst], allow_slow_non_contiguous=True)

    def setup_mod(mr):
        load_cols(Gc, ffn["g"])
        load_cols(Ac, cfg["mod"][mr, ffn["scale"] * D:(ffn["scale"] + 1) * D])
        load_cols(Bc, cfg["mod"][mr, ffn["shift"] * D:(ffn["shift"] + 1) * D])
        k.op("dve", lambda e: e.scalar_tensor_tensor(out=Ac[:, :], in0=Ac[:, :], scalar=1.0, in1=Gc[:, :],
                                                     op0=ALU.add, op1=ALU.mult), r=[Ac, Gc], w=[Ac])
        if post is not None and post["kind"] == "hT":
            load_cols(Gc, post["g"])
            load_cols(A2c, cfg["mod"][mr, post["scale"] * D:(post["scale"] + 1) * D])
            load_cols(B2c, cfg["mod"][mr, post["shift"] * D:(post["shift"] + 1) * D])
            k.op("dve", lambda e: e.scalar_tensor_tensor(out=A2c[:, :], in0=A2c[:, :], scalar=1.0, in1=Gc[:, :],
                                                         op0=ALU.add, op1=ALU.mult), r=[A2c, Gc], w=[A2c])
        k.dma("sp", G[:, :], cfg["mod"][mr, ffn["gate"] * D:(ffn["gate"] + 1) * D].partition_broadcast(128), w=[G])
        k.op("dve", lambda e: e.tensor_scalar(out=G[:, :], in0=G[:, :], scalar1=0.5, scalar2=None, op0=ALU.mult),
             r=[G], w=[G])
        if pre is not None:
            k.dma("sp", PG[:, :], cfg["mod"][mr, pre["gate"] * D:(pre["gate"] + 1) * D].partition_broadcast(128),
                  w=[PG])

    def rms_rinv(xb):
        k.op("act", lambda e: e.activation(out=tmp[:, :], in_=xb[:, :], func=ACT.Square, accum_out=ssq[:, :]),
             r=[xb], w=[tmp, ssq])
        k.op("dve", lambda e: e.tensor_scalar(out=rinv[:, :], in0=ssq[:, :], scalar1=1.0 / D, scalar2=RMS_EPS,
                                              op0=ALU.mult, op1=ALU.add), r=[ssq], w=[rinv])
        k.op("act", lambda e: e.activation(out=rinv[:, :], in_=rinv[:, :], func=ACT.Sqrt), r=[rinv], w=[rinv])
        k.op("dve", lambda e: e.reciprocal(out=rinv[:, :], in_=rinv[:, :]), r=[rinv], w=[rinv])

    def norm_mod_T(xb, A, B, dstT, s):
        rms_rinv(xb)
        k.op("dve", lambda e: e.tensor_scalar(out=tmp[:, :], in0=xb[:, :], scalar1=rinv[:, 0:1], scalar2=None,
                                              op0=ALU.mult), r=[xb, rinv], w=[tmp])
        for c in range(8):
            k.op("pe", lambda e: e.transpose(ps_t[:, c * 128:(c + 1) * 128], tmp[:, c * 128:(c + 1) * 128],
                                             identf[:, :]), r=[tmp, identf], w=[ps_t])
        for c in range(8):
            if c % 2 == 0:
                k.op("dve", lambda e: e.tensor_scalar(out=dstT[:, c, s * 128:(s + 1) * 128],
                                                      in0=ps_t[:, c * 128:(c + 1) * 128], scalar1=A[:, c:c + 1],
                                                      scalar2=B[:, c:c + 1], op0=ALU.mult, op1=ALU.add),
                     r=[ps_t, A, B], w=[dstT])
            else:
                k.op("act", lambda e: e.activation(out=dstT[:, c, s * 128:(s + 1) * 128],
                                                   in_=ps_t[:, c * 128:(c + 1) * 128], func=ACT.Identity,
                                                   scale=A[:, c:c + 1], bias=B[:, c:c + 1]),
                     r=[ps_t, A, B], w=[dstT])

    ui = 0
    yi = 0
    for (tok0, ntok, mr) in cfg["segs"]:
        setup_mod(mr)
        for t0 in range(tok0, tok0 + ntok, 512):
            n = min(512, tok0 + ntok - t0)
            ns = n // 128
            for s in range(ns):
                k.dma("sp", xt[s][:, :], cfg["xin"][t0 + s * 128:t0 + (s + 1) * 128, :], w=[xt[s]])
            if pre is not None:
                for s in range(ns):
                    k.dma("sp", tmp[:, :], pre["o"][t0 + s * 128:t0 + (s + 1) * 128, :], w=[tmp])
                    for c in range(8):
                        k.op("pe", lambda e: e.transpose(ps_t[:, c * 128:(c + 1) * 128], tmp[:, c * 128:(c + 1) * 128],
                                                         identf[:, :]), r=[tmp, identf], w=[ps_t])
                    k.op("act", lambda e: e.copy(out=hT[:, :, s * 128:(s + 1) * 128],
                                                 in_=ps_t[:, :].rearrange("p (c t) -> p c t", c=8)), r=[ps_t], w=[hT])
                for s in range(ns):
                    for hf in range(2):
                        py = ps_y[yi % 2]
                        yi += 1
                        for c in range(8):
                            mm(k, py[:, :], hT[:, c, s * 128:(s + 1) * 128], wo[:, c, hf * 512:(hf + 1) * 512],
                               c == 0, c == 7, r=[hT, wo], w=[py])
                        sl = slice(hf * 512, (hf + 1) * 512)
                        k.op("dve", lambda e, py=py, sl=sl: e.tensor_tensor(out=tmp[:, sl], in0=py[:, :],
                                                                             in1=PG[:, sl], op=ALU.mult),
                             r=[py, PG], w=[tmp])
                        k.op("dve", lambda e, s=s, sl=sl: e.tensor_tensor(out=xt[s][:, sl], in0=xt[s][:, sl],
                                                                            in1=tmp[:, sl], op=ALU.add),
                             r=[xt[s], tmp], w=[xt[s]])
            for s in range(ns):
                norm_mod_T(xt[s], Ac, Bc, hT, s)
            for fc in range(22 if cfg.get("stage", 9) >= 2 else 0):
                p1 = ps_u[ui % 4]
                p2 = ps_u[(ui + 1) % 4]
                ui += 2
                for c in range(8):
                    mm(k, p1[:, 0:n], w1[:, c, fc * 128:(fc + 1) * 128], hT[:, c, 0:n], c == 0, c == 7,
                       r=[w1, hT], w=[p1])
                for c in range(8):
                    mm(k, p2[:, 0:n], w1[:, c, DFF + fc * 128:DFF + (fc + 1) * 128], hT[:, c, 0:n], c == 0, c == 7,
                       r=[w1, hT], w=[p2])
                k.op("act", lambda e, p1=p1: e.activation(out=sg[:, 0:n], in_=p1[:, 0:n], func=ACT.Silu),
                     r=[p1], w=[sg])
                k.op("dve", lambda e, p2=p2, fc=fc: e.tensor_tensor(out=gT[:, fc, 0:n], in0=sg[:, 0:n],
                                                                     in1=p2[:, 0:n], op=ALU.mult),
                     r=[sg, p2], w=[gT])
            for s in range(ns if cfg.get("stage", 9) >= 3 else 0):
                for hf in range(2):
                    py = ps_y[yi % 2]
                    yi += 1
                    for fc in range(22):
                        mm(k, py[:, :], gT[:, fc, s * 128:(s + 1) * 128], w2[:, fc, hf * 512:(hf + 1) * 512],
                           fc == 0, fc == 21, r=[gT, w2], w=[py])
                    sl = slice(hf * 512, (hf + 1) * 512)
                    k.op("dve", lambda e, py=py, sl=sl: e.tensor_tensor(out=tmp[:, sl], in0=py[:, :], in1=G[:, sl],
                                                                         op=ALU.mult), r=[py, G], w=[tmp])
                    k.op("dve", lambda e, s=s, sl=sl: e.tensor_tensor(out=xt[s][:, sl], in0=xt[s][:, sl],
                                                                        in1=tmp[:, sl], op=ALU.add),
                         r=[xt[s], tmp], w=[xt[s]])
            for s in range(ns):
                r0 = t0 + s * 128
                if post is not None and post["kind"] == "final":
                    if mr == 0:
                        rms_rinv(xt[s])
                        k.op("dve", lambda e, s=s: e.scalar_tensor_tensor(out=tmp[:, :], in0=xt[s][:, :],
                                                                           scalar=rinv[:, 0:1], in1=gvec2[:, :],
                                                                           op0=ALU.mult, op1=ALU.mult),
                             r=[xt[s], rinv, gvec2], w=[tmp])
                        k.dma("sp", post["out"][r0:r0 + 128, :], tmp[:, :], r=[tmp])
                    continue
                k.dma("sp", cfg["xout"][r0:r0 + 128, :], xt[s][:, :], r=[xt[s]])
                if post is not None and post["kind"] == "hT":
                    norm_mod_T(xt[s], A2c, B2c, hT, s)
            if post is not None and post["kind"] == "hT":
                for c in range(8):
                    k.dma("sp", post["hT"][c * 128:(c + 1) * 128, t0:t0 + n], hT[:, c, 0:n], r=[hT])


CH = 64
DECAY_SCALE = 0.6065306597126334


def projphase(k, st, cfg):
    S = cfg["S"]
    C = cfg["C"]
    Wt = cfg["W"]
    T, CTX = cfg["T"], cfg["CTX"]
    hTd = S["hT"]
    wabc = k.sb(st, [128, 8, 2304], BF16, "wabc")
    wgd = k.sb(st, [128, 8, 128], BF16, "wgd")
    wsw = k.sb(st, [128, 8, 1280], BF16, "wsw")
    wv = Wt["w_in"].rearrange("(c p) f -> p c f", p=128)
    for c in range(8):
        k.ld("pool", wabc.v[:, c, 0:1152], wv[:, c, 0:1152])
        k.ld("pool", wabc.v[:, c, 1152:2304], wv[:, c, 1152:2304])
        k.ld("pool", wgd.v[:, c, :], wv[:, c, 3072:3200])
        k.ld("pool", wsw.v[:, c, :], Wt["w_sw"].rearrange("(c p) f -> p c f", p=128)[:, c, :])
    wda = [k.sb(st, [128, 8, 896], BF16, "wda%d" % d) for d in range(2)]
    wdb = [k.sb(st, [128, 8, 896], BF16, "wdb%d" % d) for d in range(2)]
    with contextlib.ExitStack() as st2:
        wst = k.sb(st2, [128, 8, 896], F32, "wst")
        mub = k.sb(st2, [128, 896], F32, "mub")
        wtm = k.sb(st2, [128, 8, 896], F32, "wtm")
        for d in range(2):
            for c in range(8):
                k.ld("sp", wst.v[:, c, :], Wt["w_d"][d].rearrange("(c p) f -> p c f", p=128)[:, c, :])
            k.ld("sp", mub.v[:, :], Wt["mu"][d].partition_broadcast(128))
            for c in range(8):
                k.tt("dve", wtm.v[:, c, :], wst.v[:, c, :], mub.v[:, :], ALU.mult)
            k.cp("act", wda[d].v[:, :, :], wtm.v[:, :, :])
            k.tt("dve", wdb[d].v[:, :, :], wst.v[:, :, :], wtm.v[:, :, :], ALU.subtract)
        k.barrier()
    pc = k.sb(st, [128, 24], F32, "pcols")
    k.ld("sp", pc.v[:, :], Wt["pcols"])
    oma = k.sb(st, [128, 2], F32, "oma")
    k.ts("dve", oma.v[:, :], pc.v[:, 14:16], -1.0, 1.0, ALU.mult, ALU.add)
    identf = k.sb(st, [128, 128], F32, "identf")
    k.ld("sp", identf.v[:, :], C["ident"])
    bd = k.sb(st, [128, 128], F32, "bd")
    k.ld("sp", bd.v[:, :], C["bd64"])
    e2 = k.sb(st, [128, 2], F32, "e2")
    k.ld("sp", e2.v[:, :], C["e2"])
    rtab = k.sb(st, [128, 2, 2, 3, CH], F32, "rtab")
    k.ld("sp", rtab.v[:, :, :, :, :], C["ret_tab"])
    w2s = k.sb(st, [64, 2, 256], BF16, "w2s")
    a2s = k.sb(st, [64, 2, 256], BF16, "a2s")
    g2s = k.sb(st, [128, 256], BF16, "g2s")
    for d in range(2):
        k.ld("pool", w2s.v[:, d, :], Wt["w2"][d])
        k.ld("pool", a2s.v[:, d, :], Wt["a2"][d])
    k.ld("pool", g2s.v[:, :], Wt["g2"])

    NT = 512
    hTt = k.sb(st, [128, 8, NT + 2], BF16, "hTt")
    ca = k.sb(st, [128, NT], F32, "ca")
    sa = k.sb(st, [128, NT], F32, "sa")
    cs_ = k.sb(st, [128, NT], F32, "cs")
    ss_ = k.sb(st, [128, NT], F32, "ss")
    gca = k.sb(st, [128, 2, NT], F32, "gca")
    gsa = k.sb(st, [128, 2, NT], F32, "gsa")
    F = [k.sb(st, [128, NT], F32, "f%d" % i) for i in range(12)]
    ob = k.sb(st, [128, NT], BF16, "ob")
    sbf = [k.sb(st, [128, NT], BF16, "sbf%d" % i) for i in range(2)]
    sbi = [0]

    def st_bf(dst, srcv, n):
        b = sbf[sbi[0] % 2]
        sbi[0] += 1
        k.cp("act", b.v[:, 0:n], srcv)
        k.stv("sp", dst, b.v[:, 0:n])
    tb = k.sb(st, [128, 512], F32, "tb")
    tbb = k.sb(st, [128, 512], BF16, "tbb")
    thb = k.sb(st, [64, NT], BF16, "thb")
    alb = k.sb(st, [64, NT], BF16, "alb")
    sgb = k.sb(st, [128, NT], BF16, "sgb")
    bs4 = k.sb(st, [128, 4, 4], F32, "bs4")
    pcx = k.sb(st, [128, NT // CH], F32, "pcx")
    pcx2 = k.sb(st, [128, NT // CH], F32, "pcx2")
    F2 = [k.sb(st, [128, NT], F32, "g%d" % i) for i in range(9)]
    PS = [k.ps(st, [128, 512], F32, "pp%d" % i) for i in range(8)]
    pi = [0]

    def nps():
        p = PS[pi[0] % 8]
        pi[0] += 1
        return p

    def chain(p, wlist, M, n, col0=1):
        tot = len(wlist) * 8
        i = 0
        for (wb, c0, sh) in wlist:
            for c in range(8):
                k.mm(p.v[0:M, 0:n], wb.v[:, c, c0:c0 + M], hTt.v[:, c, col0 + sh:col0 + sh + n], i == 0, i == tot - 1)
                i += 1

    def chain_tok(p, wlist, ncols, s):
        tot = len(wlist) * 8
        i = 0
        for (wb, c0, sh) in wlist:
            for c in range(8):
                k.mm(p.v[:, 0:ncols], hTt.v[:, c, 1 + sh + s * 128:1 + sh + (s + 1) * 128], wb.v[:, c, c0:c0 + ncols],
                     i == 0, i == tot - 1)
                i += 1

    def rope_out(P, Psw, ctab, stab, dst, n):
        k.tt("dve", F[10].v[:, 0:n], P.v[:, 0:n], ctab, ALU.mult)
        k.tt("dve", F[11].v[:, 0:n], Psw.v[:, 0:n], stab, ALU.mult)
        k.tt("dve", dst, F[10].v[:, 0:n], F[11].v[:, 0:n], ALU.add)

    segs = [(0, T, True), (T, CTX, False)]
    for (seg0, seglen, latent) in segs:
        for t0 in range(seg0, seg0 + seglen, NT):
            n = min(NT, seg0 + seglen - t0)
            ns = n // 128
            nch = n // CH
            lo = t0 - 1 if t0 > seg0 else t0
            hi = t0 + n + 1 if t0 + n < seg0 + seglen else t0 + n
            if lo == t0:
                k.memset("dve", hTt.v[:, :, 0:1], 0.0)
            if hi == t0 + n:
                k.memset("dve", hTt.v[:, :, n + 1:n + 2], 0.0)
            for c in range(8):
                k.ld("sp", hTt.v[:, c, 1 - (t0 - lo):1 + n + (hi - t0 - n)], hTd[c * 128:(c + 1) * 128, lo:hi])
            if latent:
                k.ld("sp", ca.v[:, 0:n], C["ropeA_c"][:, t0:t0 + n])
                k.ld("sp", sa.v[:, 0:n], C["ropeA_s"][:, t0:t0 + n])
                k.ld("sp", cs_.v[:, 0:n], C["ropeS_c"][:, t0:t0 + n])
                k.ld("sp", ss_.v[:, 0:n], C["ropeS_s"][:, t0:t0 + n])
                for qk in range(2):
                    k.ts("dve", gca.v[:, qk, 0:n], ca.v[:, 0:n], pc.v[:, 2 * qk:2 * qk + 1], None, ALU.mult)
                    k.ts("dve", gsa.v[:, qk, 0:n], sa.v[:, 0:n], pc.v[:, 2 * qk + 1:2 * qk + 2], None, ALU.mult)
            for grp, base, swb, dq, dk_, dv_ in (("A", 0, 0, S["QA"], S["KA"], S["VA"]),
                                                ("B", 512, 384, S["QB"], S["KB"], S["VB"])):
                for j in range(3):
                    P, Psw = nps(), nps()
                    chain(P, [(wabc, base + j * 128, 0)], 128, n)
                    if latent:
                        chain(Psw, [(wsw, swb + j * 128, 0)], 128, n)
                    qk = 0 if j < 2 else 1
                    if grp == "B":
                        k.act(F[0].v[:, 0:n], P.v[:, 0:n], ACT.Square)
                        pss = nps()
                        k.mm(pss.v[:, 0:n], bd.v[:, :], F[0].v[:, 0:n])
                        k.ts("dve", F[1].v[:, 0:n], pss.v[:, 0:n], 1.0 / 64, RMS_EPS, ALU.mult, ALU.add)
                        k.act(F[1].v[:, 0:n], F[1].v[:, 0:n], ACT.Sqrt)
                        k.recip(F[1].v[:, 0:n], F[1].v[:, 0:n])
                        k.tt("dve", F[2].v[:, 0:n], P.v[:, 0:n], F[1].v[:, 0:n], ALU.mult)
                        if latent:
                            k.tt("dve", F[3].v[:, 0:n], Psw.v[:, 0:n], F[1].v[:, 0:n], ALU.mult)
                            rope_out(F[2], F[3], gca.v[:, qk, 0:n], gsa.v[:, qk, 0:n], ob.v[:, 0:n], n)
                        else:
                            k.ts("dve", ob.v[:, 0:n], F[2].v[:, 0:n], pc.v[:, 2 * qk:2 * qk + 1], None, ALU.mult)
                    else:
                        if latent:
                            rope_out(P, Psw, ca.v[:, 0:n], sa.v[:, 0:n], ob.v[:, 0:n], n)
                        else:
                            k.cp("act", ob.v[:, 0:n], P.v[:, 0:n])
                    dst = dq[j * 128:(j + 1) * 128, t0:t0 + n] if j < 2 else dk_[:, t0:t0 + n]
                    k.stv("sp", dst, ob.v[:, 0:n])
                for s in range(ns):
                    P = nps()
                    chain_tok(P, [(wabc, base + 384, 0)], 128, s)
                    k.cp("act", tbb.v[:, 0:128], P.v[:, 0:128])
                    k.stv("sp", dv_[t0 + s * 128:t0 + (s + 1) * 128, :], tbb.v[:, 0:128])
            for j in range(4):
                P, Psw = nps(), nps()
                chain(P, [(wabc, 1024 + j * 128, 0)], 128, n)
                g = j % 2
                if latent:
                    chain(Psw, [(wsw, 768 + j * 128, 0)], 128, n)
                    rope_out(P, Psw, cs_.v[:, 0:n], ss_.v[:, 0:n], F[0].v[:, 0:n], n)
                else:
                    k.cp("act", F[0].v[:, 0:n], P.v[:, 0:n])
                x3 = F[0].v[:, 0:n].rr("p (c i) -> p c i", i=CH)
                for d in range(2):
                    if j < 2:
                        k.tt("dve", F[1].v[:, 0:n].rr("p (c i) -> p c i", i=CH), x3,
                             rtab.v[:, g, d, 0:1, :].bc([128, nch, CH]), ALU.mult)
                        st_bf(S["C_Rt"][d, g * 128:(g + 1) * 128, t0:t0 + n], F[1].v[:, 0:n], n)
                    else:
                        k.tt("dve", F[1].v[:, 0:n].rr("p (c i) -> p c i", i=CH), x3,
                             rtab.v[:, g, d, 1:2, :].bc([128, nch, CH]), ALU.mult)
                        st_bf(S["C_Kt"][d, g * 128:(g + 1) * 128, t0:t0 + n], F[1].v[:, 0:n], n)
                        k.tt("dve", F[2].v[:, 0:n].rr("p (c i) -> p c i", i=CH), x3,
                             rtab.v[:, g, d, 2:3, :].bc([128, nch, CH]), ALU.mult)
                        for s in range(ns):
                            pt = nps()
                            k.tr(pt.v[:, 0:128], F[2].v[:, s * 128:(s + 1) * 128], identf.v[:, :])
                            k.cp("act", tbb.v[:, 0:128], pt.v[:, 0:128])
                            k.stv("sp", S["C_Kh"][d, t0 + s * 128:t0 + (s + 1) * 128, g * 128:(g + 1) * 128],
                                  tbb.v[:, 0:128])
            for s in range(ns):
                P = nps()
                chain_tok(P, [(wabc, 1536, 0)], 256, s)
                k.cp("act", tbb.v[:, 0:256], P.v[:, 0:256])
                k.stv("sp", S["C_V"][t0 + s * 128:t0 + (s + 1) * 128, :], tbb.v[:, 0:256])
                P = nps()
                chain_tok(P, [(wabc, 1792, 0)], 512, s)
                k.act(tb.v[:, 0:512], P.v[:, 0:512], ACT.Silu)
                k.stv("sp", S["C_G"][t0 + s * 128:t0 + (s + 1) * 128, :], tb.v[:, 0:512])
            P = nps()
            chain(P, [(wgd, 0, 0)], 128, n)
            k.act(sgb.v[:, 0:n], P.v[:, 0:n], ACT.Sigmoid)
            for s in range(ns):
                P = nps()
                k.mm(P.v[:, 0:256], sgb.v[:, s * 128:(s + 1) * 128], g2s.v[:, :])
                k.cp("act", tb.v[:, 0:256], P.v[:, 0:256])
                k.stv("sp", S["D_G"][t0 + s * 128:t0 + (s + 1) * 128, :], tb.v[:, 0:256])
            for d in range(2):
                sh = -1 if d == 0 else 1
                wl = lambda c0: [(wdb[d], c0, 0), (wda[d], c0, sh)]
                P = nps()
                chain(P, wl(768), 64, n)
                k.act(thb.v[:, 0:n], P.v[0:64, 0:n], ACT.Tanh)
                P = nps()
                chain(P, wl(832), 64, n)
                k.cp("act", alb.v[:, 0:n], P.v[0:64, 0:n])
                for s in range(ns):
                    P = nps()
                    chain_tok(P, wl(512), 256, s)
                    k.cp("act", tbb.v[:, 0:256], P.v[:, 0:256])
                    k.stv("sp", S["D_V"][d, t0 + s * 128:t0 + (s + 1) * 128, :], tbb.v[:, 0:256])
                def dgroup(g, FB, pcx):
                    r_, k_, lw, cs, a_, kk, e_, t1, t2 = FB
                    gc = slice(g, g + 1)
                    P = nps()
                    chain(P, wl(g * 128), 128, n)
                    k.cp("act", r_.v[:, 0:n], P.v[:, 0:n])
                    P = nps()
                    chain(P, wl(256 + g * 128), 128, n)
                    k.cp("act", k_.v[:, 0:n], P.v[:, 0:n])
                    yield
                    P = nps()
                    k.mm(P.v[:, 0:n], w2s.v[:, d, g * 128:(g + 1) * 128], thb.v[:, 0:n])
                    k.act(lw.v[:, 0:n], P.v[:, 0:n], ACT.Sigmoid, bias=pc.v[:, 4 + 2 * d + g:5 + 2 * d + g])
                    k.ts("dve", lw.v[:, 0:n], lw.v[:, 0:n], -DECAY_SCALE, None, ALU.mult)
                    yield
                    P = nps()
                    k.mm(P.v[:, 0:n], a2s.v[:, d, g * 128:(g + 1) * 128], alb.v[:, 0:n])
                    k.act(a_.v[:, 0:n], P.v[:, 0:n], ACT.Sigmoid, bias=pc.v[:, 8 + 2 * d + g:9 + 2 * d + g])
                    yield
                    k.ts("dve", kk.v[:, 0:n], k_.v[:, 0:n], pc.v[:, 12 + g:13 + g], None, ALU.mult)
                    k.act(t1.v[:, 0:n], kk.v[:, 0:n], ACT.Square)
                    yield
                    P = nps()
                    k.mm(P.v[:, 0:n], bd.v[:, :], t1.v[:, 0:n])
                    k.act(t1.v[:, 0:n], P.v[:, 0:n], ACT.Sqrt)
                    yield
                    k.ts("dve", t1.v[:, 0:n], t1.v[:, 0:n], 1e-12, None, ALU.max)
                    k.recip(t1.v[:, 0:n], t1.v[:, 0:n])
                    yield
                    k.tt("dve", kk.v[:, 0:n], kk.v[:, 0:n], t1.v[:, 0:n], ALU.mult)
                    yield
                    k.ts("dve", t1.v[:, 0:n], a_.v[:, 0:n], pc.v[:, 14 + g:15 + g], oma.v[:, gc], ALU.mult, ALU.add)
                    k.tt("dve", k_.v[:, 0:n], k_.v[:, 0:n], t1.v[:, 0:n], ALU.mult)
                    yield
                    k.stt("dve", t1.v[:, 0:n], r_.v[:, 0:n], pc.v[:, 16 + 2 * d + g:17 + 2 * d + g], k_.v[:, 0:n],
                          ALU.mult, ALU.mult)
                    for s in range(ns):
                        P = nps()
                        k.mm(P.v[:, 0:2], t1.v[:, s * 128:(s + 1) * 128], e2.v[:, :])
                        k.cp("act", bs4.v[:, s, 2 * g:2 * g + 2], P.v[:, 0:2])
                    k.tt("dve", a_.v[:, 0:n], a_.v[:, 0:n], kk.v[:, 0:n], ALU.mult)
                    yield
                    src, dst = lw, cs
                    k.cp("act", t2.v[:, 0:n], lw.v[:, 0:n])
                    src = t2
                    for stp in (1, 2, 4, 8, 16, 32):
                        s3 = src.v[:, 0:n].rr("p (c i) -> p c i", i=CH)
                        d3 = dst.v[:, 0:n].rr("p (c i) -> p c i", i=CH)
                        if d == 0:
                            k.tt("dve", d3[:, :, stp:], s3[:, :, stp:], s3[:, :, :CH - stp], ALU.add)
                            k.cp("act", d3[:, :, :stp], s3[:, :, :stp])
                        else:
                            k.tt("dve", d3[:, :, :CH - stp], s3[:, :, :CH - stp], s3[:, :, stp:], ALU.add)
                            k.cp("act", d3[:, :, CH - stp:], s3[:, :, CH - stp:])
                        src, dst = dst, src
                        yield
                    cs = src
                    spare = dst
                    cs3 = cs.v[:, 0:n].rr("p (c i) -> p c i", i=CH)
                    tot3 = cs3[:, :, CH - 1:CH] if d == 0 else cs3[:, :, 0:1]
                    k.act(e_.v[:, 0:n], cs.v[:, 0:n], ACT.Exp)
                    yield
                    k.tt("dve", t1.v[:, 0:n], r_.v[:, 0:n], e_.v[:, 0:n], ALU.mult)
                    st_bf(S["D_Rt"][d, g * 128:(g + 1) * 128, t0:t0 + n], t1.v[:, 0:n], n)
                    yield
                    k.tt("dve", e_.v[:, 0:n], cs.v[:, 0:n], lw.v[:, 0:n], ALU.subtract)
                    k.act(e_.v[:, 0:n], e_.v[:, 0:n], ACT.Exp)
                    yield
                    k.stt("dve", t1.v[:, 0:n], kk.v[:, 0:n], -1.0, e_.v[:, 0:n], ALU.mult, ALU.mult)
                    st_bf(S["D_At"][d, g * 128:(g + 1) * 128, t0:t0 + n], t1.v[:, 0:n], n)
                    yield
                    for s in range(ns):
                        pt = nps()
                        k.tr(pt.v[:, 0:128], t1.v[:, s * 128:(s + 1) * 128], identf.v[:, :])
                        k.cp("act", tbb.v[:, 0:128], pt.v[:, 0:128])
                        k.stv("sp", S["D_Atok"][d, t0 + s * 128:t0 + (s + 1) * 128, g * 128:(g + 1) * 128],
                              tbb.v[:, 0:128])
                    k.act(e_.v[:, 0:n], cs.v[:, 0:n], ACT.Exp, scale=-1.0)
                    k.tt("dve", t1.v[:, 0:n], a_.v[:, 0:n], e_.v[:, 0:n], ALU.mult)
                    st_bf(S["D_Bt"][d, g * 128:(g + 1) * 128, t0:t0 + n], t1.v[:, 0:n], n)
                    yield
                    k.tt("dve", t1.v[:, 0:n], k_.v[:, 0:n], e_.v[:, 0:n], ALU.mult)
                    st_bf(S["D_Kt"][d, g * 128:(g + 1) * 128, t0:t0 + n], t1.v[:, 0:n], n)
                    yield
                    k.act(pcx.v[:, 0:nch], tot3.rr("p c o -> p (c o)"), ACT.Exp)
                    k.stv("sp", S["D_PC"][d, g * 128:(g + 1) * 128, t0 // CH:t0 // CH + nch], pcx.v[:, 0:nch])
                    k.tt("dve", e_.v[:, 0:n].rr("p (c i) -> p c i", i=CH), tot3.bc([128, nch, CH]), cs3, ALU.subtract)
                    k.act(e_.v[:, 0:n], e_.v[:, 0:n], ACT.Exp)
                    yield
                    for (srcb, dstd) in ((a_, S["D_Bh"]), (k_, S["D_Kh"])):
                        k.tt("dve", t1.v[:, 0:n], srcb.v[:, 0:n], e_.v[:, 0:n], ALU.mult)
                        for s in range(ns):
                            pt = nps()
                            k.tr(pt.v[:, 0:128], t1.v[:, s * 128:(s + 1) * 128], identf.v[:, :])
                            k.cp("act", tbb.v[:, 0:128], pt.v[:, 0:128])
                            k.stv("sp", dstd[d, t0 + s * 128:t0 + (s + 1) * 128, g * 128:(g + 1) * 128], tbb.v[:, 0:128])
                gens = [dgroup(0, F[0:9], pcx), dgroup(1, F2, pcx2)]
                live = list(gens)
                while live:
                    for g_ in list(live):
                        try:
                            next(g_)
                        except StopIteration:
                            live.remove(g_)
                for s in range(ns):
                    k.stv("sp", S["D_BS"][d, t0 + s * 128:t0 + (s + 1) * 128, :], bs4.v[:, s, :])


def attnphase(k, st, cfg):
    S, C, Wt = cfg["S"], cfg["C"], cfg["W"]
    T, CTX = cfg["T"], cfg["CTX"]
    TT = T + CTX
    NKC = TT // 128
    need_ctx = cfg["need_ctx"]
    kT = k.sb(st, [128, TT], BF16, "kT")
    va = k.sb(st, [128, NKC, 65], BF16, "va")
    qT = [k.sb(st, [128, 512], BF16, "qT%d" % i) for i in range(2)]
    for c0 in range(0, TT, 2048):
        k.memset("dve", kT.v[64:128, c0:min(TT, c0 + 2048)], 0.0)
    for q_ in qT:
        k.memset("dve", q_.v[64:128, :], 0.0)
    pT = [k.sb(st, [128, 512], BF16, "pT%d" % i) for i in range(3)]
    ot = [k.sb(st, [128, 64], F32, "ot%d" % i) for i in range(2)]
    rc = k.sb(st, [128, 1], F32, "rc")
    esk = k.sb(st, [128, 4], F32, "esk")
    mprev = k.sb(st, [128, 128], BF16, "mprev")
    mnext = k.sb(st, [128, 128], BF16, "mnext")
    k.ld("pool", mprev.v[:, :], C["mprev"])
    k.ld("pool", mnext.v[:, :], C["mnext"])
    k.ld("sp", esk.v[:, :], Wt["sink"].partition_broadcast(128))
    k.act(esk.v[:, :], esk.v[:, :], ACT.Exp)
    ps_s = [k.ps(st, [128, 512], F32, "pss%d" % i) for i in range(3)]
    ps_o = [k.ps(st, [128, 512], F32, "pso%d" % i) for i in range(4)]
    cnt = dict(s=0, q=0, o=0, p=0)
    scale = 64 ** -0.5

    def load_kv(kd, vd, g):
        k.ld("sp", kT.v[0:64, :], kd[g * 64:(g + 1) * 64, :])
        k.memset("dve", va.v[:, :, 64:65], 1.0)
        k.ld("sp", va.v[:, :, 0:64], vd[:, g * 64:(g + 1) * 64].rearrange("(c p) j -> p c j", p=128))

    def finish(acc, nq_sub, t0, col, sinkcol):
        for s in range(nq_sub):
            o = ot[cnt["o"] % 2]
            cnt["o"] += 1
            if sinkcol is None:
                k.recip(rc.v[:, :], acc[s].v[:, 64:65])
            else:
                k.tt("dve", rc.v[:, :], acc[s].v[:, 64:65], esk.v[:, sinkcol:sinkcol + 1], ALU.add)
                k.recip(rc.v[:, :], rc.v[:, :])
            k.ts("dve", o.v[:, :], acc[s].v[:, 0:64], rc.v[:, 0:1], None, ALU.mult)
            k.stv("sp", S["O"][t0 + s * 128:t0 + (s + 1) * 128, col:col + 64], o.v[:, :])

    def block(qd, h, t0, nq, chunks, col, sinkcol):
        q = qT[cnt["q"] % 2]
        cnt["q"] += 1
        k.ld("sp", q.v[0:64, 0:nq], qd[h * 64:(h + 1) * 64, t0:t0 + nq])
        nsub = nq // 128
        first, last = {}, {}
        for i, (kc, subs) in enumerate(chunks):
            for (s, _) in subs:
                first.setdefault(s, i)
                last[s] = i

        def qk(i):
            ps = ps_s[cnt["s"] % 3]
            cnt["s"] += 1
            kc = chunks[i][0]
            k.mm(ps.v[:, 0:nq], kT.v[:, kc * 128:(kc + 1) * 128], q.v[:, 0:nq])
            return ps

        nxt = qk(0)
        for i, (kc, subs) in enumerate(chunks):
            ps = nxt
            if i + 1 < len(chunks):
                nxt = qk(i + 1)
            p = pT[cnt["p"] % 3]
            cnt["p"] += 1
            k.act(p.v[:, 0:nq], ps.v[:, 0:nq], ACT.Exp, scale=scale)
            for (s, m_) in subs:
                if m_ is not None:
                    k.tt("dve", p.v[:, s * 128:(s + 1) * 128], p.v[:, s * 128:(s + 1) * 128], m_.v[:, :], ALU.mult)
            for (s, m_) in subs:
                k.mm(ps_o[s].v[:, 0:65], p.v[:, s * 128:(s + 1) * 128], va.v[:, kc, :], i == first[s], i == last[s])
        finish(ps_o, nsub, t0, col, sinkcol)

    for g in range(2):
        load_kv(S["KB"], S["VB"], g)
        for h in (2 * g, 2 * g + 1):
            for t0 in range(0, T, 512):
                nq = min(512, T - t0)
                block(S["QB"], h, t0, nq, [(kc, [(s, None) for s in range(nq // 128)]) for kc in range(NKC)],
                      256 + h * 64, None)
            if need_ctx:
                block(S["QB"], h, T, CTX, [(kc, [(s, None) for s in range(CTX // 128)]) for kc in range(T // 128, NKC)],
                      256 + h * 64, None)
    nb = T // 128
    cchunks = list(range(T // 128, NKC))
    for g in range(2):
        load_kv(S["KA"], S["VA"], g)
        for h in (2 * g, 2 * g + 1):
            for b0 in range(0, nb, 4):
                nsb = min(4, nb - b0)
                chunks = []
                for kc in range(max(0, b0 - 1), min(nb, b0 + nsb + 1)):
                    subs = []
                    for s in range(nsb):
                        dlt = kc - (b0 + s)
                        if dlt == -1:
                            subs.append((s, mprev))
                        elif dlt == 0:
                            subs.append((s, None))
                        elif dlt == 1:
                            subs.append((s, mnext))
                    chunks.append((kc, subs))
                for kc in cchunks:
                    chunks.append((kc, [(s, None) for s in range(nsb)]))
                block(S["QA"], h, b0 * 128, nsb * 128, chunks, h * 64, h)
            if need_ctx:
                block(S["QA"], h, T, CTX, [(kc, [(s, None) for s in range(CTX // 128)]) for kc in cchunks], h * 64, h)


def scanphase(k, st, cfg):
    S, C = cfg["S"], cfg["C"]
    T, CTX = cfg["T"], cfg["CTX"]
    TT = T + CTX
    dplr = cfg["dplr"]
    pre = "D_" if dplr else "C_"
    NCG = 2
    GT_ = NCG * CH
    NCHT = TT // CH
    f3 = lambda b: b.v[:, :, :]
    t64 = lambda nm: k.sb(st, [64, 8, 64], F32, nm)
    b64 = lambda nm: k.sb(st, [64, 8, 64], BF16, nm)
    chm = lambda nm: k.sb(st, [64, 8, GT_], BF16, nm)
    tkm = lambda nm: k.sb(st, [64, NCG, 8, 64], BF16, nm)
    RT = [chm("RT%d" % i) for i in range(2)]
    KT = [chm("KT%d" % i) for i in range(2)]
    KH = [tkm("KH%d" % i) for i in range(2)]
    VV = [tkm("VV%d" % i) for i in range(2)]
    Min = t64("Min")
    k.ld("sp", f3(Min), C["m_in"].rearrange("p (q t) -> p q t", q=8))
    id8 = t64("id8")
    k.ld("sp", f3(id8), C["id8"].rearrange("p (q t) -> p q t", q=8))
    if dplr:
        AT = [chm("AT%d" % i) for i in range(2)]
        BT = [chm("BT%d" % i) for i in range(2)]
        BH = [tkm("BH%d" % i) for i in range(2)]
        AK = [tkm("AK%d" % i) for i in range(2)]
        Mst, Mts = t64("Mst"), t64("Mts")
        k.ld("sp", f3(Mst), C["m_st"].rearrange("p (q t) -> p q t", q=8))
        k.ld("sp", f3(Mts), C["m_ts"].rearrange("p (q t) -> p q t", q=8))
        PCt = k.sb(st, [64, 8, NCHT], F32, "PCt")
        for d in range(2):
            k.ld("sp", PCt.v[:, d * 4:(d + 1) * 4, :], S["D_PC"][d].rearrange("(h j) c -> j h c", j=64))
    else:
        PCr = k.sb(st, [64, 8], F32, "PCr")
        k.ld("sp", PCr.v[:, :], C["ret_pc"])
        GTc = b64("GTc")
        k.tt("dve", f3(GTc), f3(id8), PCr.v[:, :].rr("p (q o) -> p q o", o=1).bc([64, 8, 64]), ALU.mult)
    sets = []
    for c in range(NCG):
        B = dict(ArkT=b64("ArkT%d" % c), Yv=t64("Yv%d" % c), H=t64("H%d" % c))
        if dplr:
            B.update(Ls=[b64("L%d_%d" % (i, c)) for i in range(5)], Ns=[b64("N%d_%d" % (i, c)) for i in range(6)],
                     AakT=b64("AakT%d" % c), ArbT=b64("ArbT%d" % c), Qe=b64("Qe%d" % c), GTt=b64("GTt%d" % c),
                     X=[k.sb(st, [64, 8, 128], F32, "X%d_%d" % (i, c)) for i in range(2)],
                     Xb=k.sb(st, [64, 8, 128], BF16, "Xb_%d" % c),
                     psX=k.ps(st, [64, 1024], F32, "psX%d" % c))
        sets.append(B)
    ST = [b64("ST%d" % i) for i in range(2)]
    Yt = [t64("Yt%d" % i) for i in range(2)]
    psA = [k.ps(st, [64, 512], F32, "psA%d" % i) for i in range(2)]
    psY = [k.ps(st, [64, 512], F32, "psY%d" % i) for i in range(2)]
    cnt = dict(a=0, y=0, e=0)
    p3 = lambda p: p.v[:, :].rr("p (q t) -> p q t", q=8)

    def npa():
        p = psA[cnt["a"] % 2]
        cnt["a"] += 1
        return p

    def ev():
        cnt["e"] += 1
        return "act" if cnt["e"] % 2 else "dve"

    def prod(dst, lhs, rhs, mask):
        p = npa()
        for q in range(8):
            k.mm(p3(p)[:, q, :], lhs(q), rhs(q))
        if mask is None:
            k.cp(ev(), f3(dst), p3(p))
        else:
            k.tt("dve", f3(dst), p3(p), f3(mask), ALU.mult)

    def views(rb, cl):
        sl = {d: slice(cl[d] * CH, (cl[d] + 1) * CH) for d in range(2)}
        dq = lambda q: q // 4
        chs = lambda buf: (lambda q: buf[rb].v[:, q, sl[dq(q)]])
        tks = lambda buf: (lambda q: buf[rb].v[:, cl[dq(q)], q, :])
        return sl, chs, tks

    def ppart(B, rb, cl, grp):
        sl, chs, tks = views(rb, cl)
        rt, kt, kh, vv = chs(RT), chs(KT), tks(KH), tks(VV)
        prod(B["ArkT"], kt, rt, Min)
        yield
        if dplr:
            at, bt, bh = chs(AT), chs(BT), tks(BH)
            Ls, Ns, AakT, ArbT = B["Ls"], B["Ns"], B["AakT"], B["ArbT"]
            prod(Ns[0], bt, at, Mst)
            prod(Ls[0], at, bt, Mts)
            yield
            prod(AakT, kt, at, Mst)
            prod(ArbT, bt, rt, Min)
            yield
            xc, xn = B["X"]
            xb = B["Xb"]
            for d in range(2):
                k.cp("act", xc.v[:, d * 4:(d + 1) * 4, 0:64], AK[rb].v[:, cl[d], d * 4:(d + 1) * 4, :])
            p = npa()
            for q in range(8):
                k.mm(p3(p)[:, q, :], AakT.v[:, q, :], vv(q))
            k.cp("dve", xc.v[:, :, 64:128], p3(p))
            k.cp("act", f3(xb), f3(xc))
            yield
            px = B["psX"].v[:, :].rr("p (q c) -> p q c", q=8)
            for lv in range(6):
                for q in range(8):
                    k.mm(px[:, q, :], Ns[lv].v[:, q, :], xb.v[:, q, :])
                if lv < 5:
                    if lv < 4:
                        prod(Ls[lv + 1], lambda q: Ns[lv].v[:, q, :], lambda q: Ls[lv].v[:, q, :], None)
                    prod(Ns[lv + 1], lambda q: Ls[lv].v[:, q, :], lambda q: Ns[lv].v[:, q, :], None)
                k.tt("dve", f3(xn), f3(xc), px, ALU.add)
                xc, xn = xn, xc
                k.cp("act", f3(xb), f3(xc))
                yield
            wq = lambda q: xb.v[:, q, 0:64]
            uv = lambda q: xb.v[:, q, 64:128]
            p = npa()
            for q in range(8):
                k.mm(p3(p)[:, q, :], wq(q), ArbT.v[:, q, :])
            for d in range(2):
                k.tt("dve", B["Qe"].v[:, d * 4:(d + 1) * 4, :], p3(p)[:, d * 4:(d + 1) * 4, :],
                     RT[rb].v[:, d * 4:(d + 1) * 4, sl[d]], ALU.add)
            p = npa()
            for q in range(8):
                k.mm(p3(p)[:, q, :], wq(q), bh(q))
            for d in range(2):
                c_abs = grp[d] * NCG + cl[d]
                k.tt("dve", B["GTt"].v[:, d * 4:(d + 1) * 4, :], id8.v[:, d * 4:(d + 1) * 4, :],
                     PCt.v[:, d * 4:(d + 1) * 4, c_abs:c_abs + 1].bc([64, 4, 64]), ALU.mult)
            k.tt("dve", f3(B["GTt"]), f3(B["GTt"]), p3(p), ALU.add)
            yield
        p = npa()
        for q in range(8):
            k.mm(p3(p)[:, q, :], B["ArkT"].v[:, q, :], vv(q), True, not dplr)
            if dplr:
                k.mm(p3(p)[:, q, :], ArbT.v[:, q, :], uv(q), False, True)
        k.cp(ev(), f3(B["Yv"]), p3(p))
        p = npa()
        for q in range(8):
            k.mm(p3(p)[:, q, :], kh(q), vv(q), True, not dplr)
            if dplr:
                k.mm(p3(p)[:, q, :], bh(q), uv(q), False, True)
        k.cp(ev(), f3(B["H"]), p3(p))
        yield

    k.memset("dve", f3(ST[0]), 0.0)
    ngl = T // GT_
    ncg_ctx = CTX // GT_
    fw = list(range(ngl, ngl + ncg_ctx)) + list(range(ngl))
    bw = list(range(ngl + ncg_ctx - 1, ngl - 1, -1)) + list(range(ngl - 1, -1, -1))
    order = {0: fw, 1: bw}
    step = 0
    for gi in range(ngl + ncg_ctx):
        rb = gi % 2
        grp = {d: order[d][gi] for d in range(2)}
        for d in range(2):
            t0 = grp[d] * GT_
            qs = slice(d * 4, (d + 1) * 4)
            chv = lambda nm: S[pre + nm][d, :, t0:t0 + GT_].rearrange("(h j) t -> j h t", j=64)
            tkv = lambda ap: ap[t0:t0 + GT_, :].rearrange("(n t) (h j) -> t n h j", t=CH, j=64)
            k.ld("sp", RT[rb].v[:, qs, :], chv("Rt"))
            k.ld("sp", KT[rb].v[:, qs, :], chv("Kt"))
            k.ld("sp", KH[rb].v[:, :, qs, :], tkv(S[pre + "Kh"][d]))
            k.ld("sp", VV[rb].v[:, :, qs, :], tkv(S["D_V"][d] if dplr else S["C_V"]))
            if dplr:
                k.ld("sp", AT[rb].v[:, qs, :], chv("At"))
                k.ld("sp", BT[rb].v[:, qs, :], chv("Bt"))
                k.ld("sp", BH[rb].v[:, :, qs, :], tkv(S["D_Bh"][d]))
                k.ld("sp", AK[rb].v[:, :, qs, :], tkv(S["D_Atok"][d]))
        cls = [{0: ci, 1: NCG - 1 - ci} for ci in range(NCG)]
        gens = [ppart(sets[ci], rb, cls[ci], grp) for ci in range(NCG)]
        live = list(gens)
        while live:
            for g_ in list(live):
                try:
                    next(g_)
                except StopIteration:
                    live.remove(g_)
        for ci in range(NCG):
            B, cl = sets[ci], cls[ci]
            sl, chs, tks = views(rb, cl)
            Sc, Sn = ST[step % 2], ST[(step + 1) % 2]
            qe = (lambda q: B["Qe"].v[:, q, :]) if dplr else chs(RT)
            gt = (lambda q: B["GTt"].v[:, q, :]) if dplr else (lambda q: GTc.v[:, q, :])
            py = psY[cnt["y"] % 2]
            cnt["y"] += 1
            for q in range(8):
                k.mm(p3(py)[:, q, :], qe(q), Sc.v[:, q, :])
            p = npa()
            for q in range(8):
                k.mm(p3(p)[:, q, :], gt(q), Sc.v[:, q, :])
            k.tt("dve", f3(Sn), p3(p), f3(B["H"]), ALU.add)
            yt = Yt[step % 2]
            k.tt("dve", f3(yt), p3(py), f3(B["Yv"]), ALU.add)
            for d in range(2):
                tt0 = grp[d] * GT_ + cl[d] * CH
                k.stv("sp", S[pre + "Y"][d, tt0:tt0 + CH, :].rearrange("t (h i) -> t h i", i=64),
                      yt.v[:, d * 4:(d + 1) * 4, :])
            step += 1


GN_EPS = 64e-5


def postphase(k, st, cfg):
    S, Wt = cfg["S"], cfg["W"]
    TT = cfg["T"] + cfg["CTX"]
    row = lambda nm, ap: (lambda t: (k.ld("sp", t.v[:, :], ap.partition_broadcast(128)), t)[1])(k.sb(st, [128, 256], F32, nm))
    retg = row("retg", Wt["ret_g"])
    lng = row("lng", Wt["ln_g"])
    lnb = row("lnb", Wt["ln_b"])
    gc = k.sb(st, [128, 512], F32, "gc")
    gd = k.sb(st, [128, 256], F32, "gd")
    vd = k.sb(st, [128, 4, 64], BF16, "vd")
    vdf = k.sb(st, [128, 4, 64], F32, "vdf")
    bs = k.sb(st, [128, 4], F32, "bs")

    def bufset(nm):
        return dict(y=[k.sb(st, [128, 4, 64], F32, nm + "y%d" % i) for i in range(2)],
                    yc=k.sb(st, [128, 4, 64], F32, nm + "yc"), sq=k.sb(st, [128, 4, 64], F32, nm + "sq"),
                    m4=k.sb(st, [128, 4], F32, nm + "m4"), v4=k.sb(st, [128, 4], F32, nm + "v4"),
                    acc=k.sb(st, [128, 256], F32, nm + "acc"), yn=k.sb(st, [128, 256], F32, nm + "yn"), i=0)

    BC, BD = bufset("c"), bufset("d")

    def head_norm(B, yb, g, b, dst):
        yc, sq, m4, v4 = B["yc"], B["sq"], B["m4"], B["v4"]
        k.op("dve", lambda e: e.tensor_reduce(out=m4[:, :], in_=yb[:, :, :], axis=AX.X, op=ALU.add), r=[yb], w=[m4])
        k.ts("dve", m4.v[:, :], m4.v[:, :], 1.0 / 64, None, ALU.mult)
        yield
        k.tt("dve", yc.v[:, :, :], yb.v[:, :, :], m4.v[:, :].rr("p (h o) -> p h o", o=1).bc([128, 4, 64]), ALU.subtract)
        yield
        k.tt("dve", sq.v[:, :, :], yc.v[:, :, :], yc.v[:, :, :], ALU.mult)
        yield
        k.op("dve", lambda e: e.tensor_reduce(out=v4[:, :], in_=sq[:, :, :], axis=AX.X, op=ALU.add), r=[sq], w=[v4])
        yield
        k.ts("dve", v4.v[:, :], v4.v[:, :], 1.0 / 64, GN_EPS, ALU.mult, ALU.add)
        yield
        k.act(v4.v[:, :], v4.v[:, :], ACT.Sqrt)
        yield
        k.recip(v4.v[:, :], v4.v[:, :])
        yield
        k.tt("dve", yc.v[:, :, :], yc.v[:, :, :], v4.v[:, :].rr("p (h o) -> p h o", o=1).bc([128, 4, 64]), ALU.mult)
        yield
        k.tt("dve", dst, yc.v[:, :, :].rr("p h i -> p (h i)"), g.v[:, :], ALU.mult)
        if b is not None:
            k.tt("dve", dst, dst, b.v[:, :], ALU.add)
        yield

    def part_c(tsl):
        B = BC
        acc, yn = B["acc"], B["yn"]
        k.ld("sp", gc.v[:, :], S["C_G"][tsl, :])
        for d in range(2):
            yb = B["y"][B["i"] % 2]
            B["i"] += 1
            k.ld("sp", yb.v[:, :, :], S["C_Y"][d, tsl, :].rearrange("t (h i) -> t h i", i=64))
            yield from head_norm(B, yb, retg, None, yn.v[:, :])
            if d == 0:
                k.tt("dve", acc.v[:, :], yn.v[:, :], gc.v[:, 0:256], ALU.mult)
            else:
                k.tt("dve", yn.v[:, :], yn.v[:, :], gc.v[:, 256:512], ALU.mult)
                k.tt("dve", acc.v[:, :], acc.v[:, :], yn.v[:, :], ALU.add)
            yield
        k.stv("sp", S["O"][tsl, 512:768], acc.v[:, :])

    def part_d(tsl):
        B = BD
        acc, yn = B["acc"], B["yn"]
        k.ld("sp", gd.v[:, :], S["D_G"][tsl, :])
        for d in range(2):
            yb = B["y"][B["i"] % 2]
            B["i"] += 1
            k.ld("sp", yb.v[:, :, :], S["D_Y"][d, tsl, :].rearrange("t (h i) -> t h i", i=64))
            k.ld("sp", vd.v[:, :, :], S["D_V"][d, tsl, :].rearrange("t (h i) -> t h i", i=64))
            k.ld("sp", bs.v[:, :], S["D_BS"][d, tsl, :])
            yield from head_norm(B, yb, lng, lnb, yn.v[:, :])
            k.tt("dve", vdf.v[:, :, :], vd.v[:, :, :], bs.v[:, :].rr("p (h o) -> p h o", o=1).bc([128, 4, 64]), ALU.mult)
            k.tt("dve", yn.v[:, :], yn.v[:, :], vdf.v[:, :, :].rr("p h i -> p (h i)"), ALU.add)
            yield
            if d == 0:
                k.cp("act", acc.v[:, :], yn.v[:, :])
            else:
                k.tt("dve", acc.v[:, :], acc.v[:, :], yn.v[:, :], ALU.add)
            yield
        k.tt("dve", acc.v[:, :], acc.v[:, :], gd.v[:, :], ALU.mult)
        k.stv("sp", S["O"][tsl, 768:1024], acc.v[:, :])

    for t0 in range(0, TT, 128):
        tsl = slice(t0, t0 + 128)
        live = [part_c(tsl), part_d(tsl)]
        while live:
            for g_ in list(live):
                try:
                    next(g_)
                except StopIteration:
                    live.remove(g_)


def modphase(k, st, cfg):
    cv = k.sb(st, [128, 2, 8], F32, "cv")
    for r in range(2):
        k.ld("sp", cv.v[:, r, :], cfg["cvec"][r].rearrange("(c p) -> p c", p=128), allow_slow_non_contiguous=True)
    sc = k.sb(st, [128, 8, 2], F32, "sc")
    k.act(sc.v[:, :, :].rr("p c r -> p r c"), cv.v[:, :, :], ACT.Silu)
    wm = [k.sb(st, [128, 8, 512], F32, "wm%d" % i) for i in range(2)]
    bm = k.sb(st, [2, 9 * D], F32, "bm")
    k.ld("sp", bm.v[:, :], cfg["b_mod"].partition_broadcast(2))
    ob = k.sb(st, [2, 9 * D], F32, "ob")
    pm = [k.ps(st, [128, 512], F32, "pm%d" % i) for i in range(2)]
    wv = cfg["w_mod"].rearrange("(c p) f -> p c f", p=128)
    for j in range(18):
        w = wm[j % 2]
        for c in range(8):
            k.ld("sp", w.v[:, c, :], wv[:, c, j * 512:(j + 1) * 512])
        p = pm[j % 2]
        for c in range(8):
            k.mm(p.v[0:2, :], sc.v[:, c, :], w.v[:, c, :], c == 0, c == 7)
        k.tt("dve", ob.v[:, j * 512:(j + 1) * 512], p.v[0:2, :], bm.v[:, j * 512:(j + 1) * 512], ALU.add)
    k.stv("sp", cfg["mod"], ob.v[:, :])


def build_program(T, CTX, L, stop_after=None):
    TT = T + CTX
    nc = bass.Bass("TRN2", target_bir_lowering=False, dynamic_dma_scratch_size=8192)

    def din(name, shape, dt=F32):
        return nc.dram_tensor(name, list(shape), dt, kind="ExternalInput").ap()

    def scr(name, shape, dt=F32):
        return nc.dram_tensor(name, list(shape), dt, kind="Internal").ap()

    I = dict(
        xin=din("xin", [TT, D]), cvec=din("cvec", [2, D]),
        w_mod=din("w_mod", [L, D, 9 * D]), b_mod=din("b_mod", [L, 9 * D]), norm_g=din("norm_g", [L, 3, D]),
        ffn_w_in=din("ffn_w_in", [L, 2, D, 2 * DFF]), ffn_w_out=din("ffn_w_out", [L, 2, DFF, D]),
        w_in=din("w_in", [L, D, 3456]), w_out=din("w_out", [L, D, D]), final_g=din("final_g", [D]),
        w_sw=din("w_sw", [L, D, 1280]), w_d=din("w_d", [L, 2, D, 896]), pcols=din("pcols", [L, 128, 24]),
        mu=din("mu", [L, 2, 896]), w2=din("w2", [L, 2, 64, 256]), a2=din("a2", [L, 2, 64, 256]),
        g2=din("g2", [L, 128, 256]), ret_g=din("ret_g", [L, 256]), ln_g=din("ln_g", [L, 256]),
        ln_b=din("ln_b", [L, 256]), sink=din("sink", [L, 4]),
    )
    Cn = dict(
        ident=din("ident", [128, 128]), bd64=din("bd64", [128, 128]), e2=din("e2", [128, 2]),
        ropeA_c=din("ropeA_c", [128, T]), ropeA_s=din("ropeA_s", [128, T]),
        ropeS_c=din("ropeS_c", [128, T]), ropeS_s=din("ropeS_s", [128, T]),
        ret_tab=din("ret_tab", [128, 2, 2, 3, CH]), ret_pc=din("ret_pc", [64, 8]),
        m_st=din("m_st", [64, 512]), m_ts=din("m_ts", [64, 512]), m_in=din("m_in", [64, 512]),
        id8=din("id8", [64, 512]), mprev=din("mprev", [128, 128]), mnext=din("mnext", [128, 128]),
    )
    out = nc.dram_tensor("out", [T, D], F32, kind="ExternalOutput").ap()
    dbg = stop_after is not None
    mk = (lambda name, shape, dt=F32: nc.dram_tensor(name, list(shape), dt, kind="ExternalOutput").ap()) if dbg else scr
    S = dict(
        xres=mk("xres", [TT, D]), mod=mk("modr", [2, 9 * D]), hT=mk("hT", [D, TT], BF16),
        QA=mk("QA", [256, TT], BF16), KA=mk("KA", [128, TT], BF16), VA=mk("VA", [TT, 128], BF16),
        QB=mk("QB", [256, TT], BF16), KB=mk("KB", [128, TT], BF16), VB=mk("VB", [TT, 128], BF16),
        O=mk("O", [TT, 1024]),
        C_Rt=mk("C_Rt", [2, 256, TT], BF16), C_Kt=mk("C_Kt", [2, 256, TT], BF16), C_Kh=mk("C_Kh", [2, TT, 256], BF16),
        C_V=mk("C_V", [TT, 256], BF16), C_G=mk("C_G", [TT, 512]), C_Y=mk("C_Y", [2, TT, 256]),
        D_Rt=mk("D_Rt", [2, 256, TT], BF16), D_Kt=mk("D_Kt", [2, 256, TT], BF16), D_At=mk("D_At", [2, 256, TT], BF16),
        D_Bt=mk("D_Bt", [2, 256, TT], BF16), D_Kh=mk("D_Kh", [2, TT, 256], BF16), D_Bh=mk("D_Bh", [2, TT, 256], BF16),
        D_Atok=mk("D_Atok", [2, TT, 256], BF16), D_V=mk("D_V", [2, TT, 256], BF16), D_Y=mk("D_Y", [2, TT, 256]),
        D_PC=mk("D_PC", [2, 256, TT // CH]), D_G=mk("D_G", [TT, 256]), D_BS=mk("D_BS", [2, TT, 4]),
    )
    k = K(nc)
    stages = []

    def phase(name, fn, cfg):
        if stop_after is not None and stop_after in stages:
            return
        with contextlib.ExitStack() as st:
            fn(k, st, cfg)
            k.barrier()
        stages.append(name)
        k.marks.append((name, {e.name: e.n for e in k.E.values()}))

    with k.stack:
        segs_all = [(0, T, 0), (T, CTX, 1)]
        for l in range(L):
            last = l == L - 1
            W = dict(w_in=I["w_in"][l], w_sw=I["w_sw"][l], w_d=I["w_d"][l], mu=I["mu"][l], pcols=I["pcols"][l],
                     w2=I["w2"][l], a2=I["a2"][l], g2=I["g2"][l], ret_g=I["ret_g"][l], ln_g=I["ln_g"][l],
                     ln_b=I["ln_b"][l], sink=I["sink"][l])
            base = dict(S=S, C=Cn, W=W, T=T, CTX=CTX)
            phase("mod%d" % l, modphase, dict(cvec=I["cvec"], w_mod=I["w_mod"][l], b_mod=I["b_mod"][l], mod=S["mod"]))
            phase("f1_%d" % l, rowphase, dict(
                xin=I["xin"] if l == 0 else S["xres"], xout=S["xres"], segs=segs_all, mod=S["mod"], ident=Cn["ident"],
                ffn=dict(w_in=I["ffn_w_in"][l, 0], w_out=I["ffn_w_out"][l, 0], g=I["norm_g"][l, 0], shift=0, scale=1,
                         gate=2),
                post=dict(kind="hT", g=I["norm_g"][l, 1], shift=3, scale=4, hT=S["hT"])))
            phase("proj%d" % l, projphase, base)
            phase("attn%d" % l, attnphase, dict(base, need_ctx=not last))
            phase("scanC%d" % l, scanphase, dict(base, dplr=False))
            phase("scanD%d" % l, scanphase, dict(base, dplr=True))
            phase("post%d" % l, postphase, base)
            phase("f2_%d" % l, rowphase, dict(
                xin=S["xres"], xout=S["xres"], segs=segs_all if not last else [(0, T, 0)], mod=S["mod"],
                ident=Cn["ident"],
                pre=dict(o=S["O"], w_out=I["w_out"][l], gate=5),
                ffn=dict(w_in=I["ffn_w_in"][l, 1], w_out=I["ffn_w_out"][l, 1], g=I["norm_g"][l, 2], shift=6, scale=7,
                         gate=8),
                post=dict(kind="final", g=I["final_g"], out=out) if last else None))
    return nc, k


def _consts(T):
    c = {}
    c["ident"] = np.eye(128, dtype=np.float32)
    bd = np.zeros((128, 128), np.float32)
    bd[:64, :64] = 1
    bd[64:, 64:] = 1
    c["bd64"] = bd
    e2 = np.zeros((128, 2), np.float32)
    e2[:64, 0] = 1
    e2[64:, 1] = 1
    c["e2"] = e2
    t = np.arange(T)
    row = (t // 64).astype(np.float32)
    col = (t % 64).astype(np.float32)
    inv16 = (1.0 / (np.float32(10000.0) ** (np.arange(0, 32, 2, dtype=np.float32) / np.float32(32)))).astype(np.float32)
    inv32 = (1.0 / (np.float32(10000.0) ** (np.arange(0, 64, 2, dtype=np.float32) / np.float32(64)))).astype(np.float32)
    ca = np.zeros((64, T), np.float32)
    sa = np.zeros((64, T), np.float32)
    for blk, pos in ((0, row), (32, col)):
        ang = pos[None, :] * inv16[:, None]
        ca[blk:blk + 16] = np.cos(ang)
        ca[blk + 16:blk + 32] = np.cos(ang)
        sa[blk:blk + 16] = -np.sin(ang)
        sa[blk + 16:blk + 32] = np.sin(ang)
    ang = t.astype(np.float32)[None, :] * inv32[:, None]
    cs = np.concatenate([np.cos(ang), np.cos(ang)], 0).astype(np.float32)
    ss = np.concatenate([-np.sin(ang), np.sin(ang)], 0).astype(np.float32)
    c["ropeA_c"] = np.concatenate([ca, ca], 0)
    c["ropeA_s"] = np.concatenate([sa, sa], 0)
    c["ropeS_c"] = np.concatenate([cs, cs], 0)
    c["ropeS_s"] = np.concatenate([ss, ss], 0)
    lg = np.log1p(-np.exp2(-5.0 - np.arange(4, dtype=np.float64)))
    i = np.arange(CH, dtype=np.float64)
    rt = np.zeros((128, 2, 2, 3, CH), np.float64)
    for p in range(128):
        for g in range(2):
            l_ = lg[2 * g + p // 64]
            for d in range(2):
                csum = (i + 1) * l_ if d == 0 else (CH - i) * l_
                rt[p, g, d, 0] = np.exp(csum)
                rt[p, g, d, 1] = np.exp(-csum) / 8.0
                rt[p, g, d, 2] = np.exp(CH * l_ - csum) / 8.0
    c["ret_tab"] = rt.astype(np.float32)
    c["ret_pc"] = np.tile(np.exp(CH * lg)[None, :], (64, 2)).astype(np.float32)
    s_ = np.arange(64)[:, None]
    t_ = np.arange(64)[None, :]
    lt = (s_ < t_).astype(np.float32)
    gt = (s_ > t_).astype(np.float32)
    eye = np.eye(64, dtype=np.float32)
    c["m_st"] = np.concatenate([lt] * 4 + [gt] * 4, 1)
    c["m_ts"] = np.concatenate([gt] * 4 + [lt] * 4, 1)
    c["m_in"] = np.concatenate([lt + eye] * 4 + [gt + eye] * 4, 1)
    c["id8"] = np.concatenate([eye] * 8, 1)
    j = np.arange(128)[:, None]
    ii = np.arange(128)[None, :]
    c["mprev"] = (ii <= j).astype(np.float32)
    c["mnext"] = (j <= ii).astype(np.float32)
    return c


def _host_maps(inp, T, CTX, L):
    f = lambda a: np.ascontiguousarray(np.asarray(a, dtype=np.float32))
    w_in = f(inp["w_in"])
    pa = np.concatenate([np.arange(16, 32), np.arange(0, 16), np.arange(48, 64), np.arange(32, 48)])
    psq = np.concatenate([np.arange(32, 64), np.arange(0, 32)])
    cols = []
    for base, nh, perm in ((0, 4, pa), (256, 2, pa), (512, 4, pa), (768, 2, pa), (1024, 4, psq), (1280, 4, psq)):
        for h in range(nh):
            cols.append(base + h * 64 + perm)
    cols = np.concatenate(cols)
    w_sw = np.ascontiguousarray(w_in[:, :, cols])
    dcols = [np.concatenate([np.arange(2304, 3072), np.arange(3200, 3264), np.arange(3328, 3392)]),
             np.concatenate([np.arange(2304, 3072), np.arange(3264, 3328), np.arange(3392, 3456)])]
    w_d = np.ascontiguousarray(np.stack([w_in[:, :, dc] for dc in dcols], 1))
    qkg = f(inp["qk_norm_g"])
    pcols = np.zeros((L, 128, 24), np.float32)
    two = lambda v: np.ascontiguousarray(v.reshape(2, 128).T)
    for l in range(L):
        for qk in range(2):
            g = qkg[l, qk]
            pcols[l, :, 2 * qk] = np.tile(g, 2)
            pcols[l, :, 2 * qk + 1] = np.tile(g[pa], 2)
        for d in range(2):
            pcols[l, :, 4 + 2 * d:6 + 2 * d] = two(f(inp["rwkv_w0"])[l, d])
            pcols[l, :, 8 + 2 * d:10 + 2 * d] = two(f(inp["rwkv_a0"])[l, d])
            pcols[l, :, 16 + 2 * d:18 + 2 * d] = two(f(inp["rwkv_rho"])[l, d].reshape(256))
        pcols[l, :, 12:14] = two(f(inp["rwkv_k_k"])[l])
        pcols[l, :, 14:16] = two(f(inp["rwkv_k_a"])[l])
    shared = dict(
        w_mod=f(inp["w_mod"]), b_mod=f(inp["b_mod"]), norm_g=f(inp["norm_g"]), ffn_w_in=f(inp["ffn_w_in"]),
        ffn_w_out=f(inp["ffn_w_out"]), w_in=w_in, w_out=f(inp["w_out"]), final_g=f(inp["final_norm_g"]),
        w_sw=w_sw, w_d=w_d, pcols=pcols, mu=f(inp["rwkv_mu"]), w2=f(inp["rwkv_w2"]), a2=f(inp["rwkv_a2"]),
        g2=f(inp["rwkv_g2"]), ret_g=f(inp["ret_norm_g"]), ln_g=f(inp["rwkv_ln_g"]), ln_b=f(inp["rwkv_ln_b"]),
        sink=f(inp["attn_sink"]))
    shared.update(_consts(T))
    x, c, ctx, c_ctx = f(inp["x"]), f(inp["c"]), f(inp["ctx"]), f(inp["c_ctx"])
    maps = []
    for b in range(x.shape[0]):
        m = dict(shared)
        m["xin"] = np.ascontiguousarray(np.concatenate([x[b], ctx[b]], 0))
        m["cvec"] = np.ascontiguousarray(np.stack([c[b], c_ctx], 0))
        maps.append(m)
    return maps


_PROG = {}


def kernel(**inputs):
    x = np.asarray(inputs["x"])
    B, T, _ = x.shape
    CTX = np.asarray(inputs["ctx"]).shape[1]
    L = np.asarray(inputs["w_in"]).shape[0]
    key = (T, CTX, L)
    if key not in _PROG:
        _PROG[key] = build_program(T, CTX, L)[0]
    nc = _PROG[key]
    maps = _host_maps(inputs, T, CTX, L)
    res = run_bass_kernel_spmd(nc, maps, core_ids=list(range(B)))
    return np.stack([np.asarray(r["out"], dtype=np.float32) for r in res.results], 0)
```

```python
import contextlib
import numpy as np
import concourse.bass as bass
import concourse.mybir as mybir
from concourse.bass_utils import run_bass_kernel_spmd

ACT = mybir.ActivationFunctionType
ALU = mybir.AluOpType
AX = mybir.AxisListType
F32 = mybir.dt.float32
BF16 = mybir.dt.bfloat16

D = 1024
DFF = 2816
NMOD = 9
RMS_EPS = 1e-6
STRICT = True


class PSem:
    def __init__(self, h):
        self.h = h
        self.val = 0


class Eng:
    def __init__(self, name, h, sem):
        self.name, self.h, self.sem = name, h, sem
        self.n = 0
        self.seen = {}
        self.dseen = {}


class Buf:
    def __init__(self, k, t, name):
        self.k, self.t, self.name = k, t, name
        self.w = None
        self.r = {}
        self.psem = None
        self.dma_w = 0
        self.dma_rw = 0

    def __getitem__(self, key):
        return self.t[key]

    @property
    def v(self):
        return _VA(self)


class View:
    def __init__(self, buf, ap):
        self.buf, self.ap = buf, ap

    def rr(self, pat, **kw):
        return View(self.buf, self.ap.rearrange(pat, **kw))

    def bc(self, shape):
        return View(self.buf, self.ap.to_broadcast(list(shape)))

    def __getitem__(self, key):
        return View(self.buf, self.ap[key])


class _VA:
    def __init__(self, b):
        self.b = b

    def __getitem__(self, key):
        return View(self.b, self.b.t[key])


def _bufs(*xs):
    out = []
    for x in xs:
        if isinstance(x, View) and x.buf not in out:
            out.append(x.buf)
    return out


def _ap(x):
    return x.ap if isinstance(x, View) else x


class K:
    def __init__(self, nc):
        self.nc = nc
        self.stack = contextlib.ExitStack()
        self.E = {}
        for name, h in (("pe", nc.tensor), ("act", nc.scalar), ("dve", nc.vector),
                        ("pool", nc.gpsimd), ("sp", nc.sync)):
            sem = self.stack.enter_context(nc.semaphore("s_" + name))
            self.E[name] = Eng(name, h, sem)
        self.free_psems = []
        self.n_psems = 0
        self.all_psems = []
        self.bufs = []
        self.uid = 0
        self.ninstr = 0
        self.marks = []

    def _psem(self):
        if self.free_psems:
            return self.free_psems.pop()
        self.n_psems += 1
        h = self.stack.enter_context(self.nc.semaphore("d%d" % self.n_psems))
        p = PSem(h)
        self.all_psems.append(p)
        return p

    def sb(self, st, shape, dtype, name):
        self.uid += 1
        t = st.enter_context(self.nc.sbuf_tensor("%s_%d" % (name, self.uid), list(shape), dtype))
        b = Buf(self, t, name)
        st.callback(self._release, b)
        return b

    def ps(self, st, shape, dtype, name):
        self.uid += 1
        t = st.enter_context(self.nc.psum_tensor("%s_%d" % (name, self.uid), list(shape), dtype))
        return Buf(self, t, name)

    def _release(self, b):
        if b.psem is not None and not getattr(b.psem, "sw", False):
            self.free_psems.append(b.psem)
        b.psem = None

    def _wait_eng(self, E, X, n):
        if E.seen.get(X.name, 0) >= n:
            return
        E.h.wait_ge(X.sem, n)
        E.seen[X.name] = n
        self.ninstr += 1

    def _wait_d(self, E, p, v):
        if v <= 0 or E.dseen.get(id(p), 0) >= v:
            return
        E.h.wait_ge(p.h, v)
        E.dseen[id(p)] = v
        self.ninstr += 1

    def _deps(self, E, r, w):
        for b in r:
            if b.w is not None:
                self._wait_eng(E, b.w[0], b.w[1])
            if b.psem is not None:
                self._wait_d(E, b.psem, b.dma_w)
        for b in w:
            if b.w is not None and (STRICT or b.w[0] is not E) and not (E.name == "pe" and b.w[0] is E):
                self._wait_eng(E, b.w[0], b.w[1])
            for X, n in b.r.items():
                if STRICT or X is not E:
                    self._wait_eng(E, X, n)
            if b.psem is not None:
                self._wait_d(E, b.psem, b.dma_rw)

    def op(self, eng, fn, r=(), w=()):
        E = self.E[eng]
        self._deps(E, r, w)
        ins = fn(E.h)
        E.n += 1
        ins.then_inc(E.sem, 1)
        self.ninstr += 1
        for b in w:
            b.w = (E, E.n)
            b.r = {}
        for b in r:
            if b.w is None or b.w[0] is not E or b.w[1] != E.n:
                b.r[E] = E.n
        return ins

    def dma(self, q, out, in_, r=(), w=(), **kw):
        Q = self.E[q]
        self._deps(Q, r, w)
        anchor = w[0] if w else r[0]
        if anchor.psem is None:
            if q == "pool":
                self.n_psems += 1
                anchor.psem = PSem(self.stack.enter_context(self.nc.semaphore("w%d" % self.n_psems)))
                anchor.psem.sw = True
                self.all_psems.append(anchor.psem)
            else:
                anchor.psem = self._psem()
        p = anchor.psem
        assert getattr(p, "sw", False) == (q == "pool"), "buffer mixes SW and HW DGE DMAs: " + anchor.name
        Q.h.dma_start(out=out, in_=in_, **kw).then_inc(p.h, 16)
        p.val += 16
        self.ninstr += 1
        anchor.dma_rw = p.val
        if w:
            anchor.dma_w = p.val
            anchor.w = None
            anchor.r = {}

    def mm(self, out, lhsT, rhs, start=True, stop=True):
        return self.op("pe", lambda e: e.matmul(out.ap, lhsT.ap, rhs.ap, start=start, stop=stop),
                       r=_bufs(lhsT, rhs), w=[out.buf])

    def tr(self, out, in_, ident):
        return self.op("pe", lambda e: e.transpose(out.ap, in_.ap, ident.ap), r=_bufs(in_, ident), w=[out.buf])

    def tt(self, eng, out, in0, in1, op):
        return self.op(eng, lambda e: e.tensor_tensor(out=out.ap, in0=in0.ap, in1=in1.ap, op=op),
                       r=_bufs(in0, in1), w=[out.buf])

    def ts(self, eng, out, in0, s1, s2, op0, op1=None):
        kw = {} if op1 is None else dict(op1=op1)
        return self.op(eng, lambda e: e.tensor_scalar(out=out.ap, in0=in0.ap, scalar1=_ap(s1), scalar2=_ap(s2),
                                                      op0=op0, **kw), r=_bufs(in0, s1, s2), w=[out.buf])

    def stt(self, eng, out, in0, scalar, in1, op0, op1):
        return self.op(eng, lambda e: e.scalar_tensor_tensor(out=out.ap, in0=in0.ap, scalar=_ap(scalar), in1=in1.ap,
                                                             op0=op0, op1=op1), r=_bufs(in0, scalar, in1), w=[out.buf])

    def act(self, out, in_, func, bias=None, scale=None, accum=None):
        kw = {}
        if bias is not None:
            kw["bias"] = _ap(bias)
        if scale is not None:
            kw["scale"] = _ap(scale)
        w = [out.buf]
        if accum is not None:
            kw["accum_out"] = accum.ap
            w.append(accum.buf)
        return self.op("act", lambda e: e.activation(out=out.ap, in_=in_.ap, func=func, **kw),
                       r=_bufs(in_, bias, scale), w=w)

    def cp(self, eng, out, in_):
        if eng == "act":
            return self.op("act", lambda e: e.copy(out=out.ap, in_=in_.ap), r=[in_.buf], w=[out.buf])
        return self.op(eng, lambda e: e.tensor_copy(out=out.ap, in_=in_.ap), r=[in_.buf], w=[out.buf])

    def memset(self, eng, out, val):
        return self.op(eng, lambda e: e.memset(out.ap, val), w=[out.buf])

    def recip(self, out, in_):
        return self.op("dve", lambda e: e.reciprocal(out=out.ap, in_=in_.ap), r=[in_.buf], w=[out.buf])

    def ld(self, q, out, dram, **kw):
        return self.dma(q, out.ap, dram, w=[out.buf], **kw)

    def stv(self, q, dram, in_, **kw):
        return self.dma(q, dram, in_.ap, r=[in_.buf], **kw)

    def barrier(self):
        for E in self.E.values():
            for X in self.E.values():
                if X is not E and X.n > 0:
                    self._wait_eng(E, X, X.n)
            for p in self.all_psems:
                self._wait_d(E, p, p.val)


def mm(k, out_ap, lhsT_ap, rhs_ap, start, stop, r, w):
    return k.op("pe", lambda e: e.matmul(out_ap, lhsT_ap, rhs_ap, start=start, stop=stop), r=r, w=w)


def rowphase(k, st, cfg):
    nc = k.nc
    ffn = cfg["ffn"]
    pre = cfg.get("pre")
    post = cfg.get("post")
    w1 = k.sb(st, [128, 8, 2 * DFF], BF16, "w1")
    w2 = k.sb(st, [128, 22, D], BF16, "w2")
    w1v = ffn["w_in"].rearrange("(c p) f -> p c f", p=128)
    for c in range(8):
        for hlf in range(4):
            f0 = hlf * 1408
            k.dma("pool", w1[:, c, f0:f0 + 1408], w1v[:, c, f0:f0 + 1408], w=[w1])
    w2v = ffn["w_out"].rearrange("(c p) d -> p c d", p=128)
    for c in range(22):
        k.dma("pool", w2[:, c, :], w2v[:, c, :], w=[w2])
    if pre is not None:
        wo = k.sb(st, [128, 8, D], BF16, "wo")
        wov = pre["w_out"].rearrange("(c p) d -> p c d", p=128)
        for c in range(8):
            k.dma("pool", wo[:, c, :], wov[:, c, :], w=[wo])

    identf = k.sb(st, [128, 128], F32, "identf")
    k.dma("sp", identf[:, :], cfg["ident"], w=[identf])
    xt = [k.sb(st, [128, D], F32, "xt%d" % i) for i in range(4)]
    hT = k.sb(st, [128, 8, 512], BF16, "hT")
    gT = k.sb(st, [128, 22, 512], BF16, "gT")
    sg = k.sb(st, [128, 512], BF16, "sg")
    tmp = k.sb(st, [128, D], F32, "tmp")
    ssq = k.sb(st, [128, 4], F32, "ssq")
    rinv = k.sb(st, [128, 4], F32, "rinv")
    ps_u = [k.ps(st, [128, 512], F32, "psu%d" % i) for i in range(4)]
    ps_y = [k.ps(st, [128, 512], F32, "psy%d" % i) for i in range(2)]
    ps_t = k.ps(st, [128, D], F32, "pst")
    Ac = k.sb(st, [128, 8], F32, "Ac")
    Bc = k.sb(st, [128, 8], F32, "Bc")
    Gc = k.sb(st, [128, 8], F32, "Gc")
    A2c = k.sb(st, [128, 8], F32, "A2c")
    B2c = k.sb(st, [128, 8], F32, "B2c")
    G = k.sb(st, [128, D], F32, "G")
    PG = k.sb(st, [128, D], F32, "PG") if pre is not None else None
    gvec2 = None
    if post is not None and post["kind"] == "final":
        gvec2 = k.sb(st, [128, D], F32, "gvec2")
        k.dma("sp", gvec2[:, :], post["g"].partition_broadcast(128), w=[gvec2])

    def col(ap1d):
        return ap1d.rearrange("(c p) -> p c", p=128)

    def load_cols(dst, ap1d):
        k.dma("sp", dst[:, :], col(ap1d), w=[dst], allow_slow_non_contiguous=True)

    def setup_mod(mr):
        load_cols(Gc, ffn["g"])
        load_cols(Ac, cfg["mod"][mr, ffn["scale"] * D:(ffn["scale"] + 1) * D])
        load_cols(Bc, cfg["mod"][mr, ffn["shift"] * D:(ffn["shift"] + 1) * D])
        k.op("dve", lambda e: e.scalar_tensor_tensor(out=Ac[:, :], in0=Ac[:, :], scalar=1.0, in1=Gc[:, :],
                                                     op0=ALU.add, op1=ALU.mult), r=[Ac, Gc], w=[Ac])
        if post is not None and post["kind"] == "hT":
            load_cols(Gc, post["g"])
            load_cols(A2c, cfg["mod"][mr, post["scale"] * D:(post["scale"] + 1) * D])
            load_cols(B2c, cfg["mod"][mr, post["shift"] * D:(post["shift"] + 1) * D])
            k.op("dve", lambda e: e.scalar_tensor_tensor(out=A2c[:, :], in0=A2c[:, :], scalar=1.0, in1=Gc[:, :],
                                                         op0=ALU.add, op1=ALU.mult), r=[A2c, Gc], w=[A2c])
        k.dma("sp", G[:, :], cfg["mod"][mr, ffn["gate"] * D:(ffn["gate"] + 1) * D].partition_broadcast(128), w=[G])
        k.op("dve", lambda e: e.tensor_scalar(out=G[:, :], in0=G[:, :], scalar1=0.5, scalar2=None, op0=ALU.mult),
             r=[G], w=[G])
        if pre is not None:
            k.dma("sp", PG[:, :], cfg["mod"][mr, pre["gate"] * D:(pre["gate"] + 1) * D].partition_broadcast(128),
                  w=[PG])

    try:
        tmp2 = [tmp, k.sb(st, [128, D], F32, "tmpb")]
    except AssertionError:
        tmp2 = [tmp, tmp]

    def rms_all(xs):
        m_ = len(xs)
        junk = gT[:, 0:2, :]
        for s_, xb in enumerate(xs):
            k.op("act", lambda e: e.activation(out=junk, in_=xb[:, :].rearrange("p (a b) -> p a b", a=2),
                                               func=ACT.Square, accum_out=ssq[:, s_:s_ + 1]), r=[xb], w=[gT, ssq])
        k.op("dve", lambda e: e.tensor_scalar(out=rinv[:, 0:m_], in0=ssq[:, 0:m_], scalar1=1.0 / D, scalar2=RMS_EPS,
                                              op0=ALU.mult, op1=ALU.add), r=[ssq], w=[rinv])
        k.op("act", lambda e: e.activation(out=rinv[:, 0:m_], in_=rinv[:, 0:m_], func=ACT.Sqrt), r=[rinv], w=[rinv])
        k.op("dve", lambda e: e.reciprocal(out=rinv[:, 0:m_], in_=rinv[:, 0:m_]), r=[rinv], w=[rinv])

    def norm_mod_T_all(xs, A, B, dstT):
        rms_all(xs)
        for s, xb in enumerate(xs):
            tm = tmp2[s % 2]
            k.op("dve", lambda e: e.tensor_scalar(out=tm[:, :], in0=xb[:, :], scalar1=rinv[:, s:s + 1], scalar2=None,
                                                  op0=ALU.mult), r=[xb, rinv], w=[tm])
            for c in range(8):
                k.op("pe", lambda e: e.transpose(ps_t[:, c * 128:(c + 1) * 128], tm[:, c * 128:(c + 1) * 128],
                                                 identf[:, :]), r=[tm, identf], w=[ps_t])
            for c in range(8):
                if c % 2 == 0:
                    k.op("dve", lambda e: e.tensor_scalar(out=dstT[:, c, s * 128:(s + 1) * 128],
                                                          in0=ps_t[:, c * 128:(c + 1) * 128], scalar1=A[:, c:c + 1],
                                                          scalar2=B[:, c:c + 1], op0=ALU.mult, op1=ALU.add),
                         r=[ps_t, A, B], w=[dstT])
                else:
                    k.op("act", lambda e: e.activation(out=dstT[:, c, s * 128:(s + 1) * 128],
                                                       in_=ps_t[:, c * 128:(c + 1) * 128], func=ACT.Identity,
                                                       scale=A[:, c:c + 1], bias=B[:, c:c + 1]),
                         r=[ps_t, A, B], w=[dstT])

    ui = 0
    yi = 0
    for (tok0, ntok, mr) in cfg["segs"]:
        setup_mod(mr)
        for t0 in range(tok0, tok0 + ntok, 512):
            n = min(512, tok0 + ntok - t0)
            ns = n // 128
            for s in range(ns):
                k.dma("sp", xt[s][:, :], cfg["xin"][t0 + s * 128:t0 + (s + 1) * 128, :], w=[xt[s]])
            if pre is not None:
                for s in range(ns):
                    k.dma("sp", tmp[:, :], pre["o"][t0 + s * 128:t0 + (s + 1) * 128, :], w=[tmp])
                    for c in range(8):
                        k.op("pe", lambda e: e.transpose(ps_t[:, c * 128:(c + 1) * 128], tmp[:, c * 128:(c + 1) * 128],
                                                         identf[:, :]), r=[tmp, identf], w=[ps_t])
                    k.op("act", lambda e: e.copy(out=hT[:, :, s * 128:(s + 1) * 128],
                                                 in_=ps_t[:, :].rearrange("p (c t) -> p c t", c=8)), r=[ps_t], w=[hT])
                for s in range(ns):
                    for hf in range(2):
                        py = ps_y[yi % 2]
                        yi += 1
                        for c in range(8):
                            mm(k, py[:, :], hT[:, c, s * 128:(s + 1) * 128], wo[:, c, hf * 512:(hf + 1) * 512],
                               c == 0, c == 7, r=[hT, wo], w=[py])
                        sl = slice(hf * 512, (hf + 1) * 512)
                        k.op("dve", lambda e, py=py, sl=sl: e.tensor_tensor(out=tmp[:, sl], in0=py[:, :],
                                                                             in1=PG[:, sl], op=ALU.mult),
                             r=[py, PG], w=[tmp])
                        k.op("dve", lambda e, s=s, sl=sl: e.tensor_tensor(out=xt[s][:, sl], in0=xt[s][:, sl],
                                                                            in1=tmp[:, sl], op=ALU.add),
                             r=[xt[s], tmp], w=[xt[s]])
            norm_mod_T_all(xt[0:ns], Ac, Bc, hT)
            for fc in range(22 if cfg.get("stage", 9) >= 2 else 0):
                p1 = ps_u[ui % 4]
                p2 = ps_u[(ui + 1) % 4]
                ui += 2
                for c in range(8):
                    mm(k, p1[:, 0:n], w1[:, c, fc * 128:(fc + 1) * 128], hT[:, c, 0:n], c == 0, c == 7,
                       r=[w1, hT], w=[p1])
                for c in range(8):
                    mm(k, p2[:, 0:n], w1[:, c, DFF + fc * 128:DFF + (fc + 1) * 128], hT[:, c, 0:n], c == 0, c == 7,
                       r=[w1, hT], w=[p2])
                k.op("act", lambda e, p1=p1: e.activation(out=sg[:, 0:n], in_=p1[:, 0:n], func=ACT.Silu),
                     r=[p1], w=[sg])
                k.op("dve", lambda e, p2=p2, fc=fc: e.tensor_tensor(out=gT[:, fc, 0:n], in0=sg[:, 0:n],
                                                                     in1=p2[:, 0:n], op=ALU.mult),
                     r=[sg, p2], w=[gT])
            for s in range(ns if cfg.get("stage", 9) >= 3 else 0):
                for hf in range(2):
                    py = ps_y[yi % 2]
                    yi += 1
                    for fc in range(22):
                        mm(k, py[:, :], gT[:, fc, s * 128:(s + 1) * 128], w2[:, fc, hf * 512:(hf + 1) * 512],
                           fc == 0, fc == 21, r=[gT, w2], w=[py])
                    sl = slice(hf * 512, (hf + 1) * 512)
                    k.op("dve", lambda e, py=py, sl=sl: e.tensor_tensor(out=tmp[:, sl], in0=py[:, :], in1=G[:, sl],
                                                                         op=ALU.mult), r=[py, G], w=[tmp])
                    k.op("dve", lambda e, s=s, sl=sl: e.tensor_tensor(out=xt[s][:, sl], in0=xt[s][:, sl],
                                                                        in1=tmp[:, sl], op=ALU.add),
                         r=[xt[s], tmp], w=[xt[s]])
            if post is not None and post["kind"] == "final":
                if mr == 0:
                    rms_all(xt[0:ns])
                    for s in range(ns):
                        r0 = t0 + s * 128
                        tm = tmp2[s % 2]
                        k.op("dve", lambda e: e.scalar_tensor_tensor(out=tm[:, :], in0=xt[s][:, :],
                                                                     scalar=rinv[:, s:s + 1], in1=gvec2[:, :],
                                                                     op0=ALU.mult, op1=ALU.mult),
                             r=[xt[s], rinv, gvec2], w=[tm])
                        k.dma("sp", post["out"][r0:r0 + 128, :], tm[:, :], r=[tm])
                continue
            for s in range(ns):
                r0 = t0 + s * 128
                k.dma("sp", cfg["xout"][r0:r0 + 128, :], xt[s][:, :], r=[xt[s]])
            if post is not None and post["kind"] == "hT":
                norm_mod_T_all(xt[0:ns], A2c, B2c, hT)
            if post is not None and post["kind"] == "hT":
                for c in range(8):
                    k.dma("sp", post["hT"][c * 128:(c + 1) * 128, t0:t0 + n], hT[:, c, 0:n], r=[hT])


CH = 64
DECAY_SCALE = 0.6065306597126334


def projphase(k, st, cfg):
    S = cfg["S"]
    C = cfg["C"]
    Wt = cfg["W"]
    T, CTX = cfg["T"], cfg["CTX"]
    hTd = S["hT"]
    wabc = k.sb(st, [128, 8, 2304], BF16, "wabc")
    wgd = k.sb(st, [128, 8, 128], BF16, "wgd")
    wsw = k.sb(st, [128, 8, 1280], BF16, "wsw")
    wv = Wt["w_in"].rearrange("(c p) f -> p c f", p=128)
    for c in range(8):
        k.ld("pool", wabc.v[:, c, 0:1152], wv[:, c, 0:1152])
        k.ld("pool", wabc.v[:, c, 1152:2304], wv[:, c, 1152:2304])
        k.ld("pool", wgd.v[:, c, :], wv[:, c, 3072:3200])
        k.ld("pool", wsw.v[:, c, :], Wt["w_sw"].rearrange("(c p) f -> p c f", p=128)[:, c, :])
    wda = [k.sb(st, [128, 8, 896], BF16, "wda%d" % d) for d in range(2)]
    wdb = [k.sb(st, [128, 8, 896], BF16, "wdb%d" % d) for d in range(2)]
    with contextlib.ExitStack() as st2:
        wst = k.sb(st2, [128, 8, 896], F32, "wst")
        mub = k.sb(st2, [128, 896], F32, "mub")
        wtm = k.sb(st2, [128, 8, 896], F32, "wtm")
        for d in range(2):
            for c in range(8):
                k.ld("sp", wst.v[:, c, :], Wt["w_d"][d].rearrange("(c p) f -> p c f", p=128)[:, c, :])
            k.ld("sp", mub.v[:, :], Wt["mu"][d].partition_broadcast(128))
            for c in range(8):
                k.tt("dve", wtm.v[:, c, :], wst.v[:, c, :], mub.v[:, :], ALU.mult)
            k.cp("act", wda[d].v[:, :, :], wtm.v[:, :, :])
            k.tt("dve", wdb[d].v[:, :, :], wst.v[:, :, :], wtm.v[:, :, :], ALU.subtract)
        k.barrier()
    pc = k.sb(st, [128, 24], F32, "pcols")
    k.ld("sp", pc.v[:, :], Wt["pcols"])
    oma = k.sb(st, [128, 2], F32, "oma")
    k.ts("dve", oma.v[:, :], pc.v[:, 14:16], -1.0, 1.0, ALU.mult, ALU.add)
    identf = k.sb(st, [128, 128], F32, "identf")
    k.ld("sp", identf.v[:, :], C["ident"])
    bd = k.sb(st, [128, 128], F32, "bd")
    k.ld("sp", bd.v[:, :], C["bd64"])
    e2 = k.sb(st, [128, 2], F32, "e2")
    k.ld("sp", e2.v[:, :], C["e2"])
    rtab = k.sb(st, [128, 2, 2, 3, CH], F32, "rtab")
    k.ld("sp", rtab.v[:, :, :, :, :], C["ret_tab"])
    w2s = k.sb(st, [64, 2, 256], BF16, "w2s")
    a2s = k.sb(st, [64, 2, 256], BF16, "a2s")
    g2s = k.sb(st, [128, 256], BF16, "g2s")
    for d in range(2):
        k.ld("pool", w2s.v[:, d, :], Wt["w2"][d])
        k.ld("pool", a2s.v[:, d, :], Wt["a2"][d])
    k.ld("pool", g2s.v[:, :], Wt["g2"])

    NT = 512
    hTt = k.sb(st, [128, 8, NT + 2], BF16, "hTt")
    ca = k.sb(st, [128, NT], F32, "ca")
    sa = k.sb(st, [128, NT], F32, "sa")
    cs_ = k.sb(st, [128, NT], F32, "cs")
    ss_ = k.sb(st, [128, NT], F32, "ss")
    gca = k.sb(st, [128, 2, NT], F32, "gca")
    gsa = k.sb(st, [128, 2, NT], F32, "gsa")
    F = [k.sb(st, [128, NT], F32, "f%d" % i) for i in range(12)]
    ob = k.sb(st, [128, NT], BF16, "ob")
    sbf = [k.sb(st, [128, NT], BF16, "sbf%d" % i) for i in range(2)]
    sbi = [0]

    def st_bf(dst, srcv, n):
        b = sbf[sbi[0] % 2]
        sbi[0] += 1
        k.cp("act", b.v[:, 0:n], srcv)
        k.stv("sp", dst, b.v[:, 0:n])
    tb = k.sb(st, [128, 512], F32, "tb")
    tbb = k.sb(st, [128, 512], BF16, "tbb")
    thb = k.sb(st, [64, NT], BF16, "thb")
    alb = k.sb(st, [64, NT], BF16, "alb")
    sgb = k.sb(st, [128, NT], BF16, "sgb")
    bs4 = k.sb(st, [128, 4, 4], F32, "bs4")
    pcx = k.sb(st, [128, NT // CH], F32, "pcx")
    pcx2 = k.sb(st, [128, NT // CH], F32, "pcx2")
    F2 = [k.sb(st, [128, NT], F32, "g%d" % i) for i in range(9)]
    PS = [k.ps(st, [128, 512], F32, "pp%d" % i) for i in range(8)]
    pi = [0]

    def nps():
        p = PS[pi[0] % 8]
        pi[0] += 1
        return p

    def chain(p, wlist, M, n, col0=1):
        tot = len(wlist) * 8
        i = 0
        for (wb, c0, sh) in wlist:
            for c in range(8):
                k.mm(p.v[0:M, 0:n], wb.v[:, c, c0:c0 + M], hTt.v[:, c, col0 + sh:col0 + sh + n], i == 0, i == tot - 1)
                i += 1

    def chain_tok(p, wlist, ncols, s):
        tot = len(wlist) * 8
        i = 0
        for (wb, c0, sh) in wlist:
            for c in range(8):
                k.mm(p.v[:, 0:ncols], hTt.v[:, c, 1 + sh + s * 128:1 + sh + (s + 1) * 128], wb.v[:, c, c0:c0 + ncols],
                     i == 0, i == tot - 1)
                i += 1

    def rope_out(P, Psw, ctab, stab, dst, n):
        k.tt("dve", F[10].v[:, 0:n], P.v[:, 0:n], ctab, ALU.mult)
        k.tt("dve", F[11].v[:, 0:n], Psw.v[:, 0:n], stab, ALU.mult)
        k.tt("dve", dst, F[10].v[:, 0:n], F[11].v[:, 0:n], ALU.add)

    segs = [(0, T, True), (T, CTX, False)]
    for (seg0, seglen, latent) in segs:
        for t0 in range(seg0, seg0 + seglen, NT):
            n = min(NT, seg0 + seglen - t0)
            ns = n // 128
            nch = n // CH
            lo = t0 - 1 if t0 > seg0 else t0
            hi = t0 + n + 1 if t0 + n < seg0 + seglen else t0 + n
            if lo == t0:
                k.memset("dve", hTt.v[:, :, 0:1], 0.0)
            if hi == t0 + n:
                k.memset("dve", hTt.v[:, :, n + 1:n + 2], 0.0)
            for c in range(8):
                k.ld("sp", hTt.v[:, c, 1 - (t0 - lo):1 + n + (hi - t0 - n)], hTd[c * 128:(c + 1) * 128, lo:hi])
            if latent:
                k.ld("sp", ca.v[:, 0:n], C["ropeA_c"][:, t0:t0 + n])
                k.ld("sp", sa.v[:, 0:n], C["ropeA_s"][:, t0:t0 + n])
                k.ld("sp", cs_.v[:, 0:n], C["ropeS_c"][:, t0:t0 + n])
                k.ld("sp", ss_.v[:, 0:n], C["ropeS_s"][:, t0:t0 + n])
                for qk in range(2):
                    k.ts("dve", gca.v[:, qk, 0:n], ca.v[:, 0:n], pc.v[:, 2 * qk:2 * qk + 1], None, ALU.mult)
                    k.ts("dve", gsa.v[:, qk, 0:n], sa.v[:, 0:n], pc.v[:, 2 * qk + 1:2 * qk + 2], None, ALU.mult)
            for grp, base, swb, dq, dk_, dv_ in (("A", 0, 0, S["QA"], S["KA"], S["VA"]),
                                                ("B", 512, 384, S["QB"], S["KB"], S["VB"])):
                for j in range(3):
                    P, Psw = nps(), nps()
                    chain(P, [(wabc, base + j * 128, 0)], 128, n)
                    if latent:
                        chain(Psw, [(wsw, swb + j * 128, 0)], 128, n)
                    qk = 0 if j < 2 else 1
                    if grp == "B":
                        k.act(F[0].v[:, 0:n], P.v[:, 0:n], ACT.Square)
                        pss = nps()
                        k.mm(pss.v[:, 0:n], bd.v[:, :], F[0].v[:, 0:n])
                        k.ts("dve", F[1].v[:, 0:n], pss.v[:, 0:n], 1.0 / 64, RMS_EPS, ALU.mult, ALU.add)
                        k.act(F[1].v[:, 0:n], F[1].v[:, 0:n], ACT.Sqrt)
                        k.recip(F[1].v[:, 0:n], F[1].v[:, 0:n])
                        k.tt("dve", F[2].v[:, 0:n], P.v[:, 0:n], F[1].v[:, 0:n], ALU.mult)
                        if latent:
                            k.tt("dve", F[3].v[:, 0:n], Psw.v[:, 0:n], F[1].v[:, 0:n], ALU.mult)
                            rope_out(F[2], F[3], gca.v[:, qk, 0:n], gsa.v[:, qk, 0:n], ob.v[:, 0:n], n)
                        else:
                            k.ts("dve", ob.v[:, 0:n], F[2].v[:, 0:n], pc.v[:, 2 * qk:2 * qk + 1], None, ALU.mult)
                    else:
                        if latent:
                            rope_out(P, Psw, ca.v[:, 0:n], sa.v[:, 0:n], ob.v[:, 0:n], n)
                        else:
                            k.cp("act", ob.v[:, 0:n], P.v[:, 0:n])
                    dst = dq[j * 128:(j + 1) * 128, t0:t0 + n] if j < 2 else dk_[:, t0:t0 + n]
                    k.stv("sp", dst, ob.v[:, 0:n])
                for s in range(ns):
                    P = nps()
                    chain_tok(P, [(wabc, base + 384, 0)], 128, s)
                    k.cp("act", tbb.v[:, 0:128], P.v[:, 0:128])
                    k.stv("sp", dv_[t0 + s * 128:t0 + (s + 1) * 128, :], tbb.v[:, 0:128])
            for j in range(4):
                P, Psw = nps(), nps()
                chain(P, [(wabc, 1024 + j * 128, 0)], 128, n)
                g = j % 2
                if latent:
                    chain(Psw, [(wsw, 768 + j * 128, 0)], 128, n)
                    rope_out(P, Psw, cs_.v[:, 0:n], ss_.v[:, 0:n], F[0].v[:, 0:n], n)
                else:
                    k.cp("act", F[0].v[:, 0:n], P.v[:, 0:n])
                x3 = F[0].v[:, 0:n].rr("p (c i) -> p c i", i=CH)
                for d in range(2):
                    if j < 2:
                        k.tt("dve", F[1].v[:, 0:n].rr("p (c i) -> p c i", i=CH), x3,
                             rtab.v[:, g, d, 0:1, :].bc([128, nch, CH]), ALU.mult)
                        st_bf(S["C_Rt"][d, g * 128:(g + 1) * 128, t0:t0 + n], F[1].v[:, 0:n], n)
                    else:
                        k.tt("dve", F[1].v[:, 0:n].rr("p (c i) -> p c i", i=CH), x3,
                             rtab.v[:, g, d, 1:2, :].bc([128, nch, CH]), ALU.mult)
                        st_bf(S["C_Kt"][d, g * 128:(g + 1) * 128, t0:t0 + n], F[1].v[:, 0:n], n)
                        k.tt("dve", F[2].v[:, 0:n].rr("p (c i) -> p c i", i=CH), x3,
                             rtab.v[:, g, d, 2:3, :].bc([128, nch, CH]), ALU.mult)
                        for s in range(ns):
                            pt = nps()
                            k.tr(pt.v[:, 0:128], F[2].v[:, s * 128:(s + 1) * 128], identf.v[:, :])
                            k.cp("act", tbb.v[:, 0:128], pt.v[:, 0:128])
                            k.stv("sp", S["C_Kh"][d, t0 + s * 128:t0 + (s + 1) * 128, g * 128:(g + 1) * 128],
                                  tbb.v[:, 0:128])
            for s in range(ns):
                P = nps()
                chain_tok(P, [(wabc, 1536, 0)], 256, s)
                k.cp("act", tbb.v[:, 0:256], P.v[:, 0:256])
                k.stv("sp", S["C_V"][t0 + s * 128:t0 + (s + 1) * 128, :], tbb.v[:, 0:256])
                P = nps()
                chain_tok(P, [(wabc, 1792, 0)], 512, s)
                k.act(tb.v[:, 0:512], P.v[:, 0:512], ACT.Silu)
                k.stv("sp", S["C_G"][t0 + s * 128:t0 + (s + 1) * 128, :], tb.v[:, 0:512])
            P = nps()
            chain(P, [(wgd, 0, 0)], 128, n)
            k.act(sgb.v[:, 0:n], P.v[:, 0:n], ACT.Sigmoid)
            for s in range(ns):
                P = nps()
                k.mm(P.v[:, 0:256], sgb.v[:, s * 128:(s + 1) * 128], g2s.v[:, :])
                k.cp("act", tb.v[:, 0:256], P.v[:, 0:256])
                k.stv("sp", S["D_G"][t0 + s * 128:t0 + (s + 1) * 128, :], tb.v[:, 0:256])
            for d in range(2):
                sh = -1 if d == 0 else 1
                wl = lambda c0: [(wdb[d], c0, 0), (wda[d], c0, sh)]
                P = nps()
                chain(P, wl(768), 64, n)
                k.act(thb.v[:, 0:n], P.v[0:64, 0:n], ACT.Tanh)
                P = nps()
                chain(P, wl(832), 64, n)
                k.cp("act", alb.v[:, 0:n], P.v[0:64, 0:n])
                for s in range(ns):
                    P = nps()
                    chain_tok(P, wl(512), 256, s)
                    k.cp("act", tbb.v[:, 0:256], P.v[:, 0:256])
                    k.stv("sp", S["D_V"][d, t0 + s * 128:t0 + (s + 1) * 128, :], tbb.v[:, 0:256])
                def dgroup(g, FB, pcx):
                    r_, k_, lw, cs, a_, kk, e_, t1, t2 = FB
                    gc = slice(g, g + 1)
                    P = nps()
                    chain(P, wl(g * 128), 128, n)
                    k.cp("act", r_.v[:, 0:n], P.v[:, 0:n])
                    P = nps()
                    chain(P, wl(256 + g * 128), 128, n)
                    k.cp("act", k_.v[:, 0:n], P.v[:, 0:n])
                    yield
                    P = nps()
                    k.mm(P.v[:, 0:n], w2s.v[:, d, g * 128:(g + 1) * 128], thb.v[:, 0:n])
                    k.act(lw.v[:, 0:n], P.v[:, 0:n], ACT.Sigmoid, bias=pc.v[:, 4 + 2 * d + g:5 + 2 * d + g])
                    k.ts("dve", lw.v[:, 0:n], lw.v[:, 0:n], -DECAY_SCALE, None, ALU.mult)
                    yield
                    P = nps()
                    k.mm(P.v[:, 0:n], a2s.v[:, d, g * 128:(g + 1) * 128], alb.v[:, 0:n])
                    k.act(a_.v[:, 0:n], P.v[:, 0:n], ACT.Sigmoid, bias=pc.v[:, 8 + 2 * d + g:9 + 2 * d + g])
                    yield
                    k.ts("dve", kk.v[:, 0:n], k_.v[:, 0:n], pc.v[:, 12 + g:13 + g], None, ALU.mult)
                    k.act(t1.v[:, 0:n], kk.v[:, 0:n], ACT.Square)
                    yield
                    P = nps()
                    k.mm(P.v[:, 0:n], bd.v[:, :], t1.v[:, 0:n])
                    k.act(t1.v[:, 0:n], P.v[:, 0:n], ACT.Sqrt)
                    yield
                    k.ts("dve", t1.v[:, 0:n], t1.v[:, 0:n], 1e-12, None, ALU.max)
                    k.recip(t1.v[:, 0:n], t1.v[:, 0:n])
                    yield
                    k.tt("dve", kk.v[:, 0:n], kk.v[:, 0:n], t1.v[:, 0:n], ALU.mult)
                    yield
                    k.ts("dve", t1.v[:, 0:n], a_.v[:, 0:n], pc.v[:, 14 + g:15 + g], oma.v[:, gc], ALU.mult, ALU.add)
                    k.tt("dve", k_.v[:, 0:n], k_.v[:, 0:n], t1.v[:, 0:n], ALU.mult)
                    yield
                    k.stt("dve", t1.v[:, 0:n], r_.v[:, 0:n], pc.v[:, 16 + 2 * d + g:17 + 2 * d + g], k_.v[:, 0:n],
                          ALU.mult, ALU.mult)
                    for s in range(ns):
                        P = nps()
                        k.mm(P.v[:, 0:2], t1.v[:, s * 128:(s + 1) * 128], e2.v[:, :])
                        k.cp("act", bs4.v[:, s, 2 * g:2 * g + 2], P.v[:, 0:2])
                    k.tt("dve", a_.v[:, 0:n], a_.v[:, 0:n], kk.v[:, 0:n], ALU.mult)
                    yield
                    src, dst = lw, cs
                    k.cp("act", t2.v[:, 0:n], lw.v[:, 0:n])
                    src = t2
                    for stp in (1, 2, 4, 8, 16, 32):
                        s3 = src.v[:, 0:n].rr("p (c i) -> p c i", i=CH)
                        d3 = dst.v[:, 0:n].rr("p (c i) -> p c i", i=CH)
                        if d == 0:
                            k.tt("dve", d3[:, :, stp:], s3[:, :, stp:], s3[:, :, :CH - stp], ALU.add)
                            k.cp("act", d3[:, :, :stp], s3[:, :, :stp])
                        else:
                            k.tt("dve", d3[:, :, :CH - stp], s3[:, :, :CH - stp], s3[:, :, stp:], ALU.add)
                            k.cp("act", d3[:, :, CH - stp:], s3[:, :, CH - stp:])
                        src, dst = dst, src
                        yield
                    cs = src
                    spare = dst
                    cs3 = cs.v[:, 0:n].rr("p (c i) -> p c i", i=CH)
                    tot3 = cs3[:, :, CH - 1:CH] if d == 0 else cs3[:, :, 0:1]
                    k.act(e_.v[:, 0:n], cs.v[:, 0:n], ACT.Exp)
                    yield
                    k.tt("dve", t1.v[:, 0:n], r_.v[:, 0:n], e_.v[:, 0:n], ALU.mult)
                    st_bf(S["D_Rt"][d, g * 128:(g + 1) * 128, t0:t0 + n], t1.v[:, 0:n], n)
                    yield
                    k.tt("dve", e_.v[:, 0:n], cs.v[:, 0:n], lw.v[:, 0:n], ALU.subtract)
                    k.act(e_.v[:, 0:n], e_.v[:, 0:n], ACT.Exp)
                    yield
                    k.stt("dve", t1.v[:, 0:n], kk.v[:, 0:n], -1.0, e_.v[:, 0:n], ALU.mult, ALU.mult)
                    st_bf(S["D_At"][d, g * 128:(g + 1) * 128, t0:t0 + n], t1.v[:, 0:n], n)
                    yield
                    for s in range(ns):
                        pt = nps()
                        k.tr(pt.v[:, 0:128], t1.v[:, s * 128:(s + 1) * 128], identf.v[:, :])
                        k.cp("act", tbb.v[:, 0:128], pt.v[:, 0:128])
                        k.stv("sp", S["D_Atok"][d, t0 + s * 128:t0 + (s + 1) * 128, g * 128:(g + 1) * 128],
                              tbb.v[:, 0:128])
                    k.act(e_.v[:, 0:n], cs.v[:, 0:n], ACT.Exp, scale=-1.0)
                    k.tt("dve", t1.v[:, 0:n], a_.v[:, 0:n], e_.v[:, 0:n], ALU.mult)
                    st_bf(S["D_Bt"][d, g * 128:(g + 1) * 128, t0:t0 + n], t1.v[:, 0:n], n)
                    yield
                    k.tt("dve", t1.v[:, 0:n], k_.v[:, 0:n], e_.v[:, 0:n], ALU.mult)
                    st_bf(S["D_Kt"][d, g * 128:(g + 1) * 128, t0:t0 + n], t1.v[:, 0:n], n)
                    yield
                    k.act(pcx.v[:, 0:nch], tot3.rr("p c o -> p (c o)"), ACT.Exp)
                    k.stv("sp", S["D_PC"][d, g * 128:(g + 1) * 128, t0 // CH:t0 // CH + nch], pcx.v[:, 0:nch])
                    k.tt("dve", e_.v[:, 0:n].rr("p (c i) -> p c i", i=CH), tot3.bc([128, nch, CH]), cs3, ALU.subtract)
                    k.act(e_.v[:, 0:n], e_.v[:, 0:n], ACT.Exp)
                    yield
                    for (srcb, dstd) in ((a_, S["D_Bh"]), (k_, S["D_Kh"])):
                        k.tt("dve", t1.v[:, 0:n], srcb.v[:, 0:n], e_.v[:, 0:n], ALU.mult)
                        for s in range(ns):
                            pt = nps()
                            k.tr(pt.v[:, 0:128], t1.v[:, s * 128:(s + 1) * 128], identf.v[:, :])
                            k.cp("act", tbb.v[:, 0:128], pt.v[:, 0:128])
                            k.stv("sp", dstd[d, t0 + s * 128:t0 + (s + 1) * 128, g * 128:(g + 1) * 128], tbb.v[:, 0:128])
                gens = [dgroup(0, F[0:9], pcx), dgroup(1, F2, pcx2)]
                live = list(gens)
                while live:
                    for g_ in list(live):
                        try:
                            next(g_)
                        except StopIteration:
                            live.remove(g_)
                for s in range(ns):
                    k.stv("sp", S["D_BS"][d, t0 + s * 128:t0 + (s + 1) * 128, :], bs4.v[:, s, :])


def attnphase(k, st, cfg):
    S, C, Wt = cfg["S"], cfg["C"], cfg["W"]
    T, CTX = cfg["T"], cfg["CTX"]
    TT = T + CTX
    NKC = TT // 128
    need_ctx = cfg["need_ctx"]
    kT = k.sb(st, [128, TT], BF16, "kT")
    va = k.sb(st, [128, NKC, 65], BF16, "va")
    qT = [k.sb(st, [128, 512], BF16, "qT%d" % i) for i in range(2)]
    for c0 in range(0, TT, 2048):
        k.memset("dve", kT.v[64:128, c0:min(TT, c0 + 2048)], 0.0)
    for q_ in qT:
        k.memset("dve", q_.v[64:128, :], 0.0)
    pT = [k.sb(st, [128, 512], BF16, "pT%d" % i) for i in range(3)]
    ot = [k.sb(st, [128, 64], F32, "ot%d" % i) for i in range(2)]
    rc = k.sb(st, [128, 1], F32, "rc")
    esk = k.sb(st, [128, 4], F32, "esk")
    mprev = k.sb(st, [128, 128], BF16, "mprev")
    mnext = k.sb(st, [128, 128], BF16, "mnext")
    k.ld("pool", mprev.v[:, :], C["mprev"])
    k.ld("pool", mnext.v[:, :], C["mnext"])
    k.ld("sp", esk.v[:, :], Wt["sink"].partition_broadcast(128))
    k.act(esk.v[:, :], esk.v[:, :], ACT.Exp)
    ps_s = [k.ps(st, [128, 512], F32, "pss%d" % i) for i in range(3)]
    ps_o = [k.ps(st, [128, 512], F32, "pso%d" % i) for i in range(4)]
    cnt = dict(s=0, q=0, o=0, p=0)
    scale = 64 ** -0.5

    def load_kv(kd, vd, g):
        k.ld("sp", kT.v[0:64, :], kd[g * 64:(g + 1) * 64, :])
        k.memset("dve", va.v[:, :, 64:65], 1.0)
        k.ld("sp", va.v[:, :, 0:64], vd[:, g * 64:(g + 1) * 64].rearrange("(c p) j -> p c j", p=128))

    def finish(acc, nq_sub, t0, col, sinkcol):
        for s in range(nq_sub):
            o = ot[cnt["o"] % 2]
            cnt["o"] += 1
            if sinkcol is None:
                k.recip(rc.v[:, :], acc[s].v[:, 64:65])
            else:
                k.tt("dve", rc.v[:, :], acc[s].v[:, 64:65], esk.v[:, sinkcol:sinkcol + 1], ALU.add)
                k.recip(rc.v[:, :], rc.v[:, :])
            k.ts("dve", o.v[:, :], acc[s].v[:, 0:64], rc.v[:, 0:1], None, ALU.mult)
            k.stv("sp", S["O"][t0 + s * 128:t0 + (s + 1) * 128, col:col + 64], o.v[:, :])

    def block(qd, h, t0, nq, chunks, col, sinkcol):
        q = qT[cnt["q"] % 2]
        cnt["q"] += 1
        k.ld("sp", q.v[0:64, 0:nq], qd[h * 64:(h + 1) * 64, t0:t0 + nq])
        nsub = nq // 128
        first, last = {}, {}
        for i, (kc, subs) in enumerate(chunks):
            for (s, _) in subs:
                first.setdefault(s, i)
                last[s] = i

        def qk(i):
            ps = ps_s[cnt["s"] % 3]
            cnt["s"] += 1
            kc = chunks[i][0]
            k.mm(ps.v[:, 0:nq], kT.v[:, kc * 128:(kc + 1) * 128], q.v[:, 0:nq])
            return ps

        nxt = qk(0)
        for i, (kc, subs) in enumerate(chunks):
            ps = nxt
            if i + 1 < len(chunks):
                nxt = qk(i + 1)
            p = pT[cnt["p"] % 3]
            cnt["p"] += 1
            k.act(p.v[:, 0:nq], ps.v[:, 0:nq], ACT.Exp, scale=scale)
            for (s, m_) in subs:
                if m_ is not None:
                    k.tt("dve", p.v[:, s * 128:(s + 1) * 128], p.v[:, s * 128:(s + 1) * 128], m_.v[:, :], ALU.mult)
            for (s, m_) in subs:
                k.mm(ps_o[s].v[:, 0:65], p.v[:, s * 128:(s + 1) * 128], va.v[:, kc, :], i == first[s], i == last[s])
        finish(ps_o, nsub, t0, col, sinkcol)

    for g in range(2):
        load_kv(S["KB"], S["VB"], g)
        for h in (2 * g, 2 * g + 1):
            for t0 in range(0, T, 512):
                nq = min(512, T - t0)
                block(S["QB"], h, t0, nq, [(kc, [(s, None) for s in range(nq // 128)]) for kc in range(NKC)],
                      256 + h * 64, None)
            if need_ctx:
                block(S["QB"], h, T, CTX, [(kc, [(s, None) for s in range(CTX // 128)]) for kc in range(T // 128, NKC)],
                      256 + h * 64, None)
    nb = T // 128
    cchunks = list(range(T // 128, NKC))
    for g in range(2):
        load_kv(S["KA"], S["VA"], g)
        for h in (2 * g, 2 * g + 1):
            for b0 in range(0, nb, 4):
                nsb = min(4, nb - b0)
                chunks = []
                for kc in range(max(0, b0 - 1), min(nb, b0 + nsb + 1)):
                    subs = []
                    for s in range(nsb):
                        dlt = kc - (b0 + s)
                        if dlt == -1:
                            subs.append((s, mprev))
                        elif dlt == 0:
                            subs.append((s, None))
                        elif dlt == 1:
                            subs.append((s, mnext))
                    chunks.append((kc, subs))
                for kc in cchunks:
                    chunks.append((kc, [(s, None) for s in range(nsb)]))
                block(S["QA"], h, b0 * 128, nsb * 128, chunks, h * 64, h)
            if need_ctx:
                block(S["QA"], h, T, CTX, [(kc, [(s, None) for s in range(CTX // 128)]) for kc in cchunks], h * 64, h)


def scanphase(k, st, cfg):
    S, C = cfg["S"], cfg["C"]
    T, CTX = cfg["T"], cfg["CTX"]
    TT = T + CTX
    dplr = cfg["dplr"]
    pre = "D_" if dplr else "C_"
    NCG = 2
    GT_ = NCG * CH
    NCHT = TT // CH
    f3 = lambda b: b.v[:, :, :]
    t64 = lambda nm: k.sb(st, [64, 8, 64], F32, nm)
    b64 = lambda nm: k.sb(st, [64, 8, 64], BF16, nm)
    chm = lambda nm: k.sb(st, [64, 8, GT_], BF16, nm)
    tkm = lambda nm: k.sb(st, [64, NCG, 8, 64], BF16, nm)
    RT = [chm("RT%d" % i) for i in range(2)]
    KT = [chm("KT%d" % i) for i in range(2)]
    KH = [tkm("KH%d" % i) for i in range(2)]
    VV = [tkm("VV%d" % i) for i in range(2)]
    Min = t64("Min")
    k.ld("sp", f3(Min), C["m_in"].rearrange("p (q t) -> p q t", q=8))
    id8 = t64("id8")
    k.ld("sp", f3(id8), C["id8"].rearrange("p (q t) -> p q t", q=8))
    if dplr:
        AT = [chm("AT%d" % i) for i in range(2)]
        BT = [chm("BT%d" % i) for i in range(2)]
        BH = [tkm("BH%d" % i) for i in range(2)]
        AK = [tkm("AK%d" % i) for i in range(2)]
        Mst, Mts = t64("Mst"), t64("Mts")
        k.ld("sp", f3(Mst), C["m_st"].rearrange("p (q t) -> p q t", q=8))
        k.ld("sp", f3(Mts), C["m_ts"].rearrange("p (q t) -> p q t", q=8))
        PCt = k.sb(st, [64, 8, NCHT], F32, "PCt")
        for d in range(2):
            k.ld("sp", PCt.v[:, d * 4:(d + 1) * 4, :], S["D_PC"][d].rearrange("(h j) c -> j h c", j=64))
    else:
        PCr = k.sb(st, [64, 8], F32, "PCr")
        k.ld("sp", PCr.v[:, :], C["ret_pc"])
        GTc = b64("GTc")
        k.tt("dve", f3(GTc), f3(id8), PCr.v[:, :].rr("p (q o) -> p q o", o=1).bc([64, 8, 64]), ALU.mult)
    sets = []
    for c in range(NCG):
        B = dict(ArkT=b64("ArkT%d" % c), Yv=t64("Yv%d" % c), H=t64("H%d" % c))
        if dplr:
            B.update(Ls=[b64("L%d_%d" % (i, c)) for i in range(5)], Ns=[b64("N%d_%d" % (i, c)) for i in range(6)],
                     AakT=b64("AakT%d" % c), ArbT=b64("ArbT%d" % c), Qe=b64("Qe%d" % c), GTt=b64("GTt%d" % c),
                     X=[k.sb(st, [64, 8, 128], F32, "X%d_%d" % (i, c)) for i in range(2)],
                     Xb=k.sb(st, [64, 8, 128], BF16, "Xb_%d" % c),
                     psX=k.ps(st, [64, 1024], F32, "psX%d" % c))
        sets.append(B)
    ST = [b64("ST%d" % i) for i in range(2)]
    Yt = [t64("Yt%d" % i) for i in range(2)]
    psA = [k.ps(st, [64, 512], F32, "psA%d" % i) for i in range(2)]
    psY = [k.ps(st, [64, 512], F32, "psY%d" % i) for i in range(2)]
    cnt = dict(a=0, y=0, e=0)
    p3 = lambda p: p.v[:, :].rr("p (q t) -> p q t", q=8)

    def npa():
        p = psA[cnt["a"] % 2]
        cnt["a"] += 1
        return p

    def ev():
        cnt["e"] += 1
        return "act" if cnt["e"] % 2 else "dve"

    def prod(dst, lhs, rhs, mask):
        p = npa()
        for q in range(8):
            k.mm(p3(p)[:, q, :], lhs(q), rhs(q))
        if mask is None:
            k.cp(ev(), f3(dst), p3(p))
        else:
            k.tt("dve", f3(dst), p3(p), f3(mask), ALU.mult)

    def views(rb, cl):
        sl = {d: slice(cl[d] * CH, (cl[d] + 1) * CH) for d in range(2)}
        dq = lambda q: q // 4
        chs = lambda buf: (lambda q: buf[rb].v[:, q, sl[dq(q)]])
        tks = lambda buf: (lambda q: buf[rb].v[:, cl[dq(q)], q, :])
        return sl, chs, tks

    def ppart(B, rb, cl, grp):
        sl, chs, tks = views(rb, cl)
        rt, kt, kh, vv = chs(RT), chs(KT), tks(KH), tks(VV)
        prod(B["ArkT"], kt, rt, Min)
        yield
        if dplr:
            at, bt, bh = chs(AT), chs(BT), tks(BH)
            Ls, Ns, AakT, ArbT = B["Ls"], B["Ns"], B["AakT"], B["ArbT"]
            prod(Ns[0], bt, at, Mst)
            prod(Ls[0], at, bt, Mts)
            yield
            prod(AakT, kt, at, Mst)
            prod(ArbT, bt, rt, Min)
            yield
            xc, xn = B["X"]
            xb = B["Xb"]
            for d in range(2):
                k.cp("act", xc.v[:, d * 4:(d + 1) * 4, 0:64], AK[rb].v[:, cl[d], d * 4:(d + 1) * 4, :])
            p = npa()
            for q in range(8):
                k.mm(p3(p)[:, q, :], AakT.v[:, q, :], vv(q))
            k.cp("dve", xc.v[:, :, 64:128], p3(p))
            k.cp("act", f3(xb), f3(xc))
            yield
            px = B["psX"].v[:, :].rr("p (q c) -> p q c", q=8)
            for lv in range(6):
                for q in range(8):
                    k.mm(px[:, q, :], Ns[lv].v[:, q, :], xb.v[:, q, :])
                if lv < 5:
                    if lv < 4:
                        prod(Ls[lv + 1], lambda q: Ns[lv].v[:, q, :], lambda q: Ls[lv].v[:, q, :], None)
                    prod(Ns[lv + 1], lambda q: Ls[lv].v[:, q, :], lambda q: Ns[lv].v[:, q, :], None)
                k.tt("dve", f3(xn), f3(xc), px, ALU.add)
                xc, xn = xn, xc
                k.cp("act", f3(xb), f3(xc))
                yield
            wq = lambda q: xb.v[:, q, 0:64]
            uv = lambda q: xb.v[:, q, 64:128]
            p = npa()
            for q in range(8):
                k.mm(p3(p)[:, q, :], wq(q), ArbT.v[:, q, :])
            for d in range(2):
                k.tt("dve", B["Qe"].v[:, d * 4:(d + 1) * 4, :], p3(p)[:, d * 4:(d + 1) * 4, :],
                     RT[rb].v[:, d * 4:(d + 1) * 4, sl[d]], ALU.add)
            p = npa()
            for q in range(8):
                k.mm(p3(p)[:, q, :], wq(q), bh(q))
            for d in range(2):
                c_abs = grp[d] * NCG + cl[d]
                k.tt("dve", B["GTt"].v[:, d * 4:(d + 1) * 4, :], id8.v[:, d * 4:(d + 1) * 4, :],
                     PCt.v[:, d * 4:(d + 1) * 4, c_abs:c_abs + 1].bc([64, 4, 64]), ALU.mult)
            k.tt("dve", f3(B["GTt"]), f3(B["GTt"]), p3(p), ALU.add)
            yield
        p = npa()
        for q in range(8):
            k.mm(p3(p)[:, q, :], B["ArkT"].v[:, q, :], vv(q), True, not dplr)
            if dplr:
                k.mm(p3(p)[:, q, :], ArbT.v[:, q, :], uv(q), False, True)
        k.cp(ev(), f3(B["Yv"]), p3(p))
        p = npa()
        for q in range(8):
            k.mm(p3(p)[:, q, :], kh(q), vv(q), True, not dplr)
            if dplr:
                k.mm(p3(p)[:, q, :], bh(q), uv(q), False, True)
        k.cp(ev(), f3(B["H"]), p3(p))
        yield

    k.memset("dve", f3(ST[0]), 0.0)
    ngl = T // GT_
    ncg_ctx = CTX // GT_
    fw = list(range(ngl, ngl + ncg_ctx)) + list(range(ngl))
    bw = list(range(ngl + ncg_ctx - 1, ngl - 1, -1)) + list(range(ngl - 1, -1, -1))
    order = {0: fw, 1: bw}
    step = 0
    for gi in range(ngl + ncg_ctx):
        rb = gi % 2
        grp = {d: order[d][gi] for d in range(2)}
        for d in range(2):
            t0 = grp[d] * GT_
            qs = slice(d * 4, (d + 1) * 4)
            chv = lambda nm: S[pre + nm][d, :, t0:t0 + GT_].rearrange("(h j) t -> j h t", j=64)
            tkv = lambda ap: ap[t0:t0 + GT_, :].rearrange("(n t) (h j) -> t n h j", t=CH, j=64)
            k.ld("sp", RT[rb].v[:, qs, :], chv("Rt"))
            k.ld("sp", KT[rb].v[:, qs, :], chv("Kt"))
            k.ld("sp", KH[rb].v[:, :, qs, :], tkv(S[pre + "Kh"][d]))
            k.ld("sp", VV[rb].v[:, :, qs, :], tkv(S["D_V"][d] if dplr else S["C_V"]))
            if dplr:
                k.ld("sp", AT[rb].v[:, qs, :], chv("At"))
                k.ld("sp", BT[rb].v[:, qs, :], chv("Bt"))
                k.ld("sp", BH[rb].v[:, :, qs, :], tkv(S["D_Bh"][d]))
                k.ld("sp", AK[rb].v[:, :, qs, :], tkv(S["D_Atok"][d]))
        cls = [{0: ci, 1: NCG - 1 - ci} for ci in range(NCG)]
        gens = [ppart(sets[ci], rb, cls[ci], grp) for ci in range(NCG)]
        live = list(gens)
        while live:
            for g_ in list(live):
                try:
                    next(g_)
                except StopIteration:
                    live.remove(g_)
        for ci in range(NCG):
            B, cl = sets[ci], cls[ci]
            sl, chs, tks = views(rb, cl)
            Sc, Sn = ST[step % 2], ST[(step + 1) % 2]
            qe = (lambda q: B["Qe"].v[:, q, :]) if dplr else chs(RT)
            gt = (lambda q: B["GTt"].v[:, q, :]) if dplr else (lambda q: GTc.v[:, q, :])
            py = psY[cnt["y"] % 2]
            cnt["y"] += 1
            for q in range(8):
                k.mm(p3(py)[:, q, :], qe(q), Sc.v[:, q, :])
            p = npa()
            for q in range(8):
                k.mm(p3(p)[:, q, :], gt(q), Sc.v[:, q, :])
            k.tt("dve", f3(Sn), p3(p), f3(B["H"]), ALU.add)
            yt = Yt[step % 2]
            k.tt("dve", f3(yt), p3(py), f3(B["Yv"]), ALU.add)
            for d in range(2):
                tt0 = grp[d] * GT_ + cl[d] * CH
                k.stv("sp", S[pre + "Y"][d, tt0:tt0 + CH, :].rearrange("t (h i) -> t h i", i=64),
                      yt.v[:, d * 4:(d + 1) * 4, :])
            step += 1


GN_EPS = 64e-5


def postphase(k, st, cfg):
    S, Wt = cfg["S"], cfg["W"]
    TT = cfg["T"] + cfg["CTX"]
    row = lambda nm, ap: (lambda t: (k.ld("sp", t.v[:, :], ap.partition_broadcast(128)), t)[1])(k.sb(st, [128, 256], F32, nm))
    retg = row("retg", Wt["ret_g"])
    lng = row("lng", Wt["ln_g"])
    lnb = row("lnb", Wt["ln_b"])
    gc = k.sb(st, [128, 512], F32, "gc")
    gd = k.sb(st, [128, 256], F32, "gd")
    vd = k.sb(st, [128, 4, 64], BF16, "vd")
    vdf = k.sb(st, [128, 4, 64], F32, "vdf")
    bs = k.sb(st, [128, 4], F32, "bs")

    def bufset(nm):
        return dict(y=[k.sb(st, [128, 4, 64], F32, nm + "y%d" % i) for i in range(2)],
                    yc=k.sb(st, [128, 4, 64], F32, nm + "yc"), sq=k.sb(st, [128, 4, 64], F32, nm + "sq"),
                    m4=k.sb(st, [128, 4], F32, nm + "m4"), v4=k.sb(st, [128, 4], F32, nm + "v4"),
                    acc=k.sb(st, [128, 256], F32, nm + "acc"), yn=k.sb(st, [128, 256], F32, nm + "yn"), i=0)

    BC, BD = bufset("c"), bufset("d")

    def head_norm(B, yb, g, b, dst):
        yc, sq, m4, v4 = B["yc"], B["sq"], B["m4"], B["v4"]
        k.op("dve", lambda e: e.tensor_reduce(out=m4[:, :], in_=yb[:, :, :], axis=AX.X, op=ALU.add), r=[yb], w=[m4])
        k.ts("dve", m4.v[:, :], m4.v[:, :], 1.0 / 64, None, ALU.mult)
        yield
        k.tt("dve", yc.v[:, :, :], yb.v[:, :, :], m4.v[:, :].rr("p (h o) -> p h o", o=1).bc([128, 4, 64]), ALU.subtract)
        yield
        k.tt("dve", sq.v[:, :, :], yc.v[:, :, :], yc.v[:, :, :], ALU.mult)
        yield
        k.op("dve", lambda e: e.tensor_reduce(out=v4[:, :], in_=sq[:, :, :], axis=AX.X, op=ALU.add), r=[sq], w=[v4])
        yield
        k.ts("dve", v4.v[:, :], v4.v[:, :], 1.0 / 64, GN_EPS, ALU.mult, ALU.add)
        yield
        k.act(v4.v[:, :], v4.v[:, :], ACT.Sqrt)
        yield
        k.recip(v4.v[:, :], v4.v[:, :])
        yield
        k.tt("dve", yc.v[:, :, :], yc.v[:, :, :], v4.v[:, :].rr("p (h o) -> p h o", o=1).bc([128, 4, 64]), ALU.mult)
        yield
        k.tt("dve", dst, yc.v[:, :, :].rr("p h i -> p (h i)"), g.v[:, :], ALU.mult)
        if b is not None:
            k.tt("dve", dst, dst, b.v[:, :], ALU.add)
        yield

    def part_c(tsl):
        B = BC
        acc, yn = B["acc"], B["yn"]
        k.ld("sp", gc.v[:, :], S["C_G"][tsl, :])
        for d in range(2):
            yb = B["y"][B["i"] % 2]
            B["i"] += 1
            k.ld("sp", yb.v[:, :, :], S["C_Y"][d, tsl, :].rearrange("t (h i) -> t h i", i=64))
            yield from head_norm(B, yb, retg, None, yn.v[:, :])
            if d == 0:
                k.tt("dve", acc.v[:, :], yn.v[:, :], gc.v[:, 0:256], ALU.mult)
            else:
                k.tt("dve", yn.v[:, :], yn.v[:, :], gc.v[:, 256:512], ALU.mult)
                k.tt("dve", acc.v[:, :], acc.v[:, :], yn.v[:, :], ALU.add)
            yield
        k.stv("sp", S["O"][tsl, 512:768], acc.v[:, :])

    def part_d(tsl):
        B = BD
        acc, yn = B["acc"], B["yn"]
        k.ld("sp", gd.v[:, :], S["D_G"][tsl, :])
        for d in range(2):
            yb = B["y"][B["i"] % 2]
            B["i"] += 1
            k.ld("sp", yb.v[:, :, :], S["D_Y"][d, tsl, :].rearrange("t (h i) -> t h i", i=64))
            k.ld("sp", vd.v[:, :, :], S["D_V"][d, tsl, :].rearrange("t (h i) -> t h i", i=64))
            k.ld("sp", bs.v[:, :], S["D_BS"][d, tsl, :])
            yield from head_norm(B, yb, lng, lnb, yn.v[:, :])
            k.tt("dve", vdf.v[:, :, :], vd.v[:, :, :], bs.v[:, :].rr("p (h o) -> p h o", o=1).bc([128, 4, 64]), ALU.mult)
            k.tt("dve", yn.v[:, :], yn.v[:, :], vdf.v[:, :, :].rr("p h i -> p (h i)"), ALU.add)
            yield
            if d == 0:
                k.cp("act", acc.v[:, :], yn.v[:, :])
            else:
                k.tt("dve", acc.v[:, :], acc.v[:, :], yn.v[:, :], ALU.add)
            yield
        k.tt("dve", acc.v[:, :], acc.v[:, :], gd.v[:, :], ALU.mult)
        k.stv("sp", S["O"][tsl, 768:1024], acc.v[:, :])

    for t0 in range(0, TT, 128):
        tsl = slice(t0, t0 + 128)
        live = [part_c(tsl), part_d(tsl)]
        while live:
            for g_ in list(live):
                try:
                    next(g_)
                except StopIteration:
                    live.remove(g_)


def modphase(k, st, cfg):
    cv = k.sb(st, [128, 2, 8], F32, "cv")
    for r in range(2):
        k.ld("sp", cv.v[:, r, :], cfg["cvec"][r].rearrange("(c p) -> p c", p=128), allow_slow_non_contiguous=True)
    sc = k.sb(st, [128, 8, 2], F32, "sc")
    k.act(sc.v[:, :, :].rr("p c r -> p r c"), cv.v[:, :, :], ACT.Silu)
    wm = [k.sb(st, [128, 8, 512], F32, "wm%d" % i) for i in range(2)]
    bm = k.sb(st, [2, 9 * D], F32, "bm")
    k.ld("sp", bm.v[:, :], cfg["b_mod"].partition_broadcast(2))
    ob = k.sb(st, [2, 9 * D], F32, "ob")
    pm = [k.ps(st, [128, 512], F32, "pm%d" % i) for i in range(2)]
    wv = cfg["w_mod"].rearrange("(c p) f -> p c f", p=128)
    for j in range(18):
        w = wm[j % 2]
        for c in range(8):
            k.ld("sp", w.v[:, c, :], wv[:, c, j * 512:(j + 1) * 512])
        p = pm[j % 2]
        for c in range(8):
            k.mm(p.v[0:2, :], sc.v[:, c, :], w.v[:, c, :], c == 0, c == 7)
        k.tt("dve", ob.v[:, j * 512:(j + 1) * 512], p.v[0:2, :], bm.v[:, j * 512:(j + 1) * 512], ALU.add)
    k.stv("sp", cfg["mod"], ob.v[:, :])


def build_program(T, CTX, L, stop_after=None):
    TT = T + CTX
    nc = bass.Bass("TRN2", target_bir_lowering=False, dynamic_dma_scratch_size=8192)

    def din(name, shape, dt=F32):
        return nc.dram_tensor(name, list(shape), dt, kind="ExternalInput").ap()

    def scr(name, shape, dt=F32):
        return nc.dram_tensor(name, list(shape), dt, kind="Internal").ap()

    I = dict(
        xin=din("xin", [TT, D]), cvec=din("cvec", [2, D]),
        w_mod=din("w_mod", [L, D, 9 * D]), b_mod=din("b_mod", [L, 9 * D]), norm_g=din("norm_g", [L, 3, D]),
        ffn_w_in=din("ffn_w_in", [L, 2, D, 2 * DFF]), ffn_w_out=din("ffn_w_out", [L, 2, DFF, D]),
        w_in=din("w_in", [L, D, 3456]), w_out=din("w_out", [L, D, D]), final_g=din("final_g", [D]),
        w_sw=din("w_sw", [L, D, 1280]), w_d=din("w_d", [L, 2, D, 896]), pcols=din("pcols", [L, 128, 24]),
        mu=din("mu", [L, 2, 896]), w2=din("w2", [L, 2, 64, 256]), a2=din("a2", [L, 2, 64, 256]),
        g2=din("g2", [L, 128, 256]), ret_g=din("ret_g", [L, 256]), ln_g=din("ln_g", [L, 256]),
        ln_b=din("ln_b", [L, 256]), sink=din("sink", [L, 4]),
    )
    Cn = dict(
        ident=din("ident", [128, 128]), bd64=din("bd64", [128, 128]), e2=din("e2", [128, 2]),
        ropeA_c=din("ropeA_c", [128, T]), ropeA_s=din("ropeA_s", [128, T]),
        ropeS_c=din("ropeS_c", [128, T]), ropeS_s=din("ropeS_s", [128, T]),
        ret_tab=din("ret_tab", [128, 2, 2, 3, CH]), ret_pc=din("ret_pc", [64, 8]),
        m_st=din("m_st", [64, 512]), m_ts=din("m_ts", [64, 512]), m_in=din("m_in", [64, 512]),
        id8=din("id8", [64, 512]), mprev=din("mprev", [128, 128]), mnext=din("mnext", [128, 128]),
    )
    out = nc.dram_tensor("out", [T, D], F32, kind="ExternalOutput").ap()
    dbg = stop_after is not None
    mk = (lambda name, shape, dt=F32: nc.dram_tensor(name, list(shape), dt, kind="ExternalOutput").ap()) if dbg else scr
    S = dict(
        xres=mk("xres", [TT, D]), mod=mk("modr", [2, 9 * D]), hT=mk("hT", [D, TT], BF16),
        QA=mk("QA", [256, TT], BF16), KA=mk("KA", [128, TT], BF16), VA=mk("VA", [TT, 128], BF16),
        QB=mk("QB", [256, TT], BF16), KB=mk("KB", [128, TT], BF16), VB=mk("VB", [TT, 128], BF16),
        O=mk("O", [TT, 1024]),
        C_Rt=mk("C_Rt", [2, 256, TT], BF16), C_Kt=mk("C_Kt", [2, 256, TT], BF16), C_Kh=mk("C_Kh", [2, TT, 256], BF16),
        C_V=mk("C_V", [TT, 256], BF16), C_G=mk("C_G", [TT, 512]), C_Y=mk("C_Y", [2, TT, 256]),
        D_Rt=mk("D_Rt", [2, 256, TT], BF16), D_Kt=mk("D_Kt", [2, 256, TT], BF16), D_At=mk("D_At", [2, 256, TT], BF16),
        D_Bt=mk("D_Bt", [2, 256, TT], BF16), D_Kh=mk("D_Kh", [2, TT, 256], BF16), D_Bh=mk("D_Bh", [2, TT, 256], BF16),
        D_Atok=mk("D_Atok", [2, TT, 256], BF16), D_V=mk("D_V", [2, TT, 256], BF16), D_Y=mk("D_Y", [2, TT, 256]),
        D_PC=mk("D_PC", [2, 256, TT // CH]), D_G=mk("D_G", [TT, 256]), D_BS=mk("D_BS", [2, TT, 4]),
    )
    k = K(nc)
    stages = []

    def phase(name, fn, cfg):
        if stop_after is not None and stop_after in stages:
            return
        with contextlib.ExitStack() as st:
            fn(k, st, cfg)
            k.barrier()
        stages.append(name)
        k.marks.append((name, {e.name: e.n for e in k.E.values()}))

    with k.stack:
        segs_all = [(0, T, 0), (T, CTX, 1)]
        for l in range(L):
            last = l == L - 1
            W = dict(w_in=I["w_in"][l], w_sw=I["w_sw"][l], w_d=I["w_d"][l], mu=I["mu"][l], pcols=I["pcols"][l],
                     w2=I["w2"][l], a2=I["a2"][l], g2=I["g2"][l], ret_g=I["ret_g"][l], ln_g=I["ln_g"][l],
                     ln_b=I["ln_b"][l], sink=I["sink"][l])
            base = dict(S=S, C=Cn, W=W, T=T, CTX=CTX)
            phase("mod%d" % l, modphase, dict(cvec=I["cvec"], w_mod=I["w_mod"][l], b_mod=I["b_mod"][l], mod=S["mod"]))
            phase("f1_%d" % l, rowphase, dict(
                xin=I["xin"] if l == 0 else S["xres"], xout=S["xres"], segs=segs_all, mod=S["mod"], ident=Cn["ident"],
                ffn=dict(w_in=I["ffn_w_in"][l, 0], w_out=I["ffn_w_out"][l, 0], g=I["norm_g"][l, 0], shift=0, scale=1,
                         gate=2),
                post=dict(kind="hT", g=I["norm_g"][l, 1], shift=3, scale=4, hT=S["hT"])))
            phase("proj%d" % l, projphase, base)
            phase("attn%d" % l, attnphase, dict(base, need_ctx=not last))
            phase("scanC%d" % l, scanphase, dict(base, dplr=False))
            phase("scanD%d" % l, scanphase, dict(base, dplr=True))
            phase("post%d" % l, postphase, base)
            phase("f2_%d" % l, rowphase, dict(
                xin=S["xres"], xout=S["xres"], segs=segs_all if not last else [(0, T, 0)], mod=S["mod"],
                ident=Cn["ident"],
                pre=dict(o=S["O"], w_out=I["w_out"][l], gate=5),
                ffn=dict(w_in=I["ffn_w_in"][l, 1], w_out=I["ffn_w_out"][l, 1], g=I["norm_g"][l, 2], shift=6, scale=7,
                         gate=8),
                post=dict(kind="final", g=I["final_g"], out=out) if last else None))
    return nc, k


def _consts(T):
    c = {}
    c["ident"] = np.eye(128, dtype=np.float32)
    bd = np.zeros((128, 128), np.float32)
    bd[:64, :64] = 1
    bd[64:, 64:] = 1
    c["bd64"] = bd
    e2 = np.zeros((128, 2), np.float32)
    e2[:64, 0] = 1
    e2[64:, 1] = 1
    c["e2"] = e2
    t = np.arange(T)
    row = (t // 64).astype(np.float32)
    col = (t % 64).astype(np.float32)
    inv16 = (1.0 / (np.float32(10000.0) ** (np.arange(0, 32, 2, dtype=np.float32) / np.float32(32)))).astype(np.float32)
    inv32 = (1.0 / (np.float32(10000.0) ** (np.arange(0, 64, 2, dtype=np.float32) / np.float32(64)))).astype(np.float32)
    ca = np.zeros((64, T), np.float32)
    sa = np.zeros((64, T), np.float32)
    for blk, pos in ((0, row), (32, col)):
        ang = pos[None, :] * inv16[:, None]
        ca[blk:blk + 16] = np.cos(ang)
        ca[blk + 16:blk + 32] = np.cos(ang)
        sa[blk:blk + 16] = -np.sin(ang)
        sa[blk + 16:blk + 32] = np.sin(ang)
    ang = t.astype(np.float32)[None, :] * inv32[:, None]
    cs = np.concatenate([np.cos(ang), np.cos(ang)], 0).astype(np.float32)
    ss = np.concatenate([-np.sin(ang), np.sin(ang)], 0).astype(np.float32)
    c["ropeA_c"] = np.concatenate([ca, ca], 0)
    c["ropeA_s"] = np.concatenate([sa, sa], 0)
    c["ropeS_c"] = np.concatenate([cs, cs], 0)
    c["ropeS_s"] = np.concatenate([ss, ss], 0)
    lg = np.log1p(-np.exp2(-5.0 - np.arange(4, dtype=np.float64)))
    i = np.arange(CH, dtype=np.float64)
    rt = np.zeros((128, 2, 2, 3, CH), np.float64)
    for p in range(128):
        for g in range(2):
            l_ = lg[2 * g + p // 64]
            for d in range(2):
                csum = (i + 1) * l_ if d == 0 else (CH - i) * l_
                rt[p, g, d, 0] = np.exp(csum)
                rt[p, g, d, 1] = np.exp(-csum) / 8.0
                rt[p, g, d, 2] = np.exp(CH * l_ - csum) / 8.0
    c["ret_tab"] = rt.astype(np.float32)
    c["ret_pc"] = np.tile(np.exp(CH * lg)[None, :], (64, 2)).astype(np.float32)
    s_ = np.arange(64)[:, None]
    t_ = np.arange(64)[None, :]
    lt = (s_ < t_).astype(np.float32)
    gt = (s_ > t_).astype(np.float32)
    eye = np.eye(64, dtype=np.float32)
    c["m_st"] = np.concatenate([lt] * 4 + [gt] * 4, 1)
    c["m_ts"] = np.concatenate([gt] * 4 + [lt] * 4, 1)
    c["m_in"] = np.concatenate([lt + eye] * 4 + [gt + eye] * 4, 1)
    c["id8"] = np.concatenate([eye] * 8, 1)
    j = np.arange(128)[:, None]
    ii = np.arange(128)[None, :]
    c["mprev"] = (ii <= j).astype(np.float32)
    c["mnext"] = (j <= ii).astype(np.float32)
    return c


def _host_maps(inp, T, CTX, L):
    f = lambda a: np.ascontiguousarray(np.asarray(a, dtype=np.float32))
    w_in = f(inp["w_in"])
    pa = np.concatenate([np.arange(16, 32), np.arange(0, 16), np.arange(48, 64), np.arange(32, 48)])
    psq = np.concatenate([np.arange(32, 64), np.arange(0, 32)])
    cols = []
    for base, nh, perm in ((0, 4, pa), (256, 2, pa), (512, 4, pa), (768, 2, pa), (1024, 4, psq), (1280, 4, psq)):
        for h in range(nh):
            cols.append(base + h * 64 + perm)
    cols = np.concatenate(cols)
    w_sw = np.ascontiguousarray(w_in[:, :, cols])
    dcols = [np.concatenate([np.arange(2304, 3072), np.arange(3200, 3264), np.arange(3328, 3392)]),
             np.concatenate([np.arange(2304, 3072), np.arange(3264, 3328), np.arange(3392, 3456)])]
    w_d = np.ascontiguousarray(np.stack([w_in[:, :, dc] for dc in dcols], 1))
    qkg = f(inp["qk_norm_g"])
    pcols = np.zeros((L, 128, 24), np.float32)
    two = lambda v: np.ascontiguousarray(v.reshape(2, 128).T)
    for l in range(L):
        for qk in range(2):
            g = qkg[l, qk]
            pcols[l, :, 2 * qk] = np.tile(g, 2)
            pcols[l, :, 2 * qk + 1] = np.tile(g[pa], 2)
        for d in range(2):
            pcols[l, :, 4 + 2 * d:6 + 2 * d] = two(f(inp["rwkv_w0"])[l, d])
            pcols[l, :, 8 + 2 * d:10 + 2 * d] = two(f(inp["rwkv_a0"])[l, d])
            pcols[l, :, 16 + 2 * d:18 + 2 * d] = two(f(inp["rwkv_rho"])[l, d].reshape(256))
        pcols[l, :, 12:14] = two(f(inp["rwkv_k_k"])[l])
        pcols[l, :, 14:16] = two(f(inp["rwkv_k_a"])[l])
    shared = dict(
        w_mod=f(inp["w_mod"]), b_mod=f(inp["b_mod"]), norm_g=f(inp["norm_g"]), ffn_w_in=f(inp["ffn_w_in"]),
        ffn_w_out=f(inp["ffn_w_out"]), w_in=w_in, w_out=f(inp["w_out"]), final_g=f(inp["final_norm_g"]),
        w_sw=w_sw, w_d=w_d, pcols=pcols, mu=f(inp["rwkv_mu"]), w2=f(inp["rwkv_w2"]), a2=f(inp["rwkv_a2"]),
        g2=f(inp["rwkv_g2"]), ret_g=f(inp["ret_norm_g"]), ln_g=f(inp["rwkv_ln_g"]), ln_b=f(inp["rwkv_ln_b"]),
        sink=f(inp["attn_sink"]))
    shared.update(_consts(T))
    x, c, ctx, c_ctx = f(inp["x"]), f(inp["c"]), f(inp["ctx"]), f(inp["c_ctx"])
    maps = []
    for b in range(x.shape[0]):
        m = dict(shared)
        m["xin"] = np.ascontiguousarray(np.concatenate([x[b], ctx[b]], 0))
        m["cvec"] = np.ascontiguousarray(np.stack([c[b], c_ctx], 0))
        maps.append(m)
    return maps


_PROG = {}


def kernel(**inputs):
    x = np.asarray(inputs["x"])
    B, T, _ = x.shape
    CTX = np.asarray(inputs["ctx"]).shape[1]
    L = np.asarray(inputs["w_in"]).shape[0]
    key = (T, CTX, L)
    if key not in _PROG:
        _PROG[key] = build_program(T, CTX, L)[0]
    nc = _PROG[key]
    maps = _host_maps(inputs, T, CTX, L)
    res = run_bass_kernel_spmd(nc, maps, core_ids=list(range(B)))
    return np.stack([np.asarray(r["out"], dtype=np.float32) for r in res.results], 0)
```

```python
import contextlib
import numpy as np
import concourse.bass as bass
import concourse.mybir as mybir
from concourse.bass_utils import run_bass_kernel_spmd

ACT = mybir.ActivationFunctionType
ALU = mybir.AluOpType
AX = mybir.AxisListType
F32 = mybir.dt.float32
BF16 = mybir.dt.bfloat16

D = 1024
DFF = 2816
NMOD = 9
RMS_EPS = 1e-6
STRICT = True


class PSem:
    def __init__(self, h):
        self.h = h
        self.val = 0


class Eng:
    def __init__(self, name, h, sem):
        self.name, self.h, self.sem = name, h, sem
        self.n = 0
        self.seen = {}
        self.dseen = {}


class Buf:
    def __init__(self, k, t, name):
        self.k, self.t, self.name = k, t, name
        self.w = None
        self.r = {}
        self.psem = None
        self.dma_w = 0
        self.dma_rw = 0

    def __getitem__(self, key):
        return self.t[key]

    @property
    def v(self):
        return _VA(self)


class View:
    def __init__(self, buf, ap):
        self.buf, self.ap = buf, ap

    def rr(self, pat, **kw):
        return View(self.buf, self.ap.rearrange(pat, **kw))

    def bc(self, shape):
        return View(self.buf, self.ap.to_broadcast(list(shape)))

    def __getitem__(self, key):
        return View(self.buf, self.ap[key])


class _VA:
    def __init__(self, b):
        self.b = b

    def __getitem__(self, key):
        return View(self.b, self.b.t[key])


def _bufs(*xs):
    out = []
    for x in xs:
        if isinstance(x, View) and x.buf not in out:
            out.append(x.buf)
    return out


def _ap(x):
    return x.ap if isinstance(x, View) else x


class K:
    def __init__(self, nc):
        self.nc = nc
        self.stack = contextlib.ExitStack()
        self.E = {}
        for name, h in (("pe", nc.tensor), ("act", nc.scalar), ("dve", nc.vector),
                        ("pool", nc.gpsimd), ("sp", nc.sync)):
            sem = self.stack.enter_context(nc.semaphore("s_" + name))
            self.E[name] = Eng(name, h, sem)
        self.free_psems = []
        self.n_psems = 0
        self.all_psems = []
        self.bufs = []
        self.uid = 0
        self.ninstr = 0
        self.marks = []

    def _psem(self):
        if self.free_psems:
            return self.free_psems.pop()
        self.n_psems += 1
        h = self.stack.enter_context(self.nc.semaphore("d%d" % self.n_psems))
        p = PSem(h)
        self.all_psems.append(p)
        return p

    def sb(self, st, shape, dtype, name):
        self.uid += 1
        t = st.enter_context(self.nc.sbuf_tensor("%s_%d" % (name, self.uid), list(shape), dtype))
        b = Buf(self, t, name)
        st.callback(self._release, b)
        return b

    def ps(self, st, shape, dtype, name):
        self.uid += 1
        t = st.enter_context(self.nc.psum_tensor("%s_%d" % (name, self.uid), list(shape), dtype))
        return Buf(self, t, name)

    def _release(self, b):
        if b.psem is not None and not getattr(b.psem, "sw", False):
            self.free_psems.append(b.psem)
        b.psem = None

    def _wait_eng(self, E, X, n):
        if E.seen.get(X.name, 0) >= n:
            return
        E.h.wait_ge(X.sem, n)
        E.seen[X.name] = n
        self.ninstr += 1

    def _wait_d(self, E, p, v):
        if v <= 0 or E.dseen.get(id(p), 0) >= v:
            return
        E.h.wait_ge(p.h, v)
        E.dseen[id(p)] = v
        self.ninstr += 1

    def _deps(self, E, r, w):
        for b in r:
            if b.w is not None:
                self._wait_eng(E, b.w[0], b.w[1])
            if b.psem is not None:
                self._wait_d(E, b.psem, b.dma_w)
        for b in w:
            if b.w is not None and (STRICT or b.w[0] is not E) and not (E.name == "pe" and b.w[0] is E):
                self._wait_eng(E, b.w[0], b.w[1])
            for X, n in b.r.items():
                if STRICT or X is not E:
                    self._wait_eng(E, X, n)
            if b.psem is not None:
                self._wait_d(E, b.psem, b.dma_rw)

    def op(self, eng, fn, r=(), w=()):
        E = self.E[eng]
        self._deps(E, r, w)
        ins = fn(E.h)
        E.n += 1
        ins.then_inc(E.sem, 1)
        self.ninstr += 1
        for b in w:
            b.w = (E, E.n)
            b.r = {}
        for b in r:
            if b.w is None or b.w[0] is not E or b.w[1] != E.n:
                b.r[E] = E.n
        return ins

    def dma(self, q, out, in_, r=(), w=(), **kw):
        Q = self.E[q]
        self._deps(Q, r, w)
        anchor = w[0] if w else r[0]
        if anchor.psem is None:
            if q == "pool":
                self.n_psems += 1
                anchor.psem = PSem(self.stack.enter_context(self.nc.semaphore("w%d" % self.n_psems)))
                anchor.psem.sw = True
                self.all_psems.append(anchor.psem)
            else:
                anchor.psem = self._psem()
        p = anchor.psem
        assert getattr(p, "sw", False) == (q == "pool"), "buffer mixes SW and HW DGE DMAs: " + anchor.name
        Q.h.dma_start(out=out, in_=in_, **kw).then_inc(p.h, 16)
        p.val += 16
        self.ninstr += 1
        anchor.dma_rw = p.val
        if w:
            anchor.dma_w = p.val
            anchor.w = None
            anchor.r = {}

    def mm(self, out, lhsT, rhs, start=True, stop=True):
        return self.op("pe", lambda e: e.matmul(out.ap, lhsT.ap, rhs.ap, start=start, stop=stop),
                       r=_bufs(lhsT, rhs), w=[out.buf])

    def tr(self, out, in_, ident):
        return self.op("pe", lambda e: e.transpose(out.ap, in_.ap, ident.ap), r=_bufs(in_, ident), w=[out.buf])

    def tt(self, eng, out, in0, in1, op):
        return self.op(eng, lambda e: e.tensor_tensor(out=out.ap, in0=in0.ap, in1=in1.ap, op=op),
                       r=_bufs(in0, in1), w=[out.buf])

    def ts(self, eng, out, in0, s1, s2, op0, op1=None):
        kw = {} if op1 is None else dict(op1=op1)
        return self.op(eng, lambda e: e.tensor_scalar(out=out.ap, in0=in0.ap, scalar1=_ap(s1), scalar2=_ap(s2),
                                                      op0=op0, **kw), r=_bufs(in0, s1, s2), w=[out.buf])

    def stt(self, eng, out, in0, scalar, in1, op0, op1):
        return self.op(eng, lambda e: e.scalar_tensor_tensor(out=out.ap, in0=in0.ap, scalar=_ap(scalar), in1=in1.ap,
                                                             op0=op0, op1=op1), r=_bufs(in0, scalar, in1), w=[out.buf])

    def act(self, out, in_, func, bias=None, scale=None, accum=None):
        kw = {}
        if bias is not None:
            kw["bias"] = _ap(bias)
        if scale is not None:
            kw["scale"] = _ap(scale)
        w = [out.buf]
        if accum is not None:
            kw["accum_out"] = accum.ap
            w.append(accum.buf)
        return self.op("act", lambda e: e.activation(out=out.ap, in_=in_.ap, func=func, **kw),
                       r=_bufs(in_, bias, scale), w=w)

    def cp(self, eng, out, in_):
        if eng == "act":
            return self.op("act", lambda e: e.copy(out=out.ap, in_=in_.ap), r=[in_.buf], w=[out.buf])
        return self.op(eng, lambda e: e.tensor_copy(out=out.ap, in_=in_.ap), r=[in_.buf], w=[out.buf])

    def memset(self, eng, out, val):
        return self.op(eng, lambda e: e.memset(out.ap, val), w=[out.buf])

    def recip(self, out, in_):
        return self.op("dve", lambda e: e.reciprocal(out=out.ap, in_=in_.ap), r=[in_.buf], w=[out.buf])

    def ld(self, q, out, dram, **kw):
        return self.dma(q, out.ap, dram, w=[out.buf], **kw)

    def stv(self, q, dram, in_, **kw):
        return self.dma(q, dram, in_.ap, r=[in_.buf], **kw)

    def barrier(self):
        for E in self.E.values():
            for X in self.E.values():
                if X is not E and X.n > 0:
                    self._wait_eng(E, X, X.n)
            for p in self.all_psems:
                self._wait_d(E, p, p.val)


def mm(k, out_ap, lhsT_ap, rhs_ap, start, stop, r, w):
    return k.op("pe", lambda e: e.matmul(out_ap, lhsT_ap, rhs_ap, start=start, stop=stop), r=r, w=w)


def rowphase(k, st, cfg):
    nc = k.nc
    ffn = cfg["ffn"]
    pre = cfg.get("pre")
    post = cfg.get("post")
    w1 = k.sb(st, [128, 8, 2 * DFF], BF16, "w1")
    w2 = k.sb(st, [128, 22, D], BF16, "w2")
    w1v = ffn["w_in"].rearrange("(c p) f -> p c f", p=128)
    for c in range(8):
        for hlf in range(4):
            f0 = hlf * 1408
            k.dma("pool", w1[:, c, f0:f0 + 1408], w1v[:, c, f0:f0 + 1408], w=[w1])
    w2v = ffn["w_out"].rearrange("(c p) d -> p c d", p=128)
    for c in range(22):
        k.dma("pool", w2[:, c, :], w2v[:, c, :], w=[w2])
    if pre is not None:
        wo = k.sb(st, [128, 8, D], BF16, "wo")
        wov = pre["w_out"].rearrange("(c p) d -> p c d", p=128)
        for c in range(8):
            k.dma("pool", wo[:, c, :], wov[:, c, :], w=[wo])

    identf = k.sb(st, [128, 128], F32, "identf")
    k.dma("sp", identf[:, :], cfg["ident"], w=[identf])
    xt = [k.sb(st, [128, D], F32, "xt%d" % i) for i in range(4)]
    hT = k.sb(st, [128, 8, 512], BF16, "hT")
    gT = k.sb(st, [128, 22, 512], BF16, "gT")
    sg = k.sb(st, [128, 512], BF16, "sg")
    tmp = k.sb(st, [128, D], F32, "tmp")
    ssq = k.sb(st, [128, 4], F32, "ssq")
    rinv = k.sb(st, [128, 4], F32, "rinv")
    ps_u = [k.ps(st, [128, 512], F32, "psu%d" % i) for i in range(4)]
    ps_y = [k.ps(st, [128, 512], F32, "psy%d" % i) for i in range(2)]
    ps_t = k.ps(st, [128, D], F32, "pst")
    Ac = k.sb(st, [128, 8], F32, "Ac")
    Bc = k.sb(st, [128, 8], F32, "Bc")
    Gc = k.sb(st, [128, 8], F32, "Gc")
    A2c = k.sb(st, [128, 8], F32, "A2c")
    B2c = k.sb(st, [128, 8], F32, "B2c")
    G = k.sb(st, [128, D], F32, "G")
    PG = k.sb(st, [128, D], F32, "PG") if pre is not None else None
    gvec2 = None
    if post is not None and post["kind"] == "final":
        gvec2 = k.sb(st, [128, D], F32, "gvec2")
        k.dma("sp", gvec2[:, :], post["g"].partition_broadcast(128), w=[gvec2])

    def col(ap1d):
        return ap1d.rearrange("(c p) -> p c", p=128)

    def load_cols(dst, ap1d):
        k.dma("sp", dst[:, :], col(ap1d), w=[dst], allow_slow_non_contiguous=True)

    def setup_mod(mr):
        load_cols(Gc, ffn["g"])
        load_cols(Ac, cfg["mod"][mr, ffn["scale"] * D:(ffn["scale"] + 1) * D])
        load_cols(Bc, cfg["mod"][mr, ffn["shift"] * D:(ffn["shift"] + 1) * D])
        k.op("dve", lambda e: e.scalar_tensor_tensor(out=Ac[:, :], in0=Ac[:, :], scalar=1.0, in1=Gc[:, :],
                                                     op0=ALU.add, op1=ALU.mult), r=[Ac, Gc], w=[Ac])
        if post is not None and post["kind"] == "hT":
            load_cols(Gc, post["g"])
            load_cols(A2c, cfg["mod"][mr, post["scale"] * D:(post["scale"] + 1) * D])
            load_cols(B2c, cfg["mod"][mr, post["shift"] * D:(post["shift"] + 1) * D])
            k.op("dve", lambda e: e.scalar_tensor_tensor(out=A2c[:, :], in0=A2c[:, :], scalar=1.0, in1=Gc[:, :],
                                                         op0=ALU.add, op1=ALU.mult), r=[A2c, Gc], w=[A2c])
        k.dma("sp", G[:, :], cfg["mod"][mr, ffn["gate"] * D:(ffn["gate"] + 1) * D].partition_broadcast(128), w=[G])
        k.op("dve", lambda e: e.tensor_scalar(out=G[:, :], in0=G[:, :], scalar1=0.5, scalar2=None, op0=ALU.mult),
             r=[G], w=[G])
        if pre is not None:
            k.dma("sp", PG[:, :], cfg["mod"][mr, pre["gate"] * D:(pre["gate"] + 1) * D].partition_broadcast(128),
                  w=[PG])

    try:
        tmp2 = [tmp, k.sb(st, [128, D], F32, "tmpb")]
    except AssertionError:
        tmp2 = [tmp, tmp]

    def rms_all(xs):
        m_ = len(xs)
        junk = gT[:, 0:2, :]
        for s_, xb in enumerate(xs):
            k.op("act", lambda e: e.activation(out=junk, in_=xb[:, :].rearrange("p (a b) -> p a b", a=2),
                                               func=ACT.Square, accum_out=ssq[:, s_:s_ + 1]), r=[xb], w=[gT, ssq])
        k.op("dve", lambda e: e.tensor_scalar(out=rinv[:, 0:m_], in0=ssq[:, 0:m_], scalar1=1.0 / D, scalar2=RMS_EPS,
                                              op0=ALU.mult, op1=ALU.add), r=[ssq], w=[rinv])
        k.op("act", lambda e: e.activation(out=rinv[:, 0:m_], in_=rinv[:, 0:m_], func=ACT.Sqrt), r=[rinv], w=[rinv])
        k.op("dve", lambda e: e.reciprocal(out=rinv[:, 0:m_], in_=rinv[:, 0:m_]), r=[rinv], w=[rinv])

    def norm_mod_T_all(xs, A, B, dstT):
        rms_all(xs)
        for s, xb in enumerate(xs):
            tm = tmp2[s % 2]
            k.op("dve", lambda e: e.tensor_scalar(out=tm[:, :], in0=xb[:, :], scalar1=rinv[:, s:s + 1], scalar2=None,
                                                  op0=ALU.mult), r=[xb, rinv], w=[tm])
            for c in range(8):
                k.op("pe", lambda e: e.transpose(ps_t[:, c * 128:(c + 1) * 128], tm[:, c * 128:(c + 1) * 128],
                                                 identf[:, :]), r=[tm, identf], w=[ps_t])
            for c in range(8):
                if c % 2 == 0:
                    k.op("dve", lambda e: e.tensor_scalar(out=dstT[:, c, s * 128:(s + 1) * 128],
                                                          in0=ps_t[:, c * 128:(c + 1) * 128], scalar1=A[:, c:c + 1],
                                                          scalar2=B[:, c:c + 1], op0=ALU.mult, op1=ALU.add),
                         r=[ps_t, A, B], w=[dstT])
                else:
                    k.op("act", lambda e: e.activation(out=dstT[:, c, s * 128:(s + 1) * 128],
                                                       in_=ps_t[:, c * 128:(c + 1) * 128], func=ACT.Identity,
                                                       scale=A[:, c:c + 1], bias=B[:, c:c + 1]),
                         r=[ps_t, A, B], w=[dstT])

    ui = 0
    yi = 0
    for (tok0, ntok, mr) in cfg["segs"]:
        setup_mod(mr)
        for t0 in range(tok0, tok0 + ntok, 512):
            n = min(512, tok0 + ntok - t0)
            ns = n // 128
            for s in range(ns):
                k.dma("sp", xt[s][:, :], cfg["xin"][t0 + s * 128:t0 + (s + 1) * 128, :], w=[xt[s]])
            if pre is not None:
                for s in range(ns):
                    k.dma("sp", tmp[:, :], pre["o"][t0 + s * 128:t0 + (s + 1) * 128, :], w=[tmp])
                    for c in range(8):
                        k.op("pe", lambda e: e.transpose(ps_t[:, c * 128:(c + 1) * 128], tmp[:, c * 128:(c + 1) * 128],
                                                         identf[:, :]), r=[tmp, identf], w=[ps_t])
                    k.op("act", lambda e: e.copy(out=hT[:, :, s * 128:(s + 1) * 128],
                                                 in_=ps_t[:, :].rearrange("p (c t) -> p c t", c=8)), r=[ps_t], w=[hT])
                for s in range(ns):
                    for hf in range(2):
                        py = ps_y[yi % 2]
                        yi += 1
                        for c in range(8):
                            mm(k, py[:, :], hT[:, c, s * 128:(s + 1) * 128], wo[:, c, hf * 512:(hf + 1) * 512],
                               c == 0, c == 7, r=[hT, wo], w=[py])
                        sl = slice(hf * 512, (hf + 1) * 512)
                        k.op("dve", lambda e, py=py, sl=sl: e.tensor_tensor(out=tmp[:, sl], in0=py[:, :],
                                                                             in1=PG[:, sl], op=ALU.mult),
                             r=[py, PG], w=[tmp])
                        k.op("dve", lambda e, s=s, sl=sl: e.tensor_tensor(out=xt[s][:, sl], in0=xt[s][:, sl],
                                                                            in1=tmp[:, sl], op=ALU.add),
                             r=[xt[s], tmp], w=[xt[s]])
            norm_mod_T_all(xt[0:ns], Ac, Bc, hT)
            for fc in range(22 if cfg.get("stage", 9) >= 2 else 0):
                p1 = ps_u[ui % 4]
                p2 = ps_u[(ui + 1) % 4]
                ui += 2
                for c in range(8):
                    mm(k, p1[:, 0:n], w1[:, c, fc * 128:(fc + 1) * 128], hT[:, c, 0:n], c == 0, c == 7,
                       r=[w1, hT], w=[p1])
                for c in range(8):
                    mm(k, p2[:, 0:n], w1[:, c, DFF + fc * 128:DFF + (fc + 1) * 128], hT[:, c, 0:n], c == 0, c == 7,
                       r=[w1, hT], w=[p2])
                k.op("act", lambda e, p1=p1: e.activation(out=sg[:, 0:n], in_=p1[:, 0:n], func=ACT.Silu),
                     r=[p1], w=[sg])
                k.op("dve", lambda e, p2=p2, fc=fc: e.tensor_tensor(out=gT[:, fc, 0:n], in0=sg[:, 0:n],
                                                                     in1=p2[:, 0:n], op=ALU.mult),
                     r=[sg, p2], w=[gT])
            for s in range(ns if cfg.get("stage", 9) >= 3 else 0):
                for hf in range(2):
                    py = ps_y[yi % 2]
                    yi += 1
                    for fc in range(22):
                        mm(k, py[:, :], gT[:, fc, s * 128:(s + 1) * 128], w2[:, fc, hf * 512:(hf + 1) * 512],
                           fc == 0, fc == 21, r=[gT, w2], w=[py])
                    sl = slice(hf * 512, (hf + 1) * 512)
                    k.op("dve", lambda e, py=py, sl=sl: e.tensor_tensor(out=tmp[:, sl], in0=py[:, :], in1=G[:, sl],
                                                                         op=ALU.mult), r=[py, G], w=[tmp])
                    k.op("dve", lambda e, s=s, sl=sl: e.tensor_tensor(out=xt[s][:, sl], in0=xt[s][:, sl],
                                                                        in1=tmp[:, sl], op=ALU.add),
                         r=[xt[s], tmp], w=[xt[s]])
            if post is not None and post["kind"] == "final":
                if mr == 0:
                    rms_all(xt[0:ns])
                    for s in range(ns):
                        r0 = t0 + s * 128
                        tm = tmp2[s % 2]
                        k.op("dve", lambda e: e.scalar_tensor_tensor(out=tm[:, :], in0=xt[s][:, :],
                                                                     scalar=rinv[:, s:s + 1], in1=gvec2[:, :],
                                                                     op0=ALU.mult, op1=ALU.mult),
                             r=[xt[s], rinv, gvec2], w=[tm])
                        k.dma("sp", post["out"][r0:r0 + 128, :], tm[:, :], r=[tm])
                continue
            for s in range(ns):
                r0 = t0 + s * 128
                k.dma("sp", cfg["xout"][r0:r0 + 128, :], xt[s][:, :], r=[xt[s]])
            if post is not None and post["kind"] == "hT":
                norm_mod_T_all(xt[0:ns], A2c, B2c, hT)
            if post is not None and post["kind"] == "hT":
                for c in range(8):
                    k.dma("sp", post["hT"][c * 128:(c + 1) * 128, t0:t0 + n], hT[:, c, 0:n], r=[hT])


CH = 64
DECAY_SCALE = 0.6065306597126334


def projphase(k, st, cfg):
    S = cfg["S"]
    C = cfg["C"]
    Wt = cfg["W"]
    T, CTX = cfg["T"], cfg["CTX"]
    hTd = S["hT"]
    wabc = k.sb(st, [128, 8, 2304], BF16, "wabc")
    wgd = k.sb(st, [128, 8, 128], BF16, "wgd")
    wsw = k.sb(st, [128, 8, 1280], BF16, "wsw")
    wv = Wt["w_in"].rearrange("(c p) f -> p c f", p=128)
    for c in range(8):
        k.ld("pool", wabc.v[:, c, 0:1152], wv[:, c, 0:1152])
        k.ld("pool", wabc.v[:, c, 1152:2304], wv[:, c, 1152:2304])
        k.ld("pool", wgd.v[:, c, :], wv[:, c, 3072:3200])
        k.ld("pool", wsw.v[:, c, :], Wt["w_sw"].rearrange("(c p) f -> p c f", p=128)[:, c, :])
    wda = [k.sb(st, [128, 8, 896], BF16, "wda%d" % d) for d in range(2)]
    wdb = [k.sb(st, [128, 8, 896], BF16, "wdb%d" % d) for d in range(2)]
    with contextlib.ExitStack() as st2:
        wst = k.sb(st2, [128, 8, 896], F32, "wst")
        mub = k.sb(st2, [128, 896], F32, "mub")
        wtm = k.sb(st2, [128, 8, 896], F32, "wtm")
        for d in range(2):
            for c in range(8):
                k.ld("sp", wst.v[:, c, :], Wt["w_d"][d].rearrange("(c p) f -> p c f", p=128)[:, c, :])
            k.ld("sp", mub.v[:, :], Wt["mu"][d].partition_broadcast(128))
            for c in range(8):
                k.tt("dve", wtm.v[:, c, :], wst.v[:, c, :], mub.v[:, :], ALU.mult)
            k.cp("act", wda[d].v[:, :, :], wtm.v[:, :, :])
            k.tt("dve", wdb[d].v[:, :, :], wst.v[:, :, :], wtm.v[:, :, :], ALU.subtract)
        k.barrier()
    pc = k.sb(st, [128, 24], F32, "pcols")
    k.ld("sp", pc.v[:, :], Wt["pcols"])
    oma = k.sb(st, [128, 2], F32, "oma")
    k.ts("dve", oma.v[:, :], pc.v[:, 14:16], -1.0, 1.0, ALU.mult, ALU.add)
    identf = k.sb(st, [128, 128], F32, "identf")
    k.ld("sp", identf.v[:, :], C["ident"])
    bd = k.sb(st, [128, 128], F32, "bd")
    k.ld("sp", bd.v[:, :], C["bd64"])
    e2 = k.sb(st, [128, 2], F32, "e2")
    k.ld("sp", e2.v[:, :], C["e2"])
    rtab = k.sb(st, [128, 2, 2, 3, CH], F32, "rtab")
    k.ld("sp", rtab.v[:, :, :, :, :], C["ret_tab"])
    w2s = k.sb(st, [64, 2, 256], BF16, "w2s")
    a2s = k.sb(st, [64, 2, 256], BF16, "a2s")
    g2s = k.sb(st, [128, 256], BF16, "g2s")
    for d in range(2):
        k.ld("pool", w2s.v[:, d, :], Wt["w2"][d])
        k.ld("pool", a2s.v[:, d, :], Wt["a2"][d])
    k.ld("pool", g2s.v[:, :], Wt["g2"])

    NT = 512
    hTt = k.sb(st, [128, 8, NT + 2], BF16, "hTt")
    ca = k.sb(st, [128, NT], F32, "ca")
    sa = k.sb(st, [128, NT], F32, "sa")
    cs_ = k.sb(st, [128, NT], F32, "cs")
    ss_ = k.sb(st, [128, NT], F32, "ss")
    gca = k.sb(st, [128, 2, NT], F32, "gca")
    gsa = k.sb(st, [128, 2, NT], F32, "gsa")
    F = [k.sb(st, [128, NT], F32, "f%d" % i) for i in range(12)]
    ob = k.sb(st, [128, NT], BF16, "ob")
    sbf = [k.sb(st, [128, NT], BF16, "sbf%d" % i) for i in range(2)]
    sbi = [0]

    def st_bf(dst, srcv, n):
        b = sbf[sbi[0] % 2]
        sbi[0] += 1
        k.cp("act", b.v[:, 0:n], srcv)
        k.stv("sp", dst, b.v[:, 0:n])
    tb = k.sb(st, [128, 512], F32, "tb")
    tbb = k.sb(st, [128, 512], BF16, "tbb")
    thb = k.sb(st, [64, NT], BF16, "thb")
    alb = k.sb(st, [64, NT], BF16, "alb")
    sgb = k.sb(st, [128, NT], BF16, "sgb")
    bs4 = k.sb(st, [128, 4, 4], F32, "bs4")
    pcx = k.sb(st, [128, NT // CH], F32, "pcx")
    pcx2 = k.sb(st, [128, NT // CH], F32, "pcx2")
    F2 = [k.sb(st, [128, NT], F32, "g%d" % i) for i in range(9)]
    PS = [k.ps(st, [128, 512], F32, "pp%d" % i) for i in range(8)]
    pi = [0]

    def nps():
        p = PS[pi[0] % 8]
        pi[0] += 1
        return p

    def chain(p, wlist, M, n, col0=1):
        tot = len(wlist) * 8
        i = 0
        for (wb, c0, sh) in wlist:
            for c in range(8):
                k.mm(p.v[0:M, 0:n], wb.v[:, c, c0:c0 + M], hTt.v[:, c, col0 + sh:col0 + sh + n], i == 0, i == tot - 1)
                i += 1

    def chain_tok(p, wlist, ncols, s):
        tot = len(wlist) * 8
        i = 0
        for (wb, c0, sh) in wlist:
            for c in range(8):
                k.mm(p.v[:, 0:ncols], hTt.v[:, c, 1 + sh + s * 128:1 + sh + (s + 1) * 128], wb.v[:, c, c0:c0 + ncols],
                     i == 0, i == tot - 1)
                i += 1

    def rope_out(P, Psw, ctab, stab, dst, n):
        k.tt("dve", F[10].v[:, 0:n], P.v[:, 0:n], ctab, ALU.mult)
        k.tt("dve", F[11].v[:, 0:n], Psw.v[:, 0:n], stab, ALU.mult)
        k.tt("dve", dst, F[10].v[:, 0:n], F[11].v[:, 0:n], ALU.add)

    segs = [(0, T, True), (T, CTX, False)]
    for (seg0, seglen, latent) in segs:
        for t0 in range(seg0, seg0 + seglen, NT):
            n = min(NT, seg0 + seglen - t0)
            ns = n // 128
            nch = n // CH
            lo = t0 - 1 if t0 > seg0 else t0
            hi = t0 + n + 1 if t0 + n < seg0 + seglen else t0 + n
            if lo == t0:
                k.memset("dve", hTt.v[:, :, 0:1], 0.0)
            if hi == t0 + n:
                k.memset("dve", hTt.v[:, :, n + 1:n + 2], 0.0)
            for c in range(8):
                k.ld("sp", hTt.v[:, c, 1 - (t0 - lo):1 + n + (hi - t0 - n)], hTd[c * 128:(c + 1) * 128, lo:hi])
            if latent:
                k.ld("sp", ca.v[:, 0:n], C["ropeA_c"][:, t0:t0 + n])
                k.ld("sp", sa.v[:, 0:n], C["ropeA_s"][:, t0:t0 + n])
                k.ld("sp", cs_.v[:, 0:n], C["ropeS_c"][:, t0:t0 + n])
                k.ld("sp", ss_.v[:, 0:n], C["ropeS_s"][:, t0:t0 + n])
                for qk in range(2):
                    k.ts("dve", gca.v[:, qk, 0:n], ca.v[:, 0:n], pc.v[:, 2 * qk:2 * qk + 1], None, ALU.mult)
                    k.ts("dve", gsa.v[:, qk, 0:n], sa.v[:, 0:n], pc.v[:, 2 * qk + 1:2 * qk + 2], None, ALU.mult)
            for grp, base, swb, dq, dk_, dv_ in (("A", 0, 0, S["QA"], S["KA"], S["VA"]),
                                                ("B", 512, 384, S["QB"], S["KB"], S["VB"])):
                for j in range(3):
                    P, Psw = nps(), nps()
                    chain(P, [(wabc, base + j * 128, 0)], 128, n)
                    if latent:
                        chain(Psw, [(wsw, swb + j * 128, 0)], 128, n)
                    qk = 0 if j < 2 else 1
                    if grp == "B":
                        k.act(F[0].v[:, 0:n], P.v[:, 0:n], ACT.Square)
                        pss = nps()
                        k.mm(pss.v[:, 0:n], bd.v[:, :], F[0].v[:, 0:n])
                        k.ts("dve", F[1].v[:, 0:n], pss.v[:, 0:n], 1.0 / 64, RMS_EPS, ALU.mult, ALU.add)
                        k.act(F[1].v[:, 0:n], F[1].v[:, 0:n], ACT.Sqrt)
                        k.recip(F[1].v[:, 0:n], F[1].v[:, 0:n])
                        k.tt("dve", F[2].v[:, 0:n], P.v[:, 0:n], F[1].v[:, 0:n], ALU.mult)
                        if latent:
                            k.tt("dve", F[3].v[:, 0:n], Psw.v[:, 0:n], F[1].v[:, 0:n], ALU.mult)
                            rope_out(F[2], F[3], gca.v[:, qk, 0:n], gsa.v[:, qk, 0:n], ob.v[:, 0:n], n)
                        else:
                            k.ts("dve", ob.v[:, 0:n], F[2].v[:, 0:n], pc.v[:, 2 * qk:2 * qk + 1], None, ALU.mult)
                    else:
                        if latent:
                            rope_out(P, Psw, ca.v[:, 0:n], sa.v[:, 0:n], ob.v[:, 0:n], n)
                        else:
                            k.cp("act", ob.v[:, 0:n], P.v[:, 0:n])
                    dst = dq[j * 128:(j + 1) * 128, t0:t0 + n] if j < 2 else dk_[:, t0:t0 + n]
                    k.stv("sp", dst, ob.v[:, 0:n])
                for s in range(ns):
                    P = nps()
                    chain_tok(P, [(wabc, base + 384, 0)], 128, s)
                    k.cp("act", tbb.v[:, 0:128], P.v[:, 0:128])
                    k.stv("sp", dv_[t0 + s * 128:t0 + (s + 1) * 128, :], tbb.v[:, 0:128])
            for j in range(4):
                P, Psw = nps(), nps()
                chain(P, [(wabc, 1024 + j * 128, 0)], 128, n)
                g = j % 2
                if latent:
                    chain(Psw, [(wsw, 768 + j * 128, 0)], 128, n)
                    rope_out(P, Psw, cs_.v[:, 0:n], ss_.v[:, 0:n], F[0].v[:, 0:n], n)
                else:
                    k.cp("act", F[0].v[:, 0:n], P.v[:, 0:n])
                x3 = F[0].v[:, 0:n].rr("p (c i) -> p c i", i=CH)
                for d in range(2):
                    if j < 2:
                        k.tt("dve", F[1].v[:, 0:n].rr("p (c i) -> p c i", i=CH), x3,
                             rtab.v[:, g, d, 0:1, :].bc([128, nch, CH]), ALU.mult)
                        st_bf(S["C_Rt"][d, g * 128:(g + 1) * 128, t0:t0 + n], F[1].v[:, 0:n], n)
                    else:
                        k.tt("dve", F[1].v[:, 0:n].rr("p (c i) -> p c i", i=CH), x3,
                             rtab.v[:, g, d, 1:2, :].bc([128, nch, CH]), ALU.mult)
                        st_bf(S["C_Kt"][d, g * 128:(g + 1) * 128, t0:t0 + n], F[1].v[:, 0:n], n)
                        k.tt("dve", F[2].v[:, 0:n].rr("p (c i) -> p c i", i=CH), x3,
                             rtab.v[:, g, d, 2:3, :].bc([128, nch, CH]), ALU.mult)
                        for s in range(ns):
                            pt = nps()
                            k.tr(pt.v[:, 0:128], F[2].v[:, s * 128:(s + 1) * 128], identf.v[:, :])
                            k.cp("act", tbb.v[:, 0:128], pt.v[:, 0:128])
                            k.stv("sp", S["C_Kh"][d, t0 + s * 128:t0 + (s + 1) * 128, g * 128:(g + 1) * 128],
                                  tbb.v[:, 0:128])
            for s in range(ns):
                P = nps()
                chain_tok(P, [(wabc, 1536, 0)], 256, s)
                k.cp("act", tbb.v[:, 0:256], P.v[:, 0:256])
                k.stv("sp", S["C_V"][t0 + s * 128:t0 + (s + 1) * 128, :], tbb.v[:, 0:256])
                P = nps()
                chain_tok(P, [(wabc, 1792, 0)], 512, s)
                k.act(tb.v[:, 0:512], P.v[:, 0:512], ACT.Silu)
                k.stv("sp", S["C_G"][t0 + s * 128:t0 + (s + 1) * 128, :], tb.v[:, 0:512])
            P = nps()
            chain(P, [(wgd, 0, 0)], 128, n)
            k.act(sgb.v[:, 0:n], P.v[:, 0:n], ACT.Sigmoid)
            for s in range(ns):
                P = nps()
                k.mm(P.v[:, 0:256], sgb.v[:, s * 128:(s + 1) * 128], g2s.v[:, :])
                k.cp("act", tb.v[:, 0:256], P.v[:, 0:256])
                k.stv("sp", S["D_G"][t0 + s * 128:t0 + (s + 1) * 128, :], tb.v[:, 0:256])
            for d in range(2):
                sh = -1 if d == 0 else 1
                wl = lambda c0: [(wdb[d], c0, 0), (wda[d], c0, sh)]
                P = nps()
                chain(P, wl(768), 64, n)
                k.act(thb.v[:, 0:n], P.v[0:64, 0:n], ACT.Tanh)
                P = nps()
                chain(P, wl(832), 64, n)
                k.cp("act", alb.v[:, 0:n], P.v[0:64, 0:n])
                for s in range(ns):
                    P = nps()
                    chain_tok(P, wl(512), 256, s)
                    k.cp("act", tbb.v[:, 0:256], P.v[:, 0:256])
                    k.stv("sp", S["D_V"][d, t0 + s * 128:t0 + (s + 1) * 128, :], tbb.v[:, 0:256])
                def dgroup(g, FB, pcx):
                    r_, k_, lw, cs, a_, kk, e_, t1, t2 = FB
                    gc = slice(g, g + 1)
                    P = nps()
                    chain(P, wl(g * 128), 128, n)
                    k.cp("act", r_.v[:, 0:n], P.v[:, 0:n])
                    P = nps()
                    chain(P, wl(256 + g * 128), 128, n)
                    k.cp("act", k_.v[:, 0:n], P.v[:, 0:n])
                    yield
                    P = nps()
                    k.mm(P.v[:, 0:n], w2s.v[:, d, g * 128:(g + 1) * 128], thb.v[:, 0:n])
                    k.act(lw.v[:, 0:n], P.v[:, 0:n], ACT.Sigmoid, bias=pc.v[:, 4 + 2 * d + g:5 + 2 * d + g])
                    k.ts("dve", lw.v[:, 0:n], lw.v[:, 0:n], -DECAY_SCALE, None, ALU.mult)
                    yield
                    P = nps()
                    k.mm(P.v[:, 0:n], a2s.v[:, d, g * 128:(g + 1) * 128], alb.v[:, 0:n])
                    k.act(a_.v[:, 0:n], P.v[:, 0:n], ACT.Sigmoid, bias=pc.v[:, 8 + 2 * d + g:9 + 2 * d + g])
                    yield
                    k.ts("dve", kk.v[:, 0:n], k_.v[:, 0:n], pc.v[:, 12 + g:13 + g], None, ALU.mult)
                    k.act(t1.v[:, 0:n], kk.v[:, 0:n], ACT.Square)
                    yield
                    P = nps()
                    k.mm(P.v[:, 0:n], bd.v[:, :], t1.v[:, 0:n])
                    k.act(t1.v[:, 0:n], P.v[:, 0:n], ACT.Sqrt)
                    yield
                    k.ts("dve", t1.v[:, 0:n], t1.v[:, 0:n], 1e-12, None, ALU.max)
                    k.recip(t1.v[:, 0:n], t1.v[:, 0:n])
                    yield
                    k.tt("dve", kk.v[:, 0:n], kk.v[:, 0:n], t1.v[:, 0:n], ALU.mult)
                    yield
                    k.ts("dve", t1.v[:, 0:n], a_.v[:, 0:n], pc.v[:, 14 + g:15 + g], oma.v[:, gc], ALU.mult, ALU.add)
                    k.tt("dve", k_.v[:, 0:n], k_.v[:, 0:n], t1.v[:, 0:n], ALU.mult)
                    yield
                    k.stt("dve", t1.v[:, 0:n], r_.v[:, 0:n], pc.v[:, 16 + 2 * d + g:17 + 2 * d + g], k_.v[:, 0:n],
                          ALU.mult, ALU.mult)
                    for s in range(ns):
                        P = nps()
                        k.mm(P.v[:, 0:2], t1.v[:, s * 128:(s + 1) * 128], e2.v[:, :])
                        k.cp("act", bs4.v[:, s, 2 * g:2 * g + 2], P.v[:, 0:2])
                    k.tt("dve", a_.v[:, 0:n], a_.v[:, 0:n], kk.v[:, 0:n], ALU.mult)
                    yield
                    src, dst = lw, cs
                    k.cp("act", t2.v[:, 0:n], lw.v[:, 0:n])
                    src = t2
                    for stp in (1, 2, 4, 8, 16, 32):
                        s3 = src.v[:, 0:n].rr("p (c i) -> p c i", i=CH)
                        d3 = dst.v[:, 0:n].rr("p (c i) -> p c i", i=CH)
                        if d == 0:
                            k.tt("dve", d3[:, :, stp:], s3[:, :, stp:], s3[:, :, :CH - stp], ALU.add)
                            k.cp("act", d3[:, :, :stp], s3[:, :, :stp])
                        else:
                            k.tt("dve", d3[:, :, :CH - stp], s3[:, :, :CH - stp], s3[:, :, stp:], ALU.add)
                            k.cp("act", d3[:, :, CH - stp:], s3[:, :, CH - stp:])
                        src, dst = dst, src
                        yield
                    cs = src
                    spare = dst
                    cs3 = cs.v[:, 0:n].rr("p (c i) -> p c i", i=CH)
                    tot3 = cs3[:, :, CH - 1:CH] if d == 0 else cs3[:, :, 0:1]
                    k.act(e_.v[:, 0:n], cs.v[:, 0:n], ACT.Exp)
                    yield
                    k.tt("dve", t1.v[:, 0:n], r_.v[:, 0:n], e_.v[:, 0:n], ALU.mult)
                    st_bf(S["D_Rt"][d, g * 128:(g + 1) * 128, t0:t0 + n], t1.v[:, 0:n], n)
                    yield
                    k.tt("dve", e_.v[:, 0:n], cs.v[:, 0:n], lw.v[:, 0:n], ALU.subtract)
                    k.act(e_.v[:, 0:n], e_.v[:, 0:n], ACT.Exp)
                    yield
                    k.stt("dve", t1.v[:, 0:n], kk.v[:, 0:n], -1.0, e_.v[:, 0:n], ALU.mult, ALU.mult)
                    st_bf(S["D_At"][d, g * 128:(g + 1) * 128, t0:t0 + n], t1.v[:, 0:n], n)
                    yield
                    for s in range(ns):
                        pt = nps()
                        k.tr(pt.v[:, 0:128], t1.v[:, s * 128:(s + 1) * 128], identf.v[:, :])
                        k.cp("act", tbb.v[:, 0:128], pt.v[:, 0:128])
                        k.stv("sp", S["D_Atok"][d, t0 + s * 128:t0 + (s + 1) * 128, g * 128:(g + 1) * 128],
                              tbb.v[:, 0:128])
                    k.act(e_.v[:, 0:n], cs.v[:, 0:n], ACT.Exp, scale=-1.0)
                    k.tt("dve", t1.v[:, 0:n], a_.v[:, 0:n], e_.v[:, 0:n], ALU.mult)
                    st_bf(S["D_Bt"][d, g * 128:(g + 1) * 128, t0:t0 + n], t1.v[:, 0:n], n)
                    yield
                    k.tt("dve", t1.v[:, 0:n], k_.v[:, 0:n], e_.v[:, 0:n], ALU.mult)
                    st_bf(S["D_Kt"][d, g * 128:(g + 1) * 128, t0:t0 + n], t1.v[:, 0:n], n)
                    yield
                    k.act(pcx.v[:, 0:nch], tot3.rr("p c o -> p (c o)"), ACT.Exp)
                    k.stv("sp", S["D_PC"][d, g * 128:(g + 1) * 128, t0 // CH:t0 // CH + nch], pcx.v[:, 0:nch])
                    k.tt("dve", e_.v[:, 0:n].rr("p (c i) -> p c i", i=CH), tot3.bc([128, nch, CH]), cs3, ALU.subtract)
                    k.act(e_.v[:, 0:n], e_.v[:, 0:n], ACT.Exp)
                    yield
                    for (srcb, dstd) in ((a_, S["D_Bh"]), (k_, S["D_Kh"])):
                        k.tt("dve", t1.v[:, 0:n], srcb.v[:, 0:n], e_.v[:, 0:n], ALU.mult)
                        for s in range(ns):
                            pt = nps()
                            k.tr(pt.v[:, 0:128], t1.v[:, s * 128:(s + 1) * 128], identf.v[:, :])
                            k.cp("act", tbb.v[:, 0:128], pt.v[:, 0:128])
                            k.stv("sp", dstd[d, t0 + s * 128:t0 + (s + 1) * 128, g * 128:(g + 1) * 128], tbb.v[:, 0:128])
                gens = [dgroup(0, F[0:9], pcx), dgroup(1, F2, pcx2)]
                live = list(gens)
                while live:
                    for g_ in list(live):
                        try:
                            next(g_)
                        except StopIteration:
                            live.remove(g_)
                for s in range(ns):
                    k.stv("sp", S["D_BS"][d, t0 + s * 128:t0 + (s + 1) * 128, :], bs4.v[:, s, :])


def attnphase(k, st, cfg):
    S, C, Wt = cfg["S"], cfg["C"], cfg["W"]
    T, CTX = cfg["T"], cfg["CTX"]
    TT = T + CTX
    NKC = TT // 128
    need_ctx = cfg["need_ctx"]
    kT = k.sb(st, [128, TT], BF16, "kT")
    va = k.sb(st, [128, NKC, 65], BF16, "va")
    qT = [k.sb(st, [128, 512], BF16, "qT%d" % i) for i in range(2)]
    for c0 in range(0, TT, 2048):
        k.memset("dve", kT.v[64:128, c0:min(TT, c0 + 2048)], 0.0)
    for q_ in qT:
        k.memset("dve", q_.v[64:128, :], 0.0)
    pT = [k.sb(st, [128, 512], BF16, "pT%d" % i) for i in range(3)]
    ot = [k.sb(st, [128, 64], F32, "ot%d" % i) for i in range(2)]
    rc = k.sb(st, [128, 1], F32, "rc")
    esk = k.sb(st, [128, 4], F32, "esk")
    mprev = k.sb(st, [128, 128], BF16, "mprev")
    mnext = k.sb(st, [128, 128], BF16, "mnext")
    k.ld("pool", mprev.v[:, :], C["mprev"])
    k.ld("pool", mnext.v[:, :], C["mnext"])
    k.ld("sp", esk.v[:, :], Wt["sink"].partition_broadcast(128))
    k.act(esk.v[:, :], esk.v[:, :], ACT.Exp)
    ps_s = [k.ps(st, [128, 512], F32, "pss%d" % i) for i in range(3)]
    ps_o = [k.ps(st, [128, 512], F32, "pso%d" % i) for i in range(4)]
    cnt = dict(s=0, q=0, o=0, p=0)
    scale = 64 ** -0.5

    def load_kv(kd, vd, g):
        k.ld("sp", kT.v[0:64, :], kd[g * 64:(g + 1) * 64, :])
        k.memset("dve", va.v[:, :, 64:65], 1.0)
        k.ld("sp", va.v[:, :, 0:64], vd[:, g * 64:(g + 1) * 64].rearrange("(c p) j -> p c j", p=128))

    def finish(acc, nq_sub, t0, col, sinkcol):
        for s in range(nq_sub):
            o = ot[cnt["o"] % 2]
            cnt["o"] += 1
            if sinkcol is None:
                k.recip(rc.v[:, :], acc[s].v[:, 64:65])
            else:
                k.tt("dve", rc.v[:, :], acc[s].v[:, 64:65], esk.v[:, sinkcol:sinkcol + 1], ALU.add)
                k.recip(rc.v[:, :], rc.v[:, :])
            k.ts("dve", o.v[:, :], acc[s].v[:, 0:64], rc.v[:, 0:1], None, ALU.mult)
            k.stv("sp", S["O"][t0 + s * 128:t0 + (s + 1) * 128, col:col + 64], o.v[:, :])

    def qk(q, nq, kc):
        ps = ps_s[cnt["s"] % 3]
        cnt["s"] += 1
        k.mm(ps.v[:, 0:nq], kT.v[:, kc * 128:(kc + 1) * 128], q.v[:, 0:nq])
        return ps

    def prologue(b):
        if b.get("kv") is not None:
            load_kv(*b["kv"])
        q = qT[cnt["q"] % 2]
        cnt["q"] += 1
        k.ld("sp", q.v[0:64, 0:b["nq"]], b["qd"][b["h"] * 64:(b["h"] + 1) * 64, b["t0"]:b["t0"] + b["nq"]])
        return q, qk(q, b["nq"], b["chunks"][0][0])

    def body(b, q, nxt):
        nq, chunks = b["nq"], b["chunks"]
        first, last = {}, {}
        for i, (kc, subs) in enumerate(chunks):
            for (s, _) in subs:
                first.setdefault(s, i)
                last[s] = i
        for i, (kc, subs) in enumerate(chunks):
            ps = nxt
            if i + 1 < len(chunks):
                nxt = qk(q, nq, chunks[i + 1][0])
            p = pT[cnt["p"] % 3]
            cnt["p"] += 1
            k.act(p.v[:, 0:nq], ps.v[:, 0:nq], ACT.Exp, scale=scale)
            for (s, m_) in subs:
                if m_ is not None:
                    k.tt("dve", p.v[:, s * 128:(s + 1) * 128], p.v[:, s * 128:(s + 1) * 128], m_.v[:, :], ALU.mult)
            for (s, m_) in subs:
                k.mm(ps_o[s].v[:, 0:65], p.v[:, s * 128:(s + 1) * 128], va.v[:, kc, :], i == first[s], i == last[s])

    blocks = []

    def block(qd, h, t0, nq, chunks, col, sinkcol, kv=None):
        blocks.append(dict(qd=qd, h=h, t0=t0, nq=nq, chunks=chunks, col=col, sinkcol=sinkcol, kv=kv))

    for g in range(2):
        kv = (S["KB"], S["VB"], g)
        for h in (2 * g, 2 * g + 1):
            for t0 in range(0, T, 512):
                nq = min(512, T - t0)
                block(S["QB"], h, t0, nq, [(kc, [(s, None) for s in range(nq // 128)]) for kc in range(NKC)],
                      256 + h * 64, None, kv)
                kv = None
            if need_ctx:
                block(S["QB"], h, T, CTX, [(kc, [(s, None) for s in range(CTX // 128)]) for kc in range(T // 128, NKC)],
                      256 + h * 64, None)
    nb = T // 128
    cchunks = list(range(T // 128, NKC))
    for g in range(2):
        kv = (S["KA"], S["VA"], g)
        for h in (2 * g, 2 * g + 1):
            for b0 in range(0, nb, 4):
                nsb = min(4, nb - b0)
                chunks = []
                for kc in range(max(0, b0 - 1), min(nb, b0 + nsb + 1)):
                    subs = []
                    for s in range(nsb):
                        dlt = kc - (b0 + s)
                        if dlt == -1:
                            subs.append((s, mprev))
                        elif dlt == 0:
                            subs.append((s, None))
                        elif dlt == 1:
                            subs.append((s, mnext))
                    chunks.append((kc, subs))
                for kc in cchunks:
                    chunks.append((kc, [(s, None) for s in range(nsb)]))
                block(S["QA"], h, b0 * 128, nsb * 128, chunks, h * 64, h, kv)
                kv = None
            if need_ctx:
                block(S["QA"], h, T, CTX, [(kc, [(s, None) for s in range(CTX // 128)]) for kc in cchunks], h * 64, h)
    st_ = prologue(blocks[0])
    for i, b in enumerate(blocks):
        body(b, *st_)
        if i + 1 < len(blocks):
            st_ = prologue(blocks[i + 1])
        finish(ps_o, b["nq"] // 128, b["t0"], b["col"], b["sinkcol"])


def scanphase(k, st, cfg):
    S, C = cfg["S"], cfg["C"]
    T, CTX = cfg["T"], cfg["CTX"]
    TT = T + CTX
    dplr = cfg["dplr"]
    pre = "D_" if dplr else "C_"
    NCG = 2
    GT_ = NCG * CH
    NCHT = TT // CH
    f3 = lambda b: b.v[:, :, :]
    t64 = lambda nm: k.sb(st, [64, 8, 64], F32, nm)
    b64 = lambda nm: k.sb(st, [64, 8, 64], BF16, nm)
    chm = lambda nm: k.sb(st, [64, 8, GT_], BF16, nm)
    tkm = lambda nm: k.sb(st, [64, NCG, 8, 64], BF16, nm)
    RT = [chm("RT%d" % i) for i in range(2)]
    KT = [chm("KT%d" % i) for i in range(2)]
    KH = [tkm("KH%d" % i) for i in range(2)]
    VV = [tkm("VV%d" % i) for i in range(2)]
    Min = t64("Min")
    k.ld("sp", f3(Min), C["m_in"].rearrange("p (q t) -> p q t", q=8))
    id8 = t64("id8")
    k.ld("sp", f3(id8), C["id8"].rearrange("p (q t) -> p q t", q=8))
    if dplr:
        AT = [chm("AT%d" % i) for i in range(2)]
        BT = [chm("BT%d" % i) for i in range(2)]
        BH = [tkm("BH%d" % i) for i in range(2)]
        AK = [tkm("AK%d" % i) for i in range(2)]
        Mst, Mts = t64("Mst"), t64("Mts")
        k.ld("sp", f3(Mst), C["m_st"].rearrange("p (q t) -> p q t", q=8))
        k.ld("sp", f3(Mts), C["m_ts"].rearrange("p (q t) -> p q t", q=8))
        PCt = k.sb(st, [64, 8, NCHT], F32, "PCt")
        for d in range(2):
            k.ld("sp", PCt.v[:, d * 4:(d + 1) * 4, :], S["D_PC"][d].rearrange("(h j) c -> j h c", j=64))
    else:
        PCr = k.sb(st, [64, 8], F32, "PCr")
        k.ld("sp", PCr.v[:, :], C["ret_pc"])
        GTc = b64("GTc")
        k.tt("dve", f3(GTc), f3(id8), PCr.v[:, :].rr("p (q o) -> p q o", o=1).bc([64, 8, 64]), ALU.mult)
    sets = []
    for c in range(NCG):
        B = dict(ArkT=b64("ArkT%d" % c), Yv=t64("Yv%d" % c), H=t64("H%d" % c))
        if dplr:
            B.update(Ls=[b64("L%d_%d" % (i, c)) for i in range(5)], Ns=[b64("N%d_%d" % (i, c)) for i in range(6)],
                     AakT=b64("AakT%d" % c), ArbT=b64("ArbT%d" % c), Qe=b64("Qe%d" % c), GTt=b64("GTt%d" % c),
                     X=[k.sb(st, [64, 8, 128], F32, "X%d_%d" % (i, c)) for i in range(2)],
                     Xb=k.sb(st, [64, 8, 128], BF16, "Xb_%d" % c),
                     psX=k.ps(st, [64, 1024], F32, "psX%d" % c))
        sets.append(B)
    ST = [b64("ST%d" % i) for i in range(2)]
    Yt = [t64("Yt%d" % i) for i in range(2)]
    psA = [k.ps(st, [64, 512], F32, "psA%d" % i) for i in range(2)]
    psY = [k.ps(st, [64, 512], F32, "psY%d" % i) for i in range(2)]
    cnt = dict(a=0, y=0, e=0)
    p3 = lambda p: p.v[:, :].rr("p (q t) -> p q t", q=8)

    def npa():
        p = psA[cnt["a"] % 2]
        cnt["a"] += 1
        return p

    def ev():
        cnt["e"] += 1
        return "act" if cnt["e"] % 2 else "dve"

    def prod(dst, lhs, rhs, mask):
        p = npa()
        for q in range(8):
            k.mm(p3(p)[:, q, :], lhs(q), rhs(q))
        if mask is None:
            k.cp(ev(), f3(dst), p3(p))
        else:
            k.tt("dve", f3(dst), p3(p), f3(mask), ALU.mult)

    def views(rb, cl):
        sl = {d: slice(cl[d] * CH, (cl[d] + 1) * CH) for d in range(2)}
        dq = lambda q: q // 4
        chs = lambda buf: (lambda q: buf[rb].v[:, q, sl[dq(q)]])
        tks = lambda buf: (lambda q: buf[rb].v[:, cl[dq(q)], q, :])
        return sl, chs, tks

    def ppart(B, rb, cl, grp):
        sl, chs, tks = views(rb, cl)
        rt, kt, kh, vv = chs(RT), chs(KT), tks(KH), tks(VV)
        prod(B["ArkT"], kt, rt, Min)
        yield
        if dplr:
            at, bt, bh = chs(AT), chs(BT), tks(BH)
            Ls, Ns, AakT, ArbT = B["Ls"], B["Ns"], B["AakT"], B["ArbT"]
            prod(Ns[0], bt, at, Mst)
            prod(Ls[0], at, bt, Mts)
            yield
            prod(AakT, kt, at, Mst)
            prod(ArbT, bt, rt, Min)
            yield
            xc, xn = B["X"]
            xb = B["Xb"]
            for d in range(2):
                k.cp("act", xc.v[:, d * 4:(d + 1) * 4, 0:64], AK[rb].v[:, cl[d], d * 4:(d + 1) * 4, :])
            p = npa()
            for q in range(8):
                k.mm(p3(p)[:, q, :], AakT.v[:, q, :], vv(q))
            k.cp("dve", xc.v[:, :, 64:128], p3(p))
            k.cp("act", f3(xb), f3(xc))
            yield
            px = B["psX"].v[:, :].rr("p (q c) -> p q c", q=8)
            for lv in range(6):
                for q in range(8):
                    k.mm(px[:, q, :], Ns[lv].v[:, q, :], xb.v[:, q, :])
                if lv < 5:
                    if lv < 4:
                        prod(Ls[lv + 1], lambda q: Ns[lv].v[:, q, :], lambda q: Ls[lv].v[:, q, :], None)
                    prod(Ns[lv + 1], lambda q: Ls[lv].v[:, q, :], lambda q: Ns[lv].v[:, q, :], None)
                k.tt("dve", f3(xn), f3(xc), px, ALU.add)
                xc, xn = xn, xc
                k.cp("act", f3(xb), f3(xc))
                yield
            wq = lambda q: xb.v[:, q, 0:64]
            uv = lambda q: xb.v[:, q, 64:128]
            p = npa()
            for q in range(8):
                k.mm(p3(p)[:, q, :], wq(q), ArbT.v[:, q, :])
            for d in range(2):
                k.tt("dve", B["Qe"].v[:, d * 4:(d + 1) * 4, :], p3(p)[:, d * 4:(d + 1) * 4, :],
                     RT[rb].v[:, d * 4:(d + 1) * 4, sl[d]], ALU.add)
            p = npa()
            for q in range(8):
                k.mm(p3(p)[:, q, :], wq(q), bh(q))
            for d in range(2):
                c_abs = grp[d] * NCG + cl[d]
                k.tt("dve", B["GTt"].v[:, d * 4:(d + 1) * 4, :], id8.v[:, d * 4:(d + 1) * 4, :],
                     PCt.v[:, d * 4:(d + 1) * 4, c_abs:c_abs + 1].bc([64, 4, 64]), ALU.mult)
            k.tt("dve", f3(B["GTt"]), f3(B["GTt"]), p3(p), ALU.add)
            yield
        p = npa()
        for q in range(8):
            k.mm(p3(p)[:, q, :], B["ArkT"].v[:, q, :], vv(q), True, not dplr)
            if dplr:
                k.mm(p3(p)[:, q, :], ArbT.v[:, q, :], uv(q), False, True)
        k.cp(ev(), f3(B["Yv"]), p3(p))
        p = npa()
        for q in range(8):
            k.mm(p3(p)[:, q, :], kh(q), vv(q), True, not dplr)
            if dplr:
                k.mm(p3(p)[:, q, :], bh(q), uv(q), False, True)
        k.cp(ev(), f3(B["H"]), p3(p))
        yield

    k.memset("dve", f3(ST[0]), 0.0)
    ngl = T // GT_
    ncg_ctx = CTX // GT_
    fw = list(range(ngl, ngl + ncg_ctx)) + list(range(ngl))
    bw = list(range(ngl + ncg_ctx - 1, ngl - 1, -1)) + list(range(ngl - 1, -1, -1))
    order = {0: fw, 1: bw}
    step = 0
    for gi in range(ngl + ncg_ctx):
        rb = gi % 2
        grp = {d: order[d][gi] for d in range(2)}
        for d in range(2):
            t0 = grp[d] * GT_
            qs = slice(d * 4, (d + 1) * 4)
            chv = lambda nm: S[pre + nm][d, :, t0:t0 + GT_].rearrange("(h j) t -> j h t", j=64)
            tkv = lambda ap: ap[t0:t0 + GT_, :].rearrange("(n t) (h j) -> t n h j", t=CH, j=64)
            k.ld("sp", RT[rb].v[:, qs, :], chv("Rt"))
            k.ld("sp", KT[rb].v[:, qs, :], chv("Kt"))
            k.ld("sp", KH[rb].v[:, :, qs, :], tkv(S[pre + "Kh"][d]))
            k.ld("sp", VV[rb].v[:, :, qs, :], tkv(S["D_V"][d] if dplr else S["C_V"]))
            if dplr:
                k.ld("sp", AT[rb].v[:, qs, :], chv("At"))
                k.ld("sp", BT[rb].v[:, qs, :], chv("Bt"))
                k.ld("sp", BH[rb].v[:, :, qs, :], tkv(S["D_Bh"][d]))
                k.ld("sp", AK[rb].v[:, :, qs, :], tkv(S["D_Atok"][d]))
        cls = [{0: ci, 1: NCG - 1 - ci} for ci in range(NCG)]
        gens = [ppart(sets[ci], rb, cls[ci], grp) for ci in range(NCG)]
        live = list(gens)
        while live:
            for g_ in list(live):
                try:
                    next(g_)
                except StopIteration:
                    live.remove(g_)
        for ci in range(NCG):
            B, cl = sets[ci], cls[ci]
            sl, chs, tks = views(rb, cl)
            Sc, Sn = ST[step % 2], ST[(step + 1) % 2]
            qe = (lambda q: B["Qe"].v[:, q, :]) if dplr else chs(RT)
            gt = (lambda q: B["GTt"].v[:, q, :]) if dplr else (lambda q: GTc.v[:, q, :])
            py = psY[cnt["y"] % 2]
            cnt["y"] += 1
            for q in range(8):
                k.mm(p3(py)[:, q, :], qe(q), Sc.v[:, q, :])
            p = npa()
            for q in range(8):
                k.mm(p3(p)[:, q, :], gt(q), Sc.v[:, q, :])
            k.tt("dve", f3(Sn), p3(p), f3(B["H"]), ALU.add)
            yt = Yt[step % 2]
            k.tt("dve", f3(yt), p3(py), f3(B["Yv"]), ALU.add)
            for d in range(2):
                tt0 = grp[d] * GT_ + cl[d] * CH
                k.stv("sp", S[pre + "Y"][d, tt0:tt0 + CH, :].rearrange("t (h i) -> t h i", i=64),
                      yt.v[:, d * 4:(d + 1) * 4, :])
            step += 1


GN_EPS = 64e-5


def postphase(k, st, cfg):
    S, Wt = cfg["S"], cfg["W"]
    TT = cfg["T"] + cfg["CTX"]
    row = lambda nm, ap: (lambda t: (k.ld("sp", t.v[:, :], ap.partition_broadcast(128)), t)[1])(k.sb(st, [128, 256], F32, nm))
    retg = row("retg", Wt["ret_g"])
    lng = row("lng", Wt["ln_g"])
    lnb = row("lnb", Wt["ln_b"])
    gc = k.sb(st, [128, 512], F32, "gc")
    gd = k.sb(st, [128, 256], F32, "gd")
    vd = k.sb(st, [128, 4, 64], BF16, "vd")
    vdf = k.sb(st, [128, 4, 64], F32, "vdf")
    bs = k.sb(st, [128, 4], F32, "bs")

    def bufset(nm):
        return dict(y=[k.sb(st, [128, 4, 64], F32, nm + "y%d" % i) for i in range(2)],
                    yc=k.sb(st, [128, 4, 64], F32, nm + "yc"), sq=k.sb(st, [128, 4, 64], F32, nm + "sq"),
                    m4=k.sb(st, [128, 4], F32, nm + "m4"), v4=k.sb(st, [128, 4], F32, nm + "v4"),
                    acc=k.sb(st, [128, 256], F32, nm + "acc"), yn=k.sb(st, [128, 256], F32, nm + "yn"), i=0)

    BC, BD = bufset("c"), bufset("d")

    def head_norm(B, yb, g, b, dst, eng="dve"):
        yc, sq, m4, v4 = B["yc"], B["sq"], B["m4"], B["v4"]
        k.op("dve", lambda e: e.tensor_reduce(out=m4[:, :], in_=yb[:, :, :], axis=AX.X, op=ALU.add), r=[yb], w=[m4])
        k.ts("dve", m4.v[:, :], m4.v[:, :], 1.0 / 64, None, ALU.mult)
        yield
        k.tt(eng, yc.v[:, :, :], yb.v[:, :, :], m4.v[:, :].rr("p (h o) -> p h o", o=1).bc([128, 4, 64]), ALU.subtract)
        yield
        k.tt(eng, sq.v[:, :, :], yc.v[:, :, :], yc.v[:, :, :], ALU.mult)
        yield
        k.op("dve", lambda e: e.tensor_reduce(out=v4[:, :], in_=sq[:, :, :], axis=AX.X, op=ALU.add), r=[sq], w=[v4])
        yield
        k.ts("dve", v4.v[:, :], v4.v[:, :], 1.0 / 64, GN_EPS, ALU.mult, ALU.add)
        yield
        k.act(v4.v[:, :], v4.v[:, :], ACT.Sqrt)
        yield
        k.recip(v4.v[:, :], v4.v[:, :])
        yield
        k.tt(eng, yc.v[:, :, :], yc.v[:, :, :], v4.v[:, :].rr("p (h o) -> p h o", o=1).bc([128, 4, 64]), ALU.mult)
        yield
        k.tt(eng, dst, yc.v[:, :, :].rr("p h i -> p (h i)"), g.v[:, :], ALU.mult)
        if b is not None:
            k.tt(eng, dst, dst, b.v[:, :], ALU.add)
        yield

    def part_c(tsl):
        B = BC
        acc, yn = B["acc"], B["yn"]
        k.ld("sp", gc.v[:, :], S["C_G"][tsl, :])
        for d in range(2):
            yb = B["y"][B["i"] % 2]
            B["i"] += 1
            k.ld("sp", yb.v[:, :, :], S["C_Y"][d, tsl, :].rearrange("t (h i) -> t h i", i=64))
            yield from head_norm(B, yb, retg, None, yn.v[:, :])
            if d == 0:
                k.tt("dve", acc.v[:, :], yn.v[:, :], gc.v[:, 0:256], ALU.mult)
            else:
                k.tt("dve", yn.v[:, :], yn.v[:, :], gc.v[:, 256:512], ALU.mult)
                k.tt("dve", acc.v[:, :], acc.v[:, :], yn.v[:, :], ALU.add)
            yield
        k.stv("sp", S["O"][tsl, 512:768], acc.v[:, :])

    def part_d(tsl):
        B = BD
        acc, yn = B["acc"], B["yn"]
        k.ld("sp", gd.v[:, :], S["D_G"][tsl, :])
        for d in range(2):
            yb = B["y"][B["i"] % 2]
            B["i"] += 1
            k.ld("sp", yb.v[:, :, :], S["D_Y"][d, tsl, :].rearrange("t (h i) -> t h i", i=64))
            k.ld("sp", vd.v[:, :, :], S["D_V"][d, tsl, :].rearrange("t (h i) -> t h i", i=64))
            k.ld("sp", bs.v[:, :], S["D_BS"][d, tsl, :])
            yield from head_norm(B, yb, lng, lnb, yn.v[:, :], cfg.get("post_eng", "pool"))
            k.tt("dve", vdf.v[:, :, :], vd.v[:, :, :], bs.v[:, :].rr("p (h o) -> p h o", o=1).bc([128, 4, 64]), ALU.mult)
            k.tt("dve", yn.v[:, :], yn.v[:, :], vdf.v[:, :, :].rr("p h i -> p (h i)"), ALU.add)
            yield
            if d == 0:
                k.cp("act", acc.v[:, :], yn.v[:, :])
            else:
                k.tt("dve", acc.v[:, :], acc.v[:, :], yn.v[:, :], ALU.add)
            yield
        k.tt("dve", acc.v[:, :], acc.v[:, :], gd.v[:, :], ALU.mult)
        k.stv("sp", S["O"][tsl, 768:1024], acc.v[:, :])

    for t0 in range(0, TT, 128):
        tsl = slice(t0, t0 + 128)
        live = [part_c(tsl), part_d(tsl)]
        while live:
            for g_ in list(live):
                try:
                    next(g_)
                except StopIteration:
                    live.remove(g_)


def modphase(k, st, cfg):
    cv = k.sb(st, [128, 2, 8], F32, "cv")
    for r in range(2):
        k.ld("sp", cv.v[:, r, :], cfg["cvec"][r].rearrange("(c p) -> p c", p=128), allow_slow_non_contiguous=True)
    sc = k.sb(st, [128, 8, 2], F32, "sc")
    k.act(sc.v[:, :, :].rr("p c r -> p r c"), cv.v[:, :, :], ACT.Silu)
    wm = [k.sb(st, [128, 8, 512], F32, "wm%d" % i) for i in range(2)]
    bm = k.sb(st, [2, 9 * D], F32, "bm")
    k.ld("sp", bm.v[:, :], cfg["b_mod"].partition_broadcast(2))
    ob = k.sb(st, [2, 9 * D], F32, "ob")
    pm = [k.ps(st, [128, 512], F32, "pm%d" % i) for i in range(2)]
    wv = cfg["w_mod"].rearrange("(c p) f -> p c f", p=128)
    for j in range(18):
        w = wm[j % 2]
        for c in range(8):
            k.ld("sp", w.v[:, c, :], wv[:, c, j * 512:(j + 1) * 512])
        p = pm[j % 2]
        for c in range(8):
            k.mm(p.v[0:2, :], sc.v[:, c, :], w.v[:, c, :], c == 0, c == 7)
        k.tt("dve", ob.v[:, j * 512:(j + 1) * 512], p.v[0:2, :], bm.v[:, j * 512:(j + 1) * 512], ALU.add)
    k.stv("sp", cfg["mod"], ob.v[:, :])


def build_program(T, CTX, L, stop_after=None):
    TT = T + CTX
    nc = bass.Bass("TRN2", target_bir_lowering=False, dynamic_dma_scratch_size=8192)

    def din(name, shape, dt=F32):
        return nc.dram_tensor(name, list(shape), dt, kind="ExternalInput").ap()

    def scr(name, shape, dt=F32):
        return nc.dram_tensor(name, list(shape), dt, kind="Internal").ap()

    I = dict(
        xin=din("xin", [TT, D]), cvec=din("cvec", [2, D]),
        w_mod=din("w_mod", [L, D, 9 * D]), b_mod=din("b_mod", [L, 9 * D]), norm_g=din("norm_g", [L, 3, D]),
        ffn_w_in=din("ffn_w_in", [L, 2, D, 2 * DFF]), ffn_w_out=din("ffn_w_out", [L, 2, DFF, D]),
        w_in=din("w_in", [L, D, 3456]), w_out=din("w_out", [L, D, D]), final_g=din("final_g", [D]),
        w_sw=din("w_sw", [L, D, 1280]), w_d=din("w_d", [L, 2, D, 896]), pcols=din("pcols", [L, 128, 24]),
        mu=din("mu", [L, 2, 896]), w2=din("w2", [L, 2, 64, 256]), a2=din("a2", [L, 2, 64, 256]),
        g2=din("g2", [L, 128, 256]), ret_g=din("ret_g", [L, 256]), ln_g=din("ln_g", [L, 256]),
        ln_b=din("ln_b", [L, 256]), sink=din("sink", [L, 4]),
    )
    Cn = dict(
        ident=din("ident", [128, 128]), bd64=din("bd64", [128, 128]), e2=din("e2", [128, 2]),
        ropeA_c=din("ropeA_c", [128, T]), ropeA_s=din("ropeA_s", [128, T]),
        ropeS_c=din("ropeS_c", [128, T]), ropeS_s=din("ropeS_s", [128, T]),
        ret_tab=din("ret_tab", [128, 2, 2, 3, CH]), ret_pc=din("ret_pc", [64, 8]),
        m_st=din("m_st", [64, 512]), m_ts=din("m_ts", [64, 512]), m_in=din("m_in", [64, 512]),
        id8=din("id8", [64, 512]), mprev=din("mprev", [128, 128]), mnext=din("mnext", [128, 128]),
    )
    out = nc.dram_tensor("out", [T, D], F32, kind="ExternalOutput").ap()
    dbg = stop_after is not None
    mk = (lambda name, shape, dt=F32: nc.dram_tensor(name, list(shape), dt, kind="ExternalOutput").ap()) if dbg else scr
    S = dict(
        xres=mk("xres", [TT, D]), mod=mk("modr", [2, 9 * D]), hT=mk("hT", [D, TT], BF16),
        QA=mk("QA", [256, TT], BF16), KA=mk("KA", [128, TT], BF16), VA=mk("VA", [TT, 128], BF16),
        QB=mk("QB", [256, TT], BF16), KB=mk("KB", [128, TT], BF16), VB=mk("VB", [TT, 128], BF16),
        O=mk("O", [TT, 1024]),
        C_Rt=mk("C_Rt", [2, 256, TT], BF16), C_Kt=mk("C_Kt", [2, 256, TT], BF16), C_Kh=mk("C_Kh", [2, TT, 256], BF16),
        C_V=mk("C_V", [TT, 256], BF16), C_G=mk("C_G", [TT, 512]), C_Y=mk("C_Y", [2, TT, 256]),
        D_Rt=mk("D_Rt", [2, 256, TT], BF16), D_Kt=mk("D_Kt", [2, 256, TT], BF16), D_At=mk("D_At", [2, 256, TT], BF16),
        D_Bt=mk("D_Bt", [2, 256, TT], BF16), D_Kh=mk("D_Kh", [2, TT, 256], BF16), D_Bh=mk("D_Bh", [2, TT, 256], BF16),
        D_Atok=mk("D_Atok", [2, TT, 256], BF16), D_V=mk("D_V", [2, TT, 256], BF16), D_Y=mk("D_Y", [2, TT, 256]),
        D_PC=mk("D_PC", [2, 256, TT // CH]), D_G=mk("D_G", [TT, 256]), D_BS=mk("D_BS", [2, TT, 4]),
    )
    k = K(nc)
    stages = []

    def phase(name, fn, cfg):
        if stop_after is not None and stop_after in stages:
            return
        with contextlib.ExitStack() as st:
            fn(k, st, cfg)
            k.barrier()
        stages.append(name)
        k.marks.append((name, {e.name: e.n for e in k.E.values()}))

    with k.stack:
        segs_all = [(0, T, 0), (T, CTX, 1)]
        for l in range(L):
            last = l == L - 1
            W = dict(w_in=I["w_in"][l], w_sw=I["w_sw"][l], w_d=I["w_d"][l], mu=I["mu"][l], pcols=I["pcols"][l],
                     w2=I["w2"][l], a2=I["a2"][l], g2=I["g2"][l], ret_g=I["ret_g"][l], ln_g=I["ln_g"][l],
                     ln_b=I["ln_b"][l], sink=I["sink"][l])
            base = dict(S=S, C=Cn, W=W, T=T, CTX=CTX)
            phase("mod%d" % l, modphase, dict(cvec=I["cvec"], w_mod=I["w_mod"][l], b_mod=I["b_mod"][l], mod=S["mod"]))
            phase("f1_%d" % l, rowphase, dict(
                xin=I["xin"] if l == 0 else S["xres"], xout=S["xres"], segs=segs_all, mod=S["mod"], ident=Cn["ident"],
                ffn=dict(w_in=I["ffn_w_in"][l, 0], w_out=I["ffn_w_out"][l, 0], g=I["norm_g"][l, 0], shift=0, scale=1,
                         gate=2),
                post=dict(kind="hT", g=I["norm_g"][l, 1], shift=3, scale=4, hT=S["hT"])))
            phase("proj%d" % l, projphase, base)
            phase("attn%d" % l, attnphase, dict(base, need_ctx=not last))
            phase("scanC%d" % l, scanphase, dict(base, dplr=False))
            phase("scanD%d" % l, scanphase, dict(base, dplr=True))
            phase("post%d" % l, postphase, base)
            phase("f2_%d" % l, rowphase, dict(
                xin=S["xres"], xout=S["xres"], segs=segs_all if not last else [(0, T, 0)], mod=S["mod"],
                ident=Cn["ident"],
                pre=dict(o=S["O"], w_out=I["w_out"][l], gate=5),
                ffn=dict(w_in=I["ffn_w_in"][l, 1], w_out=I["ffn_w_out"][l, 1], g=I["norm_g"][l, 2], shift=6, scale=7,
                         gate=8),
                post=dict(kind="final", g=I["final_g"], out=out) if last else None))
    return nc, k


def _consts(T):
    c = {}
    c["ident"] = np.eye(128, dtype=np.float32)
    bd = np.zeros((128, 128), np.float32)
    bd[:64, :64] = 1
    bd[64:, 64:] = 1
    c["bd64"] = bd
    e2 = np.zeros((128, 2), np.float32)
    e2[:64, 0] = 1
    e2[64:, 1] = 1
    c["e2"] = e2
    t = np.arange(T)
    row = (t // 64).astype(np.float32)
    col = (t % 64).astype(np.float32)
    inv16 = (1.0 / (np.float32(10000.0) ** (np.arange(0, 32, 2, dtype=np.float32) / np.float32(32)))).astype(np.float32)
    inv32 = (1.0 / (np.float32(10000.0) ** (np.arange(0, 64, 2, dtype=np.float32) / np.float32(64)))).astype(np.float32)
    ca = np.zeros((64, T), np.float32)
    sa = np.zeros((64, T), np.float32)
    for blk, pos in ((0, row), (32, col)):
        ang = pos[None, :] * inv16[:, None]
        ca[blk:blk + 16] = np.cos(ang)
        ca[blk + 16:blk + 32] = np.cos(ang)
        sa[blk:blk + 16] = -np.sin(ang)
        sa[blk + 16:blk + 32] = np.sin(ang)
    ang = t.astype(np.float32)[None, :] * inv32[:, None]
    cs = np.concatenate([np.cos(ang), np.cos(ang)], 0).astype(np.float32)
    ss = np.concatenate([-np.sin(ang), np.sin(ang)], 0).astype(np.float32)
    c["ropeA_c"] = np.concatenate([ca, ca], 0)
    c["ropeA_s"] = np.concatenate([sa, sa], 0)
    c["ropeS_c"] = np.concatenate([cs, cs], 0)
    c["ropeS_s"] = np.concatenate([ss, ss], 0)
    lg = np.log1p(-np.exp2(-5.0 - np.arange(4, dtype=np.float64)))
    i = np.arange(CH, dtype=np.float64)
    rt = np.zeros((128, 2, 2, 3, CH), np.float64)
    for p in range(128):
        for g in range(2):
            l_ = lg[2 * g + p // 64]
            for d in range(2):
                csum = (i + 1) * l_ if d == 0 else (CH - i) * l_
                rt[p, g, d, 0] = np.exp(csum)
                rt[p, g, d, 1] = np.exp(-csum) / 8.0
                rt[p, g, d, 2] = np.exp(CH * l_ - csum) / 8.0
    c["ret_tab"] = rt.astype(np.float32)
    c["ret_pc"] = np.tile(np.exp(CH * lg)[None, :], (64, 2)).astype(np.float32)
    s_ = np.arange(64)[:, None]
    t_ = np.arange(64)[None, :]
    lt = (s_ < t_).astype(np.float32)
    gt = (s_ > t_).astype(np.float32)
    eye = np.eye(64, dtype=np.float32)
    c["m_st"] = np.concatenate([lt] * 4 + [gt] * 4, 1)
    c["m_ts"] = np.concatenate([gt] * 4 + [lt] * 4, 1)
    c["m_in"] = np.concatenate([lt + eye] * 4 + [gt + eye] * 4, 1)
    c["id8"] = np.concatenate([eye] * 8, 1)
    j = np.arange(128)[:, None]
    ii = np.arange(128)[None, :]
    c["mprev"] = (ii <= j).astype(np.float32)
    c["mnext"] = (j <= ii).astype(np.float32)
    return c


def _host_maps(inp, T, CTX, L):
    f = lambda a: np.ascontiguousarray(np.asarray(a, dtype=np.float32))
    w_in = f(inp["w_in"])
    pa = np.concatenate([np.arange(16, 32), np.arange(0, 16), np.arange(48, 64), np.arange(32, 48)])
    psq = np.concatenate([np.arange(32, 64), np.arange(0, 32)])
    cols = []
    for base, nh, perm in ((0, 4, pa), (256, 2, pa), (512, 4, pa), (768, 2, pa), (1024, 4, psq), (1280, 4, psq)):
        for h in range(nh):
            cols.append(base + h * 64 + perm)
    cols = np.concatenate(cols)
    w_sw = np.ascontiguousarray(w_in[:, :, cols])
    dcols = [np.concatenate([np.arange(2304, 3072), np.arange(3200, 3264), np.arange(3328, 3392)]),
             np.concatenate([np.arange(2304, 3072), np.arange(3264, 3328), np.arange(3392, 3456)])]
    w_d = np.ascontiguousarray(np.stack([w_in[:, :, dc] for dc in dcols], 1))
    qkg = f(inp["qk_norm_g"])
    pcols = np.zeros((L, 128, 24), np.float32)
    two = lambda v: np.ascontiguousarray(v.reshape(2, 128).T)
    for l in range(L):
        for qk in range(2):
            g = qkg[l, qk]
            pcols[l, :, 2 * qk] = np.tile(g, 2)
            pcols[l, :, 2 * qk + 1] = np.tile(g[pa], 2)
        for d in range(2):
            pcols[l, :, 4 + 2 * d:6 + 2 * d] = two(f(inp["rwkv_w0"])[l, d])
            pcols[l, :, 8 + 2 * d:10 + 2 * d] = two(f(inp["rwkv_a0"])[l, d])
            pcols[l, :, 16 + 2 * d:18 + 2 * d] = two(f(inp["rwkv_rho"])[l, d].reshape(256))
        pcols[l, :, 12:14] = two(f(inp["rwkv_k_k"])[l])
        pcols[l, :, 14:16] = two(f(inp["rwkv_k_a"])[l])
    shared = dict(
        w_mod=f(inp["w_mod"]), b_mod=f(inp["b_mod"]), norm_g=f(inp["norm_g"]), ffn_w_in=f(inp["ffn_w_in"]),
        ffn_w_out=f(inp["ffn_w_out"]), w_in=w_in, w_out=f(inp["w_out"]), final_g=f(inp["final_norm_g"]),
        w_sw=w_sw, w_d=w_d, pcols=pcols, mu=f(inp["rwkv_mu"]), w2=f(inp["rwkv_w2"]), a2=f(inp["rwkv_a2"]),
        g2=f(inp["rwkv_g2"]), ret_g=f(inp["ret_norm_g"]), ln_g=f(inp["rwkv_ln_g"]), ln_b=f(inp["rwkv_ln_b"]),
        sink=f(inp["attn_sink"]))
    shared.update(_consts(T))
    x, c, ctx, c_ctx = f(inp["x"]), f(inp["c"]), f(inp["ctx"]), f(inp["c_ctx"])
    maps = []
    for b in range(x.shape[0]):
        m = dict(shared)
        m["xin"] = np.ascontiguousarray(np.concatenate([x[b], ctx[b]], 0))
        m["cvec"] = np.ascontiguousarray(np.stack([c[b], c_ctx], 0))
        maps.append(m)
    return maps


_PROG = {}


def kernel(**inputs):
    x = np.asarray(inputs["x"])
    B, T, _ = x.shape
    CTX = np.asarray(inputs["ctx"]).shape[1]
    L = np.asarray(inputs["w_in"]).shape[0]
    key = (T, CTX, L)
    if key not in _PROG:
        _PROG[key] = build_program(T, CTX, L)[0]
    nc = _PROG[key]
    maps = _host_maps(inputs, T, CTX, L)
    res = run_bass_kernel_spmd(nc, maps, core_ids=list(range(B)))
    return np.stack([np.asarray(r["out"], dtype=np.float32) for r in res.results], 0)
```

```python
import contextlib
import numpy as np
import concourse.bass as bass
import concourse.mybir as mybir
from concourse.bass_utils import run_bass_kernel_spmd

ACT = mybir.ActivationFunctionType
ALU = mybir.AluOpType
AX = mybir.AxisListType
F32 = mybir.dt.float32
BF16 = mybir.dt.bfloat16

D = 1024
DFF = 2816
NMOD = 9
RMS_EPS = 1e-6
STRICT = True


class PSem:
    def __init__(self, h):
        self.h = h
        self.val = 0


class Eng:
    def __init__(self, name, h, sem):
        self.name, self.h, self.sem = name, h, sem
        self.n = 0
        self.seen = {}
        self.dseen = {}


class Buf:
    def __init__(self, k, t, name):
        self.k, self.t, self.name = k, t, name
        self.w = None
        self.r = {}
        self.psem = None
        self.dma_w = 0
        self.dma_rw = 0

    def __getitem__(self, key):
        return self.t[key]

    @property
    def v(self):
        return _VA(self)


class View:
    def __init__(self, buf, ap):
        self.buf, self.ap = buf, ap

    def rr(self, pat, **kw):
        return View(self.buf, self.ap.rearrange(pat, **kw))

    def bc(self, shape):
        return View(self.buf, self.ap.to_broadcast(list(shape)))

    def __getitem__(self, key):
        return View(self.buf, self.ap[key])


class _VA:
    def __init__(self, b):
        self.b = b

    def __getitem__(self, key):
        return View(self.b, self.b.t[key])


def _bufs(*xs):
    out = []
    for x in xs:
        if isinstance(x, View) and x.buf not in out:
            out.append(x.buf)
    return out


def _ap(x):
    return x.ap if isinstance(x, View) else x


class K:
    def __init__(self, nc):
        self.nc = nc
        self.stack = contextlib.ExitStack()
        self.E = {}
        for name, h in (("pe", nc.tensor), ("act", nc.scalar), ("dve", nc.vector),
                        ("pool", nc.gpsimd), ("sp", nc.sync)):
            sem = self.stack.enter_context(nc.semaphore("s_" + name))
            self.E[name] = Eng(name, h, sem)
        self.free_psems = []
        self.n_psems = 0
        self.all_psems = []
        self.bufs = []
        self.uid = 0
        self.ninstr = 0
        self.marks = []

    def _psem(self):
        if self.free_psems:
            return self.free_psems.pop()
        self.n_psems += 1
        h = self.stack.enter_context(self.nc.semaphore("d%d" % self.n_psems))
        p = PSem(h)
        self.all_psems.append(p)
        return p

    def sb(self, st, shape, dtype, name):
        self.uid += 1
        t = st.enter_context(self.nc.sbuf_tensor("%s_%d" % (name, self.uid), list(shape), dtype))
        b = Buf(self, t, name)
        st.callback(self._release, b)
        return b

    def ps(self, st, shape, dtype, name):
        self.uid += 1
        t = st.enter_context(self.nc.psum_tensor("%s_%d" % (name, self.uid), list(shape), dtype))
        return Buf(self, t, name)

    def _release(self, b):
        if b.psem is not None and not getattr(b.psem, "sw", False):
            self.free_psems.append(b.psem)
        b.psem = None

    def _wait_eng(self, E, X, n):
        if E.seen.get(X.name, 0) >= n:
            return
        E.h.wait_ge(X.sem, n)
        E.seen[X.name] = n
        self.ninstr += 1

    def _wait_d(self, E, p, v):
        if v <= 0 or E.dseen.get(id(p), 0) >= v:
            return
        E.h.wait_ge(p.h, v)
        E.dseen[id(p)] = v
        self.ninstr += 1

    def _deps(self, E, r, w):
        for b in r:
            if b.w is not None:
                self._wait_eng(E, b.w[0], b.w[1])
            if b.psem is not None:
                self._wait_d(E, b.psem, b.dma_w)
        for b in w:
            if b.w is not None and (STRICT or b.w[0] is not E) and not (E.name == "pe" and b.w[0] is E):
                self._wait_eng(E, b.w[0], b.w[1])
            for X, n in b.r.items():
                if STRICT or X is not E:
                    self._wait_eng(E, X, n)
            if b.psem is not None:
                self._wait_d(E, b.psem, b.dma_rw)

    def op(self, eng, fn, r=(), w=()):
        E = self.E[eng]
        self._deps(E, r, w)
        ins = fn(E.h)
        E.n += 1
        ins.then_inc(E.sem, 1)
        self.ninstr += 1
        for b in w:
            b.w = (E, E.n)
            b.r = {}
        for b in r:
            if b.w is None or b.w[0] is not E or b.w[1] != E.n:
                b.r[E] = E.n
        return ins

    def dma(self, q, out, in_, r=(), w=(), **kw):
        Q = self.E[q]
        self._deps(Q, r, w)
        anchor = w[0] if w else r[0]
        if anchor.psem is None:
            if q == "pool":
                self.n_psems += 1
                anchor.psem = PSem(self.stack.enter_context(self.nc.semaphore("w%d" % self.n_psems)))
                anchor.psem.sw = True
                self.all_psems.append(anchor.psem)
            else:
                anchor.psem = self._psem()
        p = anchor.psem
        assert getattr(p, "sw", False) == (q == "pool"), "buffer mixes SW and HW DGE DMAs: " + anchor.name
        Q.h.dma_start(out=out, in_=in_, **kw).then_inc(p.h, 16)
        p.val += 16
        self.ninstr += 1
        anchor.dma_rw = p.val
        if w:
            anchor.dma_w = p.val
            anchor.w = None
            anchor.r = {}

    def mm(self, out, lhsT, rhs, start=True, stop=True):
        return self.op("pe", lambda e: e.matmul(out.ap, lhsT.ap, rhs.ap, start=start, stop=stop),
                       r=_bufs(lhsT, rhs), w=[out.buf])

    def tr(self, out, in_, ident):
        return self.op("pe", lambda e: e.transpose(out.ap, in_.ap, ident.ap), r=_bufs(in_, ident), w=[out.buf])

    def tt(self, eng, out, in0, in1, op):
        return self.op(eng, lambda e: e.tensor_tensor(out=out.ap, in0=in0.ap, in1=in1.ap, op=op),
                       r=_bufs(in0, in1), w=[out.buf])

    def ts(self, eng, out, in0, s1, s2, op0, op1=None):
        kw = {} if op1 is None else dict(op1=op1)
        return self.op(eng, lambda e: e.tensor_scalar(out=out.ap, in0=in0.ap, scalar1=_ap(s1), scalar2=_ap(s2),
                                                      op0=op0, **kw), r=_bufs(in0, s1, s2), w=[out.buf])

    def stt(self, eng, out, in0, scalar, in1, op0, op1):
        return self.op(eng, lambda e: e.scalar_tensor_tensor(out=out.ap, in0=in0.ap, scalar=_ap(scalar), in1=in1.ap,
                                                             op0=op0, op1=op1), r=_bufs(in0, scalar, in1), w=[out.buf])

    def act(self, out, in_, func, bias=None, scale=None, accum=None):
        kw = {}
        if bias is not None:
            kw["bias"] = _ap(bias)
        if scale is not None:
            kw["scale"] = _ap(scale)
        w = [out.buf]
        if accum is not None:
            kw["accum_out"] = accum.ap
            w.append(accum.buf)
        return self.op("act", lambda e: e.activation(out=out.ap, in_=in_.ap, func=func, **kw),
                       r=_bufs(in_, bias, scale), w=w)

    def cp(self, eng, out, in_):
        if eng == "act":
            return self.op("act", lambda e: e.copy(out=out.ap, in_=in_.ap), r=[in_.buf], w=[out.buf])
        return self.op(eng, lambda e: e.tensor_copy(out=out.ap, in_=in_.ap), r=[in_.buf], w=[out.buf])

    def memset(self, eng, out, val):
        return self.op(eng, lambda e: e.memset(out.ap, val), w=[out.buf])

    def recip(self, out, in_):
        return self.op("dve", lambda e: e.reciprocal(out=out.ap, in_=in_.ap), r=[in_.buf], w=[out.buf])

    def ld(self, q, out, dram, **kw):
        return self.dma(q, out.ap, dram, w=[out.buf], **kw)

    def stv(self, q, dram, in_, **kw):
        return self.dma(q, dram, in_.ap, r=[in_.buf], **kw)

    def barrier(self):
        for E in self.E.values():
            for X in self.E.values():
                if X is not E and X.n > 0:
                    self._wait_eng(E, X, X.n)
            for p in self.all_psems:
                self._wait_d(E, p, p.val)


def mm(k, out_ap, lhsT_ap, rhs_ap, start, stop, r, w):
    return k.op("pe", lambda e: e.matmul(out_ap, lhsT_ap, rhs_ap, start=start, stop=stop), r=r, w=w)


def rowphase(k, st, cfg):
    nc = k.nc
    ffn = cfg["ffn"]
    pre = cfg.get("pre")
    post = cfg.get("post")
    w1 = k.sb(st, [128, 8, 2 * DFF], BF16, "w1")
    w2 = k.sb(st, [128, 22, D], BF16, "w2")
    w1v = ffn["w_in"].rearrange("(c p) f -> p c f", p=128)
    for c in range(8):
        for hlf in range(4):
            f0 = hlf * 1408
            k.dma("pool", w1[:, c, f0:f0 + 1408], w1v[:, c, f0:f0 + 1408], w=[w1])
    w2v = ffn["w_out"].rearrange("(c p) d -> p c d", p=128)
    for c in range(22):
        k.dma("pool", w2[:, c, :], w2v[:, c, :], w=[w2])
    if pre is not None:
        wo = k.sb(st, [128, 8, D], BF16, "wo")
        wov = pre["w_out"].rearrange("(c p) d -> p c d", p=128)
        for c in range(8):
            k.dma("pool", wo[:, c, :], wov[:, c, :], w=[wo])

    identf = k.sb(st, [128, 128], F32, "identf")
    k.dma("sp", identf[:, :], cfg["ident"], w=[identf])
    xt = [k.sb(st, [128, D], F32, "xt%d" % i) for i in range(4)]
    hT = k.sb(st, [128, 8, 512], BF16, "hT")
    gT = k.sb(st, [128, 22, 512], BF16, "gT")
    sg = k.sb(st, [128, 512], BF16, "sg")
    tmp = k.sb(st, [128, D], F32, "tmp")
    ssq = k.sb(st, [128, 4], F32, "ssq")
    rinv = k.sb(st, [128, 4], F32, "rinv")
    ps_u = [k.ps(st, [128, 512], F32, "psu%d" % i) for i in range(4)]
    ps_y = [k.ps(st, [128, 512], F32, "psy%d" % i) for i in range(2)]
    ps_t = k.ps(st, [128, D], F32, "pst")
    Ac = k.sb(st, [128, 8], F32, "Ac")
    Bc = k.sb(st, [128, 8], F32, "Bc")
    Gc = k.sb(st, [128, 8], F32, "Gc")
    A2c = k.sb(st, [128, 8], F32, "A2c")
    B2c = k.sb(st, [128, 8], F32, "B2c")
    G = k.sb(st, [128, D], F32, "G")
    PG = k.sb(st, [128, D], F32, "PG") if pre is not None else None
    gvec2 = None
    if post is not None and post["kind"] == "final":
        gvec2 = k.sb(st, [128, D], F32, "gvec2")
        k.dma("sp", gvec2[:, :], post["g"].partition_broadcast(128), w=[gvec2])

    def col(ap1d):
        return ap1d.rearrange("(c p) -> p c", p=128)

    def load_cols(dst, ap1d):
        k.dma("sp", dst[:, :], col(ap1d), w=[dst], allow_slow_non_contiguous=True)

    def setup_mod(mr):
        load_cols(Gc, ffn["g"])
        load_cols(Ac, cfg["mod"][mr, ffn["scale"] * D:(ffn["scale"] + 1) * D])
        load_cols(Bc, cfg["mod"][mr, ffn["shift"] * D:(ffn["shift"] + 1) * D])
        k.op("dve", lambda e: e.scalar_tensor_tensor(out=Ac[:, :], in0=Ac[:, :], scalar=1.0, in1=Gc[:, :],
                                                     op0=ALU.add, op1=ALU.mult), r=[Ac, Gc], w=[Ac])
        if post is not None and post["kind"] == "hT":
            load_cols(Gc, post["g"])
            load_cols(A2c, cfg["mod"][mr, post["scale"] * D:(post["scale"] + 1) * D])
            load_cols(B2c, cfg["mod"][mr, post["shift"] * D:(post["shift"] + 1) * D])
            k.op("dve", lambda e: e.scalar_tensor_tensor(out=A2c[:, :], in0=A2c[:, :], scalar=1.0, in1=Gc[:, :],
                                                         op0=ALU.add, op1=ALU.mult), r=[A2c, Gc], w=[A2c])
        k.dma("sp", G[:, :], cfg["mod"][mr, ffn["gate"] * D:(ffn["gate"] + 1) * D].partition_broadcast(128), w=[G])
        k.op("dve", lambda e: e.tensor_scalar(out=G[:, :], in0=G[:, :], scalar1=0.5, scalar2=None, op0=ALU.mult),
             r=[G], w=[G])
        if pre is not None:
            k.dma("sp", PG[:, :], cfg["mod"][mr, pre["gate"] * D:(pre["gate"] + 1) * D].partition_broadcast(128),
                  w=[PG])

    try:
        tmp2 = [tmp, k.sb(st, [128, D], F32, "tmpb")]
    except AssertionError:
        tmp2 = [tmp, tmp]

    def rms_all(xs):
        m_ = len(xs)
        junk = gT[:, 0:2, :]
        for s_, xb in enumerate(xs):
            k.op("act", lambda e: e.activation(out=junk, in_=xb[:, :].rearrange("p (a b) -> p a b", a=2),
                                               func=ACT.Square, accum_out=ssq[:, s_:s_ + 1]), r=[xb], w=[gT, ssq])
        k.op("dve", lambda e: e.tensor_scalar(out=rinv[:, 0:m_], in0=ssq[:, 0:m_], scalar1=1.0 / D, scalar2=RMS_EPS,
                                              op0=ALU.mult, op1=ALU.add), r=[ssq], w=[rinv])
        k.op("act", lambda e: e.activation(out=rinv[:, 0:m_], in_=rinv[:, 0:m_], func=ACT.Sqrt), r=[rinv], w=[rinv])
        k.op("dve", lambda e: e.reciprocal(out=rinv[:, 0:m_], in_=rinv[:, 0:m_]), r=[rinv], w=[rinv])

    def norm_mod_T_all(xs, A, B, dstT):
        rms_all(xs)
        for s, xb in enumerate(xs):
            tm = tmp2[s % 2]
            k.op("dve", lambda e: e.tensor_scalar(out=tm[:, :], in0=xb[:, :], scalar1=rinv[:, s:s + 1], scalar2=None,
                                                  op0=ALU.mult), r=[xb, rinv], w=[tm])
            for c in range(8):
                k.op("pe", lambda e: e.transpose(ps_t[:, c * 128:(c + 1) * 128], tm[:, c * 128:(c + 1) * 128],
                                                 identf[:, :]), r=[tm, identf], w=[ps_t])
            for c in range(8):
                if c % 2 == 0:
                    k.op("dve", lambda e: e.tensor_scalar(out=dstT[:, c, s * 128:(s + 1) * 128],
                                                          in0=ps_t[:, c * 128:(c + 1) * 128], scalar1=A[:, c:c + 1],
                                                          scalar2=B[:, c:c + 1], op0=ALU.mult, op1=ALU.add),
                         r=[ps_t, A, B], w=[dstT])
                else:
                    k.op("act", lambda e: e.activation(out=dstT[:, c, s * 128:(s + 1) * 128],
                                                       in_=ps_t[:, c * 128:(c + 1) * 128], func=ACT.Identity,
                                                       scale=A[:, c:c + 1], bias=B[:, c:c + 1]),
                         r=[ps_t, A, B], w=[dstT])

    ui = 0
    yi = 0
    for (tok0, ntok, mr) in cfg["segs"]:
        setup_mod(mr)
        for t0 in range(tok0, tok0 + ntok, 512):
            n = min(512, tok0 + ntok - t0)
            ns = n // 128
            for s in range(ns):
                k.dma("sp", xt[s][:, :], cfg["xin"][t0 + s * 128:t0 + (s + 1) * 128, :], w=[xt[s]])
            if pre is not None:
                for s in range(ns):
                    k.dma("sp", tmp[:, :], pre["o"][t0 + s * 128:t0 + (s + 1) * 128, :], w=[tmp])
                    for c in range(8):
                        k.op("pe", lambda e: e.transpose(ps_t[:, c * 128:(c + 1) * 128], tmp[:, c * 128:(c + 1) * 128],
                                                         identf[:, :]), r=[tmp, identf], w=[ps_t])
                    k.op("act", lambda e: e.copy(out=hT[:, :, s * 128:(s + 1) * 128],
                                                 in_=ps_t[:, :].rearrange("p (c t) -> p c t", c=8)), r=[ps_t], w=[hT])
                for s in range(ns):
                    for hf in range(2):
                        py = ps_y[yi % 2]
                        yi += 1
                        for c in range(8):
                            mm(k, py[:, :], hT[:, c, s * 128:(s + 1) * 128], wo[:, c, hf * 512:(hf + 1) * 512],
                               c == 0, c == 7, r=[hT, wo], w=[py])
                        sl = slice(hf * 512, (hf + 1) * 512)
                        k.op("dve", lambda e, py=py, sl=sl: e.tensor_tensor(out=tmp[:, sl], in0=py[:, :],
                                                                             in1=PG[:, sl], op=ALU.mult),
                             r=[py, PG], w=[tmp])
                        k.op("dve", lambda e, s=s, sl=sl: e.tensor_tensor(out=xt[s][:, sl], in0=xt[s][:, sl],
                                                                            in1=tmp[:, sl], op=ALU.add),
                             r=[xt[s], tmp], w=[xt[s]])
            norm_mod_T_all(xt[0:ns], Ac, Bc, hT)
            for fc in range(22 if cfg.get("stage", 9) >= 2 else 0):
                p1 = ps_u[ui % 4]
                p2 = ps_u[(ui + 1) % 4]
                ui += 2
                for c in range(8):
                    mm(k, p1[:, 0:n], w1[:, c, fc * 128:(fc + 1) * 128], hT[:, c, 0:n], c == 0, c == 7,
                       r=[w1, hT], w=[p1])
                for c in range(8):
                    mm(k, p2[:, 0:n], w1[:, c, DFF + fc * 128:DFF + (fc + 1) * 128], hT[:, c, 0:n], c == 0, c == 7,
                       r=[w1, hT], w=[p2])
                k.op("act", lambda e, p1=p1: e.activation(out=sg[:, 0:n], in_=p1[:, 0:n], func=ACT.Silu),
                     r=[p1], w=[sg])
                k.op("dve", lambda e, p2=p2, fc=fc: e.tensor_tensor(out=gT[:, fc, 0:n], in0=sg[:, 0:n],
                                                                     in1=p2[:, 0:n], op=ALU.mult),
                     r=[sg, p2], w=[gT])
            for s in range(ns if cfg.get("stage", 9) >= 3 else 0):
                for hf in range(2):
                    py = ps_y[yi % 2]
                    yi += 1
                    for fc in range(22):
                        mm(k, py[:, :], gT[:, fc, s * 128:(s + 1) * 128], w2[:, fc, hf * 512:(hf + 1) * 512],
                           fc == 0, fc == 21, r=[gT, w2], w=[py])
                    sl = slice(hf * 512, (hf + 1) * 512)
                    k.op("dve", lambda e, py=py, sl=sl: e.tensor_tensor(out=tmp[:, sl], in0=py[:, :], in1=G[:, sl],
                                                                         op=ALU.mult), r=[py, G], w=[tmp])
                    k.op("dve", lambda e, s=s, sl=sl: e.tensor_tensor(out=xt[s][:, sl], in0=xt[s][:, sl],
                                                                        in1=tmp[:, sl], op=ALU.add),
                         r=[xt[s], tmp], w=[xt[s]])
            if post is not None and post["kind"] == "final":
                if mr == 0:
                    rms_all(xt[0:ns])
                    for s in range(ns):
                        r0 = t0 + s * 128
                        tm = tmp2[s % 2]
                        k.op("dve", lambda e: e.scalar_tensor_tensor(out=tm[:, :], in0=xt[s][:, :],
                                                                     scalar=rinv[:, s:s + 1], in1=gvec2[:, :],
                                                                     op0=ALU.mult, op1=ALU.mult),
                             r=[xt[s], rinv, gvec2], w=[tm])
                        k.dma("sp", post["out"][r0:r0 + 128, :], tm[:, :], r=[tm])
                continue
            for s in range(ns):
                r0 = t0 + s * 128
                k.dma("sp", cfg["xout"][r0:r0 + 128, :], xt[s][:, :], r=[xt[s]])
            if post is not None and post["kind"] == "hT":
                norm_mod_T_all(xt[0:ns], A2c, B2c, hT)
            if post is not None and post["kind"] == "hT":
                for c in range(8):
                    k.dma("sp", post["hT"][c * 128:(c + 1) * 128, t0:t0 + n], hT[:, c, 0:n], r=[hT])


CH = 64
DECAY_SCALE = 0.6065306597126334


def projphase(k, st, cfg):
    S = cfg["S"]
    C = cfg["C"]
    Wt = cfg["W"]
    T, CTX = cfg["T"], cfg["CTX"]
    hTd = S["hT"]
    wabc = k.sb(st, [128, 8, 2304], BF16, "wabc")
    wgd = k.sb(st, [128, 8, 128], BF16, "wgd")
    wsw = k.sb(st, [128, 8, 1280], BF16, "wsw")
    wv = Wt["w_in"].rearrange("(c p) f -> p c f", p=128)
    for c in range(8):
        k.ld("pool", wabc.v[:, c, 0:1152], wv[:, c, 0:1152])
        k.ld("pool", wabc.v[:, c, 1152:2304], wv[:, c, 1152:2304])
        k.ld("pool", wgd.v[:, c, :], wv[:, c, 3072:3200])
        k.ld("pool", wsw.v[:, c, :], Wt["w_sw"].rearrange("(c p) f -> p c f", p=128)[:, c, :])
    wda = [k.sb(st, [128, 8, 896], BF16, "wda%d" % d) for d in range(2)]
    wdb = [k.sb(st, [128, 8, 896], BF16, "wdb%d" % d) for d in range(2)]
    with contextlib.ExitStack() as st2:
        wst = k.sb(st2, [128, 8, 896], F32, "wst")
        mub = k.sb(st2, [128, 896], F32, "mub")
        wtm = k.sb(st2, [128, 8, 896], F32, "wtm")
        for d in range(2):
            for c in range(8):
                k.ld("sp", wst.v[:, c, :], Wt["w_d"][d].rearrange("(c p) f -> p c f", p=128)[:, c, :])
            k.ld("sp", mub.v[:, :], Wt["mu"][d].partition_broadcast(128))
            for c in range(8):
                k.tt("dve", wtm.v[:, c, :], wst.v[:, c, :], mub.v[:, :], ALU.mult)
            k.cp("act", wda[d].v[:, :, :], wtm.v[:, :, :])
            k.tt("dve", wdb[d].v[:, :, :], wst.v[:, :, :], wtm.v[:, :, :], ALU.subtract)
        k.barrier()
    pc = k.sb(st, [128, 24], F32, "pcols")
    k.ld("sp", pc.v[:, :], Wt["pcols"])
    oma = k.sb(st, [128, 2], F32, "oma")
    k.ts("dve", oma.v[:, :], pc.v[:, 14:16], -1.0, 1.0, ALU.mult, ALU.add)
    identf = k.sb(st, [128, 128], F32, "identf")
    k.ld("sp", identf.v[:, :], C["ident"])
    bd = k.sb(st, [128, 128], F32, "bd")
    k.ld("sp", bd.v[:, :], C["bd64"])
    e2 = k.sb(st, [128, 2], F32, "e2")
    k.ld("sp", e2.v[:, :], C["e2"])
    rtab = k.sb(st, [128, 2, 2, 3, CH], F32, "rtab")
    k.ld("sp", rtab.v[:, :, :, :, :], C["ret_tab"])
    w2s = k.sb(st, [64, 2, 256], BF16, "w2s")
    a2s = k.sb(st, [64, 2, 256], BF16, "a2s")
    g2s = k.sb(st, [128, 256], BF16, "g2s")
    for d in range(2):
        k.ld("pool", w2s.v[:, d, :], Wt["w2"][d])
        k.ld("pool", a2s.v[:, d, :], Wt["a2"][d])
    k.ld("pool", g2s.v[:, :], Wt["g2"])

    NT = 512
    hTt = k.sb(st, [128, 8, NT + 2], BF16, "hTt")
    ca = k.sb(st, [128, NT], F32, "ca")
    sa = k.sb(st, [128, NT], F32, "sa")
    cs_ = k.sb(st, [128, NT], F32, "cs")
    ss_ = k.sb(st, [128, NT], F32, "ss")
    gca = k.sb(st, [128, 2, NT], F32, "gca")
    gsa = k.sb(st, [128, 2, NT], F32, "gsa")
    F = [k.sb(st, [128, NT], F32, "f%d" % i) for i in range(12)]
    ob = k.sb(st, [128, NT], BF16, "ob")
    sbf = [k.sb(st, [128, NT], BF16, "sbf%d" % i) for i in range(2)]
    sbi = [0]

    def st_bf(dst, srcv, n):
        b = sbf[sbi[0] % 2]
        sbi[0] += 1
        k.cp("act", b.v[:, 0:n], srcv)
        k.stv("sp", dst, b.v[:, 0:n])
    tb = k.sb(st, [128, 512], F32, "tb")
    tbb = k.sb(st, [128, 512], BF16, "tbb")
    thb = k.sb(st, [64, NT], BF16, "thb")
    alb = k.sb(st, [64, NT], BF16, "alb")
    sgb = k.sb(st, [128, NT], BF16, "sgb")
    bs4 = k.sb(st, [128, 4, 4], F32, "bs4")
    pcx = k.sb(st, [128, NT // CH], F32, "pcx")
    pcx2 = k.sb(st, [128, NT // CH], F32, "pcx2")
    F2 = [k.sb(st, [128, NT], F32, "g%d" % i) for i in range(9)]
    PS = [k.ps(st, [128, 512], F32, "pp%d" % i) for i in range(8)]
    pi = [0]

    def nps():
        p = PS[pi[0] % 8]
        pi[0] += 1
        return p

    def chain(p, wlist, M, n, col0=1):
        tot = len(wlist) * 8
        i = 0
        for (wb, c0, sh) in wlist:
            for c in range(8):
                k.mm(p.v[0:M, 0:n], wb.v[:, c, c0:c0 + M], hTt.v[:, c, col0 + sh:col0 + sh + n], i == 0, i == tot - 1)
                i += 1

    def chain_tok(p, wlist, ncols, s):
        tot = len(wlist) * 8
        i = 0
        for (wb, c0, sh) in wlist:
            for c in range(8):
                k.mm(p.v[:, 0:ncols], hTt.v[:, c, 1 + sh + s * 128:1 + sh + (s + 1) * 128], wb.v[:, c, c0:c0 + ncols],
                     i == 0, i == tot - 1)
                i += 1

    def rope_out(P, Psw, ctab, stab, dst, n):
        k.tt("dve", F[10].v[:, 0:n], P.v[:, 0:n], ctab, ALU.mult)
        k.tt("dve", F[11].v[:, 0:n], Psw.v[:, 0:n], stab, ALU.mult)
        k.tt("dve", dst, F[10].v[:, 0:n], F[11].v[:, 0:n], ALU.add)

    segs = [(0, T, True), (T, CTX, False)]
    for (seg0, seglen, latent) in segs:
        for t0 in range(seg0, seg0 + seglen, NT):
            n = min(NT, seg0 + seglen - t0)
            ns = n // 128
            nch = n // CH
            lo = t0 - 1 if t0 > seg0 else t0
            hi = t0 + n + 1 if t0 + n < seg0 + seglen else t0 + n
            if lo == t0:
                k.memset("dve", hTt.v[:, :, 0:1], 0.0)
            if hi == t0 + n:
                k.memset("dve", hTt.v[:, :, n + 1:n + 2], 0.0)
            for c in range(8):
                k.ld("sp", hTt.v[:, c, 1 - (t0 - lo):1 + n + (hi - t0 - n)], hTd[c * 128:(c + 1) * 128, lo:hi])
            if latent:
                k.ld("sp", ca.v[:, 0:n], C["ropeA_c"][:, t0:t0 + n])
                k.ld("sp", sa.v[:, 0:n], C["ropeA_s"][:, t0:t0 + n])
                k.ld("sp", cs_.v[:, 0:n], C["ropeS_c"][:, t0:t0 + n])
                k.ld("sp", ss_.v[:, 0:n], C["ropeS_s"][:, t0:t0 + n])
                for qk in range(2):
                    k.ts("dve", gca.v[:, qk, 0:n], ca.v[:, 0:n], pc.v[:, 2 * qk:2 * qk + 1], None, ALU.mult)
                    k.ts("dve", gsa.v[:, qk, 0:n], sa.v[:, 0:n], pc.v[:, 2 * qk + 1:2 * qk + 2], None, ALU.mult)
            for grp, base, swb, dq, dk_, dv_ in (("A", 0, 0, S["QA"], S["KA"], S["VA"]),
                                                ("B", 512, 384, S["QB"], S["KB"], S["VB"])):
                for j in range(3):
                    P, Psw = nps(), nps()
                    chain(P, [(wabc, base + j * 128, 0)], 128, n)
                    if latent:
                        chain(Psw, [(wsw, swb + j * 128, 0)], 128, n)
                    qk = 0 if j < 2 else 1
                    if grp == "B":
                        k.act(F[0].v[:, 0:n], P.v[:, 0:n], ACT.Square)
                        pss = nps()
                        k.mm(pss.v[:, 0:n], bd.v[:, :], F[0].v[:, 0:n])
                        k.ts("dve", F[1].v[:, 0:n], pss.v[:, 0:n], 1.0 / 64, RMS_EPS, ALU.mult, ALU.add)
                        k.act(F[1].v[:, 0:n], F[1].v[:, 0:n], ACT.Sqrt)
                        k.recip(F[1].v[:, 0:n], F[1].v[:, 0:n])
                        k.tt("dve", F[2].v[:, 0:n], P.v[:, 0:n], F[1].v[:, 0:n], ALU.mult)
                        if latent:
                            k.tt("dve", F[3].v[:, 0:n], Psw.v[:, 0:n], F[1].v[:, 0:n], ALU.mult)
                            rope_out(F[2], F[3], gca.v[:, qk, 0:n], gsa.v[:, qk, 0:n], ob.v[:, 0:n], n)
                        else:
                            k.ts("dve", ob.v[:, 0:n], F[2].v[:, 0:n], pc.v[:, 2 * qk:2 * qk + 1], None, ALU.mult)
                    else:
                        if latent:
                            rope_out(P, Psw, ca.v[:, 0:n], sa.v[:, 0:n], ob.v[:, 0:n], n)
                        else:
                            k.cp("act", ob.v[:, 0:n], P.v[:, 0:n])
                    dst = dq[j * 128:(j + 1) * 128, t0:t0 + n] if j < 2 else dk_[:, t0:t0 + n]
                    k.stv("sp", dst, ob.v[:, 0:n])
                for s in range(ns):
                    P = nps()
                    chain_tok(P, [(wabc, base + 384, 0)], 128, s)
                    k.cp("act", tbb.v[:, 0:128], P.v[:, 0:128])
                    k.stv("sp", dv_[t0 + s * 128:t0 + (s + 1) * 128, :], tbb.v[:, 0:128])
            for j in range(4):
                P, Psw = nps(), nps()
                chain(P, [(wabc, 1024 + j * 128, 0)], 128, n)
                g = j % 2
                if latent:
                    chain(Psw, [(wsw, 768 + j * 128, 0)], 128, n)
                    rope_out(P, Psw, cs_.v[:, 0:n], ss_.v[:, 0:n], F[0].v[:, 0:n], n)
                else:
                    k.cp("act", F[0].v[:, 0:n], P.v[:, 0:n])
                x3 = F[0].v[:, 0:n].rr("p (c i) -> p c i", i=CH)
                for d in range(2):
                    if j < 2:
                        k.tt("dve", F[1].v[:, 0:n].rr("p (c i) -> p c i", i=CH), x3,
                             rtab.v[:, g, d, 0:1, :].bc([128, nch, CH]), ALU.mult)
                        st_bf(S["C_Rt"][d, g * 128:(g + 1) * 128, t0:t0 + n], F[1].v[:, 0:n], n)
                    else:
                        k.tt("dve", F[1].v[:, 0:n].rr("p (c i) -> p c i", i=CH), x3,
                             rtab.v[:, g, d, 1:2, :].bc([128, nch, CH]), ALU.mult)
                        st_bf(S["C_Kt"][d, g * 128:(g + 1) * 128, t0:t0 + n], F[1].v[:, 0:n], n)
                        k.tt("dve", F[2].v[:, 0:n].rr("p (c i) -> p c i", i=CH), x3,
                             rtab.v[:, g, d, 2:3, :].bc([128, nch, CH]), ALU.mult)
                        for s in range(ns):
                            pt = nps()
                            k.tr(pt.v[:, 0:128], F[2].v[:, s * 128:(s + 1) * 128], identf.v[:, :])
                            k.cp("act", tbb.v[:, 0:128], pt.v[:, 0:128])
                            k.stv("sp", S["C_Kh"][d, t0 + s * 128:t0 + (s + 1) * 128, g * 128:(g + 1) * 128],
                                  tbb.v[:, 0:128])
            for s in range(ns):
                P = nps()
                chain_tok(P, [(wabc, 1536, 0)], 256, s)
                k.cp("act", tbb.v[:, 0:256], P.v[:, 0:256])
                k.stv("sp", S["C_V"][t0 + s * 128:t0 + (s + 1) * 128, :], tbb.v[:, 0:256])
                P = nps()
                chain_tok(P, [(wabc, 1792, 0)], 512, s)
                k.act(tb.v[:, 0:512], P.v[:, 0:512], ACT.Silu)
                k.stv("sp", S["C_G"][t0 + s * 128:t0 + (s + 1) * 128, :], tb.v[:, 0:512])
            P = nps()
            chain(P, [(wgd, 0, 0)], 128, n)
            k.act(sgb.v[:, 0:n], P.v[:, 0:n], ACT.Sigmoid)
            for s in range(ns):
                P = nps()
                k.mm(P.v[:, 0:256], sgb.v[:, s * 128:(s + 1) * 128], g2s.v[:, :])
                k.cp("act", tb.v[:, 0:256], P.v[:, 0:256])
                k.stv("sp", S["D_G"][t0 + s * 128:t0 + (s + 1) * 128, :], tb.v[:, 0:256])
            for d in range(2):
                sh = -1 if d == 0 else 1
                wl = lambda c0: [(wdb[d], c0, 0), (wda[d], c0, sh)]
                P = nps()
                chain(P, wl(768), 64, n)
                k.act(thb.v[:, 0:n], P.v[0:64, 0:n], ACT.Tanh)
                P = nps()
                chain(P, wl(832), 64, n)
                k.cp("act", alb.v[:, 0:n], P.v[0:64, 0:n])
                for s in range(ns):
                    P = nps()
                    chain_tok(P, wl(512), 256, s)
                    k.cp("act", tbb.v[:, 0:256], P.v[:, 0:256])
                    k.stv("sp", S["D_V"][d, t0 + s * 128:t0 + (s + 1) * 128, :], tbb.v[:, 0:256])
                def dgroup(g, FB, pcx):
                    r_, k_, lw, cs, a_, kk, e_, t1, t2 = FB
                    gc = slice(g, g + 1)
                    P = nps()
                    chain(P, wl(g * 128), 128, n)
                    k.cp("act", r_.v[:, 0:n], P.v[:, 0:n])
                    P = nps()
                    chain(P, wl(256 + g * 128), 128, n)
                    k.cp("act", k_.v[:, 0:n], P.v[:, 0:n])
                    yield
                    P = nps()
                    k.mm(P.v[:, 0:n], w2s.v[:, d, g * 128:(g + 1) * 128], thb.v[:, 0:n])
                    k.act(lw.v[:, 0:n], P.v[:, 0:n], ACT.Sigmoid, bias=pc.v[:, 4 + 2 * d + g:5 + 2 * d + g])
                    k.ts("dve", lw.v[:, 0:n], lw.v[:, 0:n], -DECAY_SCALE, None, ALU.mult)
                    yield
                    P = nps()
                    k.mm(P.v[:, 0:n], a2s.v[:, d, g * 128:(g + 1) * 128], alb.v[:, 0:n])
                    k.act(a_.v[:, 0:n], P.v[:, 0:n], ACT.Sigmoid, bias=pc.v[:, 8 + 2 * d + g:9 + 2 * d + g])
                    yield
                    k.ts("dve", kk.v[:, 0:n], k_.v[:, 0:n], pc.v[:, 12 + g:13 + g], None, ALU.mult)
                    k.act(t1.v[:, 0:n], kk.v[:, 0:n], ACT.Square)
                    yield
                    P = nps()
                    k.mm(P.v[:, 0:n], bd.v[:, :], t1.v[:, 0:n])
                    k.act(t1.v[:, 0:n], P.v[:, 0:n], ACT.Sqrt)
                    yield
                    k.ts("dve", t1.v[:, 0:n], t1.v[:, 0:n], 1e-12, None, ALU.max)
                    k.recip(t1.v[:, 0:n], t1.v[:, 0:n])
                    yield
                    k.tt("dve", kk.v[:, 0:n], kk.v[:, 0:n], t1.v[:, 0:n], ALU.mult)
                    yield
                    k.ts("dve", t1.v[:, 0:n], a_.v[:, 0:n], pc.v[:, 14 + g:15 + g], oma.v[:, gc], ALU.mult, ALU.add)
                    k.tt("dve", k_.v[:, 0:n], k_.v[:, 0:n], t1.v[:, 0:n], ALU.mult)
                    yield
                    k.stt("dve", t1.v[:, 0:n], r_.v[:, 0:n], pc.v[:, 16 + 2 * d + g:17 + 2 * d + g], k_.v[:, 0:n],
                          ALU.mult, ALU.mult)
                    for s in range(ns):
                        P = nps()
                        k.mm(P.v[:, 0:2], t1.v[:, s * 128:(s + 1) * 128], e2.v[:, :])
                        k.cp("act", bs4.v[:, s, 2 * g:2 * g + 2], P.v[:, 0:2])
                    k.tt("dve", a_.v[:, 0:n], a_.v[:, 0:n], kk.v[:, 0:n], ALU.mult)
                    yield
                    src, dst = lw, cs
                    k.cp("act", t2.v[:, 0:n], lw.v[:, 0:n])
                    src = t2
                    for stp in (1, 2, 4, 8, 16, 32):
                        s3 = src.v[:, 0:n].rr("p (c i) -> p c i", i=CH)
                        d3 = dst.v[:, 0:n].rr("p (c i) -> p c i", i=CH)
                        if d == 0:
                            k.tt("dve", d3[:, :, stp:], s3[:, :, stp:], s3[:, :, :CH - stp], ALU.add)
                            k.cp("act", d3[:, :, :stp], s3[:, :, :stp])
                        else:
                            k.tt("dve", d3[:, :, :CH - stp], s3[:, :, :CH - stp], s3[:, :, stp:], ALU.add)
                            k.cp("act", d3[:, :, CH - stp:], s3[:, :, CH - stp:])
                        src, dst = dst, src
                        yield
                    cs = src
                    spare = dst
                    cs3 = cs.v[:, 0:n].rr("p (c i) -> p c i", i=CH)
                    tot3 = cs3[:, :, CH - 1:CH] if d == 0 else cs3[:, :, 0:1]
                    k.act(e_.v[:, 0:n], cs.v[:, 0:n], ACT.Exp)
                    yield
                    k.tt("dve", t1.v[:, 0:n], r_.v[:, 0:n], e_.v[:, 0:n], ALU.mult)
                    st_bf(S["D_Rt"][d, g * 128:(g + 1) * 128, t0:t0 + n], t1.v[:, 0:n], n)
                    yield
                    k.tt("dve", e_.v[:, 0:n], cs.v[:, 0:n], lw.v[:, 0:n], ALU.subtract)
                    k.act(e_.v[:, 0:n], e_.v[:, 0:n], ACT.Exp)
                    yield
                    k.stt("dve", t1.v[:, 0:n], kk.v[:, 0:n], -1.0, e_.v[:, 0:n], ALU.mult, ALU.mult)
                    st_bf(S["D_At"][d, g * 128:(g + 1) * 128, t0:t0 + n], t1.v[:, 0:n], n)
                    yield
                    for s in range(ns):
                        pt = nps()
                        k.tr(pt.v[:, 0:128], t1.v[:, s * 128:(s + 1) * 128], identf.v[:, :])
                        k.cp("act", tbb.v[:, 0:128], pt.v[:, 0:128])
                        k.stv("sp", S["D_Atok"][d, t0 + s * 128:t0 + (s + 1) * 128, g * 128:(g + 1) * 128],
                              tbb.v[:, 0:128])
                    k.act(e_.v[:, 0:n], cs.v[:, 0:n], ACT.Exp, scale=-1.0)
                    k.tt("dve", t1.v[:, 0:n], a_.v[:, 0:n], e_.v[:, 0:n], ALU.mult)
                    st_bf(S["D_Bt"][d, g * 128:(g + 1) * 128, t0:t0 + n], t1.v[:, 0:n], n)
                    yield
                    k.tt("dve", t1.v[:, 0:n], k_.v[:, 0:n], e_.v[:, 0:n], ALU.mult)
                    st_bf(S["D_Kt"][d, g * 128:(g + 1) * 128, t0:t0 + n], t1.v[:, 0:n], n)
                    yield
                    k.act(pcx.v[:, 0:nch], tot3.rr("p c o -> p (c o)"), ACT.Exp)
                    k.stv("sp", S["D_PC"][d, g * 128:(g + 1) * 128, t0 // CH:t0 // CH + nch], pcx.v[:, 0:nch])
                    k.tt("dve", e_.v[:, 0:n].rr("p (c i) -> p c i", i=CH), tot3.bc([128, nch, CH]), cs3, ALU.subtract)
                    k.act(e_.v[:, 0:n], e_.v[:, 0:n], ACT.Exp)
                    yield
                    for (srcb, dstd) in ((a_, S["D_Bh"]), (k_, S["D_Kh"])):
                        k.tt("dve", t1.v[:, 0:n], srcb.v[:, 0:n], e_.v[:, 0:n], ALU.mult)
                        for s in range(ns):
                            pt = nps()
                            k.tr(pt.v[:, 0:128], t1.v[:, s * 128:(s + 1) * 128], identf.v[:, :])
                            k.cp("act", tbb.v[:, 0:128], pt.v[:, 0:128])
                            k.stv("sp", dstd[d, t0 + s * 128:t0 + (s + 1) * 128, g * 128:(g + 1) * 128], tbb.v[:, 0:128])
                gens = [dgroup(0, F[0:9], pcx), dgroup(1, F2, pcx2)]
                live = list(gens)
                while live:
                    for g_ in list(live):
                        try:
                            next(g_)
                        except StopIteration:
                            live.remove(g_)
                for s in range(ns):
                    k.stv("sp", S["D_BS"][d, t0 + s * 128:t0 + (s + 1) * 128, :], bs4.v[:, s, :])


def attnphase(k, st, cfg):
    S, C, Wt = cfg["S"], cfg["C"], cfg["W"]
    T, CTX = cfg["T"], cfg["CTX"]
    TT = T + CTX
    NKC = TT // 128
    need_ctx = cfg["need_ctx"]
    kT = k.sb(st, [128, TT], BF16, "kT")
    va = k.sb(st, [128, NKC, 65], BF16, "va")
    qT = [k.sb(st, [128, 512], BF16, "qT%d" % i) for i in range(2)]
    for c0 in range(0, TT, 2048):
        k.memset("dve", kT.v[64:128, c0:min(TT, c0 + 2048)], 0.0)
    for q_ in qT:
        k.memset("dve", q_.v[64:128, :], 0.0)
    pT = [k.sb(st, [128, 512], BF16, "pT%d" % i) for i in range(3)]
    ot = [k.sb(st, [128, 64], F32, "ot%d" % i) for i in range(2)]
    rc = k.sb(st, [128, 1], F32, "rc")
    esk = k.sb(st, [128, 4], F32, "esk")
    mprev = k.sb(st, [128, 128], BF16, "mprev")
    mnext = k.sb(st, [128, 128], BF16, "mnext")
    k.ld("pool", mprev.v[:, :], C["mprev"])
    k.ld("pool", mnext.v[:, :], C["mnext"])
    k.ld("sp", esk.v[:, :], Wt["sink"].partition_broadcast(128))
    k.act(esk.v[:, :], esk.v[:, :], ACT.Exp)
    ps_s = [k.ps(st, [128, 512], F32, "pss%d" % i) for i in range(3)]
    ps_o = [k.ps(st, [128, 512], F32, "pso%d" % i) for i in range(2)]
    ps_f = k.ps(st, [128, 512], F32, "psf")
    osb = k.sb(st, [128, 512], F32, "osb")
    identf = k.sb(st, [128, 128], F32, "identf")
    k.ld("sp", identf.v[:, :], C["ident"])
    cnt = dict(s=0, q=0, o=0, p=0, a=0)
    scale = 64 ** -0.5

    def load_kv(kd, vd, g):
        k.ld("sp", kT.v[0:64, :], kd[g * 64:(g + 1) * 64, :])
        k.memset("dve", va.v[:, :, 64:65], 1.0)
        k.ld("sp", va.v[:, :, 0:64], vd[:, g * 64:(g + 1) * 64].rearrange("(c p) j -> p c j", p=128))

    def finish(po, nq_sub, t0, col, sinkcol):
        nq = nq_sub * 128
        k.cp("act", osb.v[0:65, 0:nq], po.v[0:65, 0:nq])
        for s in range(nq_sub):
            k.tr(ps_f.v[:, s * 65:(s + 1) * 65], osb.v[0:65, s * 128:(s + 1) * 128], identf.v[0:65, 0:65])
        for s in range(nq_sub):
            o = ot[cnt["o"] % 2]
            cnt["o"] += 1
            den = ps_f.v[:, s * 65 + 64:s * 65 + 65]
            if sinkcol is None:
                k.recip(rc.v[:, :], den)
            else:
                k.tt("dve", rc.v[:, :], den, esk.v[:, sinkcol:sinkcol + 1], ALU.add)
                k.recip(rc.v[:, :], rc.v[:, :])
            k.ts("dve", o.v[:, :], ps_f.v[:, s * 65:s * 65 + 64], rc.v[:, 0:1], None, ALU.mult)
            k.stv("sp", S["O"][t0 + s * 128:t0 + (s + 1) * 128, col:col + 64], o.v[:, :])

    def qk(q, nq, kc):
        ps = ps_s[cnt["s"] % 3]
        cnt["s"] += 1
        k.mm(ps.v[:, 0:nq], kT.v[:, kc * 128:(kc + 1) * 128], q.v[:, 0:nq])
        return ps

    def prologue(b):
        if b.get("kv") is not None:
            load_kv(*b["kv"])
        q = qT[cnt["q"] % 2]
        cnt["q"] += 1
        k.ld("sp", q.v[0:64, 0:b["nq"]], b["qd"][b["h"] * 64:(b["h"] + 1) * 64, b["t0"]:b["t0"] + b["nq"]])
        return q, qk(q, b["nq"], b["chunks"][0][0])

    def body(b, q, nxt):
        nq, chunks = b["nq"], b["chunks"]
        po = b["po"]
        assert len(chunks[0][1]) == nq // 128
        for i, (kc, subs) in enumerate(chunks):
            ps = nxt
            if i + 1 < len(chunks):
                nxt = qk(q, nq, chunks[i + 1][0])
            p = pT[cnt["p"] % 3]
            cnt["p"] += 1
            k.act(p.v[:, 0:nq], ps.v[:, 0:nq], ACT.Exp, scale=scale)
            for (s, m_) in subs:
                if m_ is not None:
                    k.tt("dve", p.v[:, s * 128:(s + 1) * 128], p.v[:, s * 128:(s + 1) * 128], m_.v[:, :], ALU.mult)
            c_lo = min(s for (s, _) in subs) * 128
            c_hi = (max(s for (s, _) in subs) + 1) * 128
            k.mm(po.v[0:65, c_lo:c_hi], va.v[:, kc, :], p.v[:, c_lo:c_hi], i == 0, i == len(chunks) - 1)

    blocks = []

    def block(qd, h, t0, nq, chunks, col, sinkcol, kv=None):
        blocks.append(dict(qd=qd, h=h, t0=t0, nq=nq, chunks=chunks, col=col, sinkcol=sinkcol, kv=kv,
                           po=ps_o[len(blocks) % 2]))

    for g in range(2):
        kv = (S["KB"], S["VB"], g)
        for h in (2 * g, 2 * g + 1):
            for t0 in range(0, T, 512):
                nq = min(512, T - t0)
                block(S["QB"], h, t0, nq, [(kc, [(s, None) for s in range(nq // 128)]) for kc in range(NKC)],
                      256 + h * 64, None, kv)
                kv = None
            if need_ctx:
                block(S["QB"], h, T, CTX, [(kc, [(s, None) for s in range(CTX // 128)]) for kc in range(T // 128, NKC)],
                      256 + h * 64, None)
    nb = T // 128
    cchunks = list(range(T // 128, NKC))
    for g in range(2):
        kv = (S["KA"], S["VA"], g)
        for h in (2 * g, 2 * g + 1):
            for b0 in range(0, nb, 4):
                nsb = min(4, nb - b0)
                chunks = [(kc, [(s, None) for s in range(nsb)]) for kc in cchunks]
                for kc in range(max(0, b0 - 1), min(nb, b0 + nsb + 1)):
                    subs = []
                    for s in range(nsb):
                        dlt = kc - (b0 + s)
                        if dlt == -1:
                            subs.append((s, mprev))
                        elif dlt == 0:
                            subs.append((s, None))
                        elif dlt == 1:
                            subs.append((s, mnext))
                    chunks.append((kc, subs))
                block(S["QA"], h, b0 * 128, nsb * 128, chunks, h * 64, h, kv)
                kv = None
            if need_ctx:
                block(S["QA"], h, T, CTX, [(kc, [(s, None) for s in range(CTX // 128)]) for kc in cchunks], h * 64, h)
    st_ = prologue(blocks[0])
    for i, b in enumerate(blocks):
        body(b, *st_)
        if i + 1 < len(blocks):
            st_ = prologue(blocks[i + 1])
        finish(b["po"], b["nq"] // 128, b["t0"], b["col"], b["sinkcol"])


def scanphase(k, st, cfg):
    S, C = cfg["S"], cfg["C"]
    T, CTX = cfg["T"], cfg["CTX"]
    TT = T + CTX
    dplr = cfg["dplr"]
    pre = "D_" if dplr else "C_"
    NCG = 2
    GT_ = NCG * CH
    NCHT = TT // CH
    f3 = lambda b: b.v[:, :, :]
    t64 = lambda nm: k.sb(st, [64, 8, 64], F32, nm)
    b64 = lambda nm: k.sb(st, [64, 8, 64], BF16, nm)
    chm = lambda nm: k.sb(st, [64, 8, GT_], BF16, nm)
    tkm = lambda nm: k.sb(st, [64, NCG, 8, 64], BF16, nm)
    RT = [chm("RT%d" % i) for i in range(2)]
    KT = [chm("KT%d" % i) for i in range(2)]
    KH = [tkm("KH%d" % i) for i in range(2)]
    VV = [tkm("VV%d" % i) for i in range(2)]
    Min = t64("Min")
    k.ld("sp", f3(Min), C["m_in"].rearrange("p (q t) -> p q t", q=8))
    id8 = t64("id8")
    k.ld("sp", f3(id8), C["id8"].rearrange("p (q t) -> p q t", q=8))
    if dplr:
        AT = [chm("AT%d" % i) for i in range(2)]
        BT = [chm("BT%d" % i) for i in range(2)]
        BH = [tkm("BH%d" % i) for i in range(2)]
        AK = [tkm("AK%d" % i) for i in range(2)]
        Mst, Mts = t64("Mst"), t64("Mts")
        k.ld("sp", f3(Mst), C["m_st"].rearrange("p (q t) -> p q t", q=8))
        k.ld("sp", f3(Mts), C["m_ts"].rearrange("p (q t) -> p q t", q=8))
        PCt = k.sb(st, [64, 8, NCHT], F32, "PCt")
        for d in range(2):
            k.ld("sp", PCt.v[:, d * 4:(d + 1) * 4, :], S["D_PC"][d].rearrange("(h j) c -> j h c", j=64))
    else:
        PCr = k.sb(st, [64, 8], F32, "PCr")
        k.ld("sp", PCr.v[:, :], C["ret_pc"])
        GTc = b64("GTc")
        k.tt("dve", f3(GTc), f3(id8), PCr.v[:, :].rr("p (q o) -> p q o", o=1).bc([64, 8, 64]), ALU.mult)
    sets = []
    for c in range(NCG):
        B = dict(ArkT=b64("ArkT%d" % c), Yv=t64("Yv%d" % c), H=t64("H%d" % c))
        if dplr:
            B.update(Ls=[b64("L%d_%d" % (i, c)) for i in range(5)], Ns=[b64("N%d_%d" % (i, c)) for i in range(6)],
                     AakT=b64("AakT%d" % c), ArbT=b64("ArbT%d" % c), Qe=b64("Qe%d" % c), GTt=b64("GTt%d" % c),
                     X=[k.sb(st, [64, 8, 128], F32, "X%d_%d" % (i, c)) for i in range(2)],
                     Xb=k.sb(st, [64, 8, 128], BF16, "Xb_%d" % c),
                     psX=k.ps(st, [64, 1024], F32, "psX%d" % c))
        sets.append(B)
    ST = [b64("ST%d" % i) for i in range(2)]
    Yt = [t64("Yt%d" % i) for i in range(2)]
    psA = [k.ps(st, [64, 512], F32, "psA%d" % i) for i in range(2)]
    psY = [k.ps(st, [64, 512], F32, "psY%d" % i) for i in range(2)]
    cnt = dict(a=0, y=0, e=0)
    p3 = lambda p: p.v[:, :].rr("p (q t) -> p q t", q=8)

    def npa():
        p = psA[cnt["a"] % 2]
        cnt["a"] += 1
        return p

    def ev():
        cnt["e"] += 1
        return "act" if cnt["e"] % 2 else "dve"

    def prod(dst, lhs, rhs, mask):
        p = npa()
        for q in range(8):
            k.mm(p3(p)[:, q, :], lhs(q), rhs(q))
        if mask is None:
            k.cp(ev(), f3(dst), p3(p))
        else:
            k.tt("dve", f3(dst), p3(p), f3(mask), ALU.mult)

    def views(rb, cl):
        sl = {d: slice(cl[d] * CH, (cl[d] + 1) * CH) for d in range(2)}
        dq = lambda q: q // 4
        chs = lambda buf: (lambda q: buf[rb].v[:, q, sl[dq(q)]])
        tks = lambda buf: (lambda q: buf[rb].v[:, cl[dq(q)], q, :])
        return sl, chs, tks

    def ppart(B, rb, cl, grp):
        sl, chs, tks = views(rb, cl)
        rt, kt, kh, vv = chs(RT), chs(KT), tks(KH), tks(VV)
        prod(B["ArkT"], kt, rt, Min)
        yield
        if dplr:
            at, bt, bh = chs(AT), chs(BT), tks(BH)
            Ls, Ns, AakT, ArbT = B["Ls"], B["Ns"], B["AakT"], B["ArbT"]
            prod(Ns[0], bt, at, Mst)
            prod(Ls[0], at, bt, Mts)
            yield
            prod(AakT, kt, at, Mst)
            prod(ArbT, bt, rt, Min)
            yield
            xc, xn = B["X"]
            xb = B["Xb"]
            for d in range(2):
                k.cp("act", xc.v[:, d * 4:(d + 1) * 4, 0:64], AK[rb].v[:, cl[d], d * 4:(d + 1) * 4, :])
            p = npa()
            for q in range(8):
                k.mm(p3(p)[:, q, :], AakT.v[:, q, :], vv(q))
            k.cp("dve", xc.v[:, :, 64:128], p3(p))
            k.cp("act", f3(xb), f3(xc))
            yield
            px = B["psX"].v[:, :].rr("p (q c) -> p q c", q=8)
            for lv in range(6):
                for q in range(8):
                    k.mm(px[:, q, :], Ns[lv].v[:, q, :], xb.v[:, q, :])
                if lv < 5:
                    if lv < 4:
                        prod(Ls[lv + 1], lambda q: Ns[lv].v[:, q, :], lambda q: Ls[lv].v[:, q, :], None)
                    prod(Ns[lv + 1], lambda q: Ls[lv].v[:, q, :], lambda q: Ns[lv].v[:, q, :], None)
                k.tt("dve", f3(xn), f3(xc), px, ALU.add)
                xc, xn = xn, xc
                k.cp("act", f3(xb), f3(xc))
                yield
            wq = lambda q: xb.v[:, q, 0:64]
            uv = lambda q: xb.v[:, q, 64:128]
            p = npa()
            for q in range(8):
                k.mm(p3(p)[:, q, :], wq(q), ArbT.v[:, q, :])
            for d in range(2):
                k.tt("dve", B["Qe"].v[:, d * 4:(d + 1) * 4, :], p3(p)[:, d * 4:(d + 1) * 4, :],
                     RT[rb].v[:, d * 4:(d + 1) * 4, sl[d]], ALU.add)
            p = npa()
            for q in range(8):
                k.mm(p3(p)[:, q, :], wq(q), bh(q))
            for d in range(2):
                c_abs = grp[d] * NCG + cl[d]
                k.tt("dve", B["GTt"].v[:, d * 4:(d + 1) * 4, :], id8.v[:, d * 4:(d + 1) * 4, :],
                     PCt.v[:, d * 4:(d + 1) * 4, c_abs:c_abs + 1].bc([64, 4, 64]), ALU.mult)
            k.tt("dve", f3(B["GTt"]), f3(B["GTt"]), p3(p), ALU.add)
            yield
        p = npa()
        for q in range(8):
            k.mm(p3(p)[:, q, :], B["ArkT"].v[:, q, :], vv(q), True, not dplr)
            if dplr:
                k.mm(p3(p)[:, q, :], ArbT.v[:, q, :], uv(q), False, True)
        k.cp(ev(), f3(B["Yv"]), p3(p))
        p = npa()
        for q in range(8):
            k.mm(p3(p)[:, q, :], kh(q), vv(q), True, not dplr)
            if dplr:
                k.mm(p3(p)[:, q, :], bh(q), uv(q), False, True)
        k.cp(ev(), f3(B["H"]), p3(p))
        yield

    k.memset("dve", f3(ST[0]), 0.0)
    ngl = T // GT_
    ncg_ctx = CTX // GT_
    fw = list(range(ngl, ngl + ncg_ctx)) + list(range(ngl))
    bw = list(range(ngl + ncg_ctx - 1, ngl - 1, -1)) + list(range(ngl - 1, -1, -1))
    order = {0: fw, 1: bw}
    step = 0
    for gi in range(ngl + ncg_ctx):
        rb = gi % 2
        grp = {d: order[d][gi] for d in range(2)}
        for d in range(2):
            t0 = grp[d] * GT_
            qs = slice(d * 4, (d + 1) * 4)
            chv = lambda nm: S[pre + nm][d, :, t0:t0 + GT_].rearrange("(h j) t -> j h t", j=64)
            tkv = lambda ap: ap[t0:t0 + GT_, :].rearrange("(n t) (h j) -> t n h j", t=CH, j=64)
            k.ld("sp", RT[rb].v[:, qs, :], chv("Rt"))
            k.ld("sp", KT[rb].v[:, qs, :], chv("Kt"))
            k.ld("sp", KH[rb].v[:, :, qs, :], tkv(S[pre + "Kh"][d]))
            k.ld("sp", VV[rb].v[:, :, qs, :], tkv(S["D_V"][d] if dplr else S["C_V"]))
            if dplr:
                k.ld("sp", AT[rb].v[:, qs, :], chv("At"))
                k.ld("sp", BT[rb].v[:, qs, :], chv("Bt"))
                k.ld("sp", BH[rb].v[:, :, qs, :], tkv(S["D_Bh"][d]))
                k.ld("sp", AK[rb].v[:, :, qs, :], tkv(S["D_Atok"][d]))
        cls = [{0: ci, 1: NCG - 1 - ci} for ci in range(NCG)]
        gens = [ppart(sets[ci], rb, cls[ci], grp) for ci in range(NCG)]
        live = list(gens)
        while live:
            for g_ in list(live):
                try:
                    next(g_)
                except StopIteration:
                    live.remove(g_)
        for ci in range(NCG):
            B, cl = sets[ci], cls[ci]
            sl, chs, tks = views(rb, cl)
            Sc, Sn = ST[step % 2], ST[(step + 1) % 2]
            qe = (lambda q: B["Qe"].v[:, q, :]) if dplr else chs(RT)
            gt = (lambda q: B["GTt"].v[:, q, :]) if dplr else (lambda q: GTc.v[:, q, :])
            py = psY[cnt["y"] % 2]
            cnt["y"] += 1
            for q in range(8):
                k.mm(p3(py)[:, q, :], qe(q), Sc.v[:, q, :])
            p = npa()
            for q in range(8):
                k.mm(p3(p)[:, q, :], gt(q), Sc.v[:, q, :])
            k.tt("dve", f3(Sn), p3(p), f3(B["H"]), ALU.add)
            yt = Yt[step % 2]
            k.tt("dve", f3(yt), p3(py), f3(B["Yv"]), ALU.add)
            for d in range(2):
                tt0 = grp[d] * GT_ + cl[d] * CH
                k.stv("sp", S[pre + "Y"][d, tt0:tt0 + CH, :].rearrange("t (h i) -> t h i", i=64),
                      yt.v[:, d * 4:(d + 1) * 4, :])
            step += 1


GN_EPS = 64e-5


def postphase(k, st, cfg):
    S, Wt = cfg["S"], cfg["W"]
    TT = cfg["T"] + cfg["CTX"]
    row = lambda nm, ap: (lambda t: (k.ld("sp", t.v[:, :], ap.partition_broadcast(128)), t)[1])(k.sb(st, [128, 256], F32, nm))
    retg = row("retg", Wt["ret_g"])
    lng = row("lng", Wt["ln_g"])
    lnb = row("lnb", Wt["ln_b"])
    gc = k.sb(st, [128, 512], F32, "gc")
    gd = k.sb(st, [128, 256], F32, "gd")
    vd = k.sb(st, [128, 4, 64], BF16, "vd")
    vdf = k.sb(st, [128, 4, 64], F32, "vdf")
    bs = k.sb(st, [128, 4], F32, "bs")

    def bufset(nm):
        return dict(y=[k.sb(st, [128, 4, 64], F32, nm + "y%d" % i) for i in range(2)],
                    yc=k.sb(st, [128, 4, 64], F32, nm + "yc"), sq=k.sb(st, [128, 4, 64], F32, nm + "sq"),
                    m4=k.sb(st, [128, 4], F32, nm + "m4"), v4=k.sb(st, [128, 4], F32, nm + "v4"),
                    acc=k.sb(st, [128, 256], F32, nm + "acc"), yn=k.sb(st, [128, 256], F32, nm + "yn"), i=0)

    BC, BD = bufset("c"), bufset("d")

    def head_norm(B, yb, g, b, dst):
        yc, sq, m4, v4 = B["yc"], B["sq"], B["m4"], B["v4"]
        k.op("dve", lambda e: e.tensor_reduce(out=m4[:, :], in_=yb[:, :, :], axis=AX.X, op=ALU.add), r=[yb], w=[m4])
        k.ts("dve", m4.v[:, :], m4.v[:, :], 1.0 / 64, None, ALU.mult)
        yield
        k.tt("dve", yc.v[:, :, :], yb.v[:, :, :], m4.v[:, :].rr("p (h o) -> p h o", o=1).bc([128, 4, 64]), ALU.subtract)
        yield
        k.tt("dve", sq.v[:, :, :], yc.v[:, :, :], yc.v[:, :, :], ALU.mult)
        yield
        k.op("dve", lambda e: e.tensor_reduce(out=v4[:, :], in_=sq[:, :, :], axis=AX.X, op=ALU.add), r=[sq], w=[v4])
        yield
        k.ts("dve", v4.v[:, :], v4.v[:, :], 1.0 / 64, GN_EPS, ALU.mult, ALU.add)
        yield
        k.act(v4.v[:, :], v4.v[:, :], ACT.Sqrt)
        yield
        k.recip(v4.v[:, :], v4.v[:, :])
        yield
        k.tt("dve", yc.v[:, :, :], yc.v[:, :, :], v4.v[:, :].rr("p (h o) -> p h o", o=1).bc([128, 4, 64]), ALU.mult)
        yield
        k.tt("dve", dst, yc.v[:, :, :].rr("p h i -> p (h i)"), g.v[:, :], ALU.mult)
        if b is not None:
            k.tt("dve", dst, dst, b.v[:, :], ALU.add)
        yield

    def part_c(tsl):
        B = BC
        acc, yn = B["acc"], B["yn"]
        k.ld("sp", gc.v[:, :], S["C_G"][tsl, :])
        for d in range(2):
            yb = B["y"][B["i"] % 2]
            B["i"] += 1
            k.ld("sp", yb.v[:, :, :], S["C_Y"][d, tsl, :].rearrange("t (h i) -> t h i", i=64))
            yield from head_norm(B, yb, retg, None, yn.v[:, :])
            if d == 0:
                k.tt("dve", acc.v[:, :], yn.v[:, :], gc.v[:, 0:256], ALU.mult)
            else:
                k.tt("dve", yn.v[:, :], yn.v[:, :], gc.v[:, 256:512], ALU.mult)
                k.tt("dve", acc.v[:, :], acc.v[:, :], yn.v[:, :], ALU.add)
            yield
        k.stv("sp", S["O"][tsl, 512:768], acc.v[:, :])

    def part_d(tsl):
        B = BD
        acc, yn = B["acc"], B["yn"]
        k.ld("sp", gd.v[:, :], S["D_G"][tsl, :])
        for d in range(2):
            yb = B["y"][B["i"] % 2]
            B["i"] += 1
            k.ld("sp", yb.v[:, :, :], S["D_Y"][d, tsl, :].rearrange("t (h i) -> t h i", i=64))
            k.ld("sp", vd.v[:, :, :], S["D_V"][d, tsl, :].rearrange("t (h i) -> t h i", i=64))
            k.ld("sp", bs.v[:, :], S["D_BS"][d, tsl, :])
            yield from head_norm(B, yb, lng, lnb, yn.v[:, :])
            k.tt("dve", vdf.v[:, :, :], vd.v[:, :, :], bs.v[:, :].rr("p (h o) -> p h o", o=1).bc([128, 4, 64]), ALU.mult)
            k.tt("dve", yn.v[:, :], yn.v[:, :], vdf.v[:, :, :].rr("p h i -> p (h i)"), ALU.add)
            yield
            if d == 0:
                k.cp("act", acc.v[:, :], yn.v[:, :])
            else:
                k.tt("dve", acc.v[:, :], acc.v[:, :], yn.v[:, :], ALU.add)
            yield
        k.tt("dve", acc.v[:, :], acc.v[:, :], gd.v[:, :], ALU.mult)
        k.stv("sp", S["O"][tsl, 768:1024], acc.v[:, :])

    for t0 in range(0, TT, 128):
        tsl = slice(t0, t0 + 128)
        live = [part_c(tsl), part_d(tsl)]
        while live:
            for g_ in list(live):
                try:
                    next(g_)
                except StopIteration:
                    live.remove(g_)


def modphase(k, st, cfg):
    cv = k.sb(st, [128, 2, 8], F32, "cv")
    for r in range(2):
        k.ld("sp", cv.v[:, r, :], cfg["cvec"][r].rearrange("(c p) -> p c", p=128), allow_slow_non_contiguous=True)
    sc = k.sb(st, [128, 8, 2], F32, "sc")
    k.act(sc.v[:, :, :].rr("p c r -> p r c"), cv.v[:, :, :], ACT.Silu)
    wm = [k.sb(st, [128, 8, 512], F32, "wm%d" % i) for i in range(2)]
    bm = k.sb(st, [2, 9 * D], F32, "bm")
    k.ld("sp", bm.v[:, :], cfg["b_mod"].partition_broadcast(2))
    ob = k.sb(st, [2, 9 * D], F32, "ob")
    pm = [k.ps(st, [128, 512], F32, "pm%d" % i) for i in range(2)]
    wv = cfg["w_mod"].rearrange("(c p) f -> p c f", p=128)
    for j in range(18):
        w = wm[j % 2]
        for c in range(8):
            k.ld("sp", w.v[:, c, :], wv[:, c, j * 512:(j + 1) * 512])
        p = pm[j % 2]
        for c in range(8):
            k.mm(p.v[0:2, :], sc.v[:, c, :], w.v[:, c, :], c == 0, c == 7)
        k.tt("dve", ob.v[:, j * 512:(j + 1) * 512], p.v[0:2, :], bm.v[:, j * 512:(j + 1) * 512], ALU.add)
    k.stv("sp", cfg["mod"], ob.v[:, :])


def build_program(T, CTX, L, stop_after=None):
    TT = T + CTX
    nc = bass.Bass("TRN2", target_bir_lowering=False, dynamic_dma_scratch_size=8192)

    def din(name, shape, dt=F32):
        return nc.dram_tensor(name, list(shape), dt, kind="ExternalInput").ap()

    def scr(name, shape, dt=F32):
        return nc.dram_tensor(name, list(shape), dt, kind="Internal").ap()

    I = dict(
        xin=din("xin", [TT, D]), cvec=din("cvec", [2, D]),
        w_mod=din("w_mod", [L, D, 9 * D]), b_mod=din("b_mod", [L, 9 * D]), norm_g=din("norm_g", [L, 3, D]),
        ffn_w_in=din("ffn_w_in", [L, 2, D, 2 * DFF]), ffn_w_out=din("ffn_w_out", [L, 2, DFF, D]),
        w_in=din("w_in", [L, D, 3456]), w_out=din("w_out", [L, D, D]), final_g=din("final_g", [D]),
        w_sw=din("w_sw", [L, D, 1280]), w_d=din("w_d", [L, 2, D, 896]), pcols=din("pcols", [L, 128, 24]),
        mu=din("mu", [L, 2, 896]), w2=din("w2", [L, 2, 64, 256]), a2=din("a2", [L, 2, 64, 256]),
        g2=din("g2", [L, 128, 256]), ret_g=din("ret_g", [L, 256]), ln_g=din("ln_g", [L, 256]),
        ln_b=din("ln_b", [L, 256]), sink=din("sink", [L, 4]),
    )
    Cn = dict(
        ident=din("ident", [128, 128]), bd64=din("bd64", [128, 128]), e2=din("e2", [128, 2]),
        ropeA_c=din("ropeA_c", [128, T]), ropeA_s=din("ropeA_s", [128, T]),
        ropeS_c=din("ropeS_c", [128, T]), ropeS_s=din("ropeS_s", [128, T]),
        ret_tab=din("ret_tab", [128, 2, 2, 3, CH]), ret_pc=din("ret_pc", [64, 8]),
        m_st=din("m_st", [64, 512]), m_ts=din("m_ts", [64, 512]), m_in=din("m_in", [64, 512]),
        id8=din("id8", [64, 512]), mprev=din("mprev", [128, 128]), mnext=din("mnext", [128, 128]),
    )
    out = nc.dram_tensor("out", [T, D], F32, kind="ExternalOutput").ap()
    dbg = stop_after is not None
    mk = (lambda name, shape, dt=F32: nc.dram_tensor(name, list(shape), dt, kind="ExternalOutput").ap()) if dbg else scr
    S = dict(
        xres=mk("xres", [TT, D]), mod=mk("modr", [2, 9 * D]), hT=mk("hT", [D, TT], BF16),
        QA=mk("QA", [256, TT], BF16), KA=mk("KA", [128, TT], BF16), VA=mk("VA", [TT, 128], BF16),
        QB=mk("QB", [256, TT], BF16), KB=mk("KB", [128, TT], BF16), VB=mk("VB", [TT, 128], BF16),
        O=mk("O", [TT, 1024]),
        C_Rt=mk("C_Rt", [2, 256, TT], BF16), C_Kt=mk("C_Kt", [2, 256, TT], BF16), C_Kh=mk("C_Kh", [2, TT, 256], BF16),
        C_V=mk("C_V", [TT, 256], BF16), C_G=mk("C_G", [TT, 512]), C_Y=mk("C_Y", [2, TT, 256]),
        D_Rt=mk("D_Rt", [2, 256, TT], BF16), D_Kt=mk("D_Kt", [2, 256, TT], BF16), D_At=mk("D_At", [2, 256, TT], BF16),
        D_Bt=mk("D_Bt", [2, 256, TT], BF16), D_Kh=mk("D_Kh", [2, TT, 256], BF16), D_Bh=mk("D_Bh", [2, TT, 256], BF16),
        D_Atok=mk("D_Atok", [2, TT, 256], BF16), D_V=mk("D_V", [2, TT, 256], BF16), D_Y=mk("D_Y", [2, TT, 256]),
        D_PC=mk("D_PC", [2, 256, TT // CH]), D_G=mk("D_G", [TT, 256]), D_BS=mk("D_BS", [2, TT, 4]),
    )
    k = K(nc)
    stages = []

    def phase(name, fn, cfg):
        if stop_after is not None and stop_after in stages:
            return
        with contextlib.ExitStack() as st:
            fn(k, st, cfg)
            k.barrier()
        stages.append(name)
        k.marks.append((name, {e.name: e.n for e in k.E.values()}))

    with k.stack:
        segs_all = [(0, T, 0), (T, CTX, 1)]
        for l in range(L):
            last = l == L - 1
            W = dict(w_in=I["w_in"][l], w_sw=I["w_sw"][l], w_d=I["w_d"][l], mu=I["mu"][l], pcols=I["pcols"][l],
                     w2=I["w2"][l], a2=I["a2"][l], g2=I["g2"][l], ret_g=I["ret_g"][l], ln_g=I["ln_g"][l],
                     ln_b=I["ln_b"][l], sink=I["sink"][l])
            base = dict(S=S, C=Cn, W=W, T=T, CTX=CTX)
            phase("mod%d" % l, modphase, dict(cvec=I["cvec"], w_mod=I["w_mod"][l], b_mod=I["b_mod"][l], mod=S["mod"]))
            phase("f1_%d" % l, rowphase, dict(
                xin=I["xin"] if l == 0 else S["xres"], xout=S["xres"], segs=segs_all, mod=S["mod"], ident=Cn["ident"],
                ffn=dict(w_in=I["ffn_w_in"][l, 0], w_out=I["ffn_w_out"][l, 0], g=I["norm_g"][l, 0], shift=0, scale=1,
                         gate=2),
                post=dict(kind="hT", g=I["norm_g"][l, 1], shift=3, scale=4, hT=S["hT"])))
            phase("proj%d" % l, projphase, base)
            phase("attn%d" % l, attnphase, dict(base, need_ctx=not last))
            phase("scanC%d" % l, scanphase, dict(base, dplr=False))
            phase("scanD%d" % l, scanphase, dict(base, dplr=True))
            phase("post%d" % l, postphase, base)
            phase("f2_%d" % l, rowphase, dict(
                xin=S["xres"], xout=S["xres"], segs=segs_all if not last else [(0, T, 0)], mod=S["mod"],
                ident=Cn["ident"],
                pre=dict(o=S["O"], w_out=I["w_out"][l], gate=5),
                ffn=dict(w_in=I["ffn_w_in"][l, 1], w_out=I["ffn_w_out"][l, 1], g=I["norm_g"][l, 2], shift=6, scale=7,
                         gate=8),
                post=dict(kind="final", g=I["final_g"], out=out) if last else None))
    return nc, k


def _consts(T):
    c = {}
    c["ident"] = np.eye(128, dtype=np.float32)
    bd = np.zeros((128, 128), np.float32)
    bd[:64, :64] = 1
    bd[64:, 64:] = 1
    c["bd64"] = bd
    e2 = np.zeros((128, 2), np.float32)
    e2[:64, 0] = 1
    e2[64:, 1] = 1
    c["e2"] = e2
    t = np.arange(T)
    row = (t // 64).astype(np.float32)
    col = (t % 64).astype(np.float32)
    inv16 = (1.0 / (np.float32(10000.0) ** (np.arange(0, 32, 2, dtype=np.float32) / np.float32(32)))).astype(np.float32)
    inv32 = (1.0 / (np.float32(10000.0) ** (np.arange(0, 64, 2, dtype=np.float32) / np.float32(64)))).astype(np.float32)
    ca = np.zeros((64, T), np.float32)
    sa = np.zeros((64, T), np.float32)
    for blk, pos in ((0, row), (32, col)):
        ang = pos[None, :] * inv16[:, None]
        ca[blk:blk + 16] = np.cos(ang)
        ca[blk + 16:blk + 32] = np.cos(ang)
        sa[blk:blk + 16] = -np.sin(ang)
        sa[blk + 16:blk + 32] = np.sin(ang)
    ang = t.astype(np.float32)[None, :] * inv32[:, None]
    cs = np.concatenate([np.cos(ang), np.cos(ang)], 0).astype(np.float32)
    ss = np.concatenate([-np.sin(ang), np.sin(ang)], 0).astype(np.float32)
    c["ropeA_c"] = np.concatenate([ca, ca], 0)
    c["ropeA_s"] = np.concatenate([sa, sa], 0)
    c["ropeS_c"] = np.concatenate([cs, cs], 0)
    c["ropeS_s"] = np.concatenate([ss, ss], 0)
    lg = np.log1p(-np.exp2(-5.0 - np.arange(4, dtype=np.float64)))
    i = np.arange(CH, dtype=np.float64)
    rt = np.zeros((128, 2, 2, 3, CH), np.float64)
    for p in range(128):
        for g in range(2):
            l_ = lg[2 * g + p // 64]
            for d in range(2):
                csum = (i + 1) * l_ if d == 0 else (CH - i) * l_
                rt[p, g, d, 0] = np.exp(csum)
                rt[p, g, d, 1] = np.exp(-csum) / 8.0
                rt[p, g, d, 2] = np.exp(CH * l_ - csum) / 8.0
    c["ret_tab"] = rt.astype(np.float32)
    c["ret_pc"] = np.tile(np.exp(CH * lg)[None, :], (64, 2)).astype(np.float32)
    s_ = np.arange(64)[:, None]
    t_ = np.arange(64)[None, :]
    lt = (s_ < t_).astype(np.float32)
    gt = (s_ > t_).astype(np.float32)
    eye = np.eye(64, dtype=np.float32)
    c["m_st"] = np.concatenate([lt] * 4 + [gt] * 4, 1)
    c["m_ts"] = np.concatenate([gt] * 4 + [lt] * 4, 1)
    c["m_in"] = np.concatenate([lt + eye] * 4 + [gt + eye] * 4, 1)
    c["id8"] = np.concatenate([eye] * 8, 1)
    j = np.arange(128)[:, None]
    ii = np.arange(128)[None, :]
    c["mprev"] = (ii <= j).astype(np.float32)
    c["mnext"] = (j <= ii).astype(np.float32)
    return c


def _host_maps(inp, T, CTX, L):
    f = lambda a: np.ascontiguousarray(np.asarray(a, dtype=np.float32))
    w_in = f(inp["w_in"])
    pa = np.concatenate([np.arange(16, 32), np.arange(0, 16), np.arange(48, 64), np.arange(32, 48)])
    psq = np.concatenate([np.arange(32, 64), np.arange(0, 32)])
    cols = []
    for base, nh, perm in ((0, 4, pa), (256, 2, pa), (512, 4, pa), (768, 2, pa), (1024, 4, psq), (1280, 4, psq)):
        for h in range(nh):
            cols.append(base + h * 64 + perm)
    cols = np.concatenate(cols)
    w_sw = np.ascontiguousarray(w_in[:, :, cols])
    dcols = [np.concatenate([np.arange(2304, 3072), np.arange(3200, 3264), np.arange(3328, 3392)]),
             np.concatenate([np.arange(2304, 3072), np.arange(3264, 3328), np.arange(3392, 3456)])]
    w_d = np.ascontiguousarray(np.stack([w_in[:, :, dc] for dc in dcols], 1))
    qkg = f(inp["qk_norm_g"])
    pcols = np.zeros((L, 128, 24), np.float32)
    two = lambda v: np.ascontiguousarray(v.reshape(2, 128).T)
    for l in range(L):
        for qk in range(2):
            g = qkg[l, qk]
            pcols[l, :, 2 * qk] = np.tile(g, 2)
            pcols[l, :, 2 * qk + 1] = np.tile(g[pa], 2)
        for d in range(2):
            pcols[l, :, 4 + 2 * d:6 + 2 * d] = two(f(inp["rwkv_w0"])[l, d])
            pcols[l, :, 8 + 2 * d:10 + 2 * d] = two(f(inp["rwkv_a0"])[l, d])
            pcols[l, :, 16 + 2 * d:18 + 2 * d] = two(f(inp["rwkv_rho"])[l, d].reshape(256))
        pcols[l, :, 12:14] = two(f(inp["rwkv_k_k"])[l])
        pcols[l, :, 14:16] = two(f(inp["rwkv_k_a"])[l])
    shared = dict(
        w_mod=f(inp["w_mod"]), b_mod=f(inp["b_mod"]), norm_g=f(inp["norm_g"]), ffn_w_in=f(inp["ffn_w_in"]),
        ffn_w_out=f(inp["ffn_w_out"]), w_in=w_in, w_out=f(inp["w_out"]), final_g=f(inp["final_norm_g"]),
        w_sw=w_sw, w_d=w_d, pcols=pcols, mu=f(inp["rwkv_mu"]), w2=f(inp["rwkv_w2"]), a2=f(inp["rwkv_a2"]),
        g2=f(inp["rwkv_g2"]), ret_g=f(inp["ret_norm_g"]), ln_g=f(inp["rwkv_ln_g"]), ln_b=f(inp["rwkv_ln_b"]),
        sink=f(inp["attn_sink"]))
    shared.update(_consts(T))
    x, c, ctx, c_ctx = f(inp["x"]), f(inp["c"]), f(inp["ctx"]), f(inp["c_ctx"])
    maps = []
    for b in range(x.shape[0]):
        m = dict(shared)
        m["xin"] = np.ascontiguousarray(np.concatenate([x[b], ctx[b]], 0))
        m["cvec"] = np.ascontiguousarray(np.stack([c[b], c_ctx], 0))
        maps.append(m)
    return maps


_PROG = {}


def kernel(**inputs):
    x = np.asarray(inputs["x"])
    B, T, _ = x.shape
    CTX = np.asarray(inputs["ctx"]).shape[1]
    L = np.asarray(inputs["w_in"]).shape[0]
    key = (T, CTX, L)
    if key not in _PROG:
        _PROG[key] = build_program(T, CTX, L)[0]
    nc = _PROG[key]
    maps = _host_maps(inputs, T, CTX, L)
    res = run_bass_kernel_spmd(nc, maps, core_ids=list(range(B)))
    return np.stack([np.asarray(r["out"], dtype=np.float32) for r in res.results], 0)
```

```python
import contextlib
import numpy as np
import concourse.bass as bass
import concourse.mybir as mybir
from concourse.bass_utils import run_bass_kernel_spmd

ACT = mybir.ActivationFunctionType
ALU = mybir.AluOpType
AX = mybir.AxisListType
F32 = mybir.dt.float32
BF16 = mybir.dt.bfloat16

D = 1024
DFF = 2816
NMOD = 9
RMS_EPS = 1e-6
STRICT = True


class PSem:
    def __init__(self, h):
        self.h = h
        self.val = 0


class Eng:
    def __init__(self, name, h, sem):
        self.name, self.h, self.sem = name, h, sem
        self.n = 0
        self.seen = {}
        self.dseen = {}


class Buf:
    def __init__(self, k, t, name):
        self.k, self.t, self.name = k, t, name
        self.w = None
        self.r = {}
        self.psem = None
        self.dma_w = 0
        self.dma_rw = 0

    def __getitem__(self, key):
        return self.t[key]

    @property
    def v(self):
        return _VA(self)


class View:
    def __init__(self, buf, ap):
        self.buf, self.ap = buf, ap

    def rr(self, pat, **kw):
        return View(self.buf, self.ap.rearrange(pat, **kw))

    def bc(self, shape):
        return View(self.buf, self.ap.to_broadcast(list(shape)))

    def __getitem__(self, key):
        return View(self.buf, self.ap[key])


class _VA:
    def __init__(self, b):
        self.b = b

    def __getitem__(self, key):
        return View(self.b, self.b.t[key])


def _bufs(*xs):
    out = []
    for x in xs:
        if isinstance(x, View) and x.buf not in out:
            out.append(x.buf)
    return out


def _ap(x):
    return x.ap if isinstance(x, View) else x


class K:
    def __init__(self, nc):
        self.nc = nc
        self.stack = contextlib.ExitStack()
        self.E = {}
        for name, h in (("pe", nc.tensor), ("act", nc.scalar), ("dve", nc.vector),
                        ("pool", nc.gpsimd), ("sp", nc.sync)):
            sem = self.stack.enter_context(nc.semaphore("s_" + name))
            self.E[name] = Eng(name, h, sem)
        self.free_psems = []
        self.n_psems = 0
        self.all_psems = []
        self.bufs = []
        self.uid = 0
        self.ninstr = 0
        self.marks = []

    def _psem(self):
        if self.free_psems:
            return self.free_psems.pop()
        self.n_psems += 1
        h = self.stack.enter_context(self.nc.semaphore("d%d" % self.n_psems))
        p = PSem(h)
        self.all_psems.append(p)
        return p

    def sb(self, st, shape, dtype, name):
        self.uid += 1
        t = st.enter_context(self.nc.sbuf_tensor("%s_%d" % (name, self.uid), list(shape), dtype))
        b = Buf(self, t, name)
        st.callback(self._release, b)
        return b

    def ps(self, st, shape, dtype, name):
        self.uid += 1
        t = st.enter_context(self.nc.psum_tensor("%s_%d" % (name, self.uid), list(shape), dtype))
        return Buf(self, t, name)

    def _release(self, b):
        if b.psem is not None and not getattr(b.psem, "sw", False):
            self.free_psems.append(b.psem)
        b.psem = None

    def _wait_eng(self, E, X, n):
        if E.seen.get(X.name, 0) >= n:
            return
        E.h.wait_ge(X.sem, n)
        E.seen[X.name] = n
        self.ninstr += 1

    def _wait_d(self, E, p, v):
        if v <= 0 or E.dseen.get(id(p), 0) >= v:
            return
        E.h.wait_ge(p.h, v)
        E.dseen[id(p)] = v
        self.ninstr += 1

    def _deps(self, E, r, w):
        for b in r:
            if b.w is not None:
                self._wait_eng(E, b.w[0], b.w[1])
            if b.psem is not None:
                self._wait_d(E, b.psem, b.dma_w)
        for b in w:
            if b.w is not None and (STRICT or b.w[0] is not E) and not (E.name == "pe" and b.w[0] is E):
                self._wait_eng(E, b.w[0], b.w[1])
            for X, n in b.r.items():
                if STRICT or X is not E:
                    self._wait_eng(E, X, n)
            if b.psem is not None:
                self._wait_d(E, b.psem, b.dma_rw)

    def op(self, eng, fn, r=(), w=()):
        E = self.E[eng]
        self._deps(E, r, w)
        ins = fn(E.h)
        E.n += 1
        ins.then_inc(E.sem, 1)
        self.ninstr += 1
        for b in w:
            b.w = (E, E.n)
            b.r = {}
        for b in r:
            if b.w is None or b.w[0] is not E or b.w[1] != E.n:
                b.r[E] = E.n
        return ins

    def dma(self, q, out, in_, r=(), w=(), **kw):
        Q = self.E[q]
        self._deps(Q, r, w)
        anchor = w[0] if w else r[0]
        if anchor.psem is None:
            if q == "pool":
                self.n_psems += 1
                anchor.psem = PSem(self.stack.enter_context(self.nc.semaphore("w%d" % self.n_psems)))
                anchor.psem.sw = True
                self.all_psems.append(anchor.psem)
            else:
                anchor.psem = self._psem()
        p = anchor.psem
        assert getattr(p, "sw", False) == (q == "pool"), "buffer mixes SW and HW DGE DMAs: " + anchor.name
        Q.h.dma_start(out=out, in_=in_, **kw).then_inc(p.h, 16)
        p.val += 16
        self.ninstr += 1
        anchor.dma_rw = p.val
        if w:
            anchor.dma_w = p.val
            anchor.w = None
            anchor.r = {}

    def mm(self, out, lhsT, rhs, start=True, stop=True):
        return self.op("pe", lambda e: e.matmul(out.ap, lhsT.ap, rhs.ap, start=start, stop=stop),
                       r=_bufs(lhsT, rhs), w=[out.buf])

    def tr(self, out, in_, ident):
        return self.op("pe", lambda e: e.transpose(out.ap, in_.ap, ident.ap), r=_bufs(in_, ident), w=[out.buf])

    def tt(self, eng, out, in0, in1, op):
        return self.op(eng, lambda e: e.tensor_tensor(out=out.ap, in0=in0.ap, in1=in1.ap, op=op),
                       r=_bufs(in0, in1), w=[out.buf])

    def ts(self, eng, out, in0, s1, s2, op0, op1=None):
        kw = {} if op1 is None else dict(op1=op1)
        return self.op(eng, lambda e: e.tensor_scalar(out=out.ap, in0=in0.ap, scalar1=_ap(s1), scalar2=_ap(s2),
                                                      op0=op0, **kw), r=_bufs(in0, s1, s2), w=[out.buf])

    def stt(self, eng, out, in0, scalar, in1, op0, op1):
        return self.op(eng, lambda e: e.scalar_tensor_tensor(out=out.ap, in0=in0.ap, scalar=_ap(scalar), in1=in1.ap,
                                                             op0=op0, op1=op1), r=_bufs(in0, scalar, in1), w=[out.buf])

    def act(self, out, in_, func, bias=None, scale=None, accum=None):
        kw = {}
        if bias is not None:
            kw["bias"] = _ap(bias)
        if scale is not None:
            kw["scale"] = _ap(scale)
        w = [out.buf]
        if accum is not None:
            kw["accum_out"] = accum.ap
            w.append(accum.buf)
        return self.op("act", lambda e: e.activation(out=out.ap, in_=in_.ap, func=func, **kw),
                       r=_bufs(in_, bias, scale), w=w)

    def cp(self, eng, out, in_):
        if eng == "act":
            return self.op("act", lambda e: e.copy(out=out.ap, in_=in_.ap), r=[in_.buf], w=[out.buf])
        return self.op(eng, lambda e: e.tensor_copy(out=out.ap, in_=in_.ap), r=[in_.buf], w=[out.buf])

    def memset(self, eng, out, val):
        return self.op(eng, lambda e: e.memset(out.ap, val), w=[out.buf])

    def recip(self, out, in_):
        return self.op("dve", lambda e: e.reciprocal(out=out.ap, in_=in_.ap), r=[in_.buf], w=[out.buf])

    def ld(self, q, out, dram, **kw):
        return self.dma(q, out.ap, dram, w=[out.buf], **kw)

    def stv(self, q, dram, in_, **kw):
        return self.dma(q, dram, in_.ap, r=[in_.buf], **kw)

    def barrier(self):
        for E in self.E.values():
            for X in self.E.values():
                if X is not E and X.n > 0:
                    self._wait_eng(E, X, X.n)
            for p in self.all_psems:
                self._wait_d(E, p, p.val)


def mm(k, out_ap, lhsT_ap, rhs_ap, start, stop, r, w):
    return k.op("pe", lambda e: e.matmul(out_ap, lhsT_ap, rhs_ap, start=start, stop=stop), r=r, w=w)


def rowphase(k, st, cfg):
    nc = k.nc
    ffn = cfg["ffn"]
    pre = cfg.get("pre")
    post = cfg.get("post")
    w1 = k.sb(st, [128, 8, 2 * DFF], BF16, "w1")
    w2 = k.sb(st, [128, 22, D], BF16, "w2")
    w1v = ffn["w_in"].rearrange("(c p) f -> p c f", p=128)
    for c in range(8):
        for hlf in range(4):
            f0 = hlf * 1408
            k.dma("pool", w1[:, c, f0:f0 + 1408], w1v[:, c, f0:f0 + 1408], w=[w1])
    w2v = ffn["w_out"].rearrange("(c p) d -> p c d", p=128)
    for c in range(22):
        k.dma("pool", w2[:, c, :], w2v[:, c, :], w=[w2])
    if pre is not None:
        wo = k.sb(st, [128, 8, D], BF16, "wo")
        wov = pre["w_out"].rearrange("(c p) d -> p c d", p=128)
        for c in range(8):
            k.dma("pool", wo[:, c, :], wov[:, c, :], w=[wo])

    identf = k.sb(st, [128, 128], F32, "identf")
    k.dma("sp", identf[:, :], cfg["ident"], w=[identf])
    xt = [k.sb(st, [128, D], F32, "xt%d" % i) for i in range(4)]
    hT = k.sb(st, [128, 8, 512], BF16, "hT")
    gT = k.sb(st, [128, 22, 512], BF16, "gT")
    sg = k.sb(st, [128, 512], BF16, "sg")
    tmp = k.sb(st, [128, D], F32, "tmp")
    ssq = k.sb(st, [128, 4], F32, "ssq")
    rinv = k.sb(st, [128, 4], F32, "rinv")
    ps_u = [k.ps(st, [128, 512], F32, "psu%d" % i) for i in range(4)]
    ps_y = [k.ps(st, [128, 512], F32, "psy%d" % i) for i in range(2)]
    ps_t = k.ps(st, [128, D], F32, "pst")
    Ac = k.sb(st, [128, 8], F32, "Ac")
    Bc = k.sb(st, [128, 8], F32, "Bc")
    Gc = k.sb(st, [128, 8], F32, "Gc")
    A2c = k.sb(st, [128, 8], F32, "A2c")
    B2c = k.sb(st, [128, 8], F32, "B2c")
    G = k.sb(st, [128, D], F32, "G")
    PG = k.sb(st, [128, D], F32, "PG") if pre is not None else None
    gvec2 = None
    if post is not None and post["kind"] == "final":
        gvec2 = k.sb(st, [128, D], F32, "gvec2")
        k.dma("sp", gvec2[:, :], post["g"].partition_broadcast(128), w=[gvec2])

    def col(ap1d):
        return ap1d.rearrange("(c p) -> p c", p=128)

    def load_cols(dst, ap1d):
        k.dma("sp", dst[:, :], col(ap1d), w=[dst], allow_slow_non_contiguous=True)

    def setup_mod(mr):
        load_cols(Gc, ffn["g"])
        load_cols(Ac, cfg["mod"][mr, ffn["scale"] * D:(ffn["scale"] + 1) * D])
        load_cols(Bc, cfg["mod"][mr, ffn["shift"] * D:(ffn["shift"] + 1) * D])
        k.op("dve", lambda e: e.scalar_tensor_tensor(out=Ac[:, :], in0=Ac[:, :], scalar=1.0, in1=Gc[:, :],
                                                     op0=ALU.add, op1=ALU.mult), r=[Ac, Gc], w=[Ac])
        if post is not None and post["kind"] == "hT":
            load_cols(Gc, post["g"])
            load_cols(A2c, cfg["mod"][mr, post["scale"] * D:(post["scale"] + 1) * D])
            load_cols(B2c, cfg["mod"][mr, post["shift"] * D:(post["shift"] + 1) * D])
            k.op("dve", lambda e: e.scalar_tensor_tensor(out=A2c[:, :], in0=A2c[:, :], scalar=1.0, in1=Gc[:, :],
                                                         op0=ALU.add, op1=ALU.mult), r=[A2c, Gc], w=[A2c])
        k.dma("sp", G[:, :], cfg["mod"][mr, ffn["gate"] * D:(ffn["gate"] + 1) * D].partition_broadcast(128), w=[G])
        k.op("dve", lambda e: e.tensor_scalar(out=G[:, :], in0=G[:, :], scalar1=0.5, scalar2=None, op0=ALU.mult),
             r=[G], w=[G])
        if pre is not None:
            k.dma("sp", PG[:, :], cfg["mod"][mr, pre["gate"] * D:(pre["gate"] + 1) * D].partition_broadcast(128),
                  w=[PG])

    try:
        tmp2 = [tmp, k.sb(st, [128, D], F32, "tmpb")]
    except AssertionError:
        tmp2 = [tmp, tmp]

    def rms_all(xs):
        m_ = len(xs)
        junk = gT[:, 0:2, :]
        for s_, xb in enumerate(xs):
            k.op("act", lambda e: e.activation(out=junk, in_=xb[:, :].rearrange("p (a b) -> p a b", a=2),
                                               func=ACT.Square, accum_out=ssq[:, s_:s_ + 1]), r=[xb], w=[gT, ssq])
        k.op("dve", lambda e: e.tensor_scalar(out=rinv[:, 0:m_], in0=ssq[:, 0:m_], scalar1=1.0 / D, scalar2=RMS_EPS,
                                              op0=ALU.mult, op1=ALU.add), r=[ssq], w=[rinv])
        k.op("act", lambda e: e.activation(out=rinv[:, 0:m_], in_=rinv[:, 0:m_], func=ACT.Sqrt), r=[rinv], w=[rinv])
        k.op("dve", lambda e: e.reciprocal(out=rinv[:, 0:m_], in_=rinv[:, 0:m_]), r=[rinv], w=[rinv])

    def norm_mod_T_all(xs, A, B, dstT):
        rms_all(xs)
        for s, xb in enumerate(xs):
            tm = tmp2[s % 2]
            k.op("dve", lambda e: e.tensor_scalar(out=tm[:, :], in0=xb[:, :], scalar1=rinv[:, s:s + 1], scalar2=None,
                                                  op0=ALU.mult), r=[xb, rinv], w=[tm])
            for c in range(8):
                k.op("pe", lambda e: e.transpose(ps_t[:, c * 128:(c + 1) * 128], tm[:, c * 128:(c + 1) * 128],
                                                 identf[:, :]), r=[tm, identf], w=[ps_t])
            for c in range(8):
                if c % 2 == 0:
                    k.op("dve", lambda e: e.tensor_scalar(out=dstT[:, c, s * 128:(s + 1) * 128],
                                                          in0=ps_t[:, c * 128:(c + 1) * 128], scalar1=A[:, c:c + 1],
                                                          scalar2=B[:, c:c + 1], op0=ALU.mult, op1=ALU.add),
                         r=[ps_t, A, B], w=[dstT])
                else:
                    k.op("act", lambda e: e.activation(out=dstT[:, c, s * 128:(s + 1) * 128],
                                                       in_=ps_t[:, c * 128:(c + 1) * 128], func=ACT.Identity,
                                                       scale=A[:, c:c + 1], bias=B[:, c:c + 1]),
                         r=[ps_t, A, B], w=[dstT])

    ui = 0
    yi = 0
    for (tok0, ntok, mr) in cfg["segs"]:
        setup_mod(mr)
        for t0 in range(tok0, tok0 + ntok, 512):
            n = min(512, tok0 + ntok - t0)
            ns = n // 128
            for s in range(ns):
                k.dma("sp", xt[s][:, :], cfg["xin"][t0 + s * 128:t0 + (s + 1) * 128, :], w=[xt[s]])
            if pre is not None:
                for s in range(ns):
                    k.dma("sp", tmp[:, :], pre["o"][t0 + s * 128:t0 + (s + 1) * 128, :], w=[tmp])
                    for c in range(8):
                        k.op("pe", lambda e: e.transpose(ps_t[:, c * 128:(c + 1) * 128], tmp[:, c * 128:(c + 1) * 128],
                                                         identf[:, :]), r=[tmp, identf], w=[ps_t])
                    k.op("act", lambda e: e.copy(out=hT[:, :, s * 128:(s + 1) * 128],
                                                 in_=ps_t[:, :].rearrange("p (c t) -> p c t", c=8)), r=[ps_t], w=[hT])
                for s in range(ns):
                    for hf in range(2):
                        py = ps_y[yi % 2]
                        yi += 1
                        for c in range(8):
                            mm(k, py[:, :], hT[:, c, s * 128:(s + 1) * 128], wo[:, c, hf * 512:(hf + 1) * 512],
                               c == 0, c == 7, r=[hT, wo], w=[py])
                        sl = slice(hf * 512, (hf + 1) * 512)
                        k.op("dve", lambda e, py=py, sl=sl: e.tensor_tensor(out=tmp[:, sl], in0=py[:, :],
                                                                             in1=PG[:, sl], op=ALU.mult),
                             r=[py, PG], w=[tmp])
                        k.op("dve", lambda e, s=s, sl=sl: e.tensor_tensor(out=xt[s][:, sl], in0=xt[s][:, sl],
                                                                            in1=tmp[:, sl], op=ALU.add),
                             r=[xt[s], tmp], w=[xt[s]])
            norm_mod_T_all(xt[0:ns], Ac, Bc, hT)
            for fc in range(22 if cfg.get("stage", 9) >= 2 else 0):
                p1 = ps_u[ui % 4]
                p2 = ps_u[(ui + 1) % 4]
                ui += 2
                for c in range(8):
                    mm(k, p1[:, 0:n], w1[:, c, fc * 128:(fc + 1) * 128], hT[:, c, 0:n], c == 0, c == 7,
                       r=[w1, hT], w=[p1])
                for c in range(8):
                    mm(k, p2[:, 0:n], w1[:, c, DFF + fc * 128:DFF + (fc + 1) * 128], hT[:, c, 0:n], c == 0, c == 7,
                       r=[w1, hT], w=[p2])
                k.op("act", lambda e, p1=p1: e.activation(out=sg[:, 0:n], in_=p1[:, 0:n], func=ACT.Silu),
                     r=[p1], w=[sg])
                k.op("dve", lambda e, p2=p2, fc=fc: e.tensor_tensor(out=gT[:, fc, 0:n], in0=sg[:, 0:n],
                                                                     in1=p2[:, 0:n], op=ALU.mult),
                     r=[sg, p2], w=[gT])
            for s in range(ns if cfg.get("stage", 9) >= 3 else 0):
                for hf in range(2):
                    py = ps_y[yi % 2]
                    yi += 1
                    for fc in range(22):
                        mm(k, py[:, :], gT[:, fc, s * 128:(s + 1) * 128], w2[:, fc, hf * 512:(hf + 1) * 512],
                           fc == 0, fc == 21, r=[gT, w2], w=[py])
                    sl = slice(hf * 512, (hf + 1) * 512)
                    k.op("dve", lambda e, py=py, sl=sl: e.tensor_tensor(out=tmp[:, sl], in0=py[:, :], in1=G[:, sl],
                                                                         op=ALU.mult), r=[py, G], w=[tmp])
                    k.op("dve", lambda e, s=s, sl=sl: e.tensor_tensor(out=xt[s][:, sl], in0=xt[s][:, sl],
                                                                        in1=tmp[:, sl], op=ALU.add),
                         r=[xt[s], tmp], w=[xt[s]])
            if post is not None and post["kind"] == "final":
                if mr == 0:
                    rms_all(xt[0:ns])
                    for s in range(ns):
                        r0 = t0 + s * 128
                        tm = tmp2[s % 2]
                        k.op("dve", lambda e: e.scalar_tensor_tensor(out=tm[:, :], in0=xt[s][:, :],
                                                                     scalar=rinv[:, s:s + 1], in1=gvec2[:, :],
                                                                     op0=ALU.mult, op1=ALU.mult),
                             r=[xt[s], rinv, gvec2], w=[tm])
                        k.dma("sp", post["out"][r0:r0 + 128, :], tm[:, :], r=[tm])
                continue
            for s in range(ns):
                r0 = t0 + s * 128
                k.dma("sp", cfg["xout"][r0:r0 + 128, :], xt[s][:, :], r=[xt[s]])
            if post is not None and post["kind"] == "hT":
                norm_mod_T_all(xt[0:ns], A2c, B2c, hT)
            if post is not None and post["kind"] == "hT":
                for c in range(8):
                    k.dma("sp", post["hT"][c * 128:(c + 1) * 128, t0:t0 + n], hT[:, c, 0:n], r=[hT])


CH = 64
DECAY_SCALE = 0.6065306597126334


def projphase(k, st, cfg):
    S = cfg["S"]
    C = cfg["C"]
    Wt = cfg["W"]
    T, CTX = cfg["T"], cfg["CTX"]
    hTd = S["hT"]
    wabc = k.sb(st, [128, 8, 2304], BF16, "wabc")
    wgd = k.sb(st, [128, 8, 128], BF16, "wgd")
    wsw = k.sb(st, [128, 8, 1280], BF16, "wsw")
    wv = Wt["w_in"].rearrange("(c p) f -> p c f", p=128)
    for c in range(8):
        k.ld("pool", wabc.v[:, c, 0:1152], wv[:, c, 0:1152])
        k.ld("pool", wabc.v[:, c, 1152:2304], wv[:, c, 1152:2304])
        k.ld("pool", wgd.v[:, c, :], wv[:, c, 3072:3200])
        k.ld("pool", wsw.v[:, c, :], Wt["w_sw"].rearrange("(c p) f -> p c f", p=128)[:, c, :])
    wda = [k.sb(st, [128, 8, 896], BF16, "wda%d" % d) for d in range(2)]
    wdb = [k.sb(st, [128, 8, 896], BF16, "wdb%d" % d) for d in range(2)]
    with contextlib.ExitStack() as st2:
        wst = k.sb(st2, [128, 8, 896], F32, "wst")
        mub = k.sb(st2, [128, 896], F32, "mub")
        wtm = k.sb(st2, [128, 8, 896], F32, "wtm")
        for d in range(2):
            for c in range(8):
                k.ld("sp", wst.v[:, c, :], Wt["w_d"][d].rearrange("(c p) f -> p c f", p=128)[:, c, :])
            k.ld("sp", mub.v[:, :], Wt["mu"][d].partition_broadcast(128))
            for c in range(8):
                k.tt("dve", wtm.v[:, c, :], wst.v[:, c, :], mub.v[:, :], ALU.mult)
            k.cp("act", wda[d].v[:, :, :], wtm.v[:, :, :])
            k.tt("dve", wdb[d].v[:, :, :], wst.v[:, :, :], wtm.v[:, :, :], ALU.subtract)
        k.barrier()
    pc = k.sb(st, [128, 24], F32, "pcols")
    k.ld("sp", pc.v[:, :], Wt["pcols"])
    oma = k.sb(st, [128, 2], F32, "oma")
    k.ts("dve", oma.v[:, :], pc.v[:, 14:16], -1.0, 1.0, ALU.mult, ALU.add)
    identf = k.sb(st, [128, 128], F32, "identf")
    k.ld("sp", identf.v[:, :], C["ident"])
    bd = k.sb(st, [128, 128], F32, "bd")
    k.ld("sp", bd.v[:, :], C["bd64"])
    e2 = k.sb(st, [128, 2], F32, "e2")
    k.ld("sp", e2.v[:, :], C["e2"])
    rtab = k.sb(st, [128, 2, 2, 3, CH], F32, "rtab")
    k.ld("sp", rtab.v[:, :, :, :, :], C["ret_tab"])
    w2s = k.sb(st, [64, 2, 256], BF16, "w2s")
    a2s = k.sb(st, [64, 2, 256], BF16, "a2s")
    g2s = k.sb(st, [128, 256], BF16, "g2s")
    for d in range(2):
        k.ld("pool", w2s.v[:, d, :], Wt["w2"][d])
        k.ld("pool", a2s.v[:, d, :], Wt["a2"][d])
    k.ld("pool", g2s.v[:, :], Wt["g2"])

    NT = 512
    hTt = k.sb(st, [128, 8, NT + 2], BF16, "hTt")
    ca = k.sb(st, [128, NT], F32, "ca")
    sa = k.sb(st, [128, NT], F32, "sa")
    cs_ = k.sb(st, [128, NT], F32, "cs")
    ss_ = k.sb(st, [128, NT], F32, "ss")
    gca = k.sb(st, [128, 2, NT], F32, "gca")
    gsa = k.sb(st, [128, 2, NT], F32, "gsa")
    F = [k.sb(st, [128, NT], F32, "f%d" % i) for i in range(12)]
    ob = k.sb(st, [128, NT], BF16, "ob")
    sbf = [k.sb(st, [128, NT], BF16, "sbf%d" % i) for i in range(2)]
    sbi = [0]

    def st_bf(dst, srcv, n):
        b = sbf[sbi[0] % 2]
        sbi[0] += 1
        k.cp("act", b.v[:, 0:n], srcv)
        k.stv("sp", dst, b.v[:, 0:n])
    tb = k.sb(st, [128, 512], F32, "tb")
    tbb = k.sb(st, [128, 512], BF16, "tbb")
    thb = k.sb(st, [64, NT], BF16, "thb")
    alb = k.sb(st, [64, NT], BF16, "alb")
    sgb = k.sb(st, [128, NT], BF16, "sgb")
    bs4 = k.sb(st, [128, 4, 4], F32, "bs4")
    pcx = k.sb(st, [128, NT // CH], F32, "pcx")
    pcx2 = k.sb(st, [128, NT // CH], F32, "pcx2")
    F2 = [k.sb(st, [128, NT], F32, "g%d" % i) for i in range(9)]
    PS = [k.ps(st, [128, 512], F32, "pp%d" % i) for i in range(8)]
    pi = [0]

    def nps():
        p = PS[pi[0] % 8]
        pi[0] += 1
        return p

    def chain(p, wlist, M, n, col0=1):
        tot = len(wlist) * 8
        i = 0
        for (wb, c0, sh) in wlist:
            for c in range(8):
                k.mm(p.v[0:M, 0:n], wb.v[:, c, c0:c0 + M], hTt.v[:, c, col0 + sh:col0 + sh + n], i == 0, i == tot - 1)
                i += 1

    def chain_tok(p, wlist, ncols, s):
        tot = len(wlist) * 8
        i = 0
        for (wb, c0, sh) in wlist:
            for c in range(8):
                k.mm(p.v[:, 0:ncols], hTt.v[:, c, 1 + sh + s * 128:1 + sh + (s + 1) * 128], wb.v[:, c, c0:c0 + ncols],
                     i == 0, i == tot - 1)
                i += 1

    def rope_out(P, Psw, ctab, stab, dst, n):
        k.tt("dve", F[10].v[:, 0:n], P.v[:, 0:n], ctab, ALU.mult)
        k.tt("dve", F[11].v[:, 0:n], Psw.v[:, 0:n], stab, ALU.mult)
        k.tt("dve", dst, F[10].v[:, 0:n], F[11].v[:, 0:n], ALU.add)

    segs = [(0, T, True), (T, CTX, False)]
    for (seg0, seglen, latent) in segs:
        for t0 in range(seg0, seg0 + seglen, NT):
            n = min(NT, seg0 + seglen - t0)
            ns = n // 128
            nch = n // CH
            lo = t0 - 1 if t0 > seg0 else t0
            hi = t0 + n + 1 if t0 + n < seg0 + seglen else t0 + n
            if lo == t0:
                k.memset("dve", hTt.v[:, :, 0:1], 0.0)
            if hi == t0 + n:
                k.memset("dve", hTt.v[:, :, n + 1:n + 2], 0.0)
            for c in range(8):
                k.ld("pool", hTt.v[:, c, 1 - (t0 - lo):1 + n + (hi - t0 - n)], hTd[c * 128:(c + 1) * 128, lo:hi])
            if latent:
                k.ld("pool", ca.v[:, 0:n], C["ropeA_c"][:, t0:t0 + n])
                k.ld("pool", sa.v[:, 0:n], C["ropeA_s"][:, t0:t0 + n])
                k.ld("pool", cs_.v[:, 0:n], C["ropeS_c"][:, t0:t0 + n])
                k.ld("pool", ss_.v[:, 0:n], C["ropeS_s"][:, t0:t0 + n])
                for qk in range(2):
                    k.ts("dve", gca.v[:, qk, 0:n], ca.v[:, 0:n], pc.v[:, 2 * qk:2 * qk + 1], None, ALU.mult)
                    k.ts("dve", gsa.v[:, qk, 0:n], sa.v[:, 0:n], pc.v[:, 2 * qk + 1:2 * qk + 2], None, ALU.mult)
            for grp, base, swb, dq, dk_, dv_ in (("A", 0, 0, S["QA"], S["KA"], S["VA"]),
                                                ("B", 512, 384, S["QB"], S["KB"], S["VB"])):
                for j in range(3):
                    P, Psw = nps(), nps()
                    chain(P, [(wabc, base + j * 128, 0)], 128, n)
                    if latent:
                        chain(Psw, [(wsw, swb + j * 128, 0)], 128, n)
                    qk = 0 if j < 2 else 1
                    if grp == "B":
                        k.act(F[0].v[:, 0:n], P.v[:, 0:n], ACT.Square)
                        pss = nps()
                        k.mm(pss.v[:, 0:n], bd.v[:, :], F[0].v[:, 0:n])
                        k.ts("dve", F[1].v[:, 0:n], pss.v[:, 0:n], 1.0 / 64, RMS_EPS, ALU.mult, ALU.add)
                        k.act(F[1].v[:, 0:n], F[1].v[:, 0:n], ACT.Sqrt)
                        k.recip(F[1].v[:, 0:n], F[1].v[:, 0:n])
                        k.tt("dve", F[2].v[:, 0:n], P.v[:, 0:n], F[1].v[:, 0:n], ALU.mult)
                        if latent:
                            k.tt("dve", F[3].v[:, 0:n], Psw.v[:, 0:n], F[1].v[:, 0:n], ALU.mult)
                            rope_out(F[2], F[3], gca.v[:, qk, 0:n], gsa.v[:, qk, 0:n], ob.v[:, 0:n], n)
                        else:
                            k.ts("dve", ob.v[:, 0:n], F[2].v[:, 0:n], pc.v[:, 2 * qk:2 * qk + 1], None, ALU.mult)
                    else:
                        if latent:
                            rope_out(P, Psw, ca.v[:, 0:n], sa.v[:, 0:n], ob.v[:, 0:n], n)
                        else:
                            k.cp("act", ob.v[:, 0:n], P.v[:, 0:n])
                    dst = dq[j * 128:(j + 1) * 128, t0:t0 + n] if j < 2 else dk_[:, t0:t0 + n]
                    k.stv("sp", dst, ob.v[:, 0:n])
                for s in range(ns):
                    P = nps()
                    chain_tok(P, [(wabc, base + 384, 0)], 128, s)
                    k.cp("act", tbb.v[:, 0:128], P.v[:, 0:128])
                    k.stv("sp", dv_[t0 + s * 128:t0 + (s + 1) * 128, :], tbb.v[:, 0:128])
            for j in range(4):
                P, Psw = nps(), nps()
                chain(P, [(wabc, 1024 + j * 128, 0)], 128, n)
                g = j % 2
                if latent:
                    chain(Psw, [(wsw, 768 + j * 128, 0)], 128, n)
                    rope_out(P, Psw, cs_.v[:, 0:n], ss_.v[:, 0:n], F[0].v[:, 0:n], n)
                else:
                    k.cp("act", F[0].v[:, 0:n], P.v[:, 0:n])
                x3 = F[0].v[:, 0:n].rr("p (c i) -> p c i", i=CH)
                for d in range(2):
                    if j < 2:
                        k.tt("dve", F[1].v[:, 0:n].rr("p (c i) -> p c i", i=CH), x3,
                             rtab.v[:, g, d, 0:1, :].bc([128, nch, CH]), ALU.mult)
                        st_bf(S["C_Rt"][d, g * 128:(g + 1) * 128, t0:t0 + n], F[1].v[:, 0:n], n)
                    else:
                        k.tt("dve", F[1].v[:, 0:n].rr("p (c i) -> p c i", i=CH), x3,
                             rtab.v[:, g, d, 1:2, :].bc([128, nch, CH]), ALU.mult)
                        st_bf(S["C_Kt"][d, g * 128:(g + 1) * 128, t0:t0 + n], F[1].v[:, 0:n], n)
                        k.tt("dve", F[2].v[:, 0:n].rr("p (c i) -> p c i", i=CH), x3,
                             rtab.v[:, g, d, 2:3, :].bc([128, nch, CH]), ALU.mult)
                        for s in range(ns):
                            pt = nps()
                            k.tr(pt.v[:, 0:128], F[2].v[:, s * 128:(s + 1) * 128], identf.v[:, :])
                            k.cp("act", tbb.v[:, 0:128], pt.v[:, 0:128])
                            k.stv("sp", S["C_Kh"][d, t0 + s * 128:t0 + (s + 1) * 128, g * 128:(g + 1) * 128],
                                  tbb.v[:, 0:128])
            for s in range(ns):
                P = nps()
                chain_tok(P, [(wabc, 1536, 0)], 256, s)
                k.cp("act", tbb.v[:, 0:256], P.v[:, 0:256])
                k.stv("sp", S["C_V"][t0 + s * 128:t0 + (s + 1) * 128, :], tbb.v[:, 0:256])
                P = nps()
                chain_tok(P, [(wabc, 1792, 0)], 512, s)
                k.act(tb.v[:, 0:512], P.v[:, 0:512], ACT.Silu)
                k.stv("sp", S["C_G"][t0 + s * 128:t0 + (s + 1) * 128, :], tb.v[:, 0:512])
            P = nps()
            chain(P, [(wgd, 0, 0)], 128, n)
            k.act(sgb.v[:, 0:n], P.v[:, 0:n], ACT.Sigmoid)
            for s in range(ns):
                P = nps()
                k.mm(P.v[:, 0:256], sgb.v[:, s * 128:(s + 1) * 128], g2s.v[:, :])
                k.cp("act", tb.v[:, 0:256], P.v[:, 0:256])
                k.stv("sp", S["D_G"][t0 + s * 128:t0 + (s + 1) * 128, :], tb.v[:, 0:256])
            for d in range(2):
                sh = -1 if d == 0 else 1
                wl = lambda c0: [(wdb[d], c0, 0), (wda[d], c0, sh)]
                P = nps()
                chain(P, wl(768), 64, n)
                k.act(thb.v[:, 0:n], P.v[0:64, 0:n], ACT.Tanh)
                P = nps()
                chain(P, wl(832), 64, n)
                k.cp("act", alb.v[:, 0:n], P.v[0:64, 0:n])
                for s in range(ns):
                    P = nps()
                    chain_tok(P, wl(512), 256, s)
                    k.cp("act", tbb.v[:, 0:256], P.v[:, 0:256])
                    k.stv("sp", S["D_V"][d, t0 + s * 128:t0 + (s + 1) * 128, :], tbb.v[:, 0:256])
                def dgroup(g, FB, pcx):
                    r_, k_, lw, cs, a_, kk, e_, t1, t2 = FB
                    gc = slice(g, g + 1)
                    P = nps()
                    chain(P, wl(g * 128), 128, n)
                    k.cp("act", r_.v[:, 0:n], P.v[:, 0:n])
                    P = nps()
                    chain(P, wl(256 + g * 128), 128, n)
                    k.cp("act", k_.v[:, 0:n], P.v[:, 0:n])
                    yield
                    P = nps()
                    k.mm(P.v[:, 0:n], w2s.v[:, d, g * 128:(g + 1) * 128], thb.v[:, 0:n])
                    k.act(lw.v[:, 0:n], P.v[:, 0:n], ACT.Sigmoid, bias=pc.v[:, 4 + 2 * d + g:5 + 2 * d + g])
                    k.ts("dve", lw.v[:, 0:n], lw.v[:, 0:n], -DECAY_SCALE, None, ALU.mult)
                    yield
                    P = nps()
                    k.mm(P.v[:, 0:n], a2s.v[:, d, g * 128:(g + 1) * 128], alb.v[:, 0:n])
                    k.act(a_.v[:, 0:n], P.v[:, 0:n], ACT.Sigmoid, bias=pc.v[:, 8 + 2 * d + g:9 + 2 * d + g])
                    yield
                    k.ts("dve", kk.v[:, 0:n], k_.v[:, 0:n], pc.v[:, 12 + g:13 + g], None, ALU.mult)
                    k.act(t1.v[:, 0:n], kk.v[:, 0:n], ACT.Square)
                    yield
                    P = nps()
                    k.mm(P.v[:, 0:n], bd.v[:, :], t1.v[:, 0:n])
                    k.act(t1.v[:, 0:n], P.v[:, 0:n], ACT.Sqrt)
                    yield
                    k.ts("dve", t1.v[:, 0:n], t1.v[:, 0:n], 1e-12, None, ALU.max)
                    k.recip(t1.v[:, 0:n], t1.v[:, 0:n])
                    yield
                    k.tt("dve", kk.v[:, 0:n], kk.v[:, 0:n], t1.v[:, 0:n], ALU.mult)
                    yield
                    k.ts("dve", t1.v[:, 0:n], a_.v[:, 0:n], pc.v[:, 14 + g:15 + g], oma.v[:, gc], ALU.mult, ALU.add)
                    k.tt("dve", k_.v[:, 0:n], k_.v[:, 0:n], t1.v[:, 0:n], ALU.mult)
                    yield
                    k.stt("dve", t1.v[:, 0:n], r_.v[:, 0:n], pc.v[:, 16 + 2 * d + g:17 + 2 * d + g], k_.v[:, 0:n],
                          ALU.mult, ALU.mult)
                    for s in range(ns):
                        P = nps()
                        k.mm(P.v[:, 0:2], t1.v[:, s * 128:(s + 1) * 128], e2.v[:, :])
                        k.cp("act", bs4.v[:, s, 2 * g:2 * g + 2], P.v[:, 0:2])
                    k.tt("dve", a_.v[:, 0:n], a_.v[:, 0:n], kk.v[:, 0:n], ALU.mult)
                    yield
                    src, dst = lw, cs
                    k.cp("act", t2.v[:, 0:n], lw.v[:, 0:n])
                    src = t2
                    for stp in (1, 2, 4, 8, 16, 32):
                        s3 = src.v[:, 0:n].rr("p (c i) -> p c i", i=CH)
                        d3 = dst.v[:, 0:n].rr("p (c i) -> p c i", i=CH)
                        if d == 0:
                            k.tt("dve", d3[:, :, stp:], s3[:, :, stp:], s3[:, :, :CH - stp], ALU.add)
                            k.cp("act", d3[:, :, :stp], s3[:, :, :stp])
                        else:
                            k.tt("dve", d3[:, :, :CH - stp], s3[:, :, :CH - stp], s3[:, :, stp:], ALU.add)
                            k.cp("act", d3[:, :, CH - stp:], s3[:, :, CH - stp:])
                        src, dst = dst, src
                        yield
                    cs = src
                    spare = dst
                    cs3 = cs.v[:, 0:n].rr("p (c i) -> p c i", i=CH)
                    tot3 = cs3[:, :, CH - 1:CH] if d == 0 else cs3[:, :, 0:1]
                    k.act(e_.v[:, 0:n], cs.v[:, 0:n], ACT.Exp)
                    yield
                    k.tt("dve", t1.v[:, 0:n], r_.v[:, 0:n], e_.v[:, 0:n], ALU.mult)
                    st_bf(S["D_Rt"][d, g * 128:(g + 1) * 128, t0:t0 + n], t1.v[:, 0:n], n)
                    yield
                    k.tt("dve", e_.v[:, 0:n], cs.v[:, 0:n], lw.v[:, 0:n], ALU.subtract)
                    k.act(e_.v[:, 0:n], e_.v[:, 0:n], ACT.Exp)
                    yield
                    k.stt("dve", t1.v[:, 0:n], kk.v[:, 0:n], -1.0, e_.v[:, 0:n], ALU.mult, ALU.mult)
                    st_bf(S["D_At"][d, g * 128:(g + 1) * 128, t0:t0 + n], t1.v[:, 0:n], n)
                    yield
                    for s in range(ns):
                        pt = nps()
                        k.tr(pt.v[:, 0:128], t1.v[:, s * 128:(s + 1) * 128], identf.v[:, :])
                        k.cp("act", tbb.v[:, 0:128], pt.v[:, 0:128])
                        k.stv("sp", S["D_Atok"][d, t0 + s * 128:t0 + (s + 1) * 128, g * 128:(g + 1) * 128],
                              tbb.v[:, 0:128])
                    k.act(e_.v[:, 0:n], cs.v[:, 0:n], ACT.Exp, scale=-1.0)
                    k.tt("dve", t1.v[:, 0:n], a_.v[:, 0:n], e_.v[:, 0:n], ALU.mult)
                    st_bf(S["D_Bt"][d, g * 128:(g + 1) * 128, t0:t0 + n], t1.v[:, 0:n], n)
                    yield
                    k.tt("dve", t1.v[:, 0:n], k_.v[:, 0:n], e_.v[:, 0:n], ALU.mult)
                    st_bf(S["D_Kt"][d, g * 128:(g + 1) * 128, t0:t0 + n], t1.v[:, 0:n], n)
                    yield
                    k.act(pcx.v[:, 0:nch], tot3.rr("p c o -> p (c o)"), ACT.Exp)
                    k.stv("sp", S["D_PC"][d, g * 128:(g + 1) * 128, t0 // CH:t0 // CH + nch], pcx.v[:, 0:nch])
                    k.tt("dve", e_.v[:, 0:n].rr("p (c i) -> p c i", i=CH), tot3.bc([128, nch, CH]), cs3, ALU.subtract)
                    k.act(e_.v[:, 0:n], e_.v[:, 0:n], ACT.Exp)
                    yield
                    for (srcb, dstd) in ((a_, S["D_Bh"]), (k_, S["D_Kh"])):
                        k.tt("dve", t1.v[:, 0:n], srcb.v[:, 0:n], e_.v[:, 0:n], ALU.mult)
                        for s in range(ns):
                            pt = nps()
                            k.tr(pt.v[:, 0:128], t1.v[:, s * 128:(s + 1) * 128], identf.v[:, :])
                            k.cp("act", tbb.v[:, 0:128], pt.v[:, 0:128])
                            k.stv("sp", dstd[d, t0 + s * 128:t0 + (s + 1) * 128, g * 128:(g + 1) * 128], tbb.v[:, 0:128])
                gens = [dgroup(0, F[0:9], pcx), dgroup(1, F2, pcx2)]
                live = list(gens)
                while live:
                    for g_ in list(live):
                        try:
                            next(g_)
                        except StopIteration:
                            live.remove(g_)
                for s in range(ns):
                    k.stv("sp", S["D_BS"][d, t0 + s * 128:t0 + (s + 1) * 128, :], bs4.v[:, s, :])


def attnphase(k, st, cfg):
    S, C, Wt = cfg["S"], cfg["C"], cfg["W"]
    T, CTX = cfg["T"], cfg["CTX"]
    TT = T + CTX
    NKC = TT // 128
    need_ctx = cfg["need_ctx"]
    kT = k.sb(st, [128, TT], BF16, "kT")
    va = k.sb(st, [128, NKC, 65], BF16, "va")
    qT = [k.sb(st, [128, 512], BF16, "qT%d" % i) for i in range(2)]
    for c0 in range(0, TT, 2048):
        k.memset("dve", kT.v[64:128, c0:min(TT, c0 + 2048)], 0.0)
    for q_ in qT:
        k.memset("dve", q_.v[64:128, :], 0.0)
    pT = [k.sb(st, [128, 512], BF16, "pT%d" % i) for i in range(3)]
    ot = [k.sb(st, [128, 64], F32, "ot%d" % i) for i in range(2)]
    rc = k.sb(st, [128, 1], F32, "rc")
    esk = k.sb(st, [128, 4], F32, "esk")
    mprev = k.sb(st, [128, 128], BF16, "mprev")
    mnext = k.sb(st, [128, 128], BF16, "mnext")
    k.ld("pool", mprev.v[:, :], C["mprev"])
    k.ld("pool", mnext.v[:, :], C["mnext"])
    k.ld("sp", esk.v[:, :], Wt["sink"].partition_broadcast(128))
    k.act(esk.v[:, :], esk.v[:, :], ACT.Exp)
    ps_s = [k.ps(st, [128, 512], F32, "pss%d" % i) for i in range(3)]
    ps_o = [k.ps(st, [128, 512], F32, "pso%d" % i) for i in range(2)]
    ps_f = k.ps(st, [128, 512], F32, "psf")
    osb = k.sb(st, [128, 512], F32, "osb")
    identf = k.sb(st, [128, 128], F32, "identf")
    k.ld("sp", identf.v[:, :], C["ident"])
    cnt = dict(s=0, q=0, o=0, p=0, a=0)
    scale = 64 ** -0.5

    def load_kv(kd, vd, g):
        k.ld("sp", kT.v[0:64, :], kd[g * 64:(g + 1) * 64, :])
        k.memset("dve", va.v[:, :, 64:65], 1.0)
        k.ld("sp", va.v[:, :, 0:64], vd[:, g * 64:(g + 1) * 64].rearrange("(c p) j -> p c j", p=128))

    def finish(po, nq_sub, t0, col, sinkcol):
        nq = nq_sub * 128
        k.cp("act", osb.v[0:65, 0:nq], po.v[0:65, 0:nq])
        for s in range(nq_sub):
            k.tr(ps_f.v[:, s * 65:(s + 1) * 65], osb.v[0:65, s * 128:(s + 1) * 128], identf.v[0:65, 0:65])
        for s in range(nq_sub):
            o = ot[cnt["o"] % 2]
            cnt["o"] += 1
            den = ps_f.v[:, s * 65 + 64:s * 65 + 65]
            if sinkcol is None:
                k.recip(rc.v[:, :], den)
            else:
                k.tt("dve", rc.v[:, :], den, esk.v[:, sinkcol:sinkcol + 1], ALU.add)
                k.recip(rc.v[:, :], rc.v[:, :])
            k.ts("dve", o.v[:, :], ps_f.v[:, s * 65:s * 65 + 64], rc.v[:, 0:1], None, ALU.mult)
            k.stv("sp", S["O"][t0 + s * 128:t0 + (s + 1) * 128, col:col + 64], o.v[:, :])

    def qk(q, nq, kc):
        ps = ps_s[cnt["s"] % 3]
        cnt["s"] += 1
        k.mm(ps.v[:, 0:nq], kT.v[:, kc * 128:(kc + 1) * 128], q.v[:, 0:nq])
        return ps

    def prologue(b):
        if b.get("kv") is not None:
            load_kv(*b["kv"])
        q = qT[cnt["q"] % 2]
        cnt["q"] += 1
        k.ld("sp", q.v[0:64, 0:b["nq"]], b["qd"][b["h"] * 64:(b["h"] + 1) * 64, b["t0"]:b["t0"] + b["nq"]])
        return q, qk(q, b["nq"], b["chunks"][0][0])

    def body(b, q, nxt):
        nq, chunks = b["nq"], b["chunks"]
        po = b["po"]
        assert len(chunks[0][1]) == nq // 128
        for i, (kc, subs) in enumerate(chunks):
            ps = nxt
            if i + 1 < len(chunks):
                nxt = qk(q, nq, chunks[i + 1][0])
            p = pT[cnt["p"] % 3]
            cnt["p"] += 1
            k.act(p.v[:, 0:nq], ps.v[:, 0:nq], ACT.Exp, scale=scale)
            for (s, m_) in subs:
                if m_ is not None:
                    k.tt("dve", p.v[:, s * 128:(s + 1) * 128], p.v[:, s * 128:(s + 1) * 128], m_.v[:, :], ALU.mult)
            c_lo = min(s for (s, _) in subs) * 128
            c_hi = (max(s for (s, _) in subs) + 1) * 128
            k.mm(po.v[0:65, c_lo:c_hi], va.v[:, kc, :], p.v[:, c_lo:c_hi], i == 0, i == len(chunks) - 1)

    blocks = []

    def block(qd, h, t0, nq, chunks, col, sinkcol, kv=None):
        blocks.append(dict(qd=qd, h=h, t0=t0, nq=nq, chunks=chunks, col=col, sinkcol=sinkcol, kv=kv,
                           po=ps_o[len(blocks) % 2]))

    for g in range(2):
        kv = (S["KB"], S["VB"], g)
        for h in (2 * g, 2 * g + 1):
            for t0 in range(0, T, 512):
                nq = min(512, T - t0)
                block(S["QB"], h, t0, nq, [(kc, [(s, None) for s in range(nq // 128)]) for kc in range(NKC)],
                      256 + h * 64, None, kv)
                kv = None
            if need_ctx:
                block(S["QB"], h, T, CTX, [(kc, [(s, None) for s in range(CTX // 128)]) for kc in range(T // 128, NKC)],
                      256 + h * 64, None)
    nb = T // 128
    cchunks = list(range(T // 128, NKC))
    for g in range(2):
        kv = (S["KA"], S["VA"], g)
        for h in (2 * g, 2 * g + 1):
            for b0 in range(0, nb, 4):
                nsb = min(4, nb - b0)
                chunks = [(kc, [(s, None) for s in range(nsb)]) for kc in cchunks]
                for kc in range(max(0, b0 - 1), min(nb, b0 + nsb + 1)):
                    subs = []
                    for s in range(nsb):
                        dlt = kc - (b0 + s)
                        if dlt == -1:
                            subs.append((s, mprev))
                        elif dlt == 0:
                            subs.append((s, None))
                        elif dlt == 1:
                            subs.append((s, mnext))
                    chunks.append((kc, subs))
                block(S["QA"], h, b0 * 128, nsb * 128, chunks, h * 64, h, kv)
                kv = None
            if need_ctx:
                block(S["QA"], h, T, CTX, [(kc, [(s, None) for s in range(CTX // 128)]) for kc in cchunks], h * 64, h)
    st_ = prologue(blocks[0])
    for i, b in enumerate(blocks):
        body(b, *st_)
        if i + 1 < len(blocks):
            st_ = prologue(blocks[i + 1])
        finish(b["po"], b["nq"] // 128, b["t0"], b["col"], b["sinkcol"])


def scanphase(k, st, cfg):
    S, C = cfg["S"], cfg["C"]
    T, CTX = cfg["T"], cfg["CTX"]
    TT = T + CTX
    dplr = cfg["dplr"]
    pre = "D_" if dplr else "C_"
    NCG = 2
    GT_ = NCG * CH
    NCHT = TT // CH
    f3 = lambda b: b.v[:, :, :]
    t64 = lambda nm: k.sb(st, [64, 8, 64], F32, nm)
    b64 = lambda nm: k.sb(st, [64, 8, 64], BF16, nm)
    chm = lambda nm: k.sb(st, [64, 8, GT_], BF16, nm)
    tkm = lambda nm: k.sb(st, [64, NCG, 8, 64], BF16, nm)
    RT = [chm("RT%d" % i) for i in range(2)]
    KT = [chm("KT%d" % i) for i in range(2)]
    KH = [tkm("KH%d" % i) for i in range(2)]
    VV = [tkm("VV%d" % i) for i in range(2)]
    Min = t64("Min")
    k.ld("sp", f3(Min), C["m_in"].rearrange("p (q t) -> p q t", q=8))
    id8 = t64("id8")
    k.ld("sp", f3(id8), C["id8"].rearrange("p (q t) -> p q t", q=8))
    if dplr:
        AT = [chm("AT%d" % i) for i in range(2)]
        BT = [chm("BT%d" % i) for i in range(2)]
        BH = [tkm("BH%d" % i) for i in range(2)]
        AK = [tkm("AK%d" % i) for i in range(2)]
        Mst, Mts = t64("Mst"), t64("Mts")
        k.ld("sp", f3(Mst), C["m_st"].rearrange("p (q t) -> p q t", q=8))
        k.ld("sp", f3(Mts), C["m_ts"].rearrange("p (q t) -> p q t", q=8))
        PCt = k.sb(st, [64, 8, NCHT], F32, "PCt")
        for d in range(2):
            k.ld("sp", PCt.v[:, d * 4:(d + 1) * 4, :], S["D_PC"][d].rearrange("(h j) c -> j h c", j=64))
    else:
        PCr = k.sb(st, [64, 8], F32, "PCr")
        k.ld("sp", PCr.v[:, :], C["ret_pc"])
        GTc = b64("GTc")
        k.tt("dve", f3(GTc), f3(id8), PCr.v[:, :].rr("p (q o) -> p q o", o=1).bc([64, 8, 64]), ALU.mult)
    sets = []
    for c in range(NCG):
        B = dict(ArkT=b64("ArkT%d" % c), Yv=t64("Yv%d" % c), H=t64("H%d" % c))
        if dplr:
            B.update(Ls=[b64("L%d_%d" % (i, c)) for i in range(5)], Ns=[b64("N%d_%d" % (i, c)) for i in range(6)],
                     AakT=b64("AakT%d" % c), ArbT=b64("ArbT%d" % c), Qe=b64("Qe%d" % c), GTt=b64("GTt%d" % c),
                     X=[k.sb(st, [64, 8, 128], F32, "X%d_%d" % (i, c)) for i in range(2)],
                     Xb=k.sb(st, [64, 8, 128], BF16, "Xb_%d" % c),
                     psX=k.ps(st, [64, 1024], F32, "psX%d" % c))
        sets.append(B)
    ST = [b64("ST%d" % i) for i in range(2)]
    Yt = [t64("Yt%d" % i) for i in range(2)]
    psA = [k.ps(st, [64, 512], F32, "psA%d" % i) for i in range(2)]
    psY = [k.ps(st, [64, 512], F32, "psY%d" % i) for i in range(2)]
    cnt = dict(a=0, y=0, e=0)
    p3 = lambda p: p.v[:, :].rr("p (q t) -> p q t", q=8)

    def npa():
        p = psA[cnt["a"] % 2]
        cnt["a"] += 1
        return p

    def ev():
        cnt["e"] += 1
        return "act" if cnt["e"] % 2 else "dve"

    def prod(dst, lhs, rhs, mask):
        p = npa()
        for q in range(8):
            k.mm(p3(p)[:, q, :], lhs(q), rhs(q))
        if mask is None:
            k.cp(ev(), f3(dst), p3(p))
        else:
            k.tt("dve", f3(dst), p3(p), f3(mask), ALU.mult)

    def views(rb, cl):
        sl = {d: slice(cl[d] * CH, (cl[d] + 1) * CH) for d in range(2)}
        dq = lambda q: q // 4
        chs = lambda buf: (lambda q: buf[rb].v[:, q, sl[dq(q)]])
        tks = lambda buf: (lambda q: buf[rb].v[:, cl[dq(q)], q, :])
        return sl, chs, tks

    def ppart(B, rb, cl, grp):
        sl, chs, tks = views(rb, cl)
        rt, kt, kh, vv = chs(RT), chs(KT), tks(KH), tks(VV)
        prod(B["ArkT"], kt, rt, Min)
        yield
        if dplr:
            at, bt, bh = chs(AT), chs(BT), tks(BH)
            Ls, Ns, AakT, ArbT = B["Ls"], B["Ns"], B["AakT"], B["ArbT"]
            prod(Ns[0], bt, at, Mst)
            prod(Ls[0], at, bt, Mts)
            yield
            prod(AakT, kt, at, Mst)
            prod(ArbT, bt, rt, Min)
            yield
            xc, xn = B["X"]
            xb = B["Xb"]
            for d in range(2):
                k.cp("act", xc.v[:, d * 4:(d + 1) * 4, 0:64], AK[rb].v[:, cl[d], d * 4:(d + 1) * 4, :])
            p = npa()
            for q in range(8):
                k.mm(p3(p)[:, q, :], AakT.v[:, q, :], vv(q))
            k.cp("dve", xc.v[:, :, 64:128], p3(p))
            k.cp("act", f3(xb), f3(xc))
            yield
            px = B["psX"].v[:, :].rr("p (q c) -> p q c", q=8)
            for lv in range(6):
                for q in range(8):
                    k.mm(px[:, q, :], Ns[lv].v[:, q, :], xb.v[:, q, :])
                if lv < 5:
                    if lv < 4:
                        prod(Ls[lv + 1], lambda q: Ns[lv].v[:, q, :], lambda q: Ls[lv].v[:, q, :], None)
                    prod(Ns[lv + 1], lambda q: Ls[lv].v[:, q, :], lambda q: Ns[lv].v[:, q, :], None)
                k.tt("dve", f3(xn), f3(xc), px, ALU.add)
                xc, xn = xn, xc
                k.cp("act", f3(xb), f3(xc))
                yield
            wq = lambda q: xb.v[:, q, 0:64]
            uv = lambda q: xb.v[:, q, 64:128]
            p = npa()
            for q in range(8):
                k.mm(p3(p)[:, q, :], wq(q), ArbT.v[:, q, :])
            for d in range(2):
                k.tt("dve", B["Qe"].v[:, d * 4:(d + 1) * 4, :], p3(p)[:, d * 4:(d + 1) * 4, :],
                     RT[rb].v[:, d * 4:(d + 1) * 4, sl[d]], ALU.add)
            p = npa()
            for q in range(8):
                k.mm(p3(p)[:, q, :], wq(q), bh(q))
            for d in range(2):
                c_abs = grp[d] * NCG + cl[d]
                k.tt("dve", B["GTt"].v[:, d * 4:(d + 1) * 4, :], id8.v[:, d * 4:(d + 1) * 4, :],
                     PCt.v[:, d * 4:(d + 1) * 4, c_abs:c_abs + 1].bc([64, 4, 64]), ALU.mult)
            k.tt("dve", f3(B["GTt"]), f3(B["GTt"]), p3(p), ALU.add)
            yield
        p = npa()
        for q in range(8):
            k.mm(p3(p)[:, q, :], B["ArkT"].v[:, q, :], vv(q), True, not dplr)
            if dplr:
                k.mm(p3(p)[:, q, :], ArbT.v[:, q, :], uv(q), False, True)
        k.cp(ev(), f3(B["Yv"]), p3(p))
        p = npa()
        for q in range(8):
            k.mm(p3(p)[:, q, :], kh(q), vv(q), True, not dplr)
            if dplr:
                k.mm(p3(p)[:, q, :], bh(q), uv(q), False, True)
        k.cp(ev(), f3(B["H"]), p3(p))
        yield

    k.memset("dve", f3(ST[0]), 0.0)
    ngl = T // GT_
    ncg_ctx = CTX // GT_
    fw = list(range(ngl, ngl + ncg_ctx)) + list(range(ngl))
    bw = list(range(ngl + ncg_ctx - 1, ngl - 1, -1)) + list(range(ngl - 1, -1, -1))
    order = {0: fw, 1: bw}
    step = 0
    for gi in range(ngl + ncg_ctx):
        rb = gi % 2
        grp = {d: order[d][gi] for d in range(2)}
        for d in range(2):
            t0 = grp[d] * GT_
            qs = slice(d * 4, (d + 1) * 4)
            chv = lambda nm: S[pre + nm][d, :, t0:t0 + GT_].rearrange("(h j) t -> j h t", j=64)
            tkv = lambda ap: ap[t0:t0 + GT_, :].rearrange("(n t) (h j) -> t n h j", t=CH, j=64)
            k.ld("sp", RT[rb].v[:, qs, :], chv("Rt"))
            k.ld("sp", KT[rb].v[:, qs, :], chv("Kt"))
            k.ld("sp", KH[rb].v[:, :, qs, :], tkv(S[pre + "Kh"][d]))
            k.ld("sp", VV[rb].v[:, :, qs, :], tkv(S["D_V"][d] if dplr else S["C_V"]))
            if dplr:
                k.ld("sp", AT[rb].v[:, qs, :], chv("At"))
                k.ld("sp", BT[rb].v[:, qs, :], chv("Bt"))
                k.ld("sp", BH[rb].v[:, :, qs, :], tkv(S["D_Bh"][d]))
                k.ld("sp", AK[rb].v[:, :, qs, :], tkv(S["D_Atok"][d]))
        cls = [{0: ci, 1: NCG - 1 - ci} for ci in range(NCG)]
        gens = [ppart(sets[ci], rb, cls[ci], grp) for ci in range(NCG)]
        live = list(gens)
        while live:
            for g_ in list(live):
                try:
                    next(g_)
                except StopIteration:
                    live.remove(g_)
        for ci in range(NCG):
            B, cl = sets[ci], cls[ci]
            sl, chs, tks = views(rb, cl)
            Sc, Sn = ST[step % 2], ST[(step + 1) % 2]
            qe = (lambda q: B["Qe"].v[:, q, :]) if dplr else chs(RT)
            gt = (lambda q: B["GTt"].v[:, q, :]) if dplr else (lambda q: GTc.v[:, q, :])
            py = psY[cnt["y"] % 2]
            cnt["y"] += 1
            for q in range(8):
                k.mm(p3(py)[:, q, :], qe(q), Sc.v[:, q, :])
            p = npa()
            for q in range(8):
                k.mm(p3(p)[:, q, :], gt(q), Sc.v[:, q, :])
            k.tt("dve", f3(Sn), p3(p), f3(B["H"]), ALU.add)
            yt = Yt[step % 2]
            k.tt("dve", f3(yt), p3(py), f3(B["Yv"]), ALU.add)
            for d in range(2):
                tt0 = grp[d] * GT_ + cl[d] * CH
                k.stv("sp", S[pre + "Y"][d, tt0:tt0 + CH, :].rearrange("t (h i) -> t h i", i=64),
                      yt.v[:, d * 4:(d + 1) * 4, :])
            step += 1


GN_EPS = 64e-5


def postphase(k, st, cfg):
    S, Wt = cfg["S"], cfg["W"]
    TT = cfg["T"] + cfg["CTX"]
    row = lambda nm, ap: (lambda t: (k.ld("sp", t.v[:, :], ap.partition_broadcast(128)), t)[1])(k.sb(st, [128, 256], F32, nm))
    retg = row("retg", Wt["ret_g"])
    lng = row("lng", Wt["ln_g"])
    lnb = row("lnb", Wt["ln_b"])
    gc = k.sb(st, [128, 512], F32, "gc")
    gd = k.sb(st, [128, 256], F32, "gd")
    vd = k.sb(st, [128, 4, 64], BF16, "vd")
    vdf = k.sb(st, [128, 4, 64], F32, "vdf")
    bs = k.sb(st, [128, 4], F32, "bs")

    def bufset(nm):
        return dict(y=[k.sb(st, [128, 4, 64], F32, nm + "y%d" % i) for i in range(2)],
                    yc=k.sb(st, [128, 4, 64], F32, nm + "yc"), sq=k.sb(st, [128, 4, 64], F32, nm + "sq"),
                    m4=k.sb(st, [128, 4], F32, nm + "m4"), v4=k.sb(st, [128, 4], F32, nm + "v4"),
                    acc=k.sb(st, [128, 256], F32, nm + "acc"), yn=k.sb(st, [128, 256], F32, nm + "yn"), i=0)

    BC, BD = bufset("c"), bufset("d")

    def head_norm(B, yb, g, b, dst):
        yc, sq, m4, v4 = B["yc"], B["sq"], B["m4"], B["v4"]
        k.op("dve", lambda e: e.tensor_reduce(out=m4[:, :], in_=yb[:, :, :], axis=AX.X, op=ALU.add), r=[yb], w=[m4])
        k.ts("dve", m4.v[:, :], m4.v[:, :], 1.0 / 64, None, ALU.mult)
        yield
        k.tt("dve", yc.v[:, :, :], yb.v[:, :, :], m4.v[:, :].rr("p (h o) -> p h o", o=1).bc([128, 4, 64]), ALU.subtract)
        yield
        k.tt("dve", sq.v[:, :, :], yc.v[:, :, :], yc.v[:, :, :], ALU.mult)
        yield
        k.op("dve", lambda e: e.tensor_reduce(out=v4[:, :], in_=sq[:, :, :], axis=AX.X, op=ALU.add), r=[sq], w=[v4])
        yield
        k.ts("dve", v4.v[:, :], v4.v[:, :], 1.0 / 64, GN_EPS, ALU.mult, ALU.add)
        yield
        k.act(v4.v[:, :], v4.v[:, :], ACT.Sqrt)
        yield
        k.recip(v4.v[:, :], v4.v[:, :])
        yield
        k.tt("dve", yc.v[:, :, :], yc.v[:, :, :], v4.v[:, :].rr("p (h o) -> p h o", o=1).bc([128, 4, 64]), ALU.mult)
        yield
        k.tt("dve", dst, yc.v[:, :, :].rr("p h i -> p (h i)"), g.v[:, :], ALU.mult)
        if b is not None:
            k.tt("dve", dst, dst, b.v[:, :], ALU.add)
        yield

    def part_c(tsl):
        B = BC
        acc, yn = B["acc"], B["yn"]
        k.ld("sp", gc.v[:, :], S["C_G"][tsl, :])
        for d in range(2):
            yb = B["y"][B["i"] % 2]
            B["i"] += 1
            k.ld("sp", yb.v[:, :, :], S["C_Y"][d, tsl, :].rearrange("t (h i) -> t h i", i=64))
            yield from head_norm(B, yb, retg, None, yn.v[:, :])
            if d == 0:
                k.tt("dve", acc.v[:, :], yn.v[:, :], gc.v[:, 0:256], ALU.mult)
            else:
                k.tt("dve", yn.v[:, :], yn.v[:, :], gc.v[:, 256:512], ALU.mult)
                k.tt("dve", acc.v[:, :], acc.v[:, :], yn.v[:, :], ALU.add)
            yield
        k.stv("sp", S["O"][tsl, 512:768], acc.v[:, :])

    def part_d(tsl):
        B = BD
        acc, yn = B["acc"], B["yn"]
        k.ld("sp", gd.v[:, :], S["D_G"][tsl, :])
        for d in range(2):
            yb = B["y"][B["i"] % 2]
            B["i"] += 1
            k.ld("sp", yb.v[:, :, :], S["D_Y"][d, tsl, :].rearrange("t (h i) -> t h i", i=64))
            k.ld("sp", vd.v[:, :, :], S["D_V"][d, tsl, :].rearrange("t (h i) -> t h i", i=64))
            k.ld("sp", bs.v[:, :], S["D_BS"][d, tsl, :])
            yield from head_norm(B, yb, lng, lnb, yn.v[:, :])
            k.tt("dve", vdf.v[:, :, :], vd.v[:, :, :], bs.v[:, :].rr("p (h o) -> p h o", o=1).bc([128, 4, 64]), ALU.mult)
            k.tt("dve", yn.v[:, :], yn.v[:, :], vdf.v[:, :, :].rr("p h i -> p (h i)"), ALU.add)
            yield
            if d == 0:
                k.cp("act", acc.v[:, :], yn.v[:, :])
            else:
                k.tt("dve", acc.v[:, :], acc.v[:, :], yn.v[:, :], ALU.add)
            yield
        k.tt("dve", acc.v[:, :], acc.v[:, :], gd.v[:, :], ALU.mult)
        k.stv("sp", S["O"][tsl, 768:1024], acc.v[:, :])

    for t0 in range(0, TT, 128):
        tsl = slice(t0, t0 + 128)
        live = [part_c(tsl), part_d(tsl)]
        while live:
            for g_ in list(live):
                try:
                    next(g_)
                except StopIteration:
                    live.remove(g_)


def modphase(k, st, cfg):
    cv = k.sb(st, [128, 2, 8], F32, "cv")
    for r in range(2):
        k.ld("sp", cv.v[:, r, :], cfg["cvec"][r].rearrange("(c p) -> p c", p=128), allow_slow_non_contiguous=True)
    sc = k.sb(st, [128, 8, 2], F32, "sc")
    k.act(sc.v[:, :, :].rr("p c r -> p r c"), cv.v[:, :, :], ACT.Silu)
    wm = [k.sb(st, [128, 8, 512], F32, "wm%d" % i) for i in range(2)]
    bm = k.sb(st, [2, 9 * D], F32, "bm")
    k.ld("sp", bm.v[:, :], cfg["b_mod"].partition_broadcast(2))
    ob = k.sb(st, [2, 9 * D], F32, "ob")
    pm = [k.ps(st, [128, 512], F32, "pm%d" % i) for i in range(2)]
    wv = cfg["w_mod"].rearrange("(c p) f -> p c f", p=128)
    for j in range(18):
        w = wm[j % 2]
        for c in range(8):
            k.ld("sp", w.v[:, c, :], wv[:, c, j * 512:(j + 1) * 512])
        p = pm[j % 2]
        for c in range(8):
            k.mm(p.v[0:2, :], sc.v[:, c, :], w.v[:, c, :], c == 0, c == 7)
        k.tt("dve", ob.v[:, j * 512:(j + 1) * 512], p.v[0:2, :], bm.v[:, j * 512:(j + 1) * 512], ALU.add)
    k.stv("sp", cfg["mod"], ob.v[:, :])


def build_program(T, CTX, L, stop_after=None):
    TT = T + CTX
    nc = bass.Bass("TRN2", target_bir_lowering=False, dynamic_dma_scratch_size=8192)

    def din(name, shape, dt=F32):
        return nc.dram_tensor(name, list(shape), dt, kind="ExternalInput").ap()

    def scr(name, shape, dt=F32):
        return nc.dram_tensor(name, list(shape), dt, kind="Internal").ap()

    I = dict(
        xin=din("xin", [TT, D]), cvec=din("cvec", [2, D]),
        w_mod=din("w_mod", [L, D, 9 * D]), b_mod=din("b_mod", [L, 9 * D]), norm_g=din("norm_g", [L, 3, D]),
        ffn_w_in=din("ffn_w_in", [L, 2, D, 2 * DFF]), ffn_w_out=din("ffn_w_out", [L, 2, DFF, D]),
        w_in=din("w_in", [L, D, 3456]), w_out=din("w_out", [L, D, D]), final_g=din("final_g", [D]),
        w_sw=din("w_sw", [L, D, 1280]), w_d=din("w_d", [L, 2, D, 896]), pcols=din("pcols", [L, 128, 24]),
        mu=din("mu", [L, 2, 896]), w2=din("w2", [L, 2, 64, 256]), a2=din("a2", [L, 2, 64, 256]),
        g2=din("g2", [L, 128, 256]), ret_g=din("ret_g", [L, 256]), ln_g=din("ln_g", [L, 256]),
        ln_b=din("ln_b", [L, 256]), sink=din("sink", [L, 4]),
    )
    Cn = dict(
        ident=din("ident", [128, 128]), bd64=din("bd64", [128, 128]), e2=din("e2", [128, 2]),
        ropeA_c=din("ropeA_c", [128, T]), ropeA_s=din("ropeA_s", [128, T]),
        ropeS_c=din("ropeS_c", [128, T]), ropeS_s=din("ropeS_s", [128, T]),
        ret_tab=din("ret_tab", [128, 2, 2, 3, CH]), ret_pc=din("ret_pc", [64, 8]),
        m_st=din("m_st", [64, 512]), m_ts=din("m_ts", [64, 512]), m_in=din("m_in", [64, 512]),
        id8=din("id8", [64, 512]), mprev=din("mprev", [128, 128]), mnext=din("mnext", [128, 128]),
    )
    out = nc.dram_tensor("out", [T, D], F32, kind="ExternalOutput").ap()
    dbg = stop_after is not None
    mk = (lambda name, shape, dt=F32: nc.dram_tensor(name, list(shape), dt, kind="ExternalOutput").ap()) if dbg else scr
    S = dict(
        xres=mk("xres", [TT, D]), mod=mk("modr", [2, 9 * D]), hT=mk("hT", [D, TT], BF16),
        QA=mk("QA", [256, TT], BF16), KA=mk("KA", [128, TT], BF16), VA=mk("VA", [TT, 128], BF16),
        QB=mk("QB", [256, TT], BF16), KB=mk("KB", [128, TT], BF16), VB=mk("VB", [TT, 128], BF16),
        O=mk("O", [TT, 1024]),
        C_Rt=mk("C_Rt", [2, 256, TT], BF16), C_Kt=mk("C_Kt", [2, 256, TT], BF16), C_Kh=mk("C_Kh", [2, TT, 256], BF16),
        C_V=mk("C_V", [TT, 256], BF16), C_G=mk("C_G", [TT, 512]), C_Y=mk("C_Y", [2, TT, 256]),
        D_Rt=mk("D_Rt", [2, 256, TT], BF16), D_Kt=mk("D_Kt", [2, 256, TT], BF16), D_At=mk("D_At", [2, 256, TT], BF16),
        D_Bt=mk("D_Bt", [2, 256, TT], BF16), D_Kh=mk("D_Kh", [2, TT, 256], BF16), D_Bh=mk("D_Bh", [2, TT, 256], BF16),
        D_Atok=mk("D_Atok", [2, TT, 256], BF16), D_V=mk("D_V", [2, TT, 256], BF16), D_Y=mk("D_Y", [2, TT, 256]),
        D_PC=mk("D_PC", [2, 256, TT // CH]), D_G=mk("D_G", [TT, 256]), D_BS=mk("D_BS", [2, TT, 4]),
    )
    k = K(nc)
    stages = []

    def phase(name, fn, cfg):
        if stop_after is not None and stop_after in stages:
            return
        with contextlib.ExitStack() as st:
            fn(k, st, cfg)
            k.barrier()
        stages.append(name)
        k.marks.append((name, {e.name: e.n for e in k.E.values()}))

    with k.stack:
        segs_all = [(0, T, 0), (T, CTX, 1)]
        for l in range(L):
            last = l == L - 1
            W = dict(w_in=I["w_in"][l], w_sw=I["w_sw"][l], w_d=I["w_d"][l], mu=I["mu"][l], pcols=I["pcols"][l],
                     w2=I["w2"][l], a2=I["a2"][l], g2=I["g2"][l], ret_g=I["ret_g"][l], ln_g=I["ln_g"][l],
                     ln_b=I["ln_b"][l], sink=I["sink"][l])
            base = dict(S=S, C=Cn, W=W, T=T, CTX=CTX)
            phase("mod%d" % l, modphase, dict(cvec=I["cvec"], w_mod=I["w_mod"][l], b_mod=I["b_mod"][l], mod=S["mod"]))
            phase("f1_%d" % l, rowphase, dict(
                xin=I["xin"] if l == 0 else S["xres"], xout=S["xres"], segs=segs_all, mod=S["mod"], ident=Cn["ident"],
                ffn=dict(w_in=I["ffn_w_in"][l, 0], w_out=I["ffn_w_out"][l, 0], g=I["norm_g"][l, 0], shift=0, scale=1,
                         gate=2),
                post=dict(kind="hT", g=I["norm_g"][l, 1], shift=3, scale=4, hT=S["hT"])))
            phase("proj%d" % l, projphase, base)
            phase("attn%d" % l, attnphase, dict(base, need_ctx=not last))
            phase("scanC%d" % l, scanphase, dict(base, dplr=False))
            phase("scanD%d" % l, scanphase, dict(base, dplr=True))
            phase("post%d" % l, postphase, base)
            phase("f2_%d" % l, rowphase, dict(
                xin=S["xres"], xout=S["xres"], segs=segs_all if not last else [(0, T, 0)], mod=S["mod"],
                ident=Cn["ident"],
                pre=dict(o=S["O"], w_out=I["w_out"][l], gate=5),
                ffn=dict(w_in=I["ffn_w_in"][l, 1], w_out=I["ffn_w_out"][l, 1], g=I["norm_g"][l, 2], shift=6, scale=7,
                         gate=8),
                post=dict(kind="final", g=I["final_g"], out=out) if last else None))
    return nc, k


def _consts(T):
    c = {}
    c["ident"] = np.eye(128, dtype=np.float32)
    bd = np.zeros((128, 128), np.float32)
    bd[:64, :64] = 1
    bd[64:, 64:] = 1
    c["bd64"] = bd
    e2 = np.zeros((128, 2), np.float32)
    e2[:64, 0] = 1
    e2[64:, 1] = 1
    c["e2"] = e2
    t = np.arange(T)
    row = (t // 64).astype(np.float32)
    col = (t % 64).astype(np.float32)
    inv16 = (1.0 / (np.float32(10000.0) ** (np.arange(0, 32, 2, dtype=np.float32) / np.float32(32)))).astype(np.float32)
    inv32 = (1.0 / (np.float32(10000.0) ** (np.arange(0, 64, 2, dtype=np.float32) / np.float32(64)))).astype(np.float32)
    ca = np.zeros((64, T), np.float32)
    sa = np.zeros((64, T), np.float32)
    for blk, pos in ((0, row), (32, col)):
        ang = pos[None, :] * inv16[:, None]
        ca[blk:blk + 16] = np.cos(ang)
        ca[blk + 16:blk + 32] = np.cos(ang)
        sa[blk:blk + 16] = -np.sin(ang)
        sa[blk + 16:blk + 32] = np.sin(ang)
    ang = t.astype(np.float32)[None, :] * inv32[:, None]
    cs = np.concatenate([np.cos(ang), np.cos(ang)], 0).astype(np.float32)
    ss = np.concatenate([-np.sin(ang), np.sin(ang)], 0).astype(np.float32)
    c["ropeA_c"] = np.concatenate([ca, ca], 0)
    c["ropeA_s"] = np.concatenate([sa, sa], 0)
    c["ropeS_c"] = np.concatenate([cs, cs], 0)
    c["ropeS_s"] = np.concatenate([ss, ss], 0)
    lg = np.log1p(-np.exp2(-5.0 - np.arange(4, dtype=np.float64)))
    i = np.arange(CH, dtype=np.float64)
    rt = np.zeros((128, 2, 2, 3, CH), np.float64)
    for p in range(128):
        for g in range(2):
            l_ = lg[2 * g + p // 64]
            for d in range(2):
                csum = (i + 1) * l_ if d == 0 else (CH - i) * l_
                rt[p, g, d, 0] = np.exp(csum)
                rt[p, g, d, 1] = np.exp(-csum) / 8.0
                rt[p, g, d, 2] = np.exp(CH * l_ - csum) / 8.0
    c["ret_tab"] = rt.astype(np.float32)
    c["ret_pc"] = np.tile(np.exp(CH * lg)[None, :], (64, 2)).astype(np.float32)
    s_ = np.arange(64)[:, None]
    t_ = np.arange(64)[None, :]
    lt = (s_ < t_).astype(np.float32)
    gt = (s_ > t_).astype(np.float32)
    eye = np.eye(64, dtype=np.float32)
    c["m_st"] = np.concatenate([lt] * 4 + [gt] * 4, 1)
    c["m_ts"] = np.concatenate([gt] * 4 + [lt] * 4, 1)
    c["m_in"] = np.concatenate([lt + eye] * 4 + [gt + eye] * 4, 1)
    c["id8"] = np.concatenate([eye] * 8, 1)
    j = np.arange(128)[:, None]
    ii = np.arange(128)[None, :]
    c["mprev"] = (ii <= j).astype(np.float32)
    c["mnext"] = (j <= ii).astype(np.float32)
    return c


def _host_maps(inp, T, CTX, L):
    f = lambda a: np.ascontiguousarray(np.asarray(a, dtype=np.float32))
    w_in = f(inp["w_in"])
    pa = np.concatenate([np.arange(16, 32), np.arange(0, 16), np.arange(48, 64), np.arange(32, 48)])
    psq = np.concatenate([np.arange(32, 64), np.arange(0, 32)])
    cols = []
    for base, nh, perm in ((0, 4, pa), (256, 2, pa), (512, 4, pa), (768, 2, pa), (1024, 4, psq), (1280, 4, psq)):
        for h in range(nh):
            cols.append(base + h * 64 + perm)
    cols = np.concatenate(cols)
    w_sw = np.ascontiguousarray(w_in[:, :, cols])
    dcols = [np.concatenate([np.arange(2304, 3072), np.arange(3200, 3264), np.arange(3328, 3392)]),
             np.concatenate([np.arange(2304, 3072), np.arange(3264, 3328), np.arange(3392, 3456)])]
    w_d = np.ascontiguousarray(np.stack([w_in[:, :, dc] for dc in dcols], 1))
    qkg = f(inp["qk_norm_g"])
    pcols = np.zeros((L, 128, 24), np.float32)
    two = lambda v: np.ascontiguousarray(v.reshape(2, 128).T)
    for l in range(L):
        for qk in range(2):
            g = qkg[l, qk]
            pcols[l, :, 2 * qk] = np.tile(g, 2)
            pcols[l, :, 2 * qk + 1] = np.tile(g[pa], 2)
        for d in range(2):
            pcols[l, :, 4 + 2 * d:6 + 2 * d] = two(f(inp["rwkv_w0"])[l, d])
            pcols[l, :, 8 + 2 * d:10 + 2 * d] = two(f(inp["rwkv_a0"])[l, d])
            pcols[l, :, 16 + 2 * d:18 + 2 * d] = two(f(inp["rwkv_rho"])[l, d].reshape(256))
        pcols[l, :, 12:14] = two(f(inp["rwkv_k_k"])[l])
        pcols[l, :, 14:16] = two(f(inp["rwkv_k_a"])[l])
    shared = dict(
        w_mod=f(inp["w_mod"]), b_mod=f(inp["b_mod"]), norm_g=f(inp["norm_g"]), ffn_w_in=f(inp["ffn_w_in"]),
        ffn_w_out=f(inp["ffn_w_out"]), w_in=w_in, w_out=f(inp["w_out"]), final_g=f(inp["final_norm_g"]),
        w_sw=w_sw, w_d=w_d, pcols=pcols, mu=f(inp["rwkv_mu"]), w2=f(inp["rwkv_w2"]), a2=f(inp["rwkv_a2"]),
        g2=f(inp["rwkv_g2"]), ret_g=f(inp["ret_norm_g"]), ln_g=f(inp["rwkv_ln_g"]), ln_b=f(inp["rwkv_ln_b"]),
        sink=f(inp["attn_sink"]))
    shared.update(_consts(T))
    x, c, ctx, c_ctx = f(inp["x"]), f(inp["c"]), f(inp["ctx"]), f(inp["c_ctx"])
    maps = []
    for b in range(x.shape[0]):
        m = dict(shared)
        m["xin"] = np.ascontiguousarray(np.concatenate([x[b], ctx[b]], 0))
        m["cvec"] = np.ascontiguousarray(np.stack([c[b], c_ctx], 0))
        maps.append(m)
    return maps


_PROG = {}


def kernel(**inputs):
    x = np.asarray(inputs["x"])
    B, T, _ = x.shape
    CTX = np.asarray(inputs["ctx"]).shape[1]
    L = np.asarray(inputs["w_in"]).shape[0]
    key = (T, CTX, L)
    if key not in _PROG:
        _PROG[key] = build_program(T, CTX, L)[0]
    nc = _PROG[key]
    maps = _host_maps(inputs, T, CTX, L)
    res = run_bass_kernel_spmd(nc, maps, core_ids=list(range(B)))
    return np.stack([np.asarray(r["out"], dtype=np.float32) for r in res.results], 0)
```

```python
import contextlib
import numpy as np
import concourse.bass as bass
import concourse.mybir as mybir
from concourse.bass_utils import run_bass_kernel_spmd

ACT = mybir.ActivationFunctionType
ALU = mybir.AluOpType
AX = mybir.AxisListType
F32 = mybir.dt.float32
BF16 = mybir.dt.bfloat16

D = 1024
DFF = 2816
NMOD = 9
RMS_EPS = 1e-6
STRICT = True


class PSem:
    def __init__(self, h):
        self.h = h
        self.val = 0


class Eng:
    def __init__(self, name, h, sem):
        self.name, self.h, self.sem = name, h, sem
        self.n = 0
        self.seen = {}
        self.dseen = {}


class Buf:
    def __init__(self, k, t, name):
        self.k, self.t, self.name = k, t, name
        self.w = None
        self.r = {}
        self.psem = None
        self.dma_w = 0
        self.dma_rw = 0

    def __getitem__(self, key):
        return self.t[key]

    @property
    def v(self):
        return _VA(self)


class View:
    def __init__(self, buf, ap):
        self.buf, self.ap = buf, ap

    def rr(self, pat, **kw):
        return View(self.buf, self.ap.rearrange(pat, **kw))

    def bc(self, shape):
        return View(self.buf, self.ap.to_broadcast(list(shape)))

    def __getitem__(self, key):
        return View(self.buf, self.ap[key])


class _VA:
    def __init__(self, b):
        self.b = b

    def __getitem__(self, key):
        return View(self.b, self.b.t[key])


def _bufs(*xs):
    out = []
    for x in xs:
        if isinstance(x, View) and x.buf not in out:
            out.append(x.buf)
    return out


def _ap(x):
    return x.ap if isinstance(x, View) else x


class K:
    def __init__(self, nc):
        self.nc = nc
        self.stack = contextlib.ExitStack()
        self.E = {}
        for name, h in (("pe", nc.tensor), ("act", nc.scalar), ("dve", nc.vector),
                        ("pool", nc.gpsimd), ("sp", nc.sync)):
            sem = self.stack.enter_context(nc.semaphore("s_" + name))
            self.E[name] = Eng(name, h, sem)
        self.free_psems = []
        self.n_psems = 0
        self.all_psems = []
        self.bufs = []
        self.uid = 0
        self.ninstr = 0
        self.marks = []

    def _psem(self):
        if self.free_psems:
            return self.free_psems.pop()
        self.n_psems += 1
        h = self.stack.enter_context(self.nc.semaphore("d%d" % self.n_psems))
        p = PSem(h)
        self.all_psems.append(p)
        return p

    def sb(self, st, shape, dtype, name):
        self.uid += 1
        t = st.enter_context(self.nc.sbuf_tensor("%s_%d" % (name, self.uid), list(shape), dtype))
        b = Buf(self, t, name)
        st.callback(self._release, b)
        return b

    def ps(self, st, shape, dtype, name):
        self.uid += 1
        t = st.enter_context(self.nc.psum_tensor("%s_%d" % (name, self.uid), list(shape), dtype))
        return Buf(self, t, name)

    def _release(self, b):
        if b.psem is not None and not getattr(b.psem, "sw", False):
            self.free_psems.append(b.psem)
        b.psem = None

    def _wait_eng(self, E, X, n):
        if E.seen.get(X.name, 0) >= n:
            return
        E.h.wait_ge(X.sem, n)
        E.seen[X.name] = n
        self.ninstr += 1

    def _wait_d(self, E, p, v):
        if v <= 0 or E.dseen.get(id(p), 0) >= v:
            return
        E.h.wait_ge(p.h, v)
        E.dseen[id(p)] = v
        self.ninstr += 1

    def _deps(self, E, r, w):
        for b in r:
            if b.w is not None:
                self._wait_eng(E, b.w[0], b.w[1])
            if b.psem is not None:
                self._wait_d(E, b.psem, b.dma_w)
        for b in w:
            if b.w is not None and (STRICT or b.w[0] is not E) and not (E.name == "pe" and b.w[0] is E):
                self._wait_eng(E, b.w[0], b.w[1])
            for X, n in b.r.items():
                if STRICT or X is not E:
                    self._wait_eng(E, X, n)
            if b.psem is not None:
                self._wait_d(E, b.psem, b.dma_rw)

    def op(self, eng, fn, r=(), w=()):
        E = self.E[eng]
        self._deps(E, r, w)
        ins = fn(E.h)
        E.n += 1
        ins.then_inc(E.sem, 1)
        self.ninstr += 1
        for b in w:
            b.w = (E, E.n)
            b.r = {}
        for b in r:
            if b.w is None or b.w[0] is not E or b.w[1] != E.n:
                b.r[E] = E.n
        return ins

    def dma(self, q, out, in_, r=(), w=(), **kw):
        Q = self.E[q]
        self._deps(Q, r, w)
        anchor = w[0] if w else r[0]
        if anchor.psem is None:
            if q == "pool":
                self.n_psems += 1
                anchor.psem = PSem(self.stack.enter_context(self.nc.semaphore("w%d" % self.n_psems)))
                anchor.psem.sw = True
                self.all_psems.append(anchor.psem)
            else:
                anchor.psem = self._psem()
        p = anchor.psem
        assert getattr(p, "sw", False) == (q == "pool"), "buffer mixes SW and HW DGE DMAs: " + anchor.name
        Q.h.dma_start(out=out, in_=in_, **kw).then_inc(p.h, 16)
        p.val += 16
        self.ninstr += 1
        anchor.dma_rw = p.val
        if w:
            anchor.dma_w = p.val
            anchor.w = None
            anchor.r = {}

    def mm(self, out, lhsT, rhs, start=True, stop=True):
        return self.op("pe", lambda e: e.matmul(out.ap, lhsT.ap, rhs.ap, start=start, stop=stop),
                       r=_bufs(lhsT, rhs), w=[out.buf])

    def tr(self, out, in_, ident):
        return self.op("pe", lambda e: e.transpose(out.ap, in_.ap, ident.ap), r=_bufs(in_, ident), w=[out.buf])

    def tt(self, eng, out, in0, in1, op):
        return self.op(eng, lambda e: e.tensor_tensor(out=out.ap, in0=in0.ap, in1=in1.ap, op=op),
                       r=_bufs(in0, in1), w=[out.buf])

    def ts(self, eng, out, in0, s1, s2, op0, op1=None):
        kw = {} if op1 is None else dict(op1=op1)
        return self.op(eng, lambda e: e.tensor_scalar(out=out.ap, in0=in0.ap, scalar1=_ap(s1), scalar2=_ap(s2),
                                                      op0=op0, **kw), r=_bufs(in0, s1, s2), w=[out.buf])

    def stt(self, eng, out, in0, scalar, in1, op0, op1):
        return self.op(eng, lambda e: e.scalar_tensor_tensor(out=out.ap, in0=in0.ap, scalar=_ap(scalar), in1=in1.ap,
                                                             op0=op0, op1=op1), r=_bufs(in0, scalar, in1), w=[out.buf])

    def act(self, out, in_, func, bias=None, scale=None, accum=None):
        kw = {}
        if bias is not None:
            kw["bias"] = _ap(bias)
        if scale is not None:
            kw["scale"] = _ap(scale)
        w = [out.buf]
        if accum is not None:
            kw["accum_out"] = accum.ap
            w.append(accum.buf)
        return self.op("act", lambda e: e.activation(out=out.ap, in_=in_.ap, func=func, **kw),
                       r=_bufs(in_, bias, scale), w=w)

    def cp(self, eng, out, in_):
        if eng == "act":
            return self.op("act", lambda e: e.copy(out=out.ap, in_=in_.ap), r=[in_.buf], w=[out.buf])
        return self.op(eng, lambda e: e.tensor_copy(out=out.ap, in_=in_.ap), r=[in_.buf], w=[out.buf])

    def memset(self, eng, out, val):
        return self.op(eng, lambda e: e.memset(out.ap, val), w=[out.buf])

    def recip(self, out, in_):
        return self.op("dve", lambda e: e.reciprocal(out=out.ap, in_=in_.ap), r=[in_.buf], w=[out.buf])

    def ld(self, q, out, dram, **kw):
        return self.dma(q, out.ap, dram, w=[out.buf], **kw)

    def stv(self, q, dram, in_, **kw):
        return self.dma(q, dram, in_.ap, r=[in_.buf], **kw)

    def barrier(self):
        for E in self.E.values():
            for X in self.E.values():
                if X is not E and X.n > 0:
                    self._wait_eng(E, X, X.n)
            for p in self.all_psems:
                self._wait_d(E, p, p.val)


def mm(k, out_ap, lhsT_ap, rhs_ap, start, stop, r, w):
    return k.op("pe", lambda e: e.matmul(out_ap, lhsT_ap, rhs_ap, start=start, stop=stop), r=r, w=w)


def rowphase(k, st, cfg):
    nc = k.nc
    ffn = cfg["ffn"]
    pre = cfg.get("pre")
    post = cfg.get("post")
    w1 = k.sb(st, [128, 8, 2 * DFF], BF16, "w1")
    w2 = k.sb(st, [128, 22, D], BF16, "w2")
    w1v = ffn["w_in"].rearrange("(c p) f -> p c f", p=128)
    for c in range(8):
        for hlf in range(4):
            f0 = hlf * 1408
            k.dma("pool", w1[:, c, f0:f0 + 1408], w1v[:, c, f0:f0 + 1408], w=[w1])
    w2v = ffn["w_out"].rearrange("(c p) d -> p c d", p=128)
    for c in range(22):
        k.dma("pool", w2[:, c, :], w2v[:, c, :], w=[w2])
    if pre is not None:
        wo = k.sb(st, [128, 8, D], BF16, "wo")
        wov = pre["w_out"].rearrange("(c p) d -> p c d", p=128)
        for c in range(8):
            k.dma("pool", wo[:, c, :], wov[:, c, :], w=[wo])

    identf = k.sb(st, [128, 128], F32, "identf")
    k.dma("sp", identf[:, :], cfg["ident"], w=[identf])
    xt = [k.sb(st, [128, D], F32, "xt%d" % i) for i in range(4)]
    hT = k.sb(st, [128, 8, 512], BF16, "hT")
    gT = k.sb(st, [128, 22, 512], BF16, "gT")
    sg = k.sb(st, [128, 512], BF16, "sg")
    tmp = k.sb(st, [128, D], F32, "tmp")
    ssq = k.sb(st, [128, 4], F32, "ssq")
    rinv = k.sb(st, [128, 4], F32, "rinv")
    ps_u = [k.ps(st, [128, 512], F32, "psu%d" % i) for i in range(4)]
    ps_y = [k.ps(st, [128, 512], F32, "psy%d" % i) for i in range(2)]
    ps_t = k.ps(st, [128, D], F32, "pst")
    Ac = k.sb(st, [128, 8], F32, "Ac")
    Bc = k.sb(st, [128, 8], F32, "Bc")
    Gc = k.sb(st, [128, 8], F32, "Gc")
    A2c = k.sb(st, [128, 8], F32, "A2c")
    B2c = k.sb(st, [128, 8], F32, "B2c")
    G = k.sb(st, [128, D], F32, "G")
    PG = k.sb(st, [128, D], F32, "PG") if pre is not None else None
    gvec2 = None
    if post is not None and post["kind"] == "final":
        gvec2 = k.sb(st, [128, D], F32, "gvec2")
        k.dma("sp", gvec2[:, :], post["g"].partition_broadcast(128), w=[gvec2])

    def col(ap1d):
        return ap1d.rearrange("(c p) -> p c", p=128)

    def load_cols(dst, ap1d):
        k.dma("sp", dst[:, :], col(ap1d), w=[dst], allow_slow_non_contiguous=True)

    def setup_mod(mr):
        load_cols(Gc, ffn["g"])
        load_cols(Ac, cfg["mod"][mr, ffn["scale"] * D:(ffn["scale"] + 1) * D])
        load_cols(Bc, cfg["mod"][mr, ffn["shift"] * D:(ffn["shift"] + 1) * D])
        k.op("dve", lambda e: e.scalar_tensor_tensor(out=Ac[:, :], in0=Ac[:, :], scalar=1.0, in1=Gc[:, :],
                                                     op0=ALU.add, op1=ALU.mult), r=[Ac, Gc], w=[Ac])
        if post is not None and post["kind"] == "hT":
            load_cols(Gc, post["g"])
            load_cols(A2c, cfg["mod"][mr, post["scale"] * D:(post["scale"] + 1) * D])
            load_cols(B2c, cfg["mod"][mr, post["shift"] * D:(post["shift"] + 1) * D])
            k.op("dve", lambda e: e.scalar_tensor_tensor(out=A2c[:, :], in0=A2c[:, :], scalar=1.0, in1=Gc[:, :],
                                                         op0=ALU.add, op1=ALU.mult), r=[A2c, Gc], w=[A2c])
        k.dma("sp", G[:, :], cfg["mod"][mr, ffn["gate"] * D:(ffn["gate"] + 1) * D].partition_broadcast(128), w=[G])
        k.op("dve", lambda e: e.tensor_scalar(out=G[:, :], in0=G[:, :], scalar1=0.5, scalar2=None, op0=ALU.mult),
             r=[G], w=[G])
        if pre is not None:
            k.dma("sp", PG[:, :], cfg["mod"][mr, pre["gate"] * D:(pre["gate"] + 1) * D].partition_broadcast(128),
                  w=[PG])

    try:
        tmp2 = [tmp, k.sb(st, [128, D], F32, "tmpb")]
    except AssertionError:
        tmp2 = [tmp, tmp]

    def rms_all(xs):
        m_ = len(xs)
        junk = gT[:, 0:2, :]
        for s_, xb in enumerate(xs):
            k.op("act", lambda e: e.activation(out=junk, in_=xb[:, :].rearrange("p (a b) -> p a b", a=2),
                                               func=ACT.Square, accum_out=ssq[:, s_:s_ + 1]), r=[xb], w=[gT, ssq])
        k.op("dve", lambda e: e.tensor_scalar(out=rinv[:, 0:m_], in0=ssq[:, 0:m_], scalar1=1.0 / D, scalar2=RMS_EPS,
                                              op0=ALU.mult, op1=ALU.add), r=[ssq], w=[rinv])
        k.op("act", lambda e: e.activation(out=rinv[:, 0:m_], in_=rinv[:, 0:m_], func=ACT.Sqrt), r=[rinv], w=[rinv])
        k.op("dve", lambda e: e.reciprocal(out=rinv[:, 0:m_], in_=rinv[:, 0:m_]), r=[rinv], w=[rinv])

    def norm_mod_T_all(xs, A, B, dstT):
        rms_all(xs)
        for s, xb in enumerate(xs):
            tm = tmp2[s % 2]
            k.op("dve", lambda e: e.tensor_scalar(out=tm[:, :], in0=xb[:, :], scalar1=rinv[:, s:s + 1], scalar2=None,
                                                  op0=ALU.mult), r=[xb, rinv], w=[tm])
            for c in range(8):
                k.op("pe", lambda e: e.transpose(ps_t[:, c * 128:(c + 1) * 128], tm[:, c * 128:(c + 1) * 128],
                                                 identf[:, :]), r=[tm, identf], w=[ps_t])
            for c in range(8):
                if c % 2 == 0:
                    k.op("dve", lambda e: e.tensor_scalar(out=dstT[:, c, s * 128:(s + 1) * 128],
                                                          in0=ps_t[:, c * 128:(c + 1) * 128], scalar1=A[:, c:c + 1],
                                                          scalar2=B[:, c:c + 1], op0=ALU.mult, op1=ALU.add),
                         r=[ps_t, A, B], w=[dstT])
                else:
                    k.op("act", lambda e: e.activation(out=dstT[:, c, s * 128:(s + 1) * 128],
                                                       in_=ps_t[:, c * 128:(c + 1) * 128], func=ACT.Identity,
                                                       scale=A[:, c:c + 1], bias=B[:, c:c + 1]),
                         r=[ps_t, A, B], w=[dstT])

    ui = 0
    yi = 0
    xsets = [xt]
    if pre is None:
        try:
            xsets.append([k.sb(st, [128, D], F32, "xu%d" % i) for i in range(4)])
        except AssertionError:
            pass
    tiles = []
    for (tok0, ntok, mr) in cfg["segs"]:
        for t0 in range(tok0, tok0 + ntok, 512):
            tiles.append((t0, min(512, tok0 + ntok - t0), mr, tok0, ntok))

    def emit_xloads(m):
        t0_, n_, _, _, _ = tiles[m]
        xs_ = xsets[m % len(xsets)]
        for s in range(n_ // 128):
            k.dma("sp", xs_[s][:, :], cfg["xin"][t0_ + s * 128:t0_ + (s + 1) * 128, :], w=[xs_[s]])

    if len(xsets) == 2:
        emit_xloads(0)
    cur_mr = None
    for m_i, (t0, n, mr, tok0, ntok) in enumerate(tiles):
        if mr != cur_mr:
            setup_mod(mr)
            cur_mr = mr
        if True:
            ns = n // 128
            xt = xsets[m_i % len(xsets)]
            if len(xsets) == 1:
                emit_xloads(m_i)
            elif m_i + 1 < len(tiles):
                emit_xloads(m_i + 1)
            if pre is not None:
                for s in range(ns):
                    k.dma("sp", tmp[:, :], pre["o"][t0 + s * 128:t0 + (s + 1) * 128, :], w=[tmp])
                    for c in range(8):
                        k.op("pe", lambda e: e.transpose(ps_t[:, c * 128:(c + 1) * 128], tmp[:, c * 128:(c + 1) * 128],
                                                         identf[:, :]), r=[tmp, identf], w=[ps_t])
                    k.op("act", lambda e: e.copy(out=hT[:, :, s * 128:(s + 1) * 128],
                                                 in_=ps_t[:, :].rearrange("p (c t) -> p c t", c=8)), r=[ps_t], w=[hT])
                for s in range(ns):
                    for hf in range(2):
                        py = ps_y[yi % 2]
                        yi += 1
                        for c in range(8):
                            mm(k, py[:, :], hT[:, c, s * 128:(s + 1) * 128], wo[:, c, hf * 512:(hf + 1) * 512],
                               c == 0, c == 7, r=[hT, wo], w=[py])
                        sl = slice(hf * 512, (hf + 1) * 512)
                        k.op("dve", lambda e, py=py, sl=sl: e.tensor_tensor(out=tmp[:, sl], in0=py[:, :],
                                                                             in1=PG[:, sl], op=ALU.mult),
                             r=[py, PG], w=[tmp])
                        k.op("dve", lambda e, s=s, sl=sl: e.tensor_tensor(out=xt[s][:, sl], in0=xt[s][:, sl],
                                                                            in1=tmp[:, sl], op=ALU.add),
                             r=[xt[s], tmp], w=[xt[s]])
            norm_mod_T_all(xt[0:ns], Ac, Bc, hT)
            for fc in range(22 if cfg.get("stage", 9) >= 2 else 0):
                p1 = ps_u[ui % 4]
                p2 = ps_u[(ui + 1) % 4]
                ui += 2
                for c in range(8):
                    mm(k, p1[:, 0:n], w1[:, c, fc * 128:(fc + 1) * 128], hT[:, c, 0:n], c == 0, c == 7,
                       r=[w1, hT], w=[p1])
                for c in range(8):
                    mm(k, p2[:, 0:n], w1[:, c, DFF + fc * 128:DFF + (fc + 1) * 128], hT[:, c, 0:n], c == 0, c == 7,
                       r=[w1, hT], w=[p2])
                k.op("act", lambda e, p1=p1: e.activation(out=sg[:, 0:n], in_=p1[:, 0:n], func=ACT.Silu),
                     r=[p1], w=[sg])
                k.op("dve", lambda e, p2=p2, fc=fc: e.tensor_tensor(out=gT[:, fc, 0:n], in0=sg[:, 0:n],
                                                                     in1=p2[:, 0:n], op=ALU.mult),
                     r=[sg, p2], w=[gT])
            for s in range(ns if cfg.get("stage", 9) >= 3 else 0):
                for hf in range(2):
                    py = ps_y[yi % 2]
                    yi += 1
                    for fc in range(22):
                        mm(k, py[:, :], gT[:, fc, s * 128:(s + 1) * 128], w2[:, fc, hf * 512:(hf + 1) * 512],
                           fc == 0, fc == 21, r=[gT, w2], w=[py])
                    sl = slice(hf * 512, (hf + 1) * 512)
                    k.op("dve", lambda e, py=py, sl=sl: e.tensor_tensor(out=tmp[:, sl], in0=py[:, :], in1=G[:, sl],
                                                                         op=ALU.mult), r=[py, G], w=[tmp])
                    k.op("dve", lambda e, s=s, sl=sl: e.tensor_tensor(out=xt[s][:, sl], in0=xt[s][:, sl],
                                                                        in1=tmp[:, sl], op=ALU.add),
                         r=[xt[s], tmp], w=[xt[s]])
            if post is not None and post["kind"] == "final":
                if mr == 0:
                    rms_all(xt[0:ns])
                    for s in range(ns):
                        r0 = t0 + s * 128
                        tm = tmp2[s % 2]
                        k.op("dve", lambda e: e.scalar_tensor_tensor(out=tm[:, :], in0=xt[s][:, :],
                                                                     scalar=rinv[:, s:s + 1], in1=gvec2[:, :],
                                                                     op0=ALU.mult, op1=ALU.mult),
                             r=[xt[s], rinv, gvec2], w=[tm])
                        k.dma("sp", post["out"][r0:r0 + 128, :], tm[:, :], r=[tm])
                continue
            for s in range(ns):
                r0 = t0 + s * 128
                k.dma("sp", cfg["xout"][r0:r0 + 128, :], xt[s][:, :], r=[xt[s]])
            if post is not None and post["kind"] == "hT":
                norm_mod_T_all(xt[0:ns], A2c, B2c, hT)
            if post is not None and post["kind"] == "hT":
                for c in range(8):
                    k.dma("sp", post["hT"][c * 128:(c + 1) * 128, t0:t0 + n], hT[:, c, 0:n], r=[hT])


CH = 64
DECAY_SCALE = 0.6065306597126334


def projphase(k, st, cfg):
    S = cfg["S"]
    C = cfg["C"]
    Wt = cfg["W"]
    T, CTX = cfg["T"], cfg["CTX"]
    hTd = S["hT"]
    wabc = k.sb(st, [128, 8, 2304], BF16, "wabc")
    wgd = k.sb(st, [128, 8, 128], BF16, "wgd")
    wsw = k.sb(st, [128, 8, 1280], BF16, "wsw")
    wv = Wt["w_in"].rearrange("(c p) f -> p c f", p=128)
    for c in range(8):
        k.ld("pool", wabc.v[:, c, 0:1152], wv[:, c, 0:1152])
        k.ld("pool", wabc.v[:, c, 1152:2304], wv[:, c, 1152:2304])
        k.ld("pool", wgd.v[:, c, :], wv[:, c, 3072:3200])
        k.ld("pool", wsw.v[:, c, :], Wt["w_sw"].rearrange("(c p) f -> p c f", p=128)[:, c, :])
    wda = [k.sb(st, [128, 8, 896], BF16, "wda%d" % d) for d in range(2)]
    wdb = [k.sb(st, [128, 8, 896], BF16, "wdb%d" % d) for d in range(2)]
    with contextlib.ExitStack() as st2:
        wst = k.sb(st2, [128, 8, 896], F32, "wst")
        mub = k.sb(st2, [128, 896], F32, "mub")
        wtm = k.sb(st2, [128, 8, 896], F32, "wtm")
        for d in range(2):
            for c in range(8):
                k.ld("sp", wst.v[:, c, :], Wt["w_d"][d].rearrange("(c p) f -> p c f", p=128)[:, c, :])
            k.ld("sp", mub.v[:, :], Wt["mu"][d].partition_broadcast(128))
            for c in range(8):
                k.tt("dve", wtm.v[:, c, :], wst.v[:, c, :], mub.v[:, :], ALU.mult)
            k.cp("act", wda[d].v[:, :, :], wtm.v[:, :, :])
            k.tt("dve", wdb[d].v[:, :, :], wst.v[:, :, :], wtm.v[:, :, :], ALU.subtract)
        k.barrier()
    pc = k.sb(st, [128, 24], F32, "pcols")
    k.ld("sp", pc.v[:, :], Wt["pcols"])
    oma = k.sb(st, [128, 2], F32, "oma")
    k.ts("dve", oma.v[:, :], pc.v[:, 14:16], -1.0, 1.0, ALU.mult, ALU.add)
    identf = k.sb(st, [128, 128], F32, "identf")
    k.ld("sp", identf.v[:, :], C["ident"])
    bd = k.sb(st, [128, 128], F32, "bd")
    k.ld("sp", bd.v[:, :], C["bd64"])
    e2 = k.sb(st, [128, 2], F32, "e2")
    k.ld("sp", e2.v[:, :], C["e2"])
    rtab = k.sb(st, [128, 2, 2, 3, CH], F32, "rtab")
    k.ld("sp", rtab.v[:, :, :, :, :], C["ret_tab"])
    w2s = k.sb(st, [64, 2, 256], BF16, "w2s")
    a2s = k.sb(st, [64, 2, 256], BF16, "a2s")
    g2s = k.sb(st, [128, 256], BF16, "g2s")
    for d in range(2):
        k.ld("pool", w2s.v[:, d, :], Wt["w2"][d])
        k.ld("pool", a2s.v[:, d, :], Wt["a2"][d])
    k.ld("pool", g2s.v[:, :], Wt["g2"])

    NT = 512
    hTt = k.sb(st, [128, 8, NT + 2], BF16, "hTt")
    ca = k.sb(st, [128, NT], F32, "ca")
    sa = k.sb(st, [128, NT], F32, "sa")
    cs_ = k.sb(st, [128, NT], F32, "cs")
    ss_ = k.sb(st, [128, NT], F32, "ss")
    gca = k.sb(st, [128, 2, NT], F32, "gca")
    gsa = k.sb(st, [128, 2, NT], F32, "gsa")
    F = [k.sb(st, [128, NT], F32, "f%d" % i) for i in range(12)]
    ob = k.sb(st, [128, NT], BF16, "ob")
    sbf = [k.sb(st, [128, NT], BF16, "sbf%d" % i) for i in range(2)]
    sbi = [0]

    def st_bf(dst, srcv, n):
        b = sbf[sbi[0] % 2]
        sbi[0] += 1
        k.cp("act", b.v[:, 0:n], srcv)
        k.stv("sp", dst, b.v[:, 0:n])
    tb = k.sb(st, [128, 512], F32, "tb")
    tbb = k.sb(st, [128, 512], BF16, "tbb")
    thb = k.sb(st, [64, NT], BF16, "thb")
    alb = k.sb(st, [64, NT], BF16, "alb")
    sgb = k.sb(st, [128, NT], BF16, "sgb")
    bs4 = k.sb(st, [128, 4, 4], F32, "bs4")
    pcx = k.sb(st, [128, NT // CH], F32, "pcx")
    pcx2 = k.sb(st, [128, NT // CH], F32, "pcx2")
    F2 = [k.sb(st, [128, NT], F32, "g%d" % i) for i in range(9)]
    PS = [k.ps(st, [128, 512], F32, "pp%d" % i) for i in range(8)]
    pi = [0]

    def nps():
        p = PS[pi[0] % 8]
        pi[0] += 1
        return p

    def chain(p, wlist, M, n, col0=1):
        tot = len(wlist) * 8
        i = 0
        for (wb, c0, sh) in wlist:
            for c in range(8):
                k.mm(p.v[0:M, 0:n], wb.v[:, c, c0:c0 + M], hTt.v[:, c, col0 + sh:col0 + sh + n], i == 0, i == tot - 1)
                i += 1

    def chain_tok(p, wlist, ncols, s):
        tot = len(wlist) * 8
        i = 0
        for (wb, c0, sh) in wlist:
            for c in range(8):
                k.mm(p.v[:, 0:ncols], hTt.v[:, c, 1 + sh + s * 128:1 + sh + (s + 1) * 128], wb.v[:, c, c0:c0 + ncols],
                     i == 0, i == tot - 1)
                i += 1

    def rope_out(P, Psw, ctab, stab, dst, n):
        k.tt("dve", F[10].v[:, 0:n], P.v[:, 0:n], ctab, ALU.mult)
        k.tt("dve", F[11].v[:, 0:n], Psw.v[:, 0:n], stab, ALU.mult)
        k.tt("dve", dst, F[10].v[:, 0:n], F[11].v[:, 0:n], ALU.add)

    segs = [(0, T, True), (T, CTX, False)]
    for (seg0, seglen, latent) in segs:
        for t0 in range(seg0, seg0 + seglen, NT):
            n = min(NT, seg0 + seglen - t0)
            ns = n // 128
            nch = n // CH
            lo = t0 - 1 if t0 > seg0 else t0
            hi = t0 + n + 1 if t0 + n < seg0 + seglen else t0 + n
            if lo == t0:
                k.memset("dve", hTt.v[:, :, 0:1], 0.0)
            if hi == t0 + n:
                k.memset("dve", hTt.v[:, :, n + 1:n + 2], 0.0)
            for c in range(8):
                k.ld("pool", hTt.v[:, c, 1 - (t0 - lo):1 + n + (hi - t0 - n)], hTd[c * 128:(c + 1) * 128, lo:hi])
            if latent:
                k.ld("pool", ca.v[:, 0:n], C["ropeA_c"][:, t0:t0 + n])
                k.ld("pool", sa.v[:, 0:n], C["ropeA_s"][:, t0:t0 + n])
                k.ld("pool", cs_.v[:, 0:n], C["ropeS_c"][:, t0:t0 + n])
                k.ld("pool", ss_.v[:, 0:n], C["ropeS_s"][:, t0:t0 + n])
                for qk in range(2):
                    k.ts("dve", gca.v[:, qk, 0:n], ca.v[:, 0:n], pc.v[:, 2 * qk:2 * qk + 1], None, ALU.mult)
                    k.ts("dve", gsa.v[:, qk, 0:n], sa.v[:, 0:n], pc.v[:, 2 * qk + 1:2 * qk + 2], None, ALU.mult)
            for grp, base, swb, dq, dk_, dv_ in (("A", 0, 0, S["QA"], S["KA"], S["VA"]),
                                                ("B", 512, 384, S["QB"], S["KB"], S["VB"])):
                for j in range(3):
                    P, Psw = nps(), nps()
                    chain(P, [(wabc, base + j * 128, 0)], 128, n)
                    if latent:
                        chain(Psw, [(wsw, swb + j * 128, 0)], 128, n)
                    qk = 0 if j < 2 else 1
                    if grp == "B":
                        k.act(F[0].v[:, 0:n], P.v[:, 0:n], ACT.Square)
                        pss = nps()
                        k.mm(pss.v[:, 0:n], bd.v[:, :], F[0].v[:, 0:n])
                        k.ts("dve", F[1].v[:, 0:n], pss.v[:, 0:n], 1.0 / 64, RMS_EPS, ALU.mult, ALU.add)
                        k.act(F[1].v[:, 0:n], F[1].v[:, 0:n], ACT.Sqrt)
                        k.recip(F[1].v[:, 0:n], F[1].v[:, 0:n])
                        k.tt("dve", F[2].v[:, 0:n], P.v[:, 0:n], F[1].v[:, 0:n], ALU.mult)
                        if latent:
                            k.tt("dve", F[3].v[:, 0:n], Psw.v[:, 0:n], F[1].v[:, 0:n], ALU.mult)
                            rope_out(F[2], F[3], gca.v[:, qk, 0:n], gsa.v[:, qk, 0:n], ob.v[:, 0:n], n)
                        else:
                            k.ts("dve", ob.v[:, 0:n], F[2].v[:, 0:n], pc.v[:, 2 * qk:2 * qk + 1], None, ALU.mult)
                    else:
                        if latent:
                            rope_out(P, Psw, ca.v[:, 0:n], sa.v[:, 0:n], ob.v[:, 0:n], n)
                        else:
                            k.cp("act", ob.v[:, 0:n], P.v[:, 0:n])
                    dst = dq[j * 128:(j + 1) * 128, t0:t0 + n] if j < 2 else dk_[:, t0:t0 + n]
                    k.stv("sp", dst, ob.v[:, 0:n])
                for s in range(ns):
                    P = nps()
                    chain_tok(P, [(wabc, base + 384, 0)], 128, s)
                    k.cp("act", tbb.v[:, 0:128], P.v[:, 0:128])
                    k.stv("sp", dv_[t0 + s * 128:t0 + (s + 1) * 128, :], tbb.v[:, 0:128])
            for j in range(4):
                P, Psw = nps(), nps()
                chain(P, [(wabc, 1024 + j * 128, 0)], 128, n)
                g = j % 2
                if latent:
                    chain(Psw, [(wsw, 768 + j * 128, 0)], 128, n)
                    rope_out(P, Psw, cs_.v[:, 0:n], ss_.v[:, 0:n], F[0].v[:, 0:n], n)
                else:
                    k.cp("act", F[0].v[:, 0:n], P.v[:, 0:n])
                x3 = F[0].v[:, 0:n].rr("p (c i) -> p c i", i=CH)
                for d in range(2):
                    if j < 2:
                        k.tt("dve", F[1].v[:, 0:n].rr("p (c i) -> p c i", i=CH), x3,
                             rtab.v[:, g, d, 0:1, :].bc([128, nch, CH]), ALU.mult)
                        st_bf(S["C_Rt"][d, g * 128:(g + 1) * 128, t0:t0 + n], F[1].v[:, 0:n], n)
                    else:
                        k.tt("dve", F[1].v[:, 0:n].rr("p (c i) -> p c i", i=CH), x3,
                             rtab.v[:, g, d, 1:2, :].bc([128, nch, CH]), ALU.mult)
                        st_bf(S["C_Kt"][d, g * 128:(g + 1) * 128, t0:t0 + n], F[1].v[:, 0:n], n)
                        k.tt("dve", F[2].v[:, 0:n].rr("p (c i) -> p c i", i=CH), x3,
                             rtab.v[:, g, d, 2:3, :].bc([128, nch, CH]), ALU.mult)
                        for s in range(ns):
                            pt = nps()
                            k.tr(pt.v[:, 0:128], F[2].v[:, s * 128:(s + 1) * 128], identf.v[:, :])
                            k.cp("act", tbb.v[:, 0:128], pt.v[:, 0:128])
                            k.stv("sp", S["C_Kh"][d, t0 + s * 128:t0 + (s + 1) * 128, g * 128:(g + 1) * 128],
                                  tbb.v[:, 0:128])
            for s in range(ns):
                P = nps()
                chain_tok(P, [(wabc, 1536, 0)], 256, s)
                k.cp("act", tbb.v[:, 0:256], P.v[:, 0:256])
                k.stv("sp", S["C_V"][t0 + s * 128:t0 + (s + 1) * 128, :], tbb.v[:, 0:256])
                P = nps()
                chain_tok(P, [(wabc, 1792, 0)], 512, s)
                k.act(tb.v[:, 0:512], P.v[:, 0:512], ACT.Silu)
                k.stv("sp", S["C_G"][t0 + s * 128:t0 + (s + 1) * 128, :], tb.v[:, 0:512])
            P = nps()
            chain(P, [(wgd, 0, 0)], 128, n)
            k.act(sgb.v[:, 0:n], P.v[:, 0:n], ACT.Sigmoid)
            for s in range(ns):
                P = nps()
                k.mm(P.v[:, 0:256], sgb.v[:, s * 128:(s + 1) * 128], g2s.v[:, :])
                k.cp("act", tb.v[:, 0:256], P.v[:, 0:256])
                k.stv("sp", S["D_G"][t0 + s * 128:t0 + (s + 1) * 128, :], tb.v[:, 0:256])
            for d in range(2):
                sh = -1 if d == 0 else 1
                wl = lambda c0: [(wdb[d], c0, 0), (wda[d], c0, sh)]
                P = nps()
                chain(P, wl(768), 64, n)
                k.act(thb.v[:, 0:n], P.v[0:64, 0:n], ACT.Tanh)
                P = nps()
                chain(P, wl(832), 64, n)
                k.cp("act", alb.v[:, 0:n], P.v[0:64, 0:n])
                for s in range(ns):
                    P = nps()
                    chain_tok(P, wl(512), 256, s)
                    k.cp("act", tbb.v[:, 0:256], P.v[:, 0:256])
                    k.stv("sp", S["D_V"][d, t0 + s * 128:t0 + (s + 1) * 128, :], tbb.v[:, 0:256])
                def dgroup(g, FB, pcx):
                    r_, k_, lw, cs, a_, kk, e_, t1, t2 = FB
                    gc = slice(g, g + 1)
                    P = nps()
                    chain(P, wl(g * 128), 128, n)
                    k.cp("act", r_.v[:, 0:n], P.v[:, 0:n])
                    P = nps()
                    chain(P, wl(256 + g * 128), 128, n)
                    k.cp("act", k_.v[:, 0:n], P.v[:, 0:n])
                    yield
                    P = nps()
                    k.mm(P.v[:, 0:n], w2s.v[:, d, g * 128:(g + 1) * 128], thb.v[:, 0:n])
                    k.act(lw.v[:, 0:n], P.v[:, 0:n], ACT.Sigmoid, bias=pc.v[:, 4 + 2 * d + g:5 + 2 * d + g])
                    k.ts("dve", lw.v[:, 0:n], lw.v[:, 0:n], -DECAY_SCALE, None, ALU.mult)
                    yield
                    P = nps()
                    k.mm(P.v[:, 0:n], a2s.v[:, d, g * 128:(g + 1) * 128], alb.v[:, 0:n])
                    k.act(a_.v[:, 0:n], P.v[:, 0:n], ACT.Sigmoid, bias=pc.v[:, 8 + 2 * d + g:9 + 2 * d + g])
                    yield
                    k.ts("dve", kk.v[:, 0:n], k_.v[:, 0:n], pc.v[:, 12 + g:13 + g], None, ALU.mult)
                    k.act(t1.v[:, 0:n], kk.v[:, 0:n], ACT.Square)
                    yield
                    P = nps()
                    k.mm(P.v[:, 0:n], bd.v[:, :], t1.v[:, 0:n])
                    k.act(t1.v[:, 0:n], P.v[:, 0:n], ACT.Sqrt)
                    yield
                    k.ts("dve", t1.v[:, 0:n], t1.v[:, 0:n], 1e-12, None, ALU.max)
                    k.recip(t1.v[:, 0:n], t1.v[:, 0:n])
                    yield
                    k.tt("dve", kk.v[:, 0:n], kk.v[:, 0:n], t1.v[:, 0:n], ALU.mult)
                    yield
                    k.ts("dve", t1.v[:, 0:n], a_.v[:, 0:n], pc.v[:, 14 + g:15 + g], oma.v[:, gc], ALU.mult, ALU.add)
                    k.tt("dve", k_.v[:, 0:n], k_.v[:, 0:n], t1.v[:, 0:n], ALU.mult)
                    yield
                    k.stt("dve", t1.v[:, 0:n], r_.v[:, 0:n], pc.v[:, 16 + 2 * d + g:17 + 2 * d + g], k_.v[:, 0:n],
                          ALU.mult, ALU.mult)
                    for s in range(ns):
                        P = nps()
                        k.mm(P.v[:, 0:2], t1.v[:, s * 128:(s + 1) * 128], e2.v[:, :])
                        k.cp("act", bs4.v[:, s, 2 * g:2 * g + 2], P.v[:, 0:2])
                    k.tt("dve", a_.v[:, 0:n], a_.v[:, 0:n], kk.v[:, 0:n], ALU.mult)
                    yield
                    src, dst = lw, cs
                    k.cp("act", t2.v[:, 0:n], lw.v[:, 0:n])
                    src = t2
                    for stp in (1, 2, 4, 8, 16, 32):
                        s3 = src.v[:, 0:n].rr("p (c i) -> p c i", i=CH)
                        d3 = dst.v[:, 0:n].rr("p (c i) -> p c i", i=CH)
                        if d == 0:
                            k.tt("dve", d3[:, :, stp:], s3[:, :, stp:], s3[:, :, :CH - stp], ALU.add)
                            k.cp("act", d3[:, :, :stp], s3[:, :, :stp])
                        else:
                            k.tt("dve", d3[:, :, :CH - stp], s3[:, :, :CH - stp], s3[:, :, stp:], ALU.add)
                            k.cp("act", d3[:, :, CH - stp:], s3[:, :, CH - stp:])
                        src, dst = dst, src
                        yield
                    cs = src
                    spare = dst
                    cs3 = cs.v[:, 0:n].rr("p (c i) -> p c i", i=CH)
                    tot3 = cs3[:, :, CH - 1:CH] if d == 0 else cs3[:, :, 0:1]
                    k.act(e_.v[:, 0:n], cs.v[:, 0:n], ACT.Exp)
                    yield
                    k.tt("dve", t1.v[:, 0:n], r_.v[:, 0:n], e_.v[:, 0:n], ALU.mult)
                    st_bf(S["D_Rt"][d, g * 128:(g + 1) * 128, t0:t0 + n], t1.v[:, 0:n], n)
                    yield
                    k.tt("dve", e_.v[:, 0:n], cs.v[:, 0:n], lw.v[:, 0:n], ALU.subtract)
                    k.act(e_.v[:, 0:n], e_.v[:, 0:n], ACT.Exp)
                    yield
                    k.stt("dve", t1.v[:, 0:n], kk.v[:, 0:n], -1.0, e_.v[:, 0:n], ALU.mult, ALU.mult)
                    st_bf(S["D_At"][d, g * 128:(g + 1) * 128, t0:t0 + n], t1.v[:, 0:n], n)
                    yield
                    for s in range(ns):
                        pt = nps()
                        k.tr(pt.v[:, 0:128], t1.v[:, s * 128:(s + 1) * 128], identf.v[:, :])
                        k.cp("act", tbb.v[:, 0:128], pt.v[:, 0:128])
                        k.stv("sp", S["D_Atok"][d, t0 + s * 128:t0 + (s + 1) * 128, g * 128:(g + 1) * 128],
                              tbb.v[:, 0:128])
                    k.act(e_.v[:, 0:n], cs.v[:, 0:n], ACT.Exp, scale=-1.0)
                    k.tt("dve", t1.v[:, 0:n], a_.v[:, 0:n], e_.v[:, 0:n], ALU.mult)
                    st_bf(S["D_Bt"][d, g * 128:(g + 1) * 128, t0:t0 + n], t1.v[:, 0:n], n)
                    yield
                    k.tt("dve", t1.v[:, 0:n], k_.v[:, 0:n], e_.v[:, 0:n], ALU.mult)
                    st_bf(S["D_Kt"][d, g * 128:(g + 1) * 128, t0:t0 + n], t1.v[:, 0:n], n)
                    yield
                    k.act(pcx.v[:, 0:nch], tot3.rr("p c o -> p (c o)"), ACT.Exp)
                    k.stv("sp", S["D_PC"][d, g * 128:(g + 1) * 128, t0 // CH:t0 // CH + nch], pcx.v[:, 0:nch])
                    k.tt("dve", e_.v[:, 0:n].rr("p (c i) -> p c i", i=CH), tot3.bc([128, nch, CH]), cs3, ALU.subtract)
                    k.act(e_.v[:, 0:n], e_.v[:, 0:n], ACT.Exp)
                    yield
                    for (srcb, dstd) in ((a_, S["D_Bh"]), (k_, S["D_Kh"])):
                        k.tt("dve", t1.v[:, 0:n], srcb.v[:, 0:n], e_.v[:, 0:n], ALU.mult)
                        for s in range(ns):
                            pt = nps()
                            k.tr(pt.v[:, 0:128], t1.v[:, s * 128:(s + 1) * 128], identf.v[:, :])
                            k.cp("act", tbb.v[:, 0:128], pt.v[:, 0:128])
                            k.stv("sp", dstd[d, t0 + s * 128:t0 + (s + 1) * 128, g * 128:(g + 1) * 128], tbb.v[:, 0:128])
                gens = [dgroup(0, F[0:9], pcx), dgroup(1, F2, pcx2)]
                live = list(gens)
                while live:
                    for g_ in list(live):
                        try:
                            next(g_)
                        except StopIteration:
                            live.remove(g_)
                for s in range(ns):
                    k.stv("sp", S["D_BS"][d, t0 + s * 128:t0 + (s + 1) * 128, :], bs4.v[:, s, :])


def attnphase(k, st, cfg):
    S, C, Wt = cfg["S"], cfg["C"], cfg["W"]
    T, CTX = cfg["T"], cfg["CTX"]
    TT = T + CTX
    NKC = TT // 128
    need_ctx = cfg["need_ctx"]
    kT = k.sb(st, [128, TT], BF16, "kT")
    va = k.sb(st, [128, NKC, 65], BF16, "va")
    qT = [k.sb(st, [128, 512], BF16, "qT%d" % i) for i in range(2)]
    for c0 in range(0, TT, 2048):
        k.memset("dve", kT.v[64:128, c0:min(TT, c0 + 2048)], 0.0)
    for q_ in qT:
        k.memset("dve", q_.v[64:128, :], 0.0)
    pT = [k.sb(st, [128, 512], BF16, "pT%d" % i) for i in range(3)]
    ot = [k.sb(st, [128, 64], F32, "ot%d" % i) for i in range(2)]
    rc = k.sb(st, [128, 1], F32, "rc")
    esk = k.sb(st, [128, 4], F32, "esk")
    mprev = k.sb(st, [128, 128], BF16, "mprev")
    mnext = k.sb(st, [128, 128], BF16, "mnext")
    k.ld("pool", mprev.v[:, :], C["mprev"])
    k.ld("pool", mnext.v[:, :], C["mnext"])
    k.ld("sp", esk.v[:, :], Wt["sink"].partition_broadcast(128))
    k.act(esk.v[:, :], esk.v[:, :], ACT.Exp)
    ps_s = [k.ps(st, [128, 512], F32, "pss%d" % i) for i in range(3)]
    ps_o = [k.ps(st, [128, 512], F32, "pso%d" % i) for i in range(2)]
    ps_f = k.ps(st, [128, 512], F32, "psf")
    osb = k.sb(st, [128, 512], F32, "osb")
    identf = k.sb(st, [128, 128], F32, "identf")
    k.ld("sp", identf.v[:, :], C["ident"])
    cnt = dict(s=0, q=0, o=0, p=0, a=0)
    scale = 64 ** -0.5

    def load_kv(kd, vd, g):
        k.ld("sp", kT.v[0:64, :], kd[g * 64:(g + 1) * 64, :])
        k.memset("dve", va.v[:, :, 64:65], 1.0)
        k.ld("sp", va.v[:, :, 0:64], vd[:, g * 64:(g + 1) * 64].rearrange("(c p) j -> p c j", p=128))

    def finish(po, nq_sub, t0, col, sinkcol):
        nq = nq_sub * 128
        k.cp("act", osb.v[0:65, 0:nq], po.v[0:65, 0:nq])
        for s in range(nq_sub):
            k.tr(ps_f.v[:, s * 65:(s + 1) * 65], osb.v[0:65, s * 128:(s + 1) * 128], identf.v[0:65, 0:65])
        for s in range(nq_sub):
            o = ot[cnt["o"] % 2]
            cnt["o"] += 1
            den = ps_f.v[:, s * 65 + 64:s * 65 + 65]
            if sinkcol is None:
                k.recip(rc.v[:, :], den)
            else:
                k.tt("dve", rc.v[:, :], den, esk.v[:, sinkcol:sinkcol + 1], ALU.add)
                k.recip(rc.v[:, :], rc.v[:, :])
            k.ts("dve", o.v[:, :], ps_f.v[:, s * 65:s * 65 + 64], rc.v[:, 0:1], None, ALU.mult)
            k.stv("sp", S["O"][t0 + s * 128:t0 + (s + 1) * 128, col:col + 64], o.v[:, :])

    def qk(q, nq, kc):
        ps = ps_s[cnt["s"] % 3]
        cnt["s"] += 1
        k.mm(ps.v[:, 0:nq], kT.v[:, kc * 128:(kc + 1) * 128], q.v[:, 0:nq])
        return ps

    def prologue(b):
        if b.get("kv") is not None:
            load_kv(*b["kv"])
        q = qT[cnt["q"] % 2]
        cnt["q"] += 1
        k.ld("sp", q.v[0:64, 0:b["nq"]], b["qd"][b["h"] * 64:(b["h"] + 1) * 64, b["t0"]:b["t0"] + b["nq"]])
        return q, qk(q, b["nq"], b["chunks"][0][0])

    def body(b, q, nxt):
        nq, chunks = b["nq"], b["chunks"]
        po = b["po"]
        assert len(chunks[0][1]) == nq // 128
        for i, (kc, subs) in enumerate(chunks):
            ps = nxt
            if i + 1 < len(chunks):
                nxt = qk(q, nq, chunks[i + 1][0])
            p = pT[cnt["p"] % 3]
            cnt["p"] += 1
            k.act(p.v[:, 0:nq], ps.v[:, 0:nq], ACT.Exp, scale=scale)
            for (s, m_) in subs:
                if m_ is not None:
                    k.tt("dve", p.v[:, s * 128:(s + 1) * 128], p.v[:, s * 128:(s + 1) * 128], m_.v[:, :], ALU.mult)
            c_lo = min(s for (s, _) in subs) * 128
            c_hi = (max(s for (s, _) in subs) + 1) * 128
            k.mm(po.v[0:65, c_lo:c_hi], va.v[:, kc, :], p.v[:, c_lo:c_hi], i == 0, i == len(chunks) - 1)

    blocks = []

    def block(qd, h, t0, nq, chunks, col, sinkcol, kv=None):
        blocks.append(dict(qd=qd, h=h, t0=t0, nq=nq, chunks=chunks, col=col, sinkcol=sinkcol, kv=kv,
                           po=ps_o[len(blocks) % 2]))

    for g in range(2):
        kv = (S["KB"], S["VB"], g)
        for h in (2 * g, 2 * g + 1):
            for t0 in range(0, T, 512):
                nq = min(512, T - t0)
                block(S["QB"], h, t0, nq, [(kc, [(s, None) for s in range(nq // 128)]) for kc in range(NKC)],
                      256 + h * 64, None, kv)
                kv = None
            if need_ctx:
                block(S["QB"], h, T, CTX, [(kc, [(s, None) for s in range(CTX // 128)]) for kc in range(T // 128, NKC)],
                      256 + h * 64, None)
    nb = T // 128
    cchunks = list(range(T // 128, NKC))
    for g in range(2):
        kv = (S["KA"], S["VA"], g)
        for h in (2 * g, 2 * g + 1):
            for b0 in range(0, nb, 4):
                nsb = min(4, nb - b0)
                chunks = [(kc, [(s, None) for s in range(nsb)]) for kc in cchunks]
                for kc in range(max(0, b0 - 1), min(nb, b0 + nsb + 1)):
                    subs = []
                    for s in range(nsb):
                        dlt = kc - (b0 + s)
                        if dlt == -1:
                            subs.append((s, mprev))
                        elif dlt == 0:
                            subs.append((s, None))
                        elif dlt == 1:
                            subs.append((s, mnext))
                    chunks.append((kc, subs))
                block(S["QA"], h, b0 * 128, nsb * 128, chunks, h * 64, h, kv)
                kv = None
            if need_ctx:
                block(S["QA"], h, T, CTX, [(kc, [(s, None) for s in range(CTX // 128)]) for kc in cchunks], h * 64, h)
    st_ = prologue(blocks[0])
    for i, b in enumerate(blocks):
        body(b, *st_)
        if i + 1 < len(blocks):
            st_ = prologue(blocks[i + 1])
        finish(b["po"], b["nq"] // 128, b["t0"], b["col"], b["sinkcol"])


def scanphase(k, st, cfg):
    S, C = cfg["S"], cfg["C"]
    T, CTX = cfg["T"], cfg["CTX"]
    TT = T + CTX
    dplr = cfg["dplr"]
    pre = "D_" if dplr else "C_"
    NCG = 2
    GT_ = NCG * CH
    NCHT = TT // CH
    f3 = lambda b: b.v[:, :, :]
    t64 = lambda nm: k.sb(st, [64, 8, 64], F32, nm)
    b64 = lambda nm: k.sb(st, [64, 8, 64], BF16, nm)
    chm = lambda nm: k.sb(st, [64, 8, GT_], BF16, nm)
    tkm = lambda nm: k.sb(st, [64, NCG, 8, 64], BF16, nm)
    RT = [chm("RT%d" % i) for i in range(2)]
    KT = [chm("KT%d" % i) for i in range(2)]
    KH = [tkm("KH%d" % i) for i in range(2)]
    VV = [tkm("VV%d" % i) for i in range(2)]
    Min = t64("Min")
    k.ld("sp", f3(Min), C["m_in"].rearrange("p (q t) -> p q t", q=8))
    id8 = t64("id8")
    k.ld("sp", f3(id8), C["id8"].rearrange("p (q t) -> p q t", q=8))
    if dplr:
        AT = [chm("AT%d" % i) for i in range(2)]
        BT = [chm("BT%d" % i) for i in range(2)]
        BH = [tkm("BH%d" % i) for i in range(2)]
        AK = [tkm("AK%d" % i) for i in range(2)]
        Mst, Mts = t64("Mst"), t64("Mts")
        k.ld("sp", f3(Mst), C["m_st"].rearrange("p (q t) -> p q t", q=8))
        k.ld("sp", f3(Mts), C["m_ts"].rearrange("p (q t) -> p q t", q=8))
        PCt = k.sb(st, [64, 8, NCHT], F32, "PCt")
        for d in range(2):
            k.ld("sp", PCt.v[:, d * 4:(d + 1) * 4, :], S["D_PC"][d].rearrange("(h j) c -> j h c", j=64))
    else:
        PCr = k.sb(st, [64, 8], F32, "PCr")
        k.ld("sp", PCr.v[:, :], C["ret_pc"])
        GTc = b64("GTc")
        k.tt("dve", f3(GTc), f3(id8), PCr.v[:, :].rr("p (q o) -> p q o", o=1).bc([64, 8, 64]), ALU.mult)
    sets = []
    for c in range(NCG):
        B = dict(ArkT=b64("ArkT%d" % c), Yv=t64("Yv%d" % c), H=t64("H%d" % c))
        if dplr:
            B.update(Ls=[b64("L%d_%d" % (i, c)) for i in range(5)], Ns=[b64("N%d_%d" % (i, c)) for i in range(6)],
                     AakT=b64("AakT%d" % c), ArbT=b64("ArbT%d" % c), Qe=b64("Qe%d" % c), GTt=b64("GTt%d" % c),
                     X=[k.sb(st, [64, 8, 128], F32, "X%d_%d" % (i, c)) for i in range(2)],
                     Xb=k.sb(st, [64, 8, 128], BF16, "Xb_%d" % c),
                     psX=k.ps(st, [64, 1024], F32, "psX%d" % c))
        sets.append(B)
    ST = [b64("ST%d" % i) for i in range(2)]
    Yt = [t64("Yt%d" % i) for i in range(2)]
    psA = [k.ps(st, [64, 512], F32, "psA%d" % i) for i in range(2)]
    psY = [k.ps(st, [64, 512], F32, "psY%d" % i) for i in range(2)]
    cnt = dict(a=0, y=0, e=0)
    p3 = lambda p: p.v[:, :].rr("p (q t) -> p q t", q=8)

    def npa():
        p = psA[cnt["a"] % 2]
        cnt["a"] += 1
        return p

    def ev():
        cnt["e"] += 1
        return "act" if cnt["e"] % 2 else "dve"

    def prod(dst, lhs, rhs, mask):
        p = npa()
        for q in range(8):
            k.mm(p3(p)[:, q, :], lhs(q), rhs(q))
        if mask is None:
            k.cp(ev(), f3(dst), p3(p))
        else:
            k.tt("dve", f3(dst), p3(p), f3(mask), ALU.mult)

    def views(rb, cl):
        sl = {d: slice(cl[d] * CH, (cl[d] + 1) * CH) for d in range(2)}
        dq = lambda q: q // 4
        chs = lambda buf: (lambda q: buf[rb].v[:, q, sl[dq(q)]])
        tks = lambda buf: (lambda q: buf[rb].v[:, cl[dq(q)], q, :])
        return sl, chs, tks

    def ppart(B, rb, cl, grp):
        sl, chs, tks = views(rb, cl)
        rt, kt, kh, vv = chs(RT), chs(KT), tks(KH), tks(VV)
        prod(B["ArkT"], kt, rt, Min)
        yield
        if dplr:
            at, bt, bh = chs(AT), chs(BT), tks(BH)
            Ls, Ns, AakT, ArbT = B["Ls"], B["Ns"], B["AakT"], B["ArbT"]
            prod(Ns[0], bt, at, Mst)
            prod(Ls[0], at, bt, Mts)
            yield
            prod(AakT, kt, at, Mst)
            prod(ArbT, bt, rt, Min)
            yield
            xc, xn = B["X"]
            xb = B["Xb"]
            for d in range(2):
                k.cp("act", xc.v[:, d * 4:(d + 1) * 4, 0:64], AK[rb].v[:, cl[d], d * 4:(d + 1) * 4, :])
            p = npa()
            for q in range(8):
                k.mm(p3(p)[:, q, :], AakT.v[:, q, :], vv(q))
            k.cp("dve", xc.v[:, :, 64:128], p3(p))
            k.cp("act", f3(xb), f3(xc))
            yield
            px = B["psX"].v[:, :].rr("p (q c) -> p q c", q=8)
            for lv in range(6):
                for q in range(8):
                    k.mm(px[:, q, :], Ns[lv].v[:, q, :], xb.v[:, q, :])
                if lv < 5:
                    if lv < 4:
                        prod(Ls[lv + 1], lambda q: Ns[lv].v[:, q, :], lambda q: Ls[lv].v[:, q, :], None)
                    prod(Ns[lv + 1], lambda q: Ls[lv].v[:, q, :], lambda q: Ns[lv].v[:, q, :], None)
                k.tt("dve", f3(xn), f3(xc), px, ALU.add)
                xc, xn = xn, xc
                k.cp("act", f3(xb), f3(xc))
                yield
            wq = lambda q: xb.v[:, q, 0:64]
            uv = lambda q: xb.v[:, q, 64:128]
            p = npa()
            for q in range(8):
                k.mm(p3(p)[:, q, :], wq(q), ArbT.v[:, q, :])
            for d in range(2):
                k.tt("dve", B["Qe"].v[:, d * 4:(d + 1) * 4, :], p3(p)[:, d * 4:(d + 1) * 4, :],
                     RT[rb].v[:, d * 4:(d + 1) * 4, sl[d]], ALU.add)
            p = npa()
            for q in range(8):
                k.mm(p3(p)[:, q, :], wq(q), bh(q))
            for d in range(2):
                c_abs = grp[d] * NCG + cl[d]
                k.tt("dve", B["GTt"].v[:, d * 4:(d + 1) * 4, :], id8.v[:, d * 4:(d + 1) * 4, :],
                     PCt.v[:, d * 4:(d + 1) * 4, c_abs:c_abs + 1].bc([64, 4, 64]), ALU.mult)
            k.tt("dve", f3(B["GTt"]), f3(B["GTt"]), p3(p), ALU.add)
            yield
        p = npa()
        for q in range(8):
            k.mm(p3(p)[:, q, :], B["ArkT"].v[:, q, :], vv(q), True, not dplr)
            if dplr:
                k.mm(p3(p)[:, q, :], ArbT.v[:, q, :], uv(q), False, True)
        k.cp(ev(), f3(B["Yv"]), p3(p))
        p = npa()
        for q in range(8):
            k.mm(p3(p)[:, q, :], kh(q), vv(q), True, not dplr)
            if dplr:
                k.mm(p3(p)[:, q, :], bh(q), uv(q), False, True)
        k.cp(ev(), f3(B["H"]), p3(p))
        yield

    k.memset("dve", f3(ST[0]), 0.0)
    ngl = T // GT_
    ncg_ctx = CTX // GT_
    fw = list(range(ngl, ngl + ncg_ctx)) + list(range(ngl))
    bw = list(range(ngl + ncg_ctx - 1, ngl - 1, -1)) + list(range(ngl - 1, -1, -1))
    order = {0: fw, 1: bw}
    step = 0

    def emit_loads(gi):
        rb = gi % 2
        grp = {d: order[d][gi] for d in range(2)}
        for d in range(2):
            t0 = grp[d] * GT_
            qs = slice(d * 4, (d + 1) * 4)
            chv = lambda nm: S[pre + nm][d, :, t0:t0 + GT_].rearrange("(h j) t -> j h t", j=64)
            tkv = lambda ap: ap[t0:t0 + GT_, :].rearrange("(n t) (h j) -> t n h j", t=CH, j=64)
            k.ld("sp", RT[rb].v[:, qs, :], chv("Rt"))
            k.ld("sp", KT[rb].v[:, qs, :], chv("Kt"))
            k.ld("sp", KH[rb].v[:, :, qs, :], tkv(S[pre + "Kh"][d]))
            k.ld("sp", VV[rb].v[:, :, qs, :], tkv(S["D_V"][d] if dplr else S["C_V"]))
            if dplr:
                k.ld("sp", AT[rb].v[:, qs, :], chv("At"))
                k.ld("sp", BT[rb].v[:, qs, :], chv("Bt"))
                k.ld("sp", BH[rb].v[:, :, qs, :], tkv(S["D_Bh"][d]))
                k.ld("sp", AK[rb].v[:, :, qs, :], tkv(S["D_Atok"][d]))

    emit_loads(0)
    for gi in range(ngl + ncg_ctx):
        rb = gi % 2
        grp = {d: order[d][gi] for d in range(2)}
        if gi + 1 < ngl + ncg_ctx:
            emit_loads(gi + 1)
        cls = [{0: ci, 1: NCG - 1 - ci} for ci in range(NCG)]
        gens = [ppart(sets[ci], rb, cls[ci], grp) for ci in range(NCG)]
        live = list(gens)
        while live:
            for g_ in list(live):
                try:
                    next(g_)
                except StopIteration:
                    live.remove(g_)
        for ci in range(NCG):
            B, cl = sets[ci], cls[ci]
            sl, chs, tks = views(rb, cl)
            Sc, Sn = ST[step % 2], ST[(step + 1) % 2]
            qe = (lambda q: B["Qe"].v[:, q, :]) if dplr else chs(RT)
            gt = (lambda q: B["GTt"].v[:, q, :]) if dplr else (lambda q: GTc.v[:, q, :])
            py = psY[cnt["y"] % 2]
            cnt["y"] += 1
            for q in range(8):
                k.mm(p3(py)[:, q, :], qe(q), Sc.v[:, q, :])
            p = npa()
            for q in range(8):
                k.mm(p3(p)[:, q, :], gt(q), Sc.v[:, q, :])
            k.tt("dve", f3(Sn), p3(p), f3(B["H"]), ALU.add)
            yt = Yt[step % 2]
            k.tt("dve", f3(yt), p3(py), f3(B["Yv"]), ALU.add)
            for d in range(2):
                tt0 = grp[d] * GT_ + cl[d] * CH
                k.stv("sp", S[pre + "Y"][d, tt0:tt0 + CH, :].rearrange("t (h i) -> t h i", i=64),
                      yt.v[:, d * 4:(d + 1) * 4, :])
            step += 1


GN_EPS = 64e-5


def postphase(k, st, cfg):
    S, Wt = cfg["S"], cfg["W"]
    TT = cfg["T"] + cfg["CTX"]
    row = lambda nm, ap: (lambda t: (k.ld("sp", t.v[:, :], ap.partition_broadcast(128)), t)[1])(k.sb(st, [128, 256], F32, nm))
    retg = row("retg", Wt["ret_g"])
    lng = row("lng", Wt["ln_g"])
    lnb = row("lnb", Wt["ln_b"])
    gc = k.sb(st, [128, 512], F32, "gc")
    gd = k.sb(st, [128, 256], F32, "gd")
    vd = k.sb(st, [128, 4, 64], BF16, "vd")
    vdf = k.sb(st, [128, 4, 64], F32, "vdf")
    bs = k.sb(st, [128, 4], F32, "bs")

    def bufset(nm):
        return dict(y=[k.sb(st, [128, 4, 64], F32, nm + "y%d" % i) for i in range(2)],
                    yc=k.sb(st, [128, 4, 64], F32, nm + "yc"), sq=k.sb(st, [128, 4, 64], F32, nm + "sq"),
                    m4=k.sb(st, [128, 4], F32, nm + "m4"), v4=k.sb(st, [128, 4], F32, nm + "v4"),
                    acc=k.sb(st, [128, 256], F32, nm + "acc"), yn=k.sb(st, [128, 256], F32, nm + "yn"), i=0)

    BC, BD = bufset("c"), bufset("d")

    def head_norm(B, yb, g, b, dst):
        yc, sq, m4, v4 = B["yc"], B["sq"], B["m4"], B["v4"]
        k.op("dve", lambda e: e.tensor_reduce(out=m4[:, :], in_=yb[:, :, :], axis=AX.X, op=ALU.add), r=[yb], w=[m4])
        k.ts("dve", m4.v[:, :], m4.v[:, :], 1.0 / 64, None, ALU.mult)
        yield
        k.tt("dve", yc.v[:, :, :], yb.v[:, :, :], m4.v[:, :].rr("p (h o) -> p h o", o=1).bc([128, 4, 64]), ALU.subtract)
        yield
        k.tt("dve", sq.v[:, :, :], yc.v[:, :, :], yc.v[:, :, :], ALU.mult)
        yield
        k.op("dve", lambda e: e.tensor_reduce(out=v4[:, :], in_=sq[:, :, :], axis=AX.X, op=ALU.add), r=[sq], w=[v4])
        yield
        k.ts("dve", v4.v[:, :], v4.v[:, :], 1.0 / 64, GN_EPS, ALU.mult, ALU.add)
        yield
        k.act(v4.v[:, :], v4.v[:, :], ACT.Sqrt)
        yield
        k.recip(v4.v[:, :], v4.v[:, :])
        yield
        k.tt("dve", yc.v[:, :, :], yc.v[:, :, :], v4.v[:, :].rr("p (h o) -> p h o", o=1).bc([128, 4, 64]), ALU.mult)
        yield
        k.tt("dve", dst, yc.v[:, :, :].rr("p h i -> p (h i)"), g.v[:, :], ALU.mult)
        if b is not None:
            k.tt("dve", dst, dst, b.v[:, :], ALU.add)
        yield

    def part_c(tsl):
        B = BC
        acc, yn = B["acc"], B["yn"]
        k.ld("sp", gc.v[:, :], S["C_G"][tsl, :])
        for d in range(2):
            yb = B["y"][B["i"] % 2]
            B["i"] += 1
            k.ld("sp", yb.v[:, :, :], S["C_Y"][d, tsl, :].rearrange("t (h i) -> t h i", i=64))
            yield from head_norm(B, yb, retg, None, yn.v[:, :])
            if d == 0:
                k.tt("dve", acc.v[:, :], yn.v[:, :], gc.v[:, 0:256], ALU.mult)
            else:
                k.tt("dve", yn.v[:, :], yn.v[:, :], gc.v[:, 256:512], ALU.mult)
                k.tt("dve", acc.v[:, :], acc.v[:, :], yn.v[:, :], ALU.add)
            yield
        k.stv("sp", S["O"][tsl, 512:768], acc.v[:, :])

    def part_d(tsl):
        B = BD
        acc, yn = B["acc"], B["yn"]
        k.ld("sp", gd.v[:, :], S["D_G"][tsl, :])
        for d in range(2):
            yb = B["y"][B["i"] % 2]
            B["i"] += 1
            k.ld("sp", yb.v[:, :, :], S["D_Y"][d, tsl, :].rearrange("t (h i) -> t h i", i=64))
            k.ld("sp", vd.v[:, :, :], S["D_V"][d, tsl, :].rearrange("t (h i) -> t h i", i=64))
            k.ld("sp", bs.v[:, :], S["D_BS"][d, tsl, :])
            yield from head_norm(B, yb, lng, lnb, yn.v[:, :])
            k.tt("dve", vdf.v[:, :, :], vd.v[:, :, :], bs.v[:, :].rr("p (h o) -> p h o", o=1).bc([128, 4, 64]), ALU.mult)
            k.tt("dve", yn.v[:, :], yn.v[:, :], vdf.v[:, :, :].rr("p h i -> p (h i)"), ALU.add)
            yield
            if d == 0:
                k.cp("act", acc.v[:, :], yn.v[:, :])
            else:
                k.tt("dve", acc.v[:, :], acc.v[:, :], yn.v[:, :], ALU.add)
            yield
        k.tt("dve", acc.v[:, :], acc.v[:, :], gd.v[:, :], ALU.mult)
        k.stv("sp", S["O"][tsl, 768:1024], acc.v[:, :])

    for t0 in range(0, TT, 128):
        tsl = slice(t0, t0 + 128)
        live = [part_c(tsl), part_d(tsl)]
        while live:
            for g_ in list(live):
                try:
                    next(g_)
                except StopIteration:
                    live.remove(g_)


def modphase(k, st, cfg):
    cv = k.sb(st, [128, 2, 8], F32, "cv")
    for r in range(2):
        k.ld("sp", cv.v[:, r, :], cfg["cvec"][r].rearrange("(c p) -> p c", p=128), allow_slow_non_contiguous=True)
    sc = k.sb(st, [128, 8, 2], F32, "sc")
    k.act(sc.v[:, :, :].rr("p c r -> p r c"), cv.v[:, :, :], ACT.Silu)
    wm = [k.sb(st, [128, 8, 512], F32, "wm%d" % i) for i in range(2)]
    bm = k.sb(st, [2, 9 * D], F32, "bm")
    k.ld("sp", bm.v[:, :], cfg["b_mod"].partition_broadcast(2))
    ob = k.sb(st, [2, 9 * D], F32, "ob")
    pm = [k.ps(st, [128, 512], F32, "pm%d" % i) for i in range(2)]
    wv = cfg["w_mod"].rearrange("(c p) f -> p c f", p=128)
    for j in range(18):
        w = wm[j % 2]
        for c in range(8):
            k.ld("sp", w.v[:, c, :], wv[:, c, j * 512:(j + 1) * 512])
        p = pm[j % 2]
        for c in range(8):
            k.mm(p.v[0:2, :], sc.v[:, c, :], w.v[:, c, :], c == 0, c == 7)
        k.tt("dve", ob.v[:, j * 512:(j + 1) * 512], p.v[0:2, :], bm.v[:, j * 512:(j + 1) * 512], ALU.add)
    k.stv("sp", cfg["mod"], ob.v[:, :])


def build_program(T, CTX, L, stop_after=None):
    TT = T + CTX
    nc = bass.Bass("TRN2", target_bir_lowering=False, dynamic_dma_scratch_size=8192)

    def din(name, shape, dt=F32):
        return nc.dram_tensor(name, list(shape), dt, kind="ExternalInput").ap()

    def scr(name, shape, dt=F32):
        return nc.dram_tensor(name, list(shape), dt, kind="Internal").ap()

    I = dict(
        xin=din("xin", [TT, D]), cvec=din("cvec", [2, D]),
        w_mod=din("w_mod", [L, D, 9 * D]), b_mod=din("b_mod", [L, 9 * D]), norm_g=din("norm_g", [L, 3, D]),
        ffn_w_in=din("ffn_w_in", [L, 2, D, 2 * DFF]), ffn_w_out=din("ffn_w_out", [L, 2, DFF, D]),
        w_in=din("w_in", [L, D, 3456]), w_out=din("w_out", [L, D, D]), final_g=din("final_g", [D]),
        w_sw=din("w_sw", [L, D, 1280]), w_d=din("w_d", [L, 2, D, 896]), pcols=din("pcols", [L, 128, 24]),
        mu=din("mu", [L, 2, 896]), w2=din("w2", [L, 2, 64, 256]), a2=din("a2", [L, 2, 64, 256]),
        g2=din("g2", [L, 128, 256]), ret_g=din("ret_g", [L, 256]), ln_g=din("ln_g", [L, 256]),
        ln_b=din("ln_b", [L, 256]), sink=din("sink", [L, 4]),
    )
    Cn = dict(
        ident=din("ident", [128, 128]), bd64=din("bd64", [128, 128]), e2=din("e2", [128, 2]),
        ropeA_c=din("ropeA_c", [128, T]), ropeA_s=din("ropeA_s", [128, T]),
        ropeS_c=din("ropeS_c", [128, T]), ropeS_s=din("ropeS_s", [128, T]),
        ret_tab=din("ret_tab", [128, 2, 2, 3, CH]), ret_pc=din("ret_pc", [64, 8]),
        m_st=din("m_st", [64, 512]), m_ts=din("m_ts", [64, 512]), m_in=din("m_in", [64, 512]),
        id8=din("id8", [64, 512]), mprev=din("mprev", [128, 128]), mnext=din("mnext", [128, 128]),
    )
    out = nc.dram_tensor("out", [T, D], F32, kind="ExternalOutput").ap()
    dbg = stop_after is not None
    mk = (lambda name, shape, dt=F32: nc.dram_tensor(name, list(shape), dt, kind="ExternalOutput").ap()) if dbg else scr
    S = dict(
        xres=mk("xres", [TT, D]), mod=mk("modr", [2, 9 * D]), hT=mk("hT", [D, TT], BF16),
        QA=mk("QA", [256, TT], BF16), KA=mk("KA", [128, TT], BF16), VA=mk("VA", [TT, 128], BF16),
        QB=mk("QB", [256, TT], BF16), KB=mk("KB", [128, TT], BF16), VB=mk("VB", [TT, 128], BF16),
        O=mk("O", [TT, 1024]),
        C_Rt=mk("C_Rt", [2, 256, TT], BF16), C_Kt=mk("C_Kt", [2, 256, TT], BF16), C_Kh=mk("C_Kh", [2, TT, 256], BF16),
        C_V=mk("C_V", [TT, 256], BF16), C_G=mk("C_G", [TT, 512]), C_Y=mk("C_Y", [2, TT, 256]),
        D_Rt=mk("D_Rt", [2, 256, TT], BF16), D_Kt=mk("D_Kt", [2, 256, TT], BF16), D_At=mk("D_At", [2, 256, TT], BF16),
        D_Bt=mk("D_Bt", [2, 256, TT], BF16), D_Kh=mk("D_Kh", [2, TT, 256], BF16), D_Bh=mk("D_Bh", [2, TT, 256], BF16),
        D_Atok=mk("D_Atok", [2, TT, 256], BF16), D_V=mk("D_V", [2, TT, 256], BF16), D_Y=mk("D_Y", [2, TT, 256]),
        D_PC=mk("D_PC", [2, 256, TT // CH]), D_G=mk("D_G", [TT, 256]), D_BS=mk("D_BS", [2, TT, 4]),
    )
    k = K(nc)
    stages = []

    def phase(name, fn, cfg):
        if stop_after is not None and stop_after in stages:
            return
        with contextlib.ExitStack() as st:
            fn(k, st, cfg)
            k.barrier()
        stages.append(name)
        k.marks.append((name, {e.name: e.n for e in k.E.values()}))

    with k.stack:
        segs_all = [(0, T, 0), (T, CTX, 1)]
        for l in range(L):
            last = l == L - 1
            W = dict(w_in=I["w_in"][l], w_sw=I["w_sw"][l], w_d=I["w_d"][l], mu=I["mu"][l], pcols=I["pcols"][l],
                     w2=I["w2"][l], a2=I["a2"][l], g2=I["g2"][l], ret_g=I["ret_g"][l], ln_g=I["ln_g"][l],
                     ln_b=I["ln_b"][l], sink=I["sink"][l])
            base = dict(S=S, C=Cn, W=W, T=T, CTX=CTX)
            phase("mod%d" % l, modphase, dict(cvec=I["cvec"], w_mod=I["w_mod"][l], b_mod=I["b_mod"][l], mod=S["mod"]))
            phase("f1_%d" % l, rowphase, dict(
                xin=I["xin"] if l == 0 else S["xres"], xout=S["xres"], segs=segs_all, mod=S["mod"], ident=Cn["ident"],
                ffn=dict(w_in=I["ffn_w_in"][l, 0], w_out=I["ffn_w_out"][l, 0], g=I["norm_g"][l, 0], shift=0, scale=1,
                         gate=2),
                post=dict(kind="hT", g=I["norm_g"][l, 1], shift=3, scale=4, hT=S["hT"])))
            phase("proj%d" % l, projphase, base)
            phase("attn%d" % l, attnphase, dict(base, need_ctx=not last))
            phase("scanC%d" % l, scanphase, dict(base, dplr=False))
            phase("scanD%d" % l, scanphase, dict(base, dplr=True))
            phase("post%d" % l, postphase, base)
            phase("f2_%d" % l, rowphase, dict(
                xin=S["xres"], xout=S["xres"], segs=segs_all if not last else [(0, T, 0)], mod=S["mod"],
                ident=Cn["ident"],
                pre=dict(o=S["O"], w_out=I["w_out"][l], gate=5),
                ffn=dict(w_in=I["ffn_w_in"][l, 1], w_out=I["ffn_w_out"][l, 1], g=I["norm_g"][l, 2], shift=6, scale=7,
                         gate=8),
                post=dict(kind="final", g=I["final_g"], out=out) if last else None))
    return nc, k


def _consts(T):
    c = {}
    c["ident"] = np.eye(128, dtype=np.float32)
    bd = np.zeros((128, 128), np.float32)
    bd[:64, :64] = 1
    bd[64:, 64:] = 1
    c["bd64"] = bd
    e2 = np.zeros((128, 2), np.float32)
    e2[:64, 0] = 1
    e2[64:, 1] = 1
    c["e2"] = e2
    t = np.arange(T)
    row = (t // 64).astype(np.float32)
    col = (t % 64).astype(np.float32)
    inv16 = (1.0 / (np.float32(10000.0) ** (np.arange(0, 32, 2, dtype=np.float32) / np.float32(32)))).astype(np.float32)
    inv32 = (1.0 / (np.float32(10000.0) ** (np.arange(0, 64, 2, dtype=np.float32) / np.float32(64)))).astype(np.float32)
    ca = np.zeros((64, T), np.float32)
    sa = np.zeros((64, T), np.float32)
    for blk, pos in ((0, row), (32, col)):
        ang = pos[None, :] * inv16[:, None]
        ca[blk:blk + 16] = np.cos(ang)
        ca[blk + 16:blk + 32] = np.cos(ang)
        sa[blk:blk + 16] = -np.sin(ang)
        sa[blk + 16:blk + 32] = np.sin(ang)
    ang = t.astype(np.float32)[None, :] * inv32[:, None]
    cs = np.concatenate([np.cos(ang), np.cos(ang)], 0).astype(np.float32)
    ss = np.concatenate([-np.sin(ang), np.sin(ang)], 0).astype(np.float32)
    c["ropeA_c"] = np.concatenate([ca, ca], 0)
    c["ropeA_s"] = np.concatenate([sa, sa], 0)
    c["ropeS_c"] = np.concatenate([cs, cs], 0)
    c["ropeS_s"] = np.concatenate([ss, ss], 0)
    lg = np.log1p(-np.exp2(-5.0 - np.arange(4, dtype=np.float64)))
    i = np.arange(CH, dtype=np.float64)
    rt = np.zeros((128, 2, 2, 3, CH), np.float64)
    for p in range(128):
        for g in range(2):
            l_ = lg[2 * g + p // 64]
            for d in range(2):
                csum = (i + 1) * l_ if d == 0 else (CH - i) * l_
                rt[p, g, d, 0] = np.exp(csum)
                rt[p, g, d, 1] = np.exp(-csum) / 8.0
                rt[p, g, d, 2] = np.exp(CH * l_ - csum) / 8.0
    c["ret_tab"] = rt.astype(np.float32)
    c["ret_pc"] = np.tile(np.exp(CH * lg)[None, :], (64, 2)).astype(np.float32)
    s_ = np.arange(64)[:, None]
    t_ = np.arange(64)[None, :]
    lt = (s_ < t_).astype(np.float32)
    gt = (s_ > t_).astype(np.float32)
    eye = np.eye(64, dtype=np.float32)
    c["m_st"] = np.concatenate([lt] * 4 + [gt] * 4, 1)
    c["m_ts"] = np.concatenate([gt] * 4 + [lt] * 4, 1)
    c["m_in"] = np.concatenate([lt + eye] * 4 + [gt + eye] * 4, 1)
    c["id8"] = np.concatenate([eye] * 8, 1)
    j = np.arange(128)[:, None]
    ii = np.arange(128)[None, :]
    c["mprev"] = (ii <= j).astype(np.float32)
    c["mnext"] = (j <= ii).astype(np.float32)
    return c


def _host_maps(inp, T, CTX, L):
    f = lambda a: np.ascontiguousarray(np.asarray(a, dtype=np.float32))
    w_in = f(inp["w_in"])
    pa = np.concatenate([np.arange(16, 32), np.arange(0, 16), np.arange(48, 64), np.arange(32, 48)])
    psq = np.concatenate([np.arange(32, 64), np.arange(0, 32)])
    cols = []
    for base, nh, perm in ((0, 4, pa), (256, 2, pa), (512, 4, pa), (768, 2, pa), (1024, 4, psq), (1280, 4, psq)):
        for h in range(nh):
            cols.append(base + h * 64 + perm)
    cols = np.concatenate(cols)
    w_sw = np.ascontiguousarray(w_in[:, :, cols])
    dcols = [np.concatenate([np.arange(2304, 3072), np.arange(3200, 3264), np.arange(3328, 3392)]),
             np.concatenate([np.arange(2304, 3072), np.arange(3264, 3328), np.arange(3392, 3456)])]
    w_d = np.ascontiguousarray(np.stack([w_in[:, :, dc] for dc in dcols], 1))
    qkg = f(inp["qk_norm_g"])
    pcols = np.zeros((L, 128, 24), np.float32)
    two = lambda v: np.ascontiguousarray(v.reshape(2, 128).T)
    for l in range(L):
        for qk in range(2):
            g = qkg[l, qk]
            pcols[l, :, 2 * qk] = np.tile(g, 2)
            pcols[l, :, 2 * qk + 1] = np.tile(g[pa], 2)
        for d in range(2):
            pcols[l, :, 4 + 2 * d:6 + 2 * d] = two(f(inp["rwkv_w0"])[l, d])
            pcols[l, :, 8 + 2 * d:10 + 2 * d] = two(f(inp["rwkv_a0"])[l, d])
            pcols[l, :, 16 + 2 * d:18 + 2 * d] = two(f(inp["rwkv_rho"])[l, d].reshape(256))
        pcols[l, :, 12:14] = two(f(inp["rwkv_k_k"])[l])
        pcols[l, :, 14:16] = two(f(inp["rwkv_k_a"])[l])
    shared = dict(
        w_mod=f(inp["w_mod"]), b_mod=f(inp["b_mod"]), norm_g=f(inp["norm_g"]), ffn_w_in=f(inp["ffn_w_in"]),
        ffn_w_out=f(inp["ffn_w_out"]), w_in=w_in, w_out=f(inp["w_out"]), final_g=f(inp["final_norm_g"]),
        w_sw=w_sw, w_d=w_d, pcols=pcols, mu=f(inp["rwkv_mu"]), w2=f(inp["rwkv_w2"]), a2=f(inp["rwkv_a2"]),
        g2=f(inp["rwkv_g2"]), ret_g=f(inp["ret_norm_g"]), ln_g=f(inp["rwkv_ln_g"]), ln_b=f(inp["rwkv_ln_b"]),
        sink=f(inp["attn_sink"]))
    shared.update(_consts(T))
    x, c, ctx, c_ctx = f(inp["x"]), f(inp["c"]), f(inp["ctx"]), f(inp["c_ctx"])
    maps = []
    for b in range(x.shape[0]):
        m = dict(shared)
        m["xin"] = np.ascontiguousarray(np.concatenate([x[b], ctx[b]], 0))
        m["cvec"] = np.ascontiguousarray(np.stack([c[b], c_ctx], 0))
        maps.append(m)
    return maps


_PROG = {}


def kernel(**inputs):
    x = np.asarray(inputs["x"])
    B, T, _ = x.shape
    CTX = np.asarray(inputs["ctx"]).shape[1]
    L = np.asarray(inputs["w_in"]).shape[0]
    key = (T, CTX, L)
    if key not in _PROG:
        _PROG[key] = build_program(T, CTX, L)[0]
    nc = _PROG[key]
    maps = _host_maps(inputs, T, CTX, L)
    res = run_bass_kernel_spmd(nc, maps, core_ids=list(range(B)))
    return np.stack([np.asarray(r["out"], dtype=np.float32) for r in res.results], 0)
```

```python
import contextlib
import numpy as np
import concourse.bass as bass
import concourse.mybir as mybir
from concourse.bass_utils import run_bass_kernel_spmd

ACT = mybir.ActivationFunctionType
ALU = mybir.AluOpType
AX = mybir.AxisListType
F32 = mybir.dt.float32
BF16 = mybir.dt.bfloat16

D = 1024
DFF = 2816
NMOD = 9
RMS_EPS = 1e-6
STRICT = True


class PSem:
    def __init__(self, h):
        self.h = h
        self.val = 0


class Eng:
    def __init__(self, name, h, sem):
        self.name, self.h, self.sem = name, h, sem
        self.n = 0
        self.seen = {}
        self.dseen = {}


class Buf:
    def __init__(self, k, t, name):
        self.k, self.t, self.name = k, t, name
        self.w = None
        self.r = {}
        self.psem = None
        self.dma_w = 0
        self.dma_rw = 0

    def __getitem__(self, key):
        return self.t[key]

    @property
    def v(self):
        return _VA(self)


class View:
    def __init__(self, buf, ap):
        self.buf, self.ap = buf, ap

    def rr(self, pat, **kw):
        return View(self.buf, self.ap.rearrange(pat, **kw))

    def bc(self, shape):
        return View(self.buf, self.ap.to_broadcast(list(shape)))

    def __getitem__(self, key):
        return View(self.buf, self.ap[key])


class _VA:
    def __init__(self, b):
        self.b = b

    def __getitem__(self, key):
        return View(self.b, self.b.t[key])


def _bufs(*xs):
    out = []
    for x in xs:
        if isinstance(x, View) and x.buf not in out:
            out.append(x.buf)
    return out


def _ap(x):
    return x.ap if isinstance(x, View) else x


class K:
    def __init__(self, nc):
        self.nc = nc
        self.stack = contextlib.ExitStack()
        self.E = {}
        for name, h in (("pe", nc.tensor), ("act", nc.scalar), ("dve", nc.vector),
                        ("pool", nc.gpsimd), ("sp", nc.sync)):
            sem = self.stack.enter_context(nc.semaphore("s_" + name))
            self.E[name] = Eng(name, h, sem)
        self.free_psems = []
        self.n_psems = 0
        self.all_psems = []
        self.bufs = []
        self.uid = 0
        self.ninstr = 0
        self.marks = []

    def _psem(self):
        if self.free_psems:
            return self.free_psems.pop()
        self.n_psems += 1
        h = self.stack.enter_context(self.nc.semaphore("d%d" % self.n_psems))
        p = PSem(h)
        self.all_psems.append(p)
        return p

    def sb(self, st, shape, dtype, name):
        self.uid += 1
        t = st.enter_context(self.nc.sbuf_tensor("%s_%d" % (name, self.uid), list(shape), dtype))
        b = Buf(self, t, name)
        st.callback(self._release, b)
        return b

    def ps(self, st, shape, dtype, name):
        self.uid += 1
        t = st.enter_context(self.nc.psum_tensor("%s_%d" % (name, self.uid), list(shape), dtype))
        return Buf(self, t, name)

    def _release(self, b):
        if b.psem is not None and not getattr(b.psem, "sw", False):
            self.free_psems.append(b.psem)
        b.psem = None

    def _wait_eng(self, E, X, n):
        if E.seen.get(X.name, 0) >= n:
            return
        E.h.wait_ge(X.sem, n)
        E.seen[X.name] = n
        self.ninstr += 1

    def _wait_d(self, E, p, v):
        if v <= 0 or E.dseen.get(id(p), 0) >= v:
            return
        E.h.wait_ge(p.h, v)
        E.dseen[id(p)] = v
        self.ninstr += 1

    def _deps(self, E, r, w):
        for b in r:
            if b.w is not None:
                self._wait_eng(E, b.w[0], b.w[1])
            if b.psem is not None:
                self._wait_d(E, b.psem, b.dma_w)
        for b in w:
            if b.w is not None and (STRICT or b.w[0] is not E) and not (E.name == "pe" and b.w[0] is E):
                self._wait_eng(E, b.w[0], b.w[1])
            for X, n in b.r.items():
                if STRICT or X is not E:
                    self._wait_eng(E, X, n)
            if b.psem is not None:
                self._wait_d(E, b.psem, b.dma_rw)

    def op(self, eng, fn, r=(), w=()):
        E = self.E[eng]
        self._deps(E, r, w)
        ins = fn(E.h)
        E.n += 1
        ins.then_inc(E.sem, 1)
        self.ninstr += 1
        for b in w:
            b.w = (E, E.n)
            b.r = {}
        for b in r:
            if b.w is None or b.w[0] is not E or b.w[1] != E.n:
                b.r[E] = E.n
        return ins

    def dma(self, q, out, in_, r=(), w=(), **kw):
        Q = self.E[q]
        self._deps(Q, r, w)
        anchor = w[0] if w else r[0]
        if anchor.psem is None:
            if q == "pool":
                self.n_psems += 1
                anchor.psem = PSem(self.stack.enter_context(self.nc.semaphore("w%d" % self.n_psems)))
                anchor.psem.sw = True
                self.all_psems.append(anchor.psem)
            else:
                anchor.psem = self._psem()
        p = anchor.psem
        assert getattr(p, "sw", False) == (q == "pool"), "buffer mixes SW and HW DGE DMAs: " + anchor.name
        Q.h.dma_start(out=out, in_=in_, **kw).then_inc(p.h, 16)
        p.val += 16
        self.ninstr += 1
        anchor.dma_rw = p.val
        if w:
            anchor.dma_w = p.val
            anchor.w = None
            anchor.r = {}

    def mm(self, out, lhsT, rhs, start=True, stop=True):
        return self.op("pe", lambda e: e.matmul(out.ap, lhsT.ap, rhs.ap, start=start, stop=stop),
                       r=_bufs(lhsT, rhs), w=[out.buf])

    def tr(self, out, in_, ident):
        return self.op("pe", lambda e: e.transpose(out.ap, in_.ap, ident.ap), r=_bufs(in_, ident), w=[out.buf])

    def tt(self, eng, out, in0, in1, op):
        return self.op(eng, lambda e: e.tensor_tensor(out=out.ap, in0=in0.ap, in1=in1.ap, op=op),
                       r=_bufs(in0, in1), w=[out.buf])

    def ts(self, eng, out, in0, s1, s2, op0, op1=None):
        kw = {} if op1 is None else dict(op1=op1)
        return self.op(eng, lambda e: e.tensor_scalar(out=out.ap, in0=in0.ap, scalar1=_ap(s1), scalar2=_ap(s2),
                                                      op0=op0, **kw), r=_bufs(in0, s1, s2), w=[out.buf])

    def stt(self, eng, out, in0, scalar, in1, op0, op1):
        return self.op(eng, lambda e: e.scalar_tensor_tensor(out=out.ap, in0=in0.ap, scalar=_ap(scalar), in1=in1.ap,
                                                             op0=op0, op1=op1), r=_bufs(in0, scalar, in1), w=[out.buf])

    def act(self, out, in_, func, bias=None, scale=None, accum=None):
        kw = {}
        if bias is not None:
            kw["bias"] = _ap(bias)
        if scale is not None:
            kw["scale"] = _ap(scale)
        w = [out.buf]
        if accum is not None:
            kw["accum_out"] = accum.ap
            w.append(accum.buf)
        return self.op("act", lambda e: e.activation(out=out.ap, in_=in_.ap, func=func, **kw),
                       r=_bufs(in_, bias, scale), w=w)

    def cp(self, eng, out, in_):
        if eng == "act":
            return self.op("act", lambda e: e.copy(out=out.ap, in_=in_.ap), r=[in_.buf], w=[out.buf])
        return self.op(eng, lambda e: e.tensor_copy(out=out.ap, in_=in_.ap), r=[in_.buf], w=[out.buf])

    def memset(self, eng, out, val):
        return self.op(eng, lambda e: e.memset(out.ap, val), w=[out.buf])

    def recip(self, out, in_):
        return self.op("dve", lambda e: e.reciprocal(out=out.ap, in_=in_.ap), r=[in_.buf], w=[out.buf])

    def ld(self, q, out, dram, **kw):
        return self.dma(q, out.ap, dram, w=[out.buf], **kw)

    def stv(self, q, dram, in_, **kw):
        return self.dma(q, dram, in_.ap, r=[in_.buf], **kw)

    def barrier(self):
        for E in self.E.values():
            for X in self.E.values():
                if X is not E and X.n > 0:
                    self._wait_eng(E, X, X.n)
            for p in self.all_psems:
                self._wait_d(E, p, p.val)


def mm(k, out_ap, lhsT_ap, rhs_ap, start, stop, r, w):
    return k.op("pe", lambda e: e.matmul(out_ap, lhsT_ap, rhs_ap, start=start, stop=stop), r=r, w=w)


def rowphase(k, st, cfg):
    nc = k.nc
    ffn = cfg["ffn"]
    pre = cfg.get("pre")
    post = cfg.get("post")
    w1 = k.sb(st, [128, 8, 2 * DFF], BF16, "w1")
    w2 = k.sb(st, [128, 22, D], BF16, "w2")
    w1v = ffn["w_in"].rearrange("(c p) f -> p c f", p=128)
    for c in range(8):
        for hlf in range(4):
            f0 = hlf * 1408
            k.dma("pool", w1[:, c, f0:f0 + 1408], w1v[:, c, f0:f0 + 1408], w=[w1])
    w2v = ffn["w_out"].rearrange("(c p) d -> p c d", p=128)
    for c in range(22):
        k.dma("pool", w2[:, c, :], w2v[:, c, :], w=[w2])
    if pre is not None:
        wo = k.sb(st, [128, 8, D], BF16, "wo")
        wov = pre["w_out"].rearrange("(c p) d -> p c d", p=128)
        for c in range(8):
            k.dma("pool", wo[:, c, :], wov[:, c, :], w=[wo])

    identf = k.sb(st, [128, 128], F32, "identf")
    k.dma("sp", identf[:, :], cfg["ident"], w=[identf])
    xt = [k.sb(st, [128, D], F32, "xt%d" % i) for i in range(4)]
    hT = k.sb(st, [128, 8, 512], BF16, "hT")
    gT = k.sb(st, [128, 22, 512], BF16, "gT")
    sg = k.sb(st, [128, 512], BF16, "sg")
    tmp = k.sb(st, [128, D], F32, "tmp")
    ssq = k.sb(st, [128, 4], F32, "ssq")
    rinv = k.sb(st, [128, 4], F32, "rinv")
    ps_u = [k.ps(st, [128, 512], F32, "psu%d" % i) for i in range(4)]
    ps_y = [k.ps(st, [128, 512], F32, "psy%d" % i) for i in range(2)]
    ps_t = k.ps(st, [128, D], F32, "pst")
    Ac = k.sb(st, [128, 8], F32, "Ac")
    Bc = k.sb(st, [128, 8], F32, "Bc")
    Gc = k.sb(st, [128, 8], F32, "Gc")
    A2c = k.sb(st, [128, 8], F32, "A2c")
    B2c = k.sb(st, [128, 8], F32, "B2c")
    G = k.sb(st, [128, D], F32, "G")
    PG = k.sb(st, [128, D], F32, "PG") if pre is not None else None
    gvec2 = None
    if post is not None and post["kind"] == "final":
        gvec2 = k.sb(st, [128, D], F32, "gvec2")
        k.dma("sp", gvec2[:, :], post["g"].partition_broadcast(128), w=[gvec2])

    def col(ap1d):
        return ap1d.rearrange("(c p) -> p c", p=128)

    def load_cols(dst, ap1d):
        k.dma("sp", dst[:, :], col(ap1d), w=[dst], allow_slow_non_contiguous=True)

    def setup_mod(mr):
        load_cols(Gc, ffn["g"])
        load_cols(Ac, cfg["mod"][mr, ffn["scale"] * D:(ffn["scale"] + 1) * D])
        load_cols(Bc, cfg["mod"][mr, ffn["shift"] * D:(ffn["shift"] + 1) * D])
        k.op("dve", lambda e: e.scalar_tensor_tensor(out=Ac[:, :], in0=Ac[:, :], scalar=1.0, in1=Gc[:, :],
                                                     op0=ALU.add, op1=ALU.mult), r=[Ac, Gc], w=[Ac])
        if post is not None and post["kind"] == "hT":
            load_cols(Gc, post["g"])
            load_cols(A2c, cfg["mod"][mr, post["scale"] * D:(post["scale"] + 1) * D])
            load_cols(B2c, cfg["mod"][mr, post["shift"] * D:(post["shift"] + 1) * D])
            k.op("dve", lambda e: e.scalar_tensor_tensor(out=A2c[:, :], in0=A2c[:, :], scalar=1.0, in1=Gc[:, :],
                                                         op0=ALU.add, op1=ALU.mult), r=[A2c, Gc], w=[A2c])
        k.dma("sp", G[:, :], cfg["mod"][mr, ffn["gate"] * D:(ffn["gate"] + 1) * D].partition_broadcast(128), w=[G])
        k.op("dve", lambda e: e.tensor_scalar(out=G[:, :], in0=G[:, :], scalar1=0.5, scalar2=None, op0=ALU.mult),
             r=[G], w=[G])
        if pre is not None:
            k.dma("sp", PG[:, :], cfg["mod"][mr, pre["gate"] * D:(pre["gate"] + 1) * D].partition_broadcast(128),
                  w=[PG])

    try:
        tmp2 = [tmp, k.sb(st, [128, D], F32, "tmpb")]
    except AssertionError:
        tmp2 = [tmp, tmp]

    def rms_all(xs):
        m_ = len(xs)
        junk = gT[:, 0:2, :]
        for s_, xb in enumerate(xs):
            k.op("act", lambda e: e.activation(out=junk, in_=xb[:, :].rearrange("p (a b) -> p a b", a=2),
                                               func=ACT.Square, accum_out=ssq[:, s_:s_ + 1]), r=[xb], w=[gT, ssq])
        k.op("dve", lambda e: e.tensor_scalar(out=rinv[:, 0:m_], in0=ssq[:, 0:m_], scalar1=1.0 / D, scalar2=RMS_EPS,
                                              op0=ALU.mult, op1=ALU.add), r=[ssq], w=[rinv])
        k.op("act", lambda e: e.activation(out=rinv[:, 0:m_], in_=rinv[:, 0:m_], func=ACT.Sqrt), r=[rinv], w=[rinv])
        k.op("dve", lambda e: e.reciprocal(out=rinv[:, 0:m_], in_=rinv[:, 0:m_]), r=[rinv], w=[rinv])

    def norm_mod_T_all(xs, A, B, dstT):
        rms_all(xs)
        for s, xb in enumerate(xs):
            tm = tmp2[s % 2]
            k.op("dve", lambda e: e.tensor_scalar(out=tm[:, :], in0=xb[:, :], scalar1=rinv[:, s:s + 1], scalar2=None,
                                                  op0=ALU.mult), r=[xb, rinv], w=[tm])
            for c in range(8):
                k.op("pe", lambda e: e.transpose(ps_t[:, c * 128:(c + 1) * 128], tm[:, c * 128:(c + 1) * 128],
                                                 identf[:, :]), r=[tm, identf], w=[ps_t])
            for c in range(8):
                if c % 2 == 0:
                    k.op("dve", lambda e: e.tensor_scalar(out=dstT[:, c, s * 128:(s + 1) * 128],
                                                          in0=ps_t[:, c * 128:(c + 1) * 128], scalar1=A[:, c:c + 1],
                                                          scalar2=B[:, c:c + 1], op0=ALU.mult, op1=ALU.add),
                         r=[ps_t, A, B], w=[dstT])
                else:
                    k.op("act", lambda e: e.activation(out=dstT[:, c, s * 128:(s + 1) * 128],
                                                       in_=ps_t[:, c * 128:(c + 1) * 128], func=ACT.Identity,
                                                       scale=A[:, c:c + 1], bias=B[:, c:c + 1]),
                         r=[ps_t, A, B], w=[dstT])

    ui = 0
    yi = 0
    xsets = [xt]
    if pre is None:
        try:
            xsets.append([k.sb(st, [128, D], F32, "xu%d" % i) for i in range(4)])
        except AssertionError:
            pass
    tiles = []
    for (tok0, ntok, mr) in cfg["segs"]:
        for t0 in range(tok0, tok0 + ntok, 512):
            tiles.append((t0, min(512, tok0 + ntok - t0), mr, tok0, ntok))

    def emit_xloads(m):
        t0_, n_, _, _, _ = tiles[m]
        xs_ = xsets[m % len(xsets)]
        for s in range(n_ // 128):
            k.dma("sp", xs_[s][:, :], cfg["xin"][t0_ + s * 128:t0_ + (s + 1) * 128, :], w=[xs_[s]])

    if len(xsets) == 2:
        emit_xloads(0)
    cur_mr = None
    for m_i, (t0, n, mr, tok0, ntok) in enumerate(tiles):
        if mr != cur_mr:
            setup_mod(mr)
            cur_mr = mr
        if True:
            ns = n // 128
            xt = xsets[m_i % len(xsets)]
            if len(xsets) == 1:
                emit_xloads(m_i)
            elif m_i + 1 < len(tiles):
                emit_xloads(m_i + 1)
            if pre is not None:
                for s in range(ns):
                    k.dma("sp", tmp[:, :], pre["o"][t0 + s * 128:t0 + (s + 1) * 128, :], w=[tmp])
                    for c in range(8):
                        k.op("pe", lambda e: e.transpose(ps_t[:, c * 128:(c + 1) * 128], tmp[:, c * 128:(c + 1) * 128],
                                                         identf[:, :]), r=[tmp, identf], w=[ps_t])
                    k.op("act", lambda e: e.copy(out=hT[:, :, s * 128:(s + 1) * 128],
                                                 in_=ps_t[:, :].rearrange("p (c t) -> p c t", c=8)), r=[ps_t], w=[hT])
                for s in range(ns):
                    for hf in range(2):
                        py = ps_y[yi % 2]
                        yi += 1
                        for c in range(8):
                            mm(k, py[:, :], hT[:, c, s * 128:(s + 1) * 128], wo[:, c, hf * 512:(hf + 1) * 512],
                               c == 0, c == 7, r=[hT, wo], w=[py])
                        sl = slice(hf * 512, (hf + 1) * 512)
                        k.op("dve", lambda e, py=py, sl=sl: e.tensor_tensor(out=tmp[:, sl], in0=py[:, :],
                                                                             in1=PG[:, sl], op=ALU.mult),
                             r=[py, PG], w=[tmp])
                        k.op("dve", lambda e, s=s, sl=sl: e.tensor_tensor(out=xt[s][:, sl], in0=xt[s][:, sl],
                                                                            in1=tmp[:, sl], op=ALU.add),
                             r=[xt[s], tmp], w=[xt[s]])
            norm_mod_T_all(xt[0:ns], Ac, Bc, hT)
            for fc in range(22 if cfg.get("stage", 9) >= 2 else 0):
                p1 = ps_u[ui % 4]
                p2 = ps_u[(ui + 1) % 4]
                ui += 2
                for c in range(8):
                    mm(k, p1[:, 0:n], w1[:, c, fc * 128:(fc + 1) * 128], hT[:, c, 0:n], c == 0, c == 7,
                       r=[w1, hT], w=[p1])
                for c in range(8):
                    mm(k, p2[:, 0:n], w1[:, c, DFF + fc * 128:DFF + (fc + 1) * 128], hT[:, c, 0:n], c == 0, c == 7,
                       r=[w1, hT], w=[p2])
                k.op("act", lambda e, p1=p1: e.activation(out=sg[:, 0:n], in_=p1[:, 0:n], func=ACT.Silu),
                     r=[p1], w=[sg])
                k.op("dve", lambda e, p2=p2, fc=fc: e.tensor_tensor(out=gT[:, fc, 0:n], in0=sg[:, 0:n],
                                                                     in1=p2[:, 0:n], op=ALU.mult),
                     r=[sg, p2], w=[gT])
            for s in range(ns if cfg.get("stage", 9) >= 3 else 0):
                for hf in range(2):
                    py = ps_y[yi % 2]
                    yi += 1
                    for fc in range(22):
                        mm(k, py[:, :], gT[:, fc, s * 128:(s + 1) * 128], w2[:, fc, hf * 512:(hf + 1) * 512],
                           fc == 0, fc == 21, r=[gT, w2], w=[py])
                    sl = slice(hf * 512, (hf + 1) * 512)
                    k.op("dve", lambda e, py=py, sl=sl: e.tensor_tensor(out=tmp[:, sl], in0=py[:, :], in1=G[:, sl],
                                                                         op=ALU.mult), r=[py, G], w=[tmp])
                    k.op("dve", lambda e, s=s, sl=sl: e.tensor_tensor(out=xt[s][:, sl], in0=xt[s][:, sl],
                                                                        in1=tmp[:, sl], op=ALU.add),
                         r=[xt[s], tmp], w=[xt[s]])
            if post is not None and post["kind"] == "final":
                if mr == 0:
                    rms_all(xt[0:ns])
                    for s in range(ns):
                        r0 = t0 + s * 128
                        tm = tmp2[s % 2]
                        k.op("dve", lambda e: e.scalar_tensor_tensor(out=tm[:, :], in0=xt[s][:, :],
                                                                     scalar=rinv[:, s:s + 1], in1=gvec2[:, :],
                                                                     op0=ALU.mult, op1=ALU.mult),
                             r=[xt[s], rinv, gvec2], w=[tm])
                        k.dma("sp", post["out"][r0:r0 + 128, :], tm[:, :], r=[tm])
                continue
            for s in range(ns):
                r0 = t0 + s * 128
                k.dma("sp", cfg["xout"][r0:r0 + 128, :], xt[s][:, :], r=[xt[s]])
            if post is not None and post["kind"] == "hT":
                norm_mod_T_all(xt[0:ns], A2c, B2c, hT)
            if post is not None and post["kind"] == "hT":
                for c in range(8):
                    k.dma("sp", post["hT"][c * 128:(c + 1) * 128, t0:t0 + n], hT[:, c, 0:n], r=[hT])


CH = 64
DECAY_SCALE = 0.6065306597126334


def projphase(k, st, cfg):
    S = cfg["S"]
    C = cfg["C"]
    Wt = cfg["W"]
    T, CTX = cfg["T"], cfg["CTX"]
    hTd = S["hT"]
    wabc = k.sb(st, [128, 8, 2304], BF16, "wabc")
    wgd = k.sb(st, [128, 8, 128], BF16, "wgd")
    wsw = k.sb(st, [128, 8, 1280], BF16, "wsw")
    wv = Wt["w_in"].rearrange("(c p) f -> p c f", p=128)
    for c in range(8):
        k.ld("pool", wabc.v[:, c, 0:1152], wv[:, c, 0:1152])
        k.ld("pool", wabc.v[:, c, 1152:2304], wv[:, c, 1152:2304])
        k.ld("pool", wgd.v[:, c, :], wv[:, c, 3072:3200])
        k.ld("pool", wsw.v[:, c, :], Wt["w_sw"].rearrange("(c p) f -> p c f", p=128)[:, c, :])
    wda = [k.sb(st, [128, 8, 896], BF16, "wda%d" % d) for d in range(2)]
    wdb = [k.sb(st, [128, 8, 896], BF16, "wdb%d" % d) for d in range(2)]
    with contextlib.ExitStack() as st2:
        wst = k.sb(st2, [128, 8, 896], F32, "wst")
        mub = k.sb(st2, [128, 896], F32, "mub")
        wtm = k.sb(st2, [128, 8, 896], F32, "wtm")
        for d in range(2):
            for c in range(8):
                k.ld("sp", wst.v[:, c, :], Wt["w_d"][d].rearrange("(c p) f -> p c f", p=128)[:, c, :])
            k.ld("sp", mub.v[:, :], Wt["mu"][d].partition_broadcast(128))
            for c in range(8):
                k.tt("dve", wtm.v[:, c, :], wst.v[:, c, :], mub.v[:, :], ALU.mult)
            k.cp("act", wda[d].v[:, :, :], wtm.v[:, :, :])
            k.tt("dve", wdb[d].v[:, :, :], wst.v[:, :, :], wtm.v[:, :, :], ALU.subtract)
        k.barrier()
    pc = k.sb(st, [128, 24], F32, "pcols")
    k.ld("sp", pc.v[:, :], Wt["pcols"])
    oma = k.sb(st, [128, 2], F32, "oma")
    k.ts("dve", oma.v[:, :], pc.v[:, 14:16], -1.0, 1.0, ALU.mult, ALU.add)
    identf = k.sb(st, [128, 128], F32, "identf")
    k.ld("sp", identf.v[:, :], C["ident"])
    bd = k.sb(st, [128, 128], F32, "bd")
    k.ld("sp", bd.v[:, :], C["bd64"])
    e2 = k.sb(st, [128, 2], F32, "e2")
    k.ld("sp", e2.v[:, :], C["e2"])
    rtab = k.sb(st, [128, 2, 2, 3, CH], F32, "rtab")
    k.ld("sp", rtab.v[:, :, :, :, :], C["ret_tab"])
    w2s = k.sb(st, [64, 2, 256], BF16, "w2s")
    a2s = k.sb(st, [64, 2, 256], BF16, "a2s")
    g2s = k.sb(st, [128, 256], BF16, "g2s")
    for d in range(2):
        k.ld("pool", w2s.v[:, d, :], Wt["w2"][d])
        k.ld("pool", a2s.v[:, d, :], Wt["a2"][d])
    k.ld("pool", g2s.v[:, :], Wt["g2"])

    NT = 512
    hTt = k.sb(st, [128, 8, NT + 2], BF16, "hTt")
    ca = k.sb(st, [128, NT], F32, "ca")
    sa = k.sb(st, [128, NT], F32, "sa")
    cs_ = k.sb(st, [128, NT], F32, "cs")
    ss_ = k.sb(st, [128, NT], F32, "ss")
    gca = k.sb(st, [128, 2, NT], F32, "gca")
    gsa = k.sb(st, [128, 2, NT], F32, "gsa")
    F = [k.sb(st, [128, NT], F32, "f%d" % i) for i in range(12)]
    ob = k.sb(st, [128, NT], BF16, "ob")
    sbf = [k.sb(st, [128, NT], BF16, "sbf%d" % i) for i in range(2)]
    sbi = [0]

    def st_bf(dst, srcv, n):
        b = sbf[sbi[0] % 2]
        sbi[0] += 1
        k.cp("act", b.v[:, 0:n], srcv)
        k.stv("sp", dst, b.v[:, 0:n])
    tb = k.sb(st, [128, 512], F32, "tb")
    tbb = k.sb(st, [128, 512], BF16, "tbb")
    thb = k.sb(st, [64, NT], BF16, "thb")
    alb = k.sb(st, [64, NT], BF16, "alb")
    sgb = k.sb(st, [128, NT], BF16, "sgb")
    bs4 = k.sb(st, [128, 4, 4], F32, "bs4")
    pcx = k.sb(st, [128, NT // CH], F32, "pcx")
    pcx2 = k.sb(st, [128, NT // CH], F32, "pcx2")
    F2 = [k.sb(st, [128, NT], F32, "g%d" % i) for i in range(9)]
    PS = [k.ps(st, [128, 512], F32, "pp%d" % i) for i in range(8)]
    pi = [0]

    def nps():
        p = PS[pi[0] % 8]
        pi[0] += 1
        return p

    def chain(p, wlist, M, n, col0=1):
        tot = len(wlist) * 8
        i = 0
        for (wb, c0, sh) in wlist:
            for c in range(8):
                k.mm(p.v[0:M, 0:n], wb.v[:, c, c0:c0 + M], hTt.v[:, c, col0 + sh:col0 + sh + n], i == 0, i == tot - 1)
                i += 1

    def chain_tok(p, wlist, ncols, s):
        tot = len(wlist) * 8
        i = 0
        for (wb, c0, sh) in wlist:
            for c in range(8):
                k.mm(p.v[:, 0:ncols], hTt.v[:, c, 1 + sh + s * 128:1 + sh + (s + 1) * 128], wb.v[:, c, c0:c0 + ncols],
                     i == 0, i == tot - 1)
                i += 1

    def rope_out(P, Psw, ctab, stab, dst, n):
        k.tt("dve", F[10].v[:, 0:n], P.v[:, 0:n], ctab, ALU.mult)
        k.tt("dve", F[11].v[:, 0:n], Psw.v[:, 0:n], stab, ALU.mult)
        k.tt("dve", dst, F[10].v[:, 0:n], F[11].v[:, 0:n], ALU.add)

    segs = [(0, T, True), (T, CTX, False)]
    for (seg0, seglen, latent) in segs:
        for t0 in range(seg0, seg0 + seglen, NT):
            n = min(NT, seg0 + seglen - t0)
            ns = n // 128
            nch = n // CH
            lo = t0 - 1 if t0 > seg0 else t0
            hi = t0 + n + 1 if t0 + n < seg0 + seglen else t0 + n
            if lo == t0:
                k.memset("dve", hTt.v[:, :, 0:1], 0.0)
            if hi == t0 + n:
                k.memset("dve", hTt.v[:, :, n + 1:n + 2], 0.0)
            for c in range(8):
                k.ld("pool", hTt.v[:, c, 1 - (t0 - lo):1 + n + (hi - t0 - n)], hTd[c * 128:(c + 1) * 128, lo:hi])
            if latent:
                k.ld("pool", ca.v[:, 0:n], C["ropeA_c"][:, t0:t0 + n])
                k.ld("pool", sa.v[:, 0:n], C["ropeA_s"][:, t0:t0 + n])
                k.ld("pool", cs_.v[:, 0:n], C["ropeS_c"][:, t0:t0 + n])
                k.ld("pool", ss_.v[:, 0:n], C["ropeS_s"][:, t0:t0 + n])
                for qk in range(2):
                    k.ts("dve", gca.v[:, qk, 0:n], ca.v[:, 0:n], pc.v[:, 2 * qk:2 * qk + 1], None, ALU.mult)
                    k.ts("dve", gsa.v[:, qk, 0:n], sa.v[:, 0:n], pc.v[:, 2 * qk + 1:2 * qk + 2], None, ALU.mult)
            for grp, base, swb, dq, dk_, dv_ in (("A", 0, 0, S["QA"], S["KA"], S["VA"]),
                                                ("B", 512, 384, S["QB"], S["KB"], S["VB"])):
                for j in range(3):
                    P, Psw = nps(), nps()
                    chain(P, [(wabc, base + j * 128, 0)], 128, n)
                    if latent:
                        chain(Psw, [(wsw, swb + j * 128, 0)], 128, n)
                    qk = 0 if j < 2 else 1
                    if grp == "B":
                        k.act(F[0].v[:, 0:n], P.v[:, 0:n], ACT.Square)
                        pss = nps()
                        k.mm(pss.v[:, 0:n], bd.v[:, :], F[0].v[:, 0:n])
                        k.ts("dve", F[1].v[:, 0:n], pss.v[:, 0:n], 1.0 / 64, RMS_EPS, ALU.mult, ALU.add)
                        k.act(F[1].v[:, 0:n], F[1].v[:, 0:n], ACT.Sqrt)
                        k.recip(F[1].v[:, 0:n], F[1].v[:, 0:n])
                        k.tt("dve", F[2].v[:, 0:n], P.v[:, 0:n], F[1].v[:, 0:n], ALU.mult)
                        if latent:
                            k.tt("dve", F[3].v[:, 0:n], Psw.v[:, 0:n], F[1].v[:, 0:n], ALU.mult)
                            rope_out(F[2], F[3], gca.v[:, qk, 0:n], gsa.v[:, qk, 0:n], ob.v[:, 0:n], n)
                        else:
                            k.ts("dve", ob.v[:, 0:n], F[2].v[:, 0:n], pc.v[:, 2 * qk:2 * qk + 1], None, ALU.mult)
                    else:
                        if latent:
                            rope_out(P, Psw, ca.v[:, 0:n], sa.v[:, 0:n], ob.v[:, 0:n], n)
                        else:
                            k.cp("act", ob.v[:, 0:n], P.v[:, 0:n])
                    dst = dq[j * 128:(j + 1) * 128, t0:t0 + n] if j < 2 else dk_[:, t0:t0 + n]
                    k.stv("sp", dst, ob.v[:, 0:n])
                for s in range(ns):
                    P = nps()
                    chain_tok(P, [(wabc, base + 384, 0)], 128, s)
                    k.cp("act", tbb.v[:, 0:128], P.v[:, 0:128])
                    k.stv("sp", dv_[t0 + s * 128:t0 + (s + 1) * 128, :], tbb.v[:, 0:128])
            for j in range(4):
                P, Psw = nps(), nps()
                chain(P, [(wabc, 1024 + j * 128, 0)], 128, n)
                g = j % 2
                if latent:
                    chain(Psw, [(wsw, 768 + j * 128, 0)], 128, n)
                    rope_out(P, Psw, cs_.v[:, 0:n], ss_.v[:, 0:n], F[0].v[:, 0:n], n)
                else:
                    k.cp("act", F[0].v[:, 0:n], P.v[:, 0:n])
                x3 = F[0].v[:, 0:n].rr("p (c i) -> p c i", i=CH)
                for d in range(2):
                    if j < 2:
                        k.tt("dve", F[1].v[:, 0:n].rr("p (c i) -> p c i", i=CH), x3,
                             rtab.v[:, g, d, 0:1, :].bc([128, nch, CH]), ALU.mult)
                        st_bf(S["C_Rt"][d, g * 128:(g + 1) * 128, t0:t0 + n], F[1].v[:, 0:n], n)
                    else:
                        k.tt("dve", F[1].v[:, 0:n].rr("p (c i) -> p c i", i=CH), x3,
                             rtab.v[:, g, d, 1:2, :].bc([128, nch, CH]), ALU.mult)
                        st_bf(S["C_Kt"][d, g * 128:(g + 1) * 128, t0:t0 + n], F[1].v[:, 0:n], n)
                        k.tt("dve", F[2].v[:, 0:n].rr("p (c i) -> p c i", i=CH), x3,
                             rtab.v[:, g, d, 2:3, :].bc([128, nch, CH]), ALU.mult)
                        for s in range(ns):
                            pt = nps()
                            k.tr(pt.v[:, 0:128], F[2].v[:, s * 128:(s + 1) * 128], identf.v[:, :])
                            k.cp("act", tbb.v[:, 0:128], pt.v[:, 0:128])
                            k.stv("sp", S["C_Kh"][d, t0 + s * 128:t0 + (s + 1) * 128, g * 128:(g + 1) * 128],
                                  tbb.v[:, 0:128])
            for s in range(ns):
                P = nps()
                chain_tok(P, [(wabc, 1536, 0)], 256, s)
                k.cp("act", tbb.v[:, 0:256], P.v[:, 0:256])
                k.stv("sp", S["C_V"][t0 + s * 128:t0 + (s + 1) * 128, :], tbb.v[:, 0:256])
                P = nps()
                chain_tok(P, [(wabc, 1792, 0)], 512, s)
                k.act(tb.v[:, 0:512], P.v[:, 0:512], ACT.Silu)
                k.stv("sp", S["C_G"][t0 + s * 128:t0 + (s + 1) * 128, :], tb.v[:, 0:512])
            P = nps()
            chain(P, [(wgd, 0, 0)], 128, n)
            k.act(sgb.v[:, 0:n], P.v[:, 0:n], ACT.Sigmoid)
            for s in range(ns):
                P = nps()
                k.mm(P.v[:, 0:256], sgb.v[:, s * 128:(s + 1) * 128], g2s.v[:, :])
                k.cp("act", tb.v[:, 0:256], P.v[:, 0:256])
                k.stv("sp", S["D_G"][t0 + s * 128:t0 + (s + 1) * 128, :], tb.v[:, 0:256])
            for d in range(2):
                sh = -1 if d == 0 else 1
                wl = lambda c0: [(wdb[d], c0, 0), (wda[d], c0, sh)]
                P = nps()
                chain(P, wl(768), 64, n)
                k.act(thb.v[:, 0:n], P.v[0:64, 0:n], ACT.Tanh)
                P = nps()
                chain(P, wl(832), 64, n)
                k.cp("act", alb.v[:, 0:n], P.v[0:64, 0:n])
                for s in range(ns):
                    P = nps()
                    chain_tok(P, wl(512), 256, s)
                    k.cp("act", tbb.v[:, 0:256], P.v[:, 0:256])
                    k.stv("sp", S["D_V"][d, t0 + s * 128:t0 + (s + 1) * 128, :], tbb.v[:, 0:256])
                def dgroup(g, FB, pcx):
                    r_, k_, lw, cs, a_, kk, e_, t1, t2 = FB
                    gc = slice(g, g + 1)
                    P = nps()
                    chain(P, wl(g * 128), 128, n)
                    k.cp("act", r_.v[:, 0:n], P.v[:, 0:n])
                    P = nps()
                    chain(P, wl(256 + g * 128), 128, n)
                    k.cp("act", k_.v[:, 0:n], P.v[:, 0:n])
                    yield
                    P = nps()
                    k.mm(P.v[:, 0:n], w2s.v[:, d, g * 128:(g + 1) * 128], thb.v[:, 0:n])
                    k.act(lw.v[:, 0:n], P.v[:, 0:n], ACT.Sigmoid, bias=pc.v[:, 4 + 2 * d + g:5 + 2 * d + g])
                    k.ts("dve", lw.v[:, 0:n], lw.v[:, 0:n], -DECAY_SCALE, None, ALU.mult)
                    yield
                    P = nps()
                    k.mm(P.v[:, 0:n], a2s.v[:, d, g * 128:(g + 1) * 128], alb.v[:, 0:n])
                    k.act(a_.v[:, 0:n], P.v[:, 0:n], ACT.Sigmoid, bias=pc.v[:, 8 + 2 * d + g:9 + 2 * d + g])
                    yield
                    k.ts("dve", kk.v[:, 0:n], k_.v[:, 0:n], pc.v[:, 12 + g:13 + g], None, ALU.mult)
                    k.act(t1.v[:, 0:n], kk.v[:, 0:n], ACT.Square)
                    yield
                    P = nps()
                    k.mm(P.v[:, 0:n], bd.v[:, :], t1.v[:, 0:n])
                    k.act(t1.v[:, 0:n], P.v[:, 0:n], ACT.Sqrt)
                    yield
                    k.ts("dve", t1.v[:, 0:n], t1.v[:, 0:n], 1e-12, None, ALU.max)
                    k.recip(t1.v[:, 0:n], t1.v[:, 0:n])
                    yield
                    k.tt("dve", kk.v[:, 0:n], kk.v[:, 0:n], t1.v[:, 0:n], ALU.mult)
                    yield
                    k.ts("dve", t1.v[:, 0:n], a_.v[:, 0:n], pc.v[:, 14 + g:15 + g], oma.v[:, gc], ALU.mult, ALU.add)
                    k.tt("dve", k_.v[:, 0:n], k_.v[:, 0:n], t1.v[:, 0:n], ALU.mult)
                    yield
                    k.stt("dve", t1.v[:, 0:n], r_.v[:, 0:n], pc.v[:, 16 + 2 * d + g:17 + 2 * d + g], k_.v[:, 0:n],
                          ALU.mult, ALU.mult)
                    for s in range(ns):
                        P = nps()
                        k.mm(P.v[:, 0:2], t1.v[:, s * 128:(s + 1) * 128], e2.v[:, :])
                        k.cp("act", bs4.v[:, s, 2 * g:2 * g + 2], P.v[:, 0:2])
                    k.tt("dve", a_.v[:, 0:n], a_.v[:, 0:n], kk.v[:, 0:n], ALU.mult)
                    yield
                    src, dst = lw, cs
                    k.cp("act", t2.v[:, 0:n], lw.v[:, 0:n])
                    src = t2
                    for stp in (1, 2, 4, 8, 16, 32):
                        s3 = src.v[:, 0:n].rr("p (c i) -> p c i", i=CH)
                        d3 = dst.v[:, 0:n].rr("p (c i) -> p c i", i=CH)
                        if d == 0:
                            k.tt("dve", d3[:, :, stp:], s3[:, :, stp:], s3[:, :, :CH - stp], ALU.add)
                            k.cp("act", d3[:, :, :stp], s3[:, :, :stp])
                        else:
                            k.tt("dve", d3[:, :, :CH - stp], s3[:, :, :CH - stp], s3[:, :, stp:], ALU.add)
                            k.cp("act", d3[:, :, CH - stp:], s3[:, :, CH - stp:])
                        src, dst = dst, src
                        yield
                    cs = src
                    spare = dst
                    cs3 = cs.v[:, 0:n].rr("p (c i) -> p c i", i=CH)
                    tot3 = cs3[:, :, CH - 1:CH] if d == 0 else cs3[:, :, 0:1]
                    k.act(e_.v[:, 0:n], cs.v[:, 0:n], ACT.Exp)
                    yield
                    k.tt("dve", t1.v[:, 0:n], r_.v[:, 0:n], e_.v[:, 0:n], ALU.mult)
                    st_bf(S["D_Rt"][d, g * 128:(g + 1) * 128, t0:t0 + n], t1.v[:, 0:n], n)
                    yield
                    k.tt("dve", e_.v[:, 0:n], cs.v[:, 0:n], lw.v[:, 0:n], ALU.subtract)
                    k.act(e_.v[:, 0:n], e_.v[:, 0:n], ACT.Exp)
                    yield
                    k.stt("dve", t1.v[:, 0:n], kk.v[:, 0:n], -1.0, e_.v[:, 0:n], ALU.mult, ALU.mult)
                    st_bf(S["D_At"][d, g * 128:(g + 1) * 128, t0:t0 + n], t1.v[:, 0:n], n)
                    yield
                    for s in range(ns):
                        pt = nps()
                        k.tr(pt.v[:, 0:128], t1.v[:, s * 128:(s + 1) * 128], identf.v[:, :])
                        k.cp("act", tbb.v[:, 0:128], pt.v[:, 0:128])
                        k.stv("sp", S["D_Atok"][d, t0 + s * 128:t0 + (s + 1) * 128, g * 128:(g + 1) * 128],
                              tbb.v[:, 0:128])
                    k.act(e_.v[:, 0:n], cs.v[:, 0:n], ACT.Exp, scale=-1.0)
                    k.tt("dve", t1.v[:, 0:n], a_.v[:, 0:n], e_.v[:, 0:n], ALU.mult)
                    st_bf(S["D_Bt"][d, g * 128:(g + 1) * 128, t0:t0 + n], t1.v[:, 0:n], n)
                    yield
                    k.tt("dve", t1.v[:, 0:n], k_.v[:, 0:n], e_.v[:, 0:n], ALU.mult)
                    st_bf(S["D_Kt"][d, g * 128:(g + 1) * 128, t0:t0 + n], t1.v[:, 0:n], n)
                    yield
                    k.act(pcx.v[:, 0:nch], tot3.rr("p c o -> p (c o)"), ACT.Exp)
                    k.stv("sp", S["D_PC"][d, g * 128:(g + 1) * 128, t0 // CH:t0 // CH + nch], pcx.v[:, 0:nch])
                    k.tt("dve", e_.v[:, 0:n].rr("p (c i) -> p c i", i=CH), tot3.bc([128, nch, CH]), cs3, ALU.subtract)
                    k.act(e_.v[:, 0:n], e_.v[:, 0:n], ACT.Exp)
                    yield
                    for (srcb, dstd) in ((a_, S["D_Bh"]), (k_, S["D_Kh"])):
                        k.tt("dve", t1.v[:, 0:n], srcb.v[:, 0:n], e_.v[:, 0:n], ALU.mult)
                        for s in range(ns):
                            pt = nps()
                            k.tr(pt.v[:, 0:128], t1.v[:, s * 128:(s + 1) * 128], identf.v[:, :])
                            k.cp("act", tbb.v[:, 0:128], pt.v[:, 0:128])
                            k.stv("sp", dstd[d, t0 + s * 128:t0 + (s + 1) * 128, g * 128:(g + 1) * 128], tbb.v[:, 0:128])
                gens = [dgroup(0, F[0:9], pcx), dgroup(1, F2, pcx2)]
                live = list(gens)
                while live:
                    for g_ in list(live):
                        try:
                            next(g_)
                        except StopIteration:
                            live.remove(g_)
                for s in range(ns):
                    k.stv("sp", S["D_BS"][d, t0 + s * 128:t0 + (s + 1) * 128, :], bs4.v[:, s, :])


def attnphase(k, st, cfg):
    S, C, Wt = cfg["S"], cfg["C"], cfg["W"]
    T, CTX = cfg["T"], cfg["CTX"]
    TT = T + CTX
    NKC = TT // 128
    need_ctx = cfg["need_ctx"]
    kT = k.sb(st, [128, TT], BF16, "kT")
    va = k.sb(st, [128, NKC, 65], BF16, "va")
    qT = [k.sb(st, [128, 512], BF16, "qT%d" % i) for i in range(2)]
    for c0 in range(0, TT, 2048):
        k.memset("dve", kT.v[64:128, c0:min(TT, c0 + 2048)], 0.0)
    for q_ in qT:
        k.memset("dve", q_.v[64:128, :], 0.0)
    pT = [k.sb(st, [128, 512], BF16, "pT%d" % i) for i in range(3)]
    ot = [k.sb(st, [128, 64], F32, "ot%d" % i) for i in range(2)]
    rc = k.sb(st, [128, 1], F32, "rc")
    esk = k.sb(st, [128, 4], F32, "esk")
    mprev = k.sb(st, [128, 128], BF16, "mprev")
    mnext = k.sb(st, [128, 128], BF16, "mnext")
    k.ld("pool", mprev.v[:, :], C["mprev"])
    k.ld("pool", mnext.v[:, :], C["mnext"])
    k.ld("sp", esk.v[:, :], Wt["sink"].partition_broadcast(128))
    k.act(esk.v[:, :], esk.v[:, :], ACT.Exp)
    ps_s = [k.ps(st, [128, 512], F32, "pss%d" % i) for i in range(3)]
    ps_o = [k.ps(st, [128, 512], F32, "pso%d" % i) for i in range(2)]
    ps_f = k.ps(st, [128, 512], F32, "psf")
    osb = k.sb(st, [128, 512], F32, "osb")
    identf = k.sb(st, [128, 128], F32, "identf")
    k.ld("sp", identf.v[:, :], C["ident"])
    cnt = dict(s=0, q=0, o=0, p=0, a=0)
    scale = 64 ** -0.5

    def load_kv(kd, vd, g):
        k.ld("sp", kT.v[0:64, :], kd[g * 64:(g + 1) * 64, :])
        k.memset("dve", va.v[:, :, 64:65], 1.0)
        k.ld("sp", va.v[:, :, 0:64], vd[:, g * 64:(g + 1) * 64].rearrange("(c p) j -> p c j", p=128))

    def finish(po, nq_sub, t0, col, sinkcol):
        nq = nq_sub * 128
        k.cp("act", osb.v[0:65, 0:nq], po.v[0:65, 0:nq])
        for s in range(nq_sub):
            k.tr(ps_f.v[:, s * 65:(s + 1) * 65], osb.v[0:65, s * 128:(s + 1) * 128], identf.v[0:65, 0:65])
        for s in range(nq_sub):
            o = ot[cnt["o"] % 2]
            cnt["o"] += 1
            den = ps_f.v[:, s * 65 + 64:s * 65 + 65]
            if sinkcol is None:
                k.recip(rc.v[:, :], den)
            else:
                k.tt("dve", rc.v[:, :], den, esk.v[:, sinkcol:sinkcol + 1], ALU.add)
                k.recip(rc.v[:, :], rc.v[:, :])
            k.ts("dve", o.v[:, :], ps_f.v[:, s * 65:s * 65 + 64], rc.v[:, 0:1], None, ALU.mult)
            k.stv("sp", S["O"][t0 + s * 128:t0 + (s + 1) * 128, col:col + 64], o.v[:, :])

    def qk(q, nq, kc):
        ps = ps_s[cnt["s"] % 3]
        cnt["s"] += 1
        k.mm(ps.v[:, 0:nq], kT.v[:, kc * 128:(kc + 1) * 128], q.v[:, 0:nq])
        return ps

    def prologue(b):
        if b.get("kv") is not None:
            load_kv(*b["kv"])
        q = qT[cnt["q"] % 2]
        cnt["q"] += 1
        k.ld("sp", q.v[0:64, 0:b["nq"]], b["qd"][b["h"] * 64:(b["h"] + 1) * 64, b["t0"]:b["t0"] + b["nq"]])
        return q, qk(q, b["nq"], b["chunks"][0][0])

    def body(b, q, nxt):
        nq, chunks = b["nq"], b["chunks"]
        po = b["po"]
        assert len(chunks[0][1]) == nq // 128
        for i, (kc, subs) in enumerate(chunks):
            ps = nxt
            if i + 1 < len(chunks):
                nxt = qk(q, nq, chunks[i + 1][0])
            p = pT[cnt["p"] % 3]
            cnt["p"] += 1
            k.act(p.v[:, 0:nq], ps.v[:, 0:nq], ACT.Exp, scale=scale)
            for (s, m_) in subs:
                if m_ is not None:
                    k.tt("dve", p.v[:, s * 128:(s + 1) * 128], p.v[:, s * 128:(s + 1) * 128], m_.v[:, :], ALU.mult)
            c_lo = min(s for (s, _) in subs) * 128
            c_hi = (max(s for (s, _) in subs) + 1) * 128
            k.mm(po.v[0:65, c_lo:c_hi], va.v[:, kc, :], p.v[:, c_lo:c_hi], i == 0, i == len(chunks) - 1)

    blocks = []

    def block(qd, h, t0, nq, chunks, col, sinkcol, kv=None):
        blocks.append(dict(qd=qd, h=h, t0=t0, nq=nq, chunks=chunks, col=col, sinkcol=sinkcol, kv=kv,
                           po=ps_o[len(blocks) % 2]))

    for g in range(2):
        kv = (S["KB"], S["VB"], g)
        for h in (2 * g, 2 * g + 1):
            for t0 in range(0, T, 512):
                nq = min(512, T - t0)
                block(S["QB"], h, t0, nq, [(kc, [(s, None) for s in range(nq // 128)]) for kc in range(NKC)],
                      256 + h * 64, None, kv)
                kv = None
            if need_ctx:
                block(S["QB"], h, T, CTX, [(kc, [(s, None) for s in range(CTX // 128)]) for kc in range(T // 128, NKC)],
                      256 + h * 64, None)
    nb = T // 128
    cchunks = list(range(T // 128, NKC))
    for g in range(2):
        kv = (S["KA"], S["VA"], g)
        for h in (2 * g, 2 * g + 1):
            for b0 in range(0, nb, 4):
                nsb = min(4, nb - b0)
                chunks = [(kc, [(s, None) for s in range(nsb)]) for kc in cchunks]
                for kc in range(max(0, b0 - 1), min(nb, b0 + nsb + 1)):
                    subs = []
                    for s in range(nsb):
                        dlt = kc - (b0 + s)
                        if dlt == -1:
                            subs.append((s, mprev))
                        elif dlt == 0:
                            subs.append((s, None))
                        elif dlt == 1:
                            subs.append((s, mnext))
                    chunks.append((kc, subs))
                block(S["QA"], h, b0 * 128, nsb * 128, chunks, h * 64, h, kv)
                kv = None
            if need_ctx:
                block(S["QA"], h, T, CTX, [(kc, [(s, None) for s in range(CTX // 128)]) for kc in cchunks], h * 64, h)
    st_ = prologue(blocks[0])
    for i, b in enumerate(blocks):
        body(b, *st_)
        if i + 1 < len(blocks):
            st_ = prologue(blocks[i + 1])
        finish(b["po"], b["nq"] // 128, b["t0"], b["col"], b["sinkcol"])


def scanphase(k, st, cfg):
    S, C = cfg["S"], cfg["C"]
    T, CTX = cfg["T"], cfg["CTX"]
    TT = T + CTX
    dplr = cfg["dplr"]
    pre = "D_" if dplr else "C_"
    NCG = 2
    GT_ = NCG * CH
    NCHT = TT // CH
    f3 = lambda b: b.v[:, :, :]
    t64 = lambda nm: k.sb(st, [64, 8, 64], F32, nm)
    b64 = lambda nm: k.sb(st, [64, 8, 64], BF16, nm)
    chm = lambda nm: k.sb(st, [64, 8, GT_], BF16, nm)
    tkm = lambda nm: k.sb(st, [64, NCG, 8, 64], BF16, nm)
    RT = [chm("RT%d" % i) for i in range(2)]
    KT = [chm("KT%d" % i) for i in range(2)]
    KH = [tkm("KH%d" % i) for i in range(2)]
    VV = [tkm("VV%d" % i) for i in range(2)]
    Min = t64("Min")
    k.ld("sp", f3(Min), C["m_in"].rearrange("p (q t) -> p q t", q=8))
    id8 = t64("id8")
    k.ld("sp", f3(id8), C["id8"].rearrange("p (q t) -> p q t", q=8))
    if dplr:
        AT = [chm("AT%d" % i) for i in range(2)]
        BT = [chm("BT%d" % i) for i in range(2)]
        BH = [tkm("BH%d" % i) for i in range(2)]
        AK = [tkm("AK%d" % i) for i in range(2)]
        Mst, Mts = t64("Mst"), t64("Mts")
        k.ld("sp", f3(Mst), C["m_st"].rearrange("p (q t) -> p q t", q=8))
        k.ld("sp", f3(Mts), C["m_ts"].rearrange("p (q t) -> p q t", q=8))
        PCt = k.sb(st, [64, 8, NCHT], F32, "PCt")
        for d in range(2):
            k.ld("sp", PCt.v[:, d * 4:(d + 1) * 4, :], S["D_PC"][d].rearrange("(h j) c -> j h c", j=64))
    else:
        PCr = k.sb(st, [64, 8], F32, "PCr")
        k.ld("sp", PCr.v[:, :], C["ret_pc"])
        GTc = b64("GTc")
        k.tt("dve", f3(GTc), f3(id8), PCr.v[:, :].rr("p (q o) -> p q o", o=1).bc([64, 8, 64]), ALU.mult)
    sets = []
    for c in range(NCG):
        B = dict(ArkT=b64("ArkT%d" % c), Yv=t64("Yv%d" % c), H=t64("H%d" % c))
        if dplr:
            B.update(Ls=[b64("L%d_%d" % (i, c)) for i in range(5)], Ns=[b64("N%d_%d" % (i, c)) for i in range(6)],
                     AakT=b64("AakT%d" % c), ArbT=b64("ArbT%d" % c), Qe=b64("Qe%d" % c), GTt=b64("GTt%d" % c),
                     X=[k.sb(st, [64, 8, 128], F32, "X%d_%d" % (i, c)) for i in range(2)],
                     Xb=k.sb(st, [64, 8, 128], BF16, "Xb_%d" % c),
                     psX=k.ps(st, [64, 1024], F32, "psX%d" % c))
        sets.append(B)
    ST = [b64("ST%d" % i) for i in range(2)]
    Yt = [t64("Yt%d" % i) for i in range(2)]
    psA = [k.ps(st, [64, 512], F32, "psA%d" % i) for i in range(2)]
    psY = [k.ps(st, [64, 512], F32, "psY%d" % i) for i in range(2)]
    cnt = dict(a=0, y=0, e=0)
    p3 = lambda p: p.v[:, :].rr("p (q t) -> p q t", q=8)

    def npa():
        p = psA[cnt["a"] % 2]
        cnt["a"] += 1
        return p

    def ev():
        cnt["e"] += 1
        return "act" if cnt["e"] % 2 else "dve"

    def prod(dst, lhs, rhs, mask):
        p = npa()
        for q in range(8):
            k.mm(p3(p)[:, q, :], lhs(q), rhs(q))
        if mask is None:
            k.cp(ev(), f3(dst), p3(p))
        else:
            k.tt("dve", f3(dst), p3(p), f3(mask), ALU.mult)

    def views(rb, cl):
        sl = {d: slice(cl[d] * CH, (cl[d] + 1) * CH) for d in range(2)}
        dq = lambda q: q // 4
        chs = lambda buf: (lambda q: buf[rb].v[:, q, sl[dq(q)]])
        tks = lambda buf: (lambda q: buf[rb].v[:, cl[dq(q)], q, :])
        return sl, chs, tks

    def ppart(B, rb, cl, grp):
        sl, chs, tks = views(rb, cl)
        rt, kt, kh, vv = chs(RT), chs(KT), tks(KH), tks(VV)
        prod(B["ArkT"], kt, rt, Min)
        yield
        if dplr:
            at, bt, bh = chs(AT), chs(BT), tks(BH)
            Ls, Ns, AakT, ArbT = B["Ls"], B["Ns"], B["AakT"], B["ArbT"]
            prod(Ns[0], bt, at, Mst)
            prod(Ls[0], at, bt, Mts)
            yield
            prod(AakT, kt, at, Mst)
            prod(ArbT, bt, rt, Min)
            yield
            xc, xn = B["X"]
            xb = B["Xb"]
            for d in range(2):
                k.cp("act", xc.v[:, d * 4:(d + 1) * 4, 0:64], AK[rb].v[:, cl[d], d * 4:(d + 1) * 4, :])
            p = npa()
            for q in range(8):
                k.mm(p3(p)[:, q, :], AakT.v[:, q, :], vv(q))
            k.cp("dve", xc.v[:, :, 64:128], p3(p))
            k.cp("act", f3(xb), f3(xc))
            yield
            px = B["psX"].v[:, :].rr("p (q c) -> p q c", q=8)
            for lv in range(6):
                for q in range(8):
                    k.mm(px[:, q, :], Ns[lv].v[:, q, :], xb.v[:, q, :])
                if lv < 5:
                    if lv < 4:
                        prod(Ls[lv + 1], lambda q: Ns[lv].v[:, q, :], lambda q: Ls[lv].v[:, q, :], None)
                    prod(Ns[lv + 1], lambda q: Ls[lv].v[:, q, :], lambda q: Ns[lv].v[:, q, :], None)
                k.tt("dve", f3(xn), f3(xc), px, ALU.add)
                xc, xn = xn, xc
                k.cp("act", f3(xb), f3(xc))
                yield
            wq = lambda q: xb.v[:, q, 0:64]
            uv = lambda q: xb.v[:, q, 64:128]
            p = npa()
            for q in range(8):
                k.mm(p3(p)[:, q, :], wq(q), ArbT.v[:, q, :])
            for d in range(2):
                k.tt("dve", B["Qe"].v[:, d * 4:(d + 1) * 4, :], p3(p)[:, d * 4:(d + 1) * 4, :],
                     RT[rb].v[:, d * 4:(d + 1) * 4, sl[d]], ALU.add)
            p = npa()
            for q in range(8):
                k.mm(p3(p)[:, q, :], wq(q), bh(q))
            for d in range(2):
                c_abs = grp[d] * NCG + cl[d]
                k.tt("dve", B["GTt"].v[:, d * 4:(d + 1) * 4, :], id8.v[:, d * 4:(d + 1) * 4, :],
                     PCt.v[:, d * 4:(d + 1) * 4, c_abs:c_abs + 1].bc([64, 4, 64]), ALU.mult)
            k.tt("dve", f3(B["GTt"]), f3(B["GTt"]), p3(p), ALU.add)
            yield
        p = npa()
        for q in range(8):
            k.mm(p3(p)[:, q, :], B["ArkT"].v[:, q, :], vv(q), True, not dplr)
            if dplr:
                k.mm(p3(p)[:, q, :], ArbT.v[:, q, :], uv(q), False, True)
        k.cp(ev(), f3(B["Yv"]), p3(p))
        p = npa()
        for q in range(8):
            k.mm(p3(p)[:, q, :], kh(q), vv(q), True, not dplr)
            if dplr:
                k.mm(p3(p)[:, q, :], bh(q), uv(q), False, True)
        k.cp(ev(), f3(B["H"]), p3(p))
        yield

    k.memset("dve", f3(ST[0]), 0.0)
    ngl = T // GT_
    ncg_ctx = CTX // GT_
    fw = list(range(ngl, ngl + ncg_ctx)) + list(range(ngl))
    bw = list(range(ngl + ncg_ctx - 1, ngl - 1, -1)) + list(range(ngl - 1, -1, -1))
    order = {0: fw, 1: bw}
    step = 0

    def emit_loads(gi):
        rb = gi % 2
        grp = {d: order[d][gi] for d in range(2)}
        for d in range(2):
            t0 = grp[d] * GT_
            qs = slice(d * 4, (d + 1) * 4)
            chv = lambda nm: S[pre + nm][d, :, t0:t0 + GT_].rearrange("(h j) t -> j h t", j=64)
            tkv = lambda ap: ap[t0:t0 + GT_, :].rearrange("(n t) (h j) -> t n h j", t=CH, j=64)
            k.ld("sp", RT[rb].v[:, qs, :], chv("Rt"))
            k.ld("sp", KT[rb].v[:, qs, :], chv("Kt"))
            k.ld("sp", KH[rb].v[:, :, qs, :], tkv(S[pre + "Kh"][d]))
            k.ld("sp", VV[rb].v[:, :, qs, :], tkv(S["D_V"][d] if dplr else S["C_V"]))
            if dplr:
                k.ld("sp", AT[rb].v[:, qs, :], chv("At"))
                k.ld("sp", BT[rb].v[:, qs, :], chv("Bt"))
                k.ld("sp", BH[rb].v[:, :, qs, :], tkv(S["D_Bh"][d]))
                k.ld("sp", AK[rb].v[:, :, qs, :], tkv(S["D_Atok"][d]))

    emit_loads(0)
    for gi in range(ngl + ncg_ctx):
        rb = gi % 2
        grp = {d: order[d][gi] for d in range(2)}
        if gi + 1 < ngl + ncg_ctx:
            emit_loads(gi + 1)
        cls = [{0: ci, 1: NCG - 1 - ci} for ci in range(NCG)]
        gens = [ppart(sets[ci], rb, cls[ci], grp) for ci in range(NCG)]
        live = list(gens)
        while live:
            for g_ in list(live):
                try:
                    next(g_)
                except StopIteration:
                    live.remove(g_)
        for ci in range(NCG):
            B, cl = sets[ci], cls[ci]
            sl, chs, tks = views(rb, cl)
            Sc, Sn = ST[step % 2], ST[(step + 1) % 2]
            qe = (lambda q: B["Qe"].v[:, q, :]) if dplr else chs(RT)
            gt = (lambda q: B["GTt"].v[:, q, :]) if dplr else (lambda q: GTc.v[:, q, :])
            py = psY[cnt["y"] % 2]
            cnt["y"] += 1
            for q in range(8):
                k.mm(p3(py)[:, q, :], qe(q), Sc.v[:, q, :])
            p = npa()
            for q in range(8):
                k.mm(p3(p)[:, q, :], gt(q), Sc.v[:, q, :])
            k.tt("dve", f3(Sn), p3(p), f3(B["H"]), ALU.add)
            yt = Yt[step % 2]
            k.tt("dve", f3(yt), p3(py), f3(B["Yv"]), ALU.add)
            for d in range(2):
                tt0 = grp[d] * GT_ + cl[d] * CH
                k.stv("sp", S[pre + "Y"][d, tt0:tt0 + CH, :].rearrange("t (h i) -> t h i", i=64),
                      yt.v[:, d * 4:(d + 1) * 4, :])
            step += 1


GN_EPS = 64e-5


def postphase(k, st, cfg):
    S, Wt = cfg["S"], cfg["W"]
    TT = cfg["T"] + cfg["CTX"]
    row = lambda nm, ap: (lambda t: (k.ld("sp", t.v[:, :], ap.partition_broadcast(128)), t)[1])(k.sb(st, [128, 256], F32, nm))
    retg = row("retg", Wt["ret_g"])
    lng = row("lng", Wt["ln_g"])
    lnb = row("lnb", Wt["ln_b"])
    vdf = k.sb(st, [128, 4, 64], F32, "vdf")
    LD = []
    for sl_ in range(2):
        LD.append(dict(gc=k.sb(st, [128, 512], F32, "gc%d" % sl_), gd=k.sb(st, [128, 256], F32, "gd%d" % sl_),
                       cy=[k.sb(st, [128, 4, 64], F32, "cy%d_%d" % (sl_, d)) for d in range(2)],
                       dy=[k.sb(st, [128, 4, 64], F32, "dy%d_%d" % (sl_, d)) for d in range(2)],
                       vd=[k.sb(st, [128, 4, 64], BF16, "vd%d_%d" % (sl_, d)) for d in range(2)],
                       bs=[k.sb(st, [128, 4], F32, "bs%d_%d" % (sl_, d)) for d in range(2)]))

    def emit_loads(ti):
        Lc = LD[ti % 2]
        tsl = slice(ti * 128, (ti + 1) * 128)
        k.ld("sp", Lc["gc"].v[:, :], S["C_G"][tsl, :])
        k.ld("sp", Lc["gd"].v[:, :], S["D_G"][tsl, :])
        for d in range(2):
            k.ld("sp", Lc["cy"][d].v[:, :, :], S["C_Y"][d, tsl, :].rearrange("t (h i) -> t h i", i=64))
            k.ld("sp", Lc["dy"][d].v[:, :, :], S["D_Y"][d, tsl, :].rearrange("t (h i) -> t h i", i=64))
            k.ld("sp", Lc["vd"][d].v[:, :, :], S["D_V"][d, tsl, :].rearrange("t (h i) -> t h i", i=64))
            k.ld("sp", Lc["bs"][d].v[:, :], S["D_BS"][d, tsl, :])

    def bufset(nm):
        return dict(y=[k.sb(st, [128, 4, 64], F32, nm + "y%d" % i) for i in range(2)],
                    yc=k.sb(st, [128, 4, 64], F32, nm + "yc"), sq=k.sb(st, [128, 4, 64], F32, nm + "sq"),
                    m4=k.sb(st, [128, 4], F32, nm + "m4"), v4=k.sb(st, [128, 4], F32, nm + "v4"),
                    acc=k.sb(st, [128, 256], F32, nm + "acc"), yn=k.sb(st, [128, 256], F32, nm + "yn"), i=0)

    BC, BD = bufset("c"), bufset("d")

    def head_norm(B, yb, g, b, dst):
        yc, sq, m4, v4 = B["yc"], B["sq"], B["m4"], B["v4"]
        k.op("dve", lambda e: e.tensor_reduce(out=m4[:, :], in_=yb[:, :, :], axis=AX.X, op=ALU.add), r=[yb], w=[m4])
        k.ts("dve", m4.v[:, :], m4.v[:, :], 1.0 / 64, None, ALU.mult)
        yield
        k.tt("dve", yc.v[:, :, :], yb.v[:, :, :], m4.v[:, :].rr("p (h o) -> p h o", o=1).bc([128, 4, 64]), ALU.subtract)
        yield
        k.tt("dve", sq.v[:, :, :], yc.v[:, :, :], yc.v[:, :, :], ALU.mult)
        yield
        k.op("dve", lambda e: e.tensor_reduce(out=v4[:, :], in_=sq[:, :, :], axis=AX.X, op=ALU.add), r=[sq], w=[v4])
        yield
        k.ts("dve", v4.v[:, :], v4.v[:, :], 1.0 / 64, GN_EPS, ALU.mult, ALU.add)
        yield
        k.act(v4.v[:, :], v4.v[:, :], ACT.Sqrt)
        yield
        k.recip(v4.v[:, :], v4.v[:, :])
        yield
        k.tt("dve", yc.v[:, :, :], yc.v[:, :, :], v4.v[:, :].rr("p (h o) -> p h o", o=1).bc([128, 4, 64]), ALU.mult)
        yield
        k.tt("dve", dst, yc.v[:, :, :].rr("p h i -> p (h i)"), g.v[:, :], ALU.mult)
        if b is not None:
            k.tt("dve", dst, dst, b.v[:, :], ALU.add)
        yield

    def part_c(tsl, Lc):
        B = BC
        acc, yn = B["acc"], B["yn"]
        gc = Lc["gc"]
        for d in range(2):
            yb = Lc["cy"][d]
            yield from head_norm(B, yb, retg, None, yn.v[:, :])
            if d == 0:
                k.tt("dve", acc.v[:, :], yn.v[:, :], gc.v[:, 0:256], ALU.mult)
            else:
                k.tt("dve", yn.v[:, :], yn.v[:, :], gc.v[:, 256:512], ALU.mult)
                k.tt("dve", acc.v[:, :], acc.v[:, :], yn.v[:, :], ALU.add)
            yield
        k.stv("sp", S["O"][tsl, 512:768], acc.v[:, :])

    def part_d(tsl, Lc):
        B = BD
        acc, yn = B["acc"], B["yn"]
        gd = Lc["gd"]
        for d in range(2):
            yb, vd, bs = Lc["dy"][d], Lc["vd"][d], Lc["bs"][d]
            yield from head_norm(B, yb, lng, lnb, yn.v[:, :])
            k.tt("dve", vdf.v[:, :, :], vd.v[:, :, :], bs.v[:, :].rr("p (h o) -> p h o", o=1).bc([128, 4, 64]), ALU.mult)
            k.tt("dve", yn.v[:, :], yn.v[:, :], vdf.v[:, :, :].rr("p h i -> p (h i)"), ALU.add)
            yield
            if d == 0:
                k.cp("act", acc.v[:, :], yn.v[:, :])
            else:
                k.tt("dve", acc.v[:, :], acc.v[:, :], yn.v[:, :], ALU.add)
            yield
        k.tt("dve", acc.v[:, :], acc.v[:, :], gd.v[:, :], ALU.mult)
        k.stv("sp", S["O"][tsl, 768:1024], acc.v[:, :])

    ntile = TT // 128
    emit_loads(0)
    for ti in range(ntile):
        t0 = ti * 128
        tsl = slice(t0, t0 + 128)
        if ti + 1 < ntile:
            emit_loads(ti + 1)
        live = [part_c(tsl, LD[ti % 2]), part_d(tsl, LD[ti % 2])]
        while live:
            for g_ in list(live):
                try:
                    next(g_)
                except StopIteration:
                    live.remove(g_)


def modphase(k, st, cfg):
    cv = k.sb(st, [128, 2, 8], F32, "cv")
    for r in range(2):
        k.ld("sp", cv.v[:, r, :], cfg["cvec"][r].rearrange("(c p) -> p c", p=128), allow_slow_non_contiguous=True)
    sc = k.sb(st, [128, 8, 2], F32, "sc")
    k.act(sc.v[:, :, :].rr("p c r -> p r c"), cv.v[:, :, :], ACT.Silu)
    wm = [k.sb(st, [128, 8, 512], F32, "wm%d" % i) for i in range(2)]
    bm = k.sb(st, [2, 9 * D], F32, "bm")
    k.ld("sp", bm.v[:, :], cfg["b_mod"].partition_broadcast(2))
    ob = k.sb(st, [2, 9 * D], F32, "ob")
    pm = [k.ps(st, [128, 512], F32, "pm%d" % i) for i in range(2)]
    wv = cfg["w_mod"].rearrange("(c p) f -> p c f", p=128)
    for j in range(18):
        w = wm[j % 2]
        for c in range(8):
            k.ld("sp", w.v[:, c, :], wv[:, c, j * 512:(j + 1) * 512])
        p = pm[j % 2]
        for c in range(8):
            k.mm(p.v[0:2, :], sc.v[:, c, :], w.v[:, c, :], c == 0, c == 7)
        k.tt("dve", ob.v[:, j * 512:(j + 1) * 512], p.v[0:2, :], bm.v[:, j * 512:(j + 1) * 512], ALU.add)
    k.stv("sp", cfg["mod"], ob.v[:, :])


def build_program(T, CTX, L, stop_after=None):
    TT = T + CTX
    nc = bass.Bass("TRN2", target_bir_lowering=False, dynamic_dma_scratch_size=8192)

    def din(name, shape, dt=F32):
        return nc.dram_tensor(name, list(shape), dt, kind="ExternalInput").ap()

    def scr(name, shape, dt=F32):
        return nc.dram_tensor(name, list(shape), dt, kind="Internal").ap()

    I = dict(
        xin=din("xin", [TT, D]), cvec=din("cvec", [2, D]),
        w_mod=din("w_mod", [L, D, 9 * D]), b_mod=din("b_mod", [L, 9 * D]), norm_g=din("norm_g", [L, 3, D]),
        ffn_w_in=din("ffn_w_in", [L, 2, D, 2 * DFF]), ffn_w_out=din("ffn_w_out", [L, 2, DFF, D]),
        w_in=din("w_in", [L, D, 3456]), w_out=din("w_out", [L, D, D]), final_g=din("final_g", [D]),
        w_sw=din("w_sw", [L, D, 1280]), w_d=din("w_d", [L, 2, D, 896]), pcols=din("pcols", [L, 128, 24]),
        mu=din("mu", [L, 2, 896]), w2=din("w2", [L, 2, 64, 256]), a2=din("a2", [L, 2, 64, 256]),
        g2=din("g2", [L, 128, 256]), ret_g=din("ret_g", [L, 256]), ln_g=din("ln_g", [L, 256]),
        ln_b=din("ln_b", [L, 256]), sink=din("sink", [L, 4]),
    )
    Cn = dict(
        ident=din("ident", [128, 128]), bd64=din("bd64", [128, 128]), e2=din("e2", [128, 2]),
        ropeA_c=din("ropeA_c", [128, T]), ropeA_s=din("ropeA_s", [128, T]),
        ropeS_c=din("ropeS_c", [128, T]), ropeS_s=din("ropeS_s", [128, T]),
        ret_tab=din("ret_tab", [128, 2, 2, 3, CH]), ret_pc=din("ret_pc", [64, 8]),
        m_st=din("m_st", [64, 512]), m_ts=din("m_ts", [64, 512]), m_in=din("m_in", [64, 512]),
        id8=din("id8", [64, 512]), mprev=din("mprev", [128, 128]), mnext=din("mnext", [128, 128]),
    )
    out = nc.dram_tensor("out", [T, D], F32, kind="ExternalOutput").ap()
    dbg = stop_after is not None
    mk = (lambda name, shape, dt=F32: nc.dram_tensor(name, list(shape), dt, kind="ExternalOutput").ap()) if dbg else scr
    S = dict(
        xres=mk("xres", [TT, D]), mod=mk("modr", [2, 9 * D]), hT=mk("hT", [D, TT], BF16),
        QA=mk("QA", [256, TT], BF16), KA=mk("KA", [128, TT], BF16), VA=mk("VA", [TT, 128], BF16),
        QB=mk("QB", [256, TT], BF16), KB=mk("KB", [128, TT], BF16), VB=mk("VB", [TT, 128], BF16),
        O=mk("O", [TT, 1024]),
        C_Rt=mk("C_Rt", [2, 256, TT], BF16), C_Kt=mk("C_Kt", [2, 256, TT], BF16), C_Kh=mk("C_Kh", [2, TT, 256], BF16),
        C_V=mk("C_V", [TT, 256], BF16), C_G=mk("C_G", [TT, 512]), C_Y=mk("C_Y", [2, TT, 256]),
        D_Rt=mk("D_Rt", [2, 256, TT], BF16), D_Kt=mk("D_Kt", [2, 256, TT], BF16), D_At=mk("D_At", [2, 256, TT], BF16),
        D_Bt=mk("D_Bt", [2, 256, TT], BF16), D_Kh=mk("D_Kh", [2, TT, 256], BF16), D_Bh=mk("D_Bh", [2, TT, 256], BF16),
        D_Atok=mk("D_Atok", [2, TT, 256], BF16), D_V=mk("D_V", [2, TT, 256], BF16), D_Y=mk("D_Y", [2, TT, 256]),
        D_PC=mk("D_PC", [2, 256, TT // CH]), D_G=mk("D_G", [TT, 256]), D_BS=mk("D_BS", [2, TT, 4]),
    )
    k = K(nc)
    stages = []

    def phase(name, fn, cfg):
        if stop_after is not None and stop_after in stages:
            return
        with contextlib.ExitStack() as st:
            fn(k, st, cfg)
            k.barrier()
        stages.append(name)
        k.marks.append((name, {e.name: e.n for e in k.E.values()}))

    with k.stack:
        segs_all = [(0, T, 0), (T, CTX, 1)]
        for l in range(L):
            last = l == L - 1
            W = dict(w_in=I["w_in"][l], w_sw=I["w_sw"][l], w_d=I["w_d"][l], mu=I["mu"][l], pcols=I["pcols"][l],
                     w2=I["w2"][l], a2=I["a2"][l], g2=I["g2"][l], ret_g=I["ret_g"][l], ln_g=I["ln_g"][l],
                     ln_b=I["ln_b"][l], sink=I["sink"][l])
            base = dict(S=S, C=Cn, W=W, T=T, CTX=CTX)
            phase("mod%d" % l, modphase, dict(cvec=I["cvec"], w_mod=I["w_mod"][l], b_mod=I["b_mod"][l], mod=S["mod"]))
            phase("f1_%d" % l, rowphase, dict(
                xin=I["xin"] if l == 0 else S["xres"], xout=S["xres"], segs=segs_all, mod=S["mod"], ident=Cn["ident"],
                ffn=dict(w_in=I["ffn_w_in"][l, 0], w_out=I["ffn_w_out"][l, 0], g=I["norm_g"][l, 0], shift=0, scale=1,
                         gate=2),
                post=dict(kind="hT", g=I["norm_g"][l, 1], shift=3, scale=4, hT=S["hT"])))
            phase("proj%d" % l, projphase, base)
            phase("attn%d" % l, attnphase, dict(base, need_ctx=not last))
            phase("scanC%d" % l, scanphase, dict(base, dplr=False))
            phase("scanD%d" % l, scanphase, dict(base, dplr=True))
            phase("post%d" % l, postphase, base)
            phase("f2_%d" % l, rowphase, dict(
                xin=S["xres"], xout=S["xres"], segs=segs_all if not last else [(0, T, 0)], mod=S["mod"],
                ident=Cn["ident"],
                pre=dict(o=S["O"], w_out=I["w_out"][l], gate=5),
                ffn=dict(w_in=I["ffn_w_in"][l, 1], w_out=I["ffn_w_out"][l, 1], g=I["norm_g"][l, 2], shift=6, scale=7,
                         gate=8),
                post=dict(kind="final", g=I["final_g"], out=out) if last else None))
    return nc, k


def _consts(T):
    c = {}
    c["ident"] = np.eye(128, dtype=np.float32)
    bd = np.zeros((128, 128), np.float32)
    bd[:64, :64] = 1
    bd[64:, 64:] = 1
    c["bd64"] = bd
    e2 = np.zeros((128, 2), np.float32)
    e2[:64, 0] = 1
    e2[64:, 1] = 1
    c["e2"] = e2
    t = np.arange(T)
    row = (t // 64).astype(np.float32)
    col = (t % 64).astype(np.float32)
    inv16 = (1.0 / (np.float32(10000.0) ** (np.arange(0, 32, 2, dtype=np.float32) / np.float32(32)))).astype(np.float32)
    inv32 = (1.0 / (np.float32(10000.0) ** (np.arange(0, 64, 2, dtype=np.float32) / np.float32(64)))).astype(np.float32)
    ca = np.zeros((64, T), np.float32)
    sa = np.zeros((64, T), np.float32)
    for blk, pos in ((0, row), (32, col)):
        ang = pos[None, :] * inv16[:, None]
        ca[blk:blk + 16] = np.cos(ang)
        ca[blk + 16:blk + 32] = np.cos(ang)
        sa[blk:blk + 16] = -np.sin(ang)
        sa[blk + 16:blk + 32] = np.sin(ang)
    ang = t.astype(np.float32)[None, :] * inv32[:, None]
    cs = np.concatenate([np.cos(ang), np.cos(ang)], 0).astype(np.float32)
    ss = np.concatenate([-np.sin(ang), np.sin(ang)], 0).astype(np.float32)
    c["ropeA_c"] = np.concatenate([ca, ca], 0)
    c["ropeA_s"] = np.concatenate([sa, sa], 0)
    c["ropeS_c"] = np.concatenate([cs, cs], 0)
    c["ropeS_s"] = np.concatenate([ss, ss], 0)
    lg = np.log1p(-np.exp2(-5.0 - np.arange(4, dtype=np.float64)))
    i = np.arange(CH, dtype=np.float64)
    rt = np.zeros((128, 2, 2, 3, CH), np.float64)
    for p in range(128):
        for g in range(2):
            l_ = lg[2 * g + p // 64]
            for d in range(2):
                csum = (i + 1) * l_ if d == 0 else (CH - i) * l_
                rt[p, g, d, 0] = np.exp(csum)
                rt[p, g, d, 1] = np.exp(-csum) / 8.0
                rt[p, g, d, 2] = np.exp(CH * l_ - csum) / 8.0
    c["ret_tab"] = rt.astype(np.float32)
    c["ret_pc"] = np.tile(np.exp(CH * lg)[None, :], (64, 2)).astype(np.float32)
    s_ = np.arange(64)[:, None]
    t_ = np.arange(64)[None, :]
    lt = (s_ < t_).astype(np.float32)
    gt = (s_ > t_).astype(np.float32)
    eye = np.eye(64, dtype=np.float32)
    c["m_st"] = np.concatenate([lt] * 4 + [gt] * 4, 1)
    c["m_ts"] = np.concatenate([gt] * 4 + [lt] * 4, 1)
    c["m_in"] = np.concatenate([lt + eye] * 4 + [gt + eye] * 4, 1)
    c["id8"] = np.concatenate([eye] * 8, 1)
    j = np.arange(128)[:, None]
    ii = np.arange(128)[None, :]
    c["mprev"] = (ii <= j).astype(np.float32)
    c["mnext"] = (j <= ii).astype(np.float32)
    return c


def _host_maps(inp, T, CTX, L):
    f = lambda a: np.ascontiguousarray(np.asarray(a, dtype=np.float32))
    w_in = f(inp["w_in"])
    pa = np.concatenate([np.arange(16, 32), np.arange(0, 16), np.arange(48, 64), np.arange(32, 48)])
    psq = np.concatenate([np.arange(32, 64), np.arange(0, 32)])
    cols = []
    for base, nh, perm in ((0, 4, pa), (256, 2, pa), (512, 4, pa), (768, 2, pa), (1024, 4, psq), (1280, 4, psq)):
        for h in range(nh):
            cols.append(base + h * 64 + perm)
    cols = np.concatenate(cols)
    w_sw = np.ascontiguousarray(w_in[:, :, cols])
    dcols = [np.concatenate([np.arange(2304, 3072), np.arange(3200, 3264), np.arange(3328, 3392)]),
             np.concatenate([np.arange(2304, 3072), np.arange(3264, 3328), np.arange(3392, 3456)])]
    w_d = np.ascontiguousarray(np.stack([w_in[:, :, dc] for dc in dcols], 1))
    qkg = f(inp["qk_norm_g"])
    pcols = np.zeros((L, 128, 24), np.float32)
    two = lambda v: np.ascontiguousarray(v.reshape(2, 128).T)
    for l in range(L):
        for qk in range(2):
            g = qkg[l, qk]
            pcols[l, :, 2 * qk] = np.tile(g, 2)
            pcols[l, :, 2 * qk + 1] = np.tile(g[pa], 2)
        for d in range(2):
            pcols[l, :, 4 + 2 * d:6 + 2 * d] = two(f(inp["rwkv_w0"])[l, d])
            pcols[l, :, 8 + 2 * d:10 + 2 * d] = two(f(inp["rwkv_a0"])[l, d])
            pcols[l, :, 16 + 2 * d:18 + 2 * d] = two(f(inp["rwkv_rho"])[l, d].reshape(256))
        pcols[l, :, 12:14] = two(f(inp["rwkv_k_k"])[l])
        pcols[l, :, 14:16] = two(f(inp["rwkv_k_a"])[l])
    shared = dict(
        w_mod=f(inp["w_mod"]), b_mod=f(inp["b_mod"]), norm_g=f(inp["norm_g"]), ffn_w_in=f(inp["ffn_w_in"]),
        ffn_w_out=f(inp["ffn_w_out"]), w_in=w_in, w_out=f(inp["w_out"]), final_g=f(inp["final_norm_g"]),
        w_sw=w_sw, w_d=w_d, pcols=pcols, mu=f(inp["rwkv_mu"]), w2=f(inp["rwkv_w2"]), a2=f(inp["rwkv_a2"]),
        g2=f(inp["rwkv_g2"]), ret_g=f(inp["ret_norm_g"]), ln_g=f(inp["rwkv_ln_g"]), ln_b=f(inp["rwkv_ln_b"]),
        sink=f(inp["attn_sink"]))
    shared.update(_consts(T))
    x, c, ctx, c_ctx = f(inp["x"]), f(inp["c"]), f(inp["ctx"]), f(inp["c_ctx"])
    maps = []
    for b in range(x.shape[0]):
        m = dict(shared)
        m["xin"] = np.ascontiguousarray(np.concatenate([x[b], ctx[b]], 0))
        m["cvec"] = np.ascontiguousarray(np.stack([c[b], c_ctx], 0))
        maps.append(m)
    return maps


_PROG = {}


def kernel(**inputs):
    x = np.asarray(inputs["x"])
    B, T, _ = x.shape
    CTX = np.asarray(inputs["ctx"]).shape[1]
    L = np.asarray(inputs["w_in"]).shape[0]
    key = (T, CTX, L)
    if key not in _PROG:
        _PROG[key] = build_program(T, CTX, L)[0]
    nc = _PROG[key]
    maps = _host_maps(inputs, T, CTX, L)
    res = run_bass_kernel_spmd(nc, maps, core_ids=list(range(B)))
    return np.stack([np.asarray(r["out"], dtype=np.float32) for r in res.results], 0)
```

```python
import contextlib
import numpy as np
import concourse.bass as bass
import concourse.mybir as mybir
from concourse.bass_utils import run_bass_kernel_spmd

ACT = mybir.ActivationFunctionType
ALU = mybir.AluOpType
AX = mybir.AxisListType
F32 = mybir.dt.float32
BF16 = mybir.dt.bfloat16

D = 1024
DFF = 2816
NMOD = 9
RMS_EPS = 1e-6
STRICT = True


class PSem:
    def __init__(self, h):
        self.h = h
        self.val = 0


class Eng:
    def __init__(self, name, h, sem):
        self.name, self.h, self.sem = name, h, sem
        self.n = 0
        self.seen = {}
        self.dseen = {}


class Buf:
    def __init__(self, k, t, name):
        self.k, self.t, self.name = k, t, name
        self.w = None
        self.r = {}
        self.psem = None
        self.dma_w = 0
        self.dma_rw = 0

    def __getitem__(self, key):
        return self.t[key]

    @property
    def v(self):
        return _VA(self)


class View:
    def __init__(self, buf, ap):
        self.buf, self.ap = buf, ap

    def rr(self, pat, **kw):
        return View(self.buf, self.ap.rearrange(pat, **kw))

    def bc(self, shape):
        return View(self.buf, self.ap.to_broadcast(list(shape)))

    def __getitem__(self, key):
        return View(self.buf, self.ap[key])


class _VA:
    def __init__(self, b):
        self.b = b

    def __getitem__(self, key):
        return View(self.b, self.b.t[key])


def _bufs(*xs):
    out = []
    for x in xs:
        if isinstance(x, View) and x.buf not in out:
            out.append(x.buf)
    return out


def _ap(x):
    return x.ap if isinstance(x, View) else x


class K:
    def __init__(self, nc):
        self.nc = nc
        self.stack = contextlib.ExitStack()
        self.E = {}
        for name, h in (("pe", nc.tensor), ("act", nc.scalar), ("dve", nc.vector),
                        ("pool", nc.gpsimd), ("sp", nc.sync)):
            sem = self.stack.enter_context(nc.semaphore("s_" + name))
            self.E[name] = Eng(name, h, sem)
        self.free_psems = []
        self.n_psems = 0
        self.all_psems = []
        self.bufs = []
        self.uid = 0
        self.ninstr = 0
        self.marks = []

    def _psem(self):
        if self.free_psems:
            return self.free_psems.pop()
        self.n_psems += 1
        h = self.stack.enter_context(self.nc.semaphore("d%d" % self.n_psems))
        p = PSem(h)
        self.all_psems.append(p)
        return p

    def sb(self, st, shape, dtype, name):
        self.uid += 1
        t = st.enter_context(self.nc.sbuf_tensor("%s_%d" % (name, self.uid), list(shape), dtype))
        b = Buf(self, t, name)
        st.callback(self._release, b)
        return b

    def ps(self, st, shape, dtype, name):
        self.uid += 1
        t = st.enter_context(self.nc.psum_tensor("%s_%d" % (name, self.uid), list(shape), dtype))
        return Buf(self, t, name)

    def _release(self, b):
        if b.psem is not None and not getattr(b.psem, "sw", False):
            self.free_psems.append(b.psem)
        b.psem = None

    def _wait_eng(self, E, X, n):
        if E.seen.get(X.name, 0) >= n:
            return
        E.h.wait_ge(X.sem, n)
        E.seen[X.name] = n
        self.ninstr += 1

    def _wait_d(self, E, p, v):
        if v <= 0 or E.dseen.get(id(p), 0) >= v:
            return
        E.h.wait_ge(p.h, v)
        E.dseen[id(p)] = v
        self.ninstr += 1

    def _deps(self, E, r, w):
        for b in r:
            if b.w is not None:
                self._wait_eng(E, b.w[0], b.w[1])
            if b.psem is not None:
                self._wait_d(E, b.psem, b.dma_w)
        for b in w:
            if b.w is not None and (STRICT or b.w[0] is not E) and not (E.name == "pe" and b.w[0] is E):
                self._wait_eng(E, b.w[0], b.w[1])
            for X, n in b.r.items():
                if STRICT or X is not E:
                    self._wait_eng(E, X, n)
            if b.psem is not None:
                self._wait_d(E, b.psem, b.dma_rw)

    def op(self, eng, fn, r=(), w=()):
        E = self.E[eng]
        self._deps(E, r, w)
        ins = fn(E.h)
        E.n += 1
        ins.then_inc(E.sem, 1)
        self.ninstr += 1
        for b in w:
            b.w = (E, E.n)
            b.r = {}
        for b in r:
            if b.w is None or b.w[0] is not E or b.w[1] != E.n:
                b.r[E] = E.n
        return ins

    def dma(self, q, out, in_, r=(), w=(), **kw):
        Q = self.E[q]
        self._deps(Q, r, w)
        anchor = w[0] if w else r[0]
        if anchor.psem is None:
            if q == "pool":
                self.n_psems += 1
                anchor.psem = PSem(self.stack.enter_context(self.nc.semaphore("w%d" % self.n_psems)))
                anchor.psem.sw = True
                self.all_psems.append(anchor.psem)
            else:
                anchor.psem = self._psem()
        p = anchor.psem
        assert getattr(p, "sw", False) == (q == "pool"), "buffer mixes SW and HW DGE DMAs: " + anchor.name
        Q.h.dma_start(out=out, in_=in_, **kw).then_inc(p.h, 16)
        p.val += 16
        self.ninstr += 1
        anchor.dma_rw = p.val
        if w:
            anchor.dma_w = p.val
            anchor.w = None
            anchor.r = {}

    def mm(self, out, lhsT, rhs, start=True, stop=True):
        return self.op("pe", lambda e: e.matmul(out.ap, lhsT.ap, rhs.ap, start=start, stop=stop),
                       r=_bufs(lhsT, rhs), w=[out.buf])

    def tr(self, out, in_, ident):
        return self.op("pe", lambda e: e.transpose(out.ap, in_.ap, ident.ap), r=_bufs(in_, ident), w=[out.buf])

    def tt(self, eng, out, in0, in1, op):
        return self.op(eng, lambda e: e.tensor_tensor(out=out.ap, in0=in0.ap, in1=in1.ap, op=op),
                       r=_bufs(in0, in1), w=[out.buf])

    def ts(self, eng, out, in0, s1, s2, op0, op1=None):
        kw = {} if op1 is None else dict(op1=op1)
        return self.op(eng, lambda e: e.tensor_scalar(out=out.ap, in0=in0.ap, scalar1=_ap(s1), scalar2=_ap(s2),
                                                      op0=op0, **kw), r=_bufs(in0, s1, s2), w=[out.buf])

    def stt(self, eng, out, in0, scalar, in1, op0, op1):
        return self.op(eng, lambda e: e.scalar_tensor_tensor(out=out.ap, in0=in0.ap, scalar=_ap(scalar), in1=in1.ap,
                                                             op0=op0, op1=op1), r=_bufs(in0, scalar, in1), w=[out.buf])

    def act(self, out, in_, func, bias=None, scale=None, accum=None):
        kw = {}
        if bias is not None:
            kw["bias"] = _ap(bias)
        if scale is not None:
            kw["scale"] = _ap(scale)
        w = [out.buf]
        if accum is not None:
            kw["accum_out"] = accum.ap
            w.append(accum.buf)
        return self.op("act", lambda e: e.activation(out=out.ap, in_=in_.ap, func=func, **kw),
                       r=_bufs(in_, bias, scale), w=w)

    def cp(self, eng, out, in_):
        if eng == "act":
            return self.op("act", lambda e: e.copy(out=out.ap, in_=in_.ap), r=[in_.buf], w=[out.buf])
        return self.op(eng, lambda e: e.tensor_copy(out=out.ap, in_=in_.ap), r=[in_.buf], w=[out.buf])

    def memset(self, eng, out, val):
        return self.op(eng, lambda e: e.memset(out.ap, val), w=[out.buf])

    def recip(self, out, in_):
        return self.op("dve", lambda e: e.reciprocal(out=out.ap, in_=in_.ap), r=[in_.buf], w=[out.buf])

    def ld(self, q, out, dram, **kw):
        return self.dma(q, out.ap, dram, w=[out.buf], **kw)

    def stv(self, q, dram, in_, **kw):
        return self.dma(q, dram, in_.ap, r=[in_.buf], **kw)

    def barrier(self):
        for E in self.E.values():
            for X in self.E.values():
                if X is not E and X.n > 0:
                    self._wait_eng(E, X, X.n)
            for p in self.all_psems:
                self._wait_d(E, p, p.val)


def mm(k, out_ap, lhsT_ap, rhs_ap, start, stop, r, w):
    return k.op("pe", lambda e: e.matmul(out_ap, lhsT_ap, rhs_ap, start=start, stop=stop), r=r, w=w)


def rowphase(k, st, cfg):
    nc = k.nc
    ffn = cfg["ffn"]
    pre = cfg.get("pre")
    post = cfg.get("post")
    w1 = k.sb(st, [128, 8, 2 * DFF], BF16, "w1")
    w2 = k.sb(st, [128, 22, D], BF16, "w2")
    w1v = ffn["w_in"].rearrange("(c p) f -> p c f", p=128)
    for c in range(8):
        for hlf in range(4):
            f0 = hlf * 1408
            k.dma("pool", w1[:, c, f0:f0 + 1408], w1v[:, c, f0:f0 + 1408], w=[w1])
    w2v = ffn["w_out"].rearrange("(c p) d -> p c d", p=128)
    for c in range(22):
        k.dma("pool", w2[:, c, :], w2v[:, c, :], w=[w2])
    if pre is not None:
        wo = k.sb(st, [128, 8, D], BF16, "wo")
        wov = pre["w_out"].rearrange("(c p) d -> p c d", p=128)
        for c in range(8):
            k.dma("pool", wo[:, c, :], wov[:, c, :], w=[wo])

    identf = k.sb(st, [128, 128], F32, "identf")
    k.dma("sp", identf[:, :], cfg["ident"], w=[identf])
    xt = [k.sb(st, [128, D], F32, "xt%d" % i) for i in range(4)]
    hT = k.sb(st, [128, 8, 512], BF16, "hT")
    gT = k.sb(st, [128, 22, 512], BF16, "gT")
    sg = k.sb(st, [128, 512], BF16, "sg")
    tmp = k.sb(st, [128, D], F32, "tmp")
    ssq = k.sb(st, [128, 4], F32, "ssq")
    rinv = k.sb(st, [128, 4], F32, "rinv")
    ps_u = [k.ps(st, [128, 512], F32, "psu%d" % i) for i in range(4)]
    ps_y = [k.ps(st, [128, 512], F32, "psy%d" % i) for i in range(2)]
    ps_t = k.ps(st, [128, D], F32, "pst")
    Ac = k.sb(st, [128, 8], F32, "Ac")
    Bc = k.sb(st, [128, 8], F32, "Bc")
    Gc = k.sb(st, [128, 8], F32, "Gc")
    A2c = k.sb(st, [128, 8], F32, "A2c")
    B2c = k.sb(st, [128, 8], F32, "B2c")
    G = k.sb(st, [128, D], F32, "G")
    PG = k.sb(st, [128, D], F32, "PG") if pre is not None else None
    gvec2 = None
    if post is not None and post["kind"] == "final":
        gvec2 = k.sb(st, [128, D], F32, "gvec2")
        k.dma("sp", gvec2[:, :], post["g"].partition_broadcast(128), w=[gvec2])

    def col(ap1d):
        return ap1d.rearrange("(c p) -> p c", p=128)

    def load_cols(dst, ap1d):
        k.dma("sp", dst[:, :], col(ap1d), w=[dst], allow_slow_non_contiguous=True)

    def setup_mod(mr):
        load_cols(Gc, ffn["g"])
        load_cols(Ac, cfg["mod"][mr, ffn["scale"] * D:(ffn["scale"] + 1) * D])
        load_cols(Bc, cfg["mod"][mr, ffn["shift"] * D:(ffn["shift"] + 1) * D])
        k.op("dve", lambda e: e.scalar_tensor_tensor(out=Ac[:, :], in0=Ac[:, :], scalar=1.0, in1=Gc[:, :],
                                                     op0=ALU.add, op1=ALU.mult), r=[Ac, Gc], w=[Ac])
        if post is not None and post["kind"] == "hT":
            load_cols(Gc, post["g"])
            load_cols(A2c, cfg["mod"][mr, post["scale"] * D:(post["scale"] + 1) * D])
            load_cols(B2c, cfg["mod"][mr, post["shift"] * D:(post["shift"] + 1) * D])
            k.op("dve", lambda e: e.scalar_tensor_tensor(out=A2c[:, :], in0=A2c[:, :], scalar=1.0, in1=Gc[:, :],
                                                         op0=ALU.add, op1=ALU.mult), r=[A2c, Gc], w=[A2c])
        k.dma("sp", G[:, :], cfg["mod"][mr, ffn["gate"] * D:(ffn["gate"] + 1) * D].partition_broadcast(128), w=[G])
        k.op("dve", lambda e: e.tensor_scalar(out=G[:, :], in0=G[:, :], scalar1=0.5, scalar2=None, op0=ALU.mult),
             r=[G], w=[G])
        if pre is not None:
            k.dma("sp", PG[:, :], cfg["mod"][mr, pre["gate"] * D:(pre["gate"] + 1) * D].partition_broadcast(128),
                  w=[PG])

    try:
        tmp2 = [tmp, k.sb(st, [128, D], F32, "tmpb")]
    except AssertionError:
        tmp2 = [tmp, tmp]

    def rms_all(xs):
        m_ = len(xs)
        junk = gT[:, 0:2, :]
        for s_, xb in enumerate(xs):
            k.op("act", lambda e: e.activation(out=junk, in_=xb[:, :].rearrange("p (a b) -> p a b", a=2),
                                               func=ACT.Square, accum_out=ssq[:, s_:s_ + 1]), r=[xb], w=[gT, ssq])
        k.op("dve", lambda e: e.tensor_scalar(out=rinv[:, 0:m_], in0=ssq[:, 0:m_], scalar1=1.0 / D, scalar2=RMS_EPS,
                                              op0=ALU.mult, op1=ALU.add), r=[ssq], w=[rinv])
        k.op("act", lambda e: e.activation(out=rinv[:, 0:m_], in_=rinv[:, 0:m_], func=ACT.Sqrt), r=[rinv], w=[rinv])
        k.op("dve", lambda e: e.reciprocal(out=rinv[:, 0:m_], in_=rinv[:, 0:m_]), r=[rinv], w=[rinv])

    def norm_mod_T_all(xs, A, B, dstT):
        rms_all(xs)
        for s, xb in enumerate(xs):
            tm = tmp2[s % 2]
            k.op("dve", lambda e: e.tensor_scalar(out=tm[:, :], in0=xb[:, :], scalar1=rinv[:, s:s + 1], scalar2=None,
                                                  op0=ALU.mult), r=[xb, rinv], w=[tm])
            for c in range(8):
                k.op("pe", lambda e: e.transpose(ps_t[:, c * 128:(c + 1) * 128], tm[:, c * 128:(c + 1) * 128],
                                                 identf[:, :]), r=[tm, identf], w=[ps_t])
            for c in range(8):
                if c % 2 == 0:
                    k.op("dve", lambda e: e.tensor_scalar(out=dstT[:, c, s * 128:(s + 1) * 128],
                                                          in0=ps_t[:, c * 128:(c + 1) * 128], scalar1=A[:, c:c + 1],
                                                          scalar2=B[:, c:c + 1], op0=ALU.mult, op1=ALU.add),
                         r=[ps_t, A, B], w=[dstT])
                else:
                    k.op("act", lambda e: e.activation(out=dstT[:, c, s * 128:(s + 1) * 128],
                                                       in_=ps_t[:, c * 128:(c + 1) * 128], func=ACT.Identity,
                                                       scale=A[:, c:c + 1], bias=B[:, c:c + 1]),
                         r=[ps_t, A, B], w=[dstT])

    ui = 0
    yi = 0
    xsets = [xt]
    if pre is None:
        try:
            xsets.append([k.sb(st, [128, D], F32, "xu%d" % i) for i in range(4)])
        except AssertionError:
            pass
    tiles = []
    for (tok0, ntok, mr) in cfg["segs"]:
        for t0 in range(tok0, tok0 + ntok, 512):
            tiles.append((t0, min(512, tok0 + ntok - t0), mr, tok0, ntok))

    def emit_xloads(m):
        t0_, n_, _, _, _ = tiles[m]
        xs_ = xsets[m % len(xsets)]
        for s in range(n_ // 128):
            k.dma("sp", xs_[s][:, :], cfg["xin"][t0_ + s * 128:t0_ + (s + 1) * 128, :], w=[xs_[s]])

    if len(xsets) == 2:
        emit_xloads(0)
    cur_mr = None
    for m_i, (t0, n, mr, tok0, ntok) in enumerate(tiles):
        if mr != cur_mr:
            setup_mod(mr)
            cur_mr = mr
        if True:
            ns = n // 128
            xt = xsets[m_i % len(xsets)]
            if len(xsets) == 1:
                emit_xloads(m_i)
            elif m_i + 1 < len(tiles):
                emit_xloads(m_i + 1)
            if pre is not None:
                for s in range(ns):
                    k.dma("sp", tmp[:, :], pre["o"][t0 + s * 128:t0 + (s + 1) * 128, :], w=[tmp])
                    for c in range(8):
                        k.op("pe", lambda e: e.transpose(ps_t[:, c * 128:(c + 1) * 128], tmp[:, c * 128:(c + 1) * 128],
                                                         identf[:, :]), r=[tmp, identf], w=[ps_t])
                    k.op("act", lambda e: e.copy(out=hT[:, :, s * 128:(s + 1) * 128],
                                                 in_=ps_t[:, :].rearrange("p (c t) -> p c t", c=8)), r=[ps_t], w=[hT])
                for s in range(ns):
                    for hf in range(2):
                        py = ps_y[yi % 2]
                        yi += 1
                        for c in range(8):
                            mm(k, py[:, :], hT[:, c, s * 128:(s + 1) * 128], wo[:, c, hf * 512:(hf + 1) * 512],
                               c == 0, c == 7, r=[hT, wo], w=[py])
                        sl = slice(hf * 512, (hf + 1) * 512)
                        k.op("dve", lambda e, py=py, sl=sl: e.tensor_tensor(out=tmp[:, sl], in0=py[:, :],
                                                                             in1=PG[:, sl], op=ALU.mult),
                             r=[py, PG], w=[tmp])
                        k.op("dve", lambda e, s=s, sl=sl: e.tensor_tensor(out=xt[s][:, sl], in0=xt[s][:, sl],
                                                                            in1=tmp[:, sl], op=ALU.add),
                             r=[xt[s], tmp], w=[xt[s]])
            norm_mod_T_all(xt[0:ns], Ac, Bc, hT)
            for fc in range(22 if cfg.get("stage", 9) >= 2 else 0):
                p1 = ps_u[ui % 4]
                p2 = ps_u[(ui + 1) % 4]
                ui += 2
                for c in range(8):
                    mm(k, p1[:, 0:n], w1[:, c, fc * 128:(fc + 1) * 128], hT[:, c, 0:n], c == 0, c == 7,
                       r=[w1, hT], w=[p1])
                for c in range(8):
                    mm(k, p2[:, 0:n], w1[:, c, DFF + fc * 128:DFF + (fc + 1) * 128], hT[:, c, 0:n], c == 0, c == 7,
                       r=[w1, hT], w=[p2])
                k.op("act", lambda e, p1=p1: e.activation(out=sg[:, 0:n], in_=p1[:, 0:n], func=ACT.Silu),
                     r=[p1], w=[sg])
                k.op("dve", lambda e, p2=p2, fc=fc: e.tensor_tensor(out=gT[:, fc, 0:n], in0=sg[:, 0:n],
                                                                     in1=p2[:, 0:n], op=ALU.mult),
                     r=[sg, p2], w=[gT])
            for s in range(ns if cfg.get("stage", 9) >= 3 else 0):
                for hf in range(2):
                    py = ps_y[yi % 2]
                    yi += 1
                    for fc in range(22):
                        mm(k, py[:, :], gT[:, fc, s * 128:(s + 1) * 128], w2[:, fc, hf * 512:(hf + 1) * 512],
                           fc == 0, fc == 21, r=[gT, w2], w=[py])
                    sl = slice(hf * 512, (hf + 1) * 512)
                    k.op("dve", lambda e, py=py, sl=sl: e.tensor_tensor(out=tmp[:, sl], in0=py[:, :], in1=G[:, sl],
                                                                         op=ALU.mult), r=[py, G], w=[tmp])
                    k.op("dve", lambda e, s=s, sl=sl: e.tensor_tensor(out=xt[s][:, sl], in0=xt[s][:, sl],
                                                                        in1=tmp[:, sl], op=ALU.add),
                         r=[xt[s], tmp], w=[xt[s]])
            if post is not None and post["kind"] == "final":
                if mr == 0:
                    rms_all(xt[0:ns])
                    for s in range(ns):
                        r0 = t0 + s * 128
                        tm = tmp2[s % 2]
                        k.op("dve", lambda e: e.scalar_tensor_tensor(out=tm[:, :], in0=xt[s][:, :],
                                                                     scalar=rinv[:, s:s + 1], in1=gvec2[:, :],
                                                                     op0=ALU.mult, op1=ALU.mult),
                             r=[xt[s], rinv, gvec2], w=[tm])
                        k.dma("sp", post["out"][r0:r0 + 128, :], tm[:, :], r=[tm])
                continue
            for s in range(ns):
                r0 = t0 + s * 128
                k.dma("sp", cfg["xout"][r0:r0 + 128, :], xt[s][:, :], r=[xt[s]])
            if post is not None and post["kind"] == "hT":
                norm_mod_T_all(xt[0:ns], A2c, B2c, hT)
            if post is not None and post["kind"] == "hT":
                for c in range(8):
                    k.dma("sp", post["hT"][c * 128:(c + 1) * 128, t0:t0 + n], hT[:, c, 0:n], r=[hT])


CH = 64
DECAY_SCALE = 0.6065306597126334


def projphase(k, st, cfg):
    S = cfg["S"]
    C = cfg["C"]
    Wt = cfg["W"]
    T, CTX = cfg["T"], cfg["CTX"]
    hTd = S["hT"]
    wabc = k.sb(st, [128, 8, 2304], BF16, "wabc")
    wgd = k.sb(st, [128, 8, 128], BF16, "wgd")
    wsw = k.sb(st, [128, 8, 1280], BF16, "wsw")
    wv = Wt["w_in"].rearrange("(c p) f -> p c f", p=128)
    for c in range(8):
        k.ld("pool", wabc.v[:, c, 0:1152], wv[:, c, 0:1152])
        k.ld("pool", wabc.v[:, c, 1152:2304], wv[:, c, 1152:2304])
        k.ld("pool", wgd.v[:, c, :], wv[:, c, 3072:3200])
        k.ld("pool", wsw.v[:, c, :], Wt["w_sw"].rearrange("(c p) f -> p c f", p=128)[:, c, :])
    wda = [k.sb(st, [128, 8, 896], BF16, "wda%d" % d) for d in range(2)]
    wdb = [k.sb(st, [128, 8, 896], BF16, "wdb%d" % d) for d in range(2)]
    with contextlib.ExitStack() as st2:
        wst = k.sb(st2, [128, 8, 896], F32, "wst")
        mub = k.sb(st2, [128, 896], F32, "mub")
        wtm = k.sb(st2, [128, 8, 896], F32, "wtm")
        for d in range(2):
            for c in range(8):
                k.ld("sp", wst.v[:, c, :], Wt["w_d"][d].rearrange("(c p) f -> p c f", p=128)[:, c, :])
            k.ld("sp", mub.v[:, :], Wt["mu"][d].partition_broadcast(128))
            for c in range(8):
                k.tt("dve", wtm.v[:, c, :], wst.v[:, c, :], mub.v[:, :], ALU.mult)
            k.cp("act", wda[d].v[:, :, :], wtm.v[:, :, :])
            k.tt("dve", wdb[d].v[:, :, :], wst.v[:, :, :], wtm.v[:, :, :], ALU.subtract)
        k.barrier()
    pc = k.sb(st, [128, 24], F32, "pcols")
    k.ld("sp", pc.v[:, :], Wt["pcols"])
    oma = k.sb(st, [128, 2], F32, "oma")
    k.ts("dve", oma.v[:, :], pc.v[:, 14:16], -1.0, 1.0, ALU.mult, ALU.add)
    identf = k.sb(st, [128, 128], F32, "identf")
    k.ld("sp", identf.v[:, :], C["ident"])
    bd = k.sb(st, [128, 128], F32, "bd")
    k.ld("sp", bd.v[:, :], C["bd64"])
    e2 = k.sb(st, [128, 2], F32, "e2")
    k.ld("sp", e2.v[:, :], C["e2"])
    rtab = k.sb(st, [128, 2, 2, 3, CH], F32, "rtab")
    k.ld("sp", rtab.v[:, :, :, :, :], C["ret_tab"])
    w2s = k.sb(st, [64, 2, 256], BF16, "w2s")
    a2s = k.sb(st, [64, 2, 256], BF16, "a2s")
    g2s = k.sb(st, [128, 256], BF16, "g2s")
    for d in range(2):
        k.ld("pool", w2s.v[:, d, :], Wt["w2"][d])
        k.ld("pool", a2s.v[:, d, :], Wt["a2"][d])
    k.ld("pool", g2s.v[:, :], Wt["g2"])

    NT = 512
    hTt = k.sb(st, [128, 8, NT + 2], BF16, "hTt")
    ca = k.sb(st, [128, NT], F32, "ca")
    sa = k.sb(st, [128, NT], F32, "sa")
    cs_ = k.sb(st, [128, NT], F32, "cs")
    ss_ = k.sb(st, [128, NT], F32, "ss")
    gca = k.sb(st, [128, 2, NT], F32, "gca")
    gsa = k.sb(st, [128, 2, NT], F32, "gsa")
    F = [k.sb(st, [128, NT], F32, "f%d" % i) for i in range(12)]
    ob = k.sb(st, [128, NT], BF16, "ob")
    sbf = [k.sb(st, [128, NT], BF16, "sbf%d" % i) for i in range(2)]
    sbi = [0]

    def st_bf(dst, srcv, n):
        b = sbf[sbi[0] % 2]
        sbi[0] += 1
        k.cp("act", b.v[:, 0:n], srcv)
        k.stv("sp", dst, b.v[:, 0:n])
    tb = k.sb(st, [128, 512], F32, "tb")
    tbb = k.sb(st, [128, 512], BF16, "tbb")
    thb = k.sb(st, [64, NT], BF16, "thb")
    alb = k.sb(st, [64, NT], BF16, "alb")
    sgb = k.sb(st, [128, NT], BF16, "sgb")
    bs4 = k.sb(st, [128, 4, 4], F32, "bs4")
    pcx = k.sb(st, [128, NT // CH], F32, "pcx")
    pcx2 = k.sb(st, [128, NT // CH], F32, "pcx2")
    F2 = [k.sb(st, [128, NT], F32, "g%d" % i) for i in range(9)]
    PS = [k.ps(st, [128, 512], F32, "pp%d" % i) for i in range(8)]
    pi = [0]

    def nps():
        p = PS[pi[0] % 8]
        pi[0] += 1
        return p

    def chain(p, wlist, M, n, col0=1):
        tot = len(wlist) * 8
        i = 0
        for (wb, c0, sh) in wlist:
            for c in range(8):
                k.mm(p.v[0:M, 0:n], wb.v[:, c, c0:c0 + M], hTt.v[:, c, col0 + sh:col0 + sh + n], i == 0, i == tot - 1)
                i += 1

    def chain_tok(p, wlist, ncols, s):
        tot = len(wlist) * 8
        i = 0
        for (wb, c0, sh) in wlist:
            for c in range(8):
                k.mm(p.v[:, 0:ncols], hTt.v[:, c, 1 + sh + s * 128:1 + sh + (s + 1) * 128], wb.v[:, c, c0:c0 + ncols],
                     i == 0, i == tot - 1)
                i += 1

    def rope_out(P, Psw, ctab, stab, dst, n):
        k.tt("dve", F[10].v[:, 0:n], P.v[:, 0:n], ctab, ALU.mult)
        k.tt("dve", F[11].v[:, 0:n], Psw.v[:, 0:n], stab, ALU.mult)
        k.tt("dve", dst, F[10].v[:, 0:n], F[11].v[:, 0:n], ALU.add)

    segs = [(0, T, True), (T, CTX, False)]
    for (seg0, seglen, latent) in segs:
        for t0 in range(seg0, seg0 + seglen, NT):
            n = min(NT, seg0 + seglen - t0)
            ns = n // 128
            nch = n // CH
            lo = t0 - 1 if t0 > seg0 else t0
            hi = t0 + n + 1 if t0 + n < seg0 + seglen else t0 + n
            if lo == t0:
                k.memset("dve", hTt.v[:, :, 0:1], 0.0)
            if hi == t0 + n:
                k.memset("dve", hTt.v[:, :, n + 1:n + 2], 0.0)
            for c in range(8):
                k.ld("pool", hTt.v[:, c, 1 - (t0 - lo):1 + n + (hi - t0 - n)], hTd[c * 128:(c + 1) * 128, lo:hi])
            if latent:
                k.ld("pool", ca.v[:, 0:n], C["ropeA_c"][:, t0:t0 + n])
                k.ld("pool", sa.v[:, 0:n], C["ropeA_s"][:, t0:t0 + n])
                k.ld("pool", cs_.v[:, 0:n], C["ropeS_c"][:, t0:t0 + n])
                k.ld("pool", ss_.v[:, 0:n], C["ropeS_s"][:, t0:t0 + n])
                for qk in range(2):
                    k.ts("dve", gca.v[:, qk, 0:n], ca.v[:, 0:n], pc.v[:, 2 * qk:2 * qk + 1], None, ALU.mult)
                    k.ts("dve", gsa.v[:, qk, 0:n], sa.v[:, 0:n], pc.v[:, 2 * qk + 1:2 * qk + 2], None, ALU.mult)
            for grp, base, swb, dq, dk_, dv_ in (("A", 0, 0, S["QA"], S["KA"], S["VA"]),
                                                ("B", 512, 384, S["QB"], S["KB"], S["VB"])):
                for j in range(3):
                    P, Psw = nps(), nps()
                    chain(P, [(wabc, base + j * 128, 0)], 128, n)
                    if latent:
                        chain(Psw, [(wsw, swb + j * 128, 0)], 128, n)
                    qk = 0 if j < 2 else 1
                    if grp == "B":
                        k.act(F[0].v[:, 0:n], P.v[:, 0:n], ACT.Square)
                        pss = nps()
                        k.mm(pss.v[:, 0:n], bd.v[:, :], F[0].v[:, 0:n])
                        k.ts("dve", F[1].v[:, 0:n], pss.v[:, 0:n], 1.0 / 64, RMS_EPS, ALU.mult, ALU.add)
                        k.act(F[1].v[:, 0:n], F[1].v[:, 0:n], ACT.Sqrt)
                        k.recip(F[1].v[:, 0:n], F[1].v[:, 0:n])
                        k.tt("dve", F[2].v[:, 0:n], P.v[:, 0:n], F[1].v[:, 0:n], ALU.mult)
                        if latent:
                            k.tt("dve", F[3].v[:, 0:n], Psw.v[:, 0:n], F[1].v[:, 0:n], ALU.mult)
                            rope_out(F[2], F[3], gca.v[:, qk, 0:n], gsa.v[:, qk, 0:n], ob.v[:, 0:n], n)
                        else:
                            k.ts("dve", ob.v[:, 0:n], F[2].v[:, 0:n], pc.v[:, 2 * qk:2 * qk + 1], None, ALU.mult)
                    else:
                        if latent:
                            rope_out(P, Psw, ca.v[:, 0:n], sa.v[:, 0:n], ob.v[:, 0:n], n)
                        else:
                            k.cp("act", ob.v[:, 0:n], P.v[:, 0:n])
                    dst = dq[j * 128:(j + 1) * 128, t0:t0 + n] if j < 2 else dk_[:, t0:t0 + n]
                    k.stv("sp", dst, ob.v[:, 0:n])
                for s in range(ns):
                    P = nps()
                    chain_tok(P, [(wabc, base + 384, 0)], 128, s)
                    k.cp("act", tbb.v[:, 0:128], P.v[:, 0:128])
                    k.stv("sp", dv_[t0 + s * 128:t0 + (s + 1) * 128, :], tbb.v[:, 0:128])
            for j in range(4):
                P, Psw = nps(), nps()
                chain(P, [(wabc, 1024 + j * 128, 0)], 128, n)
                g = j % 2
                if latent:
                    chain(Psw, [(wsw, 768 + j * 128, 0)], 128, n)
                    rope_out(P, Psw, cs_.v[:, 0:n], ss_.v[:, 0:n], F[0].v[:, 0:n], n)
                else:
                    k.cp("act", F[0].v[:, 0:n], P.v[:, 0:n])
                x3 = F[0].v[:, 0:n].rr("p (c i) -> p c i", i=CH)
                for d in range(2):
                    if j < 2:
                        k.tt("dve", F[1].v[:, 0:n].rr("p (c i) -> p c i", i=CH), x3,
                             rtab.v[:, g, d, 0:1, :].bc([128, nch, CH]), ALU.mult)
                        st_bf(S["C_Rt"][d, g * 128:(g + 1) * 128, t0:t0 + n], F[1].v[:, 0:n], n)
                    else:
                        k.tt("dve", F[1].v[:, 0:n].rr("p (c i) -> p c i", i=CH), x3,
                             rtab.v[:, g, d, 1:2, :].bc([128, nch, CH]), ALU.mult)
                        st_bf(S["C_Kt"][d, g * 128:(g + 1) * 128, t0:t0 + n], F[1].v[:, 0:n], n)
                        k.tt("dve", F[2].v[:, 0:n].rr("p (c i) -> p c i", i=CH), x3,
                             rtab.v[:, g, d, 2:3, :].bc([128, nch, CH]), ALU.mult)
                        for s in range(ns):
                            pt = nps()
                            k.tr(pt.v[:, 0:128], F[2].v[:, s * 128:(s + 1) * 128], identf.v[:, :])
                            k.cp("act", tbb.v[:, 0:128], pt.v[:, 0:128])
                            k.stv("sp", S["C_Kh"][d, t0 + s * 128:t0 + (s + 1) * 128, g * 128:(g + 1) * 128],
                                  tbb.v[:, 0:128])
            for s in range(ns):
                P = nps()
                chain_tok(P, [(wabc, 1536, 0)], 256, s)
                k.cp("act", tbb.v[:, 0:256], P.v[:, 0:256])
                k.stv("sp", S["C_V"][t0 + s * 128:t0 + (s + 1) * 128, :], tbb.v[:, 0:256])
                P = nps()
                chain_tok(P, [(wabc, 1792, 0)], 512, s)
                k.act(tb.v[:, 0:512], P.v[:, 0:512], ACT.Silu)
                k.stv("sp", S["C_G"][t0 + s * 128:t0 + (s + 1) * 128, :], tb.v[:, 0:512])
            P = nps()
            chain(P, [(wgd, 0, 0)], 128, n)
            k.act(sgb.v[:, 0:n], P.v[:, 0:n], ACT.Sigmoid)
            for s in range(ns):
                P = nps()
                k.mm(P.v[:, 0:256], sgb.v[:, s * 128:(s + 1) * 128], g2s.v[:, :])
                k.cp("act", tb.v[:, 0:256], P.v[:, 0:256])
                k.stv("sp", S["D_G"][t0 + s * 128:t0 + (s + 1) * 128, :], tb.v[:, 0:256])
            for d in range(2):
                sh = -1 if d == 0 else 1
                wl = lambda c0: [(wdb[d], c0, 0), (wda[d], c0, sh)]
                P = nps()
                chain(P, wl(768), 64, n)
                k.act(thb.v[:, 0:n], P.v[0:64, 0:n], ACT.Tanh)
                P = nps()
                chain(P, wl(832), 64, n)
                k.cp("act", alb.v[:, 0:n], P.v[0:64, 0:n])
                for s in range(ns):
                    P = nps()
                    chain_tok(P, wl(512), 256, s)
                    k.cp("act", tbb.v[:, 0:256], P.v[:, 0:256])
                    k.stv("sp", S["D_V"][d, t0 + s * 128:t0 + (s + 1) * 128, :], tbb.v[:, 0:256])
                def dgroup(g, FB, pcx):
                    r_, k_, lw, cs, a_, kk, e_, t1, t2 = FB
                    gc = slice(g, g + 1)
                    P = nps()
                    chain(P, wl(g * 128), 128, n)
                    k.cp("act", r_.v[:, 0:n], P.v[:, 0:n])
                    P = nps()
                    chain(P, wl(256 + g * 128), 128, n)
                    k.cp("act", k_.v[:, 0:n], P.v[:, 0:n])
                    yield
                    P = nps()
                    k.mm(P.v[:, 0:n], w2s.v[:, d, g * 128:(g + 1) * 128], thb.v[:, 0:n])
                    k.act(lw.v[:, 0:n], P.v[:, 0:n], ACT.Sigmoid, bias=pc.v[:, 4 + 2 * d + g:5 + 2 * d + g])
                    k.ts("dve", lw.v[:, 0:n], lw.v[:, 0:n], -DECAY_SCALE, None, ALU.mult)
                    yield
                    P = nps()
                    k.mm(P.v[:, 0:n], a2s.v[:, d, g * 128:(g + 1) * 128], alb.v[:, 0:n])
                    k.act(a_.v[:, 0:n], P.v[:, 0:n], ACT.Sigmoid, bias=pc.v[:, 8 + 2 * d + g:9 + 2 * d + g])
                    yield
                    k.ts("dve", kk.v[:, 0:n], k_.v[:, 0:n], pc.v[:, 12 + g:13 + g], None, ALU.mult)
                    k.act(t1.v[:, 0:n], kk.v[:, 0:n], ACT.Square)
                    yield
                    P = nps()
                    k.mm(P.v[:, 0:n], bd.v[:, :], t1.v[:, 0:n])
                    k.act(t1.v[:, 0:n], P.v[:, 0:n], ACT.Sqrt)
                    yield
                    k.ts("dve", t1.v[:, 0:n], t1.v[:, 0:n], 1e-12, None, ALU.max)
                    k.recip(t1.v[:, 0:n], t1.v[:, 0:n])
                    yield
                    k.tt("dve", kk.v[:, 0:n], kk.v[:, 0:n], t1.v[:, 0:n], ALU.mult)
                    yield
                    k.ts("dve", t1.v[:, 0:n], a_.v[:, 0:n], pc.v[:, 14 + g:15 + g], oma.v[:, gc], ALU.mult, ALU.add)
                    k.tt("dve", k_.v[:, 0:n], k_.v[:, 0:n], t1.v[:, 0:n], ALU.mult)
                    yield
                    k.stt("dve", t1.v[:, 0:n], r_.v[:, 0:n], pc.v[:, 16 + 2 * d + g:17 + 2 * d + g], k_.v[:, 0:n],
                          ALU.mult, ALU.mult)
                    for s in range(ns):
                        P = nps()
                        k.mm(P.v[:, 0:2], t1.v[:, s * 128:(s + 1) * 128], e2.v[:, :])
                        k.cp("act", bs4.v[:, s, 2 * g:2 * g + 2], P.v[:, 0:2])
                    k.tt("dve", a_.v[:, 0:n], a_.v[:, 0:n], kk.v[:, 0:n], ALU.mult)
                    yield
                    src, dst = lw, cs
                    k.cp("act", t2.v[:, 0:n], lw.v[:, 0:n])
                    src = t2
                    for stp in (1, 2, 4, 8, 16, 32):
                        s3 = src.v[:, 0:n].rr("p (c i) -> p c i", i=CH)
                        d3 = dst.v[:, 0:n].rr("p (c i) -> p c i", i=CH)
                        if d == 0:
                            k.tt("dve", d3[:, :, stp:], s3[:, :, stp:], s3[:, :, :CH - stp], ALU.add)
                            k.cp("act", d3[:, :, :stp], s3[:, :, :stp])
                        else:
                            k.tt("dve", d3[:, :, :CH - stp], s3[:, :, :CH - stp], s3[:, :, stp:], ALU.add)
                            k.cp("act", d3[:, :, CH - stp:], s3[:, :, CH - stp:])
                        src, dst = dst, src
                        yield
                    cs = src
                    spare = dst
                    cs3 = cs.v[:, 0:n].rr("p (c i) -> p c i", i=CH)
                    tot3 = cs3[:, :, CH - 1:CH] if d == 0 else cs3[:, :, 0:1]
                    k.act(e_.v[:, 0:n], cs.v[:, 0:n], ACT.Exp)
                    yield
                    k.tt("dve", t1.v[:, 0:n], r_.v[:, 0:n], e_.v[:, 0:n], ALU.mult)
                    st_bf(S["D_Rt"][d, g * 128:(g + 1) * 128, t0:t0 + n], t1.v[:, 0:n], n)
                    yield
                    k.tt("dve", e_.v[:, 0:n], cs.v[:, 0:n], lw.v[:, 0:n], ALU.subtract)
                    k.act(e_.v[:, 0:n], e_.v[:, 0:n], ACT.Exp)
                    yield
                    k.stt("dve", t1.v[:, 0:n], kk.v[:, 0:n], -1.0, e_.v[:, 0:n], ALU.mult, ALU.mult)
                    st_bf(S["D_At"][d, g * 128:(g + 1) * 128, t0:t0 + n], t1.v[:, 0:n], n)
                    yield
                    for s in range(ns):
                        pt = nps()
                        k.tr(pt.v[:, 0:128], t1.v[:, s * 128:(s + 1) * 128], identf.v[:, :])
                        k.cp("act", tbb.v[:, 0:128], pt.v[:, 0:128])
                        k.stv("sp", S["D_Atok"][d, t0 + s * 128:t0 + (s + 1) * 128, g * 128:(g + 1) * 128],
                              tbb.v[:, 0:128])
                    k.act(e_.v[:, 0:n], cs.v[:, 0:n], ACT.Exp, scale=-1.0)
                    k.tt("dve", t1.v[:, 0:n], a_.v[:, 0:n], e_.v[:, 0:n], ALU.mult)
                    st_bf(S["D_Bt"][d, g * 128:(g + 1) * 128, t0:t0 + n], t1.v[:, 0:n], n)
                    yield
                    k.tt("dve", t1.v[:, 0:n], k_.v[:, 0:n], e_.v[:, 0:n], ALU.mult)
                    st_bf(S["D_Kt"][d, g * 128:(g + 1) * 128, t0:t0 + n], t1.v[:, 0:n], n)
                    yield
                    k.act(pcx.v[:, 0:nch], tot3.rr("p c o -> p (c o)"), ACT.Exp)
                    k.stv("sp", S["D_PC"][d, g * 128:(g + 1) * 128, t0 // CH:t0 // CH + nch], pcx.v[:, 0:nch])
                    k.tt("dve", e_.v[:, 0:n].rr("p (c i) -> p c i", i=CH), tot3.bc([128, nch, CH]), cs3, ALU.subtract)
                    k.act(e_.v[:, 0:n], e_.v[:, 0:n], ACT.Exp)
                    yield
                    for (srcb, dstd) in ((a_, S["D_Bh"]), (k_, S["D_Kh"])):
                        k.tt("dve", t1.v[:, 0:n], srcb.v[:, 0:n], e_.v[:, 0:n], ALU.mult)
                        for s in range(ns):
                            pt = nps()
                            k.tr(pt.v[:, 0:128], t1.v[:, s * 128:(s + 1) * 128], identf.v[:, :])
                            k.cp("act", tbb.v[:, 0:128], pt.v[:, 0:128])
                            k.stv("sp", dstd[d, t0 + s * 128:t0 + (s + 1) * 128, g * 128:(g + 1) * 128], tbb.v[:, 0:128])
                gens = [dgroup(0, F[0:9], pcx), dgroup(1, F2, pcx2)]
                live = list(gens)
                while live:
                    for g_ in list(live):
                        try:
                            next(g_)
                        except StopIteration:
                            live.remove(g_)
                for s in range(ns):
                    k.stv("sp", S["D_BS"][d, t0 + s * 128:t0 + (s + 1) * 128, :], bs4.v[:, s, :])


def attnphase(k, st, cfg):
    S, C, Wt = cfg["S"], cfg["C"], cfg["W"]
    T, CTX = cfg["T"], cfg["CTX"]
    TT = T + CTX
    NKC = TT // 128
    need_ctx = cfg["need_ctx"]
    kT = k.sb(st, [128, TT], BF16, "kT")
    va = k.sb(st, [128, NKC, 65], BF16, "va")
    qT = [k.sb(st, [128, 512], BF16, "qT%d" % i) for i in range(2)]
    for c0 in range(0, TT, 2048):
        k.memset("dve", kT.v[64:128, c0:min(TT, c0 + 2048)], 0.0)
    for q_ in qT:
        k.memset("dve", q_.v[64:128, :], 0.0)
    pT = [k.sb(st, [128, 512], BF16, "pT%d" % i) for i in range(3)]
    ot = [k.sb(st, [128, 64], F32, "ot%d" % i) for i in range(2)]
    rc = k.sb(st, [128, 1], F32, "rc")
    esk = k.sb(st, [128, 4], F32, "esk")
    mprev = k.sb(st, [128, 128], BF16, "mprev")
    mnext = k.sb(st, [128, 128], BF16, "mnext")
    k.ld("pool", mprev.v[:, :], C["mprev"])
    k.ld("pool", mnext.v[:, :], C["mnext"])
    k.ld("sp", esk.v[:, :], Wt["sink"].partition_broadcast(128))
    k.act(esk.v[:, :], esk.v[:, :], ACT.Exp)
    ps_s = [k.ps(st, [128, 512], F32, "pss%d" % i) for i in range(3)]
    ps_o = [k.ps(st, [128, 512], F32, "pso%d" % i) for i in range(2)]
    ps_f = k.ps(st, [128, 512], F32, "psf")
    osb = k.sb(st, [128, 512], F32, "osb")
    identf = k.sb(st, [128, 128], F32, "identf")
    k.ld("sp", identf.v[:, :], C["ident"])
    cnt = dict(s=0, q=0, o=0, p=0, a=0)
    scale = 64 ** -0.5

    def load_kv(kd, vd, g):
        k.ld("sp", kT.v[0:64, :], kd[g * 64:(g + 1) * 64, :])
        k.memset("dve", va.v[:, :, 64:65], 1.0)
        k.ld("sp", va.v[:, :, 0:64], vd[:, g * 64:(g + 1) * 64].rearrange("(c p) j -> p c j", p=128))

    def finish(po, nq_sub, t0, col, sinkcol):
        nq = nq_sub * 128
        k.cp("act", osb.v[0:65, 0:nq], po.v[0:65, 0:nq])
        for s in range(nq_sub):
            k.tr(ps_f.v[:, s * 65:(s + 1) * 65], osb.v[0:65, s * 128:(s + 1) * 128], identf.v[0:65, 0:65])
        for s in range(nq_sub):
            o = ot[cnt["o"] % 2]
            cnt["o"] += 1
            den = ps_f.v[:, s * 65 + 64:s * 65 + 65]
            if sinkcol is None:
                k.recip(rc.v[:, :], den)
            else:
                k.tt("dve", rc.v[:, :], den, esk.v[:, sinkcol:sinkcol + 1], ALU.add)
                k.recip(rc.v[:, :], rc.v[:, :])
            k.ts("dve", o.v[:, :], ps_f.v[:, s * 65:s * 65 + 64], rc.v[:, 0:1], None, ALU.mult)
            k.stv("sp", S["O"][t0 + s * 128:t0 + (s + 1) * 128, col:col + 64], o.v[:, :])

    def qk(q, nq, kc):
        ps = ps_s[cnt["s"] % 3]
        cnt["s"] += 1
        k.mm(ps.v[:, 0:nq], kT.v[:, kc * 128:(kc + 1) * 128], q.v[:, 0:nq])
        return ps

    def prologue(b):
        if b.get("kv") is not None:
            load_kv(*b["kv"])
        q = qT[cnt["q"] % 2]
        cnt["q"] += 1
        k.ld("sp", q.v[0:64, 0:b["nq"]], b["qd"][b["h"] * 64:(b["h"] + 1) * 64, b["t0"]:b["t0"] + b["nq"]])
        return q, qk(q, b["nq"], b["chunks"][0][0])

    def body(b, q, nxt):
        nq, chunks = b["nq"], b["chunks"]
        po = b["po"]
        assert len(chunks[0][1]) == nq // 128
        for i, (kc, subs) in enumerate(chunks):
            ps = nxt
            if i + 1 < len(chunks):
                nxt = qk(q, nq, chunks[i + 1][0])
            p = pT[cnt["p"] % 3]
            cnt["p"] += 1
            k.act(p.v[:, 0:nq], ps.v[:, 0:nq], ACT.Exp, scale=scale)
            for (s, m_) in subs:
                if m_ is not None:
                    k.tt("dve", p.v[:, s * 128:(s + 1) * 128], p.v[:, s * 128:(s + 1) * 128], m_.v[:, :], ALU.mult)
            c_lo = min(s for (s, _) in subs) * 128
            c_hi = (max(s for (s, _) in subs) + 1) * 128
            k.mm(po.v[0:65, c_lo:c_hi], va.v[:, kc, :], p.v[:, c_lo:c_hi], i == 0, i == len(chunks) - 1)

    blocks = []

    def block(qd, h, t0, nq, chunks, col, sinkcol, kv=None):
        blocks.append(dict(qd=qd, h=h, t0=t0, nq=nq, chunks=chunks, col=col, sinkcol=sinkcol, kv=kv,
                           po=ps_o[len(blocks) % 2]))

    for g in range(2):
        kv = (S["KB"], S["VB"], g)
        for h in (2 * g, 2 * g + 1):
            for t0 in range(0, T, 512):
                nq = min(512, T - t0)
                block(S["QB"], h, t0, nq, [(kc, [(s, None) for s in range(nq // 128)]) for kc in range(NKC)],
                      256 + h * 64, None, kv)
                kv = None
            if need_ctx:
                block(S["QB"], h, T, CTX, [(kc, [(s, None) for s in range(CTX // 128)]) for kc in range(T // 128, NKC)],
                      256 + h * 64, None)
    nb = T // 128
    cchunks = list(range(T // 128, NKC))
    for g in range(2):
        kv = (S["KA"], S["VA"], g)
        for h in (2 * g, 2 * g + 1):
            for b0 in range(0, nb, 4):
                nsb = min(4, nb - b0)
                chunks = [(kc, [(s, None) for s in range(nsb)]) for kc in cchunks]
                for kc in range(max(0, b0 - 1), min(nb, b0 + nsb + 1)):
                    subs = []
                    for s in range(nsb):
                        dlt = kc - (b0 + s)
                        if dlt == -1:
                            subs.append((s, mprev))
                        elif dlt == 0:
                            subs.append((s, None))
                        elif dlt == 1:
                            subs.append((s, mnext))
                    chunks.append((kc, subs))
                block(S["QA"], h, b0 * 128, nsb * 128, chunks, h * 64, h, kv)
                kv = None
            if need_ctx:
                block(S["QA"], h, T, CTX, [(kc, [(s, None) for s in range(CTX // 128)]) for kc in cchunks], h * 64, h)
    st_ = prologue(blocks[0])
    for i, b in enumerate(blocks):
        body(b, *st_)
        if i + 1 < len(blocks):
            st_ = prologue(blocks[i + 1])
        finish(b["po"], b["nq"] // 128, b["t0"], b["col"], b["sinkcol"])


def scanphase(k, st, cfg):
    S, C = cfg["S"], cfg["C"]
    T, CTX = cfg["T"], cfg["CTX"]
    TT = T + CTX
    dplr = cfg["dplr"]
    pre = "D_" if dplr else "C_"
    NCG = 2
    GT_ = NCG * CH
    NCHT = TT // CH
    f3 = lambda b: b.v[:, :, :]
    t64 = lambda nm: k.sb(st, [64, 8, 64], F32, nm)
    b64 = lambda nm: k.sb(st, [64, 8, 64], BF16, nm)
    chm = lambda nm: k.sb(st, [64, 8, GT_], BF16, nm)
    tkm = lambda nm: k.sb(st, [64, NCG, 8, 64], BF16, nm)
    RT = [chm("RT%d" % i) for i in range(2)]
    KT = [chm("KT%d" % i) for i in range(2)]
    KH = [tkm("KH%d" % i) for i in range(2)]
    VV = [tkm("VV%d" % i) for i in range(2)]
    Min = t64("Min")
    k.ld("sp", f3(Min), C["m_in"].rearrange("p (q t) -> p q t", q=8))
    id8 = t64("id8")
    k.ld("sp", f3(id8), C["id8"].rearrange("p (q t) -> p q t", q=8))
    if dplr:
        AT = [chm("AT%d" % i) for i in range(2)]
        BT = [chm("BT%d" % i) for i in range(2)]
        BH = [tkm("BH%d" % i) for i in range(2)]
        AK = [tkm("AK%d" % i) for i in range(2)]
        Mst, Mts = t64("Mst"), t64("Mts")
        k.ld("sp", f3(Mst), C["m_st"].rearrange("p (q t) -> p q t", q=8))
        k.ld("sp", f3(Mts), C["m_ts"].rearrange("p (q t) -> p q t", q=8))
        PCt = k.sb(st, [64, 8, NCHT], F32, "PCt")
        for d in range(2):
            k.ld("sp", PCt.v[:, d * 4:(d + 1) * 4, :], S["D_PC"][d].rearrange("(h j) c -> j h c", j=64))
    else:
        PCr = k.sb(st, [64, 8], F32, "PCr")
        k.ld("sp", PCr.v[:, :], C["ret_pc"])
        GTc = b64("GTc")
        k.tt("dve", f3(GTc), f3(id8), PCr.v[:, :].rr("p (q o) -> p q o", o=1).bc([64, 8, 64]), ALU.mult)
    sets = []
    for c in range(NCG):
        B = dict(ArkT=b64("ArkT%d" % c), Yv=t64("Yv%d" % c), H=t64("H%d" % c))
        if dplr:
            B.update(Ls=[b64("L%d_%d" % (i, c)) for i in range(5)], Ns=[b64("N%d_%d" % (i, c)) for i in range(6)],
                     AakT=b64("AakT%d" % c), ArbT=b64("ArbT%d" % c), Qe=b64("Qe%d" % c), GTt=b64("GTt%d" % c),
                     X=[k.sb(st, [64, 8, 128], F32, "X%d_%d" % (i, c)) for i in range(2)],
                     Xb=k.sb(st, [64, 8, 128], BF16, "Xb_%d" % c),
                     psX=k.ps(st, [64, 1024], F32, "psX%d" % c))
        sets.append(B)
    ST = [b64("ST%d" % i) for i in range(2)]
    Yt = [t64("Yt%d" % i) for i in range(2)]
    psA = [k.ps(st, [64, 512], F32, "psA%d" % i) for i in range(2)]
    psY = [k.ps(st, [64, 512], F32, "psY%d" % i) for i in range(2)]
    cnt = dict(a=0, y=0, e=0)
    p3 = lambda p: p.v[:, :].rr("p (q t) -> p q t", q=8)

    def npa():
        p = psA[cnt["a"] % 2]
        cnt["a"] += 1
        return p

    def ev():
        cnt["e"] += 1
        return "act" if cnt["e"] % 2 else "dve"

    def prod(dst, lhs, rhs, mask):
        p = npa()
        for q in range(8):
            k.mm(p3(p)[:, q, :], lhs(q), rhs(q))
        if mask is None:
            k.cp(ev(), f3(dst), p3(p))
        else:
            k.tt("dve", f3(dst), p3(p), f3(mask), ALU.mult)

    def views(rb, cl):
        sl = {d: slice(cl[d] * CH, (cl[d] + 1) * CH) for d in range(2)}
        dq = lambda q: q // 4
        chs = lambda buf: (lambda q: buf[rb].v[:, q, sl[dq(q)]])
        tks = lambda buf: (lambda q: buf[rb].v[:, cl[dq(q)], q, :])
        return sl, chs, tks

    def ppart(B, rb, cl, grp):
        sl, chs, tks = views(rb, cl)
        rt, kt, kh, vv = chs(RT), chs(KT), tks(KH), tks(VV)
        prod(B["ArkT"], kt, rt, Min)
        yield
        if dplr:
            at, bt, bh = chs(AT), chs(BT), tks(BH)
            Ls, Ns, AakT, ArbT = B["Ls"], B["Ns"], B["AakT"], B["ArbT"]
            prod(Ns[0], bt, at, Mst)
            prod(Ls[0], at, bt, Mts)
            yield
            prod(AakT, kt, at, Mst)
            prod(ArbT, bt, rt, Min)
            yield
            xc, xn = B["X"]
            xb = B["Xb"]
            for d in range(2):
                k.cp("act", xc.v[:, d * 4:(d + 1) * 4, 0:64], AK[rb].v[:, cl[d], d * 4:(d + 1) * 4, :])
            p = npa()
            for q in range(8):
                k.mm(p3(p)[:, q, :], AakT.v[:, q, :], vv(q))
            k.cp("dve", xc.v[:, :, 64:128], p3(p))
            k.cp("act", f3(xb), f3(xc))
            yield
            px = B["psX"].v[:, :].rr("p (q c) -> p q c", q=8)
            for lv in range(6):
                for q in range(8):
                    k.mm(px[:, q, :], Ns[lv].v[:, q, :], xb.v[:, q, :])
                if lv < 5:
                    if lv < 4:
                        prod(Ls[lv + 1], lambda q: Ns[lv].v[:, q, :], lambda q: Ls[lv].v[:, q, :], None)
                    prod(Ns[lv + 1], lambda q: Ls[lv].v[:, q, :], lambda q: Ns[lv].v[:, q, :], None)
                k.tt("dve", f3(xn), f3(xc), px, ALU.add)
                xc, xn = xn, xc
                k.cp("act", f3(xb), f3(xc))
                yield
            wq = lambda q: xb.v[:, q, 0:64]
            uv = lambda q: xb.v[:, q, 64:128]
            p = npa()
            for q in range(8):
                k.mm(p3(p)[:, q, :], wq(q), ArbT.v[:, q, :])
            for d in range(2):
                k.tt("dve", B["Qe"].v[:, d * 4:(d + 1) * 4, :], p3(p)[:, d * 4:(d + 1) * 4, :],
                     RT[rb].v[:, d * 4:(d + 1) * 4, sl[d]], ALU.add)
            p = npa()
            for q in range(8):
                k.mm(p3(p)[:, q, :], wq(q), bh(q))
            for d in range(2):
                c_abs = grp[d] * NCG + cl[d]
                k.tt("dve", B["GTt"].v[:, d * 4:(d + 1) * 4, :], id8.v[:, d * 4:(d + 1) * 4, :],
                     PCt.v[:, d * 4:(d + 1) * 4, c_abs:c_abs + 1].bc([64, 4, 64]), ALU.mult)
            k.tt("dve", f3(B["GTt"]), f3(B["GTt"]), p3(p), ALU.add)
            yield
        p = npa()
        for q in range(8):
            k.mm(p3(p)[:, q, :], B["ArkT"].v[:, q, :], vv(q), True, not dplr)
            if dplr:
                k.mm(p3(p)[:, q, :], ArbT.v[:, q, :], uv(q), False, True)
        k.cp(ev(), f3(B["Yv"]), p3(p))
        p = npa()
        for q in range(8):
            k.mm(p3(p)[:, q, :], kh(q), vv(q), True, not dplr)
            if dplr:
                k.mm(p3(p)[:, q, :], bh(q), uv(q), False, True)
        k.cp(ev(), f3(B["H"]), p3(p))
        yield

    k.memset("dve", f3(ST[0]), 0.0)
    ngl = T // GT_
    ncg_ctx = CTX // GT_
    fw = list(range(ngl, ngl + ncg_ctx)) + list(range(ngl))
    bw = list(range(ngl + ncg_ctx - 1, ngl - 1, -1)) + list(range(ngl - 1, -1, -1))
    order = {0: fw, 1: bw}
    step = 0

    def emit_loads(gi):
        rb = gi % 2
        grp = {d: order[d][gi] for d in range(2)}
        for d in range(2):
            t0 = grp[d] * GT_
            qs = slice(d * 4, (d + 1) * 4)
            chv = lambda nm: S[pre + nm][d, :, t0:t0 + GT_].rearrange("(h j) t -> j h t", j=64)
            tkv = lambda ap: ap[t0:t0 + GT_, :].rearrange("(n t) (h j) -> t n h j", t=CH, j=64)
            k.ld("sp", RT[rb].v[:, qs, :], chv("Rt"))
            k.ld("sp", KT[rb].v[:, qs, :], chv("Kt"))
            k.ld("sp", KH[rb].v[:, :, qs, :], tkv(S[pre + "Kh"][d]))
            k.ld("sp", VV[rb].v[:, :, qs, :], tkv(S["D_V"][d] if dplr else S["C_V"]))
            if dplr:
                k.ld("sp", AT[rb].v[:, qs, :], chv("At"))
                k.ld("sp", BT[rb].v[:, qs, :], chv("Bt"))
                k.ld("sp", BH[rb].v[:, :, qs, :], tkv(S["D_Bh"][d]))
                k.ld("sp", AK[rb].v[:, :, qs, :], tkv(S["D_Atok"][d]))

    emit_loads(0)
    for gi in range(ngl + ncg_ctx):
        rb = gi % 2
        grp = {d: order[d][gi] for d in range(2)}
        if gi + 1 < ngl + ncg_ctx:
            emit_loads(gi + 1)
        cls = [{0: ci, 1: NCG - 1 - ci} for ci in range(NCG)]
        gens = [ppart(sets[ci], rb, cls[ci], grp) for ci in range(NCG)]
        live = list(gens)
        while live:
            for g_ in list(live):
                try:
                    next(g_)
                except StopIteration:
                    live.remove(g_)
        for ci in range(NCG):
            B, cl = sets[ci], cls[ci]
            sl, chs, tks = views(rb, cl)
            Sc, Sn = ST[step % 2], ST[(step + 1) % 2]
            qe = (lambda q: B["Qe"].v[:, q, :]) if dplr else chs(RT)
            gt = (lambda q: B["GTt"].v[:, q, :]) if dplr else (lambda q: GTc.v[:, q, :])
            py = psY[cnt["y"] % 2]
            cnt["y"] += 1
            for q in range(8):
                k.mm(p3(py)[:, q, :], qe(q), Sc.v[:, q, :])
            p = npa()
            for q in range(8):
                k.mm(p3(p)[:, q, :], gt(q), Sc.v[:, q, :])
            k.tt("dve", f3(Sn), p3(p), f3(B["H"]), ALU.add)
            yt = Yt[step % 2]
            k.tt("dve", f3(yt), p3(py), f3(B["Yv"]), ALU.add)
            for d in range(2):
                tt0 = grp[d] * GT_ + cl[d] * CH
                k.stv("sp", S[pre + "Y"][d, tt0:tt0 + CH, :].rearrange("t (h i) -> t h i", i=64),
                      yt.v[:, d * 4:(d + 1) * 4, :])
            step += 1


GN_EPS = 64e-5


def postphase(k, st, cfg):
    S, Wt = cfg["S"], cfg["W"]
    TT = cfg["T"] + cfg["CTX"]
    row = lambda nm, ap: (lambda t: (k.ld("sp", t.v[:, :], ap.partition_broadcast(128)), t)[1])(k.sb(st, [128, 256], F32, nm))
    retg = row("retg", Wt["ret_g"])
    lng = row("lng", Wt["ln_g"])
    lnb = row("lnb", Wt["ln_b"])
    vdf = k.sb(st, [128, 4, 64], F32, "vdf")
    LD = []
    for sl_ in range(2):
        LD.append(dict(gc=k.sb(st, [128, 512], F32, "gc%d" % sl_), gd=k.sb(st, [128, 256], F32, "gd%d" % sl_),
                       cy=[k.sb(st, [128, 4, 64], F32, "cy%d_%d" % (sl_, d)) for d in range(2)],
                       dy=[k.sb(st, [128, 4, 64], F32, "dy%d_%d" % (sl_, d)) for d in range(2)],
                       vd=[k.sb(st, [128, 4, 64], BF16, "vd%d_%d" % (sl_, d)) for d in range(2)],
                       bs=[k.sb(st, [128, 4], F32, "bs%d_%d" % (sl_, d)) for d in range(2)]))

    def emit_loads(ti):
        Lc = LD[ti % 2]
        tsl = slice(ti * 128, (ti + 1) * 128)
        k.ld("sp", Lc["gc"].v[:, :], S["C_G"][tsl, :])
        k.ld("sp", Lc["gd"].v[:, :], S["D_G"][tsl, :])
        for d in range(2):
            k.ld("sp", Lc["cy"][d].v[:, :, :], S["C_Y"][d, tsl, :].rearrange("t (h i) -> t h i", i=64))
            k.ld("sp", Lc["dy"][d].v[:, :, :], S["D_Y"][d, tsl, :].rearrange("t (h i) -> t h i", i=64))
            k.ld("sp", Lc["vd"][d].v[:, :, :], S["D_V"][d, tsl, :].rearrange("t (h i) -> t h i", i=64))
            k.ld("sp", Lc["bs"][d].v[:, :], S["D_BS"][d, tsl, :])

    def bufset(nm):
        return dict(y=[k.sb(st, [128, 4, 64], F32, nm + "y%d" % i) for i in range(2)],
                    yc=k.sb(st, [128, 4, 64], F32, nm + "yc"), sq=k.sb(st, [128, 4, 64], F32, nm + "sq"),
                    m4=k.sb(st, [128, 4], F32, nm + "m4"), v4=k.sb(st, [128, 4], F32, nm + "v4"),
                    acc=k.sb(st, [128, 256], F32, nm + "acc"), yn=k.sb(st, [128, 256], F32, nm + "yn"), i=0)

    BC, BD = bufset("c"), bufset("d")

    def head_norm(B, yb, g, b, dst):
        yc, sq, m4, v4 = B["yc"], B["sq"], B["m4"], B["v4"]
        k.op("dve", lambda e: e.tensor_reduce(out=m4[:, :], in_=yb[:, :, :], axis=AX.X, op=ALU.add), r=[yb], w=[m4])
        k.ts("dve", m4.v[:, :], m4.v[:, :], 1.0 / 64, None, ALU.mult)
        yield
        k.tt("dve", yc.v[:, :, :], yb.v[:, :, :], m4.v[:, :].rr("p (h o) -> p h o", o=1).bc([128, 4, 64]), ALU.subtract)
        yield
        k.tt("dve", sq.v[:, :, :], yc.v[:, :, :], yc.v[:, :, :], ALU.mult)
        yield
        k.op("dve", lambda e: e.tensor_reduce(out=v4[:, :], in_=sq[:, :, :], axis=AX.X, op=ALU.add), r=[sq], w=[v4])
        yield
        k.ts("dve", v4.v[:, :], v4.v[:, :], 1.0 / 64, GN_EPS, ALU.mult, ALU.add)
        yield
        k.act(v4.v[:, :], v4.v[:, :], ACT.Sqrt)
        yield
        k.recip(v4.v[:, :], v4.v[:, :])
        yield
        k.tt("dve", yc.v[:, :, :], yc.v[:, :, :], v4.v[:, :].rr("p (h o) -> p h o", o=1).bc([128, 4, 64]), ALU.mult)
        yield
        k.tt("dve", dst, yc.v[:, :, :].rr("p h i -> p (h i)"), g.v[:, :], ALU.mult)
        if b is not None:
            k.tt("dve", dst, dst, b.v[:, :], ALU.add)
        yield

    def part_c(tsl, Lc):
        B = BC
        acc, yn = B["acc"], B["yn"]
        gc = Lc["gc"]
        for d in range(2):
            yb = Lc["cy"][d]
            yield from head_norm(B, yb, retg, None, yn.v[:, :])
            if d == 0:
                k.tt("dve", acc.v[:, :], yn.v[:, :], gc.v[:, 0:256], ALU.mult)
            else:
                k.tt("dve", yn.v[:, :], yn.v[:, :], gc.v[:, 256:512], ALU.mult)
                k.tt("dve", acc.v[:, :], acc.v[:, :], yn.v[:, :], ALU.add)
            yield
        k.stv("sp", S["O"][tsl, 512:768], acc.v[:, :])

    def part_d(tsl, Lc):
        B = BD
        acc, yn = B["acc"], B["yn"]
        gd = Lc["gd"]
        for d in range(2):
            yb, vd, bs = Lc["dy"][d], Lc["vd"][d], Lc["bs"][d]
            yield from head_norm(B, yb, lng, lnb, yn.v[:, :])
            k.tt("dve", vdf.v[:, :, :], vd.v[:, :, :], bs.v[:, :].rr("p (h o) -> p h o", o=1).bc([128, 4, 64]), ALU.mult)
            k.tt("dve", yn.v[:, :], yn.v[:, :], vdf.v[:, :, :].rr("p h i -> p (h i)"), ALU.add)
            yield
            if d == 0:
                k.cp("act", acc.v[:, :], yn.v[:, :])
            else:
                k.tt("dve", acc.v[:, :], acc.v[:, :], yn.v[:, :], ALU.add)
            yield
        k.tt("dve", acc.v[:, :], acc.v[:, :], gd.v[:, :], ALU.mult)
        k.stv("sp", S["O"][tsl, 768:1024], acc.v[:, :])

    ntile = TT // 128
    emit_loads(0)
    for ti in range(ntile):
        t0 = ti * 128
        tsl = slice(t0, t0 + 128)
        if ti + 1 < ntile:
            emit_loads(ti + 1)
        live = [part_c(tsl, LD[ti % 2]), part_d(tsl, LD[ti % 2])]
        while live:
            for g_ in list(live):
                try:
                    next(g_)
                except StopIteration:
                    live.remove(g_)


def modphase(k, st, cfg):
    cv = k.sb(st, [128, 2, 8], F32, "cv")
    for r in range(2):
        k.ld("sp", cv.v[:, r, :], cfg["cvec"][r].rearrange("(c p) -> p c", p=128), allow_slow_non_contiguous=True)
    sc = k.sb(st, [128, 8, 2], F32, "sc")
    k.act(sc.v[:, :, :].rr("p c r -> p r c"), cv.v[:, :, :], ACT.Silu)
    wm = [k.sb(st, [128, 8, 512], F32, "wm%d" % i) for i in range(2)]
    bm = k.sb(st, [2, 9 * D], F32, "bm")
    k.ld("sp", bm.v[:, :], cfg["b_mod"].partition_broadcast(2))
    ob = k.sb(st, [2, 9 * D], F32, "ob")
    pm = [k.ps(st, [128, 512], F32, "pm%d" % i) for i in range(2)]
    wv = cfg["w_mod"].rearrange("(c p) f -> p c f", p=128)
    for j in range(18):
        w = wm[j % 2]
        for c in range(8):
            k.ld("sp" if j % 2 == 0 else "pool", w.v[:, c, :], wv[:, c, j * 512:(j + 1) * 512])
        p = pm[j % 2]
        for c in range(8):
            k.mm(p.v[0:2, :], sc.v[:, c, :], w.v[:, c, :], c == 0, c == 7)
        k.tt("dve", ob.v[:, j * 512:(j + 1) * 512], p.v[0:2, :], bm.v[:, j * 512:(j + 1) * 512], ALU.add)
    k.stv("sp", cfg["mod"], ob.v[:, :])


def build_program(T, CTX, L, stop_after=None):
    TT = T + CTX
    nc = bass.Bass("TRN2", target_bir_lowering=False, dynamic_dma_scratch_size=8192)

    def din(name, shape, dt=F32):
        return nc.dram_tensor(name, list(shape), dt, kind="ExternalInput").ap()

    def scr(name, shape, dt=F32):
        return nc.dram_tensor(name, list(shape), dt, kind="Internal").ap()

    I = dict(
        xin=din("xin", [TT, D]), cvec=din("cvec", [2, D]),
        w_mod=din("w_mod", [L, D, 9 * D]), b_mod=din("b_mod", [L, 9 * D]), norm_g=din("norm_g", [L, 3, D]),
        ffn_w_in=din("ffn_w_in", [L, 2, D, 2 * DFF]), ffn_w_out=din("ffn_w_out", [L, 2, DFF, D]),
        w_in=din("w_in", [L, D, 3456]), w_out=din("w_out", [L, D, D]), final_g=din("final_g", [D]),
        w_sw=din("w_sw", [L, D, 1280]), w_d=din("w_d", [L, 2, D, 896]), pcols=din("pcols", [L, 128, 24]),
        mu=din("mu", [L, 2, 896]), w2=din("w2", [L, 2, 64, 256]), a2=din("a2", [L, 2, 64, 256]),
        g2=din("g2", [L, 128, 256]), ret_g=din("ret_g", [L, 256]), ln_g=din("ln_g", [L, 256]),
        ln_b=din("ln_b", [L, 256]), sink=din("sink", [L, 4]),
    )
    Cn = dict(
        ident=din("ident", [128, 128]), bd64=din("bd64", [128, 128]), e2=din("e2", [128, 2]),
        ropeA_c=din("ropeA_c", [128, T]), ropeA_s=din("ropeA_s", [128, T]),
        ropeS_c=din("ropeS_c", [128, T]), ropeS_s=din("ropeS_s", [128, T]),
        ret_tab=din("ret_tab", [128, 2, 2, 3, CH]), ret_pc=din("ret_pc", [64, 8]),
        m_st=din("m_st", [64, 512]), m_ts=din("m_ts", [64, 512]), m_in=din("m_in", [64, 512]),
        id8=din("id8", [64, 512]), mprev=din("mprev", [128, 128]), mnext=din("mnext", [128, 128]),
    )
    out = nc.dram_tensor("out", [T, D], F32, kind="ExternalOutput").ap()
    dbg = stop_after is not None
    mk = (lambda name, shape, dt=F32: nc.dram_tensor(name, list(shape), dt, kind="ExternalOutput").ap()) if dbg else scr
    S = dict(
        xres=mk("xres", [TT, D]), mod=mk("modr", [2, 9 * D]), hT=mk("hT", [D, TT], BF16),
        QA=mk("QA", [256, TT], BF16), KA=mk("KA", [128, TT], BF16), VA=mk("VA", [TT, 128], BF16),
        QB=mk("QB", [256, TT], BF16), KB=mk("KB", [128, TT], BF16), VB=mk("VB", [TT, 128], BF16),
        O=mk("O", [TT, 1024]),
        C_Rt=mk("C_Rt", [2, 256, TT], BF16), C_Kt=mk("C_Kt", [2, 256, TT], BF16), C_Kh=mk("C_Kh", [2, TT, 256], BF16),
        C_V=mk("C_V", [TT, 256], BF16), C_G=mk("C_G", [TT, 512]), C_Y=mk("C_Y", [2, TT, 256]),
        D_Rt=mk("D_Rt", [2, 256, TT], BF16), D_Kt=mk("D_Kt", [2, 256, TT], BF16), D_At=mk("D_At", [2, 256, TT], BF16),
        D_Bt=mk("D_Bt", [2, 256, TT], BF16), D_Kh=mk("D_Kh", [2, TT, 256], BF16), D_Bh=mk("D_Bh", [2, TT, 256], BF16),
        D_Atok=mk("D_Atok", [2, TT, 256], BF16), D_V=mk("D_V", [2, TT, 256], BF16), D_Y=mk("D_Y", [2, TT, 256]),
        D_PC=mk("D_PC", [2, 256, TT // CH]), D_G=mk("D_G", [TT, 256]), D_BS=mk("D_BS", [2, TT, 4]),
    )
    k = K(nc)
    stages = []

    def phase(name, fn, cfg):
        if stop_after is not None and stop_after in stages:
            return
        with contextlib.ExitStack() as st:
            fn(k, st, cfg)
            k.barrier()
        stages.append(name)
        k.marks.append((name, {e.name: e.n for e in k.E.values()}))

    with k.stack:
        segs_all = [(0, T, 0), (T, CTX, 1)]
        for l in range(L):
            last = l == L - 1
            W = dict(w_in=I["w_in"][l], w_sw=I["w_sw"][l], w_d=I["w_d"][l], mu=I["mu"][l], pcols=I["pcols"][l],
                     w2=I["w2"][l], a2=I["a2"][l], g2=I["g2"][l], ret_g=I["ret_g"][l], ln_g=I["ln_g"][l],
                     ln_b=I["ln_b"][l], sink=I["sink"][l])
            base = dict(S=S, C=Cn, W=W, T=T, CTX=CTX)
            phase("mod%d" % l, modphase, dict(cvec=I["cvec"], w_mod=I["w_mod"][l], b_mod=I["b_mod"][l], mod=S["mod"]))
            phase("f1_%d" % l, rowphase, dict(
                xin=I["xin"] if l == 0 else S["xres"], xout=S["xres"], segs=segs_all, mod=S["mod"], ident=Cn["ident"],
                ffn=dict(w_in=I["ffn_w_in"][l, 0], w_out=I["ffn_w_out"][l, 0], g=I["norm_g"][l, 0], shift=0, scale=1,
                         gate=2),
                post=dict(kind="hT", g=I["norm_g"][l, 1], shift=3, scale=4, hT=S["hT"])))
            phase("proj%d" % l, projphase, base)
            phase("attn%d" % l, attnphase, dict(base, need_ctx=not last))
            phase("scanC%d" % l, scanphase, dict(base, dplr=False))
            phase("scanD%d" % l, scanphase, dict(base, dplr=True))
            phase("post%d" % l, postphase, base)
            phase("f2_%d" % l, rowphase, dict(
                xin=S["xres"], xout=S["xres"], segs=segs_all if not last else [(0, T, 0)], mod=S["mod"],
                ident=Cn["ident"],
                pre=dict(o=S["O"], w_out=I["w_out"][l], gate=5),
                ffn=dict(w_in=I["ffn_w_in"][l, 1], w_out=I["ffn_w_out"][l, 1], g=I["norm_g"][l, 2], shift=6, scale=7,
                         gate=8),
                post=dict(kind="final", g=I["final_g"], out=out) if last else None))
    return nc, k


def _consts(T):
    c = {}
    c["ident"] = np.eye(128, dtype=np.float32)
    bd = np.zeros((128, 128), np.float32)
    bd[:64, :64] = 1
    bd[64:, 64:] = 1
    c["bd64"] = bd
    e2 = np.zeros((128, 2), np.float32)
    e2[:64, 0] = 1
    e2[64:, 1] = 1
    c["e2"] = e2
    t = np.arange(T)
    row = (t // 64).astype(np.float32)
    col = (t % 64).astype(np.float32)
    inv16 = (1.0 / (np.float32(10000.0) ** (np.arange(0, 32, 2, dtype=np.float32) / np.float32(32)))).astype(np.float32)
    inv32 = (1.0 / (np.float32(10000.0) ** (np.arange(0, 64, 2, dtype=np.float32) / np.float32(64)))).astype(np.float32)
    ca = np.zeros((64, T), np.float32)
    sa = np.zeros((64, T), np.float32)
    for blk, pos in ((0, row), (32, col)):
        ang = pos[None, :] * inv16[:, None]
        ca[blk:blk + 16] = np.cos(ang)
        ca[blk + 16:blk + 32] = np.cos(ang)
        sa[blk:blk + 16] = -np.sin(ang)
        sa[blk + 16:blk + 32] = np.sin(ang)
    ang = t.astype(np.float32)[None, :] * inv32[:, None]
    cs = np.concatenate([np.cos(ang), np.cos(ang)], 0).astype(np.float32)
    ss = np.concatenate([-np.sin(ang), np.sin(ang)], 0).astype(np.float32)
    c["ropeA_c"] = np.concatenate([ca, ca], 0)
    c["ropeA_s"] = np.concatenate([sa, sa], 0)
    c["ropeS_c"] = np.concatenate([cs, cs], 0)
    c["ropeS_s"] = np.concatenate([ss, ss], 0)
    lg = np.log1p(-np.exp2(-5.0 - np.arange(4, dtype=np.float64)))
    i = np.arange(CH, dtype=np.float64)
    rt = np.zeros((128, 2, 2, 3, CH), np.float64)
    for p in range(128):
        for g in range(2):
            l_ = lg[2 * g + p // 64]
            for d in range(2):
                csum = (i + 1) * l_ if d == 0 else (CH - i) * l_
                rt[p, g, d, 0] = np.exp(csum)
                rt[p, g, d, 1] = np.exp(-csum) / 8.0
                rt[p, g, d, 2] = np.exp(CH * l_ - csum) / 8.0
    c["ret_tab"] = rt.astype(np.float32)
    c["ret_pc"] = np.tile(np.exp(CH * lg)[None, :], (64, 2)).astype(np.float32)
    s_ = np.arange(64)[:, None]
    t_ = np.arange(64)[None, :]
    lt = (s_ < t_).astype(np.float32)
    gt = (s_ > t_).astype(np.float32)
    eye = np.eye(64, dtype=np.float32)
    c["m_st"] = np.concatenate([lt] * 4 + [gt] * 4, 1)
    c["m_ts"] = np.concatenate([gt] * 4 + [lt] * 4, 1)
    c["m_in"] = np.concatenate([lt + eye] * 4 + [gt + eye] * 4, 1)
    c["id8"] = np.concatenate([eye] * 8, 1)
    j = np.arange(128)[:, None]
    ii = np.arange(128)[None, :]
    c["mprev"] = (ii <= j).astype(np.float32)
    c["mnext"] = (j <= ii).astype(np.float32)
    return c


def _host_maps(inp, T, CTX, L):
    f = lambda a: np.ascontiguousarray(np.asarray(a, dtype=np.float32))
    w_in = f(inp["w_in"])
    pa = np.concatenate([np.arange(16, 32), np.arange(0, 16), np.arange(48, 64), np.arange(32, 48)])
    psq = np.concatenate([np.arange(32, 64), np.arange(0, 32)])
    cols = []
    for base, nh, perm in ((0, 4, pa), (256, 2, pa), (512, 4, pa), (768, 2, pa), (1024, 4, psq), (1280, 4, psq)):
        for h in range(nh):
            cols.append(base + h * 64 + perm)
    cols = np.concatenate(cols)
    w_sw = np.ascontiguousarray(w_in[:, :, cols])
    dcols = [np.concatenate([np.arange(2304, 3072), np.arange(3200, 3264), np.arange(3328, 3392)]),
             np.concatenate([np.arange(2304, 3072), np.arange(3264, 3328), np.arange(3392, 3456)])]
    w_d = np.ascontiguousarray(np.stack([w_in[:, :, dc] for dc in dcols], 1))
    qkg = f(inp["qk_norm_g"])
    pcols = np.zeros((L, 128, 24), np.float32)
    two = lambda v: np.ascontiguousarray(v.reshape(2, 128).T)
    for l in range(L):
        for qk in range(2):
            g = qkg[l, qk]
            pcols[l, :, 2 * qk] = np.tile(g, 2)
            pcols[l, :, 2 * qk + 1] = np.tile(g[pa], 2)
        for d in range(2):
            pcols[l, :, 4 + 2 * d:6 + 2 * d] = two(f(inp["rwkv_w0"])[l, d])
            pcols[l, :, 8 + 2 * d:10 + 2 * d] = two(f(inp["rwkv_a0"])[l, d])
            pcols[l, :, 16 + 2 * d:18 + 2 * d] = two(f(inp["rwkv_rho"])[l, d].reshape(256))
        pcols[l, :, 12:14] = two(f(inp["rwkv_k_k"])[l])
        pcols[l, :, 14:16] = two(f(inp["rwkv_k_a"])[l])
    shared = dict(
        w_mod=f(inp["w_mod"]), b_mod=f(inp["b_mod"]), norm_g=f(inp["norm_g"]), ffn_w_in=f(inp["ffn_w_in"]),
        ffn_w_out=f(inp["ffn_w_out"]), w_in=w_in, w_out=f(inp["w_out"]), final_g=f(inp["final_norm_g"]),
        w_sw=w_sw, w_d=w_d, pcols=pcols, mu=f(inp["rwkv_mu"]), w2=f(inp["rwkv_w2"]), a2=f(inp["rwkv_a2"]),
        g2=f(inp["rwkv_g2"]), ret_g=f(inp["ret_norm_g"]), ln_g=f(inp["rwkv_ln_g"]), ln_b=f(inp["rwkv_ln_b"]),
        sink=f(inp["attn_sink"]))
    shared.update(_consts(T))
    x, c, ctx, c_ctx = f(inp["x"]), f(inp["c"]), f(inp["ctx"]), f(inp["c_ctx"])
    maps = []
    for b in range(x.shape[0]):
        m = dict(shared)
        m["xin"] = np.ascontiguousarray(np.concatenate([x[b], ctx[b]], 0))
        m["cvec"] = np.ascontiguousarray(np.stack([c[b], c_ctx], 0))
        maps.append(m)
    return maps


_PROG = {}


def kernel(**inputs):
    x = np.asarray(inputs["x"])
    B, T, _ = x.shape
    CTX = np.asarray(inputs["ctx"]).shape[1]
    L = np.asarray(inputs["w_in"]).shape[0]
    key = (T, CTX, L)
    if key not in _PROG:
        _PROG[key] = build_program(T, CTX, L)[0]
    nc = _PROG[key]
    maps = _host_maps(inputs, T, CTX, L)
    res = run_bass_kernel_spmd(nc, maps, core_ids=list(range(B)))
    return np.stack([np.asarray(r["out"], dtype=np.float32) for r in res.results], 0)
```
